# Optimizing a Trainium2 kernel written in Bass

```python
import jax, jax.numpy as jnp
from jax import lax
import numpy as np

D_MODEL = 2048
BATCH = 2
SEQ = 8192
DEPTH = 4

A_HEADS = 8
A_DK = 128
A_DV = 128
A_WIDTH = A_HEADS * A_DK
B_HEADS = 8
B_HEAD_DIM = 128
B_WIDTH = B_HEADS * B_HEAD_DIM
AB_IN = 4 * A_WIDTH + 4 * B_WIDTH + B_HEADS
AB_CAT = A_HEADS * A_DV + B_WIDTH
C_HEADS = 4
C_DK = 256
C_DV = 512
C_KW = C_HEADS * C_DK
C_VW = C_HEADS * C_DV
C_GATE_RANK = 16
C_GATE_NORMALIZER = 16.0
C_IN = 2 * C_KW + 2 * C_VW + C_GATE_RANK
D_FF = 256 * (-(-8 * D_MODEL // (3 * 256)))
N_AB = (DEPTH + 1) // 2
N_C = DEPTH // 2
CHUNK = 64
Q_BLOCK = 128
EPS = 1e-6
MASK_VALUE = -1e30
MIN_FORGET = 1e-30

kernel_name = "hybrid_hgrn2_fox_gla_adaln_trunk"


def rms_norm(x, gain):
    xf = x.astype(jnp.float32)
    y = xf * lax.rsqrt(jnp.mean(xf * xf, axis=-1, keepdims=True) + EPS)
    return (y * gain.astype(jnp.float32)).astype(x.dtype)


def modulate(x, shift, scale):
    return x * (1 + scale[:, None, :]) + shift[:, None, :]


def split_heads(x, n_heads):
    b, t, _ = x.shape
    return x.reshape(b, t, n_heads, -1).transpose(0, 2, 1, 3)


def merge_heads(x):
    b, h, t, d = x.shape
    return x.transpose(0, 2, 1, 3).reshape(b, t, h * d)


def chunk_gated_linear_recurrence(q, k, v, log_g):
    b, h, t, dk = q.shape
    dv = v.shape[-1]
    n = t // CHUNK

    def to_chunks(a):
        return jnp.moveaxis(a.astype(jnp.float32).reshape(b, h, n, CHUNK, a.shape[-1]), 2, 0)

    causal = jnp.tril(jnp.ones((CHUNK, CHUNK), dtype=bool))[:, :, None]

    def step(state, inp):
        qc, kc, vc, gc = inp
        cum = jnp.cumsum(gc, axis=-2)
        o_inter = jnp.einsum('bhtd,bhde->bhte', qc * jnp.exp(cum), state)
        diff = cum[..., :, None, :] - cum[..., None, :, :]
        decay = jnp.where(causal, jnp.exp(jnp.where(causal, diff, 0.0)), 0.0)
        scores = jnp.einsum('bhtd,bhsd,bhtsd->bhts', qc, kc, decay)
        o_intra = jnp.einsum('bhts,bhse->bhte', scores, vc)
        last = cum[..., -1:, :]
        state = jnp.exp(last[..., 0, :])[..., None] * state + jnp.einsum(
            'bhsd,bhse->bhde', kc * jnp.exp(last - cum), vc)
        return state, o_inter + o_intra

    state0 = jnp.zeros((b, h, dk, dv), jnp.float32)
    _, o = lax.scan(step, state0, (to_chunks(q), to_chunks(k), to_chunks(v), to_chunks(log_g)))
    return jnp.moveaxis(o, 0, 2).reshape(b, h, t, dv)


def forgetting_attention(q, k, v, log_f):
    b, h, t, d = q.shape
    nb = t // Q_BLOCK
    cum_f = jnp.cumsum(log_f.astype(jnp.float32), axis=-1)
    qf = q.astype(jnp.float32) * (d ** -0.5)
    kf = k.astype(jnp.float32)
    vf = v.astype(jnp.float32)
    q_blocks = jnp.moveaxis(qf.reshape(b, h, nb, Q_BLOCK, d), 2, 0)
    f_blocks = jnp.moveaxis(cum_f.reshape(b, h, nb, Q_BLOCK), 2, 0)
    key_pos = jnp.arange(t)

    def block(args):
        qb, fb, start = args
        s = jnp.einsum('bhqd,bhkd->bhqk', qb, kf) + fb[..., :, None] - cum_f[..., None, :]
        q_pos = start + jnp.arange(Q_BLOCK)
        s = jnp.where(key_pos[None, :] <= q_pos[:, None], s, MASK_VALUE)
        p = jax.nn.softmax(s, axis=-1)
        return jnp.einsum('bhqk,bhkd->bhqd', p, vf)

    o = lax.map(block, (q_blocks, f_blocks, jnp.arange(nb) * Q_BLOCK))
    return jnp.moveaxis(o, 0, 2).reshape(b, h, t, d)


def hgrn2_fox_mixer(h, w_in, w_out, lower_bound, a_out_gain, q_gain, k_gain, f_bias):
    proj = h @ w_in
    cuts = [A_WIDTH, 2 * A_WIDTH, 3 * A_WIDTH, 4 * A_WIDTH,
            4 * A_WIDTH + B_WIDTH, 4 * A_WIDTH + 2 * B_WIDTH,
            4 * A_WIDTH + 3 * B_WIDTH, 4 * A_WIDTH + 4 * B_WIDTH]
    qa, fa, ia, ga, qb, kb, vb, gb, fb = jnp.split(proj, cuts, axis=-1)
    z = split_heads(fa, A_HEADS).astype(jnp.float32)
    lb = lower_bound.astype(jnp.float32).reshape(A_HEADS, 1, A_DK)
    sig = jax.nn.sigmoid(z)
    forget = lb + (1 - lb) * sig
    log_forget = jnp.log(jnp.maximum(forget, MIN_FORGET))
    key = (1 - lb) * (1 - sig)
    o_a = chunk_gated_linear_recurrence(jax.nn.silu(split_heads(qa, A_HEADS)), key,
                                        split_heads(ia, A_HEADS), log_forget)
    o_a = rms_norm(o_a, a_out_gain[:, None, :])
    o_a = merge_heads(o_a).astype(h.dtype) * jax.nn.silu(ga)
    q = rms_norm(split_heads(qb, B_HEADS), q_gain)
    k = rms_norm(split_heads(kb, B_HEADS), k_gain)
    log_f = jax.nn.log_sigmoid(fb.astype(jnp.float32) + f_bias).transpose(0, 2, 1)
    o_b = forgetting_attention(q, k, split_heads(vb, B_HEADS), log_f)
    o_b = merge_heads(o_b).astype(h.dtype) * jax.nn.sigmoid(gb)
    return jnp.concatenate([o_a, o_b], axis=-1) @ w_out


def gla_mixer(h, w_in, w_gate_up, b_gate, out_gain, w_out):
    proj = h @ w_in
    cuts = [C_KW, 2 * C_KW, 2 * C_KW + C_VW, 2 * C_KW + 2 * C_VW]
    q, k, v, g, g_low = jnp.split(proj, cuts, axis=-1)
    log_alpha = jax.nn.log_sigmoid((g_low @ w_gate_up).astype(jnp.float32) + b_gate) / C_GATE_NORMALIZER
    o = chunk_gated_linear_recurrence(split_heads(q, C_HEADS) * (C_DK ** -0.5),
                                      split_heads(k, C_HEADS), split_heads(v, C_HEADS),
                                      split_heads(log_alpha, C_HEADS))
    o = rms_norm(o, out_gain)
    o = merge_heads(o).astype(h.dtype) * jax.nn.silu(g)
    return o @ w_out


def swiglu(h, w_in, w_out):
    a, u = jnp.split(h @ w_in, 2, axis=-1)
    return (jax.nn.silu(a) * u) @ w_out


def setup_inputs(seed: int = 0) -> dict:
    key = jax.random.key(seed)
    ks = jax.random.split(key, 24)

    def normal(k, shape, std):
        return jax.random.normal(k, shape, jnp.float32) * std

    def gain(k, shape):
        return 1.0 + normal(k, shape, 0.02)

    return {
        "x": normal(ks[0], (BATCH, SEQ, D_MODEL), 1.0),
        "c": normal(ks[1], (BATCH, D_MODEL), 1.0),
        "mod_w": normal(ks[2], (DEPTH, D_MODEL, 6 * D_MODEL), 0.5 * D_MODEL ** -0.5),
        "mod_b": normal(ks[3], (DEPTH, 6 * D_MODEL), 0.02),
        "norm_mix_gain": gain(ks[4], (DEPTH, D_MODEL)),
        "norm_ffn_gain": gain(ks[5], (DEPTH, D_MODEL)),
        "ab_w_in": normal(ks[6], (N_AB, D_MODEL, AB_IN), D_MODEL ** -0.5),
        "ab_w_out": normal(ks[7], (N_AB, AB_CAT, D_MODEL), AB_CAT ** -0.5),
        "hgrn_lb_logits": normal(ks[8], (N_AB, A_WIDTH), 1.0),
        "hgrn_out_gain": gain(ks[9], (N_AB, A_HEADS, A_DV)),
        "fox_q_gain": gain(ks[10], (N_AB, B_HEAD_DIM)),
        "fox_k_gain": gain(ks[11], (N_AB, B_HEAD_DIM)),
        "fox_f_bias": jax.random.uniform(ks[12], (N_AB, B_HEADS), jnp.float32, 1.0, 4.0),
        "gla_w_in": normal(ks[13], (N_C, D_MODEL, C_IN), D_MODEL ** -0.5),
        "gla_w_gate_up": normal(ks[14], (N_C, C_GATE_RANK, C_KW), C_GATE_RANK ** -0.5),
        "gla_b_gate": normal(ks[15], (N_C, C_KW), 0.02),
        "gla_out_gain": gain(ks[16], (N_C, C_DV)),
        "gla_w_out": normal(ks[17], (N_C, C_VW, D_MODEL), C_VW ** -0.5),
        "ffn_w_in": normal(ks[18], (DEPTH, D_MODEL, 2 * D_FF), D_MODEL ** -0.5),
        "ffn_w_out": normal(ks[19], (DEPTH, D_FF, D_MODEL), D_FF ** -0.5),
    }


def reference(x, c, mod_w, mod_b, norm_mix_gain, norm_ffn_gain, ab_w_in, ab_w_out,
              hgrn_lb_logits, hgrn_out_gain, fox_q_gain, fox_k_gain, fox_f_bias,
              gla_w_in, gla_w_gate_up, gla_b_gate, gla_out_gain, gla_w_out,
              ffn_w_in, ffn_w_out):
    probs = jax.nn.softmax(hgrn_lb_logits.astype(jnp.float32), axis=0)
    lower_bounds = jnp.concatenate(
        [jnp.zeros_like(probs[:1]), jnp.cumsum(probs[1:], axis=0)], axis=0)[:N_AB]
    lower_bounds = jnp.clip(lower_bounds, 0.0, 1.0 - 1e-6)
    silu_c = jax.nn.silu(c)
    for layer in range(DEPTH):
        mod = silu_c @ mod_w[layer] + mod_b[layer]
        sh1, sc1, g1, sh2, sc2, g2 = jnp.split(mod, 6, axis=-1)
        h = modulate(rms_norm(x, norm_mix_gain[layer]), sh1, sc1)
        if layer % 2 == 0:
            i = layer // 2
            y = hgrn2_fox_mixer(h, ab_w_in[i], ab_w_out[i], lower_bounds[i], hgrn_out_gain[i],
                                fox_q_gain[i], fox_k_gain[i], fox_f_bias[i])
        else:
            j = layer // 2
            y = gla_mixer(h, gla_w_in[j], gla_w_gate_up[j], gla_b_gate[j], gla_out_gain[j], gla_w_out[j])
        x = x + g1[:, None, :] * y
        h = modulate(rms_norm(x, norm_ffn_gain[layer]), sh2, sc2)
        x = x + g2[:, None, :] * swiglu(h, ffn_w_in[layer], ffn_w_out[layer])
    return x
```

```python
import contextlib
import numpy as np
import ml_dtypes
import concourse.bass as bass
import concourse.mybir as mybir
from concourse.bass_utils import run_bass_kernel_spmd

F32 = mybir.dt.float32
BF16 = mybir.dt.bfloat16
AF = mybir.ActivationFunctionType
ALU = mybir.AluOpType
EPS = 1e-6


class Sem:
    def __init__(self, h):
        self.h = h
        self.n = 0


class Buf:
    __slots__ = ("w", "r", "ds")

    def __init__(self):
        self.w = None
        self.r = {}
        self.ds = None


class Eng:
    def __init__(self, name, sem, is_pe=False):
        self.name = name
        self.sem = sem
        self.prog = []
        self.seen = {}
        self.is_pe = is_pe


class Kern:
    def __init__(self, nc, stack, n_dma_sems=12):
        self.nc = nc
        self.stack = stack
        mk = lambda n: Sem(stack.enter_context(nc.semaphore(n)))
        self.pe = Eng("tensor", mk("s_pe"), True)
        self.act = Eng("scalar", mk("s_act"))
        self.dve = Eng("vector", mk("s_dve"))
        self.pool = Eng("gpsimd", mk("s_pool"))
        self.sp = Eng("sync", mk("s_sp"))
        self.engs = [self.pe, self.act, self.dve, self.pool, self.sp]
        self.dsems = [None] * n_dma_sems
        self.nbuf = 0
        self.ccs = [mk("s_cc%d" % i) for i in range(4)]
        self.dsems.extend(self.ccs)
        self.ncoll = 0
        self.free_sems = []
        self.phase_sems = []
        self.pstack = None
        self.nphase = 0

    def sb(self, name, shape, dt):
        st = self.pstack if self.pstack is not None else self.stack
        return st.enter_context(self.nc.sbuf_tensor("%s_%d" % (name, self.nphase), list(shape), dt))

    def ps(self, name, shape, dt):
        st = self.pstack if self.pstack is not None else self.stack
        return st.enter_context(self.nc.psum_tensor("%s_%d" % (name, self.nphase), list(shape), dt))

    def begin_phase(self):
        self.pstack = contextlib.ExitStack()
        self.nphase += 1

    def end_phase(self):
        for e in self.engs:
            self.final_wait(e)
        self.emit()
        for e in self.engs:
            e.prog = []
        self.free_sems.extend(self.phase_sems)
        self.phase_sems = []
        self.pstack.close()
        self.pstack = None

    def coll(self, kind, in_t, out_t, R, W, groups):
        eng = self.pool
        cc = self.ccs[self.ncoll % len(self.ccs)]
        self.ncoll += 1
        waits = self._waits(eng, R, W)
        if cc.n > 0 and eng.seen.get(cc, 0) < cc.n:
            eng.seen[cc] = cc.n
            waits.append((cc, cc.n))
        cc.n += 1
        ev = (cc, cc.n)
        in_ap = in_t if isinstance(in_t, bass.AP) else in_t.ap()
        out_ap = out_t if isinstance(out_t, bass.AP) else out_t.ap()
        fn = lambda e: e.collective_compute(kind, ALU.bypass, replica_groups=groups, ins=[in_ap.opt()],
                                            outs=[out_ap.opt()])
        eng.prog.append((waits, fn, (cc, 1)))
        self._commit(ev, R, W)

    def _waits(self, eng, R, W):
        needs = {}

        def need(ev):
            s, v = ev
            if needs.get(s, 0) < v:
                needs[s] = v

        for b in R:
            if b.w is not None:
                if b.w[0] is eng.sem and eng.is_pe:
                    continue
                need(b.w)
        for b in W:
            if b.w is not None and b.w[0] is not eng.sem:
                need(b.w)
            for s, v in b.r.items():
                if s is not eng.sem:
                    need((s, v))
        out = []
        for s, v in needs.items():
            if eng.seen.get(s, 0) < v:
                eng.seen[s] = v
                out.append((s, v))
        return out

    def _commit(self, ev, R, W):
        for b in R:
            if b.r.get(ev[0], 0) < ev[1]:
                b.r[ev[0]] = ev[1]
        for b in W:
            b.w = ev
            b.r = {}

    def op(self, eng, fn, R=(), W=(), inc=True):
        waits = self._waits(eng, R, W)
        if inc:
            eng.sem.n += 1
            ev = (eng.sem, eng.sem.n)
        else:
            ev = (eng.sem, eng.sem.n + 1)
        eng.prog.append((waits, fn, (eng.sem, 1) if inc else None))
        self._commit(ev, R, W)

    def dma(self, eng, fn, R, W, dsem=None, key=None):
        if key is None:
            key = W[0] if W else R[0]
        if key.ds is None:
            if self.free_sems:
                key.ds = self.free_sems.pop()
            else:
                key.ds = Sem(self.stack.enter_context(self.nc.semaphore("s_d%d" % self.nbuf)))
                self.nbuf += 1
                self.dsems.append(key.ds)
            if self.pstack is not None:
                self.phase_sems.append(key.ds)
        dsem = key.ds
        waits = self._waits(eng, R, W)
        if dsem.n > 0 and eng.seen.get(dsem, 0) < dsem.n:
            eng.seen[dsem] = dsem.n
            waits.append((dsem, dsem.n))
        dsem.n += 16
        ev = (dsem, dsem.n)
        eng.prog.append((waits, fn, (dsem, 16)))
        self._commit(ev, R, W)

    def final_wait(self, eng):
        waits = []
        for s in [e.sem for e in self.engs] + [d for d in self.dsems if d is not None]:
            if s is not eng.sem and s.n > 0 and eng.seen.get(s, 0) < s.n:
                eng.seen[s] = s.n
                waits.append((s, s.n))
        eng.prog.append((waits, None, None))

    def emit(self):
        nc = self.nc

        def replay(eng):
            def run(e):
                for waits, fn, inc in eng.prog:
                    for s, v in waits:
                        e.wait_ge(s.h, v)
                    if fn is not None:
                        ins = fn(e)
                        if inc is not None:
                            ins.then_inc(inc[0].h, inc[1])
            return run

        with nc.Block() as block:
            block.tensor(replay(self.pe))
            block.scalar(replay(self.act))
            block.vector(replay(self.dve))
            block.gpsimd(replay(self.pool))
            block.sync(replay(self.sp))


def _chunks(n, m):
    return [(i, min(m, n - i)) for i in range(0, n, m)]


class Consts:
    def __init__(self, K, cdram, dsem):
        nc = K.nc
        self.f = K.sb("c_f32", [128, 5 * 128], F32)
        self.b = K.sb("c_bf16", [128, 5 * 128], BF16)
        self.buf = Buf()
        f, b = self.f, self.b
        K.dma(K.sp, lambda e: e.dma_start(out=f[:], in_=cdram[:, :]), [], [self.buf], dsem)
        K.op(K.dve, lambda e: e.tensor_copy(out=b[:], in_=f[:]), [self.buf], [self.buf])

    def ident_b(self, n=128):
        return self.b[0:n, 0:n]

    def tri_b(self, n=128):
        return self.b[0:n, 128:128 + n]

    def tri_f(self, n=128):
        return self.f[0:n, 128:128 + n]

    def ones_f(self, p=128, n=128):
        return self.f[0:p, 256:256 + n]

    def ones_b(self, p=128, n=128):
        return self.b[0:p, 256:256 + n]

    def sel_f(self):
        return self.f[:, 384:512]

    def ident_f(self):
        return self.f[:, 0:128]

    def negmask_b(self):
        return self.b[:, 512:640]


def make_consts():
    c = np.zeros((128, 5 * 128), np.float32)
    c[:, 0:128] = np.eye(128)
    c[:, 128:256] = np.triu(np.ones((128, 128)))
    c[:, 256:384] = 1.0
    c[127, 384:512] = 1.0
    c[:, 512:640] = -30000.0 * np.tril(np.ones((128, 128)), -1)
    return c


def emit_norm(K, C, xT, xbuf, KD, D, G, Sh, vbuf, outT, obuf, ssbank, ssb, tmp):
    sq, sqb, rs, rsb, tm, tmb = tmp["sq"], tmp["sqb"], tmp["rs"], tmp["rsb"], tmp["tm"], tmp["tmb"]
    for j in range(KD):
        s = j % 2
        K.op(K.act, lambda e, j=j, s=s: e.activation(out=sq[:, s, :], in_=xT[:, j, :], func=AF.Square),
             [xbuf], [sqb[s]])
        K.op(K.pe, lambda e, j=j, s=s: e.matmul(ssbank[:, :], lhsT=C.ones_f(), rhs=sq[:, s, :],
                                               start=(j == 0), stop=(j == KD - 1)),
             [sqb[s], C.buf], [ssb], inc=True)
    K.op(K.act, lambda e: e.activation(out=rs[:, :], in_=ssbank[:, :], func=AF.Sqrt, scale=1.0 / D, bias=EPS),
         [ssb], [rsb])
    K.op(K.dve, lambda e: e.reciprocal(out=rs[:, :], in_=rs[:, :]), [rsb], [rsb])
    for j in range(KD):
        s = j % 2
        K.op(K.dve, lambda e, j=j, s=s: e.scalar_tensor_tensor(out=tm[:, s, :], in0=xT[:, j, :], scalar=G[:, j:j + 1],
                                                             in1=rs[:, :], op0=ALU.mult, op1=ALU.mult),
             [xbuf, rsb, vbuf], [tmb[s]])
        K.op(K.act, lambda e, j=j, s=s: e.activation(out=outT[:, j, :], in_=tm[:, s, :], func=AF.Identity,
                                                   bias=Sh[:, j:j + 1], scale=1.0),
             [tmb[s], vbuf], [obuf])


def emit_mod(K, C, cfg, cT_d, modw_d, modb_d, gmix_d, gffn_d, vec_d, out_buf=None):
    L, D, KD = cfg["L"], cfg["D"], cfg["KD"]
    W6 = 6 * D
    CG = min(2048, W6)
    NB = CG // 512
    assert W6 % CG == 0
    ds_a, ds_w = K.dsems[0], K.dsems[1]
    cT = K.sb("m_cT", [128, KD], F32)
    gm = K.sb("m_gm", [128, L, KD], F32)
    gf = K.sb("m_gf", [128, L, KD], F32)
    row = K.sb("m_row", [1, W6], F32)
    brow = K.sb("m_brow", [1, W6], F32)
    wp = K.sb("m_wp", [128, 3, CG], F32)
    mt = K.sb("m_mt", [128, 6 * KD], F32)
    vec = K.sb("m_vec", [128, L, 6 * KD], F32)
    banks = [K.ps("m_ps%d" % i, [128, 512], F32) for i in range(NB)]
    tp = K.ps("m_tp", [128, 512], F32)
    b_c, b_g, b_row, b_brow, b_mt, b_vec, b_tp = Buf(), Buf(), Buf(), Buf(), Buf(), Buf(), Buf()
    b_wp = [Buf() for _ in range(3)]
    b_bk = [Buf() for _ in range(NB)]
    K.dma(K.sp, lambda e: e.dma_start(out=cT[:], in_=cT_d[:, :]), [], [b_c], ds_a)
    K.dma(K.sp, lambda e: e.dma_start(out=gm[:], in_=gmix_d[:, :, :]), [], [b_g], ds_a)
    K.dma(K.sp, lambda e: e.dma_start(out=gf[:], in_=gffn_d[:, :, :]), [], [b_g], ds_a)
    K.op(K.act, lambda e: e.activation(out=cT[:], in_=cT[:], func=AF.Silu), [b_c], [b_c])
    ip = 0
    for l in range(L):
        K.dma(K.sp, lambda e, l=l: e.dma_start(out=brow[:], in_=modb_d[l:l + 1, :]), [], [b_brow], ds_a)
        for cg in range(W6 // CG):
            for k in range(KD):
                s = ip % 3
                ip += 1
                K.dma(K.sp, lambda e, l=l, cg=cg, k=k, s=s: e.dma_start(
                    out=wp[:, s, :], in_=modw_d[l, k * 128:(k + 1) * 128, cg * CG:(cg + 1) * CG]),
                    [], [b_wp[s]], ds_w)
                for q in range(NB):
                    K.op(K.pe, lambda e, k=k, s=s, q=q: e.matmul(
                        banks[q][0:1, :], lhsT=cT[:, k:k + 1], rhs=wp[:, s, q * 512:(q + 1) * 512],
                        start=(k == 0), stop=(k == KD - 1)), [b_c, b_wp[s]], [b_bk[q]])
            for q in range(NB):
                c0 = cg * CG + q * 512
                K.op(K.dve, lambda e, q=q, c0=c0: e.tensor_tensor(
                    out=row[0:1, c0:c0 + 512], in0=banks[q][0:1, :], in1=brow[0:1, c0:c0 + 512], op=ALU.add),
                    [b_bk[q], b_brow], [b_row])
        for c in range(6 * KD):
            K.op(K.pe, lambda e, c=c: e.matmul(tp[:, c:c + 1], lhsT=row[0:1, c * 128:(c + 1) * 128],
                                               rhs=C.ones_f(1, 1), start=True, stop=True),
                 [b_row, C.buf], [b_tp])
        K.op(K.act, lambda e: e.activation(out=mt[:, :], in_=tp[:, 0:6 * KD], func=AF.Identity), [b_tp], [b_mt])
        for (dst, src_sc, gn) in ((0, 1, gm), (3, 4, gf)):
            K.op(K.dve, lambda e, l=l, dst=dst, src_sc=src_sc, gn=gn: e.scalar_tensor_tensor(
                out=vec[:, l, dst * KD:(dst + 1) * KD], in0=mt[:, src_sc * KD:(src_sc + 1) * KD], scalar=1.0,
                in1=gn[:, l, :], op0=ALU.add, op1=ALU.mult), [b_mt, b_g], [b_vec])
        for (dst, src) in ((1, 0), (2, 2), (4, 3), (5, 5)):
            K.op(K.dve, lambda e, l=l, dst=dst, src=src: e.tensor_copy(
                out=vec[:, l, dst * KD:(dst + 1) * KD], in_=mt[:, src * KD:(src + 1) * KD]), [b_mt], [b_vec])
    K.dma(K.sp, lambda e: e.dma_start(out=vec_d[:, :, :], in_=vec[:]), [b_vec], [out_buf] if out_buf else [], key=b_vec)


def norm_scratch(K, pfx):
    return {
        "sq": K.sb(pfx + "sq", [128, 2, 512], F32), "sqb": [Buf(), Buf()],
        "rs": K.sb(pfx + "rs", [128, 512], F32), "rsb": Buf(),
        "tm": K.sb(pfx + "tm", [128, 2, 512], F32), "tmb": [Buf(), Buf()],
    }


def emit_norm0(K, C, cfg, xT_d, vec_d, hT_d, h_store=None, after_tile=None):
    D, KD, T = cfg["D"], cfg["KD"], cfg["T"]
    ds_a = K.dsems[0]
    vec = K.sb("n_vec", [128, 6 * KD], F32)
    xT = K.sb("n_xT", [128, 2, KD, 512], F32)
    hT = K.sb("n_hT", [128, 2, KD, 512], BF16)
    ss = K.ps("n_ss", [128, 512], F32)
    b_v, b_ss = Buf(), Buf()
    b_x, b_h = [Buf(), Buf()], [Buf(), Buf()]
    tmp = norm_scratch(K, "n_")
    K.dma(K.sp, lambda e: e.dma_start(out=vec[:], in_=vec_d[:, 0, :]), [], [b_v], ds_a)
    xv = xT_d.rearrange("(k p) t -> p k t", p=128)
    hv = None if h_store is not None else hT_d.rearrange("(k p) t -> p k t", p=128)
    for t in range(T // 512):
        s = t % 2
        K.dma(K.sp, lambda e, t=t, s=s: e.dma_start(out=xT[:, s], in_=xv[:, :, t * 512:(t + 1) * 512]),
              [], [b_x[s]], ds_a)
        emit_norm(K, C, xT[:, s], b_x[s], KD, D, vec[:, 0:KD], vec[:, KD:2 * KD], b_v, hT[:, s], b_h[s], ss, b_ss, tmp)
        if h_store is not None:
            h_store(t, hT[:, s], b_h[s])
            if after_tile is not None:
                after_tile()
        else:
            K.dma(K.sp, lambda e, t=t, s=s: e.dma_start(out=hv[:, :, t * 512:(t + 1) * 512], in_=hT[:, s]),
                  [b_h[s]], [], ds_a)


def emit_dense(K, C, cfg, l, xT_d, oT_d, wo_d, wfi_d, wfo_d, vec_d, xTo_d, hTo_d, o_load=None, h_store=None,
               after_proj=None):
    D, KD, T, DFF, KF, KO, L = cfg["D"], cfg["KD"], cfg["T"], cfg["DFF"], cfg["KF"], cfg["KO"], cfg["L"]
    DG = D // 512 if D >= 512 else 1
    GW = min(512, D)
    GC = GW // 128
    FG = DFF // 512
    assert DFF % 512 == 0
    ds_a, ds_w, ds_s = K.dsems[0], K.dsems[1], K.dsems[2]
    last = hTo_d is None and h_store is None
    vec = K.sb("d_vec", [128, 2, 6 * KD], F32)
    xT = K.sb("d_xT", [128, KD, 512], F32)
    oT = K.sb("d_oT", [128, max(KO, KD), 512], BF16)
    aT = K.sb("d_aT", [128, KF, 512], BF16)
    sa = K.sb("d_sa", [128, 2, 512], BF16)
    WB = 4
    WSZ = max(KO * GW, KD * 512, 16 * GW)
    wb = K.sb("d_wb", [128, WB, WSZ], BF16)
    banks = [K.ps("d_ps%d" % i, [128, 512], F32) for i in range(8)]
    b_bk = [Buf() for _ in range(8)]
    b_v, b_x, b_o, b_a = Buf(), Buf(), Buf(), Buf()
    b_sa = [Buf(), Buf()]
    b_wb = [Buf() for _ in range(WB)]
    tmp = norm_scratch(K, "d_")
    K.dma(K.sp, lambda e: e.dma_start(out=vec[:, 0, :], in_=vec_d[:, l, :]), [], [b_v], ds_a)
    if not last:
        K.dma(K.sp, lambda e: e.dma_start(out=vec[:, 1, :], in_=vec_d[:, l + 1, :]), [], [b_v], ds_a)
    g1 = vec[:, 0, 2 * KD:3 * KD]
    G2 = vec[:, 0, 3 * KD:4 * KD]
    Sh2 = vec[:, 0, 4 * KD:5 * KD]
    g2 = vec[:, 0, 5 * KD:6 * KD]
    G1n = vec[:, 1, 0:KD]
    Sh1n = vec[:, 1, KD:2 * KD]
    xv = xT_d.rearrange("(k p) t -> p k t", p=128)
    ov = None if o_load is not None else oT_d.rearrange("(k p) t -> p k t", p=128)
    xov = xTo_d.rearrange("(k p) t -> p k t", p=128)
    hov = None if (last or h_store is not None) else hTo_d.rearrange("(k p) t -> p k t", p=128)
    wov = wo_d.rearrange("(k p) c -> p k c", p=128)
    wiv = wfi_d.rearrange("(k p) c -> p k c", p=128)
    wfv = wfo_d.rearrange("(k p) c -> p k c", p=128)
    st = {"w": 0, "bank": 0}

    def load_w(src_ap, kk, cc):
        s = st["w"] % WB
        st["w"] += 1
        dst = wb[:, s, 0:kk * cc].rearrange("p (k c) -> p k c", k=kk)
        K.dma(K.pool, lambda e: e.dma_start(out=dst, in_=src_ap), [], [b_wb[s]], ds_w)
        return dst, b_wb[s]

    def proj_group(pieces, rhsT, rbuf, nk, og, gvec):
        base = (st["bank"] % 2) * 4
        st["bank"] += 1
        kdone = 0
        for (wt, wbuf, k0, kn) in pieces:
            for kk in range(kn):
                for i in range(GC):
                    K.op(K.pe, lambda e, wt=wt, kk=kk, i=i, k=k0 + kk: e.matmul(
                        banks[base + i][:, :], lhsT=wt[:, kk, i * 128:(i + 1) * 128], rhs=rhsT[:, k, :],
                        start=(k == 0), stop=(k == nk - 1)), [wbuf, rbuf], [b_bk[base + i]],
                        inc=(k0 + kk == nk - 1 or kk == kn - 1))
        for i in range(GC):
            j = og * GC + i
            K.op(K.dve, lambda e, i=i, j=j: e.scalar_tensor_tensor(
                out=xT[:, j, :], in0=banks[base + i][:, :], scalar=gvec[:, j:j + 1], in1=xT[:, j, :],
                op0=ALU.mult, op1=ALU.add), [b_bk[base + i], b_x, b_v], [b_x])

    for t in range(T // 512):
        tsl = slice(t * 512, (t + 1) * 512)
        K.dma(K.sp, lambda e, tsl=tsl: e.dma_start(out=xT[:], in_=xv[:, :, tsl]), [], [b_x], ds_a)
        if o_load is not None:
            o_load(t, oT, b_o)
        else:
            K.dma(K.sp, lambda e, tsl=tsl: e.dma_start(out=oT[:, 0:KO, :], in_=ov[:, :, tsl]), [], [b_o], ds_a)
        for og in range(DG):
            wt, wbuf = load_w(wov[:, :, og * GW:(og + 1) * GW], KO, GW)
            proj_group([(wt, wbuf, 0, KO)], oT, b_o, KO, og, g1)
        if after_proj is not None:
            after_proj()
        emit_norm(K, C, xT, b_x, KD, D, G2, Sh2, b_v, oT, b_o, banks[7], b_bk[7], tmp)
        for fg in range(FG):
            wa, wab = load_w(wiv[:, :, fg * 512:(fg + 1) * 512], KD, 512)
            wu, wub = load_w(wiv[:, :, DFF + fg * 512:DFF + (fg + 1) * 512], KD, 512)
            for i in range(4):
                j = fg * 4 + i
                ba = (2 * j) % 6
                bu = ba + 1
                for (wt, wbf, bk) in ((wa, wab, ba), (wu, wub, bu)):
                    for k in range(KD):
                        K.op(K.pe, lambda e, wt=wt, k=k, i=i, bk=bk: e.matmul(
                            banks[bk][:, :], lhsT=wt[:, k, i * 128:(i + 1) * 128], rhs=oT[:, k, :],
                            start=(k == 0), stop=(k == KD - 1)), [wbf, b_o], [b_bk[bk]], inc=(k == KD - 1))
                s = j % 2
                K.op(K.act, lambda e, s=s, ba=ba: e.activation(out=sa[:, s, :], in_=banks[ba][:, :], func=AF.Silu),
                     [b_bk[ba]], [b_sa[s]])
                K.op(K.dve, lambda e, s=s, j=j, bu=bu: e.tensor_tensor(
                    out=aT[:, j, :], in0=sa[:, s, :], in1=banks[bu][:, :], op=ALU.mult),
                    [b_sa[s], b_bk[bu]], [b_a])
        for og in range(DG):
            pieces = []
            for (k0, kn) in _chunks(KF, 16):
                wt, wbuf = load_w(wfv[:, k0:k0 + kn, og * GW:(og + 1) * GW], kn, GW)
                pieces.append((wt, wbuf, k0, kn))
            proj_group(pieces, aT, b_a, KF, og, g2)
        K.dma(K.sp, lambda e, tsl=tsl: e.dma_start(out=xov[:, :, tsl], in_=xT[:]), [b_x], [], ds_s)
        if not last:
            emit_norm(K, C, xT, b_x, KD, D, G1n, Sh1n, b_v, oT, b_o, banks[7], b_bk[7], tmp)
            if h_store is not None:
                h_store(t, oT, b_o)
            else:
                K.dma(K.sp, lambda e, tsl=tsl: e.dma_start(out=hov[:, :, tsl], in_=oT[:, 0:KD, :]), [b_o], [], ds_s)


def _new():
    nc = bass.Bass("TRN2", target_bir_lowering=False)
    stack = contextlib.ExitStack()
    K = Kern(nc, stack)
    return nc, stack, K


def _din(nc, name, shape, dt=F32):
    return nc.dram_tensor(name, list(shape), dt, kind="ExternalInput").ap()


def _dout(nc, name, shape, dt=F32):
    return nc.dram_tensor(name, list(shape), dt, kind="ExternalOutput").ap()


def _finish(K, stack):
    K.final_wait(K.sp)
    K.emit()
    stack.close()


def build_mod(cfg):
    L, D, KD = cfg["L"], cfg["D"], cfg["KD"]
    nc, stack, K = _new()
    cst = _din(nc, "consts", [128, 640])
    cT = _din(nc, "cT", [128, KD])
    modw = _din(nc, "modw", [L, D, 6 * D])
    modb = _din(nc, "modb", [L, 6 * D])
    gmix = _din(nc, "gmix", [128, L, KD])
    gffn = _din(nc, "gffn", [128, L, KD])
    vec = _dout(nc, "vec", [128, L, 6 * KD])
    C = Consts(K, cst, K.dsems[0])
    emit_mod(K, C, cfg, cT, modw, modb, gmix, gffn, vec)
    _finish(K, stack)
    return nc


def build_norm0(cfg):
    L, D, KD, T = cfg["L"], cfg["D"], cfg["KD"], cfg["T"]
    nc, stack, K = _new()
    cst = _din(nc, "consts", [128, 640])
    xT = _din(nc, "xT", [D, T])
    vec = _din(nc, "vec", [128, L, 6 * KD])
    hT = _dout(nc, "hT", [D, T], BF16)
    C = Consts(K, cst, K.dsems[0])
    emit_norm0(K, C, cfg, xT, vec, hT)
    _finish(K, stack)
    return nc


def build_dense(cfg, last):
    L, D, KD, T, DFF, KO = cfg["L"], cfg["D"], cfg["KD"], cfg["T"], cfg["DFF"], cfg["KO"]
    nc, stack, K = _new()
    cst = _din(nc, "consts", [128, 640])
    xT = _din(nc, "xT", [D, T])
    oT = _din(nc, "oT", [KO * 128, T], BF16)
    wo = _din(nc, "wo", [KO * 128, D])
    wfi = _din(nc, "wfi", [D, 2 * DFF])
    wfo = _din(nc, "wfo", [DFF, D])
    vec = _din(nc, "vec", [128, 2, 6 * KD])
    xTo = _dout(nc, "xTo", [D, T])
    hTo = None if last else _dout(nc, "hTo", [D, T], BF16)
    C = Consts(K, cst, K.dsems[0])
    cfg2 = dict(cfg)
    cfg2["L"] = 2
    emit_dense(K, C, cfg2, 0, xT, oT, wo, wfi, wfo, vec, xTo, hTo)
    _finish(K, stack)
    return nc


def emit_hgrn(K, C, cfg, ab_idx, hT_d, w_d, lbl_d, gain_d, oT_d, h_load=None, o_store=None):
    D, KD, S = cfg["D"], cfg["KD"], cfg["S"]
    ds_a, ds_w, ds_s = K.dsems[0], K.dsems[1], K.dsems[2]
    CH, NCH = 64, 8
    W = K.sb("h_W", [128, KD, 1024], BF16)
    hT = K.sb("h_hT", [128, 2, KD, 512], BF16)
    lbl = K.sb("h_lbl", [128, 2, 2], F32)
    lbv = K.sb("h_lbv", [128, 3, 2], F32)
    gain = K.sb("h_gain", [64, 256], F32)
    ones = K.sb("h_ones", [128, 64], F32)
    names = ["sig", "lf", "key", "qs", "cum", "cm", "cl", "e0", "e1", "e2"]
    hf = [{n: K.sb("h_%s%d" % (n, h), [128, 512], F32) for n in names} for h in range(2)]
    hb = [{n: K.sb("h_%s%d" % (n, h), [128, 512], BF16) for n in ["qh", "qt", "kt", "kh", "k0"]} for h in range(2)]
    eL = [K.sb("h_eL%d" % h, [128, NCH], F32) for h in range(2)]
    Sst = [K.sb("h_S%d" % h, [128, 128], F32) for h in range(2)]
    Sbf = [K.sb("h_Sb%d" % h, [128, 2, 128], BF16) for h in range(2)]
    vc = K.sb("h_vc", [64, 2, 256], BF16)
    gs = K.sb("h_gs", [64, 2, 256], F32)
    PT = K.sb("h_PT", [64, 2, 2, 64], BF16)
    khs = K.sb("h_khs", [64, 2, 2, 128], BF16)
    junk = K.sb("h_junk", [64, 128], F32)
    ssq = K.sb("h_ssq", [64, 2, 2], F32)
    og = K.sb("h_og", [64, 2, 128], BF16)
    oTt = K.sb("h_oTt", [128, 2, 2, 512], BF16)
    fb2 = [K.ps("h_fb%d" % i, [128, 512], F32) for i in range(2)]
    fb = [fb2[0], fb2[1], fb2[0], fb2[1]]
    tb = [K.ps("h_tb%d" % i, [128, 512], F32) for i in range(2)]
    bkO = K.ps("h_bkO", [128, 512], F32)
    bkU = K.ps("h_bkU", [128, 512], F32)
    bkT = [K.ps("h_bkT%d" % i, [128, 1024], BF16) for i in range(2)]
    B = lambda n: [Buf() for _ in range(n)]
    b_W, b_lb, b_gain, b_ones = Buf(), Buf(), Buf(), Buf()
    b_hT, b_tb = B(2), B(2)
    b_fb2 = B(2)
    b_fb = [b_fb2[0], b_fb2[1], b_fb2[0], b_fb2[1]]
    b_hf = [{n: Buf() for n in names} for _ in range(2)]
    b_hb = [{n: Buf() for n in ["qh", "qt", "kt", "kh", "k0"]} for _ in range(2)]
    b_eL, b_S, b_Sbf = B(2), B(2), [B(2), B(2)]
    b_vc, b_gs, b_PT, b_khs, b_junk, b_ssq, b_og = B(2), B(2), B(2), B(2), Buf(), B(2), B(2)
    b_bkO, b_bkU, b_bkT = Buf(), Buf(), B(2)
    b_oTt = B(2)
    A = K.op
    wv = w_d.rearrange("(k p) c -> p k c", p=128)
    for half in range(2):
        K.dma(K.pool, lambda e, half=half: e.dma_start(out=W[:, :, half * 512:(half + 1) * 512],
                                                       in_=wv[:, :, half * 512:(half + 1) * 512]), [], [b_W], ds_w)
    K.dma(K.sp, lambda e: e.dma_start(out=lbl[:], in_=lbl_d[:, :, :]), [], [b_lb], ds_a)
    K.dma(K.sp, lambda e: e.dma_start(out=gain[:], in_=gain_d[0:1, :].partition_broadcast(64)), [], [b_gain], ds_a)
    A(K.dve, lambda e: e.memset(ones[:], 1.0), [], [b_ones])
    if ab_idx == 0:
        A(K.dve, lambda e: e.memset(lbv[:, 0, :], 0.0), [b_lb], [b_lb])
    else:
        A(K.dve, lambda e: e.tensor_tensor(out=lbv[:, 0, :], in0=lbl[:, 1, :], in1=lbl[:, 0, :], op=ALU.subtract),
          [b_lb], [b_lb])
        A(K.act, lambda e: e.activation(out=lbv[:, 0, :], in_=lbv[:, 0, :], func=AF.Sigmoid), [b_lb], [b_lb])
        A(K.dve, lambda e: e.tensor_scalar(out=lbv[:, 0, :], in0=lbv[:, 0, :], scalar1=1.0 - 1e-6, scalar2=0.0,
                                           op0=ALU.min, op1=ALU.max), [b_lb], [b_lb])
    A(K.dve, lambda e: e.tensor_scalar(out=lbv[:, 1, :], in0=lbv[:, 0, :], scalar1=-1.0, scalar2=1.0,
                                       op0=ALU.mult, op1=ALU.add), [b_lb], [b_lb])
    A(K.dve, lambda e: e.tensor_scalar(out=lbv[:, 2, :], in0=lbv[:, 0, :], scalar1=1.0, scalar2=-1.0,
                                       op0=ALU.mult, op1=ALU.add), [b_lb], [b_lb])
    for h in range(2):
        A(K.dve, lambda e, h=h: e.memset(PT[:, h], 0.0), [], [b_PT[h]])
        A(K.dve, lambda e, h=h: e.memset(hb[h]["k0"][:], 0.0), [], [b_hb[h]["k0"]])
        A(K.dve, lambda e, h=h: e.memset(hf[h]["e2"][:], 0.0), [], [b_hf[h]["e2"]])
        A(K.dve, lambda e, h=h: e.memset(Sst[h][:], 0.0), [], [b_S[h]])
        A(K.dve, lambda e, h=h: e.memset(Sbf[h][:, 0, :], 0.0), [], [b_Sbf[h][0]])
    hv = None if h_load is not None else hT_d.rearrange("(k p) t -> p k t", p=128)
    ov = None if o_store is not None else oT_d.rearrange("(h p) t -> p h t", p=128)
    cidx = 0
    for t in range(S // 512):
        s = t % 2
        if h_load is not None:
            h_load(t, hT[:, s], b_hT[s])
        else:
            K.dma(K.sp, lambda e, t=t, s=s: e.dma_start(out=hT[:, s], in_=hv[:, :, t * 512:(t + 1) * 512]),
                  [], [b_hT[s]], ds_a)
        for h in range(2):
            for blk in (2 * h, 2 * h + 1):
                for k in range(KD):
                    A(K.pe, lambda e, blk=blk, k=k, s=s: e.matmul(fb[blk][:, :], lhsT=W[:, k, blk * 128:(blk + 1) * 128],
                                                                  rhs=hT[:, s, k, :], start=(k == 0), stop=(k == KD - 1)),
                      [b_W, b_hT[s]], [b_fb[blk]], inc=(k == KD - 1))
            f, bf_, bb, bbf = hf[h], hb[h], b_hf[h], b_hb[h]
            lb_, oml, noml = lbv[:, 0, h:h + 1], lbv[:, 1, h:h + 1], lbv[:, 2, h:h + 1]
            A(K.act, lambda e, f=f, h=h: e.activation(out=f["sig"][:], in_=fb[2 * h + 1][:, :], func=AF.Sigmoid),
              [b_fb[2 * h + 1]], [bb["sig"]])
            A(K.act, lambda e, f=f, h=h: e.activation(out=f["qs"][:], in_=fb[2 * h][:, :], func=AF.Silu),
              [b_fb[2 * h]], [bb["qs"]])
            A(K.dve, lambda e, f=f, oml=oml, lb_=lb_: e.tensor_scalar(out=f["lf"][:], in0=f["sig"][:], scalar1=oml,
                                                                     scalar2=lb_, op0=ALU.mult, op1=ALU.add),
              [bb["sig"], b_lb], [bb["lf"]])
            A(K.dve, lambda e, f=f: e.tensor_scalar_max(out=f["lf"][:], in0=f["lf"][:], scalar1=1e-30),
              [bb["lf"]], [bb["lf"]])
            A(K.act, lambda e, f=f: e.activation(out=f["lf"][:], in_=f["lf"][:], func=AF.Ln), [bb["lf"]], [bb["lf"]])
            A(K.dve, lambda e, f=f, oml=oml, noml=noml: e.tensor_scalar(out=f["key"][:], in0=f["sig"][:], scalar1=noml,
                                                                       scalar2=oml, op0=ALU.mult, op1=ALU.add),
              [bb["sig"], b_lb], [bb["key"]])
            for c in range(NCH):
                A(K.dve, lambda e, f=f, c=c: e.tensor_tensor_scan(out=f["cum"][:, c * CH:(c + 1) * CH], data0=ones[:, :],
                                                                  data1=f["lf"][:, c * CH:(c + 1) * CH], initial=0.0,
                                                                  op0=ALU.mult, op1=ALU.add),
                  [bb["lf"], b_ones], [bb["cum"]])
            c3 = lambda a: a[:].rearrange("p (c t) -> p c t", c=NCH)
            A(K.dve, lambda e, f=f: e.tensor_tensor(out=c3(f["cm"]), in0=c3(f["cum"]),
                                                    in1=c3(f["cum"])[:, :, 31:32].broadcast_to([128, NCH, CH]),
                                                    op=ALU.subtract), [bb["cum"]], [bb["cm"]])
            A(K.dve, lambda e, f=f: e.tensor_tensor(out=c3(f["cl"]), in0=c3(f["cum"]),
                                                    in1=c3(f["cum"])[:, :, CH - 1:CH].broadcast_to([128, NCH, CH]),
                                                    op=ALU.subtract), [bb["cum"]], [bb["cl"]])
            A(K.act, lambda e, f=f, h=h: e.activation(out=eL[h][:, :], in_=c3(f["cum"])[:, :, CH - 1], func=AF.Exp),
              [bb["cum"]], [b_eL[h]])
            A(K.act, lambda e, f=f: e.activation(out=c3(f["e2"])[:, :, 0:32], in_=c3(f["cum"])[:, :, 0:32], func=AF.Exp,
                                                 scale=-1.0), [bb["cum"]], [bb["e2"]])
            A(K.dve, lambda e, f=f, bf_=bf_: e.tensor_tensor(out=c3(bf_["k0"])[:, :, 0:32], in0=c3(f["key"])[:, :, 0:32],
                                                             in1=c3(f["e2"])[:, :, 0:32], op=ALU.mult),
              [bb["key"], bb["e2"]], [bbf["k0"]])
            for (src, sc, es, mul, dst) in (("cum", 1.0, "e0", "qs", "qh"), ("cm", 1.0, "e1", "qs", "qt"),
                                            ("cm", -1.0, "e0", "key", "kt"), ("cl", -1.0, "e1", "key", "kh")):
                A(K.act, lambda e, f=f, src=src, sc=sc, es=es: e.activation(out=f[es][:], in_=f[src][:], func=AF.Exp,
                                                                           scale=sc), [bb[src]], [bb[es]])
                A(K.dve, lambda e, f=f, bf_=bf_, es=es, mul=mul, dst=dst: e.tensor_tensor(
                    out=bf_[dst][:], in0=f[mul][:], in1=f[es][:], op=ALU.mult), [bb[mul], bb[es]], [bbf[dst]])
        def prep(c, s=s):
            csl = slice(c * CH, (c + 1) * CH)
            c0_ = c * CH
            par = c % 2
            for k in range(KD):
                A(K.pe, lambda e, k=k: e.matmul(tb[par][0:CH, :], lhsT=hT[:, s, k, csl], rhs=W[:, k, 512:1024],
                                                start=(k == 0), stop=(k == KD - 1)),
                  [b_W, b_hT[s]], [b_tb[par]], inc=(k == KD - 1))
            A(K.act, lambda e: e.activation(out=vc[:, par, :], in_=tb[par][0:CH, 0:256], func=AF.Identity),
              [b_tb[par]], [b_vc[par]])
            A(K.act, lambda e: e.activation(out=gs[:, par, :], in_=tb[par][0:CH, 256:512], func=AF.Silu),
              [b_tb[par]], [b_gs[par]])
            A(K.dve, lambda e: e.tensor_tensor(out=gs[:, par, :], in0=gs[:, par, :], in1=gain[:, :], op=ALU.mult),
              [b_gs[par], b_gain], [b_gs[par]])
            for h in range(2):
                bf_, bbf = hb[h], b_hb[h]
                A(K.pe, lambda e, h=h, bf_=bf_: e.matmul(fb2[par][0:64, h * 64 + 32:h * 64 + 64], lhsT=bf_["kt"][:, csl],
                                                         rhs=bf_["qt"][:, c0_ + 32:c0_ + 64], start=True, stop=True),
                  [bbf["kt"], bbf["qt"]], [b_fb2[par]], inc=False)
                A(K.pe, lambda e, h=h, bf_=bf_: e.matmul(fb2[par][0:32, h * 64:h * 64 + 32], lhsT=bf_["k0"][:, c0_:c0_ + 32],
                                                         rhs=bf_["qh"][:, c0_:c0_ + 32], start=True, stop=True),
                  [bbf["k0"], bbf["qh"]], [b_fb2[par]], inc=False)
                A(K.pe, lambda e, h=h, bf_=bf_: e.transpose(bkT[par][0:64, h * 128:(h + 1) * 128], bf_["kh"][:, csl],
                                                            C.ident_b()), [bbf["kh"], C.buf], [b_bkT[par]], inc=(h == 1))
            scv = fb2[par][0:64, 0:128].rearrange("p (h t) -> p h t", h=2)
            A(K.dve, lambda e: e.tensor_tensor(out=PT[:, par, :, 32:64], in0=scv[:, :, 32:64],
                                               in1=C.f[0:64, 160:192].unsqueeze(1).broadcast_to([64, 2, 32]),
                                               op=ALU.mult), [b_fb2[par], C.buf], [b_PT[par]])
            A(K.dve, lambda e: e.tensor_tensor(out=PT[0:32, par, :, 0:32], in0=scv[0:32, :, 0:32],
                                               in1=C.f[0:32, 128:160].unsqueeze(1).broadcast_to([32, 2, 32]),
                                               op=ALU.mult), [b_fb2[par], C.buf], [b_PT[par]])
            A(K.act, lambda e: e.activation(out=khs[:, par].rearrange("p h d -> p (h d)"), in_=bkT[par][0:64, 0:256],
                                            func=AF.Identity), [b_bkT[par]], [b_khs[par]])

        def main(c, s=s):
            csl = slice(c * CH, (c + 1) * CH)
            par = c % 2
            sp_, sn = c % 2, (c + 1) % 2
            for h in range(2):
                bf_, bbf = hb[h], b_hb[h]
                A(K.pe, lambda e, h=h: e.matmul(bkO[0:64, h * 128:(h + 1) * 128], lhsT=PT[:, par, h, :],
                                                rhs=vc[:, par, h * 128:(h + 1) * 128], start=True, stop=False),
                  [b_PT[par], b_vc[par]], [b_bkO], inc=False)
                A(K.pe, lambda e, h=h, bf_=bf_: e.matmul(bkO[0:64, h * 128:(h + 1) * 128], lhsT=bf_["qh"][:, csl],
                                                         rhs=Sbf[h][:, sp_, :], start=False, stop=True),
                  [bbf["qh"], b_Sbf[h][sp_]], [b_bkO], inc=(h == 1))
            for h in range(2):
                A(K.pe, lambda e, h=h: e.matmul(bkU[:, h * 128:(h + 1) * 128], lhsT=khs[:, par, h, :],
                                                rhs=vc[:, par, h * 128:(h + 1) * 128], start=True, stop=True),
                  [b_khs[par], b_vc[par]], [b_bkU], inc=(h == 1))
            for h in range(2):
                A(K.dve, lambda e, h=h: e.scalar_tensor_tensor(out=Sst[h][:], in0=Sst[h][:], scalar=eL[h][:, c:c + 1],
                                                               in1=bkU[:, h * 128:(h + 1) * 128], op0=ALU.mult,
                                                               op1=ALU.add), [b_S[h], b_eL[h], b_bkU], [b_S[h]])
                A(K.act, lambda e, h=h: e.activation(out=Sbf[h][:, sn, :], in_=Sst[h][:], func=AF.Identity),
                  [b_S[h]], [b_Sbf[h][sn]])

        def post_a(c, s=s):
            par = c % 2
            for h in range(2):
                A(K.act, lambda e, h=h: e.activation(out=junk[:, :], in_=bkO[0:64, h * 128:(h + 1) * 128], func=AF.Square,
                                                     accum_out=ssq[:, h, 0:1]), [b_bkO], [b_junk, b_ssq[h]])
                A(K.act, lambda e, h=h: e.activation(out=ssq[:, h, 1:2], in_=ssq[:, h, 0:1], func=AF.Sqrt,
                                                     scale=1.0 / 128, bias=EPS), [b_ssq[h]], [b_ssq[h]])
                A(K.dve, lambda e, h=h: e.reciprocal(out=ssq[:, h, 1:2], in_=ssq[:, h, 1:2]), [b_ssq[h]], [b_ssq[h]])
                A(K.dve, lambda e, h=h: e.scalar_tensor_tensor(out=og[:, h, :], in0=bkO[0:64, h * 128:(h + 1) * 128],
                                                               scalar=ssq[:, h, 1:2],
                                                               in1=gs[:, par, h * 128:(h + 1) * 128],
                                                               op0=ALU.mult, op1=ALU.mult),
                  [b_bkO, b_ssq[h], b_gs[par]], [b_og[h]])

        def post_b(c, s=s):
            csl = slice(c * CH, (c + 1) * CH)
            par = (c + 1) % 2
            for h in range(2):
                A(K.pe, lambda e, h=h: e.transpose(bkT[par][:, 256 + h * 64:256 + (h + 1) * 64], og[:, h, :],
                                                   C.ident_b(64)), [b_og[h], C.buf], [b_bkT[par]], inc=(h == 1))
            A(K.act, lambda e: e.activation(out=oTt[:, s, :, csl],
                                            in_=bkT[par][:, 256:384].rearrange("p (h t) -> p h t", h=2),
                                            func=AF.Identity), [b_bkT[par]], [b_oTt[s]])

        prep(0)
        for c in range(NCH):
            if c + 1 < NCH:
                prep(c + 1)
            main(c)
            if c >= 1:
                post_b(c - 1)
            post_a(c)
        post_b(NCH - 1)
        if o_store is not None:
            o_store(t, oTt[:, s], b_oTt[s], 2)
        else:
            K.dma(K.pool, lambda e, t=t, s=s: e.dma_start(out=ov[:, :, t * 512:(t + 1) * 512], in_=oTt[:, s]),
                  [b_oTt[s]], [], ds_s)


def build_hgrn(cfg, ab_idx):
    D, S = cfg["D"], cfg["S"]
    nc, stack, K = _new()
    cst = _din(nc, "consts", [128, 640])
    hT = _din(nc, "hT", [D, S], BF16)
    w = _din(nc, "w", [D, 1024])
    lbl = _din(nc, "lbl", [128, 2, 2])
    gain = _din(nc, "gain", [1, 256])
    oT = _dout(nc, "oT", [256, S], BF16)
    C = Consts(K, cst, K.dsems[0])
    emit_hgrn(K, C, cfg, ab_idx, hT, w, lbl, gain, oT)
    _finish(K, stack)
    return nc


def emit_fox(K, C, cfg, hT_d, w_d, qkg_d, fbias_d, oT_d, h_load=None, o_store=None):
    D, KD, S = cfg["D"], cfg["KD"], cfg["S"]
    NB = S // 128
    A = K.op
    W = K.sb("x_W", [128, KD, 1026], BF16)
    hT = K.sb("x_hT", [128, 2, KD, 512], BF16)
    kTa = K.sb("x_kTa", [128, 2, S], BF16)
    va = K.sb("x_va", [128, 2, NB, 132], BF16)
    Ga = K.sb("x_Ga", [128, 2, NB], F32)
    qkg = K.sb("x_qkg", [128, 2], F32)
    nfb = K.sb("x_nfb", [128, 2], F32)
    qn = K.sb("x_qn", [128, 2, 512], BF16)
    sq = K.sb("x_sq", [128, 2, 512], F32)
    rt = K.sb("x_rt", [128, 2, 512], F32)
    sg = K.sb("x_sg", [128, 4, 256], F32)
    spv = K.sb("x_spv", [128, 2, 4], F32)
    lcs = K.sb("x_lcs", [128, 2, 4], F32)
    tots = K.sb("x_tots", [128, 2, 4], F32)
    incl = K.sb("x_incl", [128, 2, 2, 4], F32)
    excl = K.sb("x_excl", [128, 2, 4], F32)
    Bq = K.sb("x_Bq", [128, 2, NB], F32)
    PT = K.sb("x_PT", [128, 3, 512], BF16)
    rden = K.sb("x_rden", [128, 4], F32)
    dqc = K.sb("x_dqc", [128, 2, 4], F32)
    dqb = K.sb("x_dqb", [1, 2, 512], BF16)
    b_dqc, b_dqb = Buf(), Buf()
    og = K.sb("x_og", [128, 2, 128], BF16)
    oTt = K.sb("x_oTt", [128, 2, 2, 512], BF16)
    bk = [K.ps("x_bk%d" % i, [128, 512], F32) for i in range(7)]
    bkT = K.ps("x_bkT", [128, 1024], BF16)
    b_bk = [Buf() for _ in range(7)]
    b_T = Buf()
    P0, P1, N_, V_, Fb, O3, P2 = range(7)
    OB = [N_, V_, Fb, O3]
    B = lambda n: [Buf() for _ in range(n)]
    b_W, b_kTa, b_va, b_Ga, b_par = Buf(), Buf(), Buf(), Buf(), Buf()
    b_hT, b_qn, b_sq, b_rt, b_oTt = B(2), B(2), B(2), B(2), B(2)
    b_sg, b_spv, b_lcs, b_tots, b_excl = Buf(), Buf(), Buf(), Buf(), Buf()
    b_incl, b_Bq = B(2), Buf()
    b_PT = B(3)
    b_rden, b_og = Buf(), B(2)
    wv = w_d.rearrange("(k p) c -> p k c", p=128)
    K.dma(K.pool, lambda e: e.dma_start(out=W[:, :, 0:512], in_=wv[:, :, 0:512]), [], [b_W])
    K.dma(K.pool, lambda e: e.dma_start(out=W[:, :, 512:1026], in_=wv[:, :, 512:1026]), [], [b_W])
    K.dma(K.sp, lambda e: e.dma_start(out=qkg[:], in_=qkg_d[:, :]), [], [b_par])
    K.dma(K.sp, lambda e: e.dma_start(out=nfb[:], in_=fbias_d[0:1, :].partition_broadcast(128)), [], [b_par])
    A(K.dve, lambda e: e.tensor_scalar(out=qkg[:, 0:1], in0=qkg[:, 0:1], scalar1=float(128 ** -0.5), scalar2=None,
                                       op0=ALU.mult), [b_par], [b_par])
    A(K.dve, lambda e: e.tensor_scalar(out=nfb[:, :], in0=nfb[:, :], scalar1=-1.0, scalar2=None, op0=ALU.mult),
      [b_par], [b_par])
    A(K.dve, lambda e: e.memset(va[:], 1.0), [], [b_va])
    hv = None if h_load is not None else hT_d.rearrange("(k p) t -> p k t", p=128)
    ov = None if o_store is not None else oT_d.rearrange("(h p) t -> p h t", p=128)
    xs = 0
    for t in range(S // 512):
        s = t % 2
        tsl = slice(t * 512, (t + 1) * 512)
        if h_load is not None:
            h_load(t, hT[:, s], b_hT[s])
        else:
            K.dma(K.sp, lambda e, s=s, tsl=tsl: e.dma_start(out=hT[:, s], in_=hv[:, :, tsl]), [], [b_hT[s]])
        for h in range(2):
            for (which, bank) in ((0, P0), (1, P1)):
                blk = 2 * h + which
                for k in range(KD):
                    A(K.pe, lambda e, blk=blk, k=k, s=s, bank=bank: e.matmul(
                        bk[bank][:, :], lhsT=W[:, k, blk * 128:(blk + 1) * 128], rhs=hT[:, s, k, :],
                        start=(k == 0), stop=(k == KD - 1)), [b_W, b_hT[s]], [b_bk[bank]], inc=(k == KD - 1))
                x = which
                A(K.act, lambda e, x=x, bank=bank: e.activation(out=sq[:, x, :], in_=bk[bank][:, :], func=AF.Square),
                  [b_bk[bank]], [b_sq[x]])
                A(K.pe, lambda e, x=x: e.matmul(bk[N_][:, :], lhsT=C.ones_f(), rhs=sq[:, x, :], start=True, stop=True),
                  [b_sq[x], C.buf], [b_bk[N_]])
                A(K.act, lambda e, x=x: e.activation(out=rt[:, x, :], in_=bk[N_][:, :], func=AF.Sqrt, scale=1.0 / 128,
                                                     bias=EPS), [b_bk[N_]], [b_rt[x]])
                A(K.dve, lambda e, x=x: e.reciprocal(out=rt[:, x, :], in_=rt[:, x, :]), [b_rt[x]], [b_rt[x]])
                if which == 0:
                    A(K.dve, lambda e, h=h, x=x, bank=bank: e.scalar_tensor_tensor(
                        out=qn[:, h, :], in0=bk[bank][:, :], scalar=qkg[:, 0:1], in1=rt[:, x, :], op0=ALU.mult,
                        op1=ALU.mult), [b_bk[bank], b_par, b_rt[x]], [b_qn[h]])
                else:
                    A(K.dve, lambda e, h=h, x=x, bank=bank, tsl=tsl: e.scalar_tensor_tensor(
                        out=kTa[:, h, tsl], in0=bk[bank][:, :], scalar=qkg[:, 1:2], in1=rt[:, x, :], op0=ALU.mult,
                        op1=ALU.mult), [b_bk[bank], b_par, b_rt[x]], [b_kTa])
        for b in range(4):
            blk = 4 * t + b
            bsl = slice(b * 128, (b + 1) * 128)
            for k in range(KD):
                A(K.pe, lambda e, k=k, s=s, bsl=bsl: e.matmul(bk[V_][:, :], lhsT=hT[:, s, k, bsl], rhs=W[:, k, 512:1024],
                                                              start=(k == 0), stop=(k == KD - 1)),
                  [b_W, b_hT[s]], [b_bk[V_]], inc=(k == KD - 1))
            A(K.act, lambda e, blk=blk: e.activation(out=va[:, :, blk, 0:128],
                                                     in_=bk[V_][:, 0:256].rearrange("p (h e) -> p h e", h=2),
                                                     func=AF.Identity), [b_bk[V_]], [b_va])
            A(K.act, lambda e, b=b: e.activation(out=sg[:, b, :], in_=bk[V_][:, 256:512], func=AF.Sigmoid),
              [b_bk[V_]], [b_sg])
            for k in range(KD):
                A(K.pe, lambda e, k=k, s=s, bsl=bsl, b=b: e.matmul(bk[Fb][:, 2 * b:2 * b + 2], lhsT=hT[:, s, k, bsl],
                                                                   rhs=W[:, k, 1024:1026], start=(k == 0),
                                                                   stop=(k == KD - 1)),
                  [b_W, b_hT[s]], [b_bk[Fb]], inc=(k == KD - 1))
        fbv = bk[Fb][:, 0:8].rearrange("p (b h) -> p h b", h=2)
        for h in range(2):
            A(K.act, lambda e, h=h: e.activation(out=spv[:, h, :], in_=fbv[:, h, :], func=AF.Exp, scale=-1.0,
                                                 bias=nfb[:, h:h + 1]), [b_bk[Fb], b_par], [b_spv])
        A(K.act, lambda e: e.activation(out=spv[:], in_=spv[:], func=AF.Ln, bias=1.0, scale=1.0), [b_spv], [b_spv])
        A(K.pe, lambda e: e.matmul(bk[Fb][:, 16:24], lhsT=C.tri_f(), rhs=spv[:].rearrange("p h b -> p (h b)"),
                                   start=True, stop=True), [b_spv, C.buf], [b_bk[Fb]])
        A(K.act, lambda e: e.activation(out=lcs[:].rearrange("p h b -> p (h b)"), in_=bk[Fb][:, 16:24],
                                        func=AF.Identity), [b_bk[Fb]], [b_lcs])
        A(K.pe, lambda e: e.matmul(bk[Fb][:, 32:40], lhsT=C.sel_f(), rhs=lcs[:].rearrange("p h b -> p (h b)"),
                                   start=True, stop=True), [b_lcs, C.buf], [b_bk[Fb]])
        A(K.act, lambda e: e.activation(out=tots[:].rearrange("p h b -> p (h b)"), in_=bk[Fb][:, 32:40],
                                        func=AF.Identity), [b_bk[Fb]], [b_tots])
        for h in range(2):
            init = 0.0 if t == 0 else incl[:, 1 - s, h, 3:4]
            A(K.dve, lambda e, h=h, s=s, init=init: e.tensor_tensor_scan(
                out=incl[:, s, h, :], data0=C.ones_f(128, 4), data1=tots[:, h, :], initial=init, op0=ALU.mult,
                op1=ALU.add), [b_tots, C.buf, b_incl[1 - s]], [b_incl[s]])
        A(K.dve, lambda e, s=s: e.tensor_tensor(out=excl[:], in0=incl[:, s], in1=tots[:], op=ALU.subtract),
          [b_incl[s], b_tots], [b_excl])
        A(K.dve, lambda e, t=t: e.tensor_tensor(out=Ga[:, :, 4 * t:4 * t + 4], in0=lcs[:], in1=excl[:], op=ALU.add),
          [b_lcs, b_excl], [b_Ga])
        nkb = 4 * t + 4
        for h in range(2):
            A(K.dve, lambda e, h=h, s=s, nkb=nkb: e.tensor_scalar(
                out=Bq[:, h, 0:nkb], in0=Ga[:, h, 0:nkb], scalar1=incl[:, s, h, 3:4], scalar2=None,
                op0=ALU.subtract), [b_Ga, b_incl[s]], [b_Bq])
            A(K.dve, lambda e, h=h, s=s, t=t: e.tensor_scalar(
                out=dqc[:, h, :], in0=Ga[:, h, 4 * t:4 * t + 4], scalar1=incl[:, s, h, 3:4], scalar2=-1.0,
                op0=ALU.subtract, op1=ALU.mult), [b_Ga, b_incl[s]], [b_dqc])
        for h in range(2):
            for jl in range(4):
                A(K.pe, lambda e, h=h, jl=jl: e.transpose(bk[O3][0:1, jl * 128:(jl + 1) * 128], dqc[:, h, jl:jl + 1],
                                                          C.ident_f()), [b_dqc, C.buf], [b_bk[O3]])
            A(K.act, lambda e, h=h: e.activation(out=dqb[0:1, h, :], in_=bk[O3][0:1, :], func=AF.Identity),
              [b_bk[O3]], [b_dqb])
        SB = (P0, P1, P2)
        for h in range(2):
            def score(kb, h=h):
                m = max(0, kb - 4 * t)
                x3 = kb % 3
                sb_ = SB[x3]
                diag = kb >= 4 * t
                A(K.pe, lambda e: e.matmul(bk[sb_][:, m * 128:512], lhsT=kTa[:, h, kb * 128:(kb + 1) * 128],
                                           rhs=qn[:, h, m * 128:512], start=True, stop=False),
                  [b_kTa, b_qn[h]], [b_bk[sb_]], inc=False)
                A(K.pe, lambda e: e.matmul(bk[sb_][:, m * 128:512], lhsT=C.ones_b(1, 128), rhs=dqb[0:1, h, m * 128:512],
                                           start=False, stop=(not diag)), [b_dqb, C.buf], [b_bk[sb_]], inc=(not diag))
                if diag:
                    A(K.pe, lambda e: e.matmul(bk[sb_][:, m * 128:(m + 1) * 128], lhsT=C.ident_b(), rhs=C.negmask_b(),
                                               start=False, stop=True), [C.buf], [b_bk[sb_]])
                A(K.act, lambda e: e.activation(out=PT[:, x3, m * 128:512], in_=bk[sb_][:, m * 128:512], func=AF.Exp,
                                                bias=Bq[:, h, kb:kb + 1], scale=1.0), [b_bk[sb_], b_Bq], [b_PT[x3]])

            def pv(kb, h=h):
                m = max(0, kb - 4 * t)
                x3 = kb % 3
                for jl in range(m, 4):
                    j = 4 * t + jl
                    A(K.pe, lambda e, jl=jl, j=j: e.matmul(bk[OB[jl]][:, 0:129], lhsT=PT[:, x3, jl * 128:(jl + 1) * 128],
                                                           rhs=va[:, h, kb, 0:129], start=(kb == 0), stop=(kb == j)),
                      [b_PT[x3], b_va], [b_bk[OB[jl]]], inc=(jl == 3 or kb == j))
            score(0)
            score(1)
            for kb in range(2, nkb):
                score(kb)
                pv(kb - 2)
            pv(nkb - 2)
            pv(nkb - 1)
            for jl in range(4):
                A(K.dve, lambda e, jl=jl: e.reciprocal(out=rden[:, jl:jl + 1], in_=bk[OB[jl]][:, 128:129]),
                  [b_bk[OB[jl]]], [b_rden])
                x = jl % 2
                A(K.dve, lambda e, jl=jl, h=h, x=x: e.scalar_tensor_tensor(
                    out=og[:, x, :], in0=bk[OB[jl]][:, 0:128], scalar=rden[:, jl:jl + 1],
                    in1=sg[:, jl, h * 128:(h + 1) * 128], op0=ALU.mult, op1=ALU.mult),
                    [b_bk[OB[jl]], b_rden, b_sg], [b_og[x]])
                A(K.pe, lambda e, x=x: e.transpose(bkT[:, 0:128], og[:, x, :], C.ident_b()), [b_og[x], C.buf], [b_T])
                A(K.act, lambda e, s=s, h=h, jl=jl: e.activation(out=oTt[:, s, h, jl * 128:(jl + 1) * 128],
                                                                 in_=bkT[:, 0:128], func=AF.Identity),
                  [b_T], [b_oTt[s]])
        if o_store is not None:
            o_store(t, oTt[:, s], b_oTt[s], 2)
        else:
            K.dma(K.pool, lambda e, s=s, tsl=tsl: e.dma_start(out=ov[:, :, tsl], in_=oTt[:, s]), [b_oTt[s]], [])


def build_fox(cfg):
    D, S = cfg["D"], cfg["S"]
    nc, stack, K = _new()
    cst = _din(nc, "consts", [128, 640])
    hT = _din(nc, "hT", [D, S], BF16)
    w = _din(nc, "w", [D, 1026])
    qkg = _din(nc, "qkg", [128, 2])
    fbias = _din(nc, "fbias", [1, 2])
    oT = _dout(nc, "oT", [256, S], BF16)
    C = Consts(K, cst, None)
    emit_fox(K, C, cfg, hT, w, qkg, fbias, oT)
    _finish(K, stack)
    return nc


def emit_gla(K, C, cfg, hT_d, w_d, wgu_d, bg_d, gain_d, oT_d, h_load=None, o_store=None):
    D, KD, S = cfg["D"], cfg["KD"], cfg["S"]
    A = K.op
    CH, NCH = 128, 4
    NW = 1552
    W = K.sb("g_W", [128, KD, NW], BF16)
    hT = K.sb("g_hT", [128, 2, KD, 512], BF16)
    wgu = K.sb("g_wgu", [16, 256], F32)
    bg = K.sb("g_bg", [128, 2], F32)
    gain = K.sb("g_gain", [128, 512], F32)
    ones = K.sb("g_ones", [128, 128], F32)
    glT = K.sb("g_glT", [16, 512], F32)
    spt = K.sb("g_sp", [128, 2, 512], F32)
    csp = K.sb("g_csp", [128, 2, 512], F32)
    Eq = K.sb("g_Eq", [128, 2, 512], F32)
    Ek = K.sb("g_Ek", [128, 2, 512], F32)
    eL = K.sb("g_eL", [128, 2, NCH], F32)
    qh = K.sb("g_qh", [128, 2, 512], BF16)
    kt = K.sb("g_kt", [128, 2, 512], BF16)
    kh = K.sb("g_kh", [128, 2, 512], BF16)
    Sst = K.sb("g_S", [128, 2, 512], F32)
    Sbf = K.sb("g_Sbf", [128, 2, 2, 512], BF16)
    vc = K.sb("g_vc", [128, 2, 512], BF16)
    gs = K.sb("g_gs", [128, 2, 512], F32)
    PT = K.sb("g_PT", [128, 2, 128], BF16)
    khs = K.sb("g_khs", [128, 2, 256], BF16)
    junk = K.sb("g_junk", [128, 512], F32)
    ssq = K.sb("g_ssq", [128, 2], F32)
    og = K.sb("g_og", [128, 512], BF16)
    oTt = K.sb("g_oTt", [128, 2, 4, 512], BF16)
    bk = [K.ps("g_bk%d" % i, [128, 512], F32) for i in range(6)]
    bkT = K.ps("g_bkT", [128, 1024], BF16)
    bkT2 = K.ps("g_bkT2", [128, 1024], BF16)
    b_bk = [Buf() for _ in range(6)]
    b_T, b_T2 = Buf(), Buf()
    GA, Z0, Z1, V_, Gt, O_ = range(6)
    B = lambda n: [Buf() for _ in range(n)]
    b_W, b_par, b_ones, b_glT = Buf(), Buf(), Buf(), Buf()
    b_hT, b_sp, b_csp, b_Eq, b_Ek, b_eL = B(2), B(2), B(2), B(2), B(2), B(2)
    b_qh, b_kt, b_kh, b_S = B(2), B(2), B(2), B(2)
    b_Sbf = [B(2), B(2)]
    b_vc, b_gs, b_PT, b_khs, b_oTt = B(2), B(2), B(2), B(2), B(2)
    b_junk, b_ssq, b_og = Buf(), Buf(), Buf()
    wv = w_d.rearrange("(k p) c -> p k c", p=128)
    for (c0, c1) in ((0, 528), (528, 1040), (1040, 1552)):
        K.dma(K.pool, lambda e, c0=c0, c1=c1: e.dma_start(out=W[:, :, c0:c1], in_=wv[:, :, c0:c1]), [], [b_W])
    K.dma(K.sp, lambda e: e.dma_start(out=wgu[:], in_=wgu_d[:, :]), [], [b_par])
    K.dma(K.sp, lambda e: e.dma_start(out=bg[:], in_=bg_d[:, :]), [], [b_par])
    K.dma(K.sp, lambda e: e.dma_start(out=gain[:], in_=gain_d[0:1, :].partition_broadcast(128)), [], [b_par])
    A(K.dve, lambda e: e.tensor_scalar(out=bg[:, :], in0=bg[:, :], scalar1=-1.0, scalar2=None, op0=ALU.mult),
      [b_par], [b_par])
    A(K.dve, lambda e: e.memset(ones[:], 1.0), [], [b_ones])
    for dc in range(2):
        A(K.dve, lambda e, dc=dc: e.memset(Sst[:, dc, :], 0.0), [], [b_S[dc]])
        A(K.dve, lambda e, dc=dc: e.memset(Sbf[:, dc, 0, :], 0.0), [], [b_Sbf[dc][0]])
    hv = None if h_load is not None else hT_d.rearrange("(k p) t -> p k t", p=128)
    ov = None if o_store is not None else oT_d.rearrange("(h p) t -> p h t", p=128)
    c3 = lambda a: a.rearrange("p (c t) -> p c t", c=NCH)
    cg = 0
    for t in range(S // 512):
        s = t % 2
        tsl = slice(t * 512, (t + 1) * 512)
        if h_load is not None:
            h_load(t, hT[:, s], b_hT[s])
        else:
            K.dma(K.sp, lambda e, s=s, tsl=tsl: e.dma_start(out=hT[:, s], in_=hv[:, :, tsl]), [], [b_hT[s]])
        for k in range(KD):
            A(K.pe, lambda e, k=k, s=s: e.matmul(bk[GA][0:16, :], lhsT=W[:, k, 512:528], rhs=hT[:, s, k, :],
                                                 start=(k == 0), stop=(k == KD - 1)),
              [b_W, b_hT[s]], [b_bk[GA]], inc=(k == KD - 1))
        A(K.act, lambda e: e.activation(out=glT[:, :], in_=bk[GA][0:16, :], func=AF.Identity), [b_bk[GA]], [b_glT])
        for dc in range(2):
            zb = (Z0, Z1)[dc]
            A(K.pe, lambda e, dc=dc, zb=zb: e.matmul(bk[zb][:, :], lhsT=wgu[:, dc * 128:(dc + 1) * 128], rhs=glT[:, :],
                                                     start=True, stop=True), [b_par, b_glT], [b_bk[zb]])
            A(K.act, lambda e, dc=dc, zb=zb: e.activation(out=spt[:, dc, :], in_=bk[zb][:, :], func=AF.Exp, scale=-1.0,
                                                          bias=bg[:, dc:dc + 1]), [b_bk[zb], b_par], [b_sp[dc]])
            A(K.act, lambda e, dc=dc: e.activation(out=spt[:, dc, :], in_=spt[:, dc, :], func=AF.Ln, bias=1.0,
                                                   scale=1.0), [b_sp[dc]], [b_sp[dc]])
            for c in range(NCH):
                A(K.dve, lambda e, dc=dc, c=c: e.tensor_tensor_scan(
                    out=csp[:, dc, c * CH:(c + 1) * CH], data0=ones[:, :], data1=spt[:, dc, c * CH:(c + 1) * CH],
                    initial=0.0, op0=ALU.mult, op1=ALU.add), [b_sp[dc], b_ones], [b_csp[dc]])
            A(K.act, lambda e, dc=dc: e.activation(out=Eq[:, dc, :], in_=csp[:, dc, :], func=AF.Exp, scale=-1.0 / 16),
              [b_csp[dc]], [b_Eq[dc]])
            A(K.act, lambda e, dc=dc: e.activation(out=Ek[:, dc, :], in_=csp[:, dc, :], func=AF.Exp, scale=1.0 / 16),
              [b_csp[dc]], [b_Ek[dc]])
            A(K.act, lambda e, dc=dc: e.activation(out=eL[:, dc, :], in_=c3(csp[:, dc, :])[:, :, CH - 1], func=AF.Exp,
                                                   scale=-1.0 / 16), [b_csp[dc]], [b_eL[dc]])
            for (which, dst) in ((0, "q"), (1, "k")):
                blk = which * 2 + dc
                for k in range(KD):
                    A(K.pe, lambda e, blk=blk, k=k, s=s, zb=zb: e.matmul(
                        bk[zb][:, :], lhsT=W[:, k, blk * 128:(blk + 1) * 128], rhs=hT[:, s, k, :], start=(k == 0),
                        stop=(k == KD - 1)), [b_W, b_hT[s]], [b_bk[zb]], inc=(k == KD - 1))
                if which == 0:
                    A(K.dve, lambda e, dc=dc, zb=zb: e.scalar_tensor_tensor(
                        out=qh[:, dc, :], in0=bk[zb][:, :], scalar=float(256 ** -0.5), in1=Eq[:, dc, :], op0=ALU.mult,
                        op1=ALU.mult), [b_bk[zb], b_Eq[dc]], [b_qh[dc]])
                else:
                    A(K.dve, lambda e, dc=dc, zb=zb: e.tensor_tensor(out=kt[:, dc, :], in0=bk[zb][:, :],
                                                                     in1=Ek[:, dc, :], op=ALU.mult),
                      [b_bk[zb], b_Ek[dc]], [b_kt[dc]])
                    A(K.dve, lambda e, dc=dc: e.tensor_tensor(
                        out=c3(kh[:, dc, :]), in0=c3(kt[:, dc, :]),
                        in1=eL[:, dc, :].unsqueeze(2).broadcast_to([128, NCH, CH]), op=ALU.mult),
                        [b_kt[dc], b_eL[dc]], [b_kh[dc]])
        def prep(c, s=s):
            csl = slice(c * CH, (c + 1) * CH)
            ts = c % 2
            for (bank, c0) in ((V_, 528), (Gt, 1040)):
                for k in range(KD):
                    A(K.pe, lambda e, k=k, bank=bank, c0=c0: e.matmul(
                        bk[bank][:, :], lhsT=hT[:, s, k, csl], rhs=W[:, k, c0:c0 + 512], start=(k == 0),
                        stop=(k == KD - 1)), [b_W, b_hT[s]], [b_bk[bank]], inc=(k == KD - 1))
            A(K.act, lambda e: e.activation(out=vc[:, ts, :], in_=bk[V_][:, :], func=AF.Identity),
              [b_bk[V_]], [b_vc[ts]])
            A(K.act, lambda e: e.activation(out=gs[:, ts, :], in_=bk[Gt][:, :], func=AF.Silu),
              [b_bk[Gt]], [b_gs[ts]])
            A(K.dve, lambda e: e.tensor_tensor(out=gs[:, ts, :], in0=gs[:, ts, :], in1=gain[:, :], op=ALU.mult),
              [b_gs[ts], b_par], [b_gs[ts]])
            for dc in range(2):
                A(K.pe, lambda e, dc=dc: e.matmul(bk[GA][:, 0:128], lhsT=kt[:, dc, csl], rhs=qh[:, dc, csl],
                                                  start=(dc == 0), stop=(dc == 1)),
                  [b_kt[dc], b_qh[dc]], [b_bk[GA]], inc=(dc == 1))
            A(K.dve, lambda e: e.tensor_tensor(out=PT[:, ts, :], in0=bk[GA][:, 0:128], in1=C.tri_f(), op=ALU.mult),
              [b_bk[GA], C.buf], [b_PT[ts]])
            for dc in range(2):
                A(K.pe, lambda e, dc=dc: e.transpose(bkT[:, dc * 128:(dc + 1) * 128], kh[:, dc, csl], C.ident_b()),
                  [b_kh[dc], C.buf], [b_T], inc=(dc == 1))
            A(K.act, lambda e: e.activation(out=khs[:, ts, :], in_=bkT[:, 0:256], func=AF.Identity),
              [b_T], [b_khs[ts]])

        def main(c, s=s):
            csl = slice(c * CH, (c + 1) * CH)
            ts = c % 2
            sp_, sn = c % 2, (c + 1) % 2
            A(K.pe, lambda e: e.matmul(bk[O_][:, :], lhsT=PT[:, ts, :], rhs=vc[:, ts, :], start=True, stop=False),
              [b_PT[ts], b_vc[ts]], [b_bk[O_]], inc=False)
            for dc in range(2):
                A(K.pe, lambda e, dc=dc: e.matmul(bk[O_][:, :], lhsT=qh[:, dc, csl], rhs=Sbf[:, dc, sp_, :], start=False,
                                                  stop=(dc == 1)),
                  [b_qh[dc], b_Sbf[dc][sp_]], [b_bk[O_]], inc=(dc == 1))
            for dc in range(2):
                ub = (Z0, Z1)[dc]
                A(K.pe, lambda e, dc=dc, ub=ub: e.matmul(bk[ub][:, :], lhsT=khs[:, ts, dc * 128:(dc + 1) * 128],
                                                         rhs=vc[:, ts, :], start=True, stop=True),
                  [b_khs[ts], b_vc[ts]], [b_bk[ub]])
            for dc in range(2):
                ub = (Z0, Z1)[dc]
                A(K.dve, lambda e, dc=dc, ub=ub: e.scalar_tensor_tensor(
                    out=Sst[:, dc, :], in0=Sst[:, dc, :], scalar=eL[:, dc, c:c + 1], in1=bk[ub][:, :], op0=ALU.mult,
                    op1=ALU.add), [b_S[dc], b_eL[dc], b_bk[ub]], [b_S[dc]])
                A(K.act, lambda e, dc=dc: e.activation(out=Sbf[:, dc, sn, :], in_=Sst[:, dc, :], func=AF.Identity),
                  [b_S[dc]], [b_Sbf[dc][sn]])

        def post_a(c, s=s):
            ts = c % 2
            A(K.act, lambda e: e.activation(out=junk[:, :], in_=bk[O_][:, :], func=AF.Square, accum_out=ssq[:, 0:1]),
              [b_bk[O_]], [b_junk, b_ssq])
            A(K.act, lambda e: e.activation(out=ssq[:, 1:2], in_=ssq[:, 0:1], func=AF.Sqrt, scale=1.0 / 512, bias=EPS),
              [b_ssq], [b_ssq])
            A(K.dve, lambda e: e.reciprocal(out=ssq[:, 1:2], in_=ssq[:, 1:2]), [b_ssq], [b_ssq])
            A(K.dve, lambda e: e.scalar_tensor_tensor(out=og[:, :], in0=bk[O_][:, :], scalar=ssq[:, 1:2],
                                                      in1=gs[:, ts, :], op0=ALU.mult, op1=ALU.mult),
              [b_bk[O_], b_ssq, b_gs[ts]], [b_og])

        def post_b(c, s=s):
            csl = slice(c * CH, (c + 1) * CH)
            for ec in range(4):
                A(K.pe, lambda e, ec=ec: e.transpose(bkT2[:, ec * 128:(ec + 1) * 128], og[:, ec * 128:(ec + 1) * 128],
                                                     C.ident_b()), [b_og, C.buf], [b_T2], inc=(ec == 3))
            A(K.act, lambda e: e.activation(out=oTt[:, s, :, csl], in_=bkT2[:, 0:512].rearrange("p (a q) -> p a q", a=4),
                                            func=AF.Identity), [b_T2], [b_oTt[s]])

        prep(0)
        for c in range(NCH):
            if c + 1 < NCH:
                prep(c + 1)
            main(c)
            if c >= 1:
                post_b(c - 1)
            post_a(c)
        post_b(NCH - 1)
        if o_store is not None:
            o_store(t, oTt[:, s], b_oTt[s], 4)
        else:
            K.dma(K.pool, lambda e, s=s, tsl=tsl: e.dma_start(out=ov[:, :, tsl], in_=oTt[:, s]), [b_oTt[s]], [])


def build_gla(cfg):
    D, S = cfg["D"], cfg["S"]
    nc, stack, K = _new()
    cst = _din(nc, "consts", [128, 640])
    hT = _din(nc, "hT", [D, S], BF16)
    w = _din(nc, "w", [D, 1552])
    wgu = _din(nc, "wgu", [16, 256])
    bg = _din(nc, "bg", [128, 2])
    gain = _din(nc, "gain", [1, 512])
    oT = _dout(nc, "oT", [512, S], BF16)
    C = Consts(K, cst, None)
    emit_gla(K, C, cfg, hT, w, wgu, bg, gain, oT)
    _finish(K, stack)
    return nc


CFG = dict(L=4, D=2048, KD=16, T=2048, S=8192, DFF=5632, KF=44, KO=16)
_PROGS = {}


def _prog(key, fn):
    if key not in _PROGS:
        _PROGS[key] = fn()
    return _PROGS[key]


def _run(nc, in_maps):
    res = run_bass_kernel_spmd(nc, in_maps, core_ids=list(range(8)))
    return res.results


def _fm(v, kd):
    v = np.asarray(v)
    lead = v.shape[:-1]
    a = v.reshape(lead + (kd, 128))
    return np.ascontiguousarray(np.moveaxis(a, -1, 0))


def kernel_unfused(x, c, mod_w, mod_b, norm_mix_gain, norm_ffn_gain, ab_w_in, ab_w_out, hgrn_lb_logits, hgrn_out_gain,
                   fox_q_gain, fox_k_gain, fox_f_bias, gla_w_in, gla_w_gate_up, gla_b_gate, gla_out_gain, gla_w_out,
                   ffn_w_in, ffn_w_out):
    cfg = CFG
    L, D, KD, T, S, DFF = cfg["L"], cfg["D"], cfg["KD"], cfg["T"], cfg["S"], cfg["DFF"]
    f32 = lambda a: np.ascontiguousarray(np.asarray(a, dtype=np.float32))
    x, c, mod_w, mod_b = f32(x), f32(c), f32(mod_w), f32(mod_b)
    consts = make_consts()
    cores = [(cid // 4, cid % 4) for cid in range(8)]

    cfg1 = dict(cfg)
    cfg1["L"] = 1
    nc = _prog("mod", lambda: build_mod(cfg1))
    ims = []
    for (b, j) in cores:
        ims.append({"consts": consts, "cT": _fm(c[b], KD), "modw": mod_w[j:j + 1], "modb": mod_b[j:j + 1],
                    "gmix": _fm(f32(norm_mix_gain)[j:j + 1], KD), "gffn": _fm(f32(norm_ffn_gain)[j:j + 1], KD)})
    r = _run(nc, ims)
    vec = [np.ascontiguousarray(np.concatenate([r[b * 4 + j]["vec"] for j in range(4)], axis=1)) for b in range(2)]

    nc = _prog("norm0", lambda: build_norm0(cfg))
    xT = [np.ascontiguousarray(x[b, j * T:(j + 1) * T, :].T) for (b, j) in cores]
    r = _run(nc, [{"consts": consts, "xT": xT[i], "vec": vec[cores[i][0]]} for i in range(8)])
    hT = [r[i]["hT"] for i in range(8)]

    for l in range(L):
        hfull = [np.ascontiguousarray(np.concatenate(hT[b * 4:b * 4 + 4], axis=1)) for b in range(2)]
        if l % 2 == 0:
            i = l // 2
            w = f32(ab_w_in[i])
            lbl_all = f32(hgrn_lb_logits)
            ims_h, ims_f = [], []
            for (b, j) in cores:
                hh = (2 * j, 2 * j + 1)
                cs = lambda base, h_: w[:, base + h_ * 128:base + (h_ + 1) * 128]
                wh = np.concatenate([cs(0, hh[0]), cs(1024, hh[0]), cs(0, hh[1]), cs(1024, hh[1]),
                                     cs(2048, hh[0]), cs(2048, hh[1]), cs(3072, hh[0]), cs(3072, hh[1])], axis=1)
                lbl = np.stack([np.stack([lbl_all[ly, h_ * 128:(h_ + 1) * 128] for h_ in hh], axis=1)
                                for ly in range(2)], axis=1)
                ims_h.append({"consts": consts, "hT": hfull[b], "w": np.ascontiguousarray(wh),
                              "lbl": f32(lbl), "gain": f32(hgrn_out_gain[i][2 * j:2 * j + 2]).reshape(1, 256)})
                wf = np.concatenate([cs(4096, hh[0]), cs(5120, hh[0]), cs(4096, hh[1]), cs(5120, hh[1]),
                                     cs(6144, hh[0]), cs(6144, hh[1]), cs(7168, hh[0]), cs(7168, hh[1]),
                                     w[:, 8192 + hh[0]:8192 + hh[0] + 1], w[:, 8192 + hh[1]:8192 + hh[1] + 1]], axis=1)
                ims_f.append({"consts": consts, "hT": hfull[b], "w": np.ascontiguousarray(wf),
                              "qkg": f32(np.stack([fox_q_gain[i], fox_k_gain[i]], axis=1)),
                              "fbias": f32(fox_f_bias[i][2 * j:2 * j + 2]).reshape(1, 2)})
            rh = _run(_prog(("hgrn", i), lambda: build_hgrn(cfg, i)), ims_h)
            rf = _run(_prog("fox", lambda: build_fox(cfg)), ims_f)
            ofull = [np.concatenate([np.concatenate([rh[b * 4 + j]["oT"], rf[b * 4 + j]["oT"]], axis=0)
                                     for j in range(4)], axis=0) for b in range(2)]
            wo_src = f32(ab_w_out[i])
            perm = []
            for j in range(4):
                for typ in range(2):
                    for hl in range(2):
                        base = typ * 1024 + (2 * j + hl) * 128
                        perm.extend(range(base, base + 128))
            wo = np.ascontiguousarray(wo_src[np.asarray(perm)])
        else:
            i = l // 2
            w = f32(gla_w_in[i])
            ims_g = []
            for (b, j) in cores:
                wg = np.concatenate([w[:, j * 256:(j + 1) * 256], w[:, 1024 + j * 256:1024 + (j + 1) * 256],
                                     w[:, 6144:6160], w[:, 2048 + j * 512:2048 + (j + 1) * 512],
                                     w[:, 4096 + j * 512:4096 + (j + 1) * 512]], axis=1)
                ims_g.append({"consts": consts, "hT": hfull[b], "w": np.ascontiguousarray(wg),
                              "wgu": f32(gla_w_gate_up[i][:, j * 256:(j + 1) * 256]),
                              "bg": f32(np.asarray(gla_b_gate[i][j * 256:(j + 1) * 256]).reshape(2, 128).T),
                              "gain": f32(gla_out_gain[i]).reshape(1, 512)})
            rg = _run(_prog("gla", lambda: build_gla(cfg)), ims_g)
            ofull = [np.concatenate([rg[b * 4 + j]["oT"] for j in range(4)], axis=0) for b in range(2)]
            wo = f32(gla_w_out[i])
        last = (l == L - 1)
        nc = _prog(("dense", last), lambda: build_dense(cfg, last))
        wfi, wfo = f32(ffn_w_in[l]), f32(ffn_w_out[l])
        ims = []
        for ci, (b, j) in enumerate(cores):
            v2 = np.zeros((128, 2, 6 * KD), np.float32)
            v2[:, 0] = vec[b][:, l]
            if not last:
                v2[:, 1] = vec[b][:, l + 1]
            ims.append({"consts": consts, "xT": xT[ci], "oT": np.ascontiguousarray(ofull[b][:, j * T:(j + 1) * T]),
                        "wo": wo, "wfi": wfi, "wfo": wfo, "vec": v2})
        r = _run(nc, ims)
        xT = [r[ci]["xTo"] for ci in range(8)]
        if not last:
            hT = [r[ci]["hTo"] for ci in range(8)]

    out = np.empty((2, S, D), np.float32)
    for ci, (b, j) in enumerate(cores):
        out[b, j * T:(j + 1) * T, :] = xT[ci].T
    return out


GROUPS = [[0, 1, 2, 3], [4, 5, 6, 7]]
PRECAST = False


def build_fused(cfg):
    L, D, KD, T, S, DFF, KO = cfg["L"], cfg["D"], cfg["KD"], cfg["T"], cfg["S"], cfg["DFF"], cfg["KO"]
    NT = T // 512
    NQ = D // 256
    nc = bass.Bass("TRN2", target_bir_lowering=False)
    gstack = contextlib.ExitStack()
    K = Kern(nc, gstack)
    cst = _din(nc, "consts", [128, 640])
    cT = _din(nc, "cT", [128, KD])
    modw = _din(nc, "modw", [1, D, 6 * D])
    modb = _din(nc, "modb", [1, 6 * D])
    gmix = _din(nc, "gmix", [128, 1, KD])
    gffn = _din(nc, "gffn", [128, 1, KD])
    xT_in = _din(nc, "xT", [D, T])
    wfi = _din(nc, "wfi", [L, D, 2 * DFF])
    wfo = _din(nc, "wfo", [L, DFF, D])
    wo = [_din(nc, "wo%d" % l, [KO * 128, D]) for l in range(L)]
    ab, gl = {}, {}
    for i in range((L + 1) // 2):
        ab[i] = dict(wh=_din(nc, "wh%d" % i, [D, 1024]), lbl=_din(nc, "lbl%d" % i, [128, 2, 2]),
                     again=_din(nc, "again%d" % i, [1, 256]), wf=_din(nc, "wf%d" % i, [D, 1026]),
                     qkg=_din(nc, "qkg%d" % i, [128, 2]), fbias=_din(nc, "fbias%d" % i, [1, 2]))
    for i in range(L // 2):
        gl[i] = dict(wg=_din(nc, "wg%d" % i, [D, 1552]), wgu=_din(nc, "wgu%d" % i, [16, 256]),
                     bg=_din(nc, "bg%d" % i, [128, 2]), ggain=_din(nc, "ggain%d" % i, [1, 512]))
    xTo = _dout(nc, "xTo", [D, T])
    vin = nc.dram_tensor("i_vin", [128, 6 * KD], F32)
    vall = nc.dram_tensor("i_vall", [4 * 128, 6 * KD], F32)
    xs = nc.dram_tensor("i_xs", [D, T], F32)
    HK = KD // 2
    hq = nc.dram_tensor("i_hq", [NT * 2, HK * 128, 512], BF16)
    hf = nc.dram_tensor("i_hf", [NT * 2, 4 * HK * 128, 512], BF16)
    OC = 2 if NT % 2 == 0 else 1
    NOC = (S // 512) // OC
    oqs = {nm: nc.dram_tensor("i_oq" + nm, [NOC, rows, OC * 512], BF16) for nm, rows in (("h", 256), ("f", 256), ("g", 512))}
    ofs = {nm: nc.dram_tensor("i_of" + nm, [NOC, 4 * rows, OC * 512], BF16)
           for nm, rows in (("h", 256), ("f", 256), ("g", 512))}
    wbo = nc.dram_tensor("i_wbo", [KO * 128, D], BF16)
    wbi = nc.dram_tensor("i_wbi", [D, 2 * DFF], BF16)
    wbf = nc.dram_tensor("i_wbf", [DFF, D], BF16)
    b_cv = [Buf() for _ in range(4)]
    conv = {"q": [], "per": 1, "n": 0}

    def conv_plan(l, n_calls):
        q = []
        if not PRECAST:
            conv["q"] = q
            return
        for (src, dst, rows, step) in ((wo[l], wbo.ap(), KO * 128, 512), (wfi[l], wbi.ap(), D, 128),
                                       (wfo[l], wbf.ap(), DFF, 512)):
            for r0 in range(0, rows, step):
                r1 = min(rows, r0 + step)
                q.append((src[r0:r1, :], dst[r0:r1, :]))
        conv["q"] = q
        conv["per"] = -(-len(q) // n_calls)

    def conv_step(n=None):
        n = conv["per"] if n is None else n
        for _ in range(min(n, len(conv["q"]))):
            src, dst = conv["q"].pop(0)
            key = b_cv[conv["n"] % 4]
            conv["n"] += 1
            K.dma(K.pool, lambda e, src=src, dst=dst: e.dma_start(out=dst, in_=src), [], [], key=key)

    b_vin, b_vall = Buf(), Buf()
    b_hq = [Buf() for _ in range(NT * 2)]
    b_hf = [Buf() for _ in range(NT * 2)]
    b_oq = {nm: [Buf() for _ in range(NOC)] for nm in "hfg"}
    b_of = {nm: [Buf() for _ in range(NOC)] for nm in "hfg"}
    vview = vall.ap().rearrange("(l p) c -> p l c", p=128)
    pending = []

    def flush_colls():
        while pending:
            pending.pop(0)()

    def h_store(t, tile, buf):
        for half in range(2):
            i = t * 2 + half
            dst = hq.ap()[i].rearrange("(k p) t -> p k t", p=128)
            K.dma(K.sp, lambda e, dst=dst, half=half: e.dma_start(out=dst, in_=tile[:, half * HK:(half + 1) * HK, :]),
                  [buf], [b_hq[i]], key=buf)
            pending.append(lambda i=i: K.coll("AllGather", hq.ap()[i], hf.ap()[i], [b_hq[i]], [b_hf[i]], GROUPS))

    def h_load(t, dst, buf):
        r, tl = t // NT, t % NT
        for half in range(2):
            i = tl * 2 + half
            src = hf.ap()[i].rearrange("(r k p) t -> p r k t", r=4, p=128)[:, r]
            K.dma(K.sp, lambda e, src=src, half=half: e.dma_start(out=dst[:, half * HK:(half + 1) * HK, :], in_=src),
                  [b_hf[i]], [buf], key=buf)

    def make_o_store(nm):
        def o_store(t, src, buf, nh):
            ci, cl = t // OC, t % OC
            dst = oqs[nm].ap()[ci].rearrange("(h p) t -> p h t", p=128)[:, :, cl * 512:(cl + 1) * 512]
            K.dma(K.pool, lambda e: e.dma_start(out=dst, in_=src), [buf], [b_oq[nm][ci]], key=buf)
            if cl == OC - 1:
                K.coll("AllGather", oqs[nm].ap()[ci], ofs[nm].ap()[ci], [b_oq[nm][ci]], [b_of[nm][ci]], GROUPS)
            conv_step()
        return o_store

    seg_cache = {}

    def make_o_load(names):
        def o_load(tt, oT, buf):
            k0 = 0
            for nm in names:
                rows = 4 * (256 if nm in "hf" else 512)
                nk = rows // 128

                def fn(e, nm=nm, nk=nk, k0=k0):
                    if "segC" not in seg_cache:
                        seg_cache["segC"] = (e.partition_id() % 4) * (NT // OC)
                    ci = seg_cache["segC"] + tt // OC
                    src = ofs[nm].ap().rearrange("c (k p) t -> p c k t", p=128)[
                        :, bass.ds(ci, 1), :, (tt % OC) * 512:(tt % OC + 1) * 512]
                    return e.dma_start(out=oT[:, k0:k0 + nk, :].rearrange("p (o k) t -> p o k t", o=1), in_=src)
                K.dma(K.sp, fn, [b for b in b_of[nm]], [buf], key=buf)
                k0 += nk
        return o_load

    C = Consts(K, cst, None)
    cfg1 = dict(cfg)
    cfg1["L"] = 1
    K.begin_phase()
    emit_mod(K, C, cfg1, cT, modw, modb, gmix, gffn, vin.ap().rearrange("p (l c) -> p l c", l=1), out_buf=b_vin)
    K.coll("AllGather", vin, vall, [b_vin], [b_vall], GROUPS)
    K.end_phase()
    K.begin_phase()
    emit_norm0(K, C, cfg, xT_in, vview, None, h_store=h_store, after_tile=flush_colls)
    flush_colls()
    K.end_phase()
    for l in range(L):
        i = l // 2
        if l % 2 == 0:
            conv_plan(l, 2 * (S // 512))
            K.begin_phase()
            emit_hgrn(K, C, cfg, i, None, ab[i]["wh"], ab[i]["lbl"], ab[i]["again"], None, h_load=h_load,
                      o_store=make_o_store("h"))
            K.end_phase()
            K.begin_phase()
            emit_fox(K, C, cfg, None, ab[i]["wf"], ab[i]["qkg"], ab[i]["fbias"], None, h_load=h_load,
                     o_store=make_o_store("f"))
            conv_step(len(conv["q"]))
            K.end_phase()
            names = "hf"
        else:
            conv_plan(l, S // 512)
            K.begin_phase()
            emit_gla(K, C, cfg, None, gl[i]["wg"], gl[i]["wgu"], gl[i]["bg"], gl[i]["ggain"], None, h_load=h_load,
                     o_store=make_o_store("g"))
            conv_step(len(conv["q"]))
            K.end_phase()
            names = "g"
        last = (l == L - 1)
        K.begin_phase()
        dw = (wbo.ap(), wbi.ap(), wbf.ap()) if PRECAST else (wo[l], wfi[l], wfo[l])
        emit_dense(K, C, cfg, l, xT_in if l == 0 else xs.ap(), None, dw[0], dw[1], dw[2], vview,
                   xTo if last else xs.ap(), None, o_load=make_o_load(names), h_store=None if last else h_store,
                   after_proj=flush_colls)
        flush_colls()
        K.end_phase()
    gstack.close()
    return nc


def dense_k_perm(kind):
    perm = []
    if kind == "ab":
        for typ in range(2):
            for r in range(4):
                for loc in range(256):
                    perm.append(r * 512 + typ * 256 + loc)
    else:
        perm = list(range(2048))
    return np.asarray(perm)


def kernel_fused(cfg, x, c, mod_w, mod_b, norm_mix_gain, norm_ffn_gain, ab_w_in, ab_w_out, hgrn_lb_logits,
                 hgrn_out_gain, fox_q_gain, fox_k_gain, fox_f_bias, gla_w_in, gla_w_gate_up, gla_b_gate, gla_out_gain,
                 gla_w_out, ffn_w_in, ffn_w_out):
    L, D, KD, T, S, DFF = cfg["L"], cfg["D"], cfg["KD"], cfg["T"], cfg["S"], cfg["DFF"]
    f32 = lambda a: np.ascontiguousarray(np.asarray(a, dtype=np.float32))
    x, c, mod_w, mod_b = f32(x), f32(c), f32(mod_w), f32(mod_b)
    consts = make_consts()
    ab_rows = []
    for j in range(4):
        for typ in range(2):
            for hl in range(2):
                base = typ * 1024 + (2 * j + hl) * 128
                ab_rows.extend(range(base, base + 128))
    ab_rows = np.asarray(ab_rows)
    wfi, wfo = f32(ffn_w_in), f32(ffn_w_out)
    shared = {"consts": consts, "wfi": wfi, "wfo": wfo}
    for l in range(L):
        i = l // 2
        if l % 2 == 0:
            shared["wo%d" % l] = np.ascontiguousarray(f32(ab_w_out[i])[ab_rows[dense_k_perm("ab")]])
        else:
            shared["wo%d" % l] = np.ascontiguousarray(f32(gla_w_out[i])[dense_k_perm("gla")])
    lbl_all = f32(hgrn_lb_logits)
    ims = []
    for cid in range(8):
        b, j = cid // 4, cid % 4
        im = dict(shared)
        im["cT"] = _fm(c[b], KD)
        im["modw"] = mod_w[j:j + 1]
        im["modb"] = mod_b[j:j + 1]
        im["gmix"] = _fm(f32(norm_mix_gain)[j:j + 1], KD)
        im["gffn"] = _fm(f32(norm_ffn_gain)[j:j + 1], KD)
        im["xT"] = np.ascontiguousarray(x[b, j * T:(j + 1) * T, :].T)
        hh = (2 * j, 2 * j + 1)
        for i in range((L + 1) // 2):
            w = f32(ab_w_in[i])
            cs = lambda base, h_: w[:, base + h_ * 128:base + (h_ + 1) * 128]
            im["wh%d" % i] = np.ascontiguousarray(np.concatenate(
                [cs(0, hh[0]), cs(1024, hh[0]), cs(0, hh[1]), cs(1024, hh[1]),
                 cs(2048, hh[0]), cs(2048, hh[1]), cs(3072, hh[0]), cs(3072, hh[1])], axis=1))
            im["lbl%d" % i] = f32(np.stack([np.stack([lbl_all[ly, h_ * 128:(h_ + 1) * 128] for h_ in hh], axis=1)
                                            for ly in range(2)], axis=1))
            im["again%d" % i] = f32(hgrn_out_gain[i][2 * j:2 * j + 2]).reshape(1, 256)
            im["wf%d" % i] = np.ascontiguousarray(np.concatenate(
                [cs(4096, hh[0]), cs(5120, hh[0]), cs(4096, hh[1]), cs(5120, hh[1]),
                 cs(6144, hh[0]), cs(6144, hh[1]), cs(7168, hh[0]), cs(7168, hh[1]),
                 w[:, 8192 + hh[0]:8192 + hh[0] + 1], w[:, 8192 + hh[1]:8192 + hh[1] + 1]], axis=1))
            im["qkg%d" % i] = f32(np.stack([fox_q_gain[i], fox_k_gain[i]], axis=1))
            im["fbias%d" % i] = f32(fox_f_bias[i][2 * j:2 * j + 2]).reshape(1, 2)
        for i in range(L // 2):
            w = f32(gla_w_in[i])
            im["wg%d" % i] = np.ascontiguousarray(np.concatenate(
                [w[:, j * 256:(j + 1) * 256], w[:, 1024 + j * 256:1024 + (j + 1) * 256], w[:, 6144:6160],
                 w[:, 2048 + j * 512:2048 + (j + 1) * 512], w[:, 4096 + j * 512:4096 + (j + 1) * 512]], axis=1))
            im["wgu%d" % i] = f32(gla_w_gate_up[i][:, j * 256:(j + 1) * 256])
            im["bg%d" % i] = f32(np.asarray(gla_b_gate[i][j * 256:(j + 1) * 256]).reshape(2, 128).T)
            im["ggain%d" % i] = f32(gla_out_gain[i]).reshape(1, 512)
        ims.append(im)
    nc = _prog("fused", lambda: build_fused(cfg))
    r = _run(nc, ims)
    out = np.empty((2, S, D), np.float32)
    for cid in range(8):
        b, j = cid // 4, cid % 4
        out[b, j * T:(j + 1) * T, :] = r[cid]["xTo"].T
    return out


def kernel(**inputs):
    return kernel_fused(CFG, **inputs)
```

```python
import contextlib
import numpy as np
import ml_dtypes
import concourse.bass as bass
import concourse.mybir as mybir
from concourse.bass_utils import run_bass_kernel_spmd

F32 = mybir.dt.float32
BF16 = mybir.dt.bfloat16
AF = mybir.ActivationFunctionType
ALU = mybir.AluOpType
EPS = 1e-6


class Sem:
    def __init__(self, h):
        self.h = h
        self.n = 0


class Buf:
    __slots__ = ("w", "r", "ds")

    def __init__(self):
        self.w = None
        self.r = {}
        self.ds = None


class Eng:
    def __init__(self, name, sem, is_pe=False):
        self.name = name
        self.sem = sem
        self.prog = []
        self.seen = {}
        self.is_pe = is_pe


class Kern:
    def __init__(self, nc, stack, n_dma_sems=12):
        self.nc = nc
        self.stack = stack
        mk = lambda n: Sem(stack.enter_context(nc.semaphore(n)))
        self.pe = Eng("tensor", mk("s_pe"), True)
        self.act = Eng("scalar", mk("s_act"))
        self.dve = Eng("vector", mk("s_dve"))
        self.pool = Eng("gpsimd", mk("s_pool"))
        self.sp = Eng("sync", mk("s_sp"))
        self.engs = [self.pe, self.act, self.dve, self.pool, self.sp]
        self.dsems = [None] * n_dma_sems
        self.nbuf = 0
        self.ccs = [mk("s_cc%d" % i) for i in range(4)]
        self.dsems.extend(self.ccs)
        self.ncoll = 0
        self.free_sems = []
        self.phase_sems = []
        self.pstack = None
        self.nphase = 0

    def sb(self, name, shape, dt):
        st = self.pstack if self.pstack is not None else self.stack
        return st.enter_context(self.nc.sbuf_tensor("%s_%d" % (name, self.nphase), list(shape), dt))

    def ps(self, name, shape, dt):
        st = self.pstack if self.pstack is not None else self.stack
        return st.enter_context(self.nc.psum_tensor("%s_%d" % (name, self.nphase), list(shape), dt))

    def begin_phase(self):
        self.pstack = contextlib.ExitStack()
        self.nphase += 1

    def end_phase(self):
        for e in self.engs:
            self.final_wait(e)
        self.emit()
        for e in self.engs:
            e.prog = []
        self.free_sems.extend(self.phase_sems)
        self.phase_sems = []
        self.pstack.close()
        self.pstack = None

    def coll(self, kind, in_t, out_t, R, W, groups):
        eng = self.pool
        cc = self.ccs[self.ncoll % len(self.ccs)]
        self.ncoll += 1
        waits = self._waits(eng, R, W)
        if cc.n > 0 and eng.seen.get(cc, 0) < cc.n:
            eng.seen[cc] = cc.n
            waits.append((cc, cc.n))
        cc.n += 1
        ev = (cc, cc.n)
        in_ap = in_t if isinstance(in_t, bass.AP) else in_t.ap()
        out_ap = out_t if isinstance(out_t, bass.AP) else out_t.ap()
        fn = lambda e: e.collective_compute(kind, ALU.bypass, replica_groups=groups, ins=[in_ap.opt()],
                                            outs=[out_ap.opt()])
        eng.prog.append((waits, fn, (cc, 1)))
        self._commit(ev, R, W)

    def _waits(self, eng, R, W):
        needs = {}

        def need(ev):
            s, v = ev
            if needs.get(s, 0) < v:
                needs[s] = v

        for b in R:
            if b.w is not None:
                if b.w[0] is eng.sem and eng.is_pe:
                    continue
                need(b.w)
        for b in W:
            if b.w is not None and b.w[0] is not eng.sem:
                need(b.w)
            for s, v in b.r.items():
                if s is not eng.sem:
                    need((s, v))
        out = []
        for s, v in needs.items():
            if eng.seen.get(s, 0) < v:
                eng.seen[s] = v
                out.append((s, v))
        return out

    def _commit(self, ev, R, W):
        for b in R:
            if b.r.get(ev[0], 0) < ev[1]:
                b.r[ev[0]] = ev[1]
        for b in W:
            b.w = ev
            b.r = {}

    def op(self, eng, fn, R=(), W=(), inc=True):
        waits = self._waits(eng, R, W)
        if inc:
            eng.sem.n += 1
            ev = (eng.sem, eng.sem.n)
        else:
            ev = (eng.sem, eng.sem.n + 1)
        eng.prog.append((waits, fn, (eng.sem, 1) if inc else None))
        self._commit(ev, R, W)

    def dma(self, eng, fn, R, W, dsem=None, key=None):
        if key is None:
            key = W[0] if W else R[0]
        if key.ds is None:
            if self.free_sems:
                key.ds = self.free_sems.pop()
            else:
                key.ds = Sem(self.stack.enter_context(self.nc.semaphore("s_d%d" % self.nbuf)))
                self.nbuf += 1
                self.dsems.append(key.ds)
            if self.pstack is not None:
                self.phase_sems.append(key.ds)
        dsem = key.ds
        waits = self._waits(eng, R, W)
        if dsem.n > 0 and eng.seen.get(dsem, 0) < dsem.n:
            eng.seen[dsem] = dsem.n
            waits.append((dsem, dsem.n))
        dsem.n += 16
        ev = (dsem, dsem.n)
        eng.prog.append((waits, fn, (dsem, 16)))
        self._commit(ev, R, W)

    def final_wait(self, eng):
        waits = []
        for s in [e.sem for e in self.engs] + [d for d in self.dsems if d is not None]:
            if s is not eng.sem and s.n > 0 and eng.seen.get(s, 0) < s.n:
                eng.seen[s] = s.n
                waits.append((s, s.n))
        eng.prog.append((waits, None, None))

    def emit(self):
        nc = self.nc

        def replay(eng):
            def run(e):
                for waits, fn, inc in eng.prog:
                    for s, v in waits:
                        e.wait_ge(s.h, v)
                    if fn is not None:
                        ins = fn(e)
                        if inc is not None:
                            ins.then_inc(inc[0].h, inc[1])
            return run

        with nc.Block() as block:
            block.tensor(replay(self.pe))
            block.scalar(replay(self.act))
            block.vector(replay(self.dve))
            block.gpsimd(replay(self.pool))
            block.sync(replay(self.sp))


def _chunks(n, m):
    return [(i, min(m, n - i)) for i in range(0, n, m)]


class Consts:
    def __init__(self, K, cdram, dsem):
        nc = K.nc
        self.f = K.sb("c_f32", [128, 5 * 128], F32)
        self.b = K.sb("c_bf16", [128, 5 * 128], BF16)
        self.buf = Buf()
        f, b = self.f, self.b
        K.dma(K.sp, lambda e: e.dma_start(out=f[:], in_=cdram[:, :]), [], [self.buf], dsem)
        K.op(K.dve, lambda e: e.tensor_copy(out=b[:], in_=f[:]), [self.buf], [self.buf])

    def ident_b(self, n=128):
        return self.b[0:n, 0:n]

    def tri_b(self, n=128):
        return self.b[0:n, 128:128 + n]

    def tri_f(self, n=128):
        return self.f[0:n, 128:128 + n]

    def ones_f(self, p=128, n=128):
        return self.f[0:p, 256:256 + n]

    def ones_b(self, p=128, n=128):
        return self.b[0:p, 256:256 + n]

    def sel_f(self):
        return self.f[:, 384:512]

    def ident_f(self):
        return self.f[:, 0:128]

    def negmask_b(self):
        return self.b[:, 512:640]


def make_consts():
    c = np.zeros((128, 5 * 128), np.float32)
    c[:, 0:128] = np.eye(128)
    c[:, 128:256] = np.triu(np.ones((128, 128)))
    c[:, 256:384] = 1.0
    c[127, 384:512] = 1.0
    c[:, 512:640] = -30000.0 * np.tril(np.ones((128, 128)), -1)
    return c


def emit_norm(K, C, xT, xbuf, KD, D, G, Sh, vbuf, outT, obuf, ssbank, ssb, tmp):
    sq, sqb, rs, rsb, tm, tmb = tmp["sq"], tmp["sqb"], tmp["rs"], tmp["rsb"], tmp["tm"], tmp["tmb"]
    for j in range(KD):
        s = j % 2
        K.op(K.act, lambda e, j=j, s=s: e.activation(out=sq[:, s, :], in_=xT[:, j, :], func=AF.Square),
             [xbuf], [sqb[s]])
        K.op(K.pe, lambda e, j=j, s=s: e.matmul(ssbank[:, :], lhsT=C.ones_f(), rhs=sq[:, s, :],
                                               start=(j == 0), stop=(j == KD - 1)),
             [sqb[s], C.buf], [ssb], inc=True)
    K.op(K.act, lambda e: e.activation(out=rs[:, :], in_=ssbank[:, :], func=AF.Sqrt, scale=1.0 / D, bias=EPS),
         [ssb], [rsb])
    K.op(K.dve, lambda e: e.reciprocal(out=rs[:, :], in_=rs[:, :]), [rsb], [rsb])
    for j in range(KD):
        s = j % 2
        K.op(K.dve, lambda e, j=j, s=s: e.scalar_tensor_tensor(out=tm[:, s, :], in0=xT[:, j, :], scalar=G[:, j:j + 1],
                                                             in1=rs[:, :], op0=ALU.mult, op1=ALU.mult),
             [xbuf, rsb, vbuf], [tmb[s]])
        K.op(K.act, lambda e, j=j, s=s: e.activation(out=outT[:, j, :], in_=tm[:, s, :], func=AF.Identity,
                                                   bias=Sh[:, j:j + 1], scale=1.0),
             [tmb[s], vbuf], [obuf])


def emit_mod(K, C, cfg, cT_d, modw_d, modb_d, gmix_d, gffn_d, vec_d, out_buf=None):
    L, D, KD = cfg["L"], cfg["D"], cfg["KD"]
    W6 = 6 * D
    CG = min(2048, W6)
    NB = CG // 512
    assert W6 % CG == 0
    ds_a, ds_w = K.dsems[0], K.dsems[1]
    cT = K.sb("m_cT", [128, KD], F32)
    gm = K.sb("m_gm", [128, L, KD], F32)
    gf = K.sb("m_gf", [128, L, KD], F32)
    row = K.sb("m_row", [1, W6], F32)
    brow = K.sb("m_brow", [1, W6], F32)
    wp = K.sb("m_wp", [128, 3, CG], F32)
    mt = K.sb("m_mt", [128, 6 * KD], F32)
    vec = K.sb("m_vec", [128, L, 6 * KD], F32)
    banks = [K.ps("m_ps%d" % i, [128, 512], F32) for i in range(NB)]
    tp = K.ps("m_tp", [128, 512], F32)
    b_c, b_g, b_row, b_brow, b_mt, b_vec, b_tp = Buf(), Buf(), Buf(), Buf(), Buf(), Buf(), Buf()
    b_wp = [Buf() for _ in range(3)]
    b_bk = [Buf() for _ in range(NB)]
    K.dma(K.sp, lambda e: e.dma_start(out=cT[:], in_=cT_d[:, :]), [], [b_c], ds_a)
    K.dma(K.sp, lambda e: e.dma_start(out=gm[:], in_=gmix_d[:, :, :]), [], [b_g], ds_a)
    K.dma(K.sp, lambda e: e.dma_start(out=gf[:], in_=gffn_d[:, :, :]), [], [b_g], ds_a)
    K.op(K.act, lambda e: e.activation(out=cT[:], in_=cT[:], func=AF.Silu), [b_c], [b_c])
    ip = 0
    for l in range(L):
        K.dma(K.sp, lambda e, l=l: e.dma_start(out=brow[:], in_=modb_d[l:l + 1, :]), [], [b_brow], ds_a)
        for cg in range(W6 // CG):
            for k in range(KD):
                s = ip % 3
                ip += 1
                K.dma(K.sp, lambda e, l=l, cg=cg, k=k, s=s: e.dma_start(
                    out=wp[:, s, :], in_=modw_d[l, k * 128:(k + 1) * 128, cg * CG:(cg + 1) * CG]),
                    [], [b_wp[s]], ds_w)
                for q in range(NB):
                    K.op(K.pe, lambda e, k=k, s=s, q=q: e.matmul(
                        banks[q][0:1, :], lhsT=cT[:, k:k + 1], rhs=wp[:, s, q * 512:(q + 1) * 512],
                        start=(k == 0), stop=(k == KD - 1)), [b_c, b_wp[s]], [b_bk[q]])
            for q in range(NB):
                c0 = cg * CG + q * 512
                K.op(K.dve, lambda e, q=q, c0=c0: e.tensor_tensor(
                    out=row[0:1, c0:c0 + 512], in0=banks[q][0:1, :], in1=brow[0:1, c0:c0 + 512], op=ALU.add),
                    [b_bk[q], b_brow], [b_row])
        for c in range(6 * KD):
            K.op(K.pe, lambda e, c=c: e.matmul(tp[:, c:c + 1], lhsT=row[0:1, c * 128:(c + 1) * 128],
                                               rhs=C.ones_f(1, 1), start=True, stop=True),
                 [b_row, C.buf], [b_tp])
        K.op(K.act, lambda e: e.activation(out=mt[:, :], in_=tp[:, 0:6 * KD], func=AF.Identity), [b_tp], [b_mt])
        for (dst, src_sc, gn) in ((0, 1, gm), (3, 4, gf)):
            K.op(K.dve, lambda e, l=l, dst=dst, src_sc=src_sc, gn=gn: e.scalar_tensor_tensor(
                out=vec[:, l, dst * KD:(dst + 1) * KD], in0=mt[:, src_sc * KD:(src_sc + 1) * KD], scalar=1.0,
                in1=gn[:, l, :], op0=ALU.add, op1=ALU.mult), [b_mt, b_g], [b_vec])
        for (dst, src) in ((1, 0), (2, 2), (4, 3), (5, 5)):
            K.op(K.dve, lambda e, l=l, dst=dst, src=src: e.tensor_copy(
                out=vec[:, l, dst * KD:(dst + 1) * KD], in_=mt[:, src * KD:(src + 1) * KD]), [b_mt], [b_vec])
    K.dma(K.sp, lambda e: e.dma_start(out=vec_d[:, :, :], in_=vec[:]), [b_vec], [out_buf] if out_buf else [], key=b_vec)


def norm_scratch(K, pfx):
    return {
        "sq": K.sb(pfx + "sq", [128, 2, 512], F32), "sqb": [Buf(), Buf()],
        "rs": K.sb(pfx + "rs", [128, 512], F32), "rsb": Buf(),
        "tm": K.sb(pfx + "tm", [128, 2, 512], F32), "tmb": [Buf(), Buf()],
    }


def emit_norm0(K, C, cfg, xT_d, vec_d, hT_d, h_store=None, after_tile=None):
    D, KD, T = cfg["D"], cfg["KD"], cfg["T"]
    ds_a = K.dsems[0]
    vec = K.sb("n_vec", [128, 6 * KD], F32)
    xT = K.sb("n_xT", [128, 2, KD, 512], F32)
    hT = K.sb("n_hT", [128, 2, KD, 512], BF16)
    ss = K.ps("n_ss", [128, 512], F32)
    b_v, b_ss = Buf(), Buf()
    b_x, b_h = [Buf(), Buf()], [Buf(), Buf()]
    tmp = norm_scratch(K, "n_")
    K.dma(K.sp, lambda e: e.dma_start(out=vec[:], in_=vec_d[:, 0, :]), [], [b_v], ds_a)
    xv = xT_d.rearrange("(k p) t -> p k t", p=128)
    hv = None if h_store is not None else hT_d.rearrange("(k p) t -> p k t", p=128)
    for t in range(T // 512):
        s = t % 2
        K.dma(K.sp, lambda e, t=t, s=s: e.dma_start(out=xT[:, s], in_=xv[:, :, t * 512:(t + 1) * 512]),
              [], [b_x[s]], ds_a)
        emit_norm(K, C, xT[:, s], b_x[s], KD, D, vec[:, 0:KD], vec[:, KD:2 * KD], b_v, hT[:, s], b_h[s], ss, b_ss, tmp)
        if h_store is not None:
            h_store(t, hT[:, s], b_h[s])
            if after_tile is not None:
                after_tile()
        else:
            K.dma(K.sp, lambda e, t=t, s=s: e.dma_start(out=hv[:, :, t * 512:(t + 1) * 512], in_=hT[:, s]),
                  [b_h[s]], [], ds_a)


def emit_dense(K, C, cfg, l, xT_d, oT_d, wo_d, wfi_d, wfo_d, vec_d, xTo_d, hTo_d, o_load=None, h_store=None,
               after_proj=None):
    D, KD, T, DFF, KF, KO, L = cfg["D"], cfg["KD"], cfg["T"], cfg["DFF"], cfg["KF"], cfg["KO"], cfg["L"]
    DG = D // 512 if D >= 512 else 1
    GW = min(512, D)
    GC = GW // 128
    FG = DFF // 512
    assert DFF % 512 == 0
    ds_a, ds_w, ds_s = K.dsems[0], K.dsems[1], K.dsems[2]
    last = hTo_d is None and h_store is None
    vec = K.sb("d_vec", [128, 2, 6 * KD], F32)
    xin = K.sb("d_xin", [128, KD, 512], F32)
    xT = K.sb("d_xT", [128, KD, 512], F32)
    oT2 = K.sb("d_oT", [128, 2, KO, 512], BF16)
    oT = K.sb("d_hT", [128, KD, 512], BF16)
    halves = [list(range(0, (FG + 1) // 2)), list(range((FG + 1) // 2, FG))]
    AH = 4 * len(halves[0])
    aT = K.sb("d_aT", [128, AH, 512], BF16)
    sa = K.sb("d_sa", [128, 2, 512], BF16)
    WB = 3
    WSZ = max(KO * GW, KD * 512, 16 * GW)
    wb = K.sb("d_wb", [128, WB, WSZ], BF16)
    banks = [K.ps("d_ps%d" % i, [128, 512], F32) for i in range(8)]
    b_bk = [Buf() for _ in range(8)]
    b_v, b_x, b_o, b_a, b_xin = Buf(), Buf(), Buf(), Buf(), Buf()
    b_o2 = [Buf(), Buf()]
    b_sa = [Buf(), Buf()]
    b_wb = [Buf() for _ in range(WB)]
    tmp = norm_scratch(K, "d_")
    K.dma(K.sp, lambda e: e.dma_start(out=vec[:, 0, :], in_=vec_d[:, l, :]), [], [b_v], ds_a)
    if not last:
        K.dma(K.sp, lambda e: e.dma_start(out=vec[:, 1, :], in_=vec_d[:, l + 1, :]), [], [b_v], ds_a)
    g1 = vec[:, 0, 2 * KD:3 * KD]
    G2 = vec[:, 0, 3 * KD:4 * KD]
    Sh2 = vec[:, 0, 4 * KD:5 * KD]
    g2 = vec[:, 0, 5 * KD:6 * KD]
    G1n = vec[:, 1, 0:KD]
    Sh1n = vec[:, 1, KD:2 * KD]
    xv = xT_d.rearrange("(k p) t -> p k t", p=128)
    ov = None if o_load is not None else oT_d.rearrange("(k p) t -> p k t", p=128)
    xov = xTo_d.rearrange("(k p) t -> p k t", p=128)
    hov = None if (last or h_store is not None) else hTo_d.rearrange("(k p) t -> p k t", p=128)
    wov = wo_d.rearrange("(k p) c -> p k c", p=128)
    wiv = wfi_d.rearrange("(k p) c -> p k c", p=128)
    wfv = wfo_d.rearrange("(k p) c -> p k c", p=128)
    st = {"w": 0, "bank": 0}

    def load_w(src_ap, kk, cc):
        s = st["w"] % WB
        st["w"] += 1
        dst = wb[:, s, 0:kk * cc].rearrange("p (k c) -> p k c", k=kk)
        K.dma(K.pool, lambda e: e.dma_start(out=dst, in_=src_ap), [], [b_wb[s]], ds_w)
        return dst, b_wb[s]

    def proj_group(pieces, rhsT, rbuf, nk, og, gvec, src=None, sbuf=None, kofs=0):
        if src is None:
            src, sbuf = xT, b_x
        base = (st["bank"] % 2) * 4
        st["bank"] += 1
        kdone = 0
        for (wt, wbuf, k0, kn) in pieces:
            for kk in range(kn):
                for i in range(GC):
                    K.op(K.pe, lambda e, wt=wt, kk=kk, i=i, k=k0 + kk - kofs: e.matmul(
                        banks[base + i][:, :], lhsT=wt[:, kk, i * 128:(i + 1) * 128], rhs=rhsT[:, k, :],
                        start=(k == 0), stop=(k == nk - 1)), [wbuf, rbuf], [b_bk[base + i]],
                        inc=(k0 + kk - kofs == nk - 1 or kk == kn - 1))
        for i in range(GC):
            j = og * GC + i
            K.op(K.dve, lambda e, i=i, j=j: e.scalar_tensor_tensor(
                out=xT[:, j, :], in0=banks[base + i][:, :], scalar=gvec[:, j:j + 1], in1=src[:, j, :],
                op0=ALU.mult, op1=ALU.add), [b_bk[base + i], sbuf, b_v], [b_x])

    NTL = T // 512

    def loads(t):
        tsl = slice(t * 512, (t + 1) * 512)
        K.dma(K.sp, lambda e: e.dma_start(out=xin[:], in_=xv[:, :, tsl]), [], [b_xin])
        if o_load is not None:
            o_load(t, oT2[:, t % 2], b_o2[t % 2])
        else:
            K.dma(K.sp, lambda e: e.dma_start(out=oT2[:, t % 2], in_=ov[:, :, tsl]), [], [b_o2[t % 2]])

    loads(0)
    for t in range(NTL):
        tsl = slice(t * 512, (t + 1) * 512)
        for og in range(DG):
            wt, wbuf = load_w(wov[:, :, og * GW:(og + 1) * GW], KO, GW)
            proj_group([(wt, wbuf, 0, KO)], oT2[:, t % 2], b_o2[t % 2], KO, og, g1, src=xin, sbuf=b_xin)
        if t + 1 < NTL:
            loads(t + 1)
        if after_proj is not None:
            after_proj()
        emit_norm(K, C, xT, b_x, KD, D, G2, Sh2, b_v, oT, b_o, banks[7], b_bk[7], tmp)
        for hg in halves:
            if not hg:
                continue
            for fg in hg:
                wa, wab = load_w(wiv[:, :, fg * 512:(fg + 1) * 512], KD, 512)
                wu, wub = load_w(wiv[:, :, DFF + fg * 512:DFF + (fg + 1) * 512], KD, 512)
                for i in range(4):
                    j = fg * 4 + i
                    jl = j - hg[0] * 4
                    ba = (2 * j) % 6
                    bu = ba + 1
                    for (wt, wbf, bk) in ((wa, wab, ba), (wu, wub, bu)):
                        for k in range(KD):
                            K.op(K.pe, lambda e, wt=wt, k=k, i=i, bk=bk: e.matmul(
                                banks[bk][:, :], lhsT=wt[:, k, i * 128:(i + 1) * 128], rhs=oT[:, k, :],
                                start=(k == 0), stop=(k == KD - 1)), [wbf, b_o], [b_bk[bk]], inc=(k == KD - 1))
                    s_ = j % 2
                    K.op(K.act, lambda e, s_=s_, ba=ba: e.activation(out=sa[:, s_, :], in_=banks[ba][:, :], func=AF.Silu),
                         [b_bk[ba]], [b_sa[s_]])
                    K.op(K.dve, lambda e, s_=s_, jl=jl, bu=bu: e.tensor_tensor(
                        out=aT[:, jl, :], in0=sa[:, s_, :], in1=banks[bu][:, :], op=ALU.mult),
                        [b_sa[s_], b_bk[bu]], [b_a])
            kbase, nkh = hg[0] * 4, len(hg) * 4
            for og in range(DG):
                pieces = []
                for (k0, kn) in _chunks(nkh, 16):
                    wt, wbuf = load_w(wfv[:, kbase + k0:kbase + k0 + kn, og * GW:(og + 1) * GW], kn, GW)
                    pieces.append((wt, wbuf, kbase + k0, kn))
                proj_group(pieces, aT, b_a, nkh, og, g2, kofs=kbase)
        K.dma(K.sp, lambda e, tsl=tsl: e.dma_start(out=xov[:, :, tsl], in_=xT[:]), [b_x], [], ds_s)
        if not last:
            emit_norm(K, C, xT, b_x, KD, D, G1n, Sh1n, b_v, oT, b_o, banks[7], b_bk[7], tmp)
            if h_store is not None:
                h_store(t, oT, b_o)
            else:
                K.dma(K.sp, lambda e, tsl=tsl: e.dma_start(out=hov[:, :, tsl], in_=oT[:, 0:KD, :]), [b_o], [], ds_s)


def _new():
    nc = bass.Bass("TRN2", target_bir_lowering=False)
    stack = contextlib.ExitStack()
    K = Kern(nc, stack)
    return nc, stack, K


def _din(nc, name, shape, dt=F32):
    return nc.dram_tensor(name, list(shape), dt, kind="ExternalInput").ap()


def _dout(nc, name, shape, dt=F32):
    return nc.dram_tensor(name, list(shape), dt, kind="ExternalOutput").ap()


def _finish(K, stack):
    K.final_wait(K.sp)
    K.emit()
    stack.close()


def build_mod(cfg):
    L, D, KD = cfg["L"], cfg["D"], cfg["KD"]
    nc, stack, K = _new()
    cst = _din(nc, "consts", [128, 640])
    cT = _din(nc, "cT", [128, KD])
    modw = _din(nc, "modw", [L, D, 6 * D])
    modb = _din(nc, "modb", [L, 6 * D])
    gmix = _din(nc, "gmix", [128, L, KD])
    gffn = _din(nc, "gffn", [128, L, KD])
    vec = _dout(nc, "vec", [128, L, 6 * KD])
    C = Consts(K, cst, K.dsems[0])
    emit_mod(K, C, cfg, cT, modw, modb, gmix, gffn, vec)
    _finish(K, stack)
    return nc


def build_norm0(cfg):
    L, D, KD, T = cfg["L"], cfg["D"], cfg["KD"], cfg["T"]
    nc, stack, K = _new()
    cst = _din(nc, "consts", [128, 640])
    xT = _din(nc, "xT", [D, T])
    vec = _din(nc, "vec", [128, L, 6 * KD])
    hT = _dout(nc, "hT", [D, T], BF16)
    C = Consts(K, cst, K.dsems[0])
    emit_norm0(K, C, cfg, xT, vec, hT)
    _finish(K, stack)
    return nc


def build_dense(cfg, last):
    L, D, KD, T, DFF, KO = cfg["L"], cfg["D"], cfg["KD"], cfg["T"], cfg["DFF"], cfg["KO"]
    nc, stack, K = _new()
    cst = _din(nc, "consts", [128, 640])
    xT = _din(nc, "xT", [D, T])
    oT = _din(nc, "oT", [KO * 128, T], BF16)
    wo = _din(nc, "wo", [KO * 128, D])
    wfi = _din(nc, "wfi", [D, 2 * DFF])
    wfo = _din(nc, "wfo", [DFF, D])
    vec = _din(nc, "vec", [128, 2, 6 * KD])
    xTo = _dout(nc, "xTo", [D, T])
    hTo = None if last else _dout(nc, "hTo", [D, T], BF16)
    C = Consts(K, cst, K.dsems[0])
    cfg2 = dict(cfg)
    cfg2["L"] = 2
    emit_dense(K, C, cfg2, 0, xT, oT, wo, wfi, wfo, vec, xTo, hTo)
    _finish(K, stack)
    return nc


def emit_hgrn(K, C, cfg, ab_idx, hT_d, w_d, lbl_d, gain_d, oT_d, h_load=None, o_store=None):
    D, KD, S = cfg["D"], cfg["KD"], cfg["S"]
    ds_a, ds_w, ds_s = K.dsems[0], K.dsems[1], K.dsems[2]
    CH, NCH = 64, 8
    W = K.sb("h_W", [128, KD, 1024], BF16)
    hT = K.sb("h_hT", [128, 2, KD, 512], BF16)
    lbl = K.sb("h_lbl", [128, 2, 2], F32)
    lbv = K.sb("h_lbv", [128, 3, 2], F32)
    gain = K.sb("h_gain", [64, 256], F32)
    ones = K.sb("h_ones", [128, 64], F32)
    names = ["sig", "lf", "key", "qs", "cum", "cm", "cl", "e0", "e1", "e2"]
    hf = [{n: K.sb("h_%s%d" % (n, h), [128, 512], F32) for n in names} for h in range(2)]
    hb = [{n: K.sb("h_%s%d" % (n, h), [128, 512], BF16) for n in ["qh", "qt", "kt", "kh", "k0"]} for h in range(2)]
    eL = [K.sb("h_eL%d" % h, [128, NCH], F32) for h in range(2)]
    Sst = [K.sb("h_S%d" % h, [128, 128], F32) for h in range(2)]
    Sbf = [K.sb("h_Sb%d" % h, [128, 2, 128], BF16) for h in range(2)]
    vc = K.sb("h_vc", [64, 2, 256], BF16)
    gs = K.sb("h_gs", [64, 2, 256], F32)
    PT = K.sb("h_PT", [64, 2, 2, 64], BF16)
    khs = K.sb("h_khs", [64, 2, 2, 128], BF16)
    junk = K.sb("h_junk", [64, 128], F32)
    ssq = K.sb("h_ssq", [64, 2, 2], F32)
    og = K.sb("h_og", [64, 2, 128], BF16)
    oTt = K.sb("h_oTt", [128, 2, 2, 512], BF16)
    fb2 = [K.ps("h_fb%d" % i, [128, 512], F32) for i in range(2)]
    fb = [fb2[0], fb2[1], fb2[0], fb2[1]]
    tb = [K.ps("h_tb%d" % i, [128, 512], F32) for i in range(2)]
    bkO = K.ps("h_bkO", [128, 512], F32)
    bkU = K.ps("h_bkU", [128, 512], F32)
    bkT = [K.ps("h_bkT%d" % i, [128, 1024], BF16) for i in range(2)]
    B = lambda n: [Buf() for _ in range(n)]
    b_W, b_lb, b_gain, b_ones = Buf(), Buf(), Buf(), Buf()
    b_hT, b_tb = B(2), B(2)
    b_fb2 = B(2)
    b_fb = [b_fb2[0], b_fb2[1], b_fb2[0], b_fb2[1]]
    b_hf = [{n: Buf() for n in names} for _ in range(2)]
    b_hb = [{n: Buf() for n in ["qh", "qt", "kt", "kh", "k0"]} for _ in range(2)]
    b_eL, b_S, b_Sbf = B(2), B(2), [B(2), B(2)]
    b_vc, b_gs, b_PT, b_khs, b_junk, b_ssq, b_og = B(2), B(2), B(2), B(2), Buf(), B(2), B(2)
    b_bkO, b_bkU, b_bkT = Buf(), Buf(), B(2)
    b_oTt = B(2)
    A = K.op
    wv = w_d.rearrange("(k p) c -> p k c", p=128)
    for half in range(2):
        K.dma(K.pool, lambda e, half=half: e.dma_start(out=W[:, :, half * 512:(half + 1) * 512],
                                                       in_=wv[:, :, half * 512:(half + 1) * 512]), [], [b_W], ds_w)
    K.dma(K.sp, lambda e: e.dma_start(out=lbl[:], in_=lbl_d[:, :, :]), [], [b_lb], ds_a)
    K.dma(K.sp, lambda e: e.dma_start(out=gain[:], in_=gain_d[0:1, :].partition_broadcast(64)), [], [b_gain], ds_a)
    A(K.dve, lambda e: e.memset(ones[:], 1.0), [], [b_ones])
    if ab_idx == 0:
        A(K.dve, lambda e: e.memset(lbv[:, 0, :], 0.0), [b_lb], [b_lb])
    else:
        A(K.dve, lambda e: e.tensor_tensor(out=lbv[:, 0, :], in0=lbl[:, 1, :], in1=lbl[:, 0, :], op=ALU.subtract),
          [b_lb], [b_lb])
        A(K.act, lambda e: e.activation(out=lbv[:, 0, :], in_=lbv[:, 0, :], func=AF.Sigmoid), [b_lb], [b_lb])
        A(K.dve, lambda e: e.tensor_scalar(out=lbv[:, 0, :], in0=lbv[:, 0, :], scalar1=1.0 - 1e-6, scalar2=0.0,
                                           op0=ALU.min, op1=ALU.max), [b_lb], [b_lb])
    A(K.dve, lambda e: e.tensor_scalar(out=lbv[:, 1, :], in0=lbv[:, 0, :], scalar1=-1.0, scalar2=1.0,
                                       op0=ALU.mult, op1=ALU.add), [b_lb], [b_lb])
    A(K.dve, lambda e: e.tensor_scalar(out=lbv[:, 2, :], in0=lbv[:, 0, :], scalar1=1.0, scalar2=-1.0,
                                       op0=ALU.mult, op1=ALU.add), [b_lb], [b_lb])
    for h in range(2):
        A(K.dve, lambda e, h=h: e.memset(PT[:, h], 0.0), [], [b_PT[h]])
        A(K.dve, lambda e, h=h: e.memset(hb[h]["k0"][:], 0.0), [], [b_hb[h]["k0"]])
        A(K.dve, lambda e, h=h: e.memset(hf[h]["e2"][:], 0.0), [], [b_hf[h]["e2"]])
        A(K.dve, lambda e, h=h: e.memset(Sst[h][:], 0.0), [], [b_S[h]])
        A(K.dve, lambda e, h=h: e.memset(Sbf[h][:, 0, :], 0.0), [], [b_Sbf[h][0]])
    hv = None if h_load is not None else hT_d.rearrange("(k p) t -> p k t", p=128)
    ov = None if o_store is not None else oT_d.rearrange("(h p) t -> p h t", p=128)
    cidx = 0
    for t in range(S // 512):
        s = t % 2
        if h_load is not None:
            h_load(t, hT[:, s], b_hT[s])
        else:
            K.dma(K.sp, lambda e, t=t, s=s: e.dma_start(out=hT[:, s], in_=hv[:, :, t * 512:(t + 1) * 512]),
                  [], [b_hT[s]], ds_a)
        for h in range(2):
            for blk in (2 * h, 2 * h + 1):
                for k in range(KD):
                    A(K.pe, lambda e, blk=blk, k=k, s=s: e.matmul(fb[blk][:, :], lhsT=W[:, k, blk * 128:(blk + 1) * 128],
                                                                  rhs=hT[:, s, k, :], start=(k == 0), stop=(k == KD - 1)),
                      [b_W, b_hT[s]], [b_fb[blk]], inc=(k == KD - 1))
            f, bf_, bb, bbf = hf[h], hb[h], b_hf[h], b_hb[h]
            lb_, oml, noml = lbv[:, 0, h:h + 1], lbv[:, 1, h:h + 1], lbv[:, 2, h:h + 1]
            A(K.act, lambda e, f=f, h=h: e.activation(out=f["sig"][:], in_=fb[2 * h + 1][:, :], func=AF.Sigmoid),
              [b_fb[2 * h + 1]], [bb["sig"]])
            A(K.act, lambda e, f=f, h=h: e.activation(out=f["qs"][:], in_=fb[2 * h][:, :], func=AF.Silu),
              [b_fb[2 * h]], [bb["qs"]])
            A(K.dve, lambda e, f=f, oml=oml, lb_=lb_: e.tensor_scalar(out=f["lf"][:], in0=f["sig"][:], scalar1=oml,
                                                                     scalar2=lb_, op0=ALU.mult, op1=ALU.add),
              [bb["sig"], b_lb], [bb["lf"]])
            A(K.dve, lambda e, f=f: e.tensor_scalar_max(out=f["lf"][:], in0=f["lf"][:], scalar1=1e-30),
              [bb["lf"]], [bb["lf"]])
            A(K.act, lambda e, f=f: e.activation(out=f["lf"][:], in_=f["lf"][:], func=AF.Ln), [bb["lf"]], [bb["lf"]])
            A(K.dve, lambda e, f=f, oml=oml, noml=noml: e.tensor_scalar(out=f["key"][:], in0=f["sig"][:], scalar1=noml,
                                                                       scalar2=oml, op0=ALU.mult, op1=ALU.add),
              [bb["sig"], b_lb], [bb["key"]])
            for c in range(NCH):
                A(K.dve, lambda e, f=f, c=c: e.tensor_tensor_scan(out=f["cum"][:, c * CH:(c + 1) * CH], data0=ones[:, :],
                                                                  data1=f["lf"][:, c * CH:(c + 1) * CH], initial=0.0,
                                                                  op0=ALU.mult, op1=ALU.add),
                  [bb["lf"], b_ones], [bb["cum"]])
            c3 = lambda a: a[:].rearrange("p (c t) -> p c t", c=NCH)
            A(K.dve, lambda e, f=f: e.tensor_tensor(out=c3(f["cm"]), in0=c3(f["cum"]),
                                                    in1=c3(f["cum"])[:, :, 31:32].broadcast_to([128, NCH, CH]),
                                                    op=ALU.subtract), [bb["cum"]], [bb["cm"]])
            A(K.dve, lambda e, f=f: e.tensor_tensor(out=c3(f["cl"]), in0=c3(f["cum"]),
                                                    in1=c3(f["cum"])[:, :, CH - 1:CH].broadcast_to([128, NCH, CH]),
                                                    op=ALU.subtract), [bb["cum"]], [bb["cl"]])
            A(K.act, lambda e, f=f, h=h: e.activation(out=eL[h][:, :], in_=c3(f["cum"])[:, :, CH - 1], func=AF.Exp),
              [bb["cum"]], [b_eL[h]])
            A(K.act, lambda e, f=f: e.activation(out=c3(f["e2"])[:, :, 0:32], in_=c3(f["cum"])[:, :, 0:32], func=AF.Exp,
                                                 scale=-1.0), [bb["cum"]], [bb["e2"]])
            A(K.dve, lambda e, f=f, bf_=bf_: e.tensor_tensor(out=c3(bf_["k0"])[:, :, 0:32], in0=c3(f["key"])[:, :, 0:32],
                                                             in1=c3(f["e2"])[:, :, 0:32], op=ALU.mult),
              [bb["key"], bb["e2"]], [bbf["k0"]])
            for (src, sc, es, mul, dst) in (("cum", 1.0, "e0", "qs", "qh"), ("cm", 1.0, "e1", "qs", "qt"),
                                            ("cm", -1.0, "e0", "key", "kt"), ("cl", -1.0, "e1", "key", "kh")):
                A(K.act, lambda e, f=f, src=src, sc=sc, es=es: e.activation(out=f[es][:], in_=f[src][:], func=AF.Exp,
                                                                           scale=sc), [bb[src]], [bb[es]])
                A(K.dve, lambda e, f=f, bf_=bf_, es=es, mul=mul, dst=dst: e.tensor_tensor(
                    out=bf_[dst][:], in0=f[mul][:], in1=f[es][:], op=ALU.mult), [bb[mul], bb[es]], [bbf[dst]])
        def prep(c, s=s):
            csl = slice(c * CH, (c + 1) * CH)
            c0_ = c * CH
            par = c % 2
            for k in range(KD):
                A(K.pe, lambda e, k=k: e.matmul(tb[par][0:CH, :], lhsT=hT[:, s, k, csl], rhs=W[:, k, 512:1024],
                                                start=(k == 0), stop=(k == KD - 1)),
                  [b_W, b_hT[s]], [b_tb[par]], inc=(k == KD - 1))
            A(K.act, lambda e: e.activation(out=vc[:, par, :], in_=tb[par][0:CH, 0:256], func=AF.Identity),
              [b_tb[par]], [b_vc[par]])
            A(K.act, lambda e: e.activation(out=gs[:, par, :], in_=tb[par][0:CH, 256:512], func=AF.Silu),
              [b_tb[par]], [b_gs[par]])
            A(K.dve, lambda e: e.tensor_tensor(out=gs[:, par, :], in0=gs[:, par, :], in1=gain[:, :], op=ALU.mult),
              [b_gs[par], b_gain], [b_gs[par]])
            for h in range(2):
                bf_, bbf = hb[h], b_hb[h]
                A(K.pe, lambda e, h=h, bf_=bf_: e.matmul(fb2[par][0:64, h * 64 + 32:h * 64 + 64], lhsT=bf_["kt"][:, csl],
                                                         rhs=bf_["qt"][:, c0_ + 32:c0_ + 64], start=True, stop=True),
                  [bbf["kt"], bbf["qt"]], [b_fb2[par]], inc=False)
                A(K.pe, lambda e, h=h, bf_=bf_: e.matmul(fb2[par][0:32, h * 64:h * 64 + 32], lhsT=bf_["k0"][:, c0_:c0_ + 32],
                                                         rhs=bf_["qh"][:, c0_:c0_ + 32], start=True, stop=True),
                  [bbf["k0"], bbf["qh"]], [b_fb2[par]], inc=False)
                A(K.pe, lambda e, h=h, bf_=bf_: e.transpose(bkT[par][0:64, h * 128:(h + 1) * 128], bf_["kh"][:, csl],
                                                            C.ident_b()), [bbf["kh"], C.buf], [b_bkT[par]], inc=(h == 1))
            scv = fb2[par][0:64, 0:128].rearrange("p (h t) -> p h t", h=2)
            A(K.dve, lambda e: e.tensor_tensor(out=PT[:, par, :, 32:64], in0=scv[:, :, 32:64],
                                               in1=C.f[0:64, 160:192].unsqueeze(1).broadcast_to([64, 2, 32]),
                                               op=ALU.mult), [b_fb2[par], C.buf], [b_PT[par]])
            A(K.dve, lambda e: e.tensor_tensor(out=PT[0:32, par, :, 0:32], in0=scv[0:32, :, 0:32],
                                               in1=C.f[0:32, 128:160].unsqueeze(1).broadcast_to([32, 2, 32]),
                                               op=ALU.mult), [b_fb2[par], C.buf], [b_PT[par]])
            A(K.act, lambda e: e.activation(out=khs[:, par].rearrange("p h d -> p (h d)"), in_=bkT[par][0:64, 0:256],
                                            func=AF.Identity), [b_bkT[par]], [b_khs[par]])

        def main(c, s=s):
            csl = slice(c * CH, (c + 1) * CH)
            par = c % 2
            sp_, sn = c % 2, (c + 1) % 2
            for h in range(2):
                bf_, bbf = hb[h], b_hb[h]
                A(K.pe, lambda e, h=h: e.matmul(bkO[0:64, h * 128:(h + 1) * 128], lhsT=PT[:, par, h, :],
                                                rhs=vc[:, par, h * 128:(h + 1) * 128], start=True, stop=False),
                  [b_PT[par], b_vc[par]], [b_bkO], inc=False)
                A(K.pe, lambda e, h=h, bf_=bf_: e.matmul(bkO[0:64, h * 128:(h + 1) * 128], lhsT=bf_["qh"][:, csl],
                                                         rhs=Sbf[h][:, sp_, :], start=False, stop=True),
                  [bbf["qh"], b_Sbf[h][sp_]], [b_bkO], inc=(h == 1))
            for h in range(2):
                A(K.pe, lambda e, h=h: e.matmul(bkU[:, h * 128:(h + 1) * 128], lhsT=khs[:, par, h, :],
                                                rhs=vc[:, par, h * 128:(h + 1) * 128], start=True, stop=True),
                  [b_khs[par], b_vc[par]], [b_bkU], inc=(h == 1))
            for h in range(2):
                A(K.dve, lambda e, h=h: e.scalar_tensor_tensor(out=Sst[h][:], in0=Sst[h][:], scalar=eL[h][:, c:c + 1],
                                                               in1=bkU[:, h * 128:(h + 1) * 128], op0=ALU.mult,
                                                               op1=ALU.add), [b_S[h], b_eL[h], b_bkU], [b_S[h]])
                A(K.act, lambda e, h=h: e.activation(out=Sbf[h][:, sn, :], in_=Sst[h][:], func=AF.Identity),
                  [b_S[h]], [b_Sbf[h][sn]])

        def post_a(c, s=s):
            par = c % 2
            for h in range(2):
                A(K.act, lambda e, h=h: e.activation(out=junk[:, :], in_=bkO[0:64, h * 128:(h + 1) * 128], func=AF.Square,
                                                     accum_out=ssq[:, h, 0:1]), [b_bkO], [b_junk, b_ssq[h]])
                A(K.act, lambda e, h=h: e.activation(out=ssq[:, h, 1:2], in_=ssq[:, h, 0:1], func=AF.Sqrt,
                                                     scale=1.0 / 128, bias=EPS), [b_ssq[h]], [b_ssq[h]])
                A(K.dve, lambda e, h=h: e.reciprocal(out=ssq[:, h, 1:2], in_=ssq[:, h, 1:2]), [b_ssq[h]], [b_ssq[h]])
                A(K.dve, lambda e, h=h: e.scalar_tensor_tensor(out=og[:, h, :], in0=bkO[0:64, h * 128:(h + 1) * 128],
                                                               scalar=ssq[:, h, 1:2],
                                                               in1=gs[:, par, h * 128:(h + 1) * 128],
                                                               op0=ALU.mult, op1=ALU.mult),
                  [b_bkO, b_ssq[h], b_gs[par]], [b_og[h]])

        def post_b(c, s=s):
            csl = slice(c * CH, (c + 1) * CH)
            par = (c + 1) % 2
            for h in range(2):
                A(K.pe, lambda e, h=h: e.transpose(bkT[par][:, 256 + h * 64:256 + (h + 1) * 64], og[:, h, :],
                                                   C.ident_b(64)), [b_og[h], C.buf], [b_bkT[par]], inc=(h == 1))
            A(K.act, lambda e: e.activation(out=oTt[:, s, :, csl],
                                            in_=bkT[par][:, 256:384].rearrange("p (h t) -> p h t", h=2),
                                            func=AF.Identity), [b_bkT[par]], [b_oTt[s]])

        prep(0)
        for c in range(NCH):
            if c + 1 < NCH:
                prep(c + 1)
            main(c)
            if c >= 1:
                post_b(c - 1)
            post_a(c)
        post_b(NCH - 1)
        if o_store is not None:
            o_store(t, oTt[:, s], b_oTt[s], 2)
        else:
            K.dma(K.pool, lambda e, t=t, s=s: e.dma_start(out=ov[:, :, t * 512:(t + 1) * 512], in_=oTt[:, s]),
                  [b_oTt[s]], [], ds_s)


def build_hgrn(cfg, ab_idx):
    D, S = cfg["D"], cfg["S"]
    nc, stack, K = _new()
    cst = _din(nc, "consts", [128, 640])
    hT = _din(nc, "hT", [D, S], BF16)
    w = _din(nc, "w", [D, 1024])
    lbl = _din(nc, "lbl", [128, 2, 2])
    gain = _din(nc, "gain", [1, 256])
    oT = _dout(nc, "oT", [256, S], BF16)
    C = Consts(K, cst, K.dsems[0])
    emit_hgrn(K, C, cfg, ab_idx, hT, w, lbl, gain, oT)
    _finish(K, stack)
    return nc


def emit_fox(K, C, cfg, hT_d, w_d, qkg_d, fbias_d, oT_d, h_load=None, o_store=None):
    D, KD, S = cfg["D"], cfg["KD"], cfg["S"]
    NB = S // 128
    A = K.op
    W = K.sb("x_W", [128, KD, 1026], BF16)
    hT = K.sb("x_hT", [128, 2, KD, 512], BF16)
    kTa = K.sb("x_kTa", [128, 2, S], BF16)
    va = K.sb("x_va", [128, 2, NB, 132], BF16)
    Ga = K.sb("x_Ga", [128, 2, NB], F32)
    qkg = K.sb("x_qkg", [128, 2], F32)
    nfb = K.sb("x_nfb", [128, 2], F32)
    qn = K.sb("x_qn", [128, 2, 512], BF16)
    sq = K.sb("x_sq", [128, 2, 512], F32)
    rt = K.sb("x_rt", [128, 2, 512], F32)
    sg = K.sb("x_sg", [128, 4, 256], F32)
    spv = K.sb("x_spv", [128, 2, 4], F32)
    lcs = K.sb("x_lcs", [128, 2, 4], F32)
    tots = K.sb("x_tots", [128, 2, 4], F32)
    incl = K.sb("x_incl", [128, 2, 2, 4], F32)
    excl = K.sb("x_excl", [128, 2, 4], F32)
    Bq = K.sb("x_Bq", [128, 2, NB], F32)
    PT = K.sb("x_PT", [128, 3, 512], BF16)
    rden = K.sb("x_rden", [128, 4], F32)
    dqc = K.sb("x_dqc", [128, 2, 4], F32)
    dqb = K.sb("x_dqb", [1, 2, 512], F32)
    dqbc = K.sb("x_dqbc", [128, 2, 512], F32)
    stg = K.sb("x_stg", [128, 3, 512], F32)
    b_dqc, b_dqb, b_dqbc = Buf(), Buf(), Buf()
    b_stg = [Buf() for _ in range(3)]
    og = K.sb("x_og", [128, 2, 128], BF16)
    oTt = K.sb("x_oTt", [128, 2, 2, 512], BF16)
    bk = [K.ps("x_bk%d" % i, [128, 512], F32) for i in range(7)]
    bkT = K.ps("x_bkT", [128, 1024], BF16)
    b_bk = [Buf() for _ in range(7)]
    b_T = Buf()
    P0, P1, N_, V_, Fb, O3, P2 = range(7)
    OB = [N_, V_, Fb, O3]
    B = lambda n: [Buf() for _ in range(n)]
    b_W, b_kTa, b_va, b_Ga, b_par = Buf(), Buf(), Buf(), Buf(), Buf()
    b_hT, b_qn, b_sq, b_rt, b_oTt = B(2), B(2), B(2), B(2), B(2)
    b_sg, b_spv, b_lcs, b_tots, b_excl = Buf(), Buf(), Buf(), Buf(), Buf()
    b_incl, b_Bq = B(2), Buf()
    b_PT = B(3)
    b_rden, b_og = Buf(), B(2)
    wv = w_d.rearrange("(k p) c -> p k c", p=128)
    K.dma(K.pool, lambda e: e.dma_start(out=W[:, :, 0:512], in_=wv[:, :, 0:512]), [], [b_W])
    K.dma(K.pool, lambda e: e.dma_start(out=W[:, :, 512:1026], in_=wv[:, :, 512:1026]), [], [b_W])
    K.dma(K.sp, lambda e: e.dma_start(out=qkg[:], in_=qkg_d[:, :]), [], [b_par])
    K.dma(K.sp, lambda e: e.dma_start(out=nfb[:], in_=fbias_d[0:1, :].partition_broadcast(128)), [], [b_par])
    A(K.dve, lambda e: e.tensor_scalar(out=qkg[:, 0:1], in0=qkg[:, 0:1], scalar1=float(128 ** -0.5), scalar2=None,
                                       op0=ALU.mult), [b_par], [b_par])
    A(K.dve, lambda e: e.tensor_scalar(out=nfb[:, :], in0=nfb[:, :], scalar1=-1.0, scalar2=None, op0=ALU.mult),
      [b_par], [b_par])
    A(K.dve, lambda e: e.memset(va[:], 1.0), [], [b_va])
    hv = None if h_load is not None else hT_d.rearrange("(k p) t -> p k t", p=128)
    ov = None if o_store is not None else oT_d.rearrange("(h p) t -> p h t", p=128)
    xs = 0
    for t in range(S // 512):
        s = t % 2
        tsl = slice(t * 512, (t + 1) * 512)
        if h_load is not None:
            h_load(t, hT[:, s], b_hT[s])
        else:
            K.dma(K.sp, lambda e, s=s, tsl=tsl: e.dma_start(out=hT[:, s], in_=hv[:, :, tsl]), [], [b_hT[s]])
        for h in range(2):
            for (which, bank) in ((0, P0), (1, P1)):
                blk = 2 * h + which
                for k in range(KD):
                    A(K.pe, lambda e, blk=blk, k=k, s=s, bank=bank: e.matmul(
                        bk[bank][:, :], lhsT=W[:, k, blk * 128:(blk + 1) * 128], rhs=hT[:, s, k, :],
                        start=(k == 0), stop=(k == KD - 1)), [b_W, b_hT[s]], [b_bk[bank]], inc=(k == KD - 1))
                x = which
                A(K.act, lambda e, x=x, bank=bank: e.activation(out=sq[:, x, :], in_=bk[bank][:, :], func=AF.Square),
                  [b_bk[bank]], [b_sq[x]])
                A(K.pe, lambda e, x=x: e.matmul(bk[N_][:, :], lhsT=C.ones_f(), rhs=sq[:, x, :], start=True, stop=True),
                  [b_sq[x], C.buf], [b_bk[N_]])
                A(K.act, lambda e, x=x: e.activation(out=rt[:, x, :], in_=bk[N_][:, :], func=AF.Sqrt, scale=1.0 / 128,
                                                     bias=EPS), [b_bk[N_]], [b_rt[x]])
                A(K.dve, lambda e, x=x: e.reciprocal(out=rt[:, x, :], in_=rt[:, x, :]), [b_rt[x]], [b_rt[x]])
                if which == 0:
                    A(K.dve, lambda e, h=h, x=x, bank=bank: e.scalar_tensor_tensor(
                        out=qn[:, h, :], in0=bk[bank][:, :], scalar=qkg[:, 0:1], in1=rt[:, x, :], op0=ALU.mult,
                        op1=ALU.mult), [b_bk[bank], b_par, b_rt[x]], [b_qn[h]])
                else:
                    A(K.dve, lambda e, h=h, x=x, bank=bank, tsl=tsl: e.scalar_tensor_tensor(
                        out=kTa[:, h, tsl], in0=bk[bank][:, :], scalar=qkg[:, 1:2], in1=rt[:, x, :], op0=ALU.mult,
                        op1=ALU.mult), [b_bk[bank], b_par, b_rt[x]], [b_kTa])
        for b in range(4):
            blk = 4 * t + b
            bsl = slice(b * 128, (b + 1) * 128)
            for k in range(KD):
                A(K.pe, lambda e, k=k, s=s, bsl=bsl: e.matmul(bk[V_][:, :], lhsT=hT[:, s, k, bsl], rhs=W[:, k, 512:1024],
                                                              start=(k == 0), stop=(k == KD - 1)),
                  [b_W, b_hT[s]], [b_bk[V_]], inc=(k == KD - 1))
            A(K.act, lambda e, blk=blk: e.activation(out=va[:, :, blk, 0:128],
                                                     in_=bk[V_][:, 0:256].rearrange("p (h e) -> p h e", h=2),
                                                     func=AF.Identity), [b_bk[V_]], [b_va])
            A(K.act, lambda e, b=b: e.activation(out=sg[:, b, :], in_=bk[V_][:, 256:512], func=AF.Sigmoid),
              [b_bk[V_]], [b_sg])
            for k in range(KD):
                A(K.pe, lambda e, k=k, s=s, bsl=bsl, b=b: e.matmul(bk[Fb][:, 2 * b:2 * b + 2], lhsT=hT[:, s, k, bsl],
                                                                   rhs=W[:, k, 1024:1026], start=(k == 0),
                                                                   stop=(k == KD - 1)),
                  [b_W, b_hT[s]], [b_bk[Fb]], inc=(k == KD - 1))
        fbv = bk[Fb][:, 0:8].rearrange("p (b h) -> p h b", h=2)
        for h in range(2):
            A(K.act, lambda e, h=h: e.activation(out=spv[:, h, :], in_=fbv[:, h, :], func=AF.Exp, scale=-1.0,
                                                 bias=nfb[:, h:h + 1]), [b_bk[Fb], b_par], [b_spv])
        A(K.act, lambda e: e.activation(out=spv[:], in_=spv[:], func=AF.Ln, bias=1.0, scale=1.0), [b_spv], [b_spv])
        A(K.pe, lambda e: e.matmul(bk[Fb][:, 16:24], lhsT=C.tri_f(), rhs=spv[:].rearrange("p h b -> p (h b)"),
                                   start=True, stop=True), [b_spv, C.buf], [b_bk[Fb]])
        A(K.act, lambda e: e.activation(out=lcs[:].rearrange("p h b -> p (h b)"), in_=bk[Fb][:, 16:24],
                                        func=AF.Identity), [b_bk[Fb]], [b_lcs])
        A(K.pe, lambda e: e.matmul(bk[Fb][:, 32:40], lhsT=C.sel_f(), rhs=lcs[:].rearrange("p h b -> p (h b)"),
                                   start=True, stop=True), [b_lcs, C.buf], [b_bk[Fb]])
        A(K.act, lambda e: e.activation(out=tots[:].rearrange("p h b -> p (h b)"), in_=bk[Fb][:, 32:40],
                                        func=AF.Identity), [b_bk[Fb]], [b_tots])
        for h in range(2):
            init = 0.0 if t == 0 else incl[:, 1 - s, h, 3:4]
            A(K.dve, lambda e, h=h, s=s, init=init: e.tensor_tensor_scan(
                out=incl[:, s, h, :], data0=C.ones_f(128, 4), data1=tots[:, h, :], initial=init, op0=ALU.mult,
                op1=ALU.add), [b_tots, C.buf, b_incl[1 - s]], [b_incl[s]])
        A(K.dve, lambda e, s=s: e.tensor_tensor(out=excl[:], in0=incl[:, s], in1=tots[:], op=ALU.subtract),
          [b_incl[s], b_tots], [b_excl])
        A(K.dve, lambda e, t=t: e.tensor_tensor(out=Ga[:, :, 4 * t:4 * t + 4], in0=lcs[:], in1=excl[:], op=ALU.add),
          [b_lcs, b_excl], [b_Ga])
        nkb = 4 * t + 4
        for h in range(2):
            A(K.dve, lambda e, h=h, s=s, nkb=nkb: e.tensor_scalar(
                out=Bq[:, h, 0:nkb], in0=Ga[:, h, 0:nkb], scalar1=incl[:, s, h, 3:4], scalar2=None,
                op0=ALU.subtract), [b_Ga, b_incl[s]], [b_Bq])
            A(K.dve, lambda e, h=h, s=s, t=t: e.tensor_scalar(
                out=dqc[:, h, :], in0=Ga[:, h, 4 * t:4 * t + 4], scalar1=incl[:, s, h, 3:4], scalar2=-1.0,
                op0=ALU.subtract, op1=ALU.mult), [b_Ga, b_incl[s]], [b_dqc])
        for h in range(2):
            for jl in range(4):
                A(K.pe, lambda e, h=h, jl=jl: e.transpose(bk[O3][0:1, jl * 128:(jl + 1) * 128], dqc[:, h, jl:jl + 1],
                                                          C.ident_f()), [b_dqc, C.buf], [b_bk[O3]])
            A(K.act, lambda e, h=h: e.activation(out=dqb[0:1, h, :], in_=bk[O3][0:1, :], func=AF.Identity),
              [b_bk[O3]], [b_dqb])
            A(K.pe, lambda e, h=h: e.matmul(bk[O3][:, :], lhsT=C.ones_f(1, 128), rhs=dqb[0:1, h, :], start=True,
                                            stop=True), [b_dqb, C.buf], [b_bk[O3]])
            A(K.act, lambda e, h=h: e.activation(out=dqbc[:, h, :], in_=bk[O3][:, :], func=AF.Identity),
              [b_bk[O3]], [b_dqbc])
        SB = (P0, P1, P2)
        for h in range(2):
            def score(kb, h=h):
                m = max(0, kb - 4 * t)
                x3 = kb % 3
                sb_ = SB[x3]
                diag = kb >= 4 * t
                A(K.pe, lambda e: e.matmul(bk[sb_][:, m * 128:512], lhsT=kTa[:, h, kb * 128:(kb + 1) * 128],
                                           rhs=qn[:, h, m * 128:512], start=True, stop=(not diag)),
                  [b_kTa, b_qn[h]], [b_bk[sb_]], inc=(not diag))
                if diag:
                    A(K.pe, lambda e: e.matmul(bk[sb_][:, m * 128:(m + 1) * 128], lhsT=C.ident_b(), rhs=C.negmask_b(),
                                               start=False, stop=True), [C.buf], [b_bk[sb_]])
                A(K.dve, lambda e: e.tensor_tensor(out=stg[:, x3, m * 128:512], in0=bk[sb_][:, m * 128:512],
                                                   in1=dqbc[:, h, m * 128:512], op=ALU.add),
                  [b_bk[sb_], b_dqbc], [b_stg[x3]])
                A(K.act, lambda e: e.activation(out=PT[:, x3, m * 128:512], in_=stg[:, x3, m * 128:512], func=AF.Exp,
                                                bias=Bq[:, h, kb:kb + 1], scale=1.0), [b_stg[x3], b_Bq], [b_PT[x3]])

            def pv(kb, h=h):
                m = max(0, kb - 4 * t)
                x3 = kb % 3
                for jl in range(m, 4):
                    j = 4 * t + jl
                    A(K.pe, lambda e, jl=jl, j=j: e.matmul(bk[OB[jl]][:, 0:129], lhsT=PT[:, x3, jl * 128:(jl + 1) * 128],
                                                           rhs=va[:, h, kb, 0:129], start=(kb == 0), stop=(kb == j)),
                      [b_PT[x3], b_va], [b_bk[OB[jl]]], inc=(jl == 3 or kb == j))
            score(0)
            score(1)
            for kb in range(2, nkb):
                score(kb)
                pv(kb - 2)
            pv(nkb - 2)
            pv(nkb - 1)
            for jl in range(4):
                A(K.dve, lambda e, jl=jl: e.reciprocal(out=rden[:, jl:jl + 1], in_=bk[OB[jl]][:, 128:129]),
                  [b_bk[OB[jl]]], [b_rden])
                x = jl % 2
                A(K.dve, lambda e, jl=jl, h=h, x=x: e.scalar_tensor_tensor(
                    out=og[:, x, :], in0=bk[OB[jl]][:, 0:128], scalar=rden[:, jl:jl + 1],
                    in1=sg[:, jl, h * 128:(h + 1) * 128], op0=ALU.mult, op1=ALU.mult),
                    [b_bk[OB[jl]], b_rden, b_sg], [b_og[x]])
                A(K.pe, lambda e, x=x: e.transpose(bkT[:, 0:128], og[:, x, :], C.ident_b()), [b_og[x], C.buf], [b_T])
                A(K.act, lambda e, s=s, h=h, jl=jl: e.activation(out=oTt[:, s, h, jl * 128:(jl + 1) * 128],
                                                                 in_=bkT[:, 0:128], func=AF.Identity),
                  [b_T], [b_oTt[s]])
        if o_store is not None:
            o_store(t, oTt[:, s], b_oTt[s], 2)
        else:
            K.dma(K.pool, lambda e, s=s, tsl=tsl: e.dma_start(out=ov[:, :, tsl], in_=oTt[:, s]), [b_oTt[s]], [])


def build_fox(cfg):
    D, S = cfg["D"], cfg["S"]
    nc, stack, K = _new()
    cst = _din(nc, "consts", [128, 640])
    hT = _din(nc, "hT", [D, S], BF16)
    w = _din(nc, "w", [D, 1026])
    qkg = _din(nc, "qkg", [128, 2])
    fbias = _din(nc, "fbias", [1, 2])
    oT = _dout(nc, "oT", [256, S], BF16)
    C = Consts(K, cst, None)
    emit_fox(K, C, cfg, hT, w, qkg, fbias, oT)
    _finish(K, stack)
    return nc


def emit_gla(K, C, cfg, hT_d, w_d, wgu_d, bg_d, gain_d, oT_d, h_load=None, o_store=None):
    D, KD, S = cfg["D"], cfg["KD"], cfg["S"]
    A = K.op
    CH, NCH = 128, 4
    NW = 1552
    W = K.sb("g_W", [128, KD, NW], BF16)
    hT = K.sb("g_hT", [128, 2, KD, 512], BF16)
    wgu = K.sb("g_wgu", [16, 256], F32)
    bg = K.sb("g_bg", [128, 2], F32)
    gain = K.sb("g_gain", [128, 512], F32)
    ones = K.sb("g_ones", [128, 128], F32)
    glT = K.sb("g_glT", [16, 512], F32)
    spt = K.sb("g_sp", [128, 2, 512], F32)
    csp = K.sb("g_csp", [128, 2, 512], F32)
    Eq = K.sb("g_Eq", [128, 2, 512], F32)
    Ek = K.sb("g_Ek", [128, 2, 512], F32)
    eL = K.sb("g_eL", [128, 2, NCH], F32)
    qh = K.sb("g_qh", [128, 2, 512], BF16)
    kt = K.sb("g_kt", [128, 2, 512], BF16)
    kh = K.sb("g_kh", [128, 2, 512], BF16)
    Sst = K.sb("g_S", [128, 2, 512], F32)
    Sbf = K.sb("g_Sbf", [128, 2, 2, 512], BF16)
    vc = K.sb("g_vc", [128, 2, 512], BF16)
    gs = K.sb("g_gs", [128, 2, 512], F32)
    PT = K.sb("g_PT", [128, 2, 128], BF16)
    khs = K.sb("g_khs", [128, 2, 256], BF16)
    junk = K.sb("g_junk", [128, 512], F32)
    ssq = K.sb("g_ssq", [128, 2], F32)
    og = K.sb("g_og", [128, 512], BF16)
    oTt = K.sb("g_oTt", [128, 2, 4, 512], BF16)
    bk = [K.ps("g_bk%d" % i, [128, 512], F32) for i in range(6)]
    bkT = K.ps("g_bkT", [128, 1024], BF16)
    bkT2 = K.ps("g_bkT2", [128, 1024], BF16)
    b_bk = [Buf() for _ in range(6)]
    b_T, b_T2 = Buf(), Buf()
    GA, Z0, Z1, V_, Gt, O_ = range(6)
    B = lambda n: [Buf() for _ in range(n)]
    b_W, b_par, b_ones, b_glT = Buf(), Buf(), Buf(), Buf()
    b_hT, b_sp, b_csp, b_Eq, b_Ek, b_eL = B(2), B(2), B(2), B(2), B(2), B(2)
    b_qh, b_kt, b_kh, b_S = B(2), B(2), B(2), B(2)
    b_Sbf = [B(2), B(2)]
    b_vc, b_gs, b_PT, b_khs, b_oTt = B(2), B(2), B(2), B(2), B(2)
    b_junk, b_ssq, b_og = Buf(), Buf(), Buf()
    wv = w_d.rearrange("(k p) c -> p k c", p=128)
    for (c0, c1) in ((0, 528), (528, 1040), (1040, 1552)):
        K.dma(K.pool, lambda e, c0=c0, c1=c1: e.dma_start(out=W[:, :, c0:c1], in_=wv[:, :, c0:c1]), [], [b_W])
    K.dma(K.sp, lambda e: e.dma_start(out=wgu[:], in_=wgu_d[:, :]), [], [b_par])
    K.dma(K.sp, lambda e: e.dma_start(out=bg[:], in_=bg_d[:, :]), [], [b_par])
    K.dma(K.sp, lambda e: e.dma_start(out=gain[:], in_=gain_d[0:1, :].partition_broadcast(128)), [], [b_par])
    A(K.dve, lambda e: e.tensor_scalar(out=bg[:, :], in0=bg[:, :], scalar1=-1.0, scalar2=None, op0=ALU.mult),
      [b_par], [b_par])
    A(K.dve, lambda e: e.memset(ones[:], 1.0), [], [b_ones])
    for dc in range(2):
        A(K.dve, lambda e, dc=dc: e.memset(Sst[:, dc, :], 0.0), [], [b_S[dc]])
        A(K.dve, lambda e, dc=dc: e.memset(Sbf[:, dc, 0, :], 0.0), [], [b_Sbf[dc][0]])
    hv = None if h_load is not None else hT_d.rearrange("(k p) t -> p k t", p=128)
    ov = None if o_store is not None else oT_d.rearrange("(h p) t -> p h t", p=128)
    c3 = lambda a: a.rearrange("p (c t) -> p c t", c=NCH)
    cg = 0
    for t in range(S // 512):
        s = t % 2
        tsl = slice(t * 512, (t + 1) * 512)
        if h_load is not None:
            h_load(t, hT[:, s], b_hT[s])
        else:
            K.dma(K.sp, lambda e, s=s, tsl=tsl: e.dma_start(out=hT[:, s], in_=hv[:, :, tsl]), [], [b_hT[s]])
        for k in range(KD):
            A(K.pe, lambda e, k=k, s=s: e.matmul(bk[GA][0:16, :], lhsT=W[:, k, 512:528], rhs=hT[:, s, k, :],
                                                 start=(k == 0), stop=(k == KD - 1)),
              [b_W, b_hT[s]], [b_bk[GA]], inc=(k == KD - 1))
        A(K.act, lambda e: e.activation(out=glT[:, :], in_=bk[GA][0:16, :], func=AF.Identity), [b_bk[GA]], [b_glT])
        for dc in range(2):
            zb = (Z0, Z1)[dc]
            A(K.pe, lambda e, dc=dc, zb=zb: e.matmul(bk[zb][:, :], lhsT=wgu[:, dc * 128:(dc + 1) * 128], rhs=glT[:, :],
                                                     start=True, stop=True), [b_par, b_glT], [b_bk[zb]])
            A(K.act, lambda e, dc=dc, zb=zb: e.activation(out=spt[:, dc, :], in_=bk[zb][:, :], func=AF.Exp, scale=-1.0,
                                                          bias=bg[:, dc:dc + 1]), [b_bk[zb], b_par], [b_sp[dc]])
            A(K.act, lambda e, dc=dc: e.activation(out=spt[:, dc, :], in_=spt[:, dc, :], func=AF.Ln, bias=1.0,
                                                   scale=1.0), [b_sp[dc]], [b_sp[dc]])
            for c in range(NCH):
                A(K.dve, lambda e, dc=dc, c=c: e.tensor_tensor_scan(
                    out=csp[:, dc, c * CH:(c + 1) * CH], data0=ones[:, :], data1=spt[:, dc, c * CH:(c + 1) * CH],
                    initial=0.0, op0=ALU.mult, op1=ALU.add), [b_sp[dc], b_ones], [b_csp[dc]])
            A(K.act, lambda e, dc=dc: e.activation(out=Eq[:, dc, :], in_=csp[:, dc, :], func=AF.Exp, scale=-1.0 / 16),
              [b_csp[dc]], [b_Eq[dc]])
            A(K.act, lambda e, dc=dc: e.activation(out=Ek[:, dc, :], in_=csp[:, dc, :], func=AF.Exp, scale=1.0 / 16),
              [b_csp[dc]], [b_Ek[dc]])
            A(K.act, lambda e, dc=dc: e.activation(out=eL[:, dc, :], in_=c3(csp[:, dc, :])[:, :, CH - 1], func=AF.Exp,
                                                   scale=-1.0 / 16), [b_csp[dc]], [b_eL[dc]])
            for (which, dst) in ((0, "q"), (1, "k")):
                blk = which * 2 + dc
                for k in range(KD):
                    A(K.pe, lambda e, blk=blk, k=k, s=s, zb=zb: e.matmul(
                        bk[zb][:, :], lhsT=W[:, k, blk * 128:(blk + 1) * 128], rhs=hT[:, s, k, :], start=(k == 0),
                        stop=(k == KD - 1)), [b_W, b_hT[s]], [b_bk[zb]], inc=(k == KD - 1))
                if which == 0:
                    A(K.dve, lambda e, dc=dc, zb=zb: e.scalar_tensor_tensor(
                        out=qh[:, dc, :], in0=bk[zb][:, :], scalar=float(256 ** -0.5), in1=Eq[:, dc, :], op0=ALU.mult,
                        op1=ALU.mult), [b_bk[zb], b_Eq[dc]], [b_qh[dc]])
                else:
                    A(K.dve, lambda e, dc=dc, zb=zb: e.tensor_tensor(out=kt[:, dc, :], in0=bk[zb][:, :],
                                                                     in1=Ek[:, dc, :], op=ALU.mult),
                      [b_bk[zb], b_Ek[dc]], [b_kt[dc]])
                    A(K.dve, lambda e, dc=dc: e.tensor_tensor(
                        out=c3(kh[:, dc, :]), in0=c3(kt[:, dc, :]),
                        in1=eL[:, dc, :].unsqueeze(2).broadcast_to([128, NCH, CH]), op=ALU.mult),
                        [b_kt[dc], b_eL[dc]], [b_kh[dc]])
        def prep(c, s=s):
            csl = slice(c * CH, (c + 1) * CH)
            ts = c % 2
            for (bank, c0) in ((V_, 528), (Gt, 1040)):
                for k in range(KD):
                    A(K.pe, lambda e, k=k, bank=bank, c0=c0: e.matmul(
                        bk[bank][:, :], lhsT=hT[:, s, k, csl], rhs=W[:, k, c0:c0 + 512], start=(k == 0),
                        stop=(k == KD - 1)), [b_W, b_hT[s]], [b_bk[bank]], inc=(k == KD - 1))
            A(K.act, lambda e: e.activation(out=vc[:, ts, :], in_=bk[V_][:, :], func=AF.Identity),
              [b_bk[V_]], [b_vc[ts]])
            A(K.act, lambda e: e.activation(out=gs[:, ts, :], in_=bk[Gt][:, :], func=AF.Silu),
              [b_bk[Gt]], [b_gs[ts]])
            A(K.dve, lambda e: e.tensor_tensor(out=gs[:, ts, :], in0=gs[:, ts, :], in1=gain[:, :], op=ALU.mult),
              [b_gs[ts], b_par], [b_gs[ts]])
            for dc in range(2):
                A(K.pe, lambda e, dc=dc: e.matmul(bk[GA][:, 0:128], lhsT=kt[:, dc, csl], rhs=qh[:, dc, csl],
                                                  start=(dc == 0), stop=(dc == 1)),
                  [b_kt[dc], b_qh[dc]], [b_bk[GA]], inc=(dc == 1))
            A(K.dve, lambda e: e.tensor_tensor(out=PT[:, ts, :], in0=bk[GA][:, 0:128], in1=C.tri_f(), op=ALU.mult),
              [b_bk[GA], C.buf], [b_PT[ts]])
            for dc in range(2):
                A(K.pe, lambda e, dc=dc: e.transpose(bkT[:, dc * 128:(dc + 1) * 128], kh[:, dc, csl], C.ident_b()),
                  [b_kh[dc], C.buf], [b_T], inc=(dc == 1))
            A(K.act, lambda e: e.activation(out=khs[:, ts, :], in_=bkT[:, 0:256], func=AF.Identity),
              [b_T], [b_khs[ts]])

        def main(c, s=s):
            csl = slice(c * CH, (c + 1) * CH)
            ts = c % 2
            sp_, sn = c % 2, (c + 1) % 2
            A(K.pe, lambda e: e.matmul(bk[O_][:, :], lhsT=PT[:, ts, :], rhs=vc[:, ts, :], start=True, stop=False),
              [b_PT[ts], b_vc[ts]], [b_bk[O_]], inc=False)
            for dc in range(2):
                A(K.pe, lambda e, dc=dc: e.matmul(bk[O_][:, :], lhsT=qh[:, dc, csl], rhs=Sbf[:, dc, sp_, :], start=False,
                                                  stop=(dc == 1)),
                  [b_qh[dc], b_Sbf[dc][sp_]], [b_bk[O_]], inc=(dc == 1))
            for dc in range(2):
                ub = (Z0, Z1)[dc]
                A(K.pe, lambda e, dc=dc, ub=ub: e.matmul(bk[ub][:, :], lhsT=khs[:, ts, dc * 128:(dc + 1) * 128],
                                                         rhs=vc[:, ts, :], start=True, stop=True),
                  [b_khs[ts], b_vc[ts]], [b_bk[ub]])
            for dc in range(2):
                ub = (Z0, Z1)[dc]
                A(K.dve, lambda e, dc=dc, ub=ub: e.scalar_tensor_tensor(
                    out=Sst[:, dc, :], in0=Sst[:, dc, :], scalar=eL[:, dc, c:c + 1], in1=bk[ub][:, :], op0=ALU.mult,
                    op1=ALU.add), [b_S[dc], b_eL[dc], b_bk[ub]], [b_S[dc]])
                A(K.act, lambda e, dc=dc: e.activation(out=Sbf[:, dc, sn, :], in_=Sst[:, dc, :], func=AF.Identity),
                  [b_S[dc]], [b_Sbf[dc][sn]])

        def post_a(c, s=s):
            ts = c % 2
            A(K.act, lambda e: e.activation(out=junk[:, :], in_=bk[O_][:, :], func=AF.Square, accum_out=ssq[:, 0:1]),
              [b_bk[O_]], [b_junk, b_ssq])
            A(K.act, lambda e: e.activation(out=ssq[:, 1:2], in_=ssq[:, 0:1], func=AF.Sqrt, scale=1.0 / 512, bias=EPS),
              [b_ssq], [b_ssq])
            A(K.dve, lambda e: e.reciprocal(out=ssq[:, 1:2], in_=ssq[:, 1:2]), [b_ssq], [b_ssq])
            A(K.dve, lambda e: e.scalar_tensor_tensor(out=og[:, :], in0=bk[O_][:, :], scalar=ssq[:, 1:2],
                                                      in1=gs[:, ts, :], op0=ALU.mult, op1=ALU.mult),
              [b_bk[O_], b_ssq, b_gs[ts]], [b_og])

        def post_b(c, s=s):
            csl = slice(c * CH, (c + 1) * CH)
            for ec in range(4):
                A(K.pe, lambda e, ec=ec: e.transpose(bkT2[:, ec * 128:(ec + 1) * 128], og[:, ec * 128:(ec + 1) * 128],
                                                     C.ident_b()), [b_og, C.buf], [b_T2], inc=(ec == 3))
            A(K.act, lambda e: e.activation(out=oTt[:, s, :, csl], in_=bkT2[:, 0:512].rearrange("p (a q) -> p a q", a=4),
                                            func=AF.Identity), [b_T2], [b_oTt[s]])

        prep(0)
        for c in range(NCH):
            if c + 1 < NCH:
                prep(c + 1)
            main(c)
            if c >= 1:
                post_b(c - 1)
            post_a(c)
        post_b(NCH - 1)
        if o_store is not None:
            o_store(t, oTt[:, s], b_oTt[s], 4)
        else:
            K.dma(K.pool, lambda e, s=s, tsl=tsl: e.dma_start(out=ov[:, :, tsl], in_=oTt[:, s]), [b_oTt[s]], [])


def build_gla(cfg):
    D, S = cfg["D"], cfg["S"]
    nc, stack, K = _new()
    cst = _din(nc, "consts", [128, 640])
    hT = _din(nc, "hT", [D, S], BF16)
    w = _din(nc, "w", [D, 1552])
    wgu = _din(nc, "wgu", [16, 256])
    bg = _din(nc, "bg", [128, 2])
    gain = _din(nc, "gain", [1, 512])
    oT = _dout(nc, "oT", [512, S], BF16)
    C = Consts(K, cst, None)
    emit_gla(K, C, cfg, hT, w, wgu, bg, gain, oT)
    _finish(K, stack)
    return nc


CFG = dict(L=4, D=2048, KD=16, T=2048, S=8192, DFF=5632, KF=44, KO=16)
_PROGS = {}


def _prog(key, fn):
    if key not in _PROGS:
        _PROGS[key] = fn()
    return _PROGS[key]


def _run(nc, in_maps):
    res = run_bass_kernel_spmd(nc, in_maps, core_ids=list(range(8)))
    return res.results


def _fm(v, kd):
    v = np.asarray(v)
    lead = v.shape[:-1]
    a = v.reshape(lead + (kd, 128))
    return np.ascontiguousarray(np.moveaxis(a, -1, 0))


def kernel_unfused(x, c, mod_w, mod_b, norm_mix_gain, norm_ffn_gain, ab_w_in, ab_w_out, hgrn_lb_logits, hgrn_out_gain,
                   fox_q_gain, fox_k_gain, fox_f_bias, gla_w_in, gla_w_gate_up, gla_b_gate, gla_out_gain, gla_w_out,
                   ffn_w_in, ffn_w_out):
    cfg = CFG
    L, D, KD, T, S, DFF = cfg["L"], cfg["D"], cfg["KD"], cfg["T"], cfg["S"], cfg["DFF"]
    f32 = lambda a: np.ascontiguousarray(np.asarray(a, dtype=np.float32))
    x, c, mod_w, mod_b = f32(x), f32(c), f32(mod_w), f32(mod_b)
    consts = make_consts()
    cores = [(cid // 4, cid % 4) for cid in range(8)]

    cfg1 = dict(cfg)
    cfg1["L"] = 1
    nc = _prog("mod", lambda: build_mod(cfg1))
    ims = []
    for (b, j) in cores:
        ims.append({"consts": consts, "cT": _fm(c[b], KD), "modw": mod_w[j:j + 1], "modb": mod_b[j:j + 1],
                    "gmix": _fm(f32(norm_mix_gain)[j:j + 1], KD), "gffn": _fm(f32(norm_ffn_gain)[j:j + 1], KD)})
    r = _run(nc, ims)
    vec = [np.ascontiguousarray(np.concatenate([r[b * 4 + j]["vec"] for j in range(4)], axis=1)) for b in range(2)]

    nc = _prog("norm0", lambda: build_norm0(cfg))
    xT = [np.ascontiguousarray(x[b, j * T:(j + 1) * T, :].T) for (b, j) in cores]
    r = _run(nc, [{"consts": consts, "xT": xT[i], "vec": vec[cores[i][0]]} for i in range(8)])
    hT = [r[i]["hT"] for i in range(8)]

    for l in range(L):
        hfull = [np.ascontiguousarray(np.concatenate(hT[b * 4:b * 4 + 4], axis=1)) for b in range(2)]
        if l % 2 == 0:
            i = l // 2
            w = f32(ab_w_in[i])
            lbl_all = f32(hgrn_lb_logits)
            ims_h, ims_f = [], []
            for (b, j) in cores:
                hh = (2 * j, 2 * j + 1)
                cs = lambda base, h_: w[:, base + h_ * 128:base + (h_ + 1) * 128]
                wh = np.concatenate([cs(0, hh[0]), cs(1024, hh[0]), cs(0, hh[1]), cs(1024, hh[1]),
                                     cs(2048, hh[0]), cs(2048, hh[1]), cs(3072, hh[0]), cs(3072, hh[1])], axis=1)
                lbl = np.stack([np.stack([lbl_all[ly, h_ * 128:(h_ + 1) * 128] for h_ in hh], axis=1)
                                for ly in range(2)], axis=1)
                ims_h.append({"consts": consts, "hT": hfull[b], "w": np.ascontiguousarray(wh),
                              "lbl": f32(lbl), "gain": f32(hgrn_out_gain[i][2 * j:2 * j + 2]).reshape(1, 256)})
                wf = np.concatenate([cs(4096, hh[0]), cs(5120, hh[0]), cs(4096, hh[1]), cs(5120, hh[1]),
                                     cs(6144, hh[0]), cs(6144, hh[1]), cs(7168, hh[0]), cs(7168, hh[1]),
                                     w[:, 8192 + hh[0]:8192 + hh[0] + 1], w[:, 8192 + hh[1]:8192 + hh[1] + 1]], axis=1)
                ims_f.append({"consts": consts, "hT": hfull[b], "w": np.ascontiguousarray(wf),
                              "qkg": f32(np.stack([fox_q_gain[i], fox_k_gain[i]], axis=1)),
                              "fbias": f32(fox_f_bias[i][2 * j:2 * j + 2]).reshape(1, 2)})
            rh = _run(_prog(("hgrn", i), lambda: build_hgrn(cfg, i)), ims_h)
            rf = _run(_prog("fox", lambda: build_fox(cfg)), ims_f)
            ofull = [np.concatenate([np.concatenate([rh[b * 4 + j]["oT"], rf[b * 4 + j]["oT"]], axis=0)
                                     for j in range(4)], axis=0) for b in range(2)]
            wo_src = f32(ab_w_out[i])
            perm = []
            for j in range(4):
                for typ in range(2):
                    for hl in range(2):
                        base = typ * 1024 + (2 * j + hl) * 128
                        perm.extend(range(base, base + 128))
            wo = np.ascontiguousarray(wo_src[np.asarray(perm)])
        else:
            i = l // 2
            w = f32(gla_w_in[i])
            ims_g = []
            for (b, j) in cores:
                wg = np.concatenate([w[:, j * 256:(j + 1) * 256], w[:, 1024 + j * 256:1024 + (j + 1) * 256],
                                     w[:, 6144:6160], w[:, 2048 + j * 512:2048 + (j + 1) * 512],
                                     w[:, 4096 + j * 512:4096 + (j + 1) * 512]], axis=1)
                ims_g.append({"consts": consts, "hT": hfull[b], "w": np.ascontiguousarray(wg),
                              "wgu": f32(gla_w_gate_up[i][:, j * 256:(j + 1) * 256]),
                              "bg": f32(np.asarray(gla_b_gate[i][j * 256:(j + 1) * 256]).reshape(2, 128).T),
                              "gain": f32(gla_out_gain[i]).reshape(1, 512)})
            rg = _run(_prog("gla", lambda: build_gla(cfg)), ims_g)
            ofull = [np.concatenate([rg[b * 4 + j]["oT"] for j in range(4)], axis=0) for b in range(2)]
            wo = f32(gla_w_out[i])
        last = (l == L - 1)
        nc = _prog(("dense", last), lambda: build_dense(cfg, last))
        wfi, wfo = f32(ffn_w_in[l]), f32(ffn_w_out[l])
        ims = []
        for ci, (b, j) in enumerate(cores):
            v2 = np.zeros((128, 2, 6 * KD), np.float32)
            v2[:, 0] = vec[b][:, l]
            if not last:
                v2[:, 1] = vec[b][:, l + 1]
            ims.append({"consts": consts, "xT": xT[ci], "oT": np.ascontiguousarray(ofull[b][:, j * T:(j + 1) * T]),
                        "wo": wo, "wfi": wfi, "wfo": wfo, "vec": v2})
        r = _run(nc, ims)
        xT = [r[ci]["xTo"] for ci in range(8)]
        if not last:
            hT = [r[ci]["hTo"] for ci in range(8)]

    out = np.empty((2, S, D), np.float32)
    for ci, (b, j) in enumerate(cores):
        out[b, j * T:(j + 1) * T, :] = xT[ci].T
    return out


GROUPS = [[0, 1, 2, 3], [4, 5, 6, 7]]
PRECAST = False


def build_fused(cfg):
    L, D, KD, T, S, DFF, KO = cfg["L"], cfg["D"], cfg["KD"], cfg["T"], cfg["S"], cfg["DFF"], cfg["KO"]
    NT = T // 512
    NQ = D // 256
    nc = bass.Bass("TRN2", target_bir_lowering=False)
    gstack = contextlib.ExitStack()
    K = Kern(nc, gstack)
    cst = _din(nc, "consts", [128, 640])
    cT = _din(nc, "cT", [128, KD])
    modw = _din(nc, "modw", [1, D, 6 * D])
    modb = _din(nc, "modb", [1, 6 * D])
    gmix = _din(nc, "gmix", [128, 1, KD])
    gffn = _din(nc, "gffn", [128, 1, KD])
    xT_in = _din(nc, "xT", [D, T])
    wfi = _din(nc, "wfi", [L, D, 2 * DFF])
    wfo = _din(nc, "wfo", [L, DFF, D])
    wo = [_din(nc, "wo%d" % l, [KO * 128, D]) for l in range(L)]
    ab, gl = {}, {}
    for i in range((L + 1) // 2):
        ab[i] = dict(wh=_din(nc, "wh%d" % i, [D, 1024]), lbl=_din(nc, "lbl%d" % i, [128, 2, 2]),
                     again=_din(nc, "again%d" % i, [1, 256]), wf=_din(nc, "wf%d" % i, [D, 1026]),
                     qkg=_din(nc, "qkg%d" % i, [128, 2]), fbias=_din(nc, "fbias%d" % i, [1, 2]))
    for i in range(L // 2):
        gl[i] = dict(wg=_din(nc, "wg%d" % i, [D, 1552]), wgu=_din(nc, "wgu%d" % i, [16, 256]),
                     bg=_din(nc, "bg%d" % i, [128, 2]), ggain=_din(nc, "ggain%d" % i, [1, 512]))
    xTo = _dout(nc, "xTo", [D, T])
    vin = nc.dram_tensor("i_vin", [128, 6 * KD], F32)
    vall = nc.dram_tensor("i_vall", [4 * 128, 6 * KD], F32)
    xs = nc.dram_tensor("i_xs", [D, T], F32)
    HK = KD // 2
    hq = nc.dram_tensor("i_hq", [NT * 2, HK * 128, 512], BF16)
    hf = nc.dram_tensor("i_hf", [NT * 2, 4 * HK * 128, 512], BF16)
    OC = 2 if NT % 2 == 0 else 1
    NOC = (S // 512) // OC
    oqs = {nm: nc.dram_tensor("i_oq" + nm, [NOC, rows, OC * 512], BF16) for nm, rows in (("h", 256), ("f", 256), ("g", 512))}
    ofs = {nm: nc.dram_tensor("i_of" + nm, [NOC, 4 * rows, OC * 512], BF16)
           for nm, rows in (("h", 256), ("f", 256), ("g", 512))}
    wbo = nc.dram_tensor("i_wbo", [KO * 128, D], BF16)
    wbi = nc.dram_tensor("i_wbi", [D, 2 * DFF], BF16)
    wbf = nc.dram_tensor("i_wbf", [DFF, D], BF16)
    b_cv = [Buf() for _ in range(4)]
    conv = {"q": [], "per": 1, "n": 0}

    def conv_plan(l, n_calls):
        q = []
        if not PRECAST:
            conv["q"] = q
            return
        for (src, dst, rows, step) in ((wo[l], wbo.ap(), KO * 128, 512), (wfi[l], wbi.ap(), D, 128),
                                       (wfo[l], wbf.ap(), DFF, 512)):
            for r0 in range(0, rows, step):
                r1 = min(rows, r0 + step)
                q.append((src[r0:r1, :], dst[r0:r1, :]))
        conv["q"] = q
        conv["per"] = -(-len(q) // n_calls)

    def conv_step(n=None):
        n = conv["per"] if n is None else n
        for _ in range(min(n, len(conv["q"]))):
            src, dst = conv["q"].pop(0)
            key = b_cv[conv["n"] % 4]
            conv["n"] += 1
            K.dma(K.pool, lambda e, src=src, dst=dst: e.dma_start(out=dst, in_=src), [], [], key=key)

    b_vin, b_vall = Buf(), Buf()
    b_hq = [Buf() for _ in range(NT * 2)]
    b_hf = [Buf() for _ in range(NT * 2)]
    b_oq = {nm: [Buf() for _ in range(NOC)] for nm in "hfg"}
    b_of = {nm: [Buf() for _ in range(NOC)] for nm in "hfg"}
    vview = vall.ap().rearrange("(l p) c -> p l c", p=128)
    pending = []

    def flush_colls():
        while pending:
            pending.pop(0)()

    def h_store(t, tile, buf):
        for half in range(2):
            i = t * 2 + half
            dst = hq.ap()[i].rearrange("(k p) t -> p k t", p=128)
            K.dma(K.sp, lambda e, dst=dst, half=half: e.dma_start(out=dst, in_=tile[:, half * HK:(half + 1) * HK, :]),
                  [buf], [b_hq[i]], key=buf)
            pending.append(lambda i=i: K.coll("AllGather", hq.ap()[i], hf.ap()[i], [b_hq[i]], [b_hf[i]], GROUPS))

    def h_load(t, dst, buf):
        r, tl = t // NT, t % NT
        for half in range(2):
            i = tl * 2 + half
            src = hf.ap()[i].rearrange("(r k p) t -> p r k t", r=4, p=128)[:, r]
            K.dma(K.sp, lambda e, src=src, half=half: e.dma_start(out=dst[:, half * HK:(half + 1) * HK, :], in_=src),
                  [b_hf[i]], [buf], key=buf)

    def make_o_store(nm):
        def o_store(t, src, buf, nh):
            ci, cl = t // OC, t % OC
            dst = oqs[nm].ap()[ci].rearrange("(h p) t -> p h t", p=128)[:, :, cl * 512:(cl + 1) * 512]
            K.dma(K.pool, lambda e: e.dma_start(out=dst, in_=src), [buf], [b_oq[nm][ci]], key=buf)
            if cl == OC - 1:
                K.coll("AllGather", oqs[nm].ap()[ci], ofs[nm].ap()[ci], [b_oq[nm][ci]], [b_of[nm][ci]], GROUPS)
            conv_step()
        return o_store

    seg_cache = {}

    def make_o_load(names):
        def o_load(tt, oT, buf):
            k0 = 0
            for nm in names:
                rows = 4 * (256 if nm in "hf" else 512)
                nk = rows // 128

                def fn(e, nm=nm, nk=nk, k0=k0):
                    if "segC" not in seg_cache:
                        seg_cache["segC"] = (e.partition_id() % 4) * (NT // OC)
                    ci = seg_cache["segC"] + tt // OC
                    src = ofs[nm].ap().rearrange("c (k p) t -> p c k t", p=128)[
                        :, bass.ds(ci, 1), :, (tt % OC) * 512:(tt % OC + 1) * 512]
                    return e.dma_start(out=oT[:, k0:k0 + nk, :].rearrange("p (o k) t -> p o k t", o=1), in_=src)
                K.dma(K.sp, fn, [b for b in b_of[nm]], [buf], key=buf)
                k0 += nk
        return o_load

    C = Consts(K, cst, None)
    cfg1 = dict(cfg)
    cfg1["L"] = 1
    K.begin_phase()
    emit_mod(K, C, cfg1, cT, modw, modb, gmix, gffn, vin.ap().rearrange("p (l c) -> p l c", l=1), out_buf=b_vin)
    K.coll("AllGather", vin, vall, [b_vin], [b_vall], GROUPS)
    K.end_phase()
    K.begin_phase()
    emit_norm0(K, C, cfg, xT_in, vview, None, h_store=h_store, after_tile=flush_colls)
    flush_colls()
    K.end_phase()
    for l in range(L):
        i = l // 2
        if l % 2 == 0:
            conv_plan(l, 2 * (S // 512))
            K.begin_phase()
            emit_hgrn(K, C, cfg, i, None, ab[i]["wh"], ab[i]["lbl"], ab[i]["again"], None, h_load=h_load,
                      o_store=make_o_store("h"))
            K.end_phase()
            K.begin_phase()
            emit_fox(K, C, cfg, None, ab[i]["wf"], ab[i]["qkg"], ab[i]["fbias"], None, h_load=h_load,
                     o_store=make_o_store("f"))
            conv_step(len(conv["q"]))
            K.end_phase()
            names = "hf"
        else:
            conv_plan(l, S // 512)
            K.begin_phase()
            emit_gla(K, C, cfg, None, gl[i]["wg"], gl[i]["wgu"], gl[i]["bg"], gl[i]["ggain"], None, h_load=h_load,
                     o_store=make_o_store("g"))
            conv_step(len(conv["q"]))
            K.end_phase()
            names = "g"
        last = (l == L - 1)
        K.begin_phase()
        dw = (wbo.ap(), wbi.ap(), wbf.ap()) if PRECAST else (wo[l], wfi[l], wfo[l])
        emit_dense(K, C, cfg, l, xT_in if l == 0 else xs.ap(), None, dw[0], dw[1], dw[2], vview,
                   xTo if last else xs.ap(), None, o_load=make_o_load(names), h_store=None if last else h_store,
                   after_proj=flush_colls)
        flush_colls()
        K.end_phase()
    gstack.close()
    return nc


def dense_k_perm(kind):
    perm = []
    if kind == "ab":
        for typ in range(2):
            for r in range(4):
                for loc in range(256):
                    perm.append(r * 512 + typ * 256 + loc)
    else:
        perm = list(range(2048))
    return np.asarray(perm)


def kernel_fused(cfg, x, c, mod_w, mod_b, norm_mix_gain, norm_ffn_gain, ab_w_in, ab_w_out, hgrn_lb_logits,
                 hgrn_out_gain, fox_q_gain, fox_k_gain, fox_f_bias, gla_w_in, gla_w_gate_up, gla_b_gate, gla_out_gain,
                 gla_w_out, ffn_w_in, ffn_w_out):
    L, D, KD, T, S, DFF = cfg["L"], cfg["D"], cfg["KD"], cfg["T"], cfg["S"], cfg["DFF"]
    f32 = lambda a: np.ascontiguousarray(np.asarray(a, dtype=np.float32))
    x, c, mod_w, mod_b = f32(x), f32(c), f32(mod_w), f32(mod_b)
    consts = make_consts()
    ab_rows = []
    for j in range(4):
        for typ in range(2):
            for hl in range(2):
                base = typ * 1024 + (2 * j + hl) * 128
                ab_rows.extend(range(base, base + 128))
    ab_rows = np.asarray(ab_rows)
    wfi, wfo = f32(ffn_w_in), f32(ffn_w_out)
    shared = {"consts": consts, "wfi": wfi, "wfo": wfo}
    for l in range(L):
        i = l // 2
        if l % 2 == 0:
            shared["wo%d" % l] = np.ascontiguousarray(f32(ab_w_out[i])[ab_rows[dense_k_perm("ab")]])
        else:
            shared["wo%d" % l] = np.ascontiguousarray(f32(gla_w_out[i])[dense_k_perm("gla")])
    lbl_all = f32(hgrn_lb_logits)
    ims = []
    for cid in range(8):
        b, j = cid // 4, cid % 4
        im = dict(shared)
        im["cT"] = _fm(c[b], KD)
        im["modw"] = mod_w[j:j + 1]
        im["modb"] = mod_b[j:j + 1]
        im["gmix"] = _fm(f32(norm_mix_gain)[j:j + 1], KD)
        im["gffn"] = _fm(f32(norm_ffn_gain)[j:j + 1], KD)
        im["xT"] = np.ascontiguousarray(x[b, j * T:(j + 1) * T, :].T)
        hh = (2 * j, 2 * j + 1)
        for i in range((L + 1) // 2):
            w = f32(ab_w_in[i])
            cs = lambda base, h_: w[:, base + h_ * 128:base + (h_ + 1) * 128]
            im["wh%d" % i] = np.ascontiguousarray(np.concatenate(
                [cs(0, hh[0]), cs(1024, hh[0]), cs(0, hh[1]), cs(1024, hh[1]),
                 cs(2048, hh[0]), cs(2048, hh[1]), cs(3072, hh[0]), cs(3072, hh[1])], axis=1))
            im["lbl%d" % i] = f32(np.stack([np.stack([lbl_all[ly, h_ * 128:(h_ + 1) * 128] for h_ in hh], axis=1)
                                            for ly in range(2)], axis=1))
            im["again%d" % i] = f32(hgrn_out_gain[i][2 * j:2 * j + 2]).reshape(1, 256)
            im["wf%d" % i] = np.ascontiguousarray(np.concatenate(
                [cs(4096, hh[0]), cs(5120, hh[0]), cs(4096, hh[1]), cs(5120, hh[1]),
                 cs(6144, hh[0]), cs(6144, hh[1]), cs(7168, hh[0]), cs(7168, hh[1]),
                 w[:, 8192 + hh[0]:8192 + hh[0] + 1], w[:, 8192 + hh[1]:8192 + hh[1] + 1]], axis=1))
            im["qkg%d" % i] = f32(np.stack([fox_q_gain[i], fox_k_gain[i]], axis=1))
            im["fbias%d" % i] = f32(fox_f_bias[i][2 * j:2 * j + 2]).reshape(1, 2)
        for i in range(L // 2):
            w = f32(gla_w_in[i])
            im["wg%d" % i] = np.ascontiguousarray(np.concatenate(
                [w[:, j * 256:(j + 1) * 256], w[:, 1024 + j * 256:1024 + (j + 1) * 256], w[:, 6144:6160],
                 w[:, 2048 + j * 512:2048 + (j + 1) * 512], w[:, 4096 + j * 512:4096 + (j + 1) * 512]], axis=1))
            im["wgu%d" % i] = f32(gla_w_gate_up[i][:, j * 256:(j + 1) * 256])
            im["bg%d" % i] = f32(np.asarray(gla_b_gate[i][j * 256:(j + 1) * 256]).reshape(2, 128).T)
            im["ggain%d" % i] = f32(gla_out_gain[i]).reshape(1, 512)
        ims.append(im)
    nc = _prog("fused", lambda: build_fused(cfg))
    r = _run(nc, ims)
    out = np.empty((2, S, D), np.float32)
    for cid in range(8):
        b, j = cid // 4, cid % 4
        out[b, j * T:(j + 1) * T, :] = r[cid]["xTo"].T
    return out


def kernel(**inputs):
    return kernel_fused(CFG, **inputs)
```

```python
import contextlib
import numpy as np
import ml_dtypes
import concourse.bass as bass
import concourse.mybir as mybir
from concourse.bass_utils import run_bass_kernel_spmd

F32 = mybir.dt.float32
BF16 = mybir.dt.bfloat16
AF = mybir.ActivationFunctionType
ALU = mybir.AluOpType
EPS = 1e-6


class Sem:
    def __init__(self, h):
        self.h = h
        self.n = 0


class Buf:
    __slots__ = ("w", "r", "ds")

    def __init__(self):
        self.w = None
        self.r = {}
        self.ds = None


class Eng:
    def __init__(self, name, sem, is_pe=False):
        self.name = name
        self.sem = sem
        self.prog = []
        self.seen = {}
        self.is_pe = is_pe


class Kern:
    def __init__(self, nc, stack, n_dma_sems=12):
        self.nc = nc
        self.stack = stack
        mk = lambda n: Sem(stack.enter_context(nc.semaphore(n)))
        self.pe = Eng("tensor", mk("s_pe"), True)
        self.act = Eng("scalar", mk("s_act"))
        self.dve = Eng("vector", mk("s_dve"))
        self.pool = Eng("gpsimd", mk("s_pool"))
        self.sp = Eng("sync", mk("s_sp"))
        self.engs = [self.pe, self.act, self.dve, self.pool, self.sp]
        self.dsems = [None] * n_dma_sems
        self.nbuf = 0
        self.ccs = [mk("s_cc%d" % i) for i in range(4)]
        self.dsems.extend(self.ccs)
        self.ncoll = 0
        self.free_sems = []
        self.phase_sems = []
        self.pstack = None
        self.nphase = 0

    def sb(self, name, shape, dt):
        st = self.pstack if self.pstack is not None else self.stack
        return st.enter_context(self.nc.sbuf_tensor("%s_%d" % (name, self.nphase), list(shape), dt))

    def ps(self, name, shape, dt):
        st = self.pstack if self.pstack is not None else self.stack
        return st.enter_context(self.nc.psum_tensor("%s_%d" % (name, self.nphase), list(shape), dt))

    def begin_phase(self):
        self.pstack = contextlib.ExitStack()
        self.nphase += 1

    def end_phase(self):
        for e in self.engs:
            self.final_wait(e)
        self.emit()
        for e in self.engs:
            e.prog = []
        self.free_sems.extend(self.phase_sems)
        self.phase_sems = []
        self.pstack.close()
        self.pstack = None

    def coll(self, kind, in_t, out_t, R, W, groups):
        eng = self.pool
        cc = self.ccs[self.ncoll % len(self.ccs)]
        self.ncoll += 1
        waits = self._waits(eng, R, W)
        if cc.n > 0 and eng.seen.get(cc, 0) < cc.n:
            eng.seen[cc] = cc.n
            waits.append((cc, cc.n))
        cc.n += 1
        ev = (cc, cc.n)
        in_ap = in_t if isinstance(in_t, bass.AP) else in_t.ap()
        out_ap = out_t if isinstance(out_t, bass.AP) else out_t.ap()
        fn = lambda e: e.collective_compute(kind, ALU.bypass, replica_groups=groups, ins=[in_ap.opt()],
                                            outs=[out_ap.opt()])
        eng.prog.append((waits, fn, (cc, 1)))
        self._commit(ev, R, W)

    def _waits(self, eng, R, W):
        needs = {}

        def need(ev):
            s, v = ev
            if needs.get(s, 0) < v:
                needs[s] = v

        for b in R:
            if b.w is not None:
                if b.w[0] is eng.sem and eng.is_pe:
                    continue
                need(b.w)
        for b in W:
            if b.w is not None and b.w[0] is not eng.sem:
                need(b.w)
            for s, v in b.r.items():
                if s is not eng.sem:
                    need((s, v))
        out = []
        for s, v in needs.items():
            if eng.seen.get(s, 0) < v:
                eng.seen[s] = v
                out.append((s, v))
        return out

    def _commit(self, ev, R, W):
        for b in R:
            if b.r.get(ev[0], 0) < ev[1]:
                b.r[ev[0]] = ev[1]
        for b in W:
            b.w = ev
            b.r = {}

    def op(self, eng, fn, R=(), W=(), inc=True):
        waits = self._waits(eng, R, W)
        if inc:
            eng.sem.n += 1
            ev = (eng.sem, eng.sem.n)
        else:
            ev = (eng.sem, eng.sem.n + 1)
        eng.prog.append((waits, fn, (eng.sem, 1) if inc else None))
        self._commit(ev, R, W)

    def dma(self, eng, fn, R, W, dsem=None, key=None):
        if key is None:
            key = W[0] if W else R[0]
        if key.ds is None:
            if self.free_sems:
                key.ds = self.free_sems.pop()
            else:
                key.ds = Sem(self.stack.enter_context(self.nc.semaphore("s_d%d" % self.nbuf)))
                self.nbuf += 1
                self.dsems.append(key.ds)
            if self.pstack is not None:
                self.phase_sems.append(key.ds)
        dsem = key.ds
        waits = self._waits(eng, R, W)
        if dsem.n > 0 and eng.seen.get(dsem, 0) < dsem.n:
            eng.seen[dsem] = dsem.n
            waits.append((dsem, dsem.n))
        dsem.n += 16
        ev = (dsem, dsem.n)
        eng.prog.append((waits, fn, (dsem, 16)))
        self._commit(ev, R, W)

    def final_wait(self, eng):
        waits = []
        for s in [e.sem for e in self.engs] + [d for d in self.dsems if d is not None]:
            if s is not eng.sem and s.n > 0 and eng.seen.get(s, 0) < s.n:
                eng.seen[s] = s.n
                waits.append((s, s.n))
        eng.prog.append((waits, None, None))

    def emit(self):
        nc = self.nc

        def replay(eng):
            def run(e):
                for waits, fn, inc in eng.prog:
                    for s, v in waits:
                        e.wait_ge(s.h, v)
                    if fn is not None:
                        ins = fn(e)
                        if inc is not None:
                            ins.then_inc(inc[0].h, inc[1])
            return run

        with nc.Block() as block:
            block.tensor(replay(self.pe))
            block.scalar(replay(self.act))
            block.vector(replay(self.dve))
            block.gpsimd(replay(self.pool))
            block.sync(replay(self.sp))


def _chunks(n, m):
    return [(i, min(m, n - i)) for i in range(0, n, m)]


class Consts:
    def __init__(self, K, cdram, dsem):
        nc = K.nc
        self.f = K.sb("c_f32", [128, 5 * 128], F32)
        self.b = K.sb("c_bf16", [128, 5 * 128], BF16)
        self.buf = Buf()
        f, b = self.f, self.b
        K.dma(K.sp, lambda e: e.dma_start(out=f[:], in_=cdram[:, :]), [], [self.buf], dsem)
        K.op(K.dve, lambda e: e.tensor_copy(out=b[:], in_=f[:]), [self.buf], [self.buf])

    def ident_b(self, n=128):
        return self.b[0:n, 0:n]

    def tri_b(self, n=128):
        return self.b[0:n, 128:128 + n]

    def tri_f(self, n=128):
        return self.f[0:n, 128:128 + n]

    def ones_f(self, p=128, n=128):
        return self.f[0:p, 256:256 + n]

    def ones_b(self, p=128, n=128):
        return self.b[0:p, 256:256 + n]

    def sel_f(self):
        return self.f[:, 384:512]

    def ident_f(self):
        return self.f[:, 0:128]

    def negmask_b(self):
        return self.b[:, 512:640]


def make_consts():
    c = np.zeros((128, 5 * 128), np.float32)
    c[:, 0:128] = np.eye(128)
    c[:, 128:256] = np.triu(np.ones((128, 128)))
    c[:, 256:384] = 1.0
    c[127, 384:512] = 1.0
    c[:, 512:640] = -30000.0 * np.tril(np.ones((128, 128)), -1)
    return c


def emit_norm(K, C, xT, xbuf, KD, D, G, Sh, vbuf, outT, obuf, ssbank, ssb, tmp):
    sq, sqb, rs, rsb, tm, tmb = tmp["sq"], tmp["sqb"], tmp["rs"], tmp["rsb"], tmp["tm"], tmp["tmb"]
    for j in range(KD):
        s = j % 2
        K.op(K.act, lambda e, j=j, s=s: e.activation(out=sq[:, s, :], in_=xT[:, j, :], func=AF.Square),
             [xbuf], [sqb[s]])
        K.op(K.pe, lambda e, j=j, s=s: e.matmul(ssbank[:, :], lhsT=C.ones_f(), rhs=sq[:, s, :],
                                               start=(j == 0), stop=(j == KD - 1)),
             [sqb[s], C.buf], [ssb], inc=True)
    K.op(K.act, lambda e: e.activation(out=rs[:, :], in_=ssbank[:, :], func=AF.Sqrt, scale=1.0 / D, bias=EPS),
         [ssb], [rsb])
    K.op(K.dve, lambda e: e.reciprocal(out=rs[:, :], in_=rs[:, :]), [rsb], [rsb])
    for j in range(KD):
        s = j % 2
        K.op(K.dve, lambda e, j=j, s=s: e.scalar_tensor_tensor(out=tm[:, s, :], in0=xT[:, j, :], scalar=G[:, j:j + 1],
                                                             in1=rs[:, :], op0=ALU.mult, op1=ALU.mult),
             [xbuf, rsb, vbuf], [tmb[s]])
        K.op(K.act, lambda e, j=j, s=s: e.activation(out=outT[:, j, :], in_=tm[:, s, :], func=AF.Identity,
                                                   bias=Sh[:, j:j + 1], scale=1.0),
             [tmb[s], vbuf], [obuf])


def emit_mod(K, C, cfg, cT_d, modw_d, modb_d, gmix_d, gffn_d, vec_d, out_buf=None):
    L, D, KD = cfg["L"], cfg["D"], cfg["KD"]
    W6 = 6 * D
    CG = min(2048, W6)
    NB = CG // 512
    assert W6 % CG == 0
    ds_a, ds_w = K.dsems[0], K.dsems[1]
    cT = K.sb("m_cT", [128, KD], F32)
    gm = K.sb("m_gm", [128, L, KD], F32)
    gf = K.sb("m_gf", [128, L, KD], F32)
    row = K.sb("m_row", [1, W6], F32)
    brow = K.sb("m_brow", [1, W6], F32)
    wp = K.sb("m_wp", [128, 3, CG], F32)
    mt = K.sb("m_mt", [128, 6 * KD], F32)
    vec = K.sb("m_vec", [128, L, 6 * KD], F32)
    banks = [K.ps("m_ps%d" % i, [128, 512], F32) for i in range(NB)]
    tp = K.ps("m_tp", [128, 512], F32)
    b_c, b_g, b_row, b_brow, b_mt, b_vec, b_tp = Buf(), Buf(), Buf(), Buf(), Buf(), Buf(), Buf()
    b_wp = [Buf() for _ in range(3)]
    b_bk = [Buf() for _ in range(NB)]
    K.dma(K.sp, lambda e: e.dma_start(out=cT[:], in_=cT_d[:, :]), [], [b_c], ds_a)
    K.dma(K.sp, lambda e: e.dma_start(out=gm[:], in_=gmix_d[:, :, :]), [], [b_g], ds_a)
    K.dma(K.sp, lambda e: e.dma_start(out=gf[:], in_=gffn_d[:, :, :]), [], [b_g], ds_a)
    K.op(K.act, lambda e: e.activation(out=cT[:], in_=cT[:], func=AF.Silu), [b_c], [b_c])
    ip = 0
    for l in range(L):
        K.dma(K.sp, lambda e, l=l: e.dma_start(out=brow[:], in_=modb_d[l:l + 1, :]), [], [b_brow], ds_a)
        for cg in range(W6 // CG):
            for k in range(KD):
                s = ip % 3
                ip += 1
                K.dma(K.sp, lambda e, l=l, cg=cg, k=k, s=s: e.dma_start(
                    out=wp[:, s, :], in_=modw_d[l, k * 128:(k + 1) * 128, cg * CG:(cg + 1) * CG]),
                    [], [b_wp[s]], ds_w)
                for q in range(NB):
                    K.op(K.pe, lambda e, k=k, s=s, q=q: e.matmul(
                        banks[q][0:1, :], lhsT=cT[:, k:k + 1], rhs=wp[:, s, q * 512:(q + 1) * 512],
                        start=(k == 0), stop=(k == KD - 1)), [b_c, b_wp[s]], [b_bk[q]])
            for q in range(NB):
                c0 = cg * CG + q * 512
                K.op(K.dve, lambda e, q=q, c0=c0: e.tensor_tensor(
                    out=row[0:1, c0:c0 + 512], in0=banks[q][0:1, :], in1=brow[0:1, c0:c0 + 512], op=ALU.add),
                    [b_bk[q], b_brow], [b_row])
        for c in range(6 * KD):
            K.op(K.pe, lambda e, c=c: e.matmul(tp[:, c:c + 1], lhsT=row[0:1, c * 128:(c + 1) * 128],
                                               rhs=C.ones_f(1, 1), start=True, stop=True),
                 [b_row, C.buf], [b_tp])
        K.op(K.act, lambda e: e.activation(out=mt[:, :], in_=tp[:, 0:6 * KD], func=AF.Identity), [b_tp], [b_mt])
        for (dst, src_sc, gn) in ((0, 1, gm), (3, 4, gf)):
            K.op(K.dve, lambda e, l=l, dst=dst, src_sc=src_sc, gn=gn: e.scalar_tensor_tensor(
                out=vec[:, l, dst * KD:(dst + 1) * KD], in0=mt[:, src_sc * KD:(src_sc + 1) * KD], scalar=1.0,
                in1=gn[:, l, :], op0=ALU.add, op1=ALU.mult), [b_mt, b_g], [b_vec])
        for (dst, src) in ((1, 0), (2, 2), (4, 3), (5, 5)):
            K.op(K.dve, lambda e, l=l, dst=dst, src=src: e.tensor_copy(
                out=vec[:, l, dst * KD:(dst + 1) * KD], in_=mt[:, src * KD:(src + 1) * KD]), [b_mt], [b_vec])
    K.dma(K.sp, lambda e: e.dma_start(out=vec_d[:, :, :], in_=vec[:]), [b_vec], [out_buf] if out_buf else [], key=b_vec)


def norm_scratch(K, pfx):
    return {
        "sq": K.sb(pfx + "sq", [128, 2, 512], F32), "sqb": [Buf(), Buf()],
        "rs": K.sb(pfx + "rs", [128, 512], F32), "rsb": Buf(),
        "tm": K.sb(pfx + "tm", [128, 2, 512], F32), "tmb": [Buf(), Buf()],
    }


def emit_norm0(K, C, cfg, xT_d, vec_d, hT_d, h_store=None, after_tile=None):
    D, KD, T = cfg["D"], cfg["KD"], cfg["T"]
    ds_a = K.dsems[0]
    vec = K.sb("n_vec", [128, 6 * KD], F32)
    xT = K.sb("n_xT", [128, 2, KD, 512], F32)
    hT = K.sb("n_hT", [128, 2, KD, 512], BF16)
    ss = K.ps("n_ss", [128, 512], F32)
    b_v, b_ss = Buf(), Buf()
    b_x, b_h = [Buf(), Buf()], [Buf(), Buf()]
    tmp = norm_scratch(K, "n_")
    K.dma(K.sp, lambda e: e.dma_start(out=vec[:], in_=vec_d[:, 0, :]), [], [b_v], ds_a)
    xv = xT_d.rearrange("(k p) t -> p k t", p=128)
    hv = None if h_store is not None else hT_d.rearrange("(k p) t -> p k t", p=128)
    for t in range(T // 512):
        s = t % 2
        K.dma(K.sp, lambda e, t=t, s=s: e.dma_start(out=xT[:, s], in_=xv[:, :, t * 512:(t + 1) * 512]),
              [], [b_x[s]], ds_a)
        emit_norm(K, C, xT[:, s], b_x[s], KD, D, vec[:, 0:KD], vec[:, KD:2 * KD], b_v, hT[:, s], b_h[s], ss, b_ss, tmp)
        if h_store is not None:
            h_store(t, hT[:, s], b_h[s])
            if after_tile is not None:
                after_tile()
        else:
            K.dma(K.sp, lambda e, t=t, s=s: e.dma_start(out=hv[:, :, t * 512:(t + 1) * 512], in_=hT[:, s]),
                  [b_h[s]], [], ds_a)


def emit_dense(K, C, cfg, l, xT_d, oT_d, wo_d, wfi_d, wfo_d, vec_d, xTo_d, hTo_d, o_load=None, h_store=None,
               after_proj=None):
    D, KD, T, DFF, KF, KO, L = cfg["D"], cfg["KD"], cfg["T"], cfg["DFF"], cfg["KF"], cfg["KO"], cfg["L"]
    DG = D // 512 if D >= 512 else 1
    GW = min(512, D)
    GC = GW // 128
    FG = DFF // 512
    assert DFF % 512 == 0
    ds_a, ds_w, ds_s = K.dsems[0], K.dsems[1], K.dsems[2]
    last = hTo_d is None and h_store is None
    vec = K.sb("d_vec", [128, 2, 6 * KD], F32)
    xin = K.sb("d_xin", [128, KD, 512], F32)
    xT = K.sb("d_xT", [128, KD, 512], F32)
    oT2 = K.sb("d_oT", [128, 1, KO, 512], BF16)
    oT = K.sb("d_hT", [128, KD, 512], BF16)
    halves = [list(range(0, (FG + 1) // 2)), list(range((FG + 1) // 2, FG))]
    AH = 4 * len(halves[0])
    aT = K.sb("d_aT", [128, AH, 512], BF16)
    sa = K.sb("d_sa", [128, 2, 512], BF16)
    WB = 4
    WSZ = max(KO * GW, KD * 512, 16 * GW)
    wb = K.sb("d_wb", [128, WB, WSZ], BF16)
    banks = [K.ps("d_ps%d" % i, [128, 512], F32) for i in range(8)]
    b_bk = [Buf() for _ in range(8)]
    b_v, b_x, b_o, b_a, b_xin = Buf(), Buf(), Buf(), Buf(), Buf()
    b_o2 = [Buf(), Buf()]
    b_sa = [Buf(), Buf()]
    b_wb = [Buf() for _ in range(WB)]
    tmp = norm_scratch(K, "d_")
    K.dma(K.sp, lambda e: e.dma_start(out=vec[:, 0, :], in_=vec_d[:, l, :]), [], [b_v], ds_a)
    if not last:
        K.dma(K.sp, lambda e: e.dma_start(out=vec[:, 1, :], in_=vec_d[:, l + 1, :]), [], [b_v], ds_a)
    g1 = vec[:, 0, 2 * KD:3 * KD]
    G2 = vec[:, 0, 3 * KD:4 * KD]
    Sh2 = vec[:, 0, 4 * KD:5 * KD]
    g2 = vec[:, 0, 5 * KD:6 * KD]
    G1n = vec[:, 1, 0:KD]
    Sh1n = vec[:, 1, KD:2 * KD]
    xv = xT_d.rearrange("(k p) t -> p k t", p=128)
    ov = None if o_load is not None else oT_d.rearrange("(k p) t -> p k t", p=128)
    xov = xTo_d.rearrange("(k p) t -> p k t", p=128)
    hov = None if (last or h_store is not None) else hTo_d.rearrange("(k p) t -> p k t", p=128)
    wov = wo_d.rearrange("(k p) c -> p k c", p=128)
    wiv = wfi_d.rearrange("(k p) c -> p k c", p=128)
    wfv = wfo_d.rearrange("(k p) c -> p k c", p=128)
    st = {"w": 0, "bank": 0}

    def load_w(src_ap, kk, cc):
        s = st["w"] % WB
        st["w"] += 1
        dst = wb[:, s, 0:kk * cc].rearrange("p (k c) -> p k c", k=kk)
        K.dma(K.pool, lambda e: e.dma_start(out=dst, in_=src_ap), [], [b_wb[s]], ds_w)
        return dst, b_wb[s]

    def proj_group(pieces, rhsT, rbuf, nk, og, gvec, src=None, sbuf=None, kofs=0):
        if src is None:
            src, sbuf = xT, b_x
        base = (st["bank"] % 2) * 4
        st["bank"] += 1
        kdone = 0
        for (wt, wbuf, k0, kn) in pieces:
            for kk in range(kn):
                for i in range(GC):
                    K.op(K.pe, lambda e, wt=wt, kk=kk, i=i, k=k0 + kk - kofs: e.matmul(
                        banks[base + i][:, :], lhsT=wt[:, kk, i * 128:(i + 1) * 128], rhs=rhsT[:, k, :],
                        start=(k == 0), stop=(k == nk - 1)), [wbuf, rbuf], [b_bk[base + i]],
                        inc=(k0 + kk - kofs == nk - 1 or kk == kn - 1))
        for i in range(GC):
            j = og * GC + i
            K.op(K.dve, lambda e, i=i, j=j: e.scalar_tensor_tensor(
                out=xT[:, j, :], in0=banks[base + i][:, :], scalar=gvec[:, j:j + 1], in1=src[:, j, :],
                op0=ALU.mult, op1=ALU.add), [b_bk[base + i], sbuf, b_v], [b_x])

    NTL = T // 512

    def loads(t):
        tsl = slice(t * 512, (t + 1) * 512)
        K.dma(K.sp, lambda e: e.dma_start(out=xin[:], in_=xv[:, :, tsl]), [], [b_xin])
        if o_load is not None:
            o_load(t, oT2[:, 0], b_o2[0])
        else:
            K.dma(K.sp, lambda e: e.dma_start(out=oT2[:, 0], in_=ov[:, :, tsl]), [], [b_o2[0]])

    loads(0)
    for t in range(NTL):
        tsl = slice(t * 512, (t + 1) * 512)
        for og in range(DG):
            wt, wbuf = load_w(wov[:, :, og * GW:(og + 1) * GW], KO, GW)
            proj_group([(wt, wbuf, 0, KO)], oT2[:, 0], b_o2[0], KO, og, g1, src=xin, sbuf=b_xin)
        if t + 1 < NTL:
            loads(t + 1)
        emit_norm(K, C, xT, b_x, KD, D, G2, Sh2, b_v, oT, b_o, banks[7], b_bk[7], tmp)
        for hg in halves:
            if not hg:
                continue
            for fg in hg:
                wa, wab = load_w(wiv[:, :, fg * 512:(fg + 1) * 512], KD, 512)
                wu, wub = load_w(wiv[:, :, DFF + fg * 512:DFF + (fg + 1) * 512], KD, 512)
                for i in range(4):
                    j = fg * 4 + i
                    jl = j - hg[0] * 4
                    ba = (2 * j) % 6
                    bu = ba + 1
                    for (wt, wbf, bk) in ((wa, wab, ba), (wu, wub, bu)):
                        for k in range(KD):
                            K.op(K.pe, lambda e, wt=wt, k=k, i=i, bk=bk: e.matmul(
                                banks[bk][:, :], lhsT=wt[:, k, i * 128:(i + 1) * 128], rhs=oT[:, k, :],
                                start=(k == 0), stop=(k == KD - 1)), [wbf, b_o], [b_bk[bk]], inc=(k == KD - 1))
                    s_ = j % 2
                    K.op(K.act, lambda e, s_=s_, ba=ba: e.activation(out=sa[:, s_, :], in_=banks[ba][:, :], func=AF.Silu),
                         [b_bk[ba]], [b_sa[s_]])
                    K.op(K.dve, lambda e, s_=s_, jl=jl, bu=bu: e.tensor_tensor(
                        out=aT[:, jl, :], in0=sa[:, s_, :], in1=banks[bu][:, :], op=ALU.mult),
                        [b_sa[s_], b_bk[bu]], [b_a])
            kbase, nkh = hg[0] * 4, len(hg) * 4
            if hg is halves[-1] or not halves[-1]:
                if after_proj is not None:
                    after_proj()
            for og in range(DG):
                pieces = []
                for (k0, kn) in _chunks(nkh, 16):
                    wt, wbuf = load_w(wfv[:, kbase + k0:kbase + k0 + kn, og * GW:(og + 1) * GW], kn, GW)
                    pieces.append((wt, wbuf, kbase + k0, kn))
                proj_group(pieces, aT, b_a, nkh, og, g2, kofs=kbase)
        K.dma(K.sp, lambda e, tsl=tsl: e.dma_start(out=xov[:, :, tsl], in_=xT[:]), [b_x], [], ds_s)
        if not last:
            emit_norm(K, C, xT, b_x, KD, D, G1n, Sh1n, b_v, oT, b_o, banks[7], b_bk[7], tmp)
            if h_store is not None:
                h_store(t, oT, b_o)
            else:
                K.dma(K.sp, lambda e, tsl=tsl: e.dma_start(out=hov[:, :, tsl], in_=oT[:, 0:KD, :]), [b_o], [], ds_s)


def _new():
    nc = bass.Bass("TRN2", target_bir_lowering=False)
    stack = contextlib.ExitStack()
    K = Kern(nc, stack)
    return nc, stack, K


def _din(nc, name, shape, dt=F32):
    return nc.dram_tensor(name, list(shape), dt, kind="ExternalInput").ap()


def _dout(nc, name, shape, dt=F32):
    return nc.dram_tensor(name, list(shape), dt, kind="ExternalOutput").ap()


def _finish(K, stack):
    K.final_wait(K.sp)
    K.emit()
    stack.close()


def build_mod(cfg):
    L, D, KD = cfg["L"], cfg["D"], cfg["KD"]
    nc, stack, K = _new()
    cst = _din(nc, "consts", [128, 640])
    cT = _din(nc, "cT", [128, KD])
    modw = _din(nc, "modw", [L, D, 6 * D])
    modb = _din(nc, "modb", [L, 6 * D])
    gmix = _din(nc, "gmix", [128, L, KD])
    gffn = _din(nc, "gffn", [128, L, KD])
    vec = _dout(nc, "vec", [128, L, 6 * KD])
    C = Consts(K, cst, K.dsems[0])
    emit_mod(K, C, cfg, cT, modw, modb, gmix, gffn, vec)
    _finish(K, stack)
    return nc


def build_norm0(cfg):
    L, D, KD, T = cfg["L"], cfg["D"], cfg["KD"], cfg["T"]
    nc, stack, K = _new()
    cst = _din(nc, "consts", [128, 640])
    xT = _din(nc, "xT", [D, T])
    vec = _din(nc, "vec", [128, L, 6 * KD])
    hT = _dout(nc, "hT", [D, T], BF16)
    C = Consts(K, cst, K.dsems[0])
    emit_norm0(K, C, cfg, xT, vec, hT)
    _finish(K, stack)
    return nc


def build_dense(cfg, last):
    L, D, KD, T, DFF, KO = cfg["L"], cfg["D"], cfg["KD"], cfg["T"], cfg["DFF"], cfg["KO"]
    nc, stack, K = _new()
    cst = _din(nc, "consts", [128, 640])
    xT = _din(nc, "xT", [D, T])
    oT = _din(nc, "oT", [KO * 128, T], BF16)
    wo = _din(nc, "wo", [KO * 128, D])
    wfi = _din(nc, "wfi", [D, 2 * DFF])
    wfo = _din(nc, "wfo", [DFF, D])
    vec = _din(nc, "vec", [128, 2, 6 * KD])
    xTo = _dout(nc, "xTo", [D, T])
    hTo = None if last else _dout(nc, "hTo", [D, T], BF16)
    C = Consts(K, cst, K.dsems[0])
    cfg2 = dict(cfg)
    cfg2["L"] = 2
    emit_dense(K, C, cfg2, 0, xT, oT, wo, wfi, wfo, vec, xTo, hTo)
    _finish(K, stack)
    return nc


def emit_hgrn(K, C, cfg, ab_idx, hT_d, w_d, lbl_d, gain_d, oT_d, h_load=None, o_store=None):
    D, KD, S = cfg["D"], cfg["KD"], cfg["S"]
    ds_a, ds_w, ds_s = K.dsems[0], K.dsems[1], K.dsems[2]
    CH, NCH = 64, 8
    W = K.sb("h_W", [128, KD, 1024], BF16)
    hT = K.sb("h_hT", [128, 2, KD, 512], BF16)
    lbl = K.sb("h_lbl", [128, 2, 2], F32)
    lbv = K.sb("h_lbv", [128, 3, 2], F32)
    gain = K.sb("h_gain", [64, 256], F32)
    ones = K.sb("h_ones", [128, 64], F32)
    names = ["sig", "lf", "key", "qs", "cum", "cm", "cl", "e0", "e1", "e2"]
    hf = [{n: K.sb("h_%s%d" % (n, h), [128, 512], F32) for n in names} for h in range(2)]
    hb = [{n: K.sb("h_%s%d" % (n, h), [128, 512], BF16) for n in ["qh", "qt", "kt", "kh", "k0"]} for h in range(2)]
    eL = [K.sb("h_eL%d" % h, [128, NCH], F32) for h in range(2)]
    Sst = [K.sb("h_S%d" % h, [128, 128], F32) for h in range(2)]
    Sbf = [K.sb("h_Sb%d" % h, [128, 2, 128], BF16) for h in range(2)]
    vc = K.sb("h_vc", [64, 2, 256], BF16)
    gs = K.sb("h_gs", [64, 2, 256], F32)
    PT = K.sb("h_PT", [64, 2, 2, 64], BF16)
    khs = K.sb("h_khs", [64, 2, 2, 128], BF16)
    junk = K.sb("h_junk", [64, 128], F32)
    ssq = K.sb("h_ssq", [64, 2, 2], F32)
    og = K.sb("h_og", [64, 2, 128], BF16)
    oTt = K.sb("h_oTt", [128, 2, 2, 512], BF16)
    fb2 = [K.ps("h_fb%d" % i, [128, 512], F32) for i in range(2)]
    fb = [fb2[0], fb2[1], fb2[0], fb2[1]]
    tb = [K.ps("h_tb%d" % i, [128, 512], F32) for i in range(2)]
    bkO = K.ps("h_bkO", [128, 512], F32)
    bkU = K.ps("h_bkU", [128, 512], F32)
    bkT = [K.ps("h_bkT%d" % i, [128, 1024], BF16) for i in range(2)]
    B = lambda n: [Buf() for _ in range(n)]
    b_W, b_lb, b_gain, b_ones = Buf(), Buf(), Buf(), Buf()
    b_hT, b_tb = B(2), B(2)
    b_fb2 = B(2)
    b_fb = [b_fb2[0], b_fb2[1], b_fb2[0], b_fb2[1]]
    b_hf = [{n: Buf() for n in names} for _ in range(2)]
    b_hb = [{n: Buf() for n in ["qh", "qt", "kt", "kh", "k0"]} for _ in range(2)]
    b_eL, b_S, b_Sbf = B(2), B(2), [B(2), B(2)]
    b_vc, b_gs, b_PT, b_khs, b_junk, b_ssq, b_og = B(2), B(2), B(2), B(2), Buf(), B(2), B(2)
    b_bkO, b_bkU, b_bkT = Buf(), Buf(), B(2)
    b_oTt = B(2)
    A = K.op
    wv = w_d.rearrange("(k p) c -> p k c", p=128)
    for half in range(2):
        K.dma(K.pool, lambda e, half=half: e.dma_start(out=W[:, :, half * 512:(half + 1) * 512],
                                                       in_=wv[:, :, half * 512:(half + 1) * 512]), [], [b_W], ds_w)
    K.dma(K.sp, lambda e: e.dma_start(out=lbl[:], in_=lbl_d[:, :, :]), [], [b_lb], ds_a)
    K.dma(K.sp, lambda e: e.dma_start(out=gain[:], in_=gain_d[0:1, :].partition_broadcast(64)), [], [b_gain], ds_a)
    A(K.dve, lambda e: e.memset(ones[:], 1.0), [], [b_ones])
    if ab_idx == 0:
        A(K.dve, lambda e: e.memset(lbv[:, 0, :], 0.0), [b_lb], [b_lb])
    else:
        A(K.dve, lambda e: e.tensor_tensor(out=lbv[:, 0, :], in0=lbl[:, 1, :], in1=lbl[:, 0, :], op=ALU.subtract),
          [b_lb], [b_lb])
        A(K.act, lambda e: e.activation(out=lbv[:, 0, :], in_=lbv[:, 0, :], func=AF.Sigmoid), [b_lb], [b_lb])
        A(K.dve, lambda e: e.tensor_scalar(out=lbv[:, 0, :], in0=lbv[:, 0, :], scalar1=1.0 - 1e-6, scalar2=0.0,
                                           op0=ALU.min, op1=ALU.max), [b_lb], [b_lb])
    A(K.dve, lambda e: e.tensor_scalar(out=lbv[:, 1, :], in0=lbv[:, 0, :], scalar1=-1.0, scalar2=1.0,
                                       op0=ALU.mult, op1=ALU.add), [b_lb], [b_lb])
    A(K.dve, lambda e: e.tensor_scalar(out=lbv[:, 2, :], in0=lbv[:, 0, :], scalar1=1.0, scalar2=-1.0,
                                       op0=ALU.mult, op1=ALU.add), [b_lb], [b_lb])
    for h in range(2):
        A(K.dve, lambda e, h=h: e.memset(PT[:, h], 0.0), [], [b_PT[h]])
        A(K.dve, lambda e, h=h: e.memset(hb[h]["k0"][:], 0.0), [], [b_hb[h]["k0"]])
        A(K.dve, lambda e, h=h: e.memset(hf[h]["e2"][:], 0.0), [], [b_hf[h]["e2"]])
        A(K.dve, lambda e, h=h: e.memset(Sst[h][:], 0.0), [], [b_S[h]])
        A(K.dve, lambda e, h=h: e.memset(Sbf[h][:, 0, :], 0.0), [], [b_Sbf[h][0]])
    hv = None if h_load is not None else hT_d.rearrange("(k p) t -> p k t", p=128)
    ov = None if o_store is not None else oT_d.rearrange("(h p) t -> p h t", p=128)
    cidx = 0
    for t in range(S // 512):
        s = t % 2
        if h_load is not None:
            h_load(t, hT[:, s], b_hT[s])
        else:
            K.dma(K.sp, lambda e, t=t, s=s: e.dma_start(out=hT[:, s], in_=hv[:, :, t * 512:(t + 1) * 512]),
                  [], [b_hT[s]], ds_a)
        for h in range(2):
            for blk in (2 * h, 2 * h + 1):
                for k in range(KD):
                    A(K.pe, lambda e, blk=blk, k=k, s=s: e.matmul(fb[blk][:, :], lhsT=W[:, k, blk * 128:(blk + 1) * 128],
                                                                  rhs=hT[:, s, k, :], start=(k == 0), stop=(k == KD - 1)),
                      [b_W, b_hT[s]], [b_fb[blk]], inc=(k == KD - 1))
            f, bf_, bb, bbf = hf[h], hb[h], b_hf[h], b_hb[h]
            lb_, oml, noml = lbv[:, 0, h:h + 1], lbv[:, 1, h:h + 1], lbv[:, 2, h:h + 1]
            A(K.act, lambda e, f=f, h=h: e.activation(out=f["sig"][:], in_=fb[2 * h + 1][:, :], func=AF.Sigmoid),
              [b_fb[2 * h + 1]], [bb["sig"]])
            A(K.act, lambda e, f=f, h=h: e.activation(out=f["qs"][:], in_=fb[2 * h][:, :], func=AF.Silu),
              [b_fb[2 * h]], [bb["qs"]])
            A(K.dve, lambda e, f=f, oml=oml, lb_=lb_: e.tensor_scalar(out=f["lf"][:], in0=f["sig"][:], scalar1=oml,
                                                                     scalar2=lb_, op0=ALU.mult, op1=ALU.add),
              [bb["sig"], b_lb], [bb["lf"]])
            A(K.dve, lambda e, f=f: e.tensor_scalar_max(out=f["lf"][:], in0=f["lf"][:], scalar1=1e-30),
              [bb["lf"]], [bb["lf"]])
            A(K.act, lambda e, f=f: e.activation(out=f["lf"][:], in_=f["lf"][:], func=AF.Ln), [bb["lf"]], [bb["lf"]])
            A(K.dve, lambda e, f=f, oml=oml, noml=noml: e.tensor_scalar(out=f["key"][:], in0=f["sig"][:], scalar1=noml,
                                                                       scalar2=oml, op0=ALU.mult, op1=ALU.add),
              [bb["sig"], b_lb], [bb["key"]])
            for c in range(NCH):
                A(K.dve, lambda e, f=f, c=c: e.tensor_tensor_scan(out=f["cum"][:, c * CH:(c + 1) * CH], data0=ones[:, :],
                                                                  data1=f["lf"][:, c * CH:(c + 1) * CH], initial=0.0,
                                                                  op0=ALU.mult, op1=ALU.add),
                  [bb["lf"], b_ones], [bb["cum"]])
            c3 = lambda a: a[:].rearrange("p (c t) -> p c t", c=NCH)
            A(K.dve, lambda e, f=f: e.tensor_tensor(out=c3(f["cm"]), in0=c3(f["cum"]),
                                                    in1=c3(f["cum"])[:, :, 31:32].broadcast_to([128, NCH, CH]),
                                                    op=ALU.subtract), [bb["cum"]], [bb["cm"]])
            A(K.dve, lambda e, f=f: e.tensor_tensor(out=c3(f["cl"]), in0=c3(f["cum"]),
                                                    in1=c3(f["cum"])[:, :, CH - 1:CH].broadcast_to([128, NCH, CH]),
                                                    op=ALU.subtract), [bb["cum"]], [bb["cl"]])
            A(K.act, lambda e, f=f, h=h: e.activation(out=eL[h][:, :], in_=c3(f["cum"])[:, :, CH - 1], func=AF.Exp),
              [bb["cum"]], [b_eL[h]])
            A(K.act, lambda e, f=f: e.activation(out=c3(f["e2"])[:, :, 0:32], in_=c3(f["cum"])[:, :, 0:32], func=AF.Exp,
                                                 scale=-1.0), [bb["cum"]], [bb["e2"]])
            A(K.dve, lambda e, f=f, bf_=bf_: e.tensor_tensor(out=c3(bf_["k0"])[:, :, 0:32], in0=c3(f["key"])[:, :, 0:32],
                                                             in1=c3(f["e2"])[:, :, 0:32], op=ALU.mult),
              [bb["key"], bb["e2"]], [bbf["k0"]])
            for (src, sc, es, mul, dst) in (("cum", 1.0, "e0", "qs", "qh"), ("cm", 1.0, "e1", "qs", "qt"),
                                            ("cm", -1.0, "e0", "key", "kt"), ("cl", -1.0, "e1", "key", "kh")):
                A(K.act, lambda e, f=f, src=src, sc=sc, es=es: e.activation(out=f[es][:], in_=f[src][:], func=AF.Exp,
                                                                           scale=sc), [bb[src]], [bb[es]])
                A(K.dve, lambda e, f=f, bf_=bf_, es=es, mul=mul, dst=dst: e.tensor_tensor(
                    out=bf_[dst][:], in0=f[mul][:], in1=f[es][:], op=ALU.mult), [bb[mul], bb[es]], [bbf[dst]])
        def prep(c, s=s):
            csl = slice(c * CH, (c + 1) * CH)
            c0_ = c * CH
            par = c % 2
            for k in range(KD):
                A(K.pe, lambda e, k=k: e.matmul(tb[par][0:CH, :], lhsT=hT[:, s, k, csl], rhs=W[:, k, 512:1024],
                                                start=(k == 0), stop=(k == KD - 1)),
                  [b_W, b_hT[s]], [b_tb[par]], inc=(k == KD - 1))
            A(K.act, lambda e: e.activation(out=vc[:, par, :], in_=tb[par][0:CH, 0:256], func=AF.Identity),
              [b_tb[par]], [b_vc[par]])
            A(K.act, lambda e: e.activation(out=gs[:, par, :], in_=tb[par][0:CH, 256:512], func=AF.Silu),
              [b_tb[par]], [b_gs[par]])
            A(K.dve, lambda e: e.tensor_tensor(out=gs[:, par, :], in0=gs[:, par, :], in1=gain[:, :], op=ALU.mult),
              [b_gs[par], b_gain], [b_gs[par]])
            for h in range(2):
                bf_, bbf = hb[h], b_hb[h]
                A(K.pe, lambda e, h=h, bf_=bf_: e.matmul(fb2[par][0:64, h * 64 + 32:h * 64 + 64], lhsT=bf_["kt"][:, csl],
                                                         rhs=bf_["qt"][:, c0_ + 32:c0_ + 64], start=True, stop=True),
                  [bbf["kt"], bbf["qt"]], [b_fb2[par]], inc=False)
                A(K.pe, lambda e, h=h, bf_=bf_: e.matmul(fb2[par][0:32, h * 64:h * 64 + 32], lhsT=bf_["k0"][:, c0_:c0_ + 32],
                                                         rhs=bf_["qh"][:, c0_:c0_ + 32], start=True, stop=True),
                  [bbf["k0"], bbf["qh"]], [b_fb2[par]], inc=False)
                A(K.pe, lambda e, h=h, bf_=bf_: e.transpose(bkT[par][0:64, h * 128:(h + 1) * 128], bf_["kh"][:, csl],
                                                            C.ident_b()), [bbf["kh"], C.buf], [b_bkT[par]], inc=(h == 1))
            scv = fb2[par][0:64, 0:128].rearrange("p (h t) -> p h t", h=2)
            A(K.dve, lambda e: e.tensor_tensor(out=PT[:, par, :, 32:64], in0=scv[:, :, 32:64],
                                               in1=C.f[0:64, 160:192].unsqueeze(1).broadcast_to([64, 2, 32]),
                                               op=ALU.mult), [b_fb2[par], C.buf], [b_PT[par]])
            A(K.dve, lambda e: e.tensor_tensor(out=PT[0:32, par, :, 0:32], in0=scv[0:32, :, 0:32],
                                               in1=C.f[0:32, 128:160].unsqueeze(1).broadcast_to([32, 2, 32]),
                                               op=ALU.mult), [b_fb2[par], C.buf], [b_PT[par]])
            A(K.act, lambda e: e.activation(out=khs[:, par].rearrange("p h d -> p (h d)"), in_=bkT[par][0:64, 0:256],
                                            func=AF.Identity), [b_bkT[par]], [b_khs[par]])

        def main(c, s=s):
            csl = slice(c * CH, (c + 1) * CH)
            par = c % 2
            sp_, sn = c % 2, (c + 1) % 2
            for h in range(2):
                bf_, bbf = hb[h], b_hb[h]
                A(K.pe, lambda e, h=h: e.matmul(bkO[0:64, h * 128:(h + 1) * 128], lhsT=PT[:, par, h, :],
                                                rhs=vc[:, par, h * 128:(h + 1) * 128], start=True, stop=False),
                  [b_PT[par], b_vc[par]], [b_bkO], inc=False)
                A(K.pe, lambda e, h=h, bf_=bf_: e.matmul(bkO[0:64, h * 128:(h + 1) * 128], lhsT=bf_["qh"][:, csl],
                                                         rhs=Sbf[h][:, sp_, :], start=False, stop=True),
                  [bbf["qh"], b_Sbf[h][sp_]], [b_bkO], inc=(h == 1))
            for h in range(2):
                A(K.pe, lambda e, h=h: e.matmul(bkU[:, h * 128:(h + 1) * 128], lhsT=khs[:, par, h, :],
                                                rhs=vc[:, par, h * 128:(h + 1) * 128], start=True, stop=True),
                  [b_khs[par], b_vc[par]], [b_bkU], inc=(h == 1))
            for h in range(2):
                A(K.dve, lambda e, h=h: e.scalar_tensor_tensor(out=Sst[h][:], in0=Sst[h][:], scalar=eL[h][:, c:c + 1],
                                                               in1=bkU[:, h * 128:(h + 1) * 128], op0=ALU.mult,
                                                               op1=ALU.add), [b_S[h], b_eL[h], b_bkU], [b_S[h]])
                A(K.act, lambda e, h=h: e.activation(out=Sbf[h][:, sn, :], in_=Sst[h][:], func=AF.Identity),
                  [b_S[h]], [b_Sbf[h][sn]])

        def post_a(c, s=s):
            par = c % 2
            for h in range(2):
                A(K.act, lambda e, h=h: e.activation(out=junk[:, :], in_=bkO[0:64, h * 128:(h + 1) * 128], func=AF.Square,
                                                     accum_out=ssq[:, h, 0:1]), [b_bkO], [b_junk, b_ssq[h]])
                A(K.act, lambda e, h=h: e.activation(out=ssq[:, h, 1:2], in_=ssq[:, h, 0:1], func=AF.Sqrt,
                                                     scale=1.0 / 128, bias=EPS), [b_ssq[h]], [b_ssq[h]])
                A(K.dve, lambda e, h=h: e.reciprocal(out=ssq[:, h, 1:2], in_=ssq[:, h, 1:2]), [b_ssq[h]], [b_ssq[h]])
                A(K.dve, lambda e, h=h: e.scalar_tensor_tensor(out=og[:, h, :], in0=bkO[0:64, h * 128:(h + 1) * 128],
                                                               scalar=ssq[:, h, 1:2],
                                                               in1=gs[:, par, h * 128:(h + 1) * 128],
                                                               op0=ALU.mult, op1=ALU.mult),
                  [b_bkO, b_ssq[h], b_gs[par]], [b_og[h]])

        def post_b(c, s=s):
            csl = slice(c * CH, (c + 1) * CH)
            par = (c + 1) % 2
            for h in range(2):
                A(K.pe, lambda e, h=h: e.transpose(bkT[par][:, 256 + h * 64:256 + (h + 1) * 64], og[:, h, :],
                                                   C.ident_b(64)), [b_og[h], C.buf], [b_bkT[par]], inc=(h == 1))
            A(K.act, lambda e: e.activation(out=oTt[:, s, :, csl],
                                            in_=bkT[par][:, 256:384].rearrange("p (h t) -> p h t", h=2),
                                            func=AF.Identity), [b_bkT[par]], [b_oTt[s]])

        prep(0)
        for c in range(NCH):
            if c + 1 < NCH:
                prep(c + 1)
            main(c)
            if c >= 1:
                post_b(c - 1)
            post_a(c)
        post_b(NCH - 1)
        if o_store is not None:
            o_store(t, oTt[:, s], b_oTt[s], 2)
        else:
            K.dma(K.pool, lambda e, t=t, s=s: e.dma_start(out=ov[:, :, t * 512:(t + 1) * 512], in_=oTt[:, s]),
                  [b_oTt[s]], [], ds_s)


def build_hgrn(cfg, ab_idx):
    D, S = cfg["D"], cfg["S"]
    nc, stack, K = _new()
    cst = _din(nc, "consts", [128, 640])
    hT = _din(nc, "hT", [D, S], BF16)
    w = _din(nc, "w", [D, 1024])
    lbl = _din(nc, "lbl", [128, 2, 2])
    gain = _din(nc, "gain", [1, 256])
    oT = _dout(nc, "oT", [256, S], BF16)
    C = Consts(K, cst, K.dsems[0])
    emit_hgrn(K, C, cfg, ab_idx, hT, w, lbl, gain, oT)
    _finish(K, stack)
    return nc


def emit_fox(K, C, cfg, hT_d, w_d, qkg_d, fbias_d, oT_d, h_load=None, o_store=None):
    D, KD, S = cfg["D"], cfg["KD"], cfg["S"]
    NB = S // 128
    A = K.op
    W = K.sb("x_W", [128, KD, 1026], BF16)
    hT = K.sb("x_hT", [128, 2, KD, 512], BF16)
    kTa = K.sb("x_kTa", [128, 2, S], BF16)
    va = K.sb("x_va", [128, 2, NB, 132], BF16)
    Ga = K.sb("x_Ga", [128, 2, NB], F32)
    qkg = K.sb("x_qkg", [128, 2], F32)
    nfb = K.sb("x_nfb", [128, 2], F32)
    qn = K.sb("x_qn", [128, 2, 512], BF16)
    sq = K.sb("x_sq", [128, 2, 512], F32)
    rt = K.sb("x_rt", [128, 2, 512], F32)
    sg = K.sb("x_sg", [128, 4, 256], F32)
    spv = K.sb("x_spv", [128, 2, 4], F32)
    lcs = K.sb("x_lcs", [128, 2, 4], F32)
    tots = K.sb("x_tots", [128, 2, 4], F32)
    incl = K.sb("x_incl", [128, 2, 2, 4], F32)
    excl = K.sb("x_excl", [128, 2, 4], F32)
    Bq = K.sb("x_Bq", [128, 2, NB], F32)
    PT = K.sb("x_PT", [128, 3, 512], BF16)
    rden = K.sb("x_rden", [128, 4], F32)
    dqc = K.sb("x_dqc", [128, 2, 4], F32)
    dqb = K.sb("x_dqb", [1, 2, 512], F32)
    dqbc = K.sb("x_dqbc", [128, 2, 512], F32)
    stg = K.sb("x_stg", [128, 3, 512], F32)
    b_dqc, b_dqb, b_dqbc = Buf(), Buf(), Buf()
    b_stg = [Buf() for _ in range(3)]
    og = K.sb("x_og", [128, 2, 128], BF16)
    oTt = K.sb("x_oTt", [128, 2, 2, 512], BF16)
    bk = [K.ps("x_bk%d" % i, [128, 512], F32) for i in range(7)]
    bkT = K.ps("x_bkT", [128, 1024], BF16)
    b_bk = [Buf() for _ in range(7)]
    b_T = Buf()
    P0, P1, N_, V_, Fb, O3, P2 = range(7)
    OB = [N_, V_, Fb, O3]
    B = lambda n: [Buf() for _ in range(n)]
    b_W, b_kTa, b_va, b_Ga, b_par = Buf(), Buf(), Buf(), Buf(), Buf()
    b_hT, b_qn, b_sq, b_rt, b_oTt = B(2), B(2), B(2), B(2), B(2)
    b_sg, b_spv, b_lcs, b_tots, b_excl = Buf(), Buf(), Buf(), Buf(), Buf()
    b_incl, b_Bq = B(2), Buf()
    b_PT = B(3)
    b_rden, b_og = Buf(), B(2)
    wv = w_d.rearrange("(k p) c -> p k c", p=128)
    K.dma(K.pool, lambda e: e.dma_start(out=W[:, :, 0:512], in_=wv[:, :, 0:512]), [], [b_W])
    K.dma(K.pool, lambda e: e.dma_start(out=W[:, :, 512:1026], in_=wv[:, :, 512:1026]), [], [b_W])
    K.dma(K.sp, lambda e: e.dma_start(out=qkg[:], in_=qkg_d[:, :]), [], [b_par])
    K.dma(K.sp, lambda e: e.dma_start(out=nfb[:], in_=fbias_d[0:1, :].partition_broadcast(128)), [], [b_par])
    A(K.dve, lambda e: e.tensor_scalar(out=qkg[:, 0:1], in0=qkg[:, 0:1], scalar1=float(128 ** -0.5), scalar2=None,
                                       op0=ALU.mult), [b_par], [b_par])
    A(K.dve, lambda e: e.tensor_scalar(out=nfb[:, :], in0=nfb[:, :], scalar1=-1.0, scalar2=None, op0=ALU.mult),
      [b_par], [b_par])
    A(K.dve, lambda e: e.memset(va[:], 1.0), [], [b_va])
    hv = None if h_load is not None else hT_d.rearrange("(k p) t -> p k t", p=128)
    ov = None if o_store is not None else oT_d.rearrange("(h p) t -> p h t", p=128)
    xs = 0
    for t in range(S // 512):
        s = t % 2
        tsl = slice(t * 512, (t + 1) * 512)
        if h_load is not None:
            h_load(t, hT[:, s], b_hT[s])
        else:
            K.dma(K.sp, lambda e, s=s, tsl=tsl: e.dma_start(out=hT[:, s], in_=hv[:, :, tsl]), [], [b_hT[s]])
        for h in range(2):
            for (which, bank) in ((0, P0), (1, P1)):
                blk = 2 * h + which
                for k in range(KD):
                    A(K.pe, lambda e, blk=blk, k=k, s=s, bank=bank: e.matmul(
                        bk[bank][:, :], lhsT=W[:, k, blk * 128:(blk + 1) * 128], rhs=hT[:, s, k, :],
                        start=(k == 0), stop=(k == KD - 1)), [b_W, b_hT[s]], [b_bk[bank]], inc=(k == KD - 1))
                x = which
                A(K.act, lambda e, x=x, bank=bank: e.activation(out=sq[:, x, :], in_=bk[bank][:, :], func=AF.Square),
                  [b_bk[bank]], [b_sq[x]])
                A(K.pe, lambda e, x=x: e.matmul(bk[N_][:, :], lhsT=C.ones_f(), rhs=sq[:, x, :], start=True, stop=True),
                  [b_sq[x], C.buf], [b_bk[N_]])
                A(K.act, lambda e, x=x: e.activation(out=rt[:, x, :], in_=bk[N_][:, :], func=AF.Sqrt, scale=1.0 / 128,
                                                     bias=EPS), [b_bk[N_]], [b_rt[x]])
                A(K.dve, lambda e, x=x: e.reciprocal(out=rt[:, x, :], in_=rt[:, x, :]), [b_rt[x]], [b_rt[x]])
                if which == 0:
                    A(K.dve, lambda e, h=h, x=x, bank=bank: e.scalar_tensor_tensor(
                        out=qn[:, h, :], in0=bk[bank][:, :], scalar=qkg[:, 0:1], in1=rt[:, x, :], op0=ALU.mult,
                        op1=ALU.mult), [b_bk[bank], b_par, b_rt[x]], [b_qn[h]])
                else:
                    A(K.dve, lambda e, h=h, x=x, bank=bank, tsl=tsl: e.scalar_tensor_tensor(
                        out=kTa[:, h, tsl], in0=bk[bank][:, :], scalar=qkg[:, 1:2], in1=rt[:, x, :], op0=ALU.mult,
                        op1=ALU.mult), [b_bk[bank], b_par, b_rt[x]], [b_kTa])
        for b in range(4):
            blk = 4 * t + b
            bsl = slice(b * 128, (b + 1) * 128)
            for k in range(KD):
                A(K.pe, lambda e, k=k, s=s, bsl=bsl: e.matmul(bk[V_][:, :], lhsT=hT[:, s, k, bsl], rhs=W[:, k, 512:1024],
                                                              start=(k == 0), stop=(k == KD - 1)),
                  [b_W, b_hT[s]], [b_bk[V_]], inc=(k == KD - 1))
            A(K.act, lambda e, blk=blk: e.activation(out=va[:, :, blk, 0:128],
                                                     in_=bk[V_][:, 0:256].rearrange("p (h e) -> p h e", h=2),
                                                     func=AF.Identity), [b_bk[V_]], [b_va])
            A(K.act, lambda e, b=b: e.activation(out=sg[:, b, :], in_=bk[V_][:, 256:512], func=AF.Sigmoid),
              [b_bk[V_]], [b_sg])
            for k in range(KD):
                A(K.pe, lambda e, k=k, s=s, bsl=bsl, b=b: e.matmul(bk[Fb][:, 2 * b:2 * b + 2], lhsT=hT[:, s, k, bsl],
                                                                   rhs=W[:, k, 1024:1026], start=(k == 0),
                                                                   stop=(k == KD - 1)),
                  [b_W, b_hT[s]], [b_bk[Fb]], inc=(k == KD - 1))
        fbv = bk[Fb][:, 0:8].rearrange("p (b h) -> p h b", h=2)
        for h in range(2):
            A(K.act, lambda e, h=h: e.activation(out=spv[:, h, :], in_=fbv[:, h, :], func=AF.Exp, scale=-1.0,
                                                 bias=nfb[:, h:h + 1]), [b_bk[Fb], b_par], [b_spv])
        A(K.act, lambda e: e.activation(out=spv[:], in_=spv[:], func=AF.Ln, bias=1.0, scale=1.0), [b_spv], [b_spv])
        A(K.pe, lambda e: e.matmul(bk[Fb][:, 16:24], lhsT=C.tri_f(), rhs=spv[:].rearrange("p h b -> p (h b)"),
                                   start=True, stop=True), [b_spv, C.buf], [b_bk[Fb]])
        A(K.act, lambda e: e.activation(out=lcs[:].rearrange("p h b -> p (h b)"), in_=bk[Fb][:, 16:24],
                                        func=AF.Identity), [b_bk[Fb]], [b_lcs])
        A(K.pe, lambda e: e.matmul(bk[Fb][:, 32:40], lhsT=C.sel_f(), rhs=lcs[:].rearrange("p h b -> p (h b)"),
                                   start=True, stop=True), [b_lcs, C.buf], [b_bk[Fb]])
        A(K.act, lambda e: e.activation(out=tots[:].rearrange("p h b -> p (h b)"), in_=bk[Fb][:, 32:40],
                                        func=AF.Identity), [b_bk[Fb]], [b_tots])
        for h in range(2):
            init = 0.0 if t == 0 else incl[:, 1 - s, h, 3:4]
            A(K.dve, lambda e, h=h, s=s, init=init: e.tensor_tensor_scan(
                out=incl[:, s, h, :], data0=C.ones_f(128, 4), data1=tots[:, h, :], initial=init, op0=ALU.mult,
                op1=ALU.add), [b_tots, C.buf, b_incl[1 - s]], [b_incl[s]])
        A(K.dve, lambda e, s=s: e.tensor_tensor(out=excl[:], in0=incl[:, s], in1=tots[:], op=ALU.subtract),
          [b_incl[s], b_tots], [b_excl])
        A(K.dve, lambda e, t=t: e.tensor_tensor(out=Ga[:, :, 4 * t:4 * t + 4], in0=lcs[:], in1=excl[:], op=ALU.add),
          [b_lcs, b_excl], [b_Ga])
        nkb = 4 * t + 4
        for h in range(2):
            A(K.dve, lambda e, h=h, s=s, nkb=nkb: e.tensor_scalar(
                out=Bq[:, h, 0:nkb], in0=Ga[:, h, 0:nkb], scalar1=incl[:, s, h, 3:4], scalar2=None,
                op0=ALU.subtract), [b_Ga, b_incl[s]], [b_Bq])
            A(K.dve, lambda e, h=h, s=s, t=t: e.tensor_scalar(
                out=dqc[:, h, :], in0=Ga[:, h, 4 * t:4 * t + 4], scalar1=incl[:, s, h, 3:4], scalar2=-1.0,
                op0=ALU.subtract, op1=ALU.mult), [b_Ga, b_incl[s]], [b_dqc])
        for h in range(2):
            for jl in range(4):
                A(K.pe, lambda e, h=h, jl=jl: e.transpose(bk[O3][0:1, jl * 128:(jl + 1) * 128], dqc[:, h, jl:jl + 1],
                                                          C.ident_f()), [b_dqc, C.buf], [b_bk[O3]])
            A(K.act, lambda e, h=h: e.activation(out=dqb[0:1, h, :], in_=bk[O3][0:1, :], func=AF.Identity),
              [b_bk[O3]], [b_dqb])
            A(K.pe, lambda e, h=h: e.matmul(bk[O3][:, :], lhsT=C.ones_f(1, 128), rhs=dqb[0:1, h, :], start=True,
                                            stop=True), [b_dqb, C.buf], [b_bk[O3]])
            A(K.act, lambda e, h=h: e.activation(out=dqbc[:, h, :], in_=bk[O3][:, :], func=AF.Identity),
              [b_bk[O3]], [b_dqbc])
        SB = (P0, P1, P2)
        for h in range(2):
            def score(kb, h=h):
                m = max(0, kb - 4 * t)
                x3 = kb % 3
                sb_ = SB[x3]
                diag = kb >= 4 * t
                A(K.pe, lambda e: e.matmul(bk[sb_][:, m * 128:512], lhsT=kTa[:, h, kb * 128:(kb + 1) * 128],
                                           rhs=qn[:, h, m * 128:512], start=True, stop=(not diag)),
                  [b_kTa, b_qn[h]], [b_bk[sb_]], inc=(not diag))
                if diag:
                    A(K.pe, lambda e: e.matmul(bk[sb_][:, m * 128:(m + 1) * 128], lhsT=C.ident_b(), rhs=C.negmask_b(),
                                               start=False, stop=True), [C.buf], [b_bk[sb_]])
                A(K.dve, lambda e: e.tensor_tensor(out=stg[:, x3, m * 128:512], in0=bk[sb_][:, m * 128:512],
                                                   in1=dqbc[:, h, m * 128:512], op=ALU.add),
                  [b_bk[sb_], b_dqbc], [b_stg[x3]])
                A(K.act, lambda e: e.activation(out=PT[:, x3, m * 128:512], in_=stg[:, x3, m * 128:512], func=AF.Exp,
                                                bias=Bq[:, h, kb:kb + 1], scale=1.0), [b_stg[x3], b_Bq], [b_PT[x3]])

            def pv(kb, h=h):
                m = max(0, kb - 4 * t)
                x3 = kb % 3
                for jl in range(m, 4):
                    j = 4 * t + jl
                    A(K.pe, lambda e, jl=jl, j=j: e.matmul(bk[OB[jl]][:, 0:129], lhsT=PT[:, x3, jl * 128:(jl + 1) * 128],
                                                           rhs=va[:, h, kb, 0:129], start=(kb == 0), stop=(kb == j)),
                      [b_PT[x3], b_va], [b_bk[OB[jl]]], inc=(jl == 3 or kb == j))
            score(0)
            score(1)
            for kb in range(2, nkb):
                score(kb)
                pv(kb - 2)
            pv(nkb - 2)
            pv(nkb - 1)
            for jl in range(4):
                A(K.dve, lambda e, jl=jl: e.reciprocal(out=rden[:, jl:jl + 1], in_=bk[OB[jl]][:, 128:129]),
                  [b_bk[OB[jl]]], [b_rden])
                x = jl % 2
                A(K.dve, lambda e, jl=jl, h=h, x=x: e.scalar_tensor_tensor(
                    out=og[:, x, :], in0=bk[OB[jl]][:, 0:128], scalar=rden[:, jl:jl + 1],
                    in1=sg[:, jl, h * 128:(h + 1) * 128], op0=ALU.mult, op1=ALU.mult),
                    [b_bk[OB[jl]], b_rden, b_sg], [b_og[x]])
                A(K.pe, lambda e, x=x: e.transpose(bkT[:, 0:128], og[:, x, :], C.ident_b()), [b_og[x], C.buf], [b_T])
                A(K.act, lambda e, s=s, h=h, jl=jl: e.activation(out=oTt[:, s, h, jl * 128:(jl + 1) * 128],
                                                                 in_=bkT[:, 0:128], func=AF.Identity),
                  [b_T], [b_oTt[s]])
        if o_store is not None:
            o_store(t, oTt[:, s], b_oTt[s], 2)
        else:
            K.dma(K.pool, lambda e, s=s, tsl=tsl: e.dma_start(out=ov[:, :, tsl], in_=oTt[:, s]), [b_oTt[s]], [])


def build_fox(cfg):
    D, S = cfg["D"], cfg["S"]
    nc, stack, K = _new()
    cst = _din(nc, "consts", [128, 640])
    hT = _din(nc, "hT", [D, S], BF16)
    w = _din(nc, "w", [D, 1026])
    qkg = _din(nc, "qkg", [128, 2])
    fbias = _din(nc, "fbias", [1, 2])
    oT = _dout(nc, "oT", [256, S], BF16)
    C = Consts(K, cst, None)
    emit_fox(K, C, cfg, hT, w, qkg, fbias, oT)
    _finish(K, stack)
    return nc


def emit_gla(K, C, cfg, hT_d, w_d, wgu_d, bg_d, gain_d, oT_d, h_load=None, o_store=None):
    D, KD, S = cfg["D"], cfg["KD"], cfg["S"]
    A = K.op
    CH, NCH = 128, 4
    NW = 1552
    W = K.sb("g_W", [128, KD, NW], BF16)
    hT = K.sb("g_hT", [128, 2, KD, 512], BF16)
    wgu = K.sb("g_wgu", [16, 256], F32)
    bg = K.sb("g_bg", [128, 2], F32)
    gain = K.sb("g_gain", [128, 512], F32)
    ones = K.sb("g_ones", [128, 128], F32)
    glT = K.sb("g_glT", [16, 512], F32)
    spt = K.sb("g_sp", [128, 2, 512], F32)
    csp = K.sb("g_csp", [128, 2, 512], F32)
    Eq = K.sb("g_Eq", [128, 2, 512], F32)
    Ek = K.sb("g_Ek", [128, 2, 512], F32)
    eL = K.sb("g_eL", [128, 2, NCH], F32)
    qh = K.sb("g_qh", [128, 2, 512], BF16)
    kt = K.sb("g_kt", [128, 2, 512], BF16)
    kh = K.sb("g_kh", [128, 2, 512], BF16)
    Sst = K.sb("g_S", [128, 2, 512], F32)
    Sbf = K.sb("g_Sbf", [128, 2, 2, 512], BF16)
    vc = K.sb("g_vc", [128, 2, 512], BF16)
    gs = K.sb("g_gs", [128, 2, 512], F32)
    PT = K.sb("g_PT", [128, 2, 128], BF16)
    khs = K.sb("g_khs", [128, 2, 256], BF16)
    junk = K.sb("g_junk", [128, 512], F32)
    ssq = K.sb("g_ssq", [128, 2], F32)
    og = K.sb("g_og", [128, 512], BF16)
    oTt = K.sb("g_oTt", [128, 2, 4, 512], BF16)
    bk = [K.ps("g_bk%d" % i, [128, 512], F32) for i in range(6)]
    bkT = K.ps("g_bkT", [128, 1024], BF16)
    bkT2 = K.ps("g_bkT2", [128, 1024], BF16)
    b_bk = [Buf() for _ in range(6)]
    b_T, b_T2 = Buf(), Buf()
    GA, Z0, Z1, V_, Gt, O_ = range(6)
    B = lambda n: [Buf() for _ in range(n)]
    b_W, b_par, b_ones, b_glT = Buf(), Buf(), Buf(), Buf()
    b_hT, b_sp, b_csp, b_Eq, b_Ek, b_eL = B(2), B(2), B(2), B(2), B(2), B(2)
    b_qh, b_kt, b_kh, b_S = B(2), B(2), B(2), B(2)
    b_Sbf = [B(2), B(2)]
    b_vc, b_gs, b_PT, b_khs, b_oTt = B(2), B(2), B(2), B(2), B(2)
    b_junk, b_ssq, b_og = Buf(), Buf(), Buf()
    wv = w_d.rearrange("(k p) c -> p k c", p=128)
    for (c0, c1) in ((0, 528), (528, 1040), (1040, 1552)):
        K.dma(K.pool, lambda e, c0=c0, c1=c1: e.dma_start(out=W[:, :, c0:c1], in_=wv[:, :, c0:c1]), [], [b_W])
    K.dma(K.sp, lambda e: e.dma_start(out=wgu[:], in_=wgu_d[:, :]), [], [b_par])
    K.dma(K.sp, lambda e: e.dma_start(out=bg[:], in_=bg_d[:, :]), [], [b_par])
    K.dma(K.sp, lambda e: e.dma_start(out=gain[:], in_=gain_d[0:1, :].partition_broadcast(128)), [], [b_par])
    A(K.dve, lambda e: e.tensor_scalar(out=bg[:, :], in0=bg[:, :], scalar1=-1.0, scalar2=None, op0=ALU.mult),
      [b_par], [b_par])
    A(K.dve, lambda e: e.memset(ones[:], 1.0), [], [b_ones])
    for dc in range(2):
        A(K.dve, lambda e, dc=dc: e.memset(Sst[:, dc, :], 0.0), [], [b_S[dc]])
        A(K.dve, lambda e, dc=dc: e.memset(Sbf[:, dc, 0, :], 0.0), [], [b_Sbf[dc][0]])
    hv = None if h_load is not None else hT_d.rearrange("(k p) t -> p k t", p=128)
    ov = None if o_store is not None else oT_d.rearrange("(h p) t -> p h t", p=128)
    c3 = lambda a: a.rearrange("p (c t) -> p c t", c=NCH)
    cg = 0
    for t in range(S // 512):
        s = t % 2
        tsl = slice(t * 512, (t + 1) * 512)
        if h_load is not None:
            h_load(t, hT[:, s], b_hT[s])
        else:
            K.dma(K.sp, lambda e, s=s, tsl=tsl: e.dma_start(out=hT[:, s], in_=hv[:, :, tsl]), [], [b_hT[s]])
        for k in range(KD):
            A(K.pe, lambda e, k=k, s=s: e.matmul(bk[GA][0:16, :], lhsT=W[:, k, 512:528], rhs=hT[:, s, k, :],
                                                 start=(k == 0), stop=(k == KD - 1)),
              [b_W, b_hT[s]], [b_bk[GA]], inc=(k == KD - 1))
        A(K.act, lambda e: e.activation(out=glT[:, :], in_=bk[GA][0:16, :], func=AF.Identity), [b_bk[GA]], [b_glT])
        for dc in range(2):
            zb = (Z0, Z1)[dc]
            A(K.pe, lambda e, dc=dc, zb=zb: e.matmul(bk[zb][:, :], lhsT=wgu[:, dc * 128:(dc + 1) * 128], rhs=glT[:, :],
                                                     start=True, stop=True), [b_par, b_glT], [b_bk[zb]])
            A(K.act, lambda e, dc=dc, zb=zb: e.activation(out=spt[:, dc, :], in_=bk[zb][:, :], func=AF.Exp, scale=-1.0,
                                                          bias=bg[:, dc:dc + 1]), [b_bk[zb], b_par], [b_sp[dc]])
            A(K.act, lambda e, dc=dc: e.activation(out=spt[:, dc, :], in_=spt[:, dc, :], func=AF.Ln, bias=1.0,
                                                   scale=1.0), [b_sp[dc]], [b_sp[dc]])
            for c in range(NCH):
                A(K.dve, lambda e, dc=dc, c=c: e.tensor_tensor_scan(
                    out=csp[:, dc, c * CH:(c + 1) * CH], data0=ones[:, :], data1=spt[:, dc, c * CH:(c + 1) * CH],
                    initial=0.0, op0=ALU.mult, op1=ALU.add), [b_sp[dc], b_ones], [b_csp[dc]])
            A(K.act, lambda e, dc=dc: e.activation(out=Eq[:, dc, :], in_=csp[:, dc, :], func=AF.Exp, scale=-1.0 / 16),
              [b_csp[dc]], [b_Eq[dc]])
            A(K.act, lambda e, dc=dc: e.activation(out=Ek[:, dc, :], in_=csp[:, dc, :], func=AF.Exp, scale=1.0 / 16),
              [b_csp[dc]], [b_Ek[dc]])
            A(K.act, lambda e, dc=dc: e.activation(out=eL[:, dc, :], in_=c3(csp[:, dc, :])[:, :, CH - 1], func=AF.Exp,
                                                   scale=-1.0 / 16), [b_csp[dc]], [b_eL[dc]])
            for (which, dst) in ((0, "q"), (1, "k")):
                blk = which * 2 + dc
                for k in range(KD):
                    A(K.pe, lambda e, blk=blk, k=k, s=s, zb=zb: e.matmul(
                        bk[zb][:, :], lhsT=W[:, k, blk * 128:(blk + 1) * 128], rhs=hT[:, s, k, :], start=(k == 0),
                        stop=(k == KD - 1)), [b_W, b_hT[s]], [b_bk[zb]], inc=(k == KD - 1))
                if which == 0:
                    A(K.dve, lambda e, dc=dc, zb=zb: e.scalar_tensor_tensor(
                        out=qh[:, dc, :], in0=bk[zb][:, :], scalar=float(256 ** -0.5), in1=Eq[:, dc, :], op0=ALU.mult,
                        op1=ALU.mult), [b_bk[zb], b_Eq[dc]], [b_qh[dc]])
                else:
                    A(K.dve, lambda e, dc=dc, zb=zb: e.tensor_tensor(out=kt[:, dc, :], in0=bk[zb][:, :],
                                                                     in1=Ek[:, dc, :], op=ALU.mult),
                      [b_bk[zb], b_Ek[dc]], [b_kt[dc]])
                    A(K.dve, lambda e, dc=dc: e.tensor_tensor(
                        out=c3(kh[:, dc, :]), in0=c3(kt[:, dc, :]),
                        in1=eL[:, dc, :].unsqueeze(2).broadcast_to([128, NCH, CH]), op=ALU.mult),
                        [b_kt[dc], b_eL[dc]], [b_kh[dc]])
        def prep(c, s=s):
            csl = slice(c * CH, (c + 1) * CH)
            ts = c % 2
            for (bank, c0) in ((V_, 528), (Gt, 1040)):
                for k in range(KD):
                    A(K.pe, lambda e, k=k, bank=bank, c0=c0: e.matmul(
                        bk[bank][:, :], lhsT=hT[:, s, k, csl], rhs=W[:, k, c0:c0 + 512], start=(k == 0),
                        stop=(k == KD - 1)), [b_W, b_hT[s]], [b_bk[bank]], inc=(k == KD - 1))
            A(K.act, lambda e: e.activation(out=vc[:, ts, :], in_=bk[V_][:, :], func=AF.Identity),
              [b_bk[V_]], [b_vc[ts]])
            A(K.act, lambda e: e.activation(out=gs[:, ts, :], in_=bk[Gt][:, :], func=AF.Silu),
              [b_bk[Gt]], [b_gs[ts]])
            A(K.dve, lambda e: e.tensor_tensor(out=gs[:, ts, :], in0=gs[:, ts, :], in1=gain[:, :], op=ALU.mult),
              [b_gs[ts], b_par], [b_gs[ts]])
            for dc in range(2):
                A(K.pe, lambda e, dc=dc: e.matmul(bk[GA][:, 0:128], lhsT=kt[:, dc, csl], rhs=qh[:, dc, csl],
                                                  start=(dc == 0), stop=(dc == 1)),
                  [b_kt[dc], b_qh[dc]], [b_bk[GA]], inc=(dc == 1))
            A(K.dve, lambda e: e.tensor_tensor(out=PT[:, ts, :], in0=bk[GA][:, 0:128], in1=C.tri_f(), op=ALU.mult),
              [b_bk[GA], C.buf], [b_PT[ts]])
            for dc in range(2):
                A(K.pe, lambda e, dc=dc: e.transpose(bkT[:, dc * 128:(dc + 1) * 128], kh[:, dc, csl], C.ident_b()),
                  [b_kh[dc], C.buf], [b_T], inc=(dc == 1))
            A(K.act, lambda e: e.activation(out=khs[:, ts, :], in_=bkT[:, 0:256], func=AF.Identity),
              [b_T], [b_khs[ts]])

        def main(c, s=s):
            csl = slice(c * CH, (c + 1) * CH)
            ts = c % 2
            sp_, sn = c % 2, (c + 1) % 2
            A(K.pe, lambda e: e.matmul(bk[O_][:, :], lhsT=PT[:, ts, :], rhs=vc[:, ts, :], start=True, stop=False),
              [b_PT[ts], b_vc[ts]], [b_bk[O_]], inc=False)
            for dc in range(2):
                A(K.pe, lambda e, dc=dc: e.matmul(bk[O_][:, :], lhsT=qh[:, dc, csl], rhs=Sbf[:, dc, sp_, :], start=False,
                                                  stop=(dc == 1)),
                  [b_qh[dc], b_Sbf[dc][sp_]], [b_bk[O_]], inc=(dc == 1))
            for dc in range(2):
                ub = (Z0, Z1)[dc]
                A(K.pe, lambda e, dc=dc, ub=ub: e.matmul(bk[ub][:, :], lhsT=khs[:, ts, dc * 128:(dc + 1) * 128],
                                                         rhs=vc[:, ts, :], start=True, stop=True),
                  [b_khs[ts], b_vc[ts]], [b_bk[ub]])
            for dc in range(2):
                ub = (Z0, Z1)[dc]
                A(K.dve, lambda e, dc=dc, ub=ub: e.scalar_tensor_tensor(
                    out=Sst[:, dc, :], in0=Sst[:, dc, :], scalar=eL[:, dc, c:c + 1], in1=bk[ub][:, :], op0=ALU.mult,
                    op1=ALU.add), [b_S[dc], b_eL[dc], b_bk[ub]], [b_S[dc]])
                A(K.act, lambda e, dc=dc: e.activation(out=Sbf[:, dc, sn, :], in_=Sst[:, dc, :], func=AF.Identity),
                  [b_S[dc]], [b_Sbf[dc][sn]])

        def post_a(c, s=s):
            ts = c % 2
            A(K.act, lambda e: e.activation(out=junk[:, :], in_=bk[O_][:, :], func=AF.Square, accum_out=ssq[:, 0:1]),
              [b_bk[O_]], [b_junk, b_ssq])
            A(K.act, lambda e: e.activation(out=ssq[:, 1:2], in_=ssq[:, 0:1], func=AF.Sqrt, scale=1.0 / 512, bias=EPS),
              [b_ssq], [b_ssq])
            A(K.dve, lambda e: e.reciprocal(out=ssq[:, 1:2], in_=ssq[:, 1:2]), [b_ssq], [b_ssq])
            A(K.dve, lambda e: e.scalar_tensor_tensor(out=og[:, :], in0=bk[O_][:, :], scalar=ssq[:, 1:2],
                                                      in1=gs[:, ts, :], op0=ALU.mult, op1=ALU.mult),
              [b_bk[O_], b_ssq, b_gs[ts]], [b_og])

        def post_b(c, s=s):
            csl = slice(c * CH, (c + 1) * CH)
            for ec in range(4):
                A(K.pe, lambda e, ec=ec: e.transpose(bkT2[:, ec * 128:(ec + 1) * 128], og[:, ec * 128:(ec + 1) * 128],
                                                     C.ident_b()), [b_og, C.buf], [b_T2], inc=(ec == 3))
            A(K.act, lambda e: e.activation(out=oTt[:, s, :, csl], in_=bkT2[:, 0:512].rearrange("p (a q) -> p a q", a=4),
                                            func=AF.Identity), [b_T2], [b_oTt[s]])

        prep(0)
        for c in range(NCH):
            if c + 1 < NCH:
                prep(c + 1)
            main(c)
            if c >= 1:
                post_b(c - 1)
            post_a(c)
        post_b(NCH - 1)
        if o_store is not None:
            o_store(t, oTt[:, s], b_oTt[s], 4)
        else:
            K.dma(K.pool, lambda e, s=s, tsl=tsl: e.dma_start(out=ov[:, :, tsl], in_=oTt[:, s]), [b_oTt[s]], [])


def build_gla(cfg):
    D, S = cfg["D"], cfg["S"]
    nc, stack, K = _new()
    cst = _din(nc, "consts", [128, 640])
    hT = _din(nc, "hT", [D, S], BF16)
    w = _din(nc, "w", [D, 1552])
    wgu = _din(nc, "wgu", [16, 256])
    bg = _din(nc, "bg", [128, 2])
    gain = _din(nc, "gain", [1, 512])
    oT = _dout(nc, "oT", [512, S], BF16)
    C = Consts(K, cst, None)
    emit_gla(K, C, cfg, hT, w, wgu, bg, gain, oT)
    _finish(K, stack)
    return nc


CFG = dict(L=4, D=2048, KD=16, T=2048, S=8192, DFF=5632, KF=44, KO=16)
_PROGS = {}


def _prog(key, fn):
    if key not in _PROGS:
        _PROGS[key] = fn()
    return _PROGS[key]


def _run(nc, in_maps):
    res = run_bass_kernel_spmd(nc, in_maps, core_ids=list(range(8)))
    return res.results


def _fm(v, kd):
    v = np.asarray(v)
    lead = v.shape[:-1]
    a = v.reshape(lead + (kd, 128))
    return np.ascontiguousarray(np.moveaxis(a, -1, 0))


def kernel_unfused(x, c, mod_w, mod_b, norm_mix_gain, norm_ffn_gain, ab_w_in, ab_w_out, hgrn_lb_logits, hgrn_out_gain,
                   fox_q_gain, fox_k_gain, fox_f_bias, gla_w_in, gla_w_gate_up, gla_b_gate, gla_out_gain, gla_w_out,
                   ffn_w_in, ffn_w_out):
    cfg = CFG
    L, D, KD, T, S, DFF = cfg["L"], cfg["D"], cfg["KD"], cfg["T"], cfg["S"], cfg["DFF"]
    f32 = lambda a: np.ascontiguousarray(np.asarray(a, dtype=np.float32))
    x, c, mod_w, mod_b = f32(x), f32(c), f32(mod_w), f32(mod_b)
    consts = make_consts()
    cores = [(cid // 4, cid % 4) for cid in range(8)]

    cfg1 = dict(cfg)
    cfg1["L"] = 1
    nc = _prog("mod", lambda: build_mod(cfg1))
    ims = []
    for (b, j) in cores:
        ims.append({"consts": consts, "cT": _fm(c[b], KD), "modw": mod_w[j:j + 1], "modb": mod_b[j:j + 1],
                    "gmix": _fm(f32(norm_mix_gain)[j:j + 1], KD), "gffn": _fm(f32(norm_ffn_gain)[j:j + 1], KD)})
    r = _run(nc, ims)
    vec = [np.ascontiguousarray(np.concatenate([r[b * 4 + j]["vec"] for j in range(4)], axis=1)) for b in range(2)]

    nc = _prog("norm0", lambda: build_norm0(cfg))
    xT = [np.ascontiguousarray(x[b, j * T:(j + 1) * T, :].T) for (b, j) in cores]
    r = _run(nc, [{"consts": consts, "xT": xT[i], "vec": vec[cores[i][0]]} for i in range(8)])
    hT = [r[i]["hT"] for i in range(8)]

    for l in range(L):
        hfull = [np.ascontiguousarray(np.concatenate(hT[b * 4:b * 4 + 4], axis=1)) for b in range(2)]
        if l % 2 == 0:
            i = l // 2
            w = f32(ab_w_in[i])
            lbl_all = f32(hgrn_lb_logits)
            ims_h, ims_f = [], []
            for (b, j) in cores:
                hh = (2 * j, 2 * j + 1)
                cs = lambda base, h_: w[:, base + h_ * 128:base + (h_ + 1) * 128]
                wh = np.concatenate([cs(0, hh[0]), cs(1024, hh[0]), cs(0, hh[1]), cs(1024, hh[1]),
                                     cs(2048, hh[0]), cs(2048, hh[1]), cs(3072, hh[0]), cs(3072, hh[1])], axis=1)
                lbl = np.stack([np.stack([lbl_all[ly, h_ * 128:(h_ + 1) * 128] for h_ in hh], axis=1)
                                for ly in range(2)], axis=1)
                ims_h.append({"consts": consts, "hT": hfull[b], "w": np.ascontiguousarray(wh),
                              "lbl": f32(lbl), "gain": f32(hgrn_out_gain[i][2 * j:2 * j + 2]).reshape(1, 256)})
                wf = np.concatenate([cs(4096, hh[0]), cs(5120, hh[0]), cs(4096, hh[1]), cs(5120, hh[1]),
                                     cs(6144, hh[0]), cs(6144, hh[1]), cs(7168, hh[0]), cs(7168, hh[1]),
                                     w[:, 8192 + hh[0]:8192 + hh[0] + 1], w[:, 8192 + hh[1]:8192 + hh[1] + 1]], axis=1)
                ims_f.append({"consts": consts, "hT": hfull[b], "w": np.ascontiguousarray(wf),
                              "qkg": f32(np.stack([fox_q_gain[i], fox_k_gain[i]], axis=1)),
                              "fbias": f32(fox_f_bias[i][2 * j:2 * j + 2]).reshape(1, 2)})
            rh = _run(_prog(("hgrn", i), lambda: build_hgrn(cfg, i)), ims_h)
            rf = _run(_prog("fox", lambda: build_fox(cfg)), ims_f)
            ofull = [np.concatenate([np.concatenate([rh[b * 4 + j]["oT"], rf[b * 4 + j]["oT"]], axis=0)
                                     for j in range(4)], axis=0) for b in range(2)]
            wo_src = f32(ab_w_out[i])
            perm = []
            for j in range(4):
                for typ in range(2):
                    for hl in range(2):
                        base = typ * 1024 + (2 * j + hl) * 128
                        perm.extend(range(base, base + 128))
            wo = np.ascontiguousarray(wo_src[np.asarray(perm)])
        else:
            i = l // 2
            w = f32(gla_w_in[i])
            ims_g = []
            for (b, j) in cores:
                wg = np.concatenate([w[:, j * 256:(j + 1) * 256], w[:, 1024 + j * 256:1024 + (j + 1) * 256],
                                     w[:, 6144:6160], w[:, 2048 + j * 512:2048 + (j + 1) * 512],
                                     w[:, 4096 + j * 512:4096 + (j + 1) * 512]], axis=1)
                ims_g.append({"consts": consts, "hT": hfull[b], "w": np.ascontiguousarray(wg),
                              "wgu": f32(gla_w_gate_up[i][:, j * 256:(j + 1) * 256]),
                              "bg": f32(np.asarray(gla_b_gate[i][j * 256:(j + 1) * 256]).reshape(2, 128).T),
                              "gain": f32(gla_out_gain[i]).reshape(1, 512)})
            rg = _run(_prog("gla", lambda: build_gla(cfg)), ims_g)
            ofull = [np.concatenate([rg[b * 4 + j]["oT"] for j in range(4)], axis=0) for b in range(2)]
            wo = f32(gla_w_out[i])
        last = (l == L - 1)
        nc = _prog(("dense", last), lambda: build_dense(cfg, last))
        wfi, wfo = f32(ffn_w_in[l]), f32(ffn_w_out[l])
        ims = []
        for ci, (b, j) in enumerate(cores):
            v2 = np.zeros((128, 2, 6 * KD), np.float32)
            v2[:, 0] = vec[b][:, l]
            if not last:
                v2[:, 1] = vec[b][:, l + 1]
            ims.append({"consts": consts, "xT": xT[ci], "oT": np.ascontiguousarray(ofull[b][:, j * T:(j + 1) * T]),
                        "wo": wo, "wfi": wfi, "wfo": wfo, "vec": v2})
        r = _run(nc, ims)
        xT = [r[ci]["xTo"] for ci in range(8)]
        if not last:
            hT = [r[ci]["hTo"] for ci in range(8)]

    out = np.empty((2, S, D), np.float32)
    for ci, (b, j) in enumerate(cores):
        out[b, j * T:(j + 1) * T, :] = xT[ci].T
    return out


GROUPS = [[0, 1, 2, 3], [4, 5, 6, 7]]
PRECAST = False


def build_fused(cfg):
    L, D, KD, T, S, DFF, KO = cfg["L"], cfg["D"], cfg["KD"], cfg["T"], cfg["S"], cfg["DFF"], cfg["KO"]
    NT = T // 512
    NQ = D // 256
    nc = bass.Bass("TRN2", target_bir_lowering=False)
    gstack = contextlib.ExitStack()
    K = Kern(nc, gstack)
    cst = _din(nc, "consts", [128, 640])
    cT = _din(nc, "cT", [128, KD])
    modw = _din(nc, "modw", [1, D, 6 * D])
    modb = _din(nc, "modb", [1, 6 * D])
    gmix = _din(nc, "gmix", [128, 1, KD])
    gffn = _din(nc, "gffn", [128, 1, KD])
    xT_in = _din(nc, "xT", [D, T])
    wfi = _din(nc, "wfi", [L, D, 2 * DFF])
    wfo = _din(nc, "wfo", [L, DFF, D])
    wo = [_din(nc, "wo%d" % l, [KO * 128, D]) for l in range(L)]
    ab, gl = {}, {}
    for i in range((L + 1) // 2):
        ab[i] = dict(wh=_din(nc, "wh%d" % i, [D, 1024]), lbl=_din(nc, "lbl%d" % i, [128, 2, 2]),
                     again=_din(nc, "again%d" % i, [1, 256]), wf=_din(nc, "wf%d" % i, [D, 1026]),
                     qkg=_din(nc, "qkg%d" % i, [128, 2]), fbias=_din(nc, "fbias%d" % i, [1, 2]))
    for i in range(L // 2):
        gl[i] = dict(wg=_din(nc, "wg%d" % i, [D, 1552]), wgu=_din(nc, "wgu%d" % i, [16, 256]),
                     bg=_din(nc, "bg%d" % i, [128, 2]), ggain=_din(nc, "ggain%d" % i, [1, 512]))
    xTo = _dout(nc, "xTo", [D, T])
    vin = nc.dram_tensor("i_vin", [128, 6 * KD], F32)
    vall = nc.dram_tensor("i_vall", [4 * 128, 6 * KD], F32)
    xs = nc.dram_tensor("i_xs", [D, T], F32)
    HK = KD // 2
    hq = nc.dram_tensor("i_hq", [NT * 2, HK * 128, 512], BF16)
    hf = nc.dram_tensor("i_hf", [NT * 2, 4 * HK * 128, 512], BF16)
    OC = 2 if NT % 2 == 0 else 1
    NOC = (S // 512) // OC
    oqs = {nm: nc.dram_tensor("i_oq" + nm, [NOC, rows, OC * 512], BF16) for nm, rows in (("h", 256), ("f", 256), ("g", 512))}
    ofs = {nm: nc.dram_tensor("i_of" + nm, [NOC, 4 * rows, OC * 512], BF16)
           for nm, rows in (("h", 256), ("f", 256), ("g", 512))}
    wbo = nc.dram_tensor("i_wbo", [KO * 128, D], BF16)
    wbi = nc.dram_tensor("i_wbi", [D, 2 * DFF], BF16)
    wbf = nc.dram_tensor("i_wbf", [DFF, D], BF16)
    b_cv = [Buf() for _ in range(4)]
    conv = {"q": [], "per": 1, "n": 0}

    def conv_plan(l, n_calls):
        q = []
        if not PRECAST:
            conv["q"] = q
            return
        for (src, dst, rows, step) in ((wo[l], wbo.ap(), KO * 128, 512), (wfi[l], wbi.ap(), D, 128),
                                       (wfo[l], wbf.ap(), DFF, 512)):
            for r0 in range(0, rows, step):
                r1 = min(rows, r0 + step)
                q.append((src[r0:r1, :], dst[r0:r1, :]))
        conv["q"] = q
        conv["per"] = -(-len(q) // n_calls)

    def conv_step(n=None):
        n = conv["per"] if n is None else n
        for _ in range(min(n, len(conv["q"]))):
            src, dst = conv["q"].pop(0)
            key = b_cv[conv["n"] % 4]
            conv["n"] += 1
            K.dma(K.pool, lambda e, src=src, dst=dst: e.dma_start(out=dst, in_=src), [], [], key=key)

    b_vin, b_vall = Buf(), Buf()
    b_hq = [Buf() for _ in range(NT * 2)]
    b_hf = [Buf() for _ in range(NT * 2)]
    b_oq = {nm: [Buf() for _ in range(NOC)] for nm in "hfg"}
    b_of = {nm: [Buf() for _ in range(NOC)] for nm in "hfg"}
    vview = vall.ap().rearrange("(l p) c -> p l c", p=128)
    pending = []

    def flush_colls():
        while pending:
            pending.pop(0)()

    def h_store(t, tile, buf):
        for half in range(2):
            i = t * 2 + half
            dst = hq.ap()[i].rearrange("(k p) t -> p k t", p=128)
            K.dma(K.sp, lambda e, dst=dst, half=half: e.dma_start(out=dst, in_=tile[:, half * HK:(half + 1) * HK, :]),
                  [buf], [b_hq[i]], key=buf)
            pending.append(lambda i=i: K.coll("AllGather", hq.ap()[i], hf.ap()[i], [b_hq[i]], [b_hf[i]], GROUPS))

    def h_load(t, dst, buf):
        r, tl = t // NT, t % NT
        for half in range(2):
            i = tl * 2 + half
            src = hf.ap()[i].rearrange("(r k p) t -> p r k t", r=4, p=128)[:, r]
            K.dma(K.sp, lambda e, src=src, half=half: e.dma_start(out=dst[:, half * HK:(half + 1) * HK, :], in_=src),
                  [b_hf[i]], [buf], key=buf)

    def make_o_store(nm):
        def o_store(t, src, buf, nh):
            ci, cl = t // OC, t % OC
            dst = oqs[nm].ap()[ci].rearrange("(h p) t -> p h t", p=128)[:, :, cl * 512:(cl + 1) * 512]
            K.dma(K.pool, lambda e: e.dma_start(out=dst, in_=src), [buf], [b_oq[nm][ci]], key=buf)
            if cl == OC - 1:
                K.coll("AllGather", oqs[nm].ap()[ci], ofs[nm].ap()[ci], [b_oq[nm][ci]], [b_of[nm][ci]], GROUPS)
            conv_step()
        return o_store

    seg_cache = {}

    def make_o_load(names):
        def o_load(tt, oT, buf):
            k0 = 0
            for nm in names:
                rows = 4 * (256 if nm in "hf" else 512)
                nk = rows // 128

                def fn(e, nm=nm, nk=nk, k0=k0):
                    if "segC" not in seg_cache:
                        seg_cache["segC"] = (e.partition_id() % 4) * (NT // OC)
                    ci = seg_cache["segC"] + tt // OC
                    src = ofs[nm].ap().rearrange("c (k p) t -> p c k t", p=128)[
                        :, bass.ds(ci, 1), :, (tt % OC) * 512:(tt % OC + 1) * 512]
                    return e.dma_start(out=oT[:, k0:k0 + nk, :].rearrange("p (o k) t -> p o k t", o=1), in_=src)
                K.dma(K.sp, fn, [b for b in b_of[nm]], [buf], key=buf)
                k0 += nk
        return o_load

    C = Consts(K, cst, None)
    cfg1 = dict(cfg)
    cfg1["L"] = 1
    K.begin_phase()
    emit_mod(K, C, cfg1, cT, modw, modb, gmix, gffn, vin.ap().rearrange("p (l c) -> p l c", l=1), out_buf=b_vin)
    K.coll("AllGather", vin, vall, [b_vin], [b_vall], GROUPS)
    K.end_phase()
    K.begin_phase()
    emit_norm0(K, C, cfg, xT_in, vview, None, h_store=h_store, after_tile=flush_colls)
    flush_colls()
    K.end_phase()
    for l in range(L):
        i = l // 2
        if l % 2 == 0:
            conv_plan(l, 2 * (S // 512))
            K.begin_phase()
            emit_hgrn(K, C, cfg, i, None, ab[i]["wh"], ab[i]["lbl"], ab[i]["again"], None, h_load=h_load,
                      o_store=make_o_store("h"))
            K.end_phase()
            K.begin_phase()
            emit_fox(K, C, cfg, None, ab[i]["wf"], ab[i]["qkg"], ab[i]["fbias"], None, h_load=h_load,
                     o_store=make_o_store("f"))
            conv_step(len(conv["q"]))
            K.end_phase()
            names = "hf"
        else:
            conv_plan(l, S // 512)
            K.begin_phase()
            emit_gla(K, C, cfg, None, gl[i]["wg"], gl[i]["wgu"], gl[i]["bg"], gl[i]["ggain"], None, h_load=h_load,
                     o_store=make_o_store("g"))
            conv_step(len(conv["q"]))
            K.end_phase()
            names = "g"
        last = (l == L - 1)
        K.begin_phase()
        dw = (wbo.ap(), wbi.ap(), wbf.ap()) if PRECAST else (wo[l], wfi[l], wfo[l])
        emit_dense(K, C, cfg, l, xT_in if l == 0 else xs.ap(), None, dw[0], dw[1], dw[2], vview,
                   xTo if last else xs.ap(), None, o_load=make_o_load(names), h_store=None if last else h_store,
                   after_proj=flush_colls)
        flush_colls()
        K.end_phase()
    gstack.close()
    return nc


def dense_k_perm(kind):
    perm = []
    if kind == "ab":
        for typ in range(2):
            for r in range(4):
                for loc in range(256):
                    perm.append(r * 512 + typ * 256 + loc)
    else:
        perm = list(range(2048))
    return np.asarray(perm)


def kernel_fused(cfg, x, c, mod_w, mod_b, norm_mix_gain, norm_ffn_gain, ab_w_in, ab_w_out, hgrn_lb_logits,
                 hgrn_out_gain, fox_q_gain, fox_k_gain, fox_f_bias, gla_w_in, gla_w_gate_up, gla_b_gate, gla_out_gain,
                 gla_w_out, ffn_w_in, ffn_w_out):
    L, D, KD, T, S, DFF = cfg["L"], cfg["D"], cfg["KD"], cfg["T"], cfg["S"], cfg["DFF"]
    f32 = lambda a: np.ascontiguousarray(np.asarray(a, dtype=np.float32))
    x, c, mod_w, mod_b = f32(x), f32(c), f32(mod_w), f32(mod_b)
    consts = make_consts()
    ab_rows = []
    for j in range(4):
        for typ in range(2):
            for hl in range(2):
                base = typ * 1024 + (2 * j + hl) * 128
                ab_rows.extend(range(base, base + 128))
    ab_rows = np.asarray(ab_rows)
    wfi, wfo = f32(ffn_w_in), f32(ffn_w_out)
    shared = {"consts": consts, "wfi": wfi, "wfo": wfo}
    for l in range(L):
        i = l // 2
        if l % 2 == 0:
            shared["wo%d" % l] = np.ascontiguousarray(f32(ab_w_out[i])[ab_rows[dense_k_perm("ab")]])
        else:
            shared["wo%d" % l] = np.ascontiguousarray(f32(gla_w_out[i])[dense_k_perm("gla")])
    lbl_all = f32(hgrn_lb_logits)
    ims = []
    for cid in range(8):
        b, j = cid // 4, cid % 4
        im = dict(shared)
        im["cT"] = _fm(c[b], KD)
        im["modw"] = mod_w[j:j + 1]
        im["modb"] = mod_b[j:j + 1]
        im["gmix"] = _fm(f32(norm_mix_gain)[j:j + 1], KD)
        im["gffn"] = _fm(f32(norm_ffn_gain)[j:j + 1], KD)
        im["xT"] = np.ascontiguousarray(x[b, j * T:(j + 1) * T, :].T)
        hh = (2 * j, 2 * j + 1)
        for i in range((L + 1) // 2):
            w = f32(ab_w_in[i])
            cs = lambda base, h_: w[:, base + h_ * 128:base + (h_ + 1) * 128]
            im["wh%d" % i] = np.ascontiguousarray(np.concatenate(
                [cs(0, hh[0]), cs(1024, hh[0]), cs(0, hh[1]), cs(1024, hh[1]),
                 cs(2048, hh[0]), cs(2048, hh[1]), cs(3072, hh[0]), cs(3072, hh[1])], axis=1))
            im["lbl%d" % i] = f32(np.stack([np.stack([lbl_all[ly, h_ * 128:(h_ + 1) * 128] for h_ in hh], axis=1)
                                            for ly in range(2)], axis=1))
            im["again%d" % i] = f32(hgrn_out_gain[i][2 * j:2 * j + 2]).reshape(1, 256)
            im["wf%d" % i] = np.ascontiguousarray(np.concatenate(
                [cs(4096, hh[0]), cs(5120, hh[0]), cs(4096, hh[1]), cs(5120, hh[1]),
                 cs(6144, hh[0]), cs(6144, hh[1]), cs(7168, hh[0]), cs(7168, hh[1]),
                 w[:, 8192 + hh[0]:8192 + hh[0] + 1], w[:, 8192 + hh[1]:8192 + hh[1] + 1]], axis=1))
            im["qkg%d" % i] = f32(np.stack([fox_q_gain[i], fox_k_gain[i]], axis=1))
            im["fbias%d" % i] = f32(fox_f_bias[i][2 * j:2 * j + 2]).reshape(1, 2)
        for i in range(L // 2):
            w = f32(gla_w_in[i])
            im["wg%d" % i] = np.ascontiguousarray(np.concatenate(
                [w[:, j * 256:(j + 1) * 256], w[:, 1024 + j * 256:1024 + (j + 1) * 256], w[:, 6144:6160],
                 w[:, 2048 + j * 512:2048 + (j + 1) * 512], w[:, 4096 + j * 512:4096 + (j + 1) * 512]], axis=1))
            im["wgu%d" % i] = f32(gla_w_gate_up[i][:, j * 256:(j + 1) * 256])
            im["bg%d" % i] = f32(np.asarray(gla_b_gate[i][j * 256:(j + 1) * 256]).reshape(2, 128).T)
            im["ggain%d" % i] = f32(gla_out_gain[i]).reshape(1, 512)
        ims.append(im)
    nc = _prog("fused", lambda: build_fused(cfg))
    r = _run(nc, ims)
    out = np.empty((2, S, D), np.float32)
    for cid in range(8):
        b, j = cid // 4, cid % 4
        out[b, j * T:(j + 1) * T, :] = r[cid]["xTo"].T
    return out


def kernel(**inputs):
    return kernel_fused(CFG, **inputs)
```

```python
import contextlib
import numpy as np
import ml_dtypes
import concourse.bass as bass
import concourse.mybir as mybir
from concourse.bass_utils import run_bass_kernel_spmd

F32 = mybir.dt.float32
BF16 = mybir.dt.bfloat16
AF = mybir.ActivationFunctionType
ALU = mybir.AluOpType
EPS = 1e-6


class Sem:
    def __init__(self, h):
        self.h = h
        self.n = 0


class Buf:
    __slots__ = ("w", "r", "ds")

    def __init__(self):
        self.w = None
        self.r = {}
        self.ds = None


class Eng:
    def __init__(self, name, sem, is_pe=False):
        self.name = name
        self.sem = sem
        self.prog = []
        self.seen = {}
        self.is_pe = is_pe


class Kern:
    def __init__(self, nc, stack, n_dma_sems=12):
        self.nc = nc
        self.stack = stack
        mk = lambda n: Sem(stack.enter_context(nc.semaphore(n)))
        self.pe = Eng("tensor", mk("s_pe"), True)
        self.act = Eng("scalar", mk("s_act"))
        self.dve = Eng("vector", mk("s_dve"))
        self.pool = Eng("gpsimd", mk("s_pool"))
        self.sp = Eng("sync", mk("s_sp"))
        self.engs = [self.pe, self.act, self.dve, self.pool, self.sp]
        self.dsems = [None] * n_dma_sems
        self.nbuf = 0
        self.ccs = [mk("s_cc%d" % i) for i in range(4)]
        self.dsems.extend(self.ccs)
        self.ncoll = 0
        self.free_sems = []
        self.phase_sems = []
        self.pstack = None
        self.nphase = 0

    def sb(self, name, shape, dt):
        st = self.pstack if self.pstack is not None else self.stack
        return st.enter_context(self.nc.sbuf_tensor("%s_%d" % (name, self.nphase), list(shape), dt))

    def ps(self, name, shape, dt):
        st = self.pstack if self.pstack is not None else self.stack
        return st.enter_context(self.nc.psum_tensor("%s_%d" % (name, self.nphase), list(shape), dt))

    def begin_phase(self):
        self.pstack = contextlib.ExitStack()
        self.nphase += 1

    def end_phase(self):
        for e in self.engs:
            self.final_wait(e)
        self.emit()
        for e in self.engs:
            e.prog = []
        self.free_sems.extend(self.phase_sems)
        self.phase_sems = []
        self.pstack.close()
        self.pstack = None

    def coll(self, kind, in_t, out_t, R, W, groups):
        eng = self.pool
        cc = self.ccs[self.ncoll % len(self.ccs)]
        self.ncoll += 1
        waits = self._waits(eng, R, W)
        if cc.n > 0 and eng.seen.get(cc, 0) < cc.n:
            eng.seen[cc] = cc.n
            waits.append((cc, cc.n))
        cc.n += 1
        ev = (cc, cc.n)
        in_ap = in_t if isinstance(in_t, bass.AP) else in_t.ap()
        out_ap = out_t if isinstance(out_t, bass.AP) else out_t.ap()
        fn = lambda e: e.collective_compute(kind, ALU.bypass, replica_groups=groups, ins=[in_ap.opt()],
                                            outs=[out_ap.opt()])
        eng.prog.append((waits, fn, (cc, 1)))
        self._commit(ev, R, W)

    def _waits(self, eng, R, W):
        needs = {}

        def need(ev):
            s, v = ev
            if needs.get(s, 0) < v:
                needs[s] = v

        for b in R:
            if b.w is not None:
                if b.w[0] is eng.sem and eng.is_pe:
                    continue
                need(b.w)
        for b in W:
            if b.w is not None and b.w[0] is not eng.sem:
                need(b.w)
            for s, v in b.r.items():
                if s is not eng.sem:
                    need((s, v))
        out = []
        for s, v in needs.items():
            if eng.seen.get(s, 0) < v:
                eng.seen[s] = v
                out.append((s, v))
        return out

    def _commit(self, ev, R, W):
        for b in R:
            if b.r.get(ev[0], 0) < ev[1]:
                b.r[ev[0]] = ev[1]
        for b in W:
            b.w = ev
            b.r = {}

    def op(self, eng, fn, R=(), W=(), inc=True):
        waits = self._waits(eng, R, W)
        if inc:
            eng.sem.n += 1
            ev = (eng.sem, eng.sem.n)
        else:
            ev = (eng.sem, eng.sem.n + 1)
        eng.prog.append((waits, fn, (eng.sem, 1) if inc else None))
        self._commit(ev, R, W)

    def dma(self, eng, fn, R, W, dsem=None, key=None):
        if key is None:
            key = W[0] if W else R[0]
        if key.ds is None:
            if self.free_sems:
                key.ds = self.free_sems.pop()
            else:
                key.ds = Sem(self.stack.enter_context(self.nc.semaphore("s_d%d" % self.nbuf)))
                self.nbuf += 1
                self.dsems.append(key.ds)
            if self.pstack is not None:
                self.phase_sems.append(key.ds)
        dsem = key.ds
        waits = self._waits(eng, R, W)
        if dsem.n > 0 and eng.seen.get(dsem, 0) < dsem.n:
            eng.seen[dsem] = dsem.n
            waits.append((dsem, dsem.n))
        dsem.n += 16
        ev = (dsem, dsem.n)
        eng.prog.append((waits, fn, (dsem, 16)))
        self._commit(ev, R, W)

    def final_wait(self, eng):
        waits = []
        for s in [e.sem for e in self.engs] + [d for d in self.dsems if d is not None]:
            if s is not eng.sem and s.n > 0 and eng.seen.get(s, 0) < s.n:
                eng.seen[s] = s.n
                waits.append((s, s.n))
        eng.prog.append((waits, None, None))

    def emit(self):
        nc = self.nc

        def replay(eng):
            def run(e):
                for waits, fn, inc in eng.prog:
                    for s, v in waits:
                        e.wait_ge(s.h, v)
                    if fn is not None:
                        ins = fn(e)
                        if inc is not None:
                            ins.then_inc(inc[0].h, inc[1])
            return run

        with nc.Block() as block:
            block.tensor(replay(self.pe))
            block.scalar(replay(self.act))
            block.vector(replay(self.dve))
            block.gpsimd(replay(self.pool))
            block.sync(replay(self.sp))


def _chunks(n, m):
    return [(i, min(m, n - i)) for i in range(0, n, m)]


class Consts:
    def __init__(self, K, cdram, dsem):
        nc = K.nc
        self.f = K.sb("c_f32", [128, 5 * 128], F32)
        self.b = K.sb("c_bf16", [128, 5 * 128], BF16)
        self.buf = Buf()
        f, b = self.f, self.b
        K.dma(K.sp, lambda e: e.dma_start(out=f[:], in_=cdram[:, :]), [], [self.buf], dsem)
        K.op(K.dve, lambda e: e.tensor_copy(out=b[:], in_=f[:]), [self.buf], [self.buf])

    def ident_b(self, n=128):
        return self.b[0:n, 0:n]

    def tri_b(self, n=128):
        return self.b[0:n, 128:128 + n]

    def tri_f(self, n=128):
        return self.f[0:n, 128:128 + n]

    def ones_f(self, p=128, n=128):
        return self.f[0:p, 256:256 + n]

    def ones_b(self, p=128, n=128):
        return self.b[0:p, 256:256 + n]

    def sel_f(self):
        return self.f[:, 384:512]

    def ident_f(self):
        return self.f[:, 0:128]

    def negmask_b(self):
        return self.b[:, 512:640]


def make_consts():
    c = np.zeros((128, 5 * 128), np.float32)
    c[:, 0:128] = np.eye(128)
    c[:, 128:256] = np.triu(np.ones((128, 128)))
    c[:, 256:384] = 1.0
    c[127, 384:512] = 1.0
    c[:, 512:640] = -30000.0 * np.tril(np.ones((128, 128)), -1)
    return c


def emit_norm(K, C, xT, xbuf, KD, D, G, Sh, vbuf, outT, obuf, ssbank, ssb, tmp):
    sq, sqb, rs, rsb, tm, tmb = tmp["sq"], tmp["sqb"], tmp["rs"], tmp["rsb"], tmp["tm"], tmp["tmb"]
    for j in range(KD):
        s = j % 2
        K.op(K.act, lambda e, j=j, s=s: e.activation(out=sq[:, s, :], in_=xT[:, j, :], func=AF.Square),
             [xbuf], [sqb[s]])
        K.op(K.pe, lambda e, j=j, s=s: e.matmul(ssbank[:, :], lhsT=C.ones_f(), rhs=sq[:, s, :],
                                               start=(j == 0), stop=(j == KD - 1)),
             [sqb[s], C.buf], [ssb], inc=True)
    K.op(K.act, lambda e: e.activation(out=rs[:, :], in_=ssbank[:, :], func=AF.Sqrt, scale=1.0 / D, bias=EPS),
         [ssb], [rsb])
    K.op(K.dve, lambda e: e.reciprocal(out=rs[:, :], in_=rs[:, :]), [rsb], [rsb])
    for j in range(KD):
        s = j % 2
        K.op(K.dve, lambda e, j=j, s=s: e.scalar_tensor_tensor(out=tm[:, s, :], in0=xT[:, j, :], scalar=G[:, j:j + 1],
                                                             in1=rs[:, :], op0=ALU.mult, op1=ALU.mult),
             [xbuf, rsb, vbuf], [tmb[s]])
        K.op(K.act, lambda e, j=j, s=s: e.activation(out=outT[:, j, :], in_=tm[:, s, :], func=AF.Identity,
                                                   bias=Sh[:, j:j + 1], scale=1.0),
             [tmb[s], vbuf], [obuf])


def emit_mod(K, C, cfg, cT_d, modw_d, modb_d, gmix_d, gffn_d, vec_d, out_buf=None):
    L, D, KD = cfg["L"], cfg["D"], cfg["KD"]
    W6 = 6 * D
    CG = min(2048, W6)
    NB = CG // 512
    assert W6 % CG == 0
    ds_a, ds_w = K.dsems[0], K.dsems[1]
    cT = K.sb("m_cT", [128, KD], F32)
    gm = K.sb("m_gm", [128, L, KD], F32)
    gf = K.sb("m_gf", [128, L, KD], F32)
    row = K.sb("m_row", [1, W6], F32)
    brow = K.sb("m_brow", [1, W6], F32)
    wp = K.sb("m_wp", [128, 3, CG], F32)
    mt = K.sb("m_mt", [128, 6 * KD], F32)
    vec = K.sb("m_vec", [128, L, 6 * KD], F32)
    banks = [K.ps("m_ps%d" % i, [128, 512], F32) for i in range(NB)]
    tp = K.ps("m_tp", [128, 512], F32)
    b_c, b_g, b_row, b_brow, b_mt, b_vec, b_tp = Buf(), Buf(), Buf(), Buf(), Buf(), Buf(), Buf()
    b_wp = [Buf() for _ in range(3)]
    b_bk = [Buf() for _ in range(NB)]
    K.dma(K.sp, lambda e: e.dma_start(out=cT[:], in_=cT_d[:, :]), [], [b_c], ds_a)
    K.dma(K.sp, lambda e: e.dma_start(out=gm[:], in_=gmix_d[:, :, :]), [], [b_g], ds_a)
    K.dma(K.sp, lambda e: e.dma_start(out=gf[:], in_=gffn_d[:, :, :]), [], [b_g], ds_a)
    K.op(K.act, lambda e: e.activation(out=cT[:], in_=cT[:], func=AF.Silu), [b_c], [b_c])
    ip = 0
    for l in range(L):
        K.dma(K.sp, lambda e, l=l: e.dma_start(out=brow[:], in_=modb_d[l:l + 1, :]), [], [b_brow], ds_a)
        for cg in range(W6 // CG):
            for k in range(KD):
                s = ip % 3
                ip += 1
                K.dma(K.sp, lambda e, l=l, cg=cg, k=k, s=s: e.dma_start(
                    out=wp[:, s, :], in_=modw_d[l, k * 128:(k + 1) * 128, cg * CG:(cg + 1) * CG]),
                    [], [b_wp[s]], ds_w)
                for q in range(NB):
                    K.op(K.pe, lambda e, k=k, s=s, q=q: e.matmul(
                        banks[q][0:1, :], lhsT=cT[:, k:k + 1], rhs=wp[:, s, q * 512:(q + 1) * 512],
                        start=(k == 0), stop=(k == KD - 1)), [b_c, b_wp[s]], [b_bk[q]])
            for q in range(NB):
                c0 = cg * CG + q * 512
                K.op(K.dve, lambda e, q=q, c0=c0: e.tensor_tensor(
                    out=row[0:1, c0:c0 + 512], in0=banks[q][0:1, :], in1=brow[0:1, c0:c0 + 512], op=ALU.add),
                    [b_bk[q], b_brow], [b_row])
        for c in range(6 * KD):
            K.op(K.pe, lambda e, c=c: e.matmul(tp[:, c:c + 1], lhsT=row[0:1, c * 128:(c + 1) * 128],
                                               rhs=C.ones_f(1, 1), start=True, stop=True),
                 [b_row, C.buf], [b_tp])
        K.op(K.act, lambda e: e.activation(out=mt[:, :], in_=tp[:, 0:6 * KD], func=AF.Identity), [b_tp], [b_mt])
        for (dst, src_sc, gn) in ((0, 1, gm), (3, 4, gf)):
            K.op(K.dve, lambda e, l=l, dst=dst, src_sc=src_sc, gn=gn: e.scalar_tensor_tensor(
                out=vec[:, l, dst * KD:(dst + 1) * KD], in0=mt[:, src_sc * KD:(src_sc + 1) * KD], scalar=1.0,
                in1=gn[:, l, :], op0=ALU.add, op1=ALU.mult), [b_mt, b_g], [b_vec])
        for (dst, src) in ((1, 0), (2, 2), (4, 3), (5, 5)):
            K.op(K.dve, lambda e, l=l, dst=dst, src=src: e.tensor_copy(
                out=vec[:, l, dst * KD:(dst + 1) * KD], in_=mt[:, src * KD:(src + 1) * KD]), [b_mt], [b_vec])
    K.dma(K.sp, lambda e: e.dma_start(out=vec_d[:, :, :], in_=vec[:]), [b_vec], [out_buf] if out_buf else [], key=b_vec)


def norm_scratch(K, pfx):
    return {
        "sq": K.sb(pfx + "sq", [128, 2, 512], F32), "sqb": [Buf(), Buf()],
        "rs": K.sb(pfx + "rs", [128, 512], F32), "rsb": Buf(),
        "tm": K.sb(pfx + "tm", [128, 2, 512], F32), "tmb": [Buf(), Buf()],
    }


def emit_norm0(K, C, cfg, xT_d, vec_d, hT_d, h_store=None, after_tile=None):
    D, KD, T = cfg["D"], cfg["KD"], cfg["T"]
    ds_a = K.dsems[0]
    vec = K.sb("n_vec", [128, 6 * KD], F32)
    xT = K.sb("n_xT", [128, 2, KD, 512], F32)
    hT = K.sb("n_hT", [128, 2, KD, 512], BF16)
    ss = K.ps("n_ss", [128, 512], F32)
    b_v, b_ss = Buf(), Buf()
    b_x, b_h = [Buf(), Buf()], [Buf(), Buf()]
    tmp = norm_scratch(K, "n_")
    K.dma(K.sp, lambda e: e.dma_start(out=vec[:], in_=vec_d[:, 0, :]), [], [b_v], ds_a)
    xv = xT_d.rearrange("(k p) t -> p k t", p=128)
    hv = None if h_store is not None else hT_d.rearrange("(k p) t -> p k t", p=128)
    def xload(t):
        K.dma(K.sp, lambda e, t=t: e.dma_start(out=xT[:, t % 2], in_=xv[:, :, t * 512:(t + 1) * 512]),
              [], [b_x[t % 2]], ds_a)

    xload(0)
    for t in range(T // 512):
        s = t % 2
        if t + 1 < T // 512:
            xload(t + 1)
        emit_norm(K, C, xT[:, s], b_x[s], KD, D, vec[:, 0:KD], vec[:, KD:2 * KD], b_v, hT[:, s], b_h[s], ss, b_ss, tmp)
        if h_store is not None:
            h_store(t, hT[:, s], b_h[s])
            if after_tile is not None:
                after_tile()
        else:
            K.dma(K.sp, lambda e, t=t, s=s: e.dma_start(out=hv[:, :, t * 512:(t + 1) * 512], in_=hT[:, s]),
                  [b_h[s]], [], ds_a)


def emit_dense(K, C, cfg, l, xT_d, oT_d, wo_d, wfi_d, wfo_d, vec_d, xTo_d, hTo_d, o_load=None, h_store=None,
               after_proj=None):
    D, KD, T, DFF, KF, KO, L = cfg["D"], cfg["KD"], cfg["T"], cfg["DFF"], cfg["KF"], cfg["KO"], cfg["L"]
    DG = D // 512 if D >= 512 else 1
    GW = min(512, D)
    GC = GW // 128
    FG = DFF // 512
    assert DFF % 512 == 0
    ds_a, ds_w, ds_s = K.dsems[0], K.dsems[1], K.dsems[2]
    last = hTo_d is None and h_store is None
    vec = K.sb("d_vec", [128, 2, 6 * KD], F32)
    xin = K.sb("d_xin", [128, KD, 512], F32)
    xT = K.sb("d_xT", [128, KD, 512], F32)
    oT2 = K.sb("d_oT", [128, 1, KO, 512], BF16)
    oT = K.sb("d_hT", [128, KD, 512], BF16)
    halves = [list(range(0, (FG + 1) // 2)), list(range((FG + 1) // 2, FG))]
    AH = 4 * len(halves[0])
    aT = K.sb("d_aT", [128, AH, 512], BF16)
    sa = K.sb("d_sa", [128, 2, 512], BF16)
    WB = 4
    WSZ = max(KO * GW, KD * 512, 16 * GW)
    wb = K.sb("d_wb", [128, WB, WSZ], BF16)
    banks = [K.ps("d_ps%d" % i, [128, 512], F32) for i in range(8)]
    b_bk = [Buf() for _ in range(8)]
    b_v, b_x, b_o, b_a, b_xin = Buf(), Buf(), Buf(), Buf(), Buf()
    b_o2 = [Buf(), Buf()]
    b_sa = [Buf(), Buf()]
    b_wb = [Buf() for _ in range(WB)]
    tmp = norm_scratch(K, "d_")
    K.dma(K.sp, lambda e: e.dma_start(out=vec[:, 0, :], in_=vec_d[:, l, :]), [], [b_v], ds_a)
    if not last:
        K.dma(K.sp, lambda e: e.dma_start(out=vec[:, 1, :], in_=vec_d[:, l + 1, :]), [], [b_v], ds_a)
    g1 = vec[:, 0, 2 * KD:3 * KD]
    G2 = vec[:, 0, 3 * KD:4 * KD]
    Sh2 = vec[:, 0, 4 * KD:5 * KD]
    g2 = vec[:, 0, 5 * KD:6 * KD]
    G1n = vec[:, 1, 0:KD]
    Sh1n = vec[:, 1, KD:2 * KD]
    xv = xT_d.rearrange("(k p) t -> p k t", p=128)
    ov = None if o_load is not None else oT_d.rearrange("(k p) t -> p k t", p=128)
    xov = xTo_d.rearrange("(k p) t -> p k t", p=128)
    hov = None if (last or h_store is not None) else hTo_d.rearrange("(k p) t -> p k t", p=128)
    wov = wo_d.rearrange("(k p) c -> p k c", p=128)
    wiv = wfi_d.rearrange("(k p) c -> p k c", p=128)
    wfv = wfo_d.rearrange("(k p) c -> p k c", p=128)
    st = {"w": 0, "bank": 0}

    def load_w(src_ap, kk, cc):
        s = st["w"] % WB
        st["w"] += 1
        dst = wb[:, s, 0:kk * cc].rearrange("p (k c) -> p k c", k=kk)
        K.dma(K.pool, lambda e: e.dma_start(out=dst, in_=src_ap), [], [b_wb[s]], ds_w)
        return dst, b_wb[s]

    def proj_group(pieces, rhsT, rbuf, nk, og, gvec, src=None, sbuf=None, kofs=0):
        if src is None:
            src, sbuf = xT, b_x
        base = (st["bank"] % 2) * 4
        st["bank"] += 1
        kdone = 0
        for (wt, wbuf, k0, kn) in pieces:
            for kk in range(kn):
                for i in range(GC):
                    K.op(K.pe, lambda e, wt=wt, kk=kk, i=i, k=k0 + kk - kofs: e.matmul(
                        banks[base + i][:, :], lhsT=wt[:, kk, i * 128:(i + 1) * 128], rhs=rhsT[:, k, :],
                        start=(k == 0), stop=(k == nk - 1)), [wbuf, rbuf], [b_bk[base + i]],
                        inc=(k0 + kk - kofs == nk - 1 or kk == kn - 1))
        for i in range(GC):
            j = og * GC + i
            K.op(K.dve, lambda e, i=i, j=j: e.scalar_tensor_tensor(
                out=xT[:, j, :], in0=banks[base + i][:, :], scalar=gvec[:, j:j + 1], in1=src[:, j, :],
                op0=ALU.mult, op1=ALU.add), [b_bk[base + i], sbuf, b_v], [b_x])

    NTL = T // 512

    def loads(t):
        tsl = slice(t * 512, (t + 1) * 512)
        K.dma(K.sp, lambda e: e.dma_start(out=xin[:], in_=xv[:, :, tsl]), [], [b_xin])
        if o_load is not None:
            o_load(t, oT2[:, 0], b_o2[0])
        else:
            K.dma(K.sp, lambda e: e.dma_start(out=oT2[:, 0], in_=ov[:, :, tsl]), [], [b_o2[0]])

    loads(0)
    for t in range(NTL):
        tsl = slice(t * 512, (t + 1) * 512)
        for og in range(DG):
            wt, wbuf = load_w(wov[:, :, og * GW:(og + 1) * GW], KO, GW)
            proj_group([(wt, wbuf, 0, KO)], oT2[:, 0], b_o2[0], KO, og, g1, src=xin, sbuf=b_xin)
        if t + 1 < NTL:
            loads(t + 1)
        emit_norm(K, C, xT, b_x, KD, D, G2, Sh2, b_v, oT, b_o, banks[7], b_bk[7], tmp)
        for hg in halves:
            if not hg:
                continue
            for fg in hg:
                wa, wab = load_w(wiv[:, :, fg * 512:(fg + 1) * 512], KD, 512)
                wu, wub = load_w(wiv[:, :, DFF + fg * 512:DFF + (fg + 1) * 512], KD, 512)
                for i in range(4):
                    j = fg * 4 + i
                    jl = j - hg[0] * 4
                    ba = (2 * j) % 6
                    bu = ba + 1
                    for (wt, wbf, bk) in ((wa, wab, ba), (wu, wub, bu)):
                        for k in range(KD):
                            K.op(K.pe, lambda e, wt=wt, k=k, i=i, bk=bk: e.matmul(
                                banks[bk][:, :], lhsT=wt[:, k, i * 128:(i + 1) * 128], rhs=oT[:, k, :],
                                start=(k == 0), stop=(k == KD - 1)), [wbf, b_o], [b_bk[bk]], inc=(k == KD - 1))
                    s_ = j % 2
                    K.op(K.act, lambda e, s_=s_, ba=ba: e.activation(out=sa[:, s_, :], in_=banks[ba][:, :], func=AF.Silu),
                         [b_bk[ba]], [b_sa[s_]])
                    K.op(K.dve, lambda e, s_=s_, jl=jl, bu=bu: e.tensor_tensor(
                        out=aT[:, jl, :], in0=sa[:, s_, :], in1=banks[bu][:, :], op=ALU.mult),
                        [b_sa[s_], b_bk[bu]], [b_a])
            kbase, nkh = hg[0] * 4, len(hg) * 4
            if hg is halves[-1] or not halves[-1]:
                if after_proj is not None:
                    after_proj()
            for og in range(DG):
                pieces = []
                for (k0, kn) in _chunks(nkh, 16):
                    wt, wbuf = load_w(wfv[:, kbase + k0:kbase + k0 + kn, og * GW:(og + 1) * GW], kn, GW)
                    pieces.append((wt, wbuf, kbase + k0, kn))
                proj_group(pieces, aT, b_a, nkh, og, g2, kofs=kbase)
        K.dma(K.sp, lambda e, tsl=tsl: e.dma_start(out=xov[:, :, tsl], in_=xT[:]), [b_x], [], ds_s)
        if not last:
            emit_norm(K, C, xT, b_x, KD, D, G1n, Sh1n, b_v, oT, b_o, banks[7], b_bk[7], tmp)
            if h_store is not None:
                h_store(t, oT, b_o)
            else:
                K.dma(K.sp, lambda e, tsl=tsl: e.dma_start(out=hov[:, :, tsl], in_=oT[:, 0:KD, :]), [b_o], [], ds_s)


def _new():
    nc = bass.Bass("TRN2", target_bir_lowering=False)
    stack = contextlib.ExitStack()
    K = Kern(nc, stack)
    return nc, stack, K


def _din(nc, name, shape, dt=F32):
    return nc.dram_tensor(name, list(shape), dt, kind="ExternalInput").ap()


def _dout(nc, name, shape, dt=F32):
    return nc.dram_tensor(name, list(shape), dt, kind="ExternalOutput").ap()


def _finish(K, stack):
    K.final_wait(K.sp)
    K.emit()
    stack.close()


def build_mod(cfg):
    L, D, KD = cfg["L"], cfg["D"], cfg["KD"]
    nc, stack, K = _new()
    cst = _din(nc, "consts", [128, 640])
    cT = _din(nc, "cT", [128, KD])
    modw = _din(nc, "modw", [L, D, 6 * D])
    modb = _din(nc, "modb", [L, 6 * D])
    gmix = _din(nc, "gmix", [128, L, KD])
    gffn = _din(nc, "gffn", [128, L, KD])
    vec = _dout(nc, "vec", [128, L, 6 * KD])
    C = Consts(K, cst, K.dsems[0])
    emit_mod(K, C, cfg, cT, modw, modb, gmix, gffn, vec)
    _finish(K, stack)
    return nc


def build_norm0(cfg):
    L, D, KD, T = cfg["L"], cfg["D"], cfg["KD"], cfg["T"]
    nc, stack, K = _new()
    cst = _din(nc, "consts", [128, 640])
    xT = _din(nc, "xT", [D, T])
    vec = _din(nc, "vec", [128, L, 6 * KD])
    hT = _dout(nc, "hT", [D, T], BF16)
    C = Consts(K, cst, K.dsems[0])
    emit_norm0(K, C, cfg, xT, vec, hT)
    _finish(K, stack)
    return nc


def build_dense(cfg, last):
    L, D, KD, T, DFF, KO = cfg["L"], cfg["D"], cfg["KD"], cfg["T"], cfg["DFF"], cfg["KO"]
    nc, stack, K = _new()
    cst = _din(nc, "consts", [128, 640])
    xT = _din(nc, "xT", [D, T])
    oT = _din(nc, "oT", [KO * 128, T], BF16)
    wo = _din(nc, "wo", [KO * 128, D])
    wfi = _din(nc, "wfi", [D, 2 * DFF])
    wfo = _din(nc, "wfo", [DFF, D])
    vec = _din(nc, "vec", [128, 2, 6 * KD])
    xTo = _dout(nc, "xTo", [D, T])
    hTo = None if last else _dout(nc, "hTo", [D, T], BF16)
    C = Consts(K, cst, K.dsems[0])
    cfg2 = dict(cfg)
    cfg2["L"] = 2
    emit_dense(K, C, cfg2, 0, xT, oT, wo, wfi, wfo, vec, xTo, hTo)
    _finish(K, stack)
    return nc


def emit_hgrn(K, C, cfg, ab_idx, hT_d, w_d, lbl_d, gain_d, oT_d, h_load=None, o_store=None):
    D, KD, S = cfg["D"], cfg["KD"], cfg["S"]
    ds_a, ds_w, ds_s = K.dsems[0], K.dsems[1], K.dsems[2]
    CH, NCH = 64, 8
    W = K.sb("h_W", [128, KD, 1024], BF16)
    hT = K.sb("h_hT", [128, 2, KD, 512], BF16)
    lbl = K.sb("h_lbl", [128, 2, 2], F32)
    lbv = K.sb("h_lbv", [128, 3, 2], F32)
    gain = K.sb("h_gain", [64, 256], F32)
    ones = K.sb("h_ones", [128, 64], F32)
    names = ["sig", "lf", "key", "qs", "cum", "cm", "cl", "e0", "e1", "e2"]
    hf = [{n: K.sb("h_%s%d" % (n, h), [128, 512], F32) for n in names} for h in range(2)]
    hb = [{n: K.sb("h_%s%d" % (n, h), [128, 512], BF16) for n in ["qh", "qt", "kt", "kh", "k0"]} for h in range(2)]
    eL = [K.sb("h_eL%d" % h, [128, NCH], F32) for h in range(2)]
    Sst = [K.sb("h_S%d" % h, [128, 128], F32) for h in range(2)]
    Sbf = [K.sb("h_Sb%d" % h, [128, 2, 128], BF16) for h in range(2)]
    vc = K.sb("h_vc", [64, 2, 256], BF16)
    gs = K.sb("h_gs", [64, 2, 256], F32)
    PT = K.sb("h_PT", [64, 2, 2, 64], BF16)
    khs = K.sb("h_khs", [64, 2, 2, 128], BF16)
    junk = K.sb("h_junk", [64, 128], F32)
    ssq = K.sb("h_ssq", [64, 2, 2], F32)
    og = K.sb("h_og", [64, 2, 128], BF16)
    oTt = K.sb("h_oTt", [128, 2, 2, 512], BF16)
    fb2 = [K.ps("h_fb%d" % i, [128, 512], F32) for i in range(2)]
    fb = [fb2[0], fb2[1], fb2[0], fb2[1]]
    tb = [K.ps("h_tb%d" % i, [128, 512], F32) for i in range(2)]
    bkO = K.ps("h_bkO", [128, 512], F32)
    bkU = K.ps("h_bkU", [128, 512], F32)
    bkT = [K.ps("h_bkT%d" % i, [128, 1024], BF16) for i in range(2)]
    B = lambda n: [Buf() for _ in range(n)]
    b_W, b_lb, b_gain, b_ones = Buf(), Buf(), Buf(), Buf()
    b_hT, b_tb = B(2), B(2)
    b_fb2 = B(2)
    b_fb = [b_fb2[0], b_fb2[1], b_fb2[0], b_fb2[1]]
    b_hf = [{n: Buf() for n in names} for _ in range(2)]
    b_hb = [{n: Buf() for n in ["qh", "qt", "kt", "kh", "k0"]} for _ in range(2)]
    b_eL, b_S, b_Sbf = B(2), B(2), [B(2), B(2)]
    b_vc, b_gs, b_PT, b_khs, b_junk, b_ssq, b_og = B(2), B(2), B(2), B(2), Buf(), B(2), B(2)
    b_bkO, b_bkU, b_bkT = Buf(), Buf(), B(2)
    b_oTt = B(2)
    A = K.op
    wv = w_d.rearrange("(k p) c -> p k c", p=128)
    for half in range(2):
        K.dma(K.pool, lambda e, half=half: e.dma_start(out=W[:, :, half * 512:(half + 1) * 512],
                                                       in_=wv[:, :, half * 512:(half + 1) * 512]), [], [b_W], ds_w)
    K.dma(K.sp, lambda e: e.dma_start(out=lbl[:], in_=lbl_d[:, :, :]), [], [b_lb], ds_a)
    K.dma(K.sp, lambda e: e.dma_start(out=gain[:], in_=gain_d[0:1, :].partition_broadcast(64)), [], [b_gain], ds_a)
    A(K.dve, lambda e: e.memset(ones[:], 1.0), [], [b_ones])
    if ab_idx == 0:
        A(K.dve, lambda e: e.memset(lbv[:, 0, :], 0.0), [b_lb], [b_lb])
    else:
        A(K.dve, lambda e: e.tensor_tensor(out=lbv[:, 0, :], in0=lbl[:, 1, :], in1=lbl[:, 0, :], op=ALU.subtract),
          [b_lb], [b_lb])
        A(K.act, lambda e: e.activation(out=lbv[:, 0, :], in_=lbv[:, 0, :], func=AF.Sigmoid), [b_lb], [b_lb])
        A(K.dve, lambda e: e.tensor_scalar(out=lbv[:, 0, :], in0=lbv[:, 0, :], scalar1=1.0 - 1e-6, scalar2=0.0,
                                           op0=ALU.min, op1=ALU.max), [b_lb], [b_lb])
    A(K.dve, lambda e: e.tensor_scalar(out=lbv[:, 1, :], in0=lbv[:, 0, :], scalar1=-1.0, scalar2=1.0,
                                       op0=ALU.mult, op1=ALU.add), [b_lb], [b_lb])
    A(K.dve, lambda e: e.tensor_scalar(out=lbv[:, 2, :], in0=lbv[:, 0, :], scalar1=1.0, scalar2=-1.0,
                                       op0=ALU.mult, op1=ALU.add), [b_lb], [b_lb])
    for h in range(2):
        A(K.dve, lambda e, h=h: e.memset(PT[:, h], 0.0), [], [b_PT[h]])
        A(K.dve, lambda e, h=h: e.memset(hb[h]["k0"][:], 0.0), [], [b_hb[h]["k0"]])
        A(K.dve, lambda e, h=h: e.memset(hf[h]["e2"][:], 0.0), [], [b_hf[h]["e2"]])
        A(K.dve, lambda e, h=h: e.memset(Sst[h][:], 0.0), [], [b_S[h]])
        A(K.dve, lambda e, h=h: e.memset(Sbf[h][:, 0, :], 0.0), [], [b_Sbf[h][0]])
    hv = None if h_load is not None else hT_d.rearrange("(k p) t -> p k t", p=128)
    ov = None if o_store is not None else oT_d.rearrange("(h p) t -> p h t", p=128)
    cidx = 0
    for t in range(S // 512):
        s = t % 2
        if h_load is not None:
            h_load(t, hT[:, s], b_hT[s])
        else:
            K.dma(K.sp, lambda e, t=t, s=s: e.dma_start(out=hT[:, s], in_=hv[:, :, t * 512:(t + 1) * 512]),
                  [], [b_hT[s]], ds_a)
        for h in range(2):
            for blk in (2 * h, 2 * h + 1):
                for k in range(KD):
                    A(K.pe, lambda e, blk=blk, k=k, s=s: e.matmul(fb[blk][:, :], lhsT=W[:, k, blk * 128:(blk + 1) * 128],
                                                                  rhs=hT[:, s, k, :], start=(k == 0), stop=(k == KD - 1)),
                      [b_W, b_hT[s]], [b_fb[blk]], inc=(k == KD - 1))
            f, bf_, bb, bbf = hf[h], hb[h], b_hf[h], b_hb[h]
            lb_, oml, noml = lbv[:, 0, h:h + 1], lbv[:, 1, h:h + 1], lbv[:, 2, h:h + 1]
            A(K.act, lambda e, f=f, h=h: e.activation(out=f["sig"][:], in_=fb[2 * h + 1][:, :], func=AF.Sigmoid),
              [b_fb[2 * h + 1]], [bb["sig"]])
            A(K.act, lambda e, f=f, h=h: e.activation(out=f["qs"][:], in_=fb[2 * h][:, :], func=AF.Silu),
              [b_fb[2 * h]], [bb["qs"]])
            A(K.dve, lambda e, f=f, oml=oml, lb_=lb_: e.tensor_scalar(out=f["lf"][:], in0=f["sig"][:], scalar1=oml,
                                                                     scalar2=lb_, op0=ALU.mult, op1=ALU.add),
              [bb["sig"], b_lb], [bb["lf"]])
            A(K.dve, lambda e, f=f: e.tensor_scalar_max(out=f["lf"][:], in0=f["lf"][:], scalar1=1e-30),
              [bb["lf"]], [bb["lf"]])
            A(K.act, lambda e, f=f: e.activation(out=f["lf"][:], in_=f["lf"][:], func=AF.Ln), [bb["lf"]], [bb["lf"]])
            A(K.dve, lambda e, f=f, oml=oml, noml=noml: e.tensor_scalar(out=f["key"][:], in0=f["sig"][:], scalar1=noml,
                                                                       scalar2=oml, op0=ALU.mult, op1=ALU.add),
              [bb["sig"], b_lb], [bb["key"]])
            for c in range(NCH):
                A(K.dve, lambda e, f=f, c=c: e.tensor_tensor_scan(out=f["cum"][:, c * CH:(c + 1) * CH], data0=ones[:, :],
                                                                  data1=f["lf"][:, c * CH:(c + 1) * CH], initial=0.0,
                                                                  op0=ALU.mult, op1=ALU.add),
                  [bb["lf"], b_ones], [bb["cum"]])
            c3 = lambda a: a[:].rearrange("p (c t) -> p c t", c=NCH)
            A(K.dve, lambda e, f=f: e.tensor_tensor(out=c3(f["cm"]), in0=c3(f["cum"]),
                                                    in1=c3(f["cum"])[:, :, 31:32].broadcast_to([128, NCH, CH]),
                                                    op=ALU.subtract), [bb["cum"]], [bb["cm"]])
            A(K.dve, lambda e, f=f: e.tensor_tensor(out=c3(f["cl"]), in0=c3(f["cum"]),
                                                    in1=c3(f["cum"])[:, :, CH - 1:CH].broadcast_to([128, NCH, CH]),
                                                    op=ALU.subtract), [bb["cum"]], [bb["cl"]])
            A(K.act, lambda e, f=f, h=h: e.activation(out=eL[h][:, :], in_=c3(f["cum"])[:, :, CH - 1], func=AF.Exp),
              [bb["cum"]], [b_eL[h]])
            A(K.act, lambda e, f=f: e.activation(out=c3(f["e2"])[:, :, 0:32], in_=c3(f["cum"])[:, :, 0:32], func=AF.Exp,
                                                 scale=-1.0), [bb["cum"]], [bb["e2"]])
            A(K.dve, lambda e, f=f, bf_=bf_: e.tensor_tensor(out=c3(bf_["k0"])[:, :, 0:32], in0=c3(f["key"])[:, :, 0:32],
                                                             in1=c3(f["e2"])[:, :, 0:32], op=ALU.mult),
              [bb["key"], bb["e2"]], [bbf["k0"]])
            for (src, sc, es, mul, dst) in (("cum", 1.0, "e0", "qs", "qh"), ("cm", 1.0, "e1", "qs", "qt"),
                                            ("cm", -1.0, "e0", "key", "kt"), ("cl", -1.0, "e1", "key", "kh")):
                A(K.act, lambda e, f=f, src=src, sc=sc, es=es: e.activation(out=f[es][:], in_=f[src][:], func=AF.Exp,
                                                                           scale=sc), [bb[src]], [bb[es]])
                A(K.dve, lambda e, f=f, bf_=bf_, es=es, mul=mul, dst=dst: e.tensor_tensor(
                    out=bf_[dst][:], in0=f[mul][:], in1=f[es][:], op=ALU.mult), [bb[mul], bb[es]], [bbf[dst]])
        def prep_proj(c, s=s):
            csl = slice(c * CH, (c + 1) * CH)
            par = c % 2
            for k in range(KD):
                A(K.pe, lambda e, k=k: e.matmul(tb[par][0:CH, :], lhsT=hT[:, s, k, csl], rhs=W[:, k, 512:1024],
                                                start=(k == 0), stop=(k == KD - 1)),
                  [b_W, b_hT[s]], [b_tb[par]], inc=(k == KD - 1))
            A(K.act, lambda e: e.activation(out=vc[:, par, :], in_=tb[par][0:CH, 0:256], func=AF.Identity),
              [b_tb[par]], [b_vc[par]])
            A(K.act, lambda e: e.activation(out=gs[:, par, :], in_=tb[par][0:CH, 256:512], func=AF.Silu),
              [b_tb[par]], [b_gs[par]])
            A(K.dve, lambda e: e.tensor_tensor(out=gs[:, par, :], in0=gs[:, par, :], in1=gain[:, :], op=ALU.mult),
              [b_gs[par], b_gain], [b_gs[par]])

        def prep(c, s=s):
            csl = slice(c * CH, (c + 1) * CH)
            c0_ = c * CH
            par = c % 2
            if c >= 2:
                prep_proj(c)
            for h in range(2):
                bf_, bbf = hb[h], b_hb[h]
                A(K.pe, lambda e, h=h, bf_=bf_: e.matmul(fb2[par][0:64, h * 64 + 32:h * 64 + 64], lhsT=bf_["kt"][:, csl],
                                                         rhs=bf_["qt"][:, c0_ + 32:c0_ + 64], start=True, stop=True),
                  [bbf["kt"], bbf["qt"]], [b_fb2[par]], inc=False)
                A(K.pe, lambda e, h=h, bf_=bf_: e.matmul(fb2[par][0:32, h * 64:h * 64 + 32], lhsT=bf_["k0"][:, c0_:c0_ + 32],
                                                         rhs=bf_["qh"][:, c0_:c0_ + 32], start=True, stop=True),
                  [bbf["k0"], bbf["qh"]], [b_fb2[par]], inc=False)
                A(K.pe, lambda e, h=h, bf_=bf_: e.transpose(bkT[par][0:64, h * 128:(h + 1) * 128], bf_["kh"][:, csl],
                                                            C.ident_b()), [bbf["kh"], C.buf], [b_bkT[par]], inc=(h == 1))
            scv = fb2[par][0:64, 0:128].rearrange("p (h t) -> p h t", h=2)
            A(K.dve, lambda e: e.tensor_tensor(out=PT[:, par, :, 32:64], in0=scv[:, :, 32:64],
                                               in1=C.f[0:64, 160:192].unsqueeze(1).broadcast_to([64, 2, 32]),
                                               op=ALU.mult), [b_fb2[par], C.buf], [b_PT[par]])
            A(K.dve, lambda e: e.tensor_tensor(out=PT[0:32, par, :, 0:32], in0=scv[0:32, :, 0:32],
                                               in1=C.f[0:32, 128:160].unsqueeze(1).broadcast_to([32, 2, 32]),
                                               op=ALU.mult), [b_fb2[par], C.buf], [b_PT[par]])
            A(K.act, lambda e: e.activation(out=khs[:, par].rearrange("p h d -> p (h d)"), in_=bkT[par][0:64, 0:256],
                                            func=AF.Identity), [b_bkT[par]], [b_khs[par]])

        def main(c, s=s):
            csl = slice(c * CH, (c + 1) * CH)
            par = c % 2
            sp_, sn = c % 2, (c + 1) % 2
            for h in range(2):
                bf_, bbf = hb[h], b_hb[h]
                A(K.pe, lambda e, h=h: e.matmul(bkO[0:64, h * 128:(h + 1) * 128], lhsT=PT[:, par, h, :],
                                                rhs=vc[:, par, h * 128:(h + 1) * 128], start=True, stop=False),
                  [b_PT[par], b_vc[par]], [b_bkO], inc=False)
                A(K.pe, lambda e, h=h, bf_=bf_: e.matmul(bkO[0:64, h * 128:(h + 1) * 128], lhsT=bf_["qh"][:, csl],
                                                         rhs=Sbf[h][:, sp_, :], start=False, stop=True),
                  [bbf["qh"], b_Sbf[h][sp_]], [b_bkO], inc=(h == 1))
            for h in range(2):
                A(K.pe, lambda e, h=h: e.matmul(bkU[:, h * 128:(h + 1) * 128], lhsT=khs[:, par, h, :],
                                                rhs=vc[:, par, h * 128:(h + 1) * 128], start=True, stop=True),
                  [b_khs[par], b_vc[par]], [b_bkU], inc=(h == 1))
            for h in range(2):
                A(K.dve, lambda e, h=h: e.scalar_tensor_tensor(out=Sst[h][:], in0=Sst[h][:], scalar=eL[h][:, c:c + 1],
                                                               in1=bkU[:, h * 128:(h + 1) * 128], op0=ALU.mult,
                                                               op1=ALU.add), [b_S[h], b_eL[h], b_bkU], [b_S[h]])
                A(K.act, lambda e, h=h: e.activation(out=Sbf[h][:, sn, :], in_=Sst[h][:], func=AF.Identity),
                  [b_S[h]], [b_Sbf[h][sn]])

        def post_a(c, s=s):
            par = c % 2
            for h in range(2):
                A(K.act, lambda e, h=h: e.activation(out=junk[:, :], in_=bkO[0:64, h * 128:(h + 1) * 128], func=AF.Square,
                                                     accum_out=ssq[:, h, 0:1]), [b_bkO], [b_junk, b_ssq[h]])
                A(K.act, lambda e, h=h: e.activation(out=ssq[:, h, 1:2], in_=ssq[:, h, 0:1], func=AF.Sqrt,
                                                     scale=1.0 / 128, bias=EPS), [b_ssq[h]], [b_ssq[h]])
                A(K.dve, lambda e, h=h: e.reciprocal(out=ssq[:, h, 1:2], in_=ssq[:, h, 1:2]), [b_ssq[h]], [b_ssq[h]])
                A(K.dve, lambda e, h=h: e.scalar_tensor_tensor(out=og[:, h, :], in0=bkO[0:64, h * 128:(h + 1) * 128],
                                                               scalar=ssq[:, h, 1:2],
                                                               in1=gs[:, par, h * 128:(h + 1) * 128],
                                                               op0=ALU.mult, op1=ALU.mult),
                  [b_bkO, b_ssq[h], b_gs[par]], [b_og[h]])

        def post_b(c, s=s):
            csl = slice(c * CH, (c + 1) * CH)
            par = (c + 1) % 2
            for h in range(2):
                A(K.pe, lambda e, h=h: e.transpose(bkT[par][:, 256 + h * 64:256 + (h + 1) * 64], og[:, h, :],
                                                   C.ident_b(64)), [b_og[h], C.buf], [b_bkT[par]], inc=(h == 1))
            A(K.act, lambda e: e.activation(out=oTt[:, s, :, csl],
                                            in_=bkT[par][:, 256:384].rearrange("p (h t) -> p h t", h=2),
                                            func=AF.Identity), [b_bkT[par]], [b_oTt[s]])

        prep_proj(0)
        prep_proj(1)
        prep(0)
        for c in range(NCH):
            if c + 1 < NCH:
                prep(c + 1)
            main(c)
            if c >= 1:
                post_b(c - 1)
            post_a(c)
        post_b(NCH - 1)
        if o_store is not None:
            o_store(t, oTt[:, s], b_oTt[s], 2)
        else:
            K.dma(K.pool, lambda e, t=t, s=s: e.dma_start(out=ov[:, :, t * 512:(t + 1) * 512], in_=oTt[:, s]),
                  [b_oTt[s]], [], ds_s)


def build_hgrn(cfg, ab_idx):
    D, S = cfg["D"], cfg["S"]
    nc, stack, K = _new()
    cst = _din(nc, "consts", [128, 640])
    hT = _din(nc, "hT", [D, S], BF16)
    w = _din(nc, "w", [D, 1024])
    lbl = _din(nc, "lbl", [128, 2, 2])
    gain = _din(nc, "gain", [1, 256])
    oT = _dout(nc, "oT", [256, S], BF16)
    C = Consts(K, cst, K.dsems[0])
    emit_hgrn(K, C, cfg, ab_idx, hT, w, lbl, gain, oT)
    _finish(K, stack)
    return nc


def emit_fox(K, C, cfg, hT_d, w_d, qkg_d, fbias_d, oT_d, h_load=None, o_store=None):
    D, KD, S = cfg["D"], cfg["KD"], cfg["S"]
    NB = S // 128
    A = K.op
    W = K.sb("x_W", [128, KD, 1026], BF16)
    hT = K.sb("x_hT", [128, 2, KD, 512], BF16)
    kTa = K.sb("x_kTa", [128, 2, S], BF16)
    va = K.sb("x_va", [128, 2, NB, 132], BF16)
    Ga = K.sb("x_Ga", [128, 2, NB], F32)
    qkg = K.sb("x_qkg", [128, 2], F32)
    nfb = K.sb("x_nfb", [128, 2], F32)
    qn = K.sb("x_qn", [128, 2, 512], BF16)
    sq = K.sb("x_sq", [128, 2, 512], F32)
    rt = K.sb("x_rt", [128, 2, 512], F32)
    sg = K.sb("x_sg", [128, 4, 256], F32)
    spv = K.sb("x_spv", [128, 2, 4], F32)
    lcs = K.sb("x_lcs", [128, 2, 4], F32)
    tots = K.sb("x_tots", [128, 2, 4], F32)
    incl = K.sb("x_incl", [128, 2, 2, 4], F32)
    excl = K.sb("x_excl", [128, 2, 4], F32)
    Bq = K.sb("x_Bq", [128, 2, NB], F32)
    PT = K.sb("x_PT", [128, 3, 512], BF16)
    rden = K.sb("x_rden", [128, 4], F32)
    dqc = K.sb("x_dqc", [128, 2, 4], F32)
    dqb = K.sb("x_dqb", [1, 2, 512], F32)
    dqbc = K.sb("x_dqbc", [128, 2, 512], F32)
    stg = K.sb("x_stg", [128, 3, 512], F32)
    b_dqc, b_dqb, b_dqbc = Buf(), Buf(), Buf()
    b_stg = [Buf() for _ in range(3)]
    og = K.sb("x_og", [128, 2, 128], BF16)
    oTt = K.sb("x_oTt", [128, 2, 2, 512], BF16)
    bk = [K.ps("x_bk%d" % i, [128, 512], F32) for i in range(7)]
    bkT = K.ps("x_bkT", [128, 1024], BF16)
    b_bk = [Buf() for _ in range(7)]
    b_T = Buf()
    P0, P1, N_, V_, Fb, O3, P2 = range(7)
    OB = [N_, V_, Fb, O3]
    B = lambda n: [Buf() for _ in range(n)]
    b_W, b_kTa, b_va, b_Ga, b_par = Buf(), Buf(), Buf(), Buf(), Buf()
    b_hT, b_qn, b_sq, b_rt, b_oTt = B(2), B(2), B(2), B(2), B(2)
    b_sg, b_spv, b_lcs, b_tots, b_excl = Buf(), Buf(), Buf(), Buf(), Buf()
    b_incl, b_Bq = B(2), Buf()
    b_PT = B(3)
    b_rden, b_og = Buf(), B(2)
    wv = w_d.rearrange("(k p) c -> p k c", p=128)
    K.dma(K.pool, lambda e: e.dma_start(out=W[:, :, 0:512], in_=wv[:, :, 0:512]), [], [b_W])
    K.dma(K.pool, lambda e: e.dma_start(out=W[:, :, 512:1026], in_=wv[:, :, 512:1026]), [], [b_W])
    K.dma(K.sp, lambda e: e.dma_start(out=qkg[:], in_=qkg_d[:, :]), [], [b_par])
    K.dma(K.sp, lambda e: e.dma_start(out=nfb[:], in_=fbias_d[0:1, :].partition_broadcast(128)), [], [b_par])
    A(K.dve, lambda e: e.tensor_scalar(out=qkg[:, 0:1], in0=qkg[:, 0:1], scalar1=float(128 ** -0.5), scalar2=None,
                                       op0=ALU.mult), [b_par], [b_par])
    A(K.dve, lambda e: e.tensor_scalar(out=nfb[:, :], in0=nfb[:, :], scalar1=-1.0, scalar2=None, op0=ALU.mult),
      [b_par], [b_par])
    A(K.dve, lambda e: e.memset(va[:], 1.0), [], [b_va])
    hv = None if h_load is not None else hT_d.rearrange("(k p) t -> p k t", p=128)
    ov = None if o_store is not None else oT_d.rearrange("(h p) t -> p h t", p=128)
    xs = 0
    for t in range(S // 512):
        s = t % 2
        tsl = slice(t * 512, (t + 1) * 512)
        if h_load is not None:
            h_load(t, hT[:, s], b_hT[s])
        else:
            K.dma(K.sp, lambda e, s=s, tsl=tsl: e.dma_start(out=hT[:, s], in_=hv[:, :, tsl]), [], [b_hT[s]])
        for h in range(2):
            for (which, bank) in ((0, P0), (1, P1)):
                blk = 2 * h + which
                for k in range(KD):
                    A(K.pe, lambda e, blk=blk, k=k, s=s, bank=bank: e.matmul(
                        bk[bank][:, :], lhsT=W[:, k, blk * 128:(blk + 1) * 128], rhs=hT[:, s, k, :],
                        start=(k == 0), stop=(k == KD - 1)), [b_W, b_hT[s]], [b_bk[bank]], inc=(k == KD - 1))
                x = which
                A(K.act, lambda e, x=x, bank=bank: e.activation(out=sq[:, x, :], in_=bk[bank][:, :], func=AF.Square),
                  [b_bk[bank]], [b_sq[x]])
                A(K.pe, lambda e, x=x: e.matmul(bk[N_][:, :], lhsT=C.ones_f(), rhs=sq[:, x, :], start=True, stop=True),
                  [b_sq[x], C.buf], [b_bk[N_]])
                A(K.act, lambda e, x=x: e.activation(out=rt[:, x, :], in_=bk[N_][:, :], func=AF.Sqrt, scale=1.0 / 128,
                                                     bias=EPS), [b_bk[N_]], [b_rt[x]])
                A(K.dve, lambda e, x=x: e.reciprocal(out=rt[:, x, :], in_=rt[:, x, :]), [b_rt[x]], [b_rt[x]])
                if which == 0:
                    A(K.dve, lambda e, h=h, x=x, bank=bank: e.scalar_tensor_tensor(
                        out=qn[:, h, :], in0=bk[bank][:, :], scalar=qkg[:, 0:1], in1=rt[:, x, :], op0=ALU.mult,
                        op1=ALU.mult), [b_bk[bank], b_par, b_rt[x]], [b_qn[h]])
                else:
                    A(K.dve, lambda e, h=h, x=x, bank=bank, tsl=tsl: e.scalar_tensor_tensor(
                        out=kTa[:, h, tsl], in0=bk[bank][:, :], scalar=qkg[:, 1:2], in1=rt[:, x, :], op0=ALU.mult,
                        op1=ALU.mult), [b_bk[bank], b_par, b_rt[x]], [b_kTa])
        for b in range(4):
            blk = 4 * t + b
            bsl = slice(b * 128, (b + 1) * 128)
            for k in range(KD):
                A(K.pe, lambda e, k=k, s=s, bsl=bsl: e.matmul(bk[V_][:, :], lhsT=hT[:, s, k, bsl], rhs=W[:, k, 512:1024],
                                                              start=(k == 0), stop=(k == KD - 1)),
                  [b_W, b_hT[s]], [b_bk[V_]], inc=(k == KD - 1))
            A(K.act, lambda e, blk=blk: e.activation(out=va[:, :, blk, 0:128],
                                                     in_=bk[V_][:, 0:256].rearrange("p (h e) -> p h e", h=2),
                                                     func=AF.Identity), [b_bk[V_]], [b_va])
            A(K.act, lambda e, b=b: e.activation(out=sg[:, b, :], in_=bk[V_][:, 256:512], func=AF.Sigmoid),
              [b_bk[V_]], [b_sg])
            for k in range(KD):
                A(K.pe, lambda e, k=k, s=s, bsl=bsl, b=b: e.matmul(bk[Fb][:, 2 * b:2 * b + 2], lhsT=hT[:, s, k, bsl],
                                                                   rhs=W[:, k, 1024:1026], start=(k == 0),
                                                                   stop=(k == KD - 1)),
                  [b_W, b_hT[s]], [b_bk[Fb]], inc=(k == KD - 1))
        fbv = bk[Fb][:, 0:8].rearrange("p (b h) -> p h b", h=2)
        for h in range(2):
            A(K.act, lambda e, h=h: e.activation(out=spv[:, h, :], in_=fbv[:, h, :], func=AF.Exp, scale=-1.0,
                                                 bias=nfb[:, h:h + 1]), [b_bk[Fb], b_par], [b_spv])
        A(K.act, lambda e: e.activation(out=spv[:], in_=spv[:], func=AF.Ln, bias=1.0, scale=1.0), [b_spv], [b_spv])
        A(K.pe, lambda e: e.matmul(bk[Fb][:, 16:24], lhsT=C.tri_f(), rhs=spv[:].rearrange("p h b -> p (h b)"),
                                   start=True, stop=True), [b_spv, C.buf], [b_bk[Fb]])
        A(K.act, lambda e: e.activation(out=lcs[:].rearrange("p h b -> p (h b)"), in_=bk[Fb][:, 16:24],
                                        func=AF.Identity), [b_bk[Fb]], [b_lcs])
        A(K.pe, lambda e: e.matmul(bk[Fb][:, 32:40], lhsT=C.sel_f(), rhs=lcs[:].rearrange("p h b -> p (h b)"),
                                   start=True, stop=True), [b_lcs, C.buf], [b_bk[Fb]])
        A(K.act, lambda e: e.activation(out=tots[:].rearrange("p h b -> p (h b)"), in_=bk[Fb][:, 32:40],
                                        func=AF.Identity), [b_bk[Fb]], [b_tots])
        for h in range(2):
            init = 0.0 if t == 0 else incl[:, 1 - s, h, 3:4]
            A(K.dve, lambda e, h=h, s=s, init=init: e.tensor_tensor_scan(
                out=incl[:, s, h, :], data0=C.ones_f(128, 4), data1=tots[:, h, :], initial=init, op0=ALU.mult,
                op1=ALU.add), [b_tots, C.buf, b_incl[1 - s]], [b_incl[s]])
        A(K.dve, lambda e, s=s: e.tensor_tensor(out=excl[:], in0=incl[:, s], in1=tots[:], op=ALU.subtract),
          [b_incl[s], b_tots], [b_excl])
        A(K.dve, lambda e, t=t: e.tensor_tensor(out=Ga[:, :, 4 * t:4 * t + 4], in0=lcs[:], in1=excl[:], op=ALU.add),
          [b_lcs, b_excl], [b_Ga])
        nkb = 4 * t + 4
        for h in range(2):
            A(K.dve, lambda e, h=h, s=s, nkb=nkb: e.tensor_scalar(
                out=Bq[:, h, 0:nkb], in0=Ga[:, h, 0:nkb], scalar1=incl[:, s, h, 3:4], scalar2=None,
                op0=ALU.subtract), [b_Ga, b_incl[s]], [b_Bq])
            A(K.dve, lambda e, h=h, s=s, t=t: e.tensor_scalar(
                out=dqc[:, h, :], in0=Ga[:, h, 4 * t:4 * t + 4], scalar1=incl[:, s, h, 3:4], scalar2=-1.0,
                op0=ALU.subtract, op1=ALU.mult), [b_Ga, b_incl[s]], [b_dqc])
        for h in range(2):
            for jl in range(4):
                A(K.pe, lambda e, h=h, jl=jl: e.transpose(bk[O3][0:1, jl * 128:(jl + 1) * 128], dqc[:, h, jl:jl + 1],
                                                          C.ident_f()), [b_dqc, C.buf], [b_bk[O3]])
            A(K.act, lambda e, h=h: e.activation(out=dqb[0:1, h, :], in_=bk[O3][0:1, :], func=AF.Identity),
              [b_bk[O3]], [b_dqb])
            A(K.pe, lambda e, h=h: e.matmul(bk[O3][:, :], lhsT=C.ones_f(1, 128), rhs=dqb[0:1, h, :], start=True,
                                            stop=True), [b_dqb, C.buf], [b_bk[O3]])
            A(K.act, lambda e, h=h: e.activation(out=dqbc[:, h, :], in_=bk[O3][:, :], func=AF.Identity),
              [b_bk[O3]], [b_dqbc])
        SB = (P0, P1, P2)
        for h in range(2):
            def score(kb, h=h):
                m = max(0, kb - 4 * t)
                x3 = kb % 3
                sb_ = SB[x3]
                diag = kb >= 4 * t
                A(K.pe, lambda e: e.matmul(bk[sb_][:, m * 128:512], lhsT=kTa[:, h, kb * 128:(kb + 1) * 128],
                                           rhs=qn[:, h, m * 128:512], start=True, stop=(not diag)),
                  [b_kTa, b_qn[h]], [b_bk[sb_]], inc=(not diag))
                if diag:
                    A(K.pe, lambda e: e.matmul(bk[sb_][:, m * 128:(m + 1) * 128], lhsT=C.ident_b(), rhs=C.negmask_b(),
                                               start=False, stop=True), [C.buf], [b_bk[sb_]])
                A(K.dve, lambda e: e.tensor_tensor(out=stg[:, x3, m * 128:512], in0=bk[sb_][:, m * 128:512],
                                                   in1=dqbc[:, h, m * 128:512], op=ALU.add),
                  [b_bk[sb_], b_dqbc], [b_stg[x3]])
                A(K.act, lambda e: e.activation(out=PT[:, x3, m * 128:512], in_=stg[:, x3, m * 128:512], func=AF.Exp,
                                                bias=Bq[:, h, kb:kb + 1], scale=1.0), [b_stg[x3], b_Bq], [b_PT[x3]])

            def pv(kb, h=h):
                m = max(0, kb - 4 * t)
                x3 = kb % 3
                for jl in range(m, 4):
                    j = 4 * t + jl
                    A(K.pe, lambda e, jl=jl, j=j: e.matmul(bk[OB[jl]][:, 0:129], lhsT=PT[:, x3, jl * 128:(jl + 1) * 128],
                                                           rhs=va[:, h, kb, 0:129], start=(kb == 0), stop=(kb == j)),
                      [b_PT[x3], b_va], [b_bk[OB[jl]]], inc=(jl == 3 or kb == j))
            score(0)
            score(1)
            for kb in range(2, nkb):
                score(kb)
                pv(kb - 2)
            pv(nkb - 2)
            pv(nkb - 1)
            for jl in range(4):
                A(K.dve, lambda e, jl=jl: e.reciprocal(out=rden[:, jl:jl + 1], in_=bk[OB[jl]][:, 128:129]),
                  [b_bk[OB[jl]]], [b_rden])
                x = jl % 2
                A(K.dve, lambda e, jl=jl, h=h, x=x: e.scalar_tensor_tensor(
                    out=og[:, x, :], in0=bk[OB[jl]][:, 0:128], scalar=rden[:, jl:jl + 1],
                    in1=sg[:, jl, h * 128:(h + 1) * 128], op0=ALU.mult, op1=ALU.mult),
                    [b_bk[OB[jl]], b_rden, b_sg], [b_og[x]])
                A(K.pe, lambda e, x=x: e.transpose(bkT[:, 0:128], og[:, x, :], C.ident_b()), [b_og[x], C.buf], [b_T])
                A(K.act, lambda e, s=s, h=h, jl=jl: e.activation(out=oTt[:, s, h, jl * 128:(jl + 1) * 128],
                                                                 in_=bkT[:, 0:128], func=AF.Identity),
                  [b_T], [b_oTt[s]])
        if o_store is not None:
            o_store(t, oTt[:, s], b_oTt[s], 2)
        else:
            K.dma(K.pool, lambda e, s=s, tsl=tsl: e.dma_start(out=ov[:, :, tsl], in_=oTt[:, s]), [b_oTt[s]], [])


def build_fox(cfg):
    D, S = cfg["D"], cfg["S"]
    nc, stack, K = _new()
    cst = _din(nc, "consts", [128, 640])
    hT = _din(nc, "hT", [D, S], BF16)
    w = _din(nc, "w", [D, 1026])
    qkg = _din(nc, "qkg", [128, 2])
    fbias = _din(nc, "fbias", [1, 2])
    oT = _dout(nc, "oT", [256, S], BF16)
    C = Consts(K, cst, None)
    emit_fox(K, C, cfg, hT, w, qkg, fbias, oT)
    _finish(K, stack)
    return nc


def emit_gla(K, C, cfg, hT_d, w_d, wgu_d, bg_d, gain_d, oT_d, h_load=None, o_store=None):
    D, KD, S = cfg["D"], cfg["KD"], cfg["S"]
    A = K.op
    CH, NCH = 128, 4
    NW = 1552
    W = K.sb("g_W", [128, KD, NW], BF16)
    hT = K.sb("g_hT", [128, 2, KD, 512], BF16)
    wgu = K.sb("g_wgu", [16, 256], F32)
    bg = K.sb("g_bg", [128, 2], F32)
    gain = K.sb("g_gain", [128, 512], F32)
    ones = K.sb("g_ones", [128, 128], F32)
    glT = K.sb("g_glT", [16, 512], F32)
    spt = K.sb("g_sp", [128, 2, 512], F32)
    csp = K.sb("g_csp", [128, 2, 512], F32)
    Eq = K.sb("g_Eq", [128, 2, 512], F32)
    Ek = K.sb("g_Ek", [128, 2, 512], F32)
    eL = K.sb("g_eL", [128, 2, NCH], F32)
    qh = K.sb("g_qh", [128, 2, 512], BF16)
    kt = K.sb("g_kt", [128, 2, 512], BF16)
    kh = K.sb("g_kh", [128, 2, 512], BF16)
    Sst = K.sb("g_S", [128, 2, 512], F32)
    Sbf = K.sb("g_Sbf", [128, 2, 2, 512], BF16)
    vc = K.sb("g_vc", [128, 2, 512], BF16)
    gs = K.sb("g_gs", [128, 2, 512], F32)
    PT = K.sb("g_PT", [128, 2, 128], BF16)
    khs = K.sb("g_khs", [128, 2, 256], BF16)
    junk = K.sb("g_junk", [128, 512], F32)
    ssq = K.sb("g_ssq", [128, 2], F32)
    og = K.sb("g_og", [128, 512], BF16)
    oTt = K.sb("g_oTt", [128, 2, 4, 512], BF16)
    bk = [K.ps("g_bk%d" % i, [128, 512], F32) for i in range(6)]
    bkT = K.ps("g_bkT", [128, 1024], BF16)
    bkT2 = K.ps("g_bkT2", [128, 1024], BF16)
    b_bk = [Buf() for _ in range(6)]
    b_T, b_T2 = Buf(), Buf()
    GA, Z0, Z1, V_, Gt, O_ = range(6)
    B = lambda n: [Buf() for _ in range(n)]
    b_W, b_par, b_ones, b_glT = Buf(), Buf(), Buf(), Buf()
    b_hT, b_sp, b_csp, b_Eq, b_Ek, b_eL = B(2), B(2), B(2), B(2), B(2), B(2)
    b_qh, b_kt, b_kh, b_S = B(2), B(2), B(2), B(2)
    b_Sbf = [B(2), B(2)]
    b_vc, b_gs, b_PT, b_khs, b_oTt = B(2), B(2), B(2), B(2), B(2)
    b_junk, b_ssq, b_og = Buf(), Buf(), Buf()
    wv = w_d.rearrange("(k p) c -> p k c", p=128)
    for (c0, c1) in ((0, 528), (528, 1040), (1040, 1552)):
        K.dma(K.pool, lambda e, c0=c0, c1=c1: e.dma_start(out=W[:, :, c0:c1], in_=wv[:, :, c0:c1]), [], [b_W])
    K.dma(K.sp, lambda e: e.dma_start(out=wgu[:], in_=wgu_d[:, :]), [], [b_par])
    K.dma(K.sp, lambda e: e.dma_start(out=bg[:], in_=bg_d[:, :]), [], [b_par])
    K.dma(K.sp, lambda e: e.dma_start(out=gain[:], in_=gain_d[0:1, :].partition_broadcast(128)), [], [b_par])
    A(K.dve, lambda e: e.tensor_scalar(out=bg[:, :], in0=bg[:, :], scalar1=-1.0, scalar2=None, op0=ALU.mult),
      [b_par], [b_par])
    A(K.dve, lambda e: e.memset(ones[:], 1.0), [], [b_ones])
    for dc in range(2):
        A(K.dve, lambda e, dc=dc: e.memset(Sst[:, dc, :], 0.0), [], [b_S[dc]])
        A(K.dve, lambda e, dc=dc: e.memset(Sbf[:, dc, 0, :], 0.0), [], [b_Sbf[dc][0]])
    hv = None if h_load is not None else hT_d.rearrange("(k p) t -> p k t", p=128)
    ov = None if o_store is not None else oT_d.rearrange("(h p) t -> p h t", p=128)
    c3 = lambda a: a.rearrange("p (c t) -> p c t", c=NCH)
    cg = 0
    for t in range(S // 512):
        s = t % 2
        tsl = slice(t * 512, (t + 1) * 512)
        if h_load is not None:
            h_load(t, hT[:, s], b_hT[s])
        else:
            K.dma(K.sp, lambda e, s=s, tsl=tsl: e.dma_start(out=hT[:, s], in_=hv[:, :, tsl]), [], [b_hT[s]])
        def prep_proj(c, s=s):
            csl = slice(c * CH, (c + 1) * CH)
            ts = c % 2
            for (bank, c0) in ((V_, 528), (Gt, 1040)):
                for k in range(KD):
                    A(K.pe, lambda e, k=k, bank=bank, c0=c0: e.matmul(
                        bk[bank][:, :], lhsT=hT[:, s, k, csl], rhs=W[:, k, c0:c0 + 512], start=(k == 0),
                        stop=(k == KD - 1)), [b_W, b_hT[s]], [b_bk[bank]], inc=(k == KD - 1))
            A(K.act, lambda e: e.activation(out=vc[:, ts, :], in_=bk[V_][:, :], func=AF.Identity),
              [b_bk[V_]], [b_vc[ts]])
            A(K.act, lambda e: e.activation(out=gs[:, ts, :], in_=bk[Gt][:, :], func=AF.Silu),
              [b_bk[Gt]], [b_gs[ts]])
            A(K.dve, lambda e: e.tensor_tensor(out=gs[:, ts, :], in0=gs[:, ts, :], in1=gain[:, :], op=ALU.mult),
              [b_gs[ts], b_par], [b_gs[ts]])

        for k in range(KD):
            A(K.pe, lambda e, k=k, s=s: e.matmul(bk[GA][0:16, :], lhsT=W[:, k, 512:528], rhs=hT[:, s, k, :],
                                                 start=(k == 0), stop=(k == KD - 1)),
              [b_W, b_hT[s]], [b_bk[GA]], inc=(k == KD - 1))
        prep_proj(0)
        prep_proj(1)
        A(K.act, lambda e: e.activation(out=glT[:, :], in_=bk[GA][0:16, :], func=AF.Identity), [b_bk[GA]], [b_glT])
        for dc in range(2):
            zb = (Z0, Z1)[dc]
            A(K.pe, lambda e, dc=dc, zb=zb: e.matmul(bk[zb][:, :], lhsT=wgu[:, dc * 128:(dc + 1) * 128], rhs=glT[:, :],
                                                     start=True, stop=True), [b_par, b_glT], [b_bk[zb]])
            A(K.act, lambda e, dc=dc, zb=zb: e.activation(out=spt[:, dc, :], in_=bk[zb][:, :], func=AF.Exp, scale=-1.0,
                                                          bias=bg[:, dc:dc + 1]), [b_bk[zb], b_par], [b_sp[dc]])
            A(K.act, lambda e, dc=dc: e.activation(out=spt[:, dc, :], in_=spt[:, dc, :], func=AF.Ln, bias=1.0,
                                                   scale=1.0), [b_sp[dc]], [b_sp[dc]])
            for c in range(NCH):
                A(K.dve, lambda e, dc=dc, c=c: e.tensor_tensor_scan(
                    out=csp[:, dc, c * CH:(c + 1) * CH], data0=ones[:, :], data1=spt[:, dc, c * CH:(c + 1) * CH],
                    initial=0.0, op0=ALU.mult, op1=ALU.add), [b_sp[dc], b_ones], [b_csp[dc]])
            A(K.act, lambda e, dc=dc: e.activation(out=Eq[:, dc, :], in_=csp[:, dc, :], func=AF.Exp, scale=-1.0 / 16),
              [b_csp[dc]], [b_Eq[dc]])
            A(K.act, lambda e, dc=dc: e.activation(out=Ek[:, dc, :], in_=csp[:, dc, :], func=AF.Exp, scale=1.0 / 16),
              [b_csp[dc]], [b_Ek[dc]])
            A(K.act, lambda e, dc=dc: e.activation(out=eL[:, dc, :], in_=c3(csp[:, dc, :])[:, :, CH - 1], func=AF.Exp,
                                                   scale=-1.0 / 16), [b_csp[dc]], [b_eL[dc]])
            for (which, dst) in ((0, "q"), (1, "k")):
                blk = which * 2 + dc
                for k in range(KD):
                    A(K.pe, lambda e, blk=blk, k=k, s=s, zb=zb: e.matmul(
                        bk[zb][:, :], lhsT=W[:, k, blk * 128:(blk + 1) * 128], rhs=hT[:, s, k, :], start=(k == 0),
                        stop=(k == KD - 1)), [b_W, b_hT[s]], [b_bk[zb]], inc=(k == KD - 1))
                if which == 0:
                    A(K.dve, lambda e, dc=dc, zb=zb: e.scalar_tensor_tensor(
                        out=qh[:, dc, :], in0=bk[zb][:, :], scalar=float(256 ** -0.5), in1=Eq[:, dc, :], op0=ALU.mult,
                        op1=ALU.mult), [b_bk[zb], b_Eq[dc]], [b_qh[dc]])
                else:
                    A(K.dve, lambda e, dc=dc, zb=zb: e.tensor_tensor(out=kt[:, dc, :], in0=bk[zb][:, :],
                                                                     in1=Ek[:, dc, :], op=ALU.mult),
                      [b_bk[zb], b_Ek[dc]], [b_kt[dc]])
                    A(K.dve, lambda e, dc=dc: e.tensor_tensor(
                        out=c3(kh[:, dc, :]), in0=c3(kt[:, dc, :]),
                        in1=eL[:, dc, :].unsqueeze(2).broadcast_to([128, NCH, CH]), op=ALU.mult),
                        [b_kt[dc], b_eL[dc]], [b_kh[dc]])
        def prep(c, s=s):
            csl = slice(c * CH, (c + 1) * CH)
            ts = c % 2
            if c >= 2:
                prep_proj(c)
            for dc in range(2):
                A(K.pe, lambda e, dc=dc: e.matmul(bk[GA][:, 0:128], lhsT=kt[:, dc, csl], rhs=qh[:, dc, csl],
                                                  start=(dc == 0), stop=(dc == 1)),
                  [b_kt[dc], b_qh[dc]], [b_bk[GA]], inc=(dc == 1))
            A(K.dve, lambda e: e.tensor_tensor(out=PT[:, ts, :], in0=bk[GA][:, 0:128], in1=C.tri_f(), op=ALU.mult),
              [b_bk[GA], C.buf], [b_PT[ts]])
            for dc in range(2):
                A(K.pe, lambda e, dc=dc: e.transpose(bkT[:, dc * 128:(dc + 1) * 128], kh[:, dc, csl], C.ident_b()),
                  [b_kh[dc], C.buf], [b_T], inc=(dc == 1))
            A(K.act, lambda e: e.activation(out=khs[:, ts, :], in_=bkT[:, 0:256], func=AF.Identity),
              [b_T], [b_khs[ts]])

        def main(c, s=s):
            csl = slice(c * CH, (c + 1) * CH)
            ts = c % 2
            sp_, sn = c % 2, (c + 1) % 2
            A(K.pe, lambda e: e.matmul(bk[O_][:, :], lhsT=PT[:, ts, :], rhs=vc[:, ts, :], start=True, stop=False),
              [b_PT[ts], b_vc[ts]], [b_bk[O_]], inc=False)
            for dc in range(2):
                A(K.pe, lambda e, dc=dc: e.matmul(bk[O_][:, :], lhsT=qh[:, dc, csl], rhs=Sbf[:, dc, sp_, :], start=False,
                                                  stop=(dc == 1)),
                  [b_qh[dc], b_Sbf[dc][sp_]], [b_bk[O_]], inc=(dc == 1))
            for dc in range(2):
                ub = (Z0, Z1)[dc]
                A(K.pe, lambda e, dc=dc, ub=ub: e.matmul(bk[ub][:, :], lhsT=khs[:, ts, dc * 128:(dc + 1) * 128],
                                                         rhs=vc[:, ts, :], start=True, stop=True),
                  [b_khs[ts], b_vc[ts]], [b_bk[ub]])
            for dc in range(2):
                ub = (Z0, Z1)[dc]
                A(K.dve, lambda e, dc=dc, ub=ub: e.scalar_tensor_tensor(
                    out=Sst[:, dc, :], in0=Sst[:, dc, :], scalar=eL[:, dc, c:c + 1], in1=bk[ub][:, :], op0=ALU.mult,
                    op1=ALU.add), [b_S[dc], b_eL[dc], b_bk[ub]], [b_S[dc]])
                A(K.act, lambda e, dc=dc: e.activation(out=Sbf[:, dc, sn, :], in_=Sst[:, dc, :], func=AF.Identity),
                  [b_S[dc]], [b_Sbf[dc][sn]])

        def post_a(c, s=s):
            ts = c % 2
            A(K.act, lambda e: e.activation(out=junk[:, :], in_=bk[O_][:, :], func=AF.Square, accum_out=ssq[:, 0:1]),
              [b_bk[O_]], [b_junk, b_ssq])
            A(K.act, lambda e: e.activation(out=ssq[:, 1:2], in_=ssq[:, 0:1], func=AF.Sqrt, scale=1.0 / 512, bias=EPS),
              [b_ssq], [b_ssq])
            A(K.dve, lambda e: e.reciprocal(out=ssq[:, 1:2], in_=ssq[:, 1:2]), [b_ssq], [b_ssq])
            A(K.dve, lambda e: e.scalar_tensor_tensor(out=og[:, :], in0=bk[O_][:, :], scalar=ssq[:, 1:2],
                                                      in1=gs[:, ts, :], op0=ALU.mult, op1=ALU.mult),
              [b_bk[O_], b_ssq, b_gs[ts]], [b_og])

        def post_b(c, s=s):
            csl = slice(c * CH, (c + 1) * CH)
            for ec in range(4):
                A(K.pe, lambda e, ec=ec: e.transpose(bkT2[:, ec * 128:(ec + 1) * 128], og[:, ec * 128:(ec + 1) * 128],
                                                     C.ident_b()), [b_og, C.buf], [b_T2], inc=(ec == 3))
            A(K.act, lambda e: e.activation(out=oTt[:, s, :, csl], in_=bkT2[:, 0:512].rearrange("p (a q) -> p a q", a=4),
                                            func=AF.Identity), [b_T2], [b_oTt[s]])

        prep(0)
        for c in range(NCH):
            if c + 1 < NCH:
                prep(c + 1)
            main(c)
            if c >= 1:
                post_b(c - 1)
            post_a(c)
        post_b(NCH - 1)
        if o_store is not None:
            o_store(t, oTt[:, s], b_oTt[s], 4)
        else:
            K.dma(K.pool, lambda e, s=s, tsl=tsl: e.dma_start(out=ov[:, :, tsl], in_=oTt[:, s]), [b_oTt[s]], [])


def build_gla(cfg):
    D, S = cfg["D"], cfg["S"]
    nc, stack, K = _new()
    cst = _din(nc, "consts", [128, 640])
    hT = _din(nc, "hT", [D, S], BF16)
    w = _din(nc, "w", [D, 1552])
    wgu = _din(nc, "wgu", [16, 256])
    bg = _din(nc, "bg", [128, 2])
    gain = _din(nc, "gain", [1, 512])
    oT = _dout(nc, "oT", [512, S], BF16)
    C = Consts(K, cst, None)
    emit_gla(K, C, cfg, hT, w, wgu, bg, gain, oT)
    _finish(K, stack)
    return nc


CFG = dict(L=4, D=2048, KD=16, T=2048, S=8192, DFF=5632, KF=44, KO=16)
_PROGS = {}


def _prog(key, fn):
    if key not in _PROGS:
        _PROGS[key] = fn()
    return _PROGS[key]


def _run(nc, in_maps):
    res = run_bass_kernel_spmd(nc, in_maps, core_ids=list(range(8)))
    return res.results


def _fm(v, kd):
    v = np.asarray(v)
    lead = v.shape[:-1]
    a = v.reshape(lead + (kd, 128))
    return np.ascontiguousarray(np.moveaxis(a, -1, 0))


def kernel_unfused(x, c, mod_w, mod_b, norm_mix_gain, norm_ffn_gain, ab_w_in, ab_w_out, hgrn_lb_logits, hgrn_out_gain,
                   fox_q_gain, fox_k_gain, fox_f_bias, gla_w_in, gla_w_gate_up, gla_b_gate, gla_out_gain, gla_w_out,
                   ffn_w_in, ffn_w_out):
    cfg = CFG
    L, D, KD, T, S, DFF = cfg["L"], cfg["D"], cfg["KD"], cfg["T"], cfg["S"], cfg["DFF"]
    f32 = lambda a: np.ascontiguousarray(np.asarray(a, dtype=np.float32))
    x, c, mod_w, mod_b = f32(x), f32(c), f32(mod_w), f32(mod_b)
    consts = make_consts()
    cores = [(cid // 4, cid % 4) for cid in range(8)]

    cfg1 = dict(cfg)
    cfg1["L"] = 1
    nc = _prog("mod", lambda: build_mod(cfg1))
    ims = []
    for (b, j) in cores:
        ims.append({"consts": consts, "cT": _fm(c[b], KD), "modw": mod_w[j:j + 1], "modb": mod_b[j:j + 1],
                    "gmix": _fm(f32(norm_mix_gain)[j:j + 1], KD), "gffn": _fm(f32(norm_ffn_gain)[j:j + 1], KD)})
    r = _run(nc, ims)
    vec = [np.ascontiguousarray(np.concatenate([r[b * 4 + j]["vec"] for j in range(4)], axis=1)) for b in range(2)]

    nc = _prog("norm0", lambda: build_norm0(cfg))
    xT = [np.ascontiguousarray(x[b, j * T:(j + 1) * T, :].T) for (b, j) in cores]
    r = _run(nc, [{"consts": consts, "xT": xT[i], "vec": vec[cores[i][0]]} for i in range(8)])
    hT = [r[i]["hT"] for i in range(8)]

    for l in range(L):
        hfull = [np.ascontiguousarray(np.concatenate(hT[b * 4:b * 4 + 4], axis=1)) for b in range(2)]
        if l % 2 == 0:
            i = l // 2
            w = f32(ab_w_in[i])
            lbl_all = f32(hgrn_lb_logits)
            ims_h, ims_f = [], []
            for (b, j) in cores:
                hh = (2 * j, 2 * j + 1)
                cs = lambda base, h_: w[:, base + h_ * 128:base + (h_ + 1) * 128]
                wh = np.concatenate([cs(0, hh[0]), cs(1024, hh[0]), cs(0, hh[1]), cs(1024, hh[1]),
                                     cs(2048, hh[0]), cs(2048, hh[1]), cs(3072, hh[0]), cs(3072, hh[1])], axis=1)
                lbl = np.stack([np.stack([lbl_all[ly, h_ * 128:(h_ + 1) * 128] for h_ in hh], axis=1)
                                for ly in range(2)], axis=1)
                ims_h.append({"consts": consts, "hT": hfull[b], "w": np.ascontiguousarray(wh),
                              "lbl": f32(lbl), "gain": f32(hgrn_out_gain[i][2 * j:2 * j + 2]).reshape(1, 256)})
                wf = np.concatenate([cs(4096, hh[0]), cs(5120, hh[0]), cs(4096, hh[1]), cs(5120, hh[1]),
                                     cs(6144, hh[0]), cs(6144, hh[1]), cs(7168, hh[0]), cs(7168, hh[1]),
                                     w[:, 8192 + hh[0]:8192 + hh[0] + 1], w[:, 8192 + hh[1]:8192 + hh[1] + 1]], axis=1)
                ims_f.append({"consts": consts, "hT": hfull[b], "w": np.ascontiguousarray(wf),
                              "qkg": f32(np.stack([fox_q_gain[i], fox_k_gain[i]], axis=1)),
                              "fbias": f32(fox_f_bias[i][2 * j:2 * j + 2]).reshape(1, 2)})
            rh = _run(_prog(("hgrn", i), lambda: build_hgrn(cfg, i)), ims_h)
            rf = _run(_prog("fox", lambda: build_fox(cfg)), ims_f)
            ofull = [np.concatenate([np.concatenate([rh[b * 4 + j]["oT"], rf[b * 4 + j]["oT"]], axis=0)
                                     for j in range(4)], axis=0) for b in range(2)]
            wo_src = f32(ab_w_out[i])
            perm = []
            for j in range(4):
                for typ in range(2):
                    for hl in range(2):
                        base = typ * 1024 + (2 * j + hl) * 128
                        perm.extend(range(base, base + 128))
            wo = np.ascontiguousarray(wo_src[np.asarray(perm)])
        else:
            i = l // 2
            w = f32(gla_w_in[i])
            ims_g = []
            for (b, j) in cores:
                wg = np.concatenate([w[:, j * 256:(j + 1) * 256], w[:, 1024 + j * 256:1024 + (j + 1) * 256],
                                     w[:, 6144:6160], w[:, 2048 + j * 512:2048 + (j + 1) * 512],
                                     w[:, 4096 + j * 512:4096 + (j + 1) * 512]], axis=1)
                ims_g.append({"consts": consts, "hT": hfull[b], "w": np.ascontiguousarray(wg),
                              "wgu": f32(gla_w_gate_up[i][:, j * 256:(j + 1) * 256]),
                              "bg": f32(np.asarray(gla_b_gate[i][j * 256:(j + 1) * 256]).reshape(2, 128).T),
                              "gain": f32(gla_out_gain[i]).reshape(1, 512)})
            rg = _run(_prog("gla", lambda: build_gla(cfg)), ims_g)
            ofull = [np.concatenate([rg[b * 4 + j]["oT"] for j in range(4)], axis=0) for b in range(2)]
            wo = f32(gla_w_out[i])
        last = (l == L - 1)
        nc = _prog(("dense", last), lambda: build_dense(cfg, last))
        wfi, wfo = f32(ffn_w_in[l]), f32(ffn_w_out[l])
        ims = []
        for ci, (b, j) in enumerate(cores):
            v2 = np.zeros((128, 2, 6 * KD), np.float32)
            v2[:, 0] = vec[b][:, l]
            if not last:
                v2[:, 1] = vec[b][:, l + 1]
            ims.append({"consts": consts, "xT": xT[ci], "oT": np.ascontiguousarray(ofull[b][:, j * T:(j + 1) * T]),
                        "wo": wo, "wfi": wfi, "wfo": wfo, "vec": v2})
        r = _run(nc, ims)
        xT = [r[ci]["xTo"] for ci in range(8)]
        if not last:
            hT = [r[ci]["hTo"] for ci in range(8)]

    out = np.empty((2, S, D), np.float32)
    for ci, (b, j) in enumerate(cores):
        out[b, j * T:(j + 1) * T, :] = xT[ci].T
    return out


GROUPS = [[0, 1, 2, 3], [4, 5, 6, 7]]
PRECAST = False


def build_fused(cfg):
    L, D, KD, T, S, DFF, KO = cfg["L"], cfg["D"], cfg["KD"], cfg["T"], cfg["S"], cfg["DFF"], cfg["KO"]
    NT = T // 512
    NQ = D // 256
    nc = bass.Bass("TRN2", target_bir_lowering=False)
    gstack = contextlib.ExitStack()
    K = Kern(nc, gstack)
    cst = _din(nc, "consts", [128, 640])
    cT = _din(nc, "cT", [128, KD])
    modw = _din(nc, "modw", [1, D, 6 * D])
    modb = _din(nc, "modb", [1, 6 * D])
    gmix = _din(nc, "gmix", [128, 1, KD])
    gffn = _din(nc, "gffn", [128, 1, KD])
    xT_in = _din(nc, "xT", [D, T])
    wfi = _din(nc, "wfi", [L, D, 2 * DFF])
    wfo = _din(nc, "wfo", [L, DFF, D])
    wo = [_din(nc, "wo%d" % l, [KO * 128, D]) for l in range(L)]
    ab, gl = {}, {}
    for i in range((L + 1) // 2):
        ab[i] = dict(wh=_din(nc, "wh%d" % i, [D, 1024]), lbl=_din(nc, "lbl%d" % i, [128, 2, 2]),
                     again=_din(nc, "again%d" % i, [1, 256]), wf=_din(nc, "wf%d" % i, [D, 1026]),
                     qkg=_din(nc, "qkg%d" % i, [128, 2]), fbias=_din(nc, "fbias%d" % i, [1, 2]))
    for i in range(L // 2):
        gl[i] = dict(wg=_din(nc, "wg%d" % i, [D, 1552]), wgu=_din(nc, "wgu%d" % i, [16, 256]),
                     bg=_din(nc, "bg%d" % i, [128, 2]), ggain=_din(nc, "ggain%d" % i, [1, 512]))
    xTo = _dout(nc, "xTo", [D, T])
    vin = nc.dram_tensor("i_vin", [128, 6 * KD], F32)
    vall = nc.dram_tensor("i_vall", [4 * 128, 6 * KD], F32)
    xs = nc.dram_tensor("i_xs", [D, T], F32)
    HK = KD // 2
    hq = nc.dram_tensor("i_hq", [NT * 2, HK * 128, 512], BF16)
    hf = nc.dram_tensor("i_hf", [NT * 2, 4 * HK * 128, 512], BF16)
    OC = 2 if NT % 2 == 0 else 1
    NOC = (S // 512) // OC
    oqs = {nm: nc.dram_tensor("i_oq" + nm, [NOC, rows, OC * 512], BF16) for nm, rows in (("h", 256), ("f", 256), ("g", 512))}
    ofs = {nm: nc.dram_tensor("i_of" + nm, [NOC, 4 * rows, OC * 512], BF16)
           for nm, rows in (("h", 256), ("f", 256), ("g", 512))}
    wbo = nc.dram_tensor("i_wbo", [KO * 128, D], BF16)
    wbi = nc.dram_tensor("i_wbi", [D, 2 * DFF], BF16)
    wbf = nc.dram_tensor("i_wbf", [DFF, D], BF16)
    b_cv = [Buf() for _ in range(4)]
    conv = {"q": [], "per": 1, "n": 0}

    def conv_plan(l, n_calls):
        q = []
        if not PRECAST:
            conv["q"] = q
            return
        for (src, dst, rows, step) in ((wo[l], wbo.ap(), KO * 128, 512), (wfi[l], wbi.ap(), D, 128),
                                       (wfo[l], wbf.ap(), DFF, 512)):
            for r0 in range(0, rows, step):
                r1 = min(rows, r0 + step)
                q.append((src[r0:r1, :], dst[r0:r1, :]))
        conv["q"] = q
        conv["per"] = -(-len(q) // n_calls)

    def conv_step(n=None):
        n = conv["per"] if n is None else n
        for _ in range(min(n, len(conv["q"]))):
            src, dst = conv["q"].pop(0)
            key = b_cv[conv["n"] % 4]
            conv["n"] += 1
            K.dma(K.pool, lambda e, src=src, dst=dst: e.dma_start(out=dst, in_=src), [], [], key=key)

    b_vin, b_vall = Buf(), Buf()
    b_hq = [Buf() for _ in range(NT * 2)]
    b_hf = [Buf() for _ in range(NT * 2)]
    b_oq = {nm: [Buf() for _ in range(NOC)] for nm in "hfg"}
    b_of = {nm: [Buf() for _ in range(NOC)] for nm in "hfg"}
    vview = vall.ap().rearrange("(l p) c -> p l c", p=128)
    pending = []

    def flush_colls():
        while pending:
            pending.pop(0)()

    def h_store(t, tile, buf):
        for half in range(2):
            i = t * 2 + half
            dst = hq.ap()[i].rearrange("(k p) t -> p k t", p=128)
            K.dma(K.sp, lambda e, dst=dst, half=half: e.dma_start(out=dst, in_=tile[:, half * HK:(half + 1) * HK, :]),
                  [buf], [b_hq[i]], key=buf)
            pending.append(lambda i=i: K.coll("AllGather", hq.ap()[i], hf.ap()[i], [b_hq[i]], [b_hf[i]], GROUPS))

    def h_load(t, dst, buf):
        r, tl = t // NT, t % NT
        for half in range(2):
            i = tl * 2 + half
            src = hf.ap()[i].rearrange("(r k p) t -> p r k t", r=4, p=128)[:, r]
            K.dma(K.sp, lambda e, src=src, half=half: e.dma_start(out=dst[:, half * HK:(half + 1) * HK, :], in_=src),
                  [b_hf[i]], [buf], key=buf)

    def make_o_store(nm):
        def o_store(t, src, buf, nh):
            ci, cl = t // OC, t % OC
            dst = oqs[nm].ap()[ci].rearrange("(h p) t -> p h t", p=128)[:, :, cl * 512:(cl + 1) * 512]
            K.dma(K.pool, lambda e: e.dma_start(out=dst, in_=src), [buf], [b_oq[nm][ci]], key=buf)
            if cl == OC - 1:
                K.coll("AllGather", oqs[nm].ap()[ci], ofs[nm].ap()[ci], [b_oq[nm][ci]], [b_of[nm][ci]], GROUPS)
            conv_step()
        return o_store

    seg_cache = {}

    def make_o_load(names):
        def o_load(tt, oT, buf):
            k0 = 0
            for nm in names:
                rows = 4 * (256 if nm in "hf" else 512)
                nk = rows // 128

                def fn(e, nm=nm, nk=nk, k0=k0):
                    if "segC" not in seg_cache:
                        seg_cache["segC"] = (e.partition_id() % 4) * (NT // OC)
                    ci = seg_cache["segC"] + tt // OC
                    src = ofs[nm].ap().rearrange("c (k p) t -> p c k t", p=128)[
                        :, bass.ds(ci, 1), :, (tt % OC) * 512:(tt % OC + 1) * 512]
                    return e.dma_start(out=oT[:, k0:k0 + nk, :].rearrange("p (o k) t -> p o k t", o=1), in_=src)
                K.dma(K.sp, fn, [b for b in b_of[nm]], [buf], key=buf)
                k0 += nk
        return o_load

    C = Consts(K, cst, None)
    cfg1 = dict(cfg)
    cfg1["L"] = 1
    K.begin_phase()
    emit_mod(K, C, cfg1, cT, modw, modb, gmix, gffn, vin.ap().rearrange("p (l c) -> p l c", l=1), out_buf=b_vin)
    K.coll("AllGather", vin, vall, [b_vin], [b_vall], GROUPS)
    K.end_phase()
    K.begin_phase()
    emit_norm0(K, C, cfg, xT_in, vview, None, h_store=h_store, after_tile=flush_colls)
    flush_colls()
    K.end_phase()
    for l in range(L):
        i = l // 2
        if l % 2 == 0:
            conv_plan(l, 2 * (S // 512))
            K.begin_phase()
            emit_hgrn(K, C, cfg, i, None, ab[i]["wh"], ab[i]["lbl"], ab[i]["again"], None, h_load=h_load,
                      o_store=make_o_store("h"))
            K.end_phase()
            K.begin_phase()
            emit_fox(K, C, cfg, None, ab[i]["wf"], ab[i]["qkg"], ab[i]["fbias"], None, h_load=h_load,
                     o_store=make_o_store("f"))
            conv_step(len(conv["q"]))
            K.end_phase()
            names = "hf"
        else:
            conv_plan(l, S // 512)
            K.begin_phase()
            emit_gla(K, C, cfg, None, gl[i]["wg"], gl[i]["wgu"], gl[i]["bg"], gl[i]["ggain"], None, h_load=h_load,
                     o_store=make_o_store("g"))
            conv_step(len(conv["q"]))
            K.end_phase()
            names = "g"
        last = (l == L - 1)
        K.begin_phase()
        dw = (wbo.ap(), wbi.ap(), wbf.ap()) if PRECAST else (wo[l], wfi[l], wfo[l])
        emit_dense(K, C, cfg, l, xT_in if l == 0 else xs.ap(), None, dw[0], dw[1], dw[2], vview,
                   xTo if last else xs.ap(), None, o_load=make_o_load(names), h_store=None if last else h_store,
                   after_proj=flush_colls)
        flush_colls()
        K.end_phase()
    gstack.close()
    return nc


def dense_k_perm(kind):
    perm = []
    if kind == "ab":
        for typ in range(2):
            for r in range(4):
                for loc in range(256):
                    perm.append(r * 512 + typ * 256 + loc)
    else:
        perm = list(range(2048))
    return np.asarray(perm)


def kernel_fused(cfg, x, c, mod_w, mod_b, norm_mix_gain, norm_ffn_gain, ab_w_in, ab_w_out, hgrn_lb_logits,
                 hgrn_out_gain, fox_q_gain, fox_k_gain, fox_f_bias, gla_w_in, gla_w_gate_up, gla_b_gate, gla_out_gain,
                 gla_w_out, ffn_w_in, ffn_w_out):
    L, D, KD, T, S, DFF = cfg["L"], cfg["D"], cfg["KD"], cfg["T"], cfg["S"], cfg["DFF"]
    f32 = lambda a: np.ascontiguousarray(np.asarray(a, dtype=np.float32))
    x, c, mod_w, mod_b = f32(x), f32(c), f32(mod_w), f32(mod_b)
    consts = make_consts()
    ab_rows = []
    for j in range(4):
        for typ in range(2):
            for hl in range(2):
                base = typ * 1024 + (2 * j + hl) * 128
                ab_rows.extend(range(base, base + 128))
    ab_rows = np.asarray(ab_rows)
    wfi, wfo = f32(ffn_w_in), f32(ffn_w_out)
    shared = {"consts": consts, "wfi": wfi, "wfo": wfo}
    for l in range(L):
        i = l // 2
        if l % 2 == 0:
            shared["wo%d" % l] = np.ascontiguousarray(f32(ab_w_out[i])[ab_rows[dense_k_perm("ab")]])
        else:
            shared["wo%d" % l] = np.ascontiguousarray(f32(gla_w_out[i])[dense_k_perm("gla")])
    lbl_all = f32(hgrn_lb_logits)
    ims = []
    for cid in range(8):
        b, j = cid // 4, cid % 4
        im = dict(shared)
        im["cT"] = _fm(c[b], KD)
        im["modw"] = mod_w[j:j + 1]
        im["modb"] = mod_b[j:j + 1]
        im["gmix"] = _fm(f32(norm_mix_gain)[j:j + 1], KD)
        im["gffn"] = _fm(f32(norm_ffn_gain)[j:j + 1], KD)
        im["xT"] = np.ascontiguousarray(x[b, j * T:(j + 1) * T, :].T)
        hh = (2 * j, 2 * j + 1)
        for i in range((L + 1) // 2):
            w = f32(ab_w_in[i])
            cs = lambda base, h_: w[:, base + h_ * 128:base + (h_ + 1) * 128]
            im["wh%d" % i] = np.ascontiguousarray(np.concatenate(
                [cs(0, hh[0]), cs(1024, hh[0]), cs(0, hh[1]), cs(1024, hh[1]),
                 cs(2048, hh[0]), cs(2048, hh[1]), cs(3072, hh[0]), cs(3072, hh[1])], axis=1))
            im["lbl%d" % i] = f32(np.stack([np.stack([lbl_all[ly, h_ * 128:(h_ + 1) * 128] for h_ in hh], axis=1)
                                            for ly in range(2)], axis=1))
            im["again%d" % i] = f32(hgrn_out_gain[i][2 * j:2 * j + 2]).reshape(1, 256)
            im["wf%d" % i] = np.ascontiguousarray(np.concatenate(
                [cs(4096, hh[0]), cs(5120, hh[0]), cs(4096, hh[1]), cs(5120, hh[1]),
                 cs(6144, hh[0]), cs(6144, hh[1]), cs(7168, hh[0]), cs(7168, hh[1]),
                 w[:, 8192 + hh[0]:8192 + hh[0] + 1], w[:, 8192 + hh[1]:8192 + hh[1] + 1]], axis=1))
            im["qkg%d" % i] = f32(np.stack([fox_q_gain[i], fox_k_gain[i]], axis=1))
            im["fbias%d" % i] = f32(fox_f_bias[i][2 * j:2 * j + 2]).reshape(1, 2)
        for i in range(L // 2):
            w = f32(gla_w_in[i])
            im["wg%d" % i] = np.ascontiguousarray(np.concatenate(
                [w[:, j * 256:(j + 1) * 256], w[:, 1024 + j * 256:1024 + (j + 1) * 256], w[:, 6144:6160],
                 w[:, 2048 + j * 512:2048 + (j + 1) * 512], w[:, 4096 + j * 512:4096 + (j + 1) * 512]], axis=1))
            im["wgu%d" % i] = f32(gla_w_gate_up[i][:, j * 256:(j + 1) * 256])
            im["bg%d" % i] = f32(np.asarray(gla_b_gate[i][j * 256:(j + 1) * 256]).reshape(2, 128).T)
            im["ggain%d" % i] = f32(gla_out_gain[i]).reshape(1, 512)
        ims.append(im)
    nc = _prog("fused", lambda: build_fused(cfg))
    r = _run(nc, ims)
    out = np.empty((2, S, D), np.float32)
    for cid in range(8):
        b, j = cid // 4, cid % 4
        out[b, j * T:(j + 1) * T, :] = r[cid]["xTo"].T
    return out


def kernel(**inputs):
    return kernel_fused(CFG, **inputs)
```

```python
import contextlib
import numpy as np
import ml_dtypes
import concourse.bass as bass
import concourse.mybir as mybir
from concourse.bass_utils import run_bass_kernel_spmd

F32 = mybir.dt.float32
BF16 = mybir.dt.bfloat16
AF = mybir.ActivationFunctionType
ALU = mybir.AluOpType
EPS = 1e-6


class Sem:
    def __init__(self, h):
        self.h = h
        self.n = 0


class Buf:
    __slots__ = ("w", "r", "ds")

    def __init__(self):
        self.w = None
        self.r = {}
        self.ds = None


class Eng:
    def __init__(self, name, sem, is_pe=False):
        self.name = name
        self.sem = sem
        self.prog = []
        self.seen = {}
        self.is_pe = is_pe


class Kern:
    def __init__(self, nc, stack, n_dma_sems=12):
        self.nc = nc
        self.stack = stack
        mk = lambda n: Sem(stack.enter_context(nc.semaphore(n)))
        self.pe = Eng("tensor", mk("s_pe"), True)
        self.act = Eng("scalar", mk("s_act"))
        self.dve = Eng("vector", mk("s_dve"))
        self.pool = Eng("gpsimd", mk("s_pool"))
        self.sp = Eng("sync", mk("s_sp"))
        self.engs = [self.pe, self.act, self.dve, self.pool, self.sp]
        self.dsems = [None] * n_dma_sems
        self.nbuf = 0
        self.ccs = [mk("s_cc%d" % i) for i in range(4)]
        self.dsems.extend(self.ccs)
        self.ncoll = 0
        self.free_sems = []
        self.phase_sems = []
        self.pstack = None
        self.nphase = 0

    def sb(self, name, shape, dt):
        st = self.pstack if self.pstack is not None else self.stack
        return st.enter_context(self.nc.sbuf_tensor("%s_%d" % (name, self.nphase), list(shape), dt))

    def ps(self, name, shape, dt):
        st = self.pstack if self.pstack is not None else self.stack
        return st.enter_context(self.nc.psum_tensor("%s_%d" % (name, self.nphase), list(shape), dt))

    def begin_phase(self):
        self.pstack = contextlib.ExitStack()
        self.nphase += 1

    def end_phase(self, last=False):
        for e in self.engs:
            self.final_wait(e, skip_cc=not last)
        self.emit()
        for e in self.engs:
            e.prog = []
        self.free_sems.extend(self.phase_sems)
        self.phase_sems = []
        self.pstack.close()
        self.pstack = None

    def coll(self, kind, in_t, out_t, R, W, groups):
        eng = self.pool
        cc = self.ccs[self.ncoll % len(self.ccs)]
        self.ncoll += 1
        waits = self._waits(eng, R, W)
        if cc.n > 0 and eng.seen.get(cc, 0) < cc.n:
            eng.seen[cc] = cc.n
            waits.append((cc, cc.n))
        cc.n += 1
        ev = (cc, cc.n)
        in_ap = in_t if isinstance(in_t, bass.AP) else in_t.ap()
        out_ap = out_t if isinstance(out_t, bass.AP) else out_t.ap()
        fn = lambda e: e.collective_compute(kind, ALU.bypass, replica_groups=groups, ins=[in_ap.opt()],
                                            outs=[out_ap.opt()])
        eng.prog.append((waits, fn, (cc, 1)))
        self._commit(ev, R, W)

    def _waits(self, eng, R, W):
        needs = {}

        def need(ev):
            s, v = ev
            if needs.get(s, 0) < v:
                needs[s] = v

        for b in R:
            if b.w is not None:
                if b.w[0] is eng.sem and eng.is_pe:
                    continue
                need(b.w)
        for b in W:
            if b.w is not None and b.w[0] is not eng.sem:
                need(b.w)
            for s, v in b.r.items():
                if s is not eng.sem:
                    need((s, v))
        out = []
        for s, v in needs.items():
            if eng.seen.get(s, 0) < v:
                eng.seen[s] = v
                out.append((s, v))
        return out

    def _commit(self, ev, R, W):
        for b in R:
            if b.r.get(ev[0], 0) < ev[1]:
                b.r[ev[0]] = ev[1]
        for b in W:
            b.w = ev
            b.r = {}

    def op(self, eng, fn, R=(), W=(), inc=True):
        waits = self._waits(eng, R, W)
        if inc:
            eng.sem.n += 1
            ev = (eng.sem, eng.sem.n)
        else:
            ev = (eng.sem, eng.sem.n + 1)
        eng.prog.append((waits, fn, (eng.sem, 1) if inc else None))
        self._commit(ev, R, W)

    def dma(self, eng, fn, R, W, dsem=None, key=None):
        if key is None:
            key = W[0] if W else R[0]
        if key.ds is None:
            if self.free_sems:
                key.ds = self.free_sems.pop()
            else:
                key.ds = Sem(self.stack.enter_context(self.nc.semaphore("s_d%d" % self.nbuf)))
                self.nbuf += 1
                self.dsems.append(key.ds)
            if self.pstack is not None:
                self.phase_sems.append(key.ds)
        dsem = key.ds
        waits = self._waits(eng, R, W)
        if dsem.n > 0 and eng.seen.get(dsem, 0) < dsem.n:
            eng.seen[dsem] = dsem.n
            waits.append((dsem, dsem.n))
        dsem.n += 16
        ev = (dsem, dsem.n)
        eng.prog.append((waits, fn, (dsem, 16)))
        self._commit(ev, R, W)

    def final_wait(self, eng, skip_cc=False):
        waits = []
        for s in [e.sem for e in self.engs] + [d for d in self.dsems if d is not None]:
            if skip_cc and s in self.ccs:
                continue
            if s is not eng.sem and s.n > 0 and eng.seen.get(s, 0) < s.n:
                eng.seen[s] = s.n
                waits.append((s, s.n))
        eng.prog.append((waits, None, None))

    def emit(self):
        nc = self.nc

        def replay(eng):
            def run(e):
                for waits, fn, inc in eng.prog:
                    for s, v in waits:
                        e.wait_ge(s.h, v)
                    if fn is not None:
                        ins = fn(e)
                        if inc is not None:
                            ins.then_inc(inc[0].h, inc[1])
            return run

        with nc.Block() as block:
            block.tensor(replay(self.pe))
            block.scalar(replay(self.act))
            block.vector(replay(self.dve))
            block.gpsimd(replay(self.pool))
            block.sync(replay(self.sp))


def _chunks(n, m):
    return [(i, min(m, n - i)) for i in range(0, n, m)]


class Consts:
    def __init__(self, K, cdram, dsem):
        nc = K.nc
        self.f = K.sb("c_f32", [128, 5 * 128], F32)
        self.b = K.sb("c_bf16", [128, 5 * 128], BF16)
        self.buf = Buf()
        f, b = self.f, self.b
        K.dma(K.sp, lambda e: e.dma_start(out=f[:], in_=cdram[:, :]), [], [self.buf], dsem)
        K.op(K.dve, lambda e: e.tensor_copy(out=b[:], in_=f[:]), [self.buf], [self.buf])

    def ident_b(self, n=128):
        return self.b[0:n, 0:n]

    def tri_b(self, n=128):
        return self.b[0:n, 128:128 + n]

    def tri_f(self, n=128):
        return self.f[0:n, 128:128 + n]

    def ones_f(self, p=128, n=128):
        return self.f[0:p, 256:256 + n]

    def ones_b(self, p=128, n=128):
        return self.b[0:p, 256:256 + n]

    def sel_f(self):
        return self.f[:, 384:512]

    def ident_f(self):
        return self.f[:, 0:128]

    def negmask_b(self):
        return self.b[:, 512:640]


def make_consts():
    c = np.zeros((128, 5 * 128), np.float32)
    c[:, 0:128] = np.eye(128)
    c[:, 128:256] = np.triu(np.ones((128, 128)))
    c[:, 256:384] = 1.0
    c[127, 384:512] = 1.0
    c[:, 512:640] = -30000.0 * np.tril(np.ones((128, 128)), -1)
    return c


def emit_norm(K, C, xT, xbuf, KD, D, G, Sh, vbuf, outT, obuf, ssbank, ssb, tmp):
    sq, sqb, rs, rsb, tm, tmb = tmp["sq"], tmp["sqb"], tmp["rs"], tmp["rsb"], tmp["tm"], tmp["tmb"]
    for j in range(KD):
        s = j % 2
        K.op(K.act, lambda e, j=j, s=s: e.activation(out=sq[:, s, :], in_=xT[:, j, :], func=AF.Square),
             [xbuf], [sqb[s]])
        K.op(K.pe, lambda e, j=j, s=s: e.matmul(ssbank[:, :], lhsT=C.ones_f(), rhs=sq[:, s, :],
                                               start=(j == 0), stop=(j == KD - 1)),
             [sqb[s], C.buf], [ssb], inc=True)
    K.op(K.act, lambda e: e.activation(out=rs[:, :], in_=ssbank[:, :], func=AF.Sqrt, scale=1.0 / D, bias=EPS),
         [ssb], [rsb])
    K.op(K.dve, lambda e: e.reciprocal(out=rs[:, :], in_=rs[:, :]), [rsb], [rsb])
    for j in range(KD):
        s = j % 2
        K.op(K.dve, lambda e, j=j, s=s: e.scalar_tensor_tensor(out=tm[:, s, :], in0=xT[:, j, :], scalar=G[:, j:j + 1],
                                                             in1=rs[:, :], op0=ALU.mult, op1=ALU.mult),
             [xbuf, rsb, vbuf], [tmb[s]])
        K.op(K.act, lambda e, j=j, s=s: e.activation(out=outT[:, j, :], in_=tm[:, s, :], func=AF.Identity,
                                                   bias=Sh[:, j:j + 1], scale=1.0),
             [tmb[s], vbuf], [obuf])


def emit_mod(K, C, cfg, cT_d, modw_d, modb_d, gmix_d, gffn_d, vec_d, out_buf=None):
    L, D, KD = cfg["L"], cfg["D"], cfg["KD"]
    W6 = 6 * D
    CG = min(2048, W6)
    NB = CG // 512
    assert W6 % CG == 0
    ds_a, ds_w = K.dsems[0], K.dsems[1]
    cT = K.sb("m_cT", [128, KD], F32)
    gm = K.sb("m_gm", [128, L, KD], F32)
    gf = K.sb("m_gf", [128, L, KD], F32)
    row = K.sb("m_row", [1, W6], F32)
    brow = K.sb("m_brow", [1, W6], F32)
    wp = K.sb("m_wp", [128, 3, CG], F32)
    mt = K.sb("m_mt", [128, 6 * KD], F32)
    vec = K.sb("m_vec", [128, L, 6 * KD], F32)
    banks = [K.ps("m_ps%d" % i, [128, 512], F32) for i in range(NB)]
    tp = K.ps("m_tp", [128, 512], F32)
    b_c, b_g, b_row, b_brow, b_mt, b_vec, b_tp = Buf(), Buf(), Buf(), Buf(), Buf(), Buf(), Buf()
    b_wp = [Buf() for _ in range(3)]
    b_bk = [Buf() for _ in range(NB)]
    K.dma(K.sp, lambda e: e.dma_start(out=cT[:], in_=cT_d[:, :]), [], [b_c], ds_a)
    K.dma(K.sp, lambda e: e.dma_start(out=gm[:], in_=gmix_d[:, :, :]), [], [b_g], ds_a)
    K.dma(K.sp, lambda e: e.dma_start(out=gf[:], in_=gffn_d[:, :, :]), [], [b_g], ds_a)
    K.op(K.act, lambda e: e.activation(out=cT[:], in_=cT[:], func=AF.Silu), [b_c], [b_c])
    ip = 0
    for l in range(L):
        K.dma(K.sp, lambda e, l=l: e.dma_start(out=brow[:], in_=modb_d[l:l + 1, :]), [], [b_brow], ds_a)
        for cg in range(W6 // CG):
            for k in range(KD):
                s = ip % 3
                ip += 1
                K.dma(K.sp, lambda e, l=l, cg=cg, k=k, s=s: e.dma_start(
                    out=wp[:, s, :], in_=modw_d[l, k * 128:(k + 1) * 128, cg * CG:(cg + 1) * CG]),
                    [], [b_wp[s]], ds_w)
                for q in range(NB):
                    K.op(K.pe, lambda e, k=k, s=s, q=q: e.matmul(
                        banks[q][0:1, :], lhsT=cT[:, k:k + 1], rhs=wp[:, s, q * 512:(q + 1) * 512],
                        start=(k == 0), stop=(k == KD - 1)), [b_c, b_wp[s]], [b_bk[q]])
            for q in range(NB):
                c0 = cg * CG + q * 512
                K.op(K.dve, lambda e, q=q, c0=c0: e.tensor_tensor(
                    out=row[0:1, c0:c0 + 512], in0=banks[q][0:1, :], in1=brow[0:1, c0:c0 + 512], op=ALU.add),
                    [b_bk[q], b_brow], [b_row])
        for c in range(6 * KD):
            K.op(K.pe, lambda e, c=c: e.matmul(tp[:, c:c + 1], lhsT=row[0:1, c * 128:(c + 1) * 128],
                                               rhs=C.ones_f(1, 1), start=True, stop=True),
                 [b_row, C.buf], [b_tp])
        K.op(K.act, lambda e: e.activation(out=mt[:, :], in_=tp[:, 0:6 * KD], func=AF.Identity), [b_tp], [b_mt])
        for (dst, src_sc, gn) in ((0, 1, gm), (3, 4, gf)):
            K.op(K.dve, lambda e, l=l, dst=dst, src_sc=src_sc, gn=gn: e.scalar_tensor_tensor(
                out=vec[:, l, dst * KD:(dst + 1) * KD], in0=mt[:, src_sc * KD:(src_sc + 1) * KD], scalar=1.0,
                in1=gn[:, l, :], op0=ALU.add, op1=ALU.mult), [b_mt, b_g], [b_vec])
        for (dst, src) in ((1, 0), (2, 2), (4, 3), (5, 5)):
            K.op(K.dve, lambda e, l=l, dst=dst, src=src: e.tensor_copy(
                out=vec[:, l, dst * KD:(dst + 1) * KD], in_=mt[:, src * KD:(src + 1) * KD]), [b_mt], [b_vec])
    K.dma(K.sp, lambda e: e.dma_start(out=vec_d[:, :, :], in_=vec[:]), [b_vec], [out_buf] if out_buf else [], key=b_vec)


def norm_scratch(K, pfx):
    return {
        "sq": K.sb(pfx + "sq", [128, 2, 512], F32), "sqb": [Buf(), Buf()],
        "rs": K.sb(pfx + "rs", [128, 512], F32), "rsb": Buf(),
        "tm": K.sb(pfx + "tm", [128, 2, 512], F32), "tmb": [Buf(), Buf()],
    }


def emit_norm0(K, C, cfg, xT_d, vec_d, hT_d, h_store=None, after_tile=None):
    D, KD, T = cfg["D"], cfg["KD"], cfg["T"]
    ds_a = K.dsems[0]
    vec = K.sb("n_vec", [128, 6 * KD], F32)
    xT = K.sb("n_xT", [128, 2, KD, 512], F32)
    hT = K.sb("n_hT", [128, 2, KD, 512], BF16)
    ss = K.ps("n_ss", [128, 512], F32)
    b_v, b_ss = Buf(), Buf()
    b_x, b_h = [Buf(), Buf()], [Buf(), Buf()]
    tmp = norm_scratch(K, "n_")
    K.dma(K.sp, lambda e: e.dma_start(out=vec[:], in_=vec_d[:, 0, :]), [], [b_v], ds_a)
    xv = xT_d.rearrange("(k p) t -> p k t", p=128)
    hv = None if h_store is not None else hT_d.rearrange("(k p) t -> p k t", p=128)
    def xload(t):
        K.dma(K.sp, lambda e, t=t: e.dma_start(out=xT[:, t % 2], in_=xv[:, :, t * 512:(t + 1) * 512]),
              [], [b_x[t % 2]], ds_a)

    xload(0)
    for t in range(T // 512):
        s = t % 2
        if t + 1 < T // 512:
            xload(t + 1)
        emit_norm(K, C, xT[:, s], b_x[s], KD, D, vec[:, 0:KD], vec[:, KD:2 * KD], b_v, hT[:, s], b_h[s], ss, b_ss, tmp)
        if h_store is not None:
            h_store(t, hT[:, s], b_h[s])
            if after_tile is not None:
                after_tile()
        else:
            K.dma(K.sp, lambda e, t=t, s=s: e.dma_start(out=hv[:, :, t * 512:(t + 1) * 512], in_=hT[:, s]),
                  [b_h[s]], [], ds_a)


def emit_dense(K, C, cfg, l, xT_d, oT_d, wo_d, wfi_d, wfo_d, vec_d, xTo_d, hTo_d, o_load=None, h_store=None,
               after_proj=None):
    D, KD, T, DFF, KF, KO, L = cfg["D"], cfg["KD"], cfg["T"], cfg["DFF"], cfg["KF"], cfg["KO"], cfg["L"]
    DG = D // 512 if D >= 512 else 1
    GW = min(512, D)
    GC = GW // 128
    FG = DFF // 512
    assert DFF % 512 == 0
    ds_a, ds_w, ds_s = K.dsems[0], K.dsems[1], K.dsems[2]
    last = hTo_d is None and h_store is None
    vec = K.sb("d_vec", [128, 2, 6 * KD], F32)
    xin = K.sb("d_xin", [128, KD, 512], F32)
    xT = K.sb("d_xT", [128, KD, 512], F32)
    oT2 = K.sb("d_oT", [128, 1, KO, 512], BF16)
    oT = K.sb("d_hT", [128, KD, 512], BF16)
    halves = [list(range(0, (FG + 1) // 2)), list(range((FG + 1) // 2, FG))]
    AH = 4 * len(halves[0])
    aT = K.sb("d_aT", [128, AH, 512], BF16)
    sa = K.sb("d_sa", [128, 2, 512], BF16)
    WB = 4
    WSZ = max(KO * GW, KD * 512, 16 * GW)
    wb = K.sb("d_wb", [128, WB, WSZ], BF16)
    banks = [K.ps("d_ps%d" % i, [128, 512], F32) for i in range(8)]
    b_bk = [Buf() for _ in range(8)]
    b_v, b_x, b_o, b_a, b_xin = Buf(), Buf(), Buf(), Buf(), Buf()
    b_o2 = [Buf(), Buf()]
    b_sa = [Buf(), Buf()]
    b_wb = [Buf() for _ in range(WB)]
    tmp = norm_scratch(K, "d_")
    K.dma(K.sp, lambda e: e.dma_start(out=vec[:, 0, :], in_=vec_d[:, l, :]), [], [b_v], ds_a)
    if not last:
        K.dma(K.sp, lambda e: e.dma_start(out=vec[:, 1, :], in_=vec_d[:, l + 1, :]), [], [b_v], ds_a)
    g1 = vec[:, 0, 2 * KD:3 * KD]
    G2 = vec[:, 0, 3 * KD:4 * KD]
    Sh2 = vec[:, 0, 4 * KD:5 * KD]
    g2 = vec[:, 0, 5 * KD:6 * KD]
    G1n = vec[:, 1, 0:KD]
    Sh1n = vec[:, 1, KD:2 * KD]
    xv = xT_d.rearrange("(k p) t -> p k t", p=128)
    ov = None if o_load is not None else oT_d.rearrange("(k p) t -> p k t", p=128)
    xov = xTo_d.rearrange("(k p) t -> p k t", p=128)
    hov = None if (last or h_store is not None) else hTo_d.rearrange("(k p) t -> p k t", p=128)
    wov = wo_d.rearrange("(k p) c -> p k c", p=128)
    wiv = wfi_d.rearrange("(k p) c -> p k c", p=128)
    wfv = wfo_d.rearrange("(k p) c -> p k c", p=128)
    st = {"w": 0, "bank": 0}

    def load_w(src_ap, kk, cc):
        s = st["w"] % WB
        st["w"] += 1
        dst = wb[:, s, 0:kk * cc].rearrange("p (k c) -> p k c", k=kk)
        K.dma(K.pool, lambda e: e.dma_start(out=dst, in_=src_ap), [], [b_wb[s]], ds_w)
        return dst, b_wb[s]

    def proj_group(pieces, rhsT, rbuf, nk, og, gvec, src=None, sbuf=None, kofs=0):
        if src is None:
            src, sbuf = xT, b_x
        base = (st["bank"] % 2) * 4
        st["bank"] += 1
        kdone = 0
        for (wt, wbuf, k0, kn) in pieces:
            for kk in range(kn):
                for i in range(GC):
                    K.op(K.pe, lambda e, wt=wt, kk=kk, i=i, k=k0 + kk - kofs: e.matmul(
                        banks[base + i][:, :], lhsT=wt[:, kk, i * 128:(i + 1) * 128], rhs=rhsT[:, k, :],
                        start=(k == 0), stop=(k == nk - 1)), [wbuf, rbuf], [b_bk[base + i]],
                        inc=(k0 + kk - kofs == nk - 1 or kk == kn - 1))
        for i in range(GC):
            j = og * GC + i
            K.op(K.dve, lambda e, i=i, j=j: e.scalar_tensor_tensor(
                out=xT[:, j, :], in0=banks[base + i][:, :], scalar=gvec[:, j:j + 1], in1=src[:, j, :],
                op0=ALU.mult, op1=ALU.add), [b_bk[base + i], sbuf, b_v], [b_x])

    NTL = T // 512

    def loads(t):
        tsl = slice(t * 512, (t + 1) * 512)
        K.dma(K.sp, lambda e: e.dma_start(out=xin[:], in_=xv[:, :, tsl]), [], [b_xin])
        if o_load is not None:
            o_load(t, oT2[:, 0], b_o2[0])
        else:
            K.dma(K.sp, lambda e: e.dma_start(out=oT2[:, 0], in_=ov[:, :, tsl]), [], [b_o2[0]])

    loads(0)
    for t in range(NTL):
        tsl = slice(t * 512, (t + 1) * 512)
        for og in range(DG):
            wt, wbuf = load_w(wov[:, :, og * GW:(og + 1) * GW], KO, GW)
            proj_group([(wt, wbuf, 0, KO)], oT2[:, 0], b_o2[0], KO, og, g1, src=xin, sbuf=b_xin)
        if t + 1 < NTL:
            loads(t + 1)
        emit_norm(K, C, xT, b_x, KD, D, G2, Sh2, b_v, oT, b_o, banks[7], b_bk[7], tmp)
        for hg in halves:
            if not hg:
                continue
            for fg in hg:
                wa, wab = load_w(wiv[:, :, fg * 512:(fg + 1) * 512], KD, 512)
                wu, wub = load_w(wiv[:, :, DFF + fg * 512:DFF + (fg + 1) * 512], KD, 512)
                for i in range(4):
                    j = fg * 4 + i
                    jl = j - hg[0] * 4
                    ba = (2 * j) % 6
                    bu = ba + 1
                    for (wt, wbf, bk) in ((wa, wab, ba), (wu, wub, bu)):
                        for k in range(KD):
                            K.op(K.pe, lambda e, wt=wt, k=k, i=i, bk=bk: e.matmul(
                                banks[bk][:, :], lhsT=wt[:, k, i * 128:(i + 1) * 128], rhs=oT[:, k, :],
                                start=(k == 0), stop=(k == KD - 1)), [wbf, b_o], [b_bk[bk]], inc=(k == KD - 1))
                    s_ = j % 2
                    K.op(K.act, lambda e, s_=s_, ba=ba: e.activation(out=sa[:, s_, :], in_=banks[ba][:, :], func=AF.Silu),
                         [b_bk[ba]], [b_sa[s_]])
                    K.op(K.dve, lambda e, s_=s_, jl=jl, bu=bu: e.tensor_tensor(
                        out=aT[:, jl, :], in0=sa[:, s_, :], in1=banks[bu][:, :], op=ALU.mult),
                        [b_sa[s_], b_bk[bu]], [b_a])
            kbase, nkh = hg[0] * 4, len(hg) * 4
            if hg is halves[-1] or not halves[-1]:
                if after_proj is not None:
                    after_proj()
            for og in range(DG):
                pieces = []
                for (k0, kn) in _chunks(nkh, 16):
                    wt, wbuf = load_w(wfv[:, kbase + k0:kbase + k0 + kn, og * GW:(og + 1) * GW], kn, GW)
                    pieces.append((wt, wbuf, kbase + k0, kn))
                proj_group(pieces, aT, b_a, nkh, og, g2, kofs=kbase)
        K.dma(K.sp, lambda e, tsl=tsl: e.dma_start(out=xov[:, :, tsl], in_=xT[:]), [b_x], [], ds_s)
        if not last:
            emit_norm(K, C, xT, b_x, KD, D, G1n, Sh1n, b_v, oT, b_o, banks[7], b_bk[7], tmp)
            if h_store is not None:
                h_store(t, oT, b_o)
            else:
                K.dma(K.sp, lambda e, tsl=tsl: e.dma_start(out=hov[:, :, tsl], in_=oT[:, 0:KD, :]), [b_o], [], ds_s)


def _new():
    nc = bass.Bass("TRN2", target_bir_lowering=False)
    stack = contextlib.ExitStack()
    K = Kern(nc, stack)
    return nc, stack, K


def _din(nc, name, shape, dt=F32):
    return nc.dram_tensor(name, list(shape), dt, kind="ExternalInput").ap()


def _dout(nc, name, shape, dt=F32):
    return nc.dram_tensor(name, list(shape), dt, kind="ExternalOutput").ap()


def _finish(K, stack):
    K.final_wait(K.sp)
    K.emit()
    stack.close()


def build_mod(cfg):
    L, D, KD = cfg["L"], cfg["D"], cfg["KD"]
    nc, stack, K = _new()
    cst = _din(nc, "consts", [128, 640])
    cT = _din(nc, "cT", [128, KD])
    modw = _din(nc, "modw", [L, D, 6 * D])
    modb = _din(nc, "modb", [L, 6 * D])
    gmix = _din(nc, "gmix", [128, L, KD])
    gffn = _din(nc, "gffn", [128, L, KD])
    vec = _dout(nc, "vec", [128, L, 6 * KD])
    C = Consts(K, cst, K.dsems[0])
    emit_mod(K, C, cfg, cT, modw, modb, gmix, gffn, vec)
    _finish(K, stack)
    return nc


def build_norm0(cfg):
    L, D, KD, T = cfg["L"], cfg["D"], cfg["KD"], cfg["T"]
    nc, stack, K = _new()
    cst = _din(nc, "consts", [128, 640])
    xT = _din(nc, "xT", [D, T])
    vec = _din(nc, "vec", [128, L, 6 * KD])
    hT = _dout(nc, "hT", [D, T], BF16)
    C = Consts(K, cst, K.dsems[0])
    emit_norm0(K, C, cfg, xT, vec, hT)
    _finish(K, stack)
    return nc


def build_dense(cfg, last):
    L, D, KD, T, DFF, KO = cfg["L"], cfg["D"], cfg["KD"], cfg["T"], cfg["DFF"], cfg["KO"]
    nc, stack, K = _new()
    cst = _din(nc, "consts", [128, 640])
    xT = _din(nc, "xT", [D, T])
    oT = _din(nc, "oT", [KO * 128, T], BF16)
    wo = _din(nc, "wo", [KO * 128, D])
    wfi = _din(nc, "wfi", [D, 2 * DFF])
    wfo = _din(nc, "wfo", [DFF, D])
    vec = _din(nc, "vec", [128, 2, 6 * KD])
    xTo = _dout(nc, "xTo", [D, T])
    hTo = None if last else _dout(nc, "hTo", [D, T], BF16)
    C = Consts(K, cst, K.dsems[0])
    cfg2 = dict(cfg)
    cfg2["L"] = 2
    emit_dense(K, C, cfg2, 0, xT, oT, wo, wfi, wfo, vec, xTo, hTo)
    _finish(K, stack)
    return nc


def emit_hgrn(K, C, cfg, ab_idx, hT_d, w_d, lbl_d, gain_d, oT_d, h_load=None, o_store=None):
    D, KD, S = cfg["D"], cfg["KD"], cfg["S"]
    ds_a, ds_w, ds_s = K.dsems[0], K.dsems[1], K.dsems[2]
    CH, NCH = 64, 8
    W = K.sb("h_W", [128, KD, 1024], BF16)
    hT = K.sb("h_hT", [128, 2, KD, 512], BF16)
    lbl = K.sb("h_lbl", [128, 2, 2], F32)
    lbv = K.sb("h_lbv", [128, 3, 2], F32)
    gain = K.sb("h_gain", [64, 256], F32)
    ones = K.sb("h_ones", [128, 64], F32)
    names = ["sig", "lf", "key", "qs", "cum", "cm", "cl", "e0", "e1", "e2"]
    hf = [{n: K.sb("h_%s%d" % (n, h), [128, 512], F32) for n in names} for h in range(2)]
    hb = [{n: K.sb("h_%s%d" % (n, h), [128, 512], BF16) for n in ["qh", "qt", "kt", "kh", "k0"]} for h in range(2)]
    eL = [K.sb("h_eL%d" % h, [128, NCH], F32) for h in range(2)]
    Sst = [K.sb("h_S%d" % h, [128, 128], F32) for h in range(2)]
    Sbf = [K.sb("h_Sb%d" % h, [128, 2, 128], BF16) for h in range(2)]
    vc = K.sb("h_vc", [64, 2, 256], BF16)
    gs = K.sb("h_gs", [64, 2, 256], F32)
    PT = K.sb("h_PT", [64, 2, 2, 64], BF16)
    khs = K.sb("h_khs", [64, 2, 2, 128], BF16)
    junk = K.sb("h_junk", [64, 128], F32)
    ssq = K.sb("h_ssq", [64, 2, 2], F32)
    og = K.sb("h_og", [64, 2, 128], BF16)
    oTt = K.sb("h_oTt", [128, 2, 2, 512], BF16)
    fb2 = [K.ps("h_fb%d" % i, [128, 512], F32) for i in range(2)]
    fb = [fb2[0], fb2[1], fb2[0], fb2[1]]
    tb = [K.ps("h_tb%d" % i, [128, 512], F32) for i in range(2)]
    bkO = K.ps("h_bkO", [128, 512], F32)
    bkU = K.ps("h_bkU", [128, 512], F32)
    bkT = [K.ps("h_bkT%d" % i, [128, 1024], BF16) for i in range(2)]
    B = lambda n: [Buf() for _ in range(n)]
    b_W, b_lb, b_gain, b_ones = Buf(), Buf(), Buf(), Buf()
    b_hT, b_tb = B(2), B(2)
    b_fb2 = B(2)
    b_fb = [b_fb2[0], b_fb2[1], b_fb2[0], b_fb2[1]]
    b_hf = [{n: Buf() for n in names} for _ in range(2)]
    b_hb = [{n: Buf() for n in ["qh", "qt", "kt", "kh", "k0"]} for _ in range(2)]
    b_eL, b_S, b_Sbf = B(2), B(2), [B(2), B(2)]
    b_vc, b_gs, b_PT, b_khs, b_junk, b_ssq, b_og = B(2), B(2), B(2), B(2), Buf(), B(2), B(2)
    b_bkO, b_bkU, b_bkT = Buf(), Buf(), B(2)
    b_oTt = B(2)
    A = K.op
    wv = w_d.rearrange("(k p) c -> p k c", p=128)
    for half in range(2):
        K.dma(K.pool, lambda e, half=half: e.dma_start(out=W[:, :, half * 512:(half + 1) * 512],
                                                       in_=wv[:, :, half * 512:(half + 1) * 512]), [], [b_W], ds_w)
    K.dma(K.sp, lambda e: e.dma_start(out=lbl[:], in_=lbl_d[:, :, :]), [], [b_lb], ds_a)
    K.dma(K.sp, lambda e: e.dma_start(out=gain[:], in_=gain_d[0:1, :].partition_broadcast(64)), [], [b_gain], ds_a)
    A(K.dve, lambda e: e.memset(ones[:], 1.0), [], [b_ones])
    if ab_idx == 0:
        A(K.dve, lambda e: e.memset(lbv[:, 0, :], 0.0), [b_lb], [b_lb])
    else:
        A(K.dve, lambda e: e.tensor_tensor(out=lbv[:, 0, :], in0=lbl[:, 1, :], in1=lbl[:, 0, :], op=ALU.subtract),
          [b_lb], [b_lb])
        A(K.act, lambda e: e.activation(out=lbv[:, 0, :], in_=lbv[:, 0, :], func=AF.Sigmoid), [b_lb], [b_lb])
        A(K.dve, lambda e: e.tensor_scalar(out=lbv[:, 0, :], in0=lbv[:, 0, :], scalar1=1.0 - 1e-6, scalar2=0.0,
                                           op0=ALU.min, op1=ALU.max), [b_lb], [b_lb])
    A(K.dve, lambda e: e.tensor_scalar(out=lbv[:, 1, :], in0=lbv[:, 0, :], scalar1=-1.0, scalar2=1.0,
                                       op0=ALU.mult, op1=ALU.add), [b_lb], [b_lb])
    A(K.dve, lambda e: e.tensor_scalar(out=lbv[:, 2, :], in0=lbv[:, 0, :], scalar1=1.0, scalar2=-1.0,
                                       op0=ALU.mult, op1=ALU.add), [b_lb], [b_lb])
    for h in range(2):
        A(K.dve, lambda e, h=h: e.memset(PT[:, h], 0.0), [], [b_PT[h]])
        A(K.dve, lambda e, h=h: e.memset(hb[h]["k0"][:], 0.0), [], [b_hb[h]["k0"]])
        A(K.dve, lambda e, h=h: e.memset(hf[h]["e2"][:], 0.0), [], [b_hf[h]["e2"]])
        A(K.dve, lambda e, h=h: e.memset(Sst[h][:], 0.0), [], [b_S[h]])
        A(K.dve, lambda e, h=h: e.memset(Sbf[h][:, 0, :], 0.0), [], [b_Sbf[h][0]])
    hv = None if h_load is not None else hT_d.rearrange("(k p) t -> p k t", p=128)
    ov = None if o_store is not None else oT_d.rearrange("(h p) t -> p h t", p=128)
    cidx = 0
    for t in range(S // 512):
        s = t % 2
        if h_load is not None:
            h_load(t, hT[:, s], b_hT[s])
        else:
            K.dma(K.sp, lambda e, t=t, s=s: e.dma_start(out=hT[:, s], in_=hv[:, :, t * 512:(t + 1) * 512]),
                  [], [b_hT[s]], ds_a)
        for h in range(2):
            for blk in (2 * h, 2 * h + 1):
                for k in range(KD):
                    A(K.pe, lambda e, blk=blk, k=k, s=s: e.matmul(fb[blk][:, :], lhsT=W[:, k, blk * 128:(blk + 1) * 128],
                                                                  rhs=hT[:, s, k, :], start=(k == 0), stop=(k == KD - 1)),
                      [b_W, b_hT[s]], [b_fb[blk]], inc=(k == KD - 1))
            f, bf_, bb, bbf = hf[h], hb[h], b_hf[h], b_hb[h]
            lb_, oml, noml = lbv[:, 0, h:h + 1], lbv[:, 1, h:h + 1], lbv[:, 2, h:h + 1]
            A(K.act, lambda e, f=f, h=h: e.activation(out=f["sig"][:], in_=fb[2 * h + 1][:, :], func=AF.Sigmoid),
              [b_fb[2 * h + 1]], [bb["sig"]])
            A(K.act, lambda e, f=f, h=h: e.activation(out=f["qs"][:], in_=fb[2 * h][:, :], func=AF.Silu),
              [b_fb[2 * h]], [bb["qs"]])
            A(K.dve, lambda e, f=f, oml=oml, lb_=lb_: e.tensor_scalar(out=f["lf"][:], in0=f["sig"][:], scalar1=oml,
                                                                     scalar2=lb_, op0=ALU.mult, op1=ALU.add),
              [bb["sig"], b_lb], [bb["lf"]])
            A(K.dve, lambda e, f=f: e.tensor_scalar_max(out=f["lf"][:], in0=f["lf"][:], scalar1=1e-30),
              [bb["lf"]], [bb["lf"]])
            A(K.act, lambda e, f=f: e.activation(out=f["lf"][:], in_=f["lf"][:], func=AF.Ln), [bb["lf"]], [bb["lf"]])
            A(K.dve, lambda e, f=f, oml=oml, noml=noml: e.tensor_scalar(out=f["key"][:], in0=f["sig"][:], scalar1=noml,
                                                                       scalar2=oml, op0=ALU.mult, op1=ALU.add),
              [bb["sig"], b_lb], [bb["key"]])
            for c in range(NCH):
                A(K.dve, lambda e, f=f, c=c: e.tensor_tensor_scan(out=f["cum"][:, c * CH:(c + 1) * CH], data0=ones[:, :],
                                                                  data1=f["lf"][:, c * CH:(c + 1) * CH], initial=0.0,
                                                                  op0=ALU.mult, op1=ALU.add),
                  [bb["lf"], b_ones], [bb["cum"]])
            c3 = lambda a: a[:].rearrange("p (c t) -> p c t", c=NCH)
            A(K.dve, lambda e, f=f: e.tensor_tensor(out=c3(f["cm"]), in0=c3(f["cum"]),
                                                    in1=c3(f["cum"])[:, :, 31:32].broadcast_to([128, NCH, CH]),
                                                    op=ALU.subtract), [bb["cum"]], [bb["cm"]])
            A(K.dve, lambda e, f=f: e.tensor_tensor(out=c3(f["cl"]), in0=c3(f["cum"]),
                                                    in1=c3(f["cum"])[:, :, CH - 1:CH].broadcast_to([128, NCH, CH]),
                                                    op=ALU.subtract), [bb["cum"]], [bb["cl"]])
            A(K.act, lambda e, f=f, h=h: e.activation(out=eL[h][:, :], in_=c3(f["cum"])[:, :, CH - 1], func=AF.Exp),
              [bb["cum"]], [b_eL[h]])
            A(K.act, lambda e, f=f: e.activation(out=c3(f["e2"])[:, :, 0:32], in_=c3(f["cum"])[:, :, 0:32], func=AF.Exp,
                                                 scale=-1.0), [bb["cum"]], [bb["e2"]])
            A(K.dve, lambda e, f=f, bf_=bf_: e.tensor_tensor(out=c3(bf_["k0"])[:, :, 0:32], in0=c3(f["key"])[:, :, 0:32],
                                                             in1=c3(f["e2"])[:, :, 0:32], op=ALU.mult),
              [bb["key"], bb["e2"]], [bbf["k0"]])
            for (src, sc, es, mul, dst) in (("cum", 1.0, "e0", "qs", "qh"), ("cm", 1.0, "e1", "qs", "qt"),
                                            ("cm", -1.0, "e0", "key", "kt"), ("cl", -1.0, "e1", "key", "kh")):
                A(K.act, lambda e, f=f, src=src, sc=sc, es=es: e.activation(out=f[es][:], in_=f[src][:], func=AF.Exp,
                                                                           scale=sc), [bb[src]], [bb[es]])
                A(K.dve, lambda e, f=f, bf_=bf_, es=es, mul=mul, dst=dst: e.tensor_tensor(
                    out=bf_[dst][:], in0=f[mul][:], in1=f[es][:], op=ALU.mult), [bb[mul], bb[es]], [bbf[dst]])
        def prep_proj(c, s=s):
            csl = slice(c * CH, (c + 1) * CH)
            par = c % 2
            for k in range(KD):
                A(K.pe, lambda e, k=k: e.matmul(tb[par][0:CH, :], lhsT=hT[:, s, k, csl], rhs=W[:, k, 512:1024],
                                                start=(k == 0), stop=(k == KD - 1)),
                  [b_W, b_hT[s]], [b_tb[par]], inc=(k == KD - 1))
            A(K.act, lambda e: e.activation(out=vc[:, par, :], in_=tb[par][0:CH, 0:256], func=AF.Identity),
              [b_tb[par]], [b_vc[par]])
            A(K.act, lambda e: e.activation(out=gs[:, par, :], in_=tb[par][0:CH, 256:512], func=AF.Silu),
              [b_tb[par]], [b_gs[par]])
            A(K.dve, lambda e: e.tensor_tensor(out=gs[:, par, :], in0=gs[:, par, :], in1=gain[:, :], op=ALU.mult),
              [b_gs[par], b_gain], [b_gs[par]])

        def prep(c, s=s):
            csl = slice(c * CH, (c + 1) * CH)
            c0_ = c * CH
            par = c % 2
            if c >= 2:
                prep_proj(c)
            for h in range(2):
                bf_, bbf = hb[h], b_hb[h]
                A(K.pe, lambda e, h=h, bf_=bf_: e.matmul(fb2[par][0:64, h * 64 + 32:h * 64 + 64], lhsT=bf_["kt"][:, csl],
                                                         rhs=bf_["qt"][:, c0_ + 32:c0_ + 64], start=True, stop=True),
                  [bbf["kt"], bbf["qt"]], [b_fb2[par]], inc=False)
                A(K.pe, lambda e, h=h, bf_=bf_: e.matmul(fb2[par][0:32, h * 64:h * 64 + 32], lhsT=bf_["k0"][:, c0_:c0_ + 32],
                                                         rhs=bf_["qh"][:, c0_:c0_ + 32], start=True, stop=True),
                  [bbf["k0"], bbf["qh"]], [b_fb2[par]], inc=False)
                A(K.pe, lambda e, h=h, bf_=bf_: e.transpose(bkT[par][0:64, h * 128:(h + 1) * 128], bf_["kh"][:, csl],
                                                            C.ident_b()), [bbf["kh"], C.buf], [b_bkT[par]], inc=(h == 1))
            scv = fb2[par][0:64, 0:128].rearrange("p (h t) -> p h t", h=2)
            A(K.dve, lambda e: e.tensor_tensor(out=PT[:, par, :, 32:64], in0=scv[:, :, 32:64],
                                               in1=C.f[0:64, 160:192].unsqueeze(1).broadcast_to([64, 2, 32]),
                                               op=ALU.mult), [b_fb2[par], C.buf], [b_PT[par]])
            A(K.dve, lambda e: e.tensor_tensor(out=PT[0:32, par, :, 0:32], in0=scv[0:32, :, 0:32],
                                               in1=C.f[0:32, 128:160].unsqueeze(1).broadcast_to([32, 2, 32]),
                                               op=ALU.mult), [b_fb2[par], C.buf], [b_PT[par]])
            A(K.act, lambda e: e.activation(out=khs[:, par].rearrange("p h d -> p (h d)"), in_=bkT[par][0:64, 0:256],
                                            func=AF.Identity), [b_bkT[par]], [b_khs[par]])

        def main(c, s=s):
            csl = slice(c * CH, (c + 1) * CH)
            par = c % 2
            sp_, sn = c % 2, (c + 1) % 2
            for h in range(2):
                bf_, bbf = hb[h], b_hb[h]
                A(K.pe, lambda e, h=h: e.matmul(bkO[0:64, h * 128:(h + 1) * 128], lhsT=PT[:, par, h, :],
                                                rhs=vc[:, par, h * 128:(h + 1) * 128], start=True, stop=False),
                  [b_PT[par], b_vc[par]], [b_bkO], inc=False)
                A(K.pe, lambda e, h=h, bf_=bf_: e.matmul(bkO[0:64, h * 128:(h + 1) * 128], lhsT=bf_["qh"][:, csl],
                                                         rhs=Sbf[h][:, sp_, :], start=False, stop=True),
                  [bbf["qh"], b_Sbf[h][sp_]], [b_bkO], inc=(h == 1))
            for h in range(2):
                A(K.pe, lambda e, h=h: e.matmul(bkU[:, h * 128:(h + 1) * 128], lhsT=khs[:, par, h, :],
                                                rhs=vc[:, par, h * 128:(h + 1) * 128], start=True, stop=True),
                  [b_khs[par], b_vc[par]], [b_bkU], inc=(h == 1))
            for h in range(2):
                A(K.dve, lambda e, h=h: e.scalar_tensor_tensor(out=Sst[h][:], in0=Sst[h][:], scalar=eL[h][:, c:c + 1],
                                                               in1=bkU[:, h * 128:(h + 1) * 128], op0=ALU.mult,
                                                               op1=ALU.add), [b_S[h], b_eL[h], b_bkU], [b_S[h]])
                A(K.act, lambda e, h=h: e.activation(out=Sbf[h][:, sn, :], in_=Sst[h][:], func=AF.Identity),
                  [b_S[h]], [b_Sbf[h][sn]])

        def post_a(c, s=s):
            par = c % 2
            for h in range(2):
                A(K.act, lambda e, h=h: e.activation(out=junk[:, :], in_=bkO[0:64, h * 128:(h + 1) * 128], func=AF.Square,
                                                     accum_out=ssq[:, h, 0:1]), [b_bkO], [b_junk, b_ssq[h]])
                A(K.act, lambda e, h=h: e.activation(out=ssq[:, h, 1:2], in_=ssq[:, h, 0:1], func=AF.Sqrt,
                                                     scale=1.0 / 128, bias=EPS), [b_ssq[h]], [b_ssq[h]])
                A(K.dve, lambda e, h=h: e.reciprocal(out=ssq[:, h, 1:2], in_=ssq[:, h, 1:2]), [b_ssq[h]], [b_ssq[h]])
                A(K.dve, lambda e, h=h: e.scalar_tensor_tensor(out=og[:, h, :], in0=bkO[0:64, h * 128:(h + 1) * 128],
                                                               scalar=ssq[:, h, 1:2],
                                                               in1=gs[:, par, h * 128:(h + 1) * 128],
                                                               op0=ALU.mult, op1=ALU.mult),
                  [b_bkO, b_ssq[h], b_gs[par]], [b_og[h]])

        def post_b(c, s=s):
            csl = slice(c * CH, (c + 1) * CH)
            par = (c + 1) % 2
            for h in range(2):
                A(K.pe, lambda e, h=h: e.transpose(bkT[par][:, 256 + h * 64:256 + (h + 1) * 64], og[:, h, :],
                                                   C.ident_b(64)), [b_og[h], C.buf], [b_bkT[par]], inc=(h == 1))
            A(K.act, lambda e: e.activation(out=oTt[:, s, :, csl],
                                            in_=bkT[par][:, 256:384].rearrange("p (h t) -> p h t", h=2),
                                            func=AF.Identity), [b_bkT[par]], [b_oTt[s]])

        prep_proj(0)
        prep_proj(1)
        prep(0)
        for c in range(NCH):
            if c + 1 < NCH:
                prep(c + 1)
            main(c)
            if c >= 1:
                post_b(c - 1)
            post_a(c)
        post_b(NCH - 1)
        if o_store is not None:
            o_store(t, oTt[:, s], b_oTt[s], 2)
        else:
            K.dma(K.pool, lambda e, t=t, s=s: e.dma_start(out=ov[:, :, t * 512:(t + 1) * 512], in_=oTt[:, s]),
                  [b_oTt[s]], [], ds_s)


def build_hgrn(cfg, ab_idx):
    D, S = cfg["D"], cfg["S"]
    nc, stack, K = _new()
    cst = _din(nc, "consts", [128, 640])
    hT = _din(nc, "hT", [D, S], BF16)
    w = _din(nc, "w", [D, 1024])
    lbl = _din(nc, "lbl", [128, 2, 2])
    gain = _din(nc, "gain", [1, 256])
    oT = _dout(nc, "oT", [256, S], BF16)
    C = Consts(K, cst, K.dsems[0])
    emit_hgrn(K, C, cfg, ab_idx, hT, w, lbl, gain, oT)
    _finish(K, stack)
    return nc


def emit_fox(K, C, cfg, hT_d, w_d, qkg_d, fbias_d, oT_d, h_load=None, o_store=None):
    D, KD, S = cfg["D"], cfg["KD"], cfg["S"]
    NB = S // 128
    A = K.op
    W = K.sb("x_W", [128, KD, 1026], BF16)
    hT = K.sb("x_hT", [128, 2, KD, 512], BF16)
    kTa = K.sb("x_kTa", [128, 2, S], BF16)
    va = K.sb("x_va", [128, 2, NB, 132], BF16)
    Ga = K.sb("x_Ga", [128, 2, NB], F32)
    qkg = K.sb("x_qkg", [128, 2], F32)
    nfb = K.sb("x_nfb", [128, 2], F32)
    qn = K.sb("x_qn", [128, 2, 512], BF16)
    sq = K.sb("x_sq", [128, 2, 512], F32)
    rt = K.sb("x_rt", [128, 2, 512], F32)
    sg = K.sb("x_sg", [128, 4, 256], F32)
    spv = K.sb("x_spv", [128, 2, 4], F32)
    lcs = K.sb("x_lcs", [128, 2, 4], F32)
    tots = K.sb("x_tots", [128, 2, 4], F32)
    incl = K.sb("x_incl", [128, 2, 2, 4], F32)
    excl = K.sb("x_excl", [128, 2, 4], F32)
    Bq = K.sb("x_Bq", [128, 2, NB], F32)
    PT = K.sb("x_PT", [128, 3, 512], BF16)
    rden = K.sb("x_rden", [128, 4], F32)
    dqc = K.sb("x_dqc", [128, 2, 4], F32)
    dqb = K.sb("x_dqb", [1, 2, 512], F32)
    dqbc = K.sb("x_dqbc", [128, 2, 512], F32)
    stg = K.sb("x_stg", [128, 3, 512], F32)
    b_dqc, b_dqb, b_dqbc = Buf(), Buf(), Buf()
    b_stg = [Buf() for _ in range(3)]
    og = K.sb("x_og", [128, 2, 128], BF16)
    oTt = K.sb("x_oTt", [128, 2, 2, 512], BF16)
    bk = [K.ps("x_bk%d" % i, [128, 512], F32) for i in range(7)]
    bkT = K.ps("x_bkT", [128, 1024], BF16)
    b_bk = [Buf() for _ in range(7)]
    b_T = Buf()
    P0, P1, N_, V_, Fb, O3, P2 = range(7)
    OB = [N_, V_, Fb, O3]
    B = lambda n: [Buf() for _ in range(n)]
    b_W, b_kTa, b_va, b_Ga, b_par = Buf(), Buf(), Buf(), Buf(), Buf()
    b_hT, b_qn, b_sq, b_rt, b_oTt = B(2), B(2), B(2), B(2), B(2)
    b_sg, b_spv, b_lcs, b_tots, b_excl = Buf(), Buf(), Buf(), Buf(), Buf()
    b_incl, b_Bq = B(2), Buf()
    b_PT = B(3)
    b_rden, b_og = Buf(), B(2)
    wv = w_d.rearrange("(k p) c -> p k c", p=128)
    K.dma(K.pool, lambda e: e.dma_start(out=W[:, :, 0:512], in_=wv[:, :, 0:512]), [], [b_W])
    K.dma(K.pool, lambda e: e.dma_start(out=W[:, :, 512:1026], in_=wv[:, :, 512:1026]), [], [b_W])
    K.dma(K.sp, lambda e: e.dma_start(out=qkg[:], in_=qkg_d[:, :]), [], [b_par])
    K.dma(K.sp, lambda e: e.dma_start(out=nfb[:], in_=fbias_d[0:1, :].partition_broadcast(128)), [], [b_par])
    A(K.dve, lambda e: e.tensor_scalar(out=qkg[:, 0:1], in0=qkg[:, 0:1], scalar1=float(128 ** -0.5), scalar2=None,
                                       op0=ALU.mult), [b_par], [b_par])
    A(K.dve, lambda e: e.tensor_scalar(out=nfb[:, :], in0=nfb[:, :], scalar1=-1.0, scalar2=None, op0=ALU.mult),
      [b_par], [b_par])
    A(K.dve, lambda e: e.memset(va[:], 1.0), [], [b_va])
    hv = None if h_load is not None else hT_d.rearrange("(k p) t -> p k t", p=128)
    ov = None if o_store is not None else oT_d.rearrange("(h p) t -> p h t", p=128)
    xs = 0
    for t in range(S // 512):
        s = t % 2
        tsl = slice(t * 512, (t + 1) * 512)
        if h_load is not None:
            h_load(t, hT[:, s], b_hT[s])
        else:
            K.dma(K.sp, lambda e, s=s, tsl=tsl: e.dma_start(out=hT[:, s], in_=hv[:, :, tsl]), [], [b_hT[s]])
        for h in range(2):
            for (which, bank) in ((0, P0), (1, P1)):
                blk = 2 * h + which
                for k in range(KD):
                    A(K.pe, lambda e, blk=blk, k=k, s=s, bank=bank: e.matmul(
                        bk[bank][:, :], lhsT=W[:, k, blk * 128:(blk + 1) * 128], rhs=hT[:, s, k, :],
                        start=(k == 0), stop=(k == KD - 1)), [b_W, b_hT[s]], [b_bk[bank]], inc=(k == KD - 1))
                x = which
                A(K.act, lambda e, x=x, bank=bank: e.activation(out=sq[:, x, :], in_=bk[bank][:, :], func=AF.Square),
                  [b_bk[bank]], [b_sq[x]])
                A(K.pe, lambda e, x=x: e.matmul(bk[N_][:, :], lhsT=C.ones_f(), rhs=sq[:, x, :], start=True, stop=True),
                  [b_sq[x], C.buf], [b_bk[N_]])
                A(K.act, lambda e, x=x: e.activation(out=rt[:, x, :], in_=bk[N_][:, :], func=AF.Sqrt, scale=1.0 / 128,
                                                     bias=EPS), [b_bk[N_]], [b_rt[x]])
                A(K.dve, lambda e, x=x: e.reciprocal(out=rt[:, x, :], in_=rt[:, x, :]), [b_rt[x]], [b_rt[x]])
                if which == 0:
                    A(K.dve, lambda e, h=h, x=x, bank=bank: e.scalar_tensor_tensor(
                        out=qn[:, h, :], in0=bk[bank][:, :], scalar=qkg[:, 0:1], in1=rt[:, x, :], op0=ALU.mult,
                        op1=ALU.mult), [b_bk[bank], b_par, b_rt[x]], [b_qn[h]])
                else:
                    A(K.dve, lambda e, h=h, x=x, bank=bank, tsl=tsl: e.scalar_tensor_tensor(
                        out=kTa[:, h, tsl], in0=bk[bank][:, :], scalar=qkg[:, 1:2], in1=rt[:, x, :], op0=ALU.mult,
                        op1=ALU.mult), [b_bk[bank], b_par, b_rt[x]], [b_kTa])
        for b in range(4):
            blk = 4 * t + b
            bsl = slice(b * 128, (b + 1) * 128)
            for k in range(KD):
                A(K.pe, lambda e, k=k, s=s, bsl=bsl: e.matmul(bk[V_][:, :], lhsT=hT[:, s, k, bsl], rhs=W[:, k, 512:1024],
                                                              start=(k == 0), stop=(k == KD - 1)),
                  [b_W, b_hT[s]], [b_bk[V_]], inc=(k == KD - 1))
            A(K.act, lambda e, blk=blk: e.activation(out=va[:, :, blk, 0:128],
                                                     in_=bk[V_][:, 0:256].rearrange("p (h e) -> p h e", h=2),
                                                     func=AF.Identity), [b_bk[V_]], [b_va])
            A(K.act, lambda e, b=b: e.activation(out=sg[:, b, :], in_=bk[V_][:, 256:512], func=AF.Sigmoid),
              [b_bk[V_]], [b_sg])
            for k in range(KD):
                A(K.pe, lambda e, k=k, s=s, bsl=bsl, b=b: e.matmul(bk[Fb][:, 2 * b:2 * b + 2], lhsT=hT[:, s, k, bsl],
                                                                   rhs=W[:, k, 1024:1026], start=(k == 0),
                                                                   stop=(k == KD - 1)),
                  [b_W, b_hT[s]], [b_bk[Fb]], inc=(k == KD - 1))
        fbv = bk[Fb][:, 0:8].rearrange("p (b h) -> p h b", h=2)
        for h in range(2):
            A(K.act, lambda e, h=h: e.activation(out=spv[:, h, :], in_=fbv[:, h, :], func=AF.Exp, scale=-1.0,
                                                 bias=nfb[:, h:h + 1]), [b_bk[Fb], b_par], [b_spv])
        A(K.act, lambda e: e.activation(out=spv[:], in_=spv[:], func=AF.Ln, bias=1.0, scale=1.0), [b_spv], [b_spv])
        A(K.pe, lambda e: e.matmul(bk[Fb][:, 16:24], lhsT=C.tri_f(), rhs=spv[:].rearrange("p h b -> p (h b)"),
                                   start=True, stop=True), [b_spv, C.buf], [b_bk[Fb]])
        A(K.act, lambda e: e.activation(out=lcs[:].rearrange("p h b -> p (h b)"), in_=bk[Fb][:, 16:24],
                                        func=AF.Identity), [b_bk[Fb]], [b_lcs])
        A(K.pe, lambda e: e.matmul(bk[Fb][:, 32:40], lhsT=C.sel_f(), rhs=lcs[:].rearrange("p h b -> p (h b)"),
                                   start=True, stop=True), [b_lcs, C.buf], [b_bk[Fb]])
        A(K.act, lambda e: e.activation(out=tots[:].rearrange("p h b -> p (h b)"), in_=bk[Fb][:, 32:40],
                                        func=AF.Identity), [b_bk[Fb]], [b_tots])
        for h in range(2):
            init = 0.0 if t == 0 else incl[:, 1 - s, h, 3:4]
            A(K.dve, lambda e, h=h, s=s, init=init: e.tensor_tensor_scan(
                out=incl[:, s, h, :], data0=C.ones_f(128, 4), data1=tots[:, h, :], initial=init, op0=ALU.mult,
                op1=ALU.add), [b_tots, C.buf, b_incl[1 - s]], [b_incl[s]])
        A(K.dve, lambda e, s=s: e.tensor_tensor(out=excl[:], in0=incl[:, s], in1=tots[:], op=ALU.subtract),
          [b_incl[s], b_tots], [b_excl])
        A(K.dve, lambda e, t=t: e.tensor_tensor(out=Ga[:, :, 4 * t:4 * t + 4], in0=lcs[:], in1=excl[:], op=ALU.add),
          [b_lcs, b_excl], [b_Ga])
        nkb = 4 * t + 4
        for h in range(2):
            A(K.dve, lambda e, h=h, s=s, nkb=nkb: e.tensor_scalar(
                out=Bq[:, h, 0:nkb], in0=Ga[:, h, 0:nkb], scalar1=incl[:, s, h, 3:4], scalar2=None,
                op0=ALU.subtract), [b_Ga, b_incl[s]], [b_Bq])
            A(K.dve, lambda e, h=h, s=s, t=t: e.tensor_scalar(
                out=dqc[:, h, :], in0=Ga[:, h, 4 * t:4 * t + 4], scalar1=incl[:, s, h, 3:4], scalar2=-1.0,
                op0=ALU.subtract, op1=ALU.mult), [b_Ga, b_incl[s]], [b_dqc])
        for h in range(2):
            for jl in range(4):
                A(K.pe, lambda e, h=h, jl=jl: e.transpose(bk[O3][0:1, jl * 128:(jl + 1) * 128], dqc[:, h, jl:jl + 1],
                                                          C.ident_f()), [b_dqc, C.buf], [b_bk[O3]])
            A(K.act, lambda e, h=h: e.activation(out=dqb[0:1, h, :], in_=bk[O3][0:1, :], func=AF.Identity),
              [b_bk[O3]], [b_dqb])
            A(K.pe, lambda e, h=h: e.matmul(bk[O3][:, :], lhsT=C.ones_f(1, 128), rhs=dqb[0:1, h, :], start=True,
                                            stop=True), [b_dqb, C.buf], [b_bk[O3]])
            A(K.act, lambda e, h=h: e.activation(out=dqbc[:, h, :], in_=bk[O3][:, :], func=AF.Identity),
              [b_bk[O3]], [b_dqbc])
        SB = (P0, P1, P2)
        for h in range(2):
            def score(kb, h=h):
                m = max(0, kb - 4 * t)
                x3 = kb % 3
                sb_ = SB[x3]
                diag = kb >= 4 * t
                A(K.pe, lambda e: e.matmul(bk[sb_][:, m * 128:512], lhsT=kTa[:, h, kb * 128:(kb + 1) * 128],
                                           rhs=qn[:, h, m * 128:512], start=True, stop=(not diag)),
                  [b_kTa, b_qn[h]], [b_bk[sb_]], inc=(not diag))
                if diag:
                    A(K.pe, lambda e: e.matmul(bk[sb_][:, m * 128:(m + 1) * 128], lhsT=C.ident_b(), rhs=C.negmask_b(),
                                               start=False, stop=True), [C.buf], [b_bk[sb_]])
                A(K.dve, lambda e: e.tensor_tensor(out=stg[:, x3, m * 128:512], in0=bk[sb_][:, m * 128:512],
                                                   in1=dqbc[:, h, m * 128:512], op=ALU.add),
                  [b_bk[sb_], b_dqbc], [b_stg[x3]])
                A(K.act, lambda e: e.activation(out=PT[:, x3, m * 128:512], in_=stg[:, x3, m * 128:512], func=AF.Exp,
                                                bias=Bq[:, h, kb:kb + 1], scale=1.0), [b_stg[x3], b_Bq], [b_PT[x3]])

            def pv(kb, h=h):
                m = max(0, kb - 4 * t)
                x3 = kb % 3
                for jl in range(m, 4):
                    j = 4 * t + jl
                    A(K.pe, lambda e, jl=jl, j=j: e.matmul(bk[OB[jl]][:, 0:129], lhsT=PT[:, x3, jl * 128:(jl + 1) * 128],
                                                           rhs=va[:, h, kb, 0:129], start=(kb == 0), stop=(kb == j)),
                      [b_PT[x3], b_va], [b_bk[OB[jl]]], inc=(jl == 3 or kb == j))
            score(0)
            score(1)
            for kb in range(2, nkb):
                score(kb)
                pv(kb - 2)
            pv(nkb - 2)
            pv(nkb - 1)
            for jl in range(4):
                A(K.dve, lambda e, jl=jl: e.reciprocal(out=rden[:, jl:jl + 1], in_=bk[OB[jl]][:, 128:129]),
                  [b_bk[OB[jl]]], [b_rden])
                x = jl % 2
                A(K.dve, lambda e, jl=jl, h=h, x=x: e.scalar_tensor_tensor(
                    out=og[:, x, :], in0=bk[OB[jl]][:, 0:128], scalar=rden[:, jl:jl + 1],
                    in1=sg[:, jl, h * 128:(h + 1) * 128], op0=ALU.mult, op1=ALU.mult),
                    [b_bk[OB[jl]], b_rden, b_sg], [b_og[x]])
                A(K.pe, lambda e, x=x: e.transpose(bkT[:, 0:128], og[:, x, :], C.ident_b()), [b_og[x], C.buf], [b_T])
                A(K.act, lambda e, s=s, h=h, jl=jl: e.activation(out=oTt[:, s, h, jl * 128:(jl + 1) * 128],
                                                                 in_=bkT[:, 0:128], func=AF.Identity),
                  [b_T], [b_oTt[s]])
        if o_store is not None:
            o_store(t, oTt[:, s], b_oTt[s], 2)
        else:
            K.dma(K.pool, lambda e, s=s, tsl=tsl: e.dma_start(out=ov[:, :, tsl], in_=oTt[:, s]), [b_oTt[s]], [])


def build_fox(cfg):
    D, S = cfg["D"], cfg["S"]
    nc, stack, K = _new()
    cst = _din(nc, "consts", [128, 640])
    hT = _din(nc, "hT", [D, S], BF16)
    w = _din(nc, "w", [D, 1026])
    qkg = _din(nc, "qkg", [128, 2])
    fbias = _din(nc, "fbias", [1, 2])
    oT = _dout(nc, "oT", [256, S], BF16)
    C = Consts(K, cst, None)
    emit_fox(K, C, cfg, hT, w, qkg, fbias, oT)
    _finish(K, stack)
    return nc


def emit_gla(K, C, cfg, hT_d, w_d, wgu_d, bg_d, gain_d, oT_d, h_load=None, o_store=None):
    D, KD, S = cfg["D"], cfg["KD"], cfg["S"]
    A = K.op
    CH, NCH = 128, 4
    NW = 1552
    W = K.sb("g_W", [128, KD, NW], BF16)
    hT = K.sb("g_hT", [128, 2, KD, 512], BF16)
    wgu = K.sb("g_wgu", [16, 256], F32)
    bg = K.sb("g_bg", [128, 2], F32)
    gain = K.sb("g_gain", [128, 512], F32)
    ones = K.sb("g_ones", [128, 128], F32)
    glT = K.sb("g_glT", [16, 512], F32)
    spt = K.sb("g_sp", [128, 2, 512], F32)
    csp = K.sb("g_csp", [128, 2, 512], F32)
    Eq = K.sb("g_Eq", [128, 2, 512], F32)
    Ek = K.sb("g_Ek", [128, 2, 512], F32)
    eL = K.sb("g_eL", [128, 2, NCH], F32)
    qh = K.sb("g_qh", [128, 2, 512], BF16)
    kt = K.sb("g_kt", [128, 2, 512], BF16)
    kh = K.sb("g_kh", [128, 2, 512], BF16)
    Sst = K.sb("g_S", [128, 2, 512], F32)
    Sbf = K.sb("g_Sbf", [128, 2, 2, 512], BF16)
    vc = K.sb("g_vc", [128, 2, 512], BF16)
    gs = K.sb("g_gs", [128, 2, 512], F32)
    PT = K.sb("g_PT", [128, 2, 128], BF16)
    khs = K.sb("g_khs", [128, 2, 256], BF16)
    junk = K.sb("g_junk", [128, 512], F32)
    ssq = K.sb("g_ssq", [128, 2], F32)
    og = K.sb("g_og", [128, 512], BF16)
    oTt = K.sb("g_oTt", [128, 2, 4, 512], BF16)
    bk = [K.ps("g_bk%d" % i, [128, 512], F32) for i in range(6)]
    bkT = K.ps("g_bkT", [128, 1024], BF16)
    bkT2 = K.ps("g_bkT2", [128, 1024], BF16)
    b_bk = [Buf() for _ in range(6)]
    b_T, b_T2 = Buf(), Buf()
    GA, Z0, Z1, V_, Gt, O_ = range(6)
    B = lambda n: [Buf() for _ in range(n)]
    b_W, b_par, b_ones, b_glT = Buf(), Buf(), Buf(), Buf()
    b_hT, b_sp, b_csp, b_Eq, b_Ek, b_eL = B(2), B(2), B(2), B(2), B(2), B(2)
    b_qh, b_kt, b_kh, b_S = B(2), B(2), B(2), B(2)
    b_Sbf = [B(2), B(2)]
    b_vc, b_gs, b_PT, b_khs, b_oTt = B(2), B(2), B(2), B(2), B(2)
    b_junk, b_ssq, b_og = Buf(), Buf(), Buf()
    wv = w_d.rearrange("(k p) c -> p k c", p=128)
    for (c0, c1) in ((0, 528), (528, 1040), (1040, 1552)):
        K.dma(K.pool, lambda e, c0=c0, c1=c1: e.dma_start(out=W[:, :, c0:c1], in_=wv[:, :, c0:c1]), [], [b_W])
    K.dma(K.sp, lambda e: e.dma_start(out=wgu[:], in_=wgu_d[:, :]), [], [b_par])
    K.dma(K.sp, lambda e: e.dma_start(out=bg[:], in_=bg_d[:, :]), [], [b_par])
    K.dma(K.sp, lambda e: e.dma_start(out=gain[:], in_=gain_d[0:1, :].partition_broadcast(128)), [], [b_par])
    A(K.dve, lambda e: e.tensor_scalar(out=bg[:, :], in0=bg[:, :], scalar1=-1.0, scalar2=None, op0=ALU.mult),
      [b_par], [b_par])
    A(K.dve, lambda e: e.memset(ones[:], 1.0), [], [b_ones])
    for dc in range(2):
        A(K.dve, lambda e, dc=dc: e.memset(Sst[:, dc, :], 0.0), [], [b_S[dc]])
        A(K.dve, lambda e, dc=dc: e.memset(Sbf[:, dc, 0, :], 0.0), [], [b_Sbf[dc][0]])
    hv = None if h_load is not None else hT_d.rearrange("(k p) t -> p k t", p=128)
    ov = None if o_store is not None else oT_d.rearrange("(h p) t -> p h t", p=128)
    c3 = lambda a: a.rearrange("p (c t) -> p c t", c=NCH)
    cg = 0
    for t in range(S // 512):
        s = t % 2
        tsl = slice(t * 512, (t + 1) * 512)
        if h_load is not None:
            h_load(t, hT[:, s], b_hT[s])
        else:
            K.dma(K.sp, lambda e, s=s, tsl=tsl: e.dma_start(out=hT[:, s], in_=hv[:, :, tsl]), [], [b_hT[s]])
        def prep_proj(c, s=s):
            csl = slice(c * CH, (c + 1) * CH)
            ts = c % 2
            for (bank, c0) in ((V_, 528), (Gt, 1040)):
                for k in range(KD):
                    A(K.pe, lambda e, k=k, bank=bank, c0=c0: e.matmul(
                        bk[bank][:, :], lhsT=hT[:, s, k, csl], rhs=W[:, k, c0:c0 + 512], start=(k == 0),
                        stop=(k == KD - 1)), [b_W, b_hT[s]], [b_bk[bank]], inc=(k == KD - 1))
            A(K.act, lambda e: e.activation(out=vc[:, ts, :], in_=bk[V_][:, :], func=AF.Identity),
              [b_bk[V_]], [b_vc[ts]])
            A(K.act, lambda e: e.activation(out=gs[:, ts, :], in_=bk[Gt][:, :], func=AF.Silu),
              [b_bk[Gt]], [b_gs[ts]])
            A(K.dve, lambda e: e.tensor_tensor(out=gs[:, ts, :], in0=gs[:, ts, :], in1=gain[:, :], op=ALU.mult),
              [b_gs[ts], b_par], [b_gs[ts]])

        for k in range(KD):
            A(K.pe, lambda e, k=k, s=s: e.matmul(bk[GA][0:16, :], lhsT=W[:, k, 512:528], rhs=hT[:, s, k, :],
                                                 start=(k == 0), stop=(k == KD - 1)),
              [b_W, b_hT[s]], [b_bk[GA]], inc=(k == KD - 1))
        prep_proj(0)
        prep_proj(1)
        A(K.act, lambda e: e.activation(out=glT[:, :], in_=bk[GA][0:16, :], func=AF.Identity), [b_bk[GA]], [b_glT])
        for dc in range(2):
            zb = (Z0, Z1)[dc]
            A(K.pe, lambda e, dc=dc, zb=zb: e.matmul(bk[zb][:, :], lhsT=wgu[:, dc * 128:(dc + 1) * 128], rhs=glT[:, :],
                                                     start=True, stop=True), [b_par, b_glT], [b_bk[zb]])
            A(K.act, lambda e, dc=dc, zb=zb: e.activation(out=spt[:, dc, :], in_=bk[zb][:, :], func=AF.Exp, scale=-1.0,
                                                          bias=bg[:, dc:dc + 1]), [b_bk[zb], b_par], [b_sp[dc]])
            A(K.act, lambda e, dc=dc: e.activation(out=spt[:, dc, :], in_=spt[:, dc, :], func=AF.Ln, bias=1.0,
                                                   scale=1.0), [b_sp[dc]], [b_sp[dc]])
            for c in range(NCH):
                A(K.dve, lambda e, dc=dc, c=c: e.tensor_tensor_scan(
                    out=csp[:, dc, c * CH:(c + 1) * CH], data0=ones[:, :], data1=spt[:, dc, c * CH:(c + 1) * CH],
                    initial=0.0, op0=ALU.mult, op1=ALU.add), [b_sp[dc], b_ones], [b_csp[dc]])
            A(K.act, lambda e, dc=dc: e.activation(out=Eq[:, dc, :], in_=csp[:, dc, :], func=AF.Exp, scale=-1.0 / 16),
              [b_csp[dc]], [b_Eq[dc]])
            A(K.act, lambda e, dc=dc: e.activation(out=Ek[:, dc, :], in_=csp[:, dc, :], func=AF.Exp, scale=1.0 / 16),
              [b_csp[dc]], [b_Ek[dc]])
            A(K.act, lambda e, dc=dc: e.activation(out=eL[:, dc, :], in_=c3(csp[:, dc, :])[:, :, CH - 1], func=AF.Exp,
                                                   scale=-1.0 / 16), [b_csp[dc]], [b_eL[dc]])
            for (which, dst) in ((0, "q"), (1, "k")):
                blk = which * 2 + dc
                for k in range(KD):
                    A(K.pe, lambda e, blk=blk, k=k, s=s, zb=zb: e.matmul(
                        bk[zb][:, :], lhsT=W[:, k, blk * 128:(blk + 1) * 128], rhs=hT[:, s, k, :], start=(k == 0),
                        stop=(k == KD - 1)), [b_W, b_hT[s]], [b_bk[zb]], inc=(k == KD - 1))
                if which == 0:
                    A(K.dve, lambda e, dc=dc, zb=zb: e.scalar_tensor_tensor(
                        out=qh[:, dc, :], in0=bk[zb][:, :], scalar=float(256 ** -0.5), in1=Eq[:, dc, :], op0=ALU.mult,
                        op1=ALU.mult), [b_bk[zb], b_Eq[dc]], [b_qh[dc]])
                else:
                    A(K.dve, lambda e, dc=dc, zb=zb: e.tensor_tensor(out=kt[:, dc, :], in0=bk[zb][:, :],
                                                                     in1=Ek[:, dc, :], op=ALU.mult),
                      [b_bk[zb], b_Ek[dc]], [b_kt[dc]])
                    A(K.dve, lambda e, dc=dc: e.tensor_tensor(
                        out=c3(kh[:, dc, :]), in0=c3(kt[:, dc, :]),
                        in1=eL[:, dc, :].unsqueeze(2).broadcast_to([128, NCH, CH]), op=ALU.mult),
                        [b_kt[dc], b_eL[dc]], [b_kh[dc]])
        def prep(c, s=s):
            csl = slice(c * CH, (c + 1) * CH)
            ts = c % 2
            if c >= 2:
                prep_proj(c)
            for dc in range(2):
                A(K.pe, lambda e, dc=dc: e.matmul(bk[GA][:, 0:128], lhsT=kt[:, dc, csl], rhs=qh[:, dc, csl],
                                                  start=(dc == 0), stop=(dc == 1)),
                  [b_kt[dc], b_qh[dc]], [b_bk[GA]], inc=(dc == 1))
            A(K.dve, lambda e: e.tensor_tensor(out=PT[:, ts, :], in0=bk[GA][:, 0:128], in1=C.tri_f(), op=ALU.mult),
              [b_bk[GA], C.buf], [b_PT[ts]])
            for dc in range(2):
                A(K.pe, lambda e, dc=dc: e.transpose(bkT[:, dc * 128:(dc + 1) * 128], kh[:, dc, csl], C.ident_b()),
                  [b_kh[dc], C.buf], [b_T], inc=(dc == 1))
            A(K.act, lambda e: e.activation(out=khs[:, ts, :], in_=bkT[:, 0:256], func=AF.Identity),
              [b_T], [b_khs[ts]])

        def main(c, s=s):
            csl = slice(c * CH, (c + 1) * CH)
            ts = c % 2
            sp_, sn = c % 2, (c + 1) % 2
            A(K.pe, lambda e: e.matmul(bk[O_][:, :], lhsT=PT[:, ts, :], rhs=vc[:, ts, :], start=True, stop=False),
              [b_PT[ts], b_vc[ts]], [b_bk[O_]], inc=False)
            for dc in range(2):
                A(K.pe, lambda e, dc=dc: e.matmul(bk[O_][:, :], lhsT=qh[:, dc, csl], rhs=Sbf[:, dc, sp_, :], start=False,
                                                  stop=(dc == 1)),
                  [b_qh[dc], b_Sbf[dc][sp_]], [b_bk[O_]], inc=(dc == 1))
            for dc in range(2):
                ub = (Z0, Z1)[dc]
                A(K.pe, lambda e, dc=dc, ub=ub: e.matmul(bk[ub][:, :], lhsT=khs[:, ts, dc * 128:(dc + 1) * 128],
                                                         rhs=vc[:, ts, :], start=True, stop=True),
                  [b_khs[ts], b_vc[ts]], [b_bk[ub]])
            for dc in range(2):
                ub = (Z0, Z1)[dc]
                A(K.dve, lambda e, dc=dc, ub=ub: e.scalar_tensor_tensor(
                    out=Sst[:, dc, :], in0=Sst[:, dc, :], scalar=eL[:, dc, c:c + 1], in1=bk[ub][:, :], op0=ALU.mult,
                    op1=ALU.add), [b_S[dc], b_eL[dc], b_bk[ub]], [b_S[dc]])
                A(K.act, lambda e, dc=dc: e.activation(out=Sbf[:, dc, sn, :], in_=Sst[:, dc, :], func=AF.Identity),
                  [b_S[dc]], [b_Sbf[dc][sn]])

        def post_a(c, s=s):
            ts = c % 2
            A(K.act, lambda e: e.activation(out=junk[:, :], in_=bk[O_][:, :], func=AF.Square, accum_out=ssq[:, 0:1]),
              [b_bk[O_]], [b_junk, b_ssq])
            A(K.act, lambda e: e.activation(out=ssq[:, 1:2], in_=ssq[:, 0:1], func=AF.Sqrt, scale=1.0 / 512, bias=EPS),
              [b_ssq], [b_ssq])
            A(K.dve, lambda e: e.reciprocal(out=ssq[:, 1:2], in_=ssq[:, 1:2]), [b_ssq], [b_ssq])
            A(K.dve, lambda e: e.scalar_tensor_tensor(out=og[:, :], in0=bk[O_][:, :], scalar=ssq[:, 1:2],
                                                      in1=gs[:, ts, :], op0=ALU.mult, op1=ALU.mult),
              [b_bk[O_], b_ssq, b_gs[ts]], [b_og])

        def post_b(c, s=s):
            csl = slice(c * CH, (c + 1) * CH)
            for ec in range(4):
                A(K.pe, lambda e, ec=ec: e.transpose(bkT2[:, ec * 128:(ec + 1) * 128], og[:, ec * 128:(ec + 1) * 128],
                                                     C.ident_b()), [b_og, C.buf], [b_T2], inc=(ec == 3))
            A(K.act, lambda e: e.activation(out=oTt[:, s, :, csl], in_=bkT2[:, 0:512].rearrange("p (a q) -> p a q", a=4),
                                            func=AF.Identity), [b_T2], [b_oTt[s]])

        prep(0)
        for c in range(NCH):
            if c + 1 < NCH:
                prep(c + 1)
            main(c)
            if c >= 1:
                post_b(c - 1)
            post_a(c)
        post_b(NCH - 1)
        if o_store is not None:
            o_store(t, oTt[:, s], b_oTt[s], 4)
        else:
            K.dma(K.pool, lambda e, s=s, tsl=tsl: e.dma_start(out=ov[:, :, tsl], in_=oTt[:, s]), [b_oTt[s]], [])


def build_gla(cfg):
    D, S = cfg["D"], cfg["S"]
    nc, stack, K = _new()
    cst = _din(nc, "consts", [128, 640])
    hT = _din(nc, "hT", [D, S], BF16)
    w = _din(nc, "w", [D, 1552])
    wgu = _din(nc, "wgu", [16, 256])
    bg = _din(nc, "bg", [128, 2])
    gain = _din(nc, "gain", [1, 512])
    oT = _dout(nc, "oT", [512, S], BF16)
    C = Consts(K, cst, None)
    emit_gla(K, C, cfg, hT, w, wgu, bg, gain, oT)
    _finish(K, stack)
    return nc


CFG = dict(L=4, D=2048, KD=16, T=2048, S=8192, DFF=5632, KF=44, KO=16)
_PROGS = {}


def _prog(key, fn):
    if key not in _PROGS:
        _PROGS[key] = fn()
    return _PROGS[key]


def _run(nc, in_maps):
    res = run_bass_kernel_spmd(nc, in_maps, core_ids=list(range(8)))
    return res.results


def _fm(v, kd):
    v = np.asarray(v)
    lead = v.shape[:-1]
    a = v.reshape(lead + (kd, 128))
    return np.ascontiguousarray(np.moveaxis(a, -1, 0))


def kernel_unfused(x, c, mod_w, mod_b, norm_mix_gain, norm_ffn_gain, ab_w_in, ab_w_out, hgrn_lb_logits, hgrn_out_gain,
                   fox_q_gain, fox_k_gain, fox_f_bias, gla_w_in, gla_w_gate_up, gla_b_gate, gla_out_gain, gla_w_out,
                   ffn_w_in, ffn_w_out):
    cfg = CFG
    L, D, KD, T, S, DFF = cfg["L"], cfg["D"], cfg["KD"], cfg["T"], cfg["S"], cfg["DFF"]
    f32 = lambda a: np.ascontiguousarray(np.asarray(a, dtype=np.float32))
    x, c, mod_w, mod_b = f32(x), f32(c), f32(mod_w), f32(mod_b)
    consts = make_consts()
    cores = [(cid // 4, cid % 4) for cid in range(8)]

    cfg1 = dict(cfg)
    cfg1["L"] = 1
    nc = _prog("mod", lambda: build_mod(cfg1))
    ims = []
    for (b, j) in cores:
        ims.append({"consts": consts, "cT": _fm(c[b], KD), "modw": mod_w[j:j + 1], "modb": mod_b[j:j + 1],
                    "gmix": _fm(f32(norm_mix_gain)[j:j + 1], KD), "gffn": _fm(f32(norm_ffn_gain)[j:j + 1], KD)})
    r = _run(nc, ims)
    vec = [np.ascontiguousarray(np.concatenate([r[b * 4 + j]["vec"] for j in range(4)], axis=1)) for b in range(2)]

    nc = _prog("norm0", lambda: build_norm0(cfg))
    xT = [np.ascontiguousarray(x[b, j * T:(j + 1) * T, :].T) for (b, j) in cores]
    r = _run(nc, [{"consts": consts, "xT": xT[i], "vec": vec[cores[i][0]]} for i in range(8)])
    hT = [r[i]["hT"] for i in range(8)]

    for l in range(L):
        hfull = [np.ascontiguousarray(np.concatenate(hT[b * 4:b * 4 + 4], axis=1)) for b in range(2)]
        if l % 2 == 0:
            i = l // 2
            w = f32(ab_w_in[i])
            lbl_all = f32(hgrn_lb_logits)
            ims_h, ims_f = [], []
            for (b, j) in cores:
                hh = (2 * j, 2 * j + 1)
                cs = lambda base, h_: w[:, base + h_ * 128:base + (h_ + 1) * 128]
                wh = np.concatenate([cs(0, hh[0]), cs(1024, hh[0]), cs(0, hh[1]), cs(1024, hh[1]),
                                     cs(2048, hh[0]), cs(2048, hh[1]), cs(3072, hh[0]), cs(3072, hh[1])], axis=1)
                lbl = np.stack([np.stack([lbl_all[ly, h_ * 128:(h_ + 1) * 128] for h_ in hh], axis=1)
                                for ly in range(2)], axis=1)
                ims_h.append({"consts": consts, "hT": hfull[b], "w": np.ascontiguousarray(wh),
                              "lbl": f32(lbl), "gain": f32(hgrn_out_gain[i][2 * j:2 * j + 2]).reshape(1, 256)})
                wf = np.concatenate([cs(4096, hh[0]), cs(5120, hh[0]), cs(4096, hh[1]), cs(5120, hh[1]),
                                     cs(6144, hh[0]), cs(6144, hh[1]), cs(7168, hh[0]), cs(7168, hh[1]),
                                     w[:, 8192 + hh[0]:8192 + hh[0] + 1], w[:, 8192 + hh[1]:8192 + hh[1] + 1]], axis=1)
                ims_f.append({"consts": consts, "hT": hfull[b], "w": np.ascontiguousarray(wf),
                              "qkg": f32(np.stack([fox_q_gain[i], fox_k_gain[i]], axis=1)),
                              "fbias": f32(fox_f_bias[i][2 * j:2 * j + 2]).reshape(1, 2)})
            rh = _run(_prog(("hgrn", i), lambda: build_hgrn(cfg, i)), ims_h)
            rf = _run(_prog("fox", lambda: build_fox(cfg)), ims_f)
            ofull = [np.concatenate([np.concatenate([rh[b * 4 + j]["oT"], rf[b * 4 + j]["oT"]], axis=0)
                                     for j in range(4)], axis=0) for b in range(2)]
            wo_src = f32(ab_w_out[i])
            perm = []
            for j in range(4):
                for typ in range(2):
                    for hl in range(2):
                        base = typ * 1024 + (2 * j + hl) * 128
                        perm.extend(range(base, base + 128))
            wo = np.ascontiguousarray(wo_src[np.asarray(perm)])
        else:
            i = l // 2
            w = f32(gla_w_in[i])
            ims_g = []
            for (b, j) in cores:
                wg = np.concatenate([w[:, j * 256:(j + 1) * 256], w[:, 1024 + j * 256:1024 + (j + 1) * 256],
                                     w[:, 6144:6160], w[:, 2048 + j * 512:2048 + (j + 1) * 512],
                                     w[:, 4096 + j * 512:4096 + (j + 1) * 512]], axis=1)
                ims_g.append({"consts": consts, "hT": hfull[b], "w": np.ascontiguousarray(wg),
                              "wgu": f32(gla_w_gate_up[i][:, j * 256:(j + 1) * 256]),
                              "bg": f32(np.asarray(gla_b_gate[i][j * 256:(j + 1) * 256]).reshape(2, 128).T),
                              "gain": f32(gla_out_gain[i]).reshape(1, 512)})
            rg = _run(_prog("gla", lambda: build_gla(cfg)), ims_g)
            ofull = [np.concatenate([rg[b * 4 + j]["oT"] for j in range(4)], axis=0) for b in range(2)]
            wo = f32(gla_w_out[i])
        last = (l == L - 1)
        nc = _prog(("dense", last), lambda: build_dense(cfg, last))
        wfi, wfo = f32(ffn_w_in[l]), f32(ffn_w_out[l])
        ims = []
        for ci, (b, j) in enumerate(cores):
            v2 = np.zeros((128, 2, 6 * KD), np.float32)
            v2[:, 0] = vec[b][:, l]
            if not last:
                v2[:, 1] = vec[b][:, l + 1]
            ims.append({"consts": consts, "xT": xT[ci], "oT": np.ascontiguousarray(ofull[b][:, j * T:(j + 1) * T]),
                        "wo": wo, "wfi": wfi, "wfo": wfo, "vec": v2})
        r = _run(nc, ims)
        xT = [r[ci]["xTo"] for ci in range(8)]
        if not last:
            hT = [r[ci]["hTo"] for ci in range(8)]

    out = np.empty((2, S, D), np.float32)
    for ci, (b, j) in enumerate(cores):
        out[b, j * T:(j + 1) * T, :] = xT[ci].T
    return out


GROUPS = [[0, 1, 2, 3], [4, 5, 6, 7]]
PRECAST = False


def build_fused(cfg):
    L, D, KD, T, S, DFF, KO = cfg["L"], cfg["D"], cfg["KD"], cfg["T"], cfg["S"], cfg["DFF"], cfg["KO"]
    NT = T // 512
    NQ = D // 256
    nc = bass.Bass("TRN2", target_bir_lowering=False)
    gstack = contextlib.ExitStack()
    K = Kern(nc, gstack)
    cst = _din(nc, "consts", [128, 640])
    cT = _din(nc, "cT", [128, KD])
    modw = _din(nc, "modw", [1, D, 6 * D])
    modb = _din(nc, "modb", [1, 6 * D])
    gmix = _din(nc, "gmix", [128, 1, KD])
    gffn = _din(nc, "gffn", [128, 1, KD])
    xT_in = _din(nc, "xT", [D, T])
    wfi = _din(nc, "wfi", [L, D, 2 * DFF])
    wfo = _din(nc, "wfo", [L, DFF, D])
    wo = [_din(nc, "wo%d" % l, [KO * 128, D]) for l in range(L)]
    ab, gl = {}, {}
    for i in range((L + 1) // 2):
        ab[i] = dict(wh=_din(nc, "wh%d" % i, [D, 1024]), lbl=_din(nc, "lbl%d" % i, [128, 2, 2]),
                     again=_din(nc, "again%d" % i, [1, 256]), wf=_din(nc, "wf%d" % i, [D, 1026]),
                     qkg=_din(nc, "qkg%d" % i, [128, 2]), fbias=_din(nc, "fbias%d" % i, [1, 2]))
    for i in range(L // 2):
        gl[i] = dict(wg=_din(nc, "wg%d" % i, [D, 1552]), wgu=_din(nc, "wgu%d" % i, [16, 256]),
                     bg=_din(nc, "bg%d" % i, [128, 2]), ggain=_din(nc, "ggain%d" % i, [1, 512]))
    xTo = _dout(nc, "xTo", [D, T])
    vin = nc.dram_tensor("i_vin", [128, 6 * KD], F32)
    vall = nc.dram_tensor("i_vall", [4 * 128, 6 * KD], F32)
    xs = nc.dram_tensor("i_xs", [D, T], F32)
    HK = KD // 2
    hq = nc.dram_tensor("i_hq", [NT * 2, HK * 128, 512], BF16)
    hf = nc.dram_tensor("i_hf", [NT * 2, 4 * HK * 128, 512], BF16)
    OC = 2 if NT % 2 == 0 else 1
    NOC = (S // 512) // OC
    oqs = {nm: nc.dram_tensor("i_oq" + nm, [NOC, rows, OC * 512], BF16) for nm, rows in (("h", 256), ("f", 256), ("g", 512))}
    ofs = {nm: nc.dram_tensor("i_of" + nm, [NOC, 4 * rows, OC * 512], BF16)
           for nm, rows in (("h", 256), ("f", 256), ("g", 512))}
    wbo = nc.dram_tensor("i_wbo", [KO * 128, D], BF16)
    wbi = nc.dram_tensor("i_wbi", [D, 2 * DFF], BF16)
    wbf = nc.dram_tensor("i_wbf", [DFF, D], BF16)
    b_cv = [Buf() for _ in range(4)]
    conv = {"q": [], "per": 1, "n": 0}

    def conv_plan(l, n_calls):
        q = []
        if not PRECAST:
            conv["q"] = q
            return
        for (src, dst, rows, step) in ((wo[l], wbo.ap(), KO * 128, 512), (wfi[l], wbi.ap(), D, 128),
                                       (wfo[l], wbf.ap(), DFF, 512)):
            for r0 in range(0, rows, step):
                r1 = min(rows, r0 + step)
                q.append((src[r0:r1, :], dst[r0:r1, :]))
        conv["q"] = q
        conv["per"] = -(-len(q) // n_calls)

    def conv_step(n=None):
        n = conv["per"] if n is None else n
        for _ in range(min(n, len(conv["q"]))):
            src, dst = conv["q"].pop(0)
            key = b_cv[conv["n"] % 4]
            conv["n"] += 1
            K.dma(K.pool, lambda e, src=src, dst=dst: e.dma_start(out=dst, in_=src), [], [], key=key)

    b_vin, b_vall = Buf(), Buf()
    b_hq = [Buf() for _ in range(NT * 2)]
    b_hf = [Buf() for _ in range(NT * 2)]
    b_oq = {nm: [Buf() for _ in range(NOC)] for nm in "hfg"}
    b_of = {nm: [Buf() for _ in range(NOC)] for nm in "hfg"}
    vview = vall.ap().rearrange("(l p) c -> p l c", p=128)
    pending = []

    def flush_colls():
        while pending:
            pending.pop(0)()

    def h_store(t, tile, buf):
        for half in range(2):
            i = t * 2 + half
            dst = hq.ap()[i].rearrange("(k p) t -> p k t", p=128)
            K.dma(K.sp, lambda e, dst=dst, half=half: e.dma_start(out=dst, in_=tile[:, half * HK:(half + 1) * HK, :]),
                  [buf], [b_hq[i]], key=buf)
            pending.append(lambda i=i: K.coll("AllGather", hq.ap()[i], hf.ap()[i], [b_hq[i]], [b_hf[i]], GROUPS))

    def h_load(t, dst, buf):
        r, tl = t // NT, t % NT
        for half in range(2):
            i = tl * 2 + half
            src = hf.ap()[i].rearrange("(r k p) t -> p r k t", r=4, p=128)[:, r]
            K.dma(K.sp, lambda e, src=src, half=half: e.dma_start(out=dst[:, half * HK:(half + 1) * HK, :], in_=src),
                  [b_hf[i]], [buf], key=buf)

    def make_o_store(nm):
        def o_store(t, src, buf, nh):
            ci, cl = t // OC, t % OC
            dst = oqs[nm].ap()[ci].rearrange("(h p) t -> p h t", p=128)[:, :, cl * 512:(cl + 1) * 512]
            K.dma(K.pool, lambda e: e.dma_start(out=dst, in_=src), [buf], [b_oq[nm][ci]], key=buf)
            if cl == OC - 1:
                K.coll("AllGather", oqs[nm].ap()[ci], ofs[nm].ap()[ci], [b_oq[nm][ci]], [b_of[nm][ci]], GROUPS)
            conv_step()
        return o_store

    seg_cache = {}

    def make_o_load(names):
        def o_load(tt, oT, buf):
            k0 = 0
            for nm in names:
                rows = 4 * (256 if nm in "hf" else 512)
                nk = rows // 128

                def fn(e, nm=nm, nk=nk, k0=k0):
                    if "segC" not in seg_cache:
                        seg_cache["segC"] = (e.partition_id() % 4) * (NT // OC)
                    ci = seg_cache["segC"] + tt // OC
                    src = ofs[nm].ap().rearrange("c (k p) t -> p c k t", p=128)[
                        :, bass.ds(ci, 1), :, (tt % OC) * 512:(tt % OC + 1) * 512]
                    return e.dma_start(out=oT[:, k0:k0 + nk, :].rearrange("p (o k) t -> p o k t", o=1), in_=src)
                K.dma(K.sp, fn, [b for b in b_of[nm]], [buf], key=buf)
                k0 += nk
        return o_load

    C = Consts(K, cst, None)
    cfg1 = dict(cfg)
    cfg1["L"] = 1
    K.begin_phase()
    emit_mod(K, C, cfg1, cT, modw, modb, gmix, gffn, vin.ap().rearrange("p (l c) -> p l c", l=1), out_buf=b_vin)
    K.coll("AllGather", vin, vall, [b_vin], [b_vall], GROUPS)
    K.end_phase(last=True)
    K.begin_phase()
    emit_norm0(K, C, cfg, xT_in, vview, None, h_store=h_store, after_tile=flush_colls)
    flush_colls()
    K.end_phase()
    for l in range(L):
        i = l // 2
        if l % 2 == 0:
            conv_plan(l, 2 * (S // 512))
            K.begin_phase()
            emit_hgrn(K, C, cfg, i, None, ab[i]["wh"], ab[i]["lbl"], ab[i]["again"], None, h_load=h_load,
                      o_store=make_o_store("h"))
            K.end_phase()
            K.begin_phase()
            emit_fox(K, C, cfg, None, ab[i]["wf"], ab[i]["qkg"], ab[i]["fbias"], None, h_load=h_load,
                     o_store=make_o_store("f"))
            conv_step(len(conv["q"]))
            K.end_phase()
            names = "hf"
        else:
            conv_plan(l, S // 512)
            K.begin_phase()
            emit_gla(K, C, cfg, None, gl[i]["wg"], gl[i]["wgu"], gl[i]["bg"], gl[i]["ggain"], None, h_load=h_load,
                     o_store=make_o_store("g"))
            conv_step(len(conv["q"]))
            K.end_phase()
            names = "g"
        last = (l == L - 1)
        K.begin_phase()
        dw = (wbo.ap(), wbi.ap(), wbf.ap()) if PRECAST else (wo[l], wfi[l], wfo[l])
        emit_dense(K, C, cfg, l, xT_in if l == 0 else xs.ap(), None, dw[0], dw[1], dw[2], vview,
                   xTo if last else xs.ap(), None, o_load=make_o_load(names), h_store=None if last else h_store,
                   after_proj=flush_colls)
        flush_colls()
        K.end_phase(last=last)
    gstack.close()
    return nc


def dense_k_perm(kind):
    perm = []
    if kind == "ab":
        for typ in range(2):
            for r in range(4):
                for loc in range(256):
                    perm.append(r * 512 + typ * 256 + loc)
    else:
        perm = list(range(2048))
    return np.asarray(perm)


def kernel_fused(cfg, x, c, mod_w, mod_b, norm_mix_gain, norm_ffn_gain, ab_w_in, ab_w_out, hgrn_lb_logits,
                 hgrn_out_gain, fox_q_gain, fox_k_gain, fox_f_bias, gla_w_in, gla_w_gate_up, gla_b_gate, gla_out_gain,
                 gla_w_out, ffn_w_in, ffn_w_out):
    L, D, KD, T, S, DFF = cfg["L"], cfg["D"], cfg["KD"], cfg["T"], cfg["S"], cfg["DFF"]
    f32 = lambda a: np.ascontiguousarray(np.asarray(a, dtype=np.float32))
    x, c, mod_w, mod_b = f32(x), f32(c), f32(mod_w), f32(mod_b)
    consts = make_consts()
    ab_rows = []
    for j in range(4):
        for typ in range(2):
            for hl in range(2):
                base = typ * 1024 + (2 * j + hl) * 128
                ab_rows.extend(range(base, base + 128))
    ab_rows = np.asarray(ab_rows)
    wfi, wfo = f32(ffn_w_in), f32(ffn_w_out)
    shared = {"consts": consts, "wfi": wfi, "wfo": wfo}
    for l in range(L):
        i = l // 2
        if l % 2 == 0:
            shared["wo%d" % l] = np.ascontiguousarray(f32(ab_w_out[i])[ab_rows[dense_k_perm("ab")]])
        else:
            shared["wo%d" % l] = np.ascontiguousarray(f32(gla_w_out[i])[dense_k_perm("gla")])
    lbl_all = f32(hgrn_lb_logits)
    ims = []
    for cid in range(8):
        b, j = cid // 4, cid % 4
        im = dict(shared)
        im["cT"] = _fm(c[b], KD)
        im["modw"] = mod_w[j:j + 1]
        im["modb"] = mod_b[j:j + 1]
        im["gmix"] = _fm(f32(norm_mix_gain)[j:j + 1], KD)
        im["gffn"] = _fm(f32(norm_ffn_gain)[j:j + 1], KD)
        im["xT"] = np.ascontiguousarray(x[b, j * T:(j + 1) * T, :].T)
        hh = (2 * j, 2 * j + 1)
        for i in range((L + 1) // 2):
            w = f32(ab_w_in[i])
            cs = lambda base, h_: w[:, base + h_ * 128:base + (h_ + 1) * 128]
            im["wh%d" % i] = np.ascontiguousarray(np.concatenate(
                [cs(0, hh[0]), cs(1024, hh[0]), cs(0, hh[1]), cs(1024, hh[1]),
                 cs(2048, hh[0]), cs(2048, hh[1]), cs(3072, hh[0]), cs(3072, hh[1])], axis=1))
            im["lbl%d" % i] = f32(np.stack([np.stack([lbl_all[ly, h_ * 128:(h_ + 1) * 128] for h_ in hh], axis=1)
                                            for ly in range(2)], axis=1))
            im["again%d" % i] = f32(hgrn_out_gain[i][2 * j:2 * j + 2]).reshape(1, 256)
            im["wf%d" % i] = np.ascontiguousarray(np.concatenate(
                [cs(4096, hh[0]), cs(5120, hh[0]), cs(4096, hh[1]), cs(5120, hh[1]),
                 cs(6144, hh[0]), cs(6144, hh[1]), cs(7168, hh[0]), cs(7168, hh[1]),
                 w[:, 8192 + hh[0]:8192 + hh[0] + 1], w[:, 8192 + hh[1]:8192 + hh[1] + 1]], axis=1))
            im["qkg%d" % i] = f32(np.stack([fox_q_gain[i], fox_k_gain[i]], axis=1))
            im["fbias%d" % i] = f32(fox_f_bias[i][2 * j:2 * j + 2]).reshape(1, 2)
        for i in range(L // 2):
            w = f32(gla_w_in[i])
            im["wg%d" % i] = np.ascontiguousarray(np.concatenate(
                [w[:, j * 256:(j + 1) * 256], w[:, 1024 + j * 256:1024 + (j + 1) * 256], w[:, 6144:6160],
                 w[:, 2048 + j * 512:2048 + (j + 1) * 512], w[:, 4096 + j * 512:4096 + (j + 1) * 512]], axis=1))
            im["wgu%d" % i] = f32(gla_w_gate_up[i][:, j * 256:(j + 1) * 256])
            im["bg%d" % i] = f32(np.asarray(gla_b_gate[i][j * 256:(j + 1) * 256]).reshape(2, 128).T)
            im["ggain%d" % i] = f32(gla_out_gain[i]).reshape(1, 512)
        ims.append(im)
    nc = _prog("fused", lambda: build_fused(cfg))
    r = _run(nc, ims)
    out = np.empty((2, S, D), np.float32)
    for cid in range(8):
        b, j = cid // 4, cid % 4
        out[b, j * T:(j + 1) * T, :] = r[cid]["xTo"].T
    return out


def kernel(**inputs):
    return kernel_fused(CFG, **inputs)
```

```python
import contextlib
import numpy as np
import ml_dtypes
import concourse.bass as bass
import concourse.mybir as mybir
from concourse.bass_utils import run_bass_kernel_spmd

F32 = mybir.dt.float32
BF16 = mybir.dt.bfloat16
AF = mybir.ActivationFunctionType
ALU = mybir.AluOpType
EPS = 1e-6


class Sem:
    def __init__(self, h):
        self.h = h
        self.n = 0


class Buf:
    __slots__ = ("w", "r", "ds")

    def __init__(self):
        self.w = None
        self.r = {}
        self.ds = None


class Eng:
    def __init__(self, name, sem, is_pe=False):
        self.name = name
        self.sem = sem
        self.prog = []
        self.seen = {}
        self.is_pe = is_pe


class Kern:
    def __init__(self, nc, stack, n_dma_sems=12):
        self.nc = nc
        self.stack = stack
        mk = lambda n: Sem(stack.enter_context(nc.semaphore(n)))
        self.pe = Eng("tensor", mk("s_pe"), True)
        self.act = Eng("scalar", mk("s_act"))
        self.dve = Eng("vector", mk("s_dve"))
        self.pool = Eng("gpsimd", mk("s_pool"))
        self.sp = Eng("sync", mk("s_sp"))
        self.engs = [self.pe, self.act, self.dve, self.pool, self.sp]
        self.dsems = [None] * n_dma_sems
        self.nbuf = 0
        self.ccs = [mk("s_cc%d" % i) for i in range(4)]
        self.dsems.extend(self.ccs)
        self.ncoll = 0
        self.free_sems = []
        self.phase_sems = []
        self.pstack = None
        self.nphase = 0

    def sb(self, name, shape, dt):
        st = self.pstack if self.pstack is not None else self.stack
        return st.enter_context(self.nc.sbuf_tensor("%s_%d" % (name, self.nphase), list(shape), dt))

    def ps(self, name, shape, dt):
        st = self.pstack if self.pstack is not None else self.stack
        return st.enter_context(self.nc.psum_tensor("%s_%d" % (name, self.nphase), list(shape), dt))

    def begin_phase(self):
        self.pstack = contextlib.ExitStack()
        self.nphase += 1

    def end_phase(self, last=False):
        for e in self.engs:
            self.final_wait(e, skip_cc=not last)
        self.emit()
        for e in self.engs:
            e.prog = []
        self.free_sems.extend(self.phase_sems)
        self.phase_sems = []
        self.pstack.close()
        self.pstack = None

    def coll(self, kind, in_t, out_t, R, W, groups):
        eng = self.pool
        cc = self.ccs[self.ncoll % len(self.ccs)]
        self.ncoll += 1
        waits = self._waits(eng, R, W)
        if cc.n > 0 and eng.seen.get(cc, 0) < cc.n:
            eng.seen[cc] = cc.n
            waits.append((cc, cc.n))
        cc.n += 1
        ev = (cc, cc.n)
        in_ap = in_t if isinstance(in_t, bass.AP) else in_t.ap()
        out_ap = out_t if isinstance(out_t, bass.AP) else out_t.ap()
        fn = lambda e: e.collective_compute(kind, ALU.bypass, replica_groups=groups, ins=[in_ap.opt()],
                                            outs=[out_ap.opt()])
        eng.prog.append((waits, fn, (cc, 1)))
        self._commit(ev, R, W)

    def _waits(self, eng, R, W):
        needs = {}

        def need(ev):
            s, v = ev
            if needs.get(s, 0) < v:
                needs[s] = v

        for b in R:
            if b.w is not None:
                if b.w[0] is eng.sem and eng.is_pe:
                    continue
                need(b.w)
        for b in W:
            if b.w is not None and b.w[0] is not eng.sem:
                need(b.w)
            for s, v in b.r.items():
                if s is not eng.sem:
                    need((s, v))
        out = []
        for s, v in needs.items():
            if eng.seen.get(s, 0) < v:
                eng.seen[s] = v
                out.append((s, v))
        return out

    def _commit(self, ev, R, W):
        for b in R:
            if b.r.get(ev[0], 0) < ev[1]:
                b.r[ev[0]] = ev[1]
        for b in W:
            b.w = ev
            b.r = {}

    def op(self, eng, fn, R=(), W=(), inc=True):
        waits = self._waits(eng, R, W)
        if inc:
            eng.sem.n += 1
            ev = (eng.sem, eng.sem.n)
        else:
            ev = (eng.sem, eng.sem.n + 1)
        eng.prog.append((waits, fn, (eng.sem, 1) if inc else None))
        self._commit(ev, R, W)

    def dma(self, eng, fn, R, W, dsem=None, key=None):
        if key is None:
            key = W[0] if W else R[0]
        if key.ds is None:
            if self.free_sems:
                key.ds = self.free_sems.pop()
            else:
                key.ds = Sem(self.stack.enter_context(self.nc.semaphore("s_d%d" % self.nbuf)))
                self.nbuf += 1
                self.dsems.append(key.ds)
            if self.pstack is not None:
                self.phase_sems.append(key.ds)
        dsem = key.ds
        waits = self._waits(eng, R, W)
        if dsem.n > 0 and eng.seen.get(dsem, 0) < dsem.n:
            eng.seen[dsem] = dsem.n
            waits.append((dsem, dsem.n))
        dsem.n += 16
        ev = (dsem, dsem.n)
        eng.prog.append((waits, fn, (dsem, 16)))
        self._commit(ev, R, W)

    def final_wait(self, eng, skip_cc=False):
        waits = []
        for s in [e.sem for e in self.engs] + [d for d in self.dsems if d is not None]:
            if skip_cc and s in self.ccs:
                continue
            if s is not eng.sem and s.n > 0 and eng.seen.get(s, 0) < s.n:
                eng.seen[s] = s.n
                waits.append((s, s.n))
        eng.prog.append((waits, None, None))

    def emit(self):
        nc = self.nc

        def replay(eng):
            def run(e):
                for waits, fn, inc in eng.prog:
                    for s, v in waits:
                        e.wait_ge(s.h, v)
                    if fn is not None:
                        ins = fn(e)
                        if inc is not None:
                            ins.then_inc(inc[0].h, inc[1])
            return run

        with nc.Block() as block:
            block.tensor(replay(self.pe))
            block.scalar(replay(self.act))
            block.vector(replay(self.dve))
            block.gpsimd(replay(self.pool))
            block.sync(replay(self.sp))


def _chunks(n, m):
    return [(i, min(m, n - i)) for i in range(0, n, m)]


class Consts:
    def __init__(self, K, cdram, dsem):
        nc = K.nc
        self.f = K.sb("c_f32", [128, 5 * 128], F32)
        self.b = K.sb("c_bf16", [128, 5 * 128], BF16)
        self.buf = Buf()
        f, b = self.f, self.b
        K.dma(K.sp, lambda e: e.dma_start(out=f[:], in_=cdram[:, :]), [], [self.buf], dsem)
        K.op(K.dve, lambda e: e.tensor_copy(out=b[:], in_=f[:]), [self.buf], [self.buf])

    def ident_b(self, n=128):
        return self.b[0:n, 0:n]

    def tri_b(self, n=128):
        return self.b[0:n, 128:128 + n]

    def tri_f(self, n=128):
        return self.f[0:n, 128:128 + n]

    def ones_f(self, p=128, n=128):
        return self.f[0:p, 256:256 + n]

    def ones_b(self, p=128, n=128):
        return self.b[0:p, 256:256 + n]

    def sel_f(self):
        return self.f[:, 384:512]

    def ident_f(self):
        return self.f[:, 0:128]

    def negmask_b(self):
        return self.b[:, 512:640]


def make_consts():
    c = np.zeros((128, 5 * 128), np.float32)
    c[:, 0:128] = np.eye(128)
    c[:, 128:256] = np.triu(np.ones((128, 128)))
    c[:, 256:384] = 1.0
    c[127, 384:512] = 1.0
    c[:, 512:640] = -30000.0 * np.tril(np.ones((128, 128)), -1)
    return c


def emit_norm(K, C, xT, xbuf, KD, D, G, Sh, vbuf, outT, obuf, ssbank, ssb, tmp):
    sq, sqb, rs, rsb, tm, tmb = tmp["sq"], tmp["sqb"], tmp["rs"], tmp["rsb"], tmp["tm"], tmp["tmb"]
    for j in range(KD):
        s = j % 2
        K.op(K.act, lambda e, j=j, s=s: e.activation(out=sq[:, s, :], in_=xT[:, j, :], func=AF.Square),
             [xbuf], [sqb[s]])
        K.op(K.pe, lambda e, j=j, s=s: e.matmul(ssbank[:, :], lhsT=C.ones_f(), rhs=sq[:, s, :],
                                               start=(j == 0), stop=(j == KD - 1)),
             [sqb[s], C.buf], [ssb], inc=True)
    K.op(K.act, lambda e: e.activation(out=rs[:, :], in_=ssbank[:, :], func=AF.Sqrt, scale=1.0 / D, bias=EPS),
         [ssb], [rsb])
    K.op(K.dve, lambda e: e.reciprocal(out=rs[:, :], in_=rs[:, :]), [rsb], [rsb])
    for j in range(KD):
        s = j % 2
        K.op(K.dve, lambda e, j=j, s=s: e.scalar_tensor_tensor(out=tm[:, s, :], in0=xT[:, j, :], scalar=G[:, j:j + 1],
                                                             in1=rs[:, :], op0=ALU.mult, op1=ALU.mult),
             [xbuf, rsb, vbuf], [tmb[s]])
        K.op(K.act, lambda e, j=j, s=s: e.activation(out=outT[:, j, :], in_=tm[:, s, :], func=AF.Identity,
                                                   bias=Sh[:, j:j + 1], scale=1.0),
             [tmb[s], vbuf], [obuf])


def emit_mod(K, C, cfg, cT_d, modw_d, modb_d, gmix_d, gffn_d, vec_d, out_buf=None):
    L, D, KD = cfg["L"], cfg["D"], cfg["KD"]
    W6 = 6 * D
    CG = min(2048, W6)
    NB = CG // 512
    assert W6 % CG == 0
    ds_a, ds_w = K.dsems[0], K.dsems[1]
    cT = K.sb("m_cT", [128, KD], F32)
    gm = K.sb("m_gm", [128, L, KD], F32)
    gf = K.sb("m_gf", [128, L, KD], F32)
    row = K.sb("m_row", [1, W6], F32)
    brow = K.sb("m_brow", [1, W6], F32)
    wp = K.sb("m_wp", [128, 3, CG], F32)
    mt = K.sb("m_mt", [128, 6 * KD], F32)
    vec = K.sb("m_vec", [128, L, 6 * KD], F32)
    banks = [K.ps("m_ps%d" % i, [128, 512], F32) for i in range(NB)]
    tp = K.ps("m_tp", [128, 512], F32)
    b_c, b_g, b_row, b_brow, b_mt, b_vec, b_tp = Buf(), Buf(), Buf(), Buf(), Buf(), Buf(), Buf()
    b_wp = [Buf() for _ in range(3)]
    b_bk = [Buf() for _ in range(NB)]
    K.dma(K.sp, lambda e: e.dma_start(out=cT[:], in_=cT_d[:, :]), [], [b_c], ds_a)
    K.dma(K.sp, lambda e: e.dma_start(out=gm[:], in_=gmix_d[:, :, :]), [], [b_g], ds_a)
    K.dma(K.sp, lambda e: e.dma_start(out=gf[:], in_=gffn_d[:, :, :]), [], [b_g], ds_a)
    K.op(K.act, lambda e: e.activation(out=cT[:], in_=cT[:], func=AF.Silu), [b_c], [b_c])
    ip = 0
    for l in range(L):
        K.dma(K.sp, lambda e, l=l: e.dma_start(out=brow[:], in_=modb_d[l:l + 1, :]), [], [b_brow], ds_a)
        for cg in range(W6 // CG):
            for k in range(KD):
                s = ip % 3
                ip += 1
                K.dma(K.sp, lambda e, l=l, cg=cg, k=k, s=s: e.dma_start(
                    out=wp[:, s, :], in_=modw_d[l, k * 128:(k + 1) * 128, cg * CG:(cg + 1) * CG]),
                    [], [b_wp[s]], ds_w)
                for q in range(NB):
                    K.op(K.pe, lambda e, k=k, s=s, q=q: e.matmul(
                        banks[q][0:1, :], lhsT=cT[:, k:k + 1], rhs=wp[:, s, q * 512:(q + 1) * 512],
                        start=(k == 0), stop=(k == KD - 1)), [b_c, b_wp[s]], [b_bk[q]])
            for q in range(NB):
                c0 = cg * CG + q * 512
                K.op(K.dve, lambda e, q=q, c0=c0: e.tensor_tensor(
                    out=row[0:1, c0:c0 + 512], in0=banks[q][0:1, :], in1=brow[0:1, c0:c0 + 512], op=ALU.add),
                    [b_bk[q], b_brow], [b_row])
        for c in range(6 * KD):
            K.op(K.pe, lambda e, c=c: e.matmul(tp[:, c:c + 1], lhsT=row[0:1, c * 128:(c + 1) * 128],
                                               rhs=C.ones_f(1, 1), start=True, stop=True),
                 [b_row, C.buf], [b_tp])
        K.op(K.act, lambda e: e.activation(out=mt[:, :], in_=tp[:, 0:6 * KD], func=AF.Identity), [b_tp], [b_mt])
        for (dst, src_sc, gn) in ((0, 1, gm), (3, 4, gf)):
            K.op(K.dve, lambda e, l=l, dst=dst, src_sc=src_sc, gn=gn: e.scalar_tensor_tensor(
                out=vec[:, l, dst * KD:(dst + 1) * KD], in0=mt[:, src_sc * KD:(src_sc + 1) * KD], scalar=1.0,
                in1=gn[:, l, :], op0=ALU.add, op1=ALU.mult), [b_mt, b_g], [b_vec])
        for (dst, src) in ((1, 0), (2, 2), (4, 3), (5, 5)):
            K.op(K.dve, lambda e, l=l, dst=dst, src=src: e.tensor_copy(
                out=vec[:, l, dst * KD:(dst + 1) * KD], in_=mt[:, src * KD:(src + 1) * KD]), [b_mt], [b_vec])
    K.dma(K.sp, lambda e: e.dma_start(out=vec_d[:, :, :], in_=vec[:]), [b_vec], [out_buf] if out_buf else [], key=b_vec)


def norm_scratch(K, pfx):
    return {
        "sq": K.sb(pfx + "sq", [128, 2, 512], F32), "sqb": [Buf(), Buf()],
        "rs": K.sb(pfx + "rs", [128, 512], F32), "rsb": Buf(),
        "tm": K.sb(pfx + "tm", [128, 2, 512], F32), "tmb": [Buf(), Buf()],
    }


def emit_norm0(K, C, cfg, xT_d, vec_d, hT_d, h_store=None, after_tile=None):
    D, KD, T = cfg["D"], cfg["KD"], cfg["T"]
    ds_a = K.dsems[0]
    vec = K.sb("n_vec", [128, 6 * KD], F32)
    xT = K.sb("n_xT", [128, 2, KD, 512], F32)
    hT = K.sb("n_hT", [128, 2, KD, 512], BF16)
    ss = K.ps("n_ss", [128, 512], F32)
    b_v, b_ss = Buf(), Buf()
    b_x, b_h = [Buf(), Buf()], [Buf(), Buf()]
    tmp = norm_scratch(K, "n_")
    K.dma(K.sp, lambda e: e.dma_start(out=vec[:], in_=vec_d[:, 0, :]), [], [b_v], ds_a)
    xv = xT_d.rearrange("(k p) t -> p k t", p=128)
    hv = None if h_store is not None else hT_d.rearrange("(k p) t -> p k t", p=128)
    def xload(t):
        K.dma(K.sp, lambda e, t=t: e.dma_start(out=xT[:, t % 2], in_=xv[:, :, t * 512:(t + 1) * 512]),
              [], [b_x[t % 2]], ds_a)

    xload(0)
    for t in range(T // 512):
        s = t % 2
        if t + 1 < T // 512:
            xload(t + 1)
        emit_norm(K, C, xT[:, s], b_x[s], KD, D, vec[:, 0:KD], vec[:, KD:2 * KD], b_v, hT[:, s], b_h[s], ss, b_ss, tmp)
        if h_store is not None:
            h_store(t, hT[:, s], b_h[s])
            if after_tile is not None:
                after_tile()
        else:
            K.dma(K.sp, lambda e, t=t, s=s: e.dma_start(out=hv[:, :, t * 512:(t + 1) * 512], in_=hT[:, s]),
                  [b_h[s]], [], ds_a)


def emit_dense(K, C, cfg, l, xT_d, oT_d, wo_d, wfi_d, wfo_d, vec_d, xTo_d, hTo_d, o_load=None, h_store=None,
               after_proj=None, at_start=None):
    D, KD, T, DFF, KF, KO, L = cfg["D"], cfg["KD"], cfg["T"], cfg["DFF"], cfg["KF"], cfg["KO"], cfg["L"]
    DG = D // 512 if D >= 512 else 1
    GW = min(512, D)
    GC = GW // 128
    FG = DFF // 512
    assert DFF % 512 == 0
    ds_a, ds_w, ds_s = K.dsems[0], K.dsems[1], K.dsems[2]
    last = hTo_d is None and h_store is None
    vec = K.sb("d_vec", [128, 2, 6 * KD], F32)
    xin = K.sb("d_xin", [128, KD, 512], F32)
    xT = K.sb("d_xT", [128, KD, 512], F32)
    oT2 = K.sb("d_oT", [128, 1, KO, 512], BF16)
    oT = K.sb("d_hT", [128, KD, 512], BF16)
    halves = [list(range(0, (FG + 1) // 2)), list(range((FG + 1) // 2, FG))]
    AH = 4 * len(halves[0])
    aT = K.sb("d_aT", [128, AH, 512], BF16)
    sa = K.sb("d_sa", [128, 2, 512], BF16)
    WB = 4
    WSZ = max(KO * GW, KD * 512, 16 * GW)
    wb = K.sb("d_wb", [128, WB, WSZ], BF16)
    banks = [K.ps("d_ps%d" % i, [128, 512], F32) for i in range(8)]
    b_bk = [Buf() for _ in range(8)]
    b_v, b_x, b_o, b_a, b_xin = Buf(), Buf(), Buf(), Buf(), Buf()
    b_o2 = [Buf(), Buf()]
    b_sa = [Buf(), Buf()]
    b_wb = [Buf() for _ in range(WB)]
    tmp = norm_scratch(K, "d_")
    K.dma(K.sp, lambda e: e.dma_start(out=vec[:, 0, :], in_=vec_d[:, l, :]), [], [b_v], ds_a)
    if not last:
        K.dma(K.sp, lambda e: e.dma_start(out=vec[:, 1, :], in_=vec_d[:, l + 1, :]), [], [b_v], ds_a)
    g1 = vec[:, 0, 2 * KD:3 * KD]
    G2 = vec[:, 0, 3 * KD:4 * KD]
    Sh2 = vec[:, 0, 4 * KD:5 * KD]
    g2 = vec[:, 0, 5 * KD:6 * KD]
    G1n = vec[:, 1, 0:KD]
    Sh1n = vec[:, 1, KD:2 * KD]
    xv = xT_d.rearrange("(k p) t -> p k t", p=128)
    ov = None if o_load is not None else oT_d.rearrange("(k p) t -> p k t", p=128)
    xov = xTo_d.rearrange("(k p) t -> p k t", p=128)
    hov = None if (last or h_store is not None) else hTo_d.rearrange("(k p) t -> p k t", p=128)
    wov = wo_d.rearrange("(k p) c -> p k c", p=128)
    wiv = wfi_d.rearrange("(k p) c -> p k c", p=128)
    wfv = wfo_d.rearrange("(k p) c -> p k c", p=128)
    st = {"w": 0, "bank": 0}

    def load_w(src_ap, kk, cc):
        s = st["w"] % WB
        st["w"] += 1
        dst = wb[:, s, 0:kk * cc].rearrange("p (k c) -> p k c", k=kk)
        K.dma(K.pool, lambda e: e.dma_start(out=dst, in_=src_ap), [], [b_wb[s]], ds_w)
        return dst, b_wb[s]

    def proj_group(pieces, rhsT, rbuf, nk, og, gvec, src=None, sbuf=None, kofs=0):
        if src is None:
            src, sbuf = xT, b_x
        base = (st["bank"] % 2) * 4
        st["bank"] += 1
        kdone = 0
        for (wt, wbuf, k0, kn) in pieces:
            for kk in range(kn):
                for i in range(GC):
                    K.op(K.pe, lambda e, wt=wt, kk=kk, i=i, k=k0 + kk - kofs: e.matmul(
                        banks[base + i][:, :], lhsT=wt[:, kk, i * 128:(i + 1) * 128], rhs=rhsT[:, k, :],
                        start=(k == 0), stop=(k == nk - 1)), [wbuf, rbuf], [b_bk[base + i]],
                        inc=(k0 + kk - kofs == nk - 1 or kk == kn - 1))
        for i in range(GC):
            j = og * GC + i
            K.op(K.dve, lambda e, i=i, j=j: e.scalar_tensor_tensor(
                out=xT[:, j, :], in0=banks[base + i][:, :], scalar=gvec[:, j:j + 1], in1=src[:, j, :],
                op0=ALU.mult, op1=ALU.add), [b_bk[base + i], sbuf, b_v], [b_x])

    NTL = T // 512

    def loads(t):
        tsl = slice(t * 512, (t + 1) * 512)
        K.dma(K.sp, lambda e: e.dma_start(out=xin[:], in_=xv[:, :, tsl]), [], [b_xin])
        if o_load is not None:
            o_load(t, oT2[:, 0], b_o2[0])
        else:
            K.dma(K.sp, lambda e: e.dma_start(out=oT2[:, 0], in_=ov[:, :, tsl]), [], [b_o2[0]])

    first_w = load_w(wov[:, :, 0:GW], KO, GW)
    if at_start is not None:
        at_start()
    loads(0)
    for t in range(NTL):
        tsl = slice(t * 512, (t + 1) * 512)
        for og in range(DG):
            if t == 0 and og == 0:
                wt, wbuf = first_w
            else:
                wt, wbuf = load_w(wov[:, :, og * GW:(og + 1) * GW], KO, GW)
            proj_group([(wt, wbuf, 0, KO)], oT2[:, 0], b_o2[0], KO, og, g1, src=xin, sbuf=b_xin)
        if t + 1 < NTL:
            loads(t + 1)
        emit_norm(K, C, xT, b_x, KD, D, G2, Sh2, b_v, oT, b_o, banks[7], b_bk[7], tmp)
        for hg in halves:
            if not hg:
                continue
            for fg in hg:
                wa, wab = load_w(wiv[:, :, fg * 512:(fg + 1) * 512], KD, 512)
                wu, wub = load_w(wiv[:, :, DFF + fg * 512:DFF + (fg + 1) * 512], KD, 512)
                for i in range(4):
                    j = fg * 4 + i
                    jl = j - hg[0] * 4
                    ba = (2 * j) % 6
                    bu = ba + 1
                    for (wt, wbf, bk) in ((wa, wab, ba), (wu, wub, bu)):
                        for k in range(KD):
                            K.op(K.pe, lambda e, wt=wt, k=k, i=i, bk=bk: e.matmul(
                                banks[bk][:, :], lhsT=wt[:, k, i * 128:(i + 1) * 128], rhs=oT[:, k, :],
                                start=(k == 0), stop=(k == KD - 1)), [wbf, b_o], [b_bk[bk]], inc=(k == KD - 1))
                    s_ = j % 2
                    K.op(K.act, lambda e, s_=s_, ba=ba: e.activation(out=sa[:, s_, :], in_=banks[ba][:, :], func=AF.Silu),
                         [b_bk[ba]], [b_sa[s_]])
                    K.op(K.dve, lambda e, s_=s_, jl=jl, bu=bu: e.tensor_tensor(
                        out=aT[:, jl, :], in0=sa[:, s_, :], in1=banks[bu][:, :], op=ALU.mult),
                        [b_sa[s_], b_bk[bu]], [b_a])
            kbase, nkh = hg[0] * 4, len(hg) * 4
            if hg is halves[-1] or not halves[-1]:
                if after_proj is not None:
                    after_proj()
            for og in range(DG):
                pieces = []
                for (k0, kn) in _chunks(nkh, 16):
                    wt, wbuf = load_w(wfv[:, kbase + k0:kbase + k0 + kn, og * GW:(og + 1) * GW], kn, GW)
                    pieces.append((wt, wbuf, kbase + k0, kn))
                proj_group(pieces, aT, b_a, nkh, og, g2, kofs=kbase)
        K.dma(K.sp, lambda e, tsl=tsl: e.dma_start(out=xov[:, :, tsl], in_=xT[:]), [b_x], [], ds_s)
        if not last:
            emit_norm(K, C, xT, b_x, KD, D, G1n, Sh1n, b_v, oT, b_o, banks[7], b_bk[7], tmp)
            if h_store is not None:
                h_store(t, oT, b_o)
            else:
                K.dma(K.sp, lambda e, tsl=tsl: e.dma_start(out=hov[:, :, tsl], in_=oT[:, 0:KD, :]), [b_o], [], ds_s)


def _new():
    nc = bass.Bass("TRN2", target_bir_lowering=False)
    stack = contextlib.ExitStack()
    K = Kern(nc, stack)
    return nc, stack, K


def _din(nc, name, shape, dt=F32):
    return nc.dram_tensor(name, list(shape), dt, kind="ExternalInput").ap()


def _dout(nc, name, shape, dt=F32):
    return nc.dram_tensor(name, list(shape), dt, kind="ExternalOutput").ap()


def _finish(K, stack):
    K.final_wait(K.sp)
    K.emit()
    stack.close()


def build_mod(cfg):
    L, D, KD = cfg["L"], cfg["D"], cfg["KD"]
    nc, stack, K = _new()
    cst = _din(nc, "consts", [128, 640])
    cT = _din(nc, "cT", [128, KD])
    modw = _din(nc, "modw", [L, D, 6 * D])
    modb = _din(nc, "modb", [L, 6 * D])
    gmix = _din(nc, "gmix", [128, L, KD])
    gffn = _din(nc, "gffn", [128, L, KD])
    vec = _dout(nc, "vec", [128, L, 6 * KD])
    C = Consts(K, cst, K.dsems[0])
    emit_mod(K, C, cfg, cT, modw, modb, gmix, gffn, vec)
    _finish(K, stack)
    return nc


def build_norm0(cfg):
    L, D, KD, T = cfg["L"], cfg["D"], cfg["KD"], cfg["T"]
    nc, stack, K = _new()
    cst = _din(nc, "consts", [128, 640])
    xT = _din(nc, "xT", [D, T])
    vec = _din(nc, "vec", [128, L, 6 * KD])
    hT = _dout(nc, "hT", [D, T], BF16)
    C = Consts(K, cst, K.dsems[0])
    emit_norm0(K, C, cfg, xT, vec, hT)
    _finish(K, stack)
    return nc


def build_dense(cfg, last):
    L, D, KD, T, DFF, KO = cfg["L"], cfg["D"], cfg["KD"], cfg["T"], cfg["DFF"], cfg["KO"]
    nc, stack, K = _new()
    cst = _din(nc, "consts", [128, 640])
    xT = _din(nc, "xT", [D, T])
    oT = _din(nc, "oT", [KO * 128, T], BF16)
    wo = _din(nc, "wo", [KO * 128, D])
    wfi = _din(nc, "wfi", [D, 2 * DFF])
    wfo = _din(nc, "wfo", [DFF, D])
    vec = _din(nc, "vec", [128, 2, 6 * KD])
    xTo = _dout(nc, "xTo", [D, T])
    hTo = None if last else _dout(nc, "hTo", [D, T], BF16)
    C = Consts(K, cst, K.dsems[0])
    cfg2 = dict(cfg)
    cfg2["L"] = 2
    emit_dense(K, C, cfg2, 0, xT, oT, wo, wfi, wfo, vec, xTo, hTo)
    _finish(K, stack)
    return nc


def emit_hgrn(K, C, cfg, ab_idx, hT_d, w_d, lbl_d, gain_d, oT_d, h_load=None, o_store=None, after_setup=None):
    D, KD, S = cfg["D"], cfg["KD"], cfg["S"]
    ds_a, ds_w, ds_s = K.dsems[0], K.dsems[1], K.dsems[2]
    CH, NCH = 64, 8
    W = K.sb("h_W", [128, KD, 1024], BF16)
    hT = K.sb("h_hT", [128, 2, KD, 512], BF16)
    lbl = K.sb("h_lbl", [128, 2, 2], F32)
    lbv = K.sb("h_lbv", [128, 3, 2], F32)
    gain = K.sb("h_gain", [64, 256], F32)
    ones = K.sb("h_ones", [128, 64], F32)
    names = ["sig", "lf", "key", "qs", "cum", "cm", "cl", "e0", "e1", "e2"]
    hf = [{n: K.sb("h_%s%d" % (n, h), [128, 512], F32) for n in names} for h in range(2)]
    hb = [{n: K.sb("h_%s%d" % (n, h), [128, 512], BF16) for n in ["qh", "qt", "kt", "kh", "k0"]} for h in range(2)]
    eL = [K.sb("h_eL%d" % h, [128, NCH], F32) for h in range(2)]
    Sst = [K.sb("h_S%d" % h, [128, 128], F32) for h in range(2)]
    Sbf = [K.sb("h_Sb%d" % h, [128, 2, 128], BF16) for h in range(2)]
    vc = K.sb("h_vc", [64, 2, 256], BF16)
    gs = K.sb("h_gs", [64, 2, 256], F32)
    PT = K.sb("h_PT", [64, 2, 2, 64], BF16)
    khs = K.sb("h_khs", [64, 2, 2, 128], BF16)
    junk = K.sb("h_junk", [64, 128], F32)
    ssq = K.sb("h_ssq", [64, 2, 2], F32)
    og = K.sb("h_og", [64, 2, 128], BF16)
    oTt = K.sb("h_oTt", [128, 2, 2, 512], BF16)
    fb2 = [K.ps("h_fb%d" % i, [128, 512], F32) for i in range(2)]
    fb = [fb2[0], fb2[1], fb2[0], fb2[1]]
    tb = [K.ps("h_tb%d" % i, [128, 512], F32) for i in range(2)]
    bkO = K.ps("h_bkO", [128, 512], F32)
    bkU = K.ps("h_bkU", [128, 512], F32)
    bkT = [K.ps("h_bkT%d" % i, [128, 1024], BF16) for i in range(2)]
    B = lambda n: [Buf() for _ in range(n)]
    b_W, b_lb, b_gain, b_ones = Buf(), Buf(), Buf(), Buf()
    b_hT, b_tb = B(2), B(2)
    b_fb2 = B(2)
    b_fb = [b_fb2[0], b_fb2[1], b_fb2[0], b_fb2[1]]
    b_hf = [{n: Buf() for n in names} for _ in range(2)]
    b_hb = [{n: Buf() for n in ["qh", "qt", "kt", "kh", "k0"]} for _ in range(2)]
    b_eL, b_S, b_Sbf = B(2), B(2), [B(2), B(2)]
    b_vc, b_gs, b_PT, b_khs, b_junk, b_ssq, b_og = B(2), B(2), B(2), B(2), Buf(), B(2), B(2)
    b_bkO, b_bkU, b_bkT = Buf(), Buf(), B(2)
    b_oTt = B(2)
    A = K.op
    wv = w_d.rearrange("(k p) c -> p k c", p=128)
    for half in range(2):
        K.dma(K.pool, lambda e, half=half: e.dma_start(out=W[:, :, half * 512:(half + 1) * 512],
                                                       in_=wv[:, :, half * 512:(half + 1) * 512]), [], [b_W], ds_w)
    if after_setup is not None:
        after_setup()
    K.dma(K.sp, lambda e: e.dma_start(out=lbl[:], in_=lbl_d[:, :, :]), [], [b_lb], ds_a)
    K.dma(K.sp, lambda e: e.dma_start(out=gain[:], in_=gain_d[0:1, :].partition_broadcast(64)), [], [b_gain], ds_a)
    A(K.dve, lambda e: e.memset(ones[:], 1.0), [], [b_ones])
    if ab_idx == 0:
        A(K.dve, lambda e: e.memset(lbv[:, 0, :], 0.0), [b_lb], [b_lb])
    else:
        A(K.dve, lambda e: e.tensor_tensor(out=lbv[:, 0, :], in0=lbl[:, 1, :], in1=lbl[:, 0, :], op=ALU.subtract),
          [b_lb], [b_lb])
        A(K.act, lambda e: e.activation(out=lbv[:, 0, :], in_=lbv[:, 0, :], func=AF.Sigmoid), [b_lb], [b_lb])
        A(K.dve, lambda e: e.tensor_scalar(out=lbv[:, 0, :], in0=lbv[:, 0, :], scalar1=1.0 - 1e-6, scalar2=0.0,
                                           op0=ALU.min, op1=ALU.max), [b_lb], [b_lb])
    A(K.dve, lambda e: e.tensor_scalar(out=lbv[:, 1, :], in0=lbv[:, 0, :], scalar1=-1.0, scalar2=1.0,
                                       op0=ALU.mult, op1=ALU.add), [b_lb], [b_lb])
    A(K.dve, lambda e: e.tensor_scalar(out=lbv[:, 2, :], in0=lbv[:, 0, :], scalar1=1.0, scalar2=-1.0,
                                       op0=ALU.mult, op1=ALU.add), [b_lb], [b_lb])
    for h in range(2):
        A(K.dve, lambda e, h=h: e.memset(PT[:, h], 0.0), [], [b_PT[h]])
        A(K.dve, lambda e, h=h: e.memset(hb[h]["k0"][:], 0.0), [], [b_hb[h]["k0"]])
        A(K.dve, lambda e, h=h: e.memset(hf[h]["e2"][:], 0.0), [], [b_hf[h]["e2"]])
        A(K.dve, lambda e, h=h: e.memset(Sst[h][:], 0.0), [], [b_S[h]])
        A(K.dve, lambda e, h=h: e.memset(Sbf[h][:, 0, :], 0.0), [], [b_Sbf[h][0]])
    hv = None if h_load is not None else hT_d.rearrange("(k p) t -> p k t", p=128)
    ov = None if o_store is not None else oT_d.rearrange("(h p) t -> p h t", p=128)
    cidx = 0
    for t in range(S // 512):
        s = t % 2
        if h_load is not None:
            h_load(t, hT[:, s], b_hT[s])
        else:
            K.dma(K.sp, lambda e, t=t, s=s: e.dma_start(out=hT[:, s], in_=hv[:, :, t * 512:(t + 1) * 512]),
                  [], [b_hT[s]], ds_a)
        for h in range(2):
            for blk in (2 * h, 2 * h + 1):
                for k in range(KD):
                    A(K.pe, lambda e, blk=blk, k=k, s=s: e.matmul(fb[blk][:, :], lhsT=W[:, k, blk * 128:(blk + 1) * 128],
                                                                  rhs=hT[:, s, k, :], start=(k == 0), stop=(k == KD - 1)),
                      [b_W, b_hT[s]], [b_fb[blk]], inc=(k == KD - 1))
            f, bf_, bb, bbf = hf[h], hb[h], b_hf[h], b_hb[h]
            lb_, oml, noml = lbv[:, 0, h:h + 1], lbv[:, 1, h:h + 1], lbv[:, 2, h:h + 1]
            A(K.act, lambda e, f=f, h=h: e.activation(out=f["sig"][:], in_=fb[2 * h + 1][:, :], func=AF.Sigmoid),
              [b_fb[2 * h + 1]], [bb["sig"]])
            A(K.act, lambda e, f=f, h=h: e.activation(out=f["qs"][:], in_=fb[2 * h][:, :], func=AF.Silu),
              [b_fb[2 * h]], [bb["qs"]])
            A(K.dve, lambda e, f=f, oml=oml, lb_=lb_: e.tensor_scalar(out=f["lf"][:], in0=f["sig"][:], scalar1=oml,
                                                                     scalar2=lb_, op0=ALU.mult, op1=ALU.add),
              [bb["sig"], b_lb], [bb["lf"]])
            A(K.dve, lambda e, f=f: e.tensor_scalar_max(out=f["lf"][:], in0=f["lf"][:], scalar1=1e-30),
              [bb["lf"]], [bb["lf"]])
            A(K.act, lambda e, f=f: e.activation(out=f["lf"][:], in_=f["lf"][:], func=AF.Ln), [bb["lf"]], [bb["lf"]])
            A(K.dve, lambda e, f=f, oml=oml, noml=noml: e.tensor_scalar(out=f["key"][:], in0=f["sig"][:], scalar1=noml,
                                                                       scalar2=oml, op0=ALU.mult, op1=ALU.add),
              [bb["sig"], b_lb], [bb["key"]])
            for c in range(NCH):
                A(K.dve, lambda e, f=f, c=c: e.tensor_tensor_scan(out=f["cum"][:, c * CH:(c + 1) * CH], data0=ones[:, :],
                                                                  data1=f["lf"][:, c * CH:(c + 1) * CH], initial=0.0,
                                                                  op0=ALU.mult, op1=ALU.add),
                  [bb["lf"], b_ones], [bb["cum"]])
            c3 = lambda a: a[:].rearrange("p (c t) -> p c t", c=NCH)
            A(K.dve, lambda e, f=f: e.tensor_tensor(out=c3(f["cm"]), in0=c3(f["cum"]),
                                                    in1=c3(f["cum"])[:, :, 31:32].broadcast_to([128, NCH, CH]),
                                                    op=ALU.subtract), [bb["cum"]], [bb["cm"]])
            A(K.dve, lambda e, f=f: e.tensor_tensor(out=c3(f["cl"]), in0=c3(f["cum"]),
                                                    in1=c3(f["cum"])[:, :, CH - 1:CH].broadcast_to([128, NCH, CH]),
                                                    op=ALU.subtract), [bb["cum"]], [bb["cl"]])
            A(K.act, lambda e, f=f, h=h: e.activation(out=eL[h][:, :], in_=c3(f["cum"])[:, :, CH - 1], func=AF.Exp),
              [bb["cum"]], [b_eL[h]])
            A(K.act, lambda e, f=f: e.activation(out=c3(f["e2"])[:, :, 0:32], in_=c3(f["cum"])[:, :, 0:32], func=AF.Exp,
                                                 scale=-1.0), [bb["cum"]], [bb["e2"]])
            A(K.dve, lambda e, f=f, bf_=bf_: e.tensor_tensor(out=c3(bf_["k0"])[:, :, 0:32], in0=c3(f["key"])[:, :, 0:32],
                                                             in1=c3(f["e2"])[:, :, 0:32], op=ALU.mult),
              [bb["key"], bb["e2"]], [bbf["k0"]])
            for (src, sc, es, mul, dst) in (("cum", 1.0, "e0", "qs", "qh"), ("cm", 1.0, "e1", "qs", "qt"),
                                            ("cm", -1.0, "e0", "key", "kt"), ("cl", -1.0, "e1", "key", "kh")):
                A(K.act, lambda e, f=f, src=src, sc=sc, es=es: e.activation(out=f[es][:], in_=f[src][:], func=AF.Exp,
                                                                           scale=sc), [bb[src]], [bb[es]])
                A(K.dve, lambda e, f=f, bf_=bf_, es=es, mul=mul, dst=dst: e.tensor_tensor(
                    out=bf_[dst][:], in0=f[mul][:], in1=f[es][:], op=ALU.mult), [bb[mul], bb[es]], [bbf[dst]])
        def prep_proj(c, s=s):
            csl = slice(c * CH, (c + 1) * CH)
            par = c % 2
            for k in range(KD):
                A(K.pe, lambda e, k=k: e.matmul(tb[par][0:CH, :], lhsT=hT[:, s, k, csl], rhs=W[:, k, 512:1024],
                                                start=(k == 0), stop=(k == KD - 1)),
                  [b_W, b_hT[s]], [b_tb[par]], inc=(k == KD - 1))
            A(K.act, lambda e: e.activation(out=vc[:, par, :], in_=tb[par][0:CH, 0:256], func=AF.Identity),
              [b_tb[par]], [b_vc[par]])
            A(K.act, lambda e: e.activation(out=gs[:, par, :], in_=tb[par][0:CH, 256:512], func=AF.Silu),
              [b_tb[par]], [b_gs[par]])
            A(K.dve, lambda e: e.tensor_tensor(out=gs[:, par, :], in0=gs[:, par, :], in1=gain[:, :], op=ALU.mult),
              [b_gs[par], b_gain], [b_gs[par]])

        def prep(c, s=s):
            csl = slice(c * CH, (c + 1) * CH)
            c0_ = c * CH
            par = c % 2
            if c >= 2:
                prep_proj(c)
            for h in range(2):
                bf_, bbf = hb[h], b_hb[h]
                A(K.pe, lambda e, h=h, bf_=bf_: e.matmul(fb2[par][0:64, h * 64 + 32:h * 64 + 64], lhsT=bf_["kt"][:, csl],
                                                         rhs=bf_["qt"][:, c0_ + 32:c0_ + 64], start=True, stop=True),
                  [bbf["kt"], bbf["qt"]], [b_fb2[par]], inc=False)
                A(K.pe, lambda e, h=h, bf_=bf_: e.matmul(fb2[par][0:32, h * 64:h * 64 + 32], lhsT=bf_["k0"][:, c0_:c0_ + 32],
                                                         rhs=bf_["qh"][:, c0_:c0_ + 32], start=True, stop=True),
                  [bbf["k0"], bbf["qh"]], [b_fb2[par]], inc=False)
                A(K.pe, lambda e, h=h, bf_=bf_: e.transpose(bkT[par][0:64, h * 128:(h + 1) * 128], bf_["kh"][:, csl],
                                                            C.ident_b()), [bbf["kh"], C.buf], [b_bkT[par]], inc=(h == 1))
            scv = fb2[par][0:64, 0:128].rearrange("p (h t) -> p h t", h=2)
            A(K.dve, lambda e: e.tensor_tensor(out=PT[:, par, :, 32:64], in0=scv[:, :, 32:64],
                                               in1=C.f[0:64, 160:192].unsqueeze(1).broadcast_to([64, 2, 32]),
                                               op=ALU.mult), [b_fb2[par], C.buf], [b_PT[par]])
            A(K.dve, lambda e: e.tensor_tensor(out=PT[0:32, par, :, 0:32], in0=scv[0:32, :, 0:32],
                                               in1=C.f[0:32, 128:160].unsqueeze(1).broadcast_to([32, 2, 32]),
                                               op=ALU.mult), [b_fb2[par], C.buf], [b_PT[par]])
            A(K.act, lambda e: e.activation(out=khs[:, par].rearrange("p h d -> p (h d)"), in_=bkT[par][0:64, 0:256],
                                            func=AF.Identity), [b_bkT[par]], [b_khs[par]])

        def main(c, s=s):
            csl = slice(c * CH, (c + 1) * CH)
            par = c % 2
            sp_, sn = c % 2, (c + 1) % 2
            for h in range(2):
                bf_, bbf = hb[h], b_hb[h]
                A(K.pe, lambda e, h=h: e.matmul(bkO[0:64, h * 128:(h + 1) * 128], lhsT=PT[:, par, h, :],
                                                rhs=vc[:, par, h * 128:(h + 1) * 128], start=True, stop=False),
                  [b_PT[par], b_vc[par]], [b_bkO], inc=False)
                A(K.pe, lambda e, h=h, bf_=bf_: e.matmul(bkO[0:64, h * 128:(h + 1) * 128], lhsT=bf_["qh"][:, csl],
                                                         rhs=Sbf[h][:, sp_, :], start=False, stop=True),
                  [bbf["qh"], b_Sbf[h][sp_]], [b_bkO], inc=(h == 1))
            for h in range(2):
                A(K.pe, lambda e, h=h: e.matmul(bkU[:, h * 128:(h + 1) * 128], lhsT=khs[:, par, h, :],
                                                rhs=vc[:, par, h * 128:(h + 1) * 128], start=True, stop=True),
                  [b_khs[par], b_vc[par]], [b_bkU], inc=(h == 1))
            for h in range(2):
                A(K.dve, lambda e, h=h: e.scalar_tensor_tensor(out=Sst[h][:], in0=Sst[h][:], scalar=eL[h][:, c:c + 1],
                                                               in1=bkU[:, h * 128:(h + 1) * 128], op0=ALU.mult,
                                                               op1=ALU.add), [b_S[h], b_eL[h], b_bkU], [b_S[h]])
                A(K.act, lambda e, h=h: e.activation(out=Sbf[h][:, sn, :], in_=Sst[h][:], func=AF.Identity),
                  [b_S[h]], [b_Sbf[h][sn]])

        def post_a(c, s=s):
            par = c % 2
            for h in range(2):
                A(K.act, lambda e, h=h: e.activation(out=junk[:, :], in_=bkO[0:64, h * 128:(h + 1) * 128], func=AF.Square,
                                                     accum_out=ssq[:, h, 0:1]), [b_bkO], [b_junk, b_ssq[h]])
                A(K.act, lambda e, h=h: e.activation(out=ssq[:, h, 1:2], in_=ssq[:, h, 0:1], func=AF.Sqrt,
                                                     scale=1.0 / 128, bias=EPS), [b_ssq[h]], [b_ssq[h]])
                A(K.dve, lambda e, h=h: e.reciprocal(out=ssq[:, h, 1:2], in_=ssq[:, h, 1:2]), [b_ssq[h]], [b_ssq[h]])
                A(K.dve, lambda e, h=h: e.scalar_tensor_tensor(out=og[:, h, :], in0=bkO[0:64, h * 128:(h + 1) * 128],
                                                               scalar=ssq[:, h, 1:2],
                                                               in1=gs[:, par, h * 128:(h + 1) * 128],
                                                               op0=ALU.mult, op1=ALU.mult),
                  [b_bkO, b_ssq[h], b_gs[par]], [b_og[h]])

        def post_b(c, s=s):
            csl = slice(c * CH, (c + 1) * CH)
            par = (c + 1) % 2
            for h in range(2):
                A(K.pe, lambda e, h=h: e.transpose(bkT[par][:, 256 + h * 64:256 + (h + 1) * 64], og[:, h, :],
                                                   C.ident_b(64)), [b_og[h], C.buf], [b_bkT[par]], inc=(h == 1))
            A(K.act, lambda e: e.activation(out=oTt[:, s, :, csl],
                                            in_=bkT[par][:, 256:384].rearrange("p (h t) -> p h t", h=2),
                                            func=AF.Identity), [b_bkT[par]], [b_oTt[s]])

        prep_proj(0)
        prep_proj(1)
        prep(0)
        for c in range(NCH):
            if c + 1 < NCH:
                prep(c + 1)
            main(c)
            if c >= 1:
                post_b(c - 1)
            post_a(c)
        post_b(NCH - 1)
        if o_store is not None:
            o_store(t, oTt[:, s], b_oTt[s], 2)
        else:
            K.dma(K.pool, lambda e, t=t, s=s: e.dma_start(out=ov[:, :, t * 512:(t + 1) * 512], in_=oTt[:, s]),
                  [b_oTt[s]], [], ds_s)


def build_hgrn(cfg, ab_idx):
    D, S = cfg["D"], cfg["S"]
    nc, stack, K = _new()
    cst = _din(nc, "consts", [128, 640])
    hT = _din(nc, "hT", [D, S], BF16)
    w = _din(nc, "w", [D, 1024])
    lbl = _din(nc, "lbl", [128, 2, 2])
    gain = _din(nc, "gain", [1, 256])
    oT = _dout(nc, "oT", [256, S], BF16)
    C = Consts(K, cst, K.dsems[0])
    emit_hgrn(K, C, cfg, ab_idx, hT, w, lbl, gain, oT)
    _finish(K, stack)
    return nc


def emit_fox(K, C, cfg, hT_d, w_d, qkg_d, fbias_d, oT_d, h_load=None, o_store=None, after_setup=None):
    D, KD, S = cfg["D"], cfg["KD"], cfg["S"]
    NB = S // 128
    A = K.op
    W = K.sb("x_W", [128, KD, 1026], BF16)
    hT = K.sb("x_hT", [128, 2, KD, 512], BF16)
    kTa = K.sb("x_kTa", [128, 2, S], BF16)
    va = K.sb("x_va", [128, 2, NB, 132], BF16)
    Ga = K.sb("x_Ga", [128, 2, NB], F32)
    qkg = K.sb("x_qkg", [128, 2], F32)
    nfb = K.sb("x_nfb", [128, 2], F32)
    qn = K.sb("x_qn", [128, 2, 512], BF16)
    sq = K.sb("x_sq", [128, 2, 512], F32)
    rt = K.sb("x_rt", [128, 2, 512], F32)
    sg = K.sb("x_sg", [128, 4, 256], F32)
    spv = K.sb("x_spv", [128, 2, 4], F32)
    lcs = K.sb("x_lcs", [128, 2, 4], F32)
    tots = K.sb("x_tots", [128, 2, 4], F32)
    incl = K.sb("x_incl", [128, 2, 2, 4], F32)
    excl = K.sb("x_excl", [128, 2, 4], F32)
    Bq = K.sb("x_Bq", [128, 2, NB], F32)
    PT = K.sb("x_PT", [128, 3, 512], BF16)
    rden = K.sb("x_rden", [128, 4], F32)
    dqc = K.sb("x_dqc", [128, 2, 4], F32)
    dqb = K.sb("x_dqb", [1, 2, 512], F32)
    dqbc = K.sb("x_dqbc", [128, 2, 512], F32)
    stg = K.sb("x_stg", [128, 3, 512], F32)
    b_dqc, b_dqb, b_dqbc = Buf(), Buf(), Buf()
    b_stg = [Buf() for _ in range(3)]
    og = K.sb("x_og", [128, 2, 128], BF16)
    oTt = K.sb("x_oTt", [128, 2, 2, 512], BF16)
    bk = [K.ps("x_bk%d" % i, [128, 512], F32) for i in range(7)]
    bkT = K.ps("x_bkT", [128, 1024], BF16)
    b_bk = [Buf() for _ in range(7)]
    b_T = Buf()
    P0, P1, N_, V_, Fb, O3, P2 = range(7)
    OB = [N_, V_, Fb, O3]
    B = lambda n: [Buf() for _ in range(n)]
    b_W, b_kTa, b_va, b_Ga, b_par = Buf(), Buf(), Buf(), Buf(), Buf()
    b_hT, b_qn, b_sq, b_rt, b_oTt = B(2), B(2), B(2), B(2), B(2)
    b_sg, b_spv, b_lcs, b_tots, b_excl = Buf(), Buf(), Buf(), Buf(), Buf()
    b_incl, b_Bq = B(2), Buf()
    b_PT = B(3)
    b_rden, b_og = Buf(), B(2)
    wv = w_d.rearrange("(k p) c -> p k c", p=128)
    K.dma(K.pool, lambda e: e.dma_start(out=W[:, :, 0:512], in_=wv[:, :, 0:512]), [], [b_W])
    K.dma(K.pool, lambda e: e.dma_start(out=W[:, :, 512:1026], in_=wv[:, :, 512:1026]), [], [b_W])
    if after_setup is not None:
        after_setup()
    K.dma(K.sp, lambda e: e.dma_start(out=qkg[:], in_=qkg_d[:, :]), [], [b_par])
    K.dma(K.sp, lambda e: e.dma_start(out=nfb[:], in_=fbias_d[0:1, :].partition_broadcast(128)), [], [b_par])
    A(K.dve, lambda e: e.tensor_scalar(out=qkg[:, 0:1], in0=qkg[:, 0:1], scalar1=float(128 ** -0.5), scalar2=None,
                                       op0=ALU.mult), [b_par], [b_par])
    A(K.dve, lambda e: e.tensor_scalar(out=nfb[:, :], in0=nfb[:, :], scalar1=-1.0, scalar2=None, op0=ALU.mult),
      [b_par], [b_par])
    A(K.dve, lambda e: e.memset(va[:], 1.0), [], [b_va])
    hv = None if h_load is not None else hT_d.rearrange("(k p) t -> p k t", p=128)
    ov = None if o_store is not None else oT_d.rearrange("(h p) t -> p h t", p=128)
    xs = 0
    for t in range(S // 512):
        s = t % 2
        tsl = slice(t * 512, (t + 1) * 512)
        if h_load is not None:
            h_load(t, hT[:, s], b_hT[s])
        else:
            K.dma(K.sp, lambda e, s=s, tsl=tsl: e.dma_start(out=hT[:, s], in_=hv[:, :, tsl]), [], [b_hT[s]])
        for h in range(2):
            for (which, bank) in ((0, P0), (1, P1)):
                blk = 2 * h + which
                for k in range(KD):
                    A(K.pe, lambda e, blk=blk, k=k, s=s, bank=bank: e.matmul(
                        bk[bank][:, :], lhsT=W[:, k, blk * 128:(blk + 1) * 128], rhs=hT[:, s, k, :],
                        start=(k == 0), stop=(k == KD - 1)), [b_W, b_hT[s]], [b_bk[bank]], inc=(k == KD - 1))
                x = which
                A(K.act, lambda e, x=x, bank=bank: e.activation(out=sq[:, x, :], in_=bk[bank][:, :], func=AF.Square),
                  [b_bk[bank]], [b_sq[x]])
                A(K.pe, lambda e, x=x: e.matmul(bk[N_][:, :], lhsT=C.ones_f(), rhs=sq[:, x, :], start=True, stop=True),
                  [b_sq[x], C.buf], [b_bk[N_]])
                A(K.act, lambda e, x=x: e.activation(out=rt[:, x, :], in_=bk[N_][:, :], func=AF.Sqrt, scale=1.0 / 128,
                                                     bias=EPS), [b_bk[N_]], [b_rt[x]])
                A(K.dve, lambda e, x=x: e.reciprocal(out=rt[:, x, :], in_=rt[:, x, :]), [b_rt[x]], [b_rt[x]])
                if which == 0:
                    A(K.dve, lambda e, h=h, x=x, bank=bank: e.scalar_tensor_tensor(
                        out=qn[:, h, :], in0=bk[bank][:, :], scalar=qkg[:, 0:1], in1=rt[:, x, :], op0=ALU.mult,
                        op1=ALU.mult), [b_bk[bank], b_par, b_rt[x]], [b_qn[h]])
                else:
                    A(K.dve, lambda e, h=h, x=x, bank=bank, tsl=tsl: e.scalar_tensor_tensor(
                        out=kTa[:, h, tsl], in0=bk[bank][:, :], scalar=qkg[:, 1:2], in1=rt[:, x, :], op0=ALU.mult,
                        op1=ALU.mult), [b_bk[bank], b_par, b_rt[x]], [b_kTa])
        for b in range(4):
            blk = 4 * t + b
            bsl = slice(b * 128, (b + 1) * 128)
            for k in range(KD):
                A(K.pe, lambda e, k=k, s=s, bsl=bsl: e.matmul(bk[V_][:, :], lhsT=hT[:, s, k, bsl], rhs=W[:, k, 512:1024],
                                                              start=(k == 0), stop=(k == KD - 1)),
                  [b_W, b_hT[s]], [b_bk[V_]], inc=(k == KD - 1))
            A(K.act, lambda e, blk=blk: e.activation(out=va[:, :, blk, 0:128],
                                                     in_=bk[V_][:, 0:256].rearrange("p (h e) -> p h e", h=2),
                                                     func=AF.Identity), [b_bk[V_]], [b_va])
            A(K.act, lambda e, b=b: e.activation(out=sg[:, b, :], in_=bk[V_][:, 256:512], func=AF.Sigmoid),
              [b_bk[V_]], [b_sg])
            for k in range(KD):
                A(K.pe, lambda e, k=k, s=s, bsl=bsl, b=b: e.matmul(bk[Fb][:, 2 * b:2 * b + 2], lhsT=hT[:, s, k, bsl],
                                                                   rhs=W[:, k, 1024:1026], start=(k == 0),
                                                                   stop=(k == KD - 1)),
                  [b_W, b_hT[s]], [b_bk[Fb]], inc=(k == KD - 1))
        fbv = bk[Fb][:, 0:8].rearrange("p (b h) -> p h b", h=2)
        for h in range(2):
            A(K.act, lambda e, h=h: e.activation(out=spv[:, h, :], in_=fbv[:, h, :], func=AF.Exp, scale=-1.0,
                                                 bias=nfb[:, h:h + 1]), [b_bk[Fb], b_par], [b_spv])
        A(K.act, lambda e: e.activation(out=spv[:], in_=spv[:], func=AF.Ln, bias=1.0, scale=1.0), [b_spv], [b_spv])
        A(K.pe, lambda e: e.matmul(bk[Fb][:, 16:24], lhsT=C.tri_f(), rhs=spv[:].rearrange("p h b -> p (h b)"),
                                   start=True, stop=True), [b_spv, C.buf], [b_bk[Fb]])
        A(K.act, lambda e: e.activation(out=lcs[:].rearrange("p h b -> p (h b)"), in_=bk[Fb][:, 16:24],
                                        func=AF.Identity), [b_bk[Fb]], [b_lcs])
        A(K.pe, lambda e: e.matmul(bk[Fb][:, 32:40], lhsT=C.sel_f(), rhs=lcs[:].rearrange("p h b -> p (h b)"),
                                   start=True, stop=True), [b_lcs, C.buf], [b_bk[Fb]])
        A(K.act, lambda e: e.activation(out=tots[:].rearrange("p h b -> p (h b)"), in_=bk[Fb][:, 32:40],
                                        func=AF.Identity), [b_bk[Fb]], [b_tots])
        for h in range(2):
            init = 0.0 if t == 0 else incl[:, 1 - s, h, 3:4]
            A(K.dve, lambda e, h=h, s=s, init=init: e.tensor_tensor_scan(
                out=incl[:, s, h, :], data0=C.ones_f(128, 4), data1=tots[:, h, :], initial=init, op0=ALU.mult,
                op1=ALU.add), [b_tots, C.buf, b_incl[1 - s]], [b_incl[s]])
        A(K.dve, lambda e, s=s: e.tensor_tensor(out=excl[:], in0=incl[:, s], in1=tots[:], op=ALU.subtract),
          [b_incl[s], b_tots], [b_excl])
        A(K.dve, lambda e, t=t: e.tensor_tensor(out=Ga[:, :, 4 * t:4 * t + 4], in0=lcs[:], in1=excl[:], op=ALU.add),
          [b_lcs, b_excl], [b_Ga])
        nkb = 4 * t + 4
        for h in range(2):
            A(K.dve, lambda e, h=h, s=s, nkb=nkb: e.tensor_scalar(
                out=Bq[:, h, 0:nkb], in0=Ga[:, h, 0:nkb], scalar1=incl[:, s, h, 3:4], scalar2=None,
                op0=ALU.subtract), [b_Ga, b_incl[s]], [b_Bq])
            A(K.dve, lambda e, h=h, s=s, t=t: e.tensor_scalar(
                out=dqc[:, h, :], in0=Ga[:, h, 4 * t:4 * t + 4], scalar1=incl[:, s, h, 3:4], scalar2=-1.0,
                op0=ALU.subtract, op1=ALU.mult), [b_Ga, b_incl[s]], [b_dqc])
        for h in range(2):
            for jl in range(4):
                A(K.pe, lambda e, h=h, jl=jl: e.transpose(bk[O3][0:1, jl * 128:(jl + 1) * 128], dqc[:, h, jl:jl + 1],
                                                          C.ident_f()), [b_dqc, C.buf], [b_bk[O3]])
            A(K.act, lambda e, h=h: e.activation(out=dqb[0:1, h, :], in_=bk[O3][0:1, :], func=AF.Identity),
              [b_bk[O3]], [b_dqb])
            A(K.pe, lambda e, h=h: e.matmul(bk[O3][:, :], lhsT=C.ones_f(1, 128), rhs=dqb[0:1, h, :], start=True,
                                            stop=True), [b_dqb, C.buf], [b_bk[O3]])
            A(K.act, lambda e, h=h: e.activation(out=dqbc[:, h, :], in_=bk[O3][:, :], func=AF.Identity),
              [b_bk[O3]], [b_dqbc])
        SB = (P0, P1, P2)
        for h in range(2):
            def score(kb, h=h):
                m = max(0, kb - 4 * t)
                x3 = kb % 3
                sb_ = SB[x3]
                diag = kb >= 4 * t
                A(K.pe, lambda e: e.matmul(bk[sb_][:, m * 128:512], lhsT=kTa[:, h, kb * 128:(kb + 1) * 128],
                                           rhs=qn[:, h, m * 128:512], start=True, stop=(not diag)),
                  [b_kTa, b_qn[h]], [b_bk[sb_]], inc=(not diag))
                if diag:
                    A(K.pe, lambda e: e.matmul(bk[sb_][:, m * 128:(m + 1) * 128], lhsT=C.ident_b(), rhs=C.negmask_b(),
                                               start=False, stop=True), [C.buf], [b_bk[sb_]])
                A(K.dve, lambda e: e.tensor_tensor(out=stg[:, x3, m * 128:512], in0=bk[sb_][:, m * 128:512],
                                                   in1=dqbc[:, h, m * 128:512], op=ALU.add),
                  [b_bk[sb_], b_dqbc], [b_stg[x3]])
                A(K.act, lambda e: e.activation(out=PT[:, x3, m * 128:512], in_=stg[:, x3, m * 128:512], func=AF.Exp,
                                                bias=Bq[:, h, kb:kb + 1], scale=1.0), [b_stg[x3], b_Bq], [b_PT[x3]])

            def pv(kb, h=h):
                m = max(0, kb - 4 * t)
                x3 = kb % 3
                for jl in range(m, 4):
                    j = 4 * t + jl
                    A(K.pe, lambda e, jl=jl, j=j: e.matmul(bk[OB[jl]][:, 0:129], lhsT=PT[:, x3, jl * 128:(jl + 1) * 128],
                                                           rhs=va[:, h, kb, 0:129], start=(kb == 0), stop=(kb == j)),
                      [b_PT[x3], b_va], [b_bk[OB[jl]]], inc=(jl == 3 or kb == j))
            score(0)
            score(1)
            for kb in range(2, nkb):
                score(kb)
                pv(kb - 2)
            pv(nkb - 2)
            pv(nkb - 1)
            for jl in range(4):
                A(K.dve, lambda e, jl=jl: e.reciprocal(out=rden[:, jl:jl + 1], in_=bk[OB[jl]][:, 128:129]),
                  [b_bk[OB[jl]]], [b_rden])
                x = jl % 2
                A(K.dve, lambda e, jl=jl, h=h, x=x: e.scalar_tensor_tensor(
                    out=og[:, x, :], in0=bk[OB[jl]][:, 0:128], scalar=rden[:, jl:jl + 1],
                    in1=sg[:, jl, h * 128:(h + 1) * 128], op0=ALU.mult, op1=ALU.mult),
                    [b_bk[OB[jl]], b_rden, b_sg], [b_og[x]])
                A(K.pe, lambda e, x=x: e.transpose(bkT[:, 0:128], og[:, x, :], C.ident_b()), [b_og[x], C.buf], [b_T])
                A(K.act, lambda e, s=s, h=h, jl=jl: e.activation(out=oTt[:, s, h, jl * 128:(jl + 1) * 128],
                                                                 in_=bkT[:, 0:128], func=AF.Identity),
                  [b_T], [b_oTt[s]])
        if o_store is not None:
            o_store(t, oTt[:, s], b_oTt[s], 2)
        else:
            K.dma(K.pool, lambda e, s=s, tsl=tsl: e.dma_start(out=ov[:, :, tsl], in_=oTt[:, s]), [b_oTt[s]], [])


def build_fox(cfg):
    D, S = cfg["D"], cfg["S"]
    nc, stack, K = _new()
    cst = _din(nc, "consts", [128, 640])
    hT = _din(nc, "hT", [D, S], BF16)
    w = _din(nc, "w", [D, 1026])
    qkg = _din(nc, "qkg", [128, 2])
    fbias = _din(nc, "fbias", [1, 2])
    oT = _dout(nc, "oT", [256, S], BF16)
    C = Consts(K, cst, None)
    emit_fox(K, C, cfg, hT, w, qkg, fbias, oT)
    _finish(K, stack)
    return nc


def emit_gla(K, C, cfg, hT_d, w_d, wgu_d, bg_d, gain_d, oT_d, h_load=None, o_store=None, after_setup=None):
    D, KD, S = cfg["D"], cfg["KD"], cfg["S"]
    A = K.op
    CH, NCH = 128, 4
    NW = 1552
    W = K.sb("g_W", [128, KD, NW], BF16)
    hT = K.sb("g_hT", [128, 2, KD, 512], BF16)
    wgu = K.sb("g_wgu", [16, 256], F32)
    bg = K.sb("g_bg", [128, 2], F32)
    gain = K.sb("g_gain", [128, 512], F32)
    ones = K.sb("g_ones", [128, 128], F32)
    glT = K.sb("g_glT", [16, 512], F32)
    spt = K.sb("g_sp", [128, 2, 512], F32)
    csp = K.sb("g_csp", [128, 2, 512], F32)
    Eq = K.sb("g_Eq", [128, 2, 512], F32)
    Ek = K.sb("g_Ek", [128, 2, 512], F32)
    eL = K.sb("g_eL", [128, 2, NCH], F32)
    qh = K.sb("g_qh", [128, 2, 512], BF16)
    kt = K.sb("g_kt", [128, 2, 512], BF16)
    kh = K.sb("g_kh", [128, 2, 512], BF16)
    Sst = K.sb("g_S", [128, 2, 512], F32)
    Sbf = K.sb("g_Sbf", [128, 2, 2, 512], BF16)
    vc = K.sb("g_vc", [128, 2, 512], BF16)
    gs = K.sb("g_gs", [128, 2, 512], F32)
    PT = K.sb("g_PT", [128, 2, 128], BF16)
    khs = K.sb("g_khs", [128, 2, 256], BF16)
    junk = K.sb("g_junk", [128, 512], F32)
    ssq = K.sb("g_ssq", [128, 2], F32)
    og = K.sb("g_og", [128, 512], BF16)
    oTt = K.sb("g_oTt", [128, 2, 4, 512], BF16)
    bk = [K.ps("g_bk%d" % i, [128, 512], F32) for i in range(6)]
    bkT = K.ps("g_bkT", [128, 1024], BF16)
    bkT2 = K.ps("g_bkT2", [128, 1024], BF16)
    b_bk = [Buf() for _ in range(6)]
    b_T, b_T2 = Buf(), Buf()
    GA, Z0, Z1, V_, Gt, O_ = range(6)
    B = lambda n: [Buf() for _ in range(n)]
    b_W, b_par, b_ones, b_glT = Buf(), Buf(), Buf(), Buf()
    b_hT, b_sp, b_csp, b_Eq, b_Ek, b_eL = B(2), B(2), B(2), B(2), B(2), B(2)
    b_qh, b_kt, b_kh, b_S = B(2), B(2), B(2), B(2)
    b_Sbf = [B(2), B(2)]
    b_vc, b_gs, b_PT, b_khs, b_oTt = B(2), B(2), B(2), B(2), B(2)
    b_junk, b_ssq, b_og = Buf(), Buf(), Buf()
    wv = w_d.rearrange("(k p) c -> p k c", p=128)
    for (c0, c1) in ((0, 528), (528, 1040), (1040, 1552)):
        K.dma(K.pool, lambda e, c0=c0, c1=c1: e.dma_start(out=W[:, :, c0:c1], in_=wv[:, :, c0:c1]), [], [b_W])
    if after_setup is not None:
        after_setup()
    K.dma(K.sp, lambda e: e.dma_start(out=wgu[:], in_=wgu_d[:, :]), [], [b_par])
    K.dma(K.sp, lambda e: e.dma_start(out=bg[:], in_=bg_d[:, :]), [], [b_par])
    K.dma(K.sp, lambda e: e.dma_start(out=gain[:], in_=gain_d[0:1, :].partition_broadcast(128)), [], [b_par])
    A(K.dve, lambda e: e.tensor_scalar(out=bg[:, :], in0=bg[:, :], scalar1=-1.0, scalar2=None, op0=ALU.mult),
      [b_par], [b_par])
    A(K.dve, lambda e: e.memset(ones[:], 1.0), [], [b_ones])
    for dc in range(2):
        A(K.dve, lambda e, dc=dc: e.memset(Sst[:, dc, :], 0.0), [], [b_S[dc]])
        A(K.dve, lambda e, dc=dc: e.memset(Sbf[:, dc, 0, :], 0.0), [], [b_Sbf[dc][0]])
    hv = None if h_load is not None else hT_d.rearrange("(k p) t -> p k t", p=128)
    ov = None if o_store is not None else oT_d.rearrange("(h p) t -> p h t", p=128)
    c3 = lambda a: a.rearrange("p (c t) -> p c t", c=NCH)
    cg = 0
    for t in range(S // 512):
        s = t % 2
        tsl = slice(t * 512, (t + 1) * 512)
        if h_load is not None:
            h_load(t, hT[:, s], b_hT[s])
        else:
            K.dma(K.sp, lambda e, s=s, tsl=tsl: e.dma_start(out=hT[:, s], in_=hv[:, :, tsl]), [], [b_hT[s]])
        def prep_proj(c, s=s):
            csl = slice(c * CH, (c + 1) * CH)
            ts = c % 2
            for (bank, c0) in ((V_, 528), (Gt, 1040)):
                for k in range(KD):
                    A(K.pe, lambda e, k=k, bank=bank, c0=c0: e.matmul(
                        bk[bank][:, :], lhsT=hT[:, s, k, csl], rhs=W[:, k, c0:c0 + 512], start=(k == 0),
                        stop=(k == KD - 1)), [b_W, b_hT[s]], [b_bk[bank]], inc=(k == KD - 1))
            A(K.act, lambda e: e.activation(out=vc[:, ts, :], in_=bk[V_][:, :], func=AF.Identity),
              [b_bk[V_]], [b_vc[ts]])
            A(K.act, lambda e: e.activation(out=gs[:, ts, :], in_=bk[Gt][:, :], func=AF.Silu),
              [b_bk[Gt]], [b_gs[ts]])
            A(K.dve, lambda e: e.tensor_tensor(out=gs[:, ts, :], in0=gs[:, ts, :], in1=gain[:, :], op=ALU.mult),
              [b_gs[ts], b_par], [b_gs[ts]])

        for k in range(KD):
            A(K.pe, lambda e, k=k, s=s: e.matmul(bk[GA][0:16, :], lhsT=W[:, k, 512:528], rhs=hT[:, s, k, :],
                                                 start=(k == 0), stop=(k == KD - 1)),
              [b_W, b_hT[s]], [b_bk[GA]], inc=(k == KD - 1))
        prep_proj(0)
        prep_proj(1)
        A(K.act, lambda e: e.activation(out=glT[:, :], in_=bk[GA][0:16, :], func=AF.Identity), [b_bk[GA]], [b_glT])
        for dc in range(2):
            zb = (Z0, Z1)[dc]
            A(K.pe, lambda e, dc=dc, zb=zb: e.matmul(bk[zb][:, :], lhsT=wgu[:, dc * 128:(dc + 1) * 128], rhs=glT[:, :],
                                                     start=True, stop=True), [b_par, b_glT], [b_bk[zb]])
            A(K.act, lambda e, dc=dc, zb=zb: e.activation(out=spt[:, dc, :], in_=bk[zb][:, :], func=AF.Exp, scale=-1.0,
                                                          bias=bg[:, dc:dc + 1]), [b_bk[zb], b_par], [b_sp[dc]])
            A(K.act, lambda e, dc=dc: e.activation(out=spt[:, dc, :], in_=spt[:, dc, :], func=AF.Ln, bias=1.0,
                                                   scale=1.0), [b_sp[dc]], [b_sp[dc]])
            for c in range(NCH):
                A(K.dve, lambda e, dc=dc, c=c: e.tensor_tensor_scan(
                    out=csp[:, dc, c * CH:(c + 1) * CH], data0=ones[:, :], data1=spt[:, dc, c * CH:(c + 1) * CH],
                    initial=0.0, op0=ALU.mult, op1=ALU.add), [b_sp[dc], b_ones], [b_csp[dc]])
            A(K.act, lambda e, dc=dc: e.activation(out=Eq[:, dc, :], in_=csp[:, dc, :], func=AF.Exp, scale=-1.0 / 16),
              [b_csp[dc]], [b_Eq[dc]])
            A(K.act, lambda e, dc=dc: e.activation(out=Ek[:, dc, :], in_=csp[:, dc, :], func=AF.Exp, scale=1.0 / 16),
              [b_csp[dc]], [b_Ek[dc]])
            A(K.act, lambda e, dc=dc: e.activation(out=eL[:, dc, :], in_=c3(csp[:, dc, :])[:, :, CH - 1], func=AF.Exp,
                                                   scale=-1.0 / 16), [b_csp[dc]], [b_eL[dc]])
            for (which, dst) in ((0, "q"), (1, "k")):
                blk = which * 2 + dc
                for k in range(KD):
                    A(K.pe, lambda e, blk=blk, k=k, s=s, zb=zb: e.matmul(
                        bk[zb][:, :], lhsT=W[:, k, blk * 128:(blk + 1) * 128], rhs=hT[:, s, k, :], start=(k == 0),
                        stop=(k == KD - 1)), [b_W, b_hT[s]], [b_bk[zb]], inc=(k == KD - 1))
                if which == 0:
                    A(K.dve, lambda e, dc=dc, zb=zb: e.scalar_tensor_tensor(
                        out=qh[:, dc, :], in0=bk[zb][:, :], scalar=float(256 ** -0.5), in1=Eq[:, dc, :], op0=ALU.mult,
                        op1=ALU.mult), [b_bk[zb], b_Eq[dc]], [b_qh[dc]])
                else:
                    A(K.dve, lambda e, dc=dc, zb=zb: e.tensor_tensor(out=kt[:, dc, :], in0=bk[zb][:, :],
                                                                     in1=Ek[:, dc, :], op=ALU.mult),
                      [b_bk[zb], b_Ek[dc]], [b_kt[dc]])
                    A(K.dve, lambda e, dc=dc: e.tensor_tensor(
                        out=c3(kh[:, dc, :]), in0=c3(kt[:, dc, :]),
                        in1=eL[:, dc, :].unsqueeze(2).broadcast_to([128, NCH, CH]), op=ALU.mult),
                        [b_kt[dc], b_eL[dc]], [b_kh[dc]])
        def prep(c, s=s):
            csl = slice(c * CH, (c + 1) * CH)
            ts = c % 2
            if c >= 2:
                prep_proj(c)
            for dc in range(2):
                A(K.pe, lambda e, dc=dc: e.matmul(bk[GA][:, 0:128], lhsT=kt[:, dc, csl], rhs=qh[:, dc, csl],
                                                  start=(dc == 0), stop=(dc == 1)),
                  [b_kt[dc], b_qh[dc]], [b_bk[GA]], inc=(dc == 1))
            A(K.dve, lambda e: e.tensor_tensor(out=PT[:, ts, :], in0=bk[GA][:, 0:128], in1=C.tri_f(), op=ALU.mult),
              [b_bk[GA], C.buf], [b_PT[ts]])
            for dc in range(2):
                A(K.pe, lambda e, dc=dc: e.transpose(bkT[:, dc * 128:(dc + 1) * 128], kh[:, dc, csl], C.ident_b()),
                  [b_kh[dc], C.buf], [b_T], inc=(dc == 1))
            A(K.act, lambda e: e.activation(out=khs[:, ts, :], in_=bkT[:, 0:256], func=AF.Identity),
              [b_T], [b_khs[ts]])

        def main(c, s=s):
            csl = slice(c * CH, (c + 1) * CH)
            ts = c % 2
            sp_, sn = c % 2, (c + 1) % 2
            A(K.pe, lambda e: e.matmul(bk[O_][:, :], lhsT=PT[:, ts, :], rhs=vc[:, ts, :], start=True, stop=False),
              [b_PT[ts], b_vc[ts]], [b_bk[O_]], inc=False)
            for dc in range(2):
                A(K.pe, lambda e, dc=dc: e.matmul(bk[O_][:, :], lhsT=qh[:, dc, csl], rhs=Sbf[:, dc, sp_, :], start=False,
                                                  stop=(dc == 1)),
                  [b_qh[dc], b_Sbf[dc][sp_]], [b_bk[O_]], inc=(dc == 1))
            for dc in range(2):
                ub = (Z0, Z1)[dc]
                A(K.pe, lambda e, dc=dc, ub=ub: e.matmul(bk[ub][:, :], lhsT=khs[:, ts, dc * 128:(dc + 1) * 128],
                                                         rhs=vc[:, ts, :], start=True, stop=True),
                  [b_khs[ts], b_vc[ts]], [b_bk[ub]])
            for dc in range(2):
                ub = (Z0, Z1)[dc]
                A(K.dve, lambda e, dc=dc, ub=ub: e.scalar_tensor_tensor(
                    out=Sst[:, dc, :], in0=Sst[:, dc, :], scalar=eL[:, dc, c:c + 1], in1=bk[ub][:, :], op0=ALU.mult,
                    op1=ALU.add), [b_S[dc], b_eL[dc], b_bk[ub]], [b_S[dc]])
                A(K.act, lambda e, dc=dc: e.activation(out=Sbf[:, dc, sn, :], in_=Sst[:, dc, :], func=AF.Identity),
                  [b_S[dc]], [b_Sbf[dc][sn]])

        def post_a(c, s=s):
            ts = c % 2
            A(K.act, lambda e: e.activation(out=junk[:, :], in_=bk[O_][:, :], func=AF.Square, accum_out=ssq[:, 0:1]),
              [b_bk[O_]], [b_junk, b_ssq])
            A(K.act, lambda e: e.activation(out=ssq[:, 1:2], in_=ssq[:, 0:1], func=AF.Sqrt, scale=1.0 / 512, bias=EPS),
              [b_ssq], [b_ssq])
            A(K.dve, lambda e: e.reciprocal(out=ssq[:, 1:2], in_=ssq[:, 1:2]), [b_ssq], [b_ssq])
            A(K.dve, lambda e: e.scalar_tensor_tensor(out=og[:, :], in0=bk[O_][:, :], scalar=ssq[:, 1:2],
                                                      in1=gs[:, ts, :], op0=ALU.mult, op1=ALU.mult),
              [b_bk[O_], b_ssq, b_gs[ts]], [b_og])

        def post_b(c, s=s):
            csl = slice(c * CH, (c + 1) * CH)
            for ec in range(4):
                A(K.pe, lambda e, ec=ec: e.transpose(bkT2[:, ec * 128:(ec + 1) * 128], og[:, ec * 128:(ec + 1) * 128],
                                                     C.ident_b()), [b_og, C.buf], [b_T2], inc=(ec == 3))
            A(K.act, lambda e: e.activation(out=oTt[:, s, :, csl], in_=bkT2[:, 0:512].rearrange("p (a q) -> p a q", a=4),
                                            func=AF.Identity), [b_T2], [b_oTt[s]])

        prep(0)
        for c in range(NCH):
            if c + 1 < NCH:
                prep(c + 1)
            main(c)
            if c >= 1:
                post_b(c - 1)
            post_a(c)
        post_b(NCH - 1)
        if o_store is not None:
            o_store(t, oTt[:, s], b_oTt[s], 4)
        else:
            K.dma(K.pool, lambda e, s=s, tsl=tsl: e.dma_start(out=ov[:, :, tsl], in_=oTt[:, s]), [b_oTt[s]], [])


def build_gla(cfg):
    D, S = cfg["D"], cfg["S"]
    nc, stack, K = _new()
    cst = _din(nc, "consts", [128, 640])
    hT = _din(nc, "hT", [D, S], BF16)
    w = _din(nc, "w", [D, 1552])
    wgu = _din(nc, "wgu", [16, 256])
    bg = _din(nc, "bg", [128, 2])
    gain = _din(nc, "gain", [1, 512])
    oT = _dout(nc, "oT", [512, S], BF16)
    C = Consts(K, cst, None)
    emit_gla(K, C, cfg, hT, w, wgu, bg, gain, oT)
    _finish(K, stack)
    return nc


CFG = dict(L=4, D=2048, KD=16, T=2048, S=8192, DFF=5632, KF=44, KO=16)
_PROGS = {}


def _prog(key, fn):
    if key not in _PROGS:
        _PROGS[key] = fn()
    return _PROGS[key]


def _run(nc, in_maps):
    res = run_bass_kernel_spmd(nc, in_maps, core_ids=list(range(8)))
    return res.results


def _fm(v, kd):
    v = np.asarray(v)
    lead = v.shape[:-1]
    a = v.reshape(lead + (kd, 128))
    return np.ascontiguousarray(np.moveaxis(a, -1, 0))


def kernel_unfused(x, c, mod_w, mod_b, norm_mix_gain, norm_ffn_gain, ab_w_in, ab_w_out, hgrn_lb_logits, hgrn_out_gain,
                   fox_q_gain, fox_k_gain, fox_f_bias, gla_w_in, gla_w_gate_up, gla_b_gate, gla_out_gain, gla_w_out,
                   ffn_w_in, ffn_w_out):
    cfg = CFG
    L, D, KD, T, S, DFF = cfg["L"], cfg["D"], cfg["KD"], cfg["T"], cfg["S"], cfg["DFF"]
    f32 = lambda a: np.ascontiguousarray(np.asarray(a, dtype=np.float32))
    x, c, mod_w, mod_b = f32(x), f32(c), f32(mod_w), f32(mod_b)
    consts = make_consts()
    cores = [(cid // 4, cid % 4) for cid in range(8)]

    cfg1 = dict(cfg)
    cfg1["L"] = 1
    nc = _prog("mod", lambda: build_mod(cfg1))
    ims = []
    for (b, j) in cores:
        ims.append({"consts": consts, "cT": _fm(c[b], KD), "modw": mod_w[j:j + 1], "modb": mod_b[j:j + 1],
                    "gmix": _fm(f32(norm_mix_gain)[j:j + 1], KD), "gffn": _fm(f32(norm_ffn_gain)[j:j + 1], KD)})
    r = _run(nc, ims)
    vec = [np.ascontiguousarray(np.concatenate([r[b * 4 + j]["vec"] for j in range(4)], axis=1)) for b in range(2)]

    nc = _prog("norm0", lambda: build_norm0(cfg))
    xT = [np.ascontiguousarray(x[b, j * T:(j + 1) * T, :].T) for (b, j) in cores]
    r = _run(nc, [{"consts": consts, "xT": xT[i], "vec": vec[cores[i][0]]} for i in range(8)])
    hT = [r[i]["hT"] for i in range(8)]

    for l in range(L):
        hfull = [np.ascontiguousarray(np.concatenate(hT[b * 4:b * 4 + 4], axis=1)) for b in range(2)]
        if l % 2 == 0:
            i = l // 2
            w = f32(ab_w_in[i])
            lbl_all = f32(hgrn_lb_logits)
            ims_h, ims_f = [], []
            for (b, j) in cores:
                hh = (2 * j, 2 * j + 1)
                cs = lambda base, h_: w[:, base + h_ * 128:base + (h_ + 1) * 128]
                wh = np.concatenate([cs(0, hh[0]), cs(1024, hh[0]), cs(0, hh[1]), cs(1024, hh[1]),
                                     cs(2048, hh[0]), cs(2048, hh[1]), cs(3072, hh[0]), cs(3072, hh[1])], axis=1)
                lbl = np.stack([np.stack([lbl_all[ly, h_ * 128:(h_ + 1) * 128] for h_ in hh], axis=1)
                                for ly in range(2)], axis=1)
                ims_h.append({"consts": consts, "hT": hfull[b], "w": np.ascontiguousarray(wh),
                              "lbl": f32(lbl), "gain": f32(hgrn_out_gain[i][2 * j:2 * j + 2]).reshape(1, 256)})
                wf = np.concatenate([cs(4096, hh[0]), cs(5120, hh[0]), cs(4096, hh[1]), cs(5120, hh[1]),
                                     cs(6144, hh[0]), cs(6144, hh[1]), cs(7168, hh[0]), cs(7168, hh[1]),
                                     w[:, 8192 + hh[0]:8192 + hh[0] + 1], w[:, 8192 + hh[1]:8192 + hh[1] + 1]], axis=1)
                ims_f.append({"consts": consts, "hT": hfull[b], "w": np.ascontiguousarray(wf),
                              "qkg": f32(np.stack([fox_q_gain[i], fox_k_gain[i]], axis=1)),
                              "fbias": f32(fox_f_bias[i][2 * j:2 * j + 2]).reshape(1, 2)})
            rh = _run(_prog(("hgrn", i), lambda: build_hgrn(cfg, i)), ims_h)
            rf = _run(_prog("fox", lambda: build_fox(cfg)), ims_f)
            ofull = [np.concatenate([np.concatenate([rh[b * 4 + j]["oT"], rf[b * 4 + j]["oT"]], axis=0)
                                     for j in range(4)], axis=0) for b in range(2)]
            wo_src = f32(ab_w_out[i])
            perm = []
            for j in range(4):
                for typ in range(2):
                    for hl in range(2):
                        base = typ * 1024 + (2 * j + hl) * 128
                        perm.extend(range(base, base + 128))
            wo = np.ascontiguousarray(wo_src[np.asarray(perm)])
        else:
            i = l // 2
            w = f32(gla_w_in[i])
            ims_g = []
            for (b, j) in cores:
                wg = np.concatenate([w[:, j * 256:(j + 1) * 256], w[:, 1024 + j * 256:1024 + (j + 1) * 256],
                                     w[:, 6144:6160], w[:, 2048 + j * 512:2048 + (j + 1) * 512],
                                     w[:, 4096 + j * 512:4096 + (j + 1) * 512]], axis=1)
                ims_g.append({"consts": consts, "hT": hfull[b], "w": np.ascontiguousarray(wg),
                              "wgu": f32(gla_w_gate_up[i][:, j * 256:(j + 1) * 256]),
                              "bg": f32(np.asarray(gla_b_gate[i][j * 256:(j + 1) * 256]).reshape(2, 128).T),
                              "gain": f32(gla_out_gain[i]).reshape(1, 512)})
            rg = _run(_prog("gla", lambda: build_gla(cfg)), ims_g)
            ofull = [np.concatenate([rg[b * 4 + j]["oT"] for j in range(4)], axis=0) for b in range(2)]
            wo = f32(gla_w_out[i])
        last = (l == L - 1)
        nc = _prog(("dense", last), lambda: build_dense(cfg, last))
        wfi, wfo = f32(ffn_w_in[l]), f32(ffn_w_out[l])
        ims = []
        for ci, (b, j) in enumerate(cores):
            v2 = np.zeros((128, 2, 6 * KD), np.float32)
            v2[:, 0] = vec[b][:, l]
            if not last:
                v2[:, 1] = vec[b][:, l + 1]
            ims.append({"consts": consts, "xT": xT[ci], "oT": np.ascontiguousarray(ofull[b][:, j * T:(j + 1) * T]),
                        "wo": wo, "wfi": wfi, "wfo": wfo, "vec": v2})
        r = _run(nc, ims)
        xT = [r[ci]["xTo"] for ci in range(8)]
        if not last:
            hT = [r[ci]["hTo"] for ci in range(8)]

    out = np.empty((2, S, D), np.float32)
    for ci, (b, j) in enumerate(cores):
        out[b, j * T:(j + 1) * T, :] = xT[ci].T
    return out


GROUPS = [[0, 1, 2, 3], [4, 5, 6, 7]]
PRECAST = False


def build_fused(cfg):
    L, D, KD, T, S, DFF, KO = cfg["L"], cfg["D"], cfg["KD"], cfg["T"], cfg["S"], cfg["DFF"], cfg["KO"]
    NT = T // 512
    NQ = D // 256
    nc = bass.Bass("TRN2", target_bir_lowering=False)
    gstack = contextlib.ExitStack()
    K = Kern(nc, gstack)
    cst = _din(nc, "consts", [128, 640])
    cT = _din(nc, "cT", [128, KD])
    modw = _din(nc, "modw", [1, D, 6 * D])
    modb = _din(nc, "modb", [1, 6 * D])
    gmix = _din(nc, "gmix", [128, 1, KD])
    gffn = _din(nc, "gffn", [128, 1, KD])
    xT_in = _din(nc, "xT", [D, T])
    wfi = _din(nc, "wfi", [L, D, 2 * DFF])
    wfo = _din(nc, "wfo", [L, DFF, D])
    wo = [_din(nc, "wo%d" % l, [KO * 128, D]) for l in range(L)]
    ab, gl = {}, {}
    for i in range((L + 1) // 2):
        ab[i] = dict(wh=_din(nc, "wh%d" % i, [D, 1024]), lbl=_din(nc, "lbl%d" % i, [128, 2, 2]),
                     again=_din(nc, "again%d" % i, [1, 256]), wf=_din(nc, "wf%d" % i, [D, 1026]),
                     qkg=_din(nc, "qkg%d" % i, [128, 2]), fbias=_din(nc, "fbias%d" % i, [1, 2]))
    for i in range(L // 2):
        gl[i] = dict(wg=_din(nc, "wg%d" % i, [D, 1552]), wgu=_din(nc, "wgu%d" % i, [16, 256]),
                     bg=_din(nc, "bg%d" % i, [128, 2]), ggain=_din(nc, "ggain%d" % i, [1, 512]))
    xTo = _dout(nc, "xTo", [D, T])
    vin = nc.dram_tensor("i_vin", [128, 6 * KD], F32)
    vall = nc.dram_tensor("i_vall", [4 * 128, 6 * KD], F32)
    xs = nc.dram_tensor("i_xs", [D, T], F32)
    HK = KD // 2
    hq = nc.dram_tensor("i_hq", [NT * 2, HK * 128, 512], BF16)
    hf = nc.dram_tensor("i_hf", [NT * 2, 4 * HK * 128, 512], BF16)
    OC = 2 if NT % 2 == 0 else 1
    NOC = (S // 512) // OC
    oqs = {nm: nc.dram_tensor("i_oq" + nm, [NOC, rows, OC * 512], BF16) for nm, rows in (("h", 256), ("f", 256), ("g", 512))}
    ofs = {nm: nc.dram_tensor("i_of" + nm, [NOC, 4 * rows, OC * 512], BF16)
           for nm, rows in (("h", 256), ("f", 256), ("g", 512))}
    wbo = nc.dram_tensor("i_wbo", [KO * 128, D], BF16)
    wbi = nc.dram_tensor("i_wbi", [D, 2 * DFF], BF16)
    wbf = nc.dram_tensor("i_wbf", [DFF, D], BF16)
    b_cv = [Buf() for _ in range(4)]
    conv = {"q": [], "per": 1, "n": 0}

    def conv_plan(l, n_calls):
        q = []
        if not PRECAST:
            conv["q"] = q
            return
        for (src, dst, rows, step) in ((wo[l], wbo.ap(), KO * 128, 512), (wfi[l], wbi.ap(), D, 128),
                                       (wfo[l], wbf.ap(), DFF, 512)):
            for r0 in range(0, rows, step):
                r1 = min(rows, r0 + step)
                q.append((src[r0:r1, :], dst[r0:r1, :]))
        conv["q"] = q
        conv["per"] = -(-len(q) // n_calls)

    def conv_step(n=None):
        n = conv["per"] if n is None else n
        for _ in range(min(n, len(conv["q"]))):
            src, dst = conv["q"].pop(0)
            key = b_cv[conv["n"] % 4]
            conv["n"] += 1
            K.dma(K.pool, lambda e, src=src, dst=dst: e.dma_start(out=dst, in_=src), [], [], key=key)

    b_vin, b_vall = Buf(), Buf()
    b_hq = [Buf() for _ in range(NT * 2)]
    b_hf = [Buf() for _ in range(NT * 2)]
    b_oq = {nm: [Buf() for _ in range(NOC)] for nm in "hfg"}
    b_of = {nm: [Buf() for _ in range(NOC)] for nm in "hfg"}
    vview = vall.ap().rearrange("(l p) c -> p l c", p=128)
    pending = []

    def flush_colls():
        while pending:
            pending.pop(0)()

    def h_store(t, tile, buf):
        for half in range(2):
            i = t * 2 + half
            dst = hq.ap()[i].rearrange("(k p) t -> p k t", p=128)
            K.dma(K.sp, lambda e, dst=dst, half=half: e.dma_start(out=dst, in_=tile[:, half * HK:(half + 1) * HK, :]),
                  [buf], [b_hq[i]], key=buf)
            pending.append(lambda i=i: K.coll("AllGather", hq.ap()[i], hf.ap()[i], [b_hq[i]], [b_hf[i]], GROUPS))

    def h_load(t, dst, buf):
        r, tl = t // NT, t % NT
        for half in range(2):
            i = tl * 2 + half
            src = hf.ap()[i].rearrange("(r k p) t -> p r k t", r=4, p=128)[:, r]
            K.dma(K.sp, lambda e, src=src, half=half: e.dma_start(out=dst[:, half * HK:(half + 1) * HK, :], in_=src),
                  [b_hf[i]], [buf], key=buf)

    def make_o_store(nm):
        def o_store(t, src, buf, nh):
            ci, cl = t // OC, t % OC
            dst = oqs[nm].ap()[ci].rearrange("(h p) t -> p h t", p=128)[:, :, cl * 512:(cl + 1) * 512]
            K.dma(K.pool, lambda e: e.dma_start(out=dst, in_=src), [buf], [b_oq[nm][ci]], key=buf)
            flush_colls()
            if cl == OC - 1:
                pending.append(lambda: K.coll("AllGather", oqs[nm].ap()[ci], ofs[nm].ap()[ci], [b_oq[nm][ci]],
                                              [b_of[nm][ci]], GROUPS))
            conv_step()
        return o_store

    seg_cache = {}

    def make_o_load(names):
        def o_load(tt, oT, buf):
            k0 = 0
            for nm in names:
                rows = 4 * (256 if nm in "hf" else 512)
                nk = rows // 128

                def fn(e, nm=nm, nk=nk, k0=k0):
                    if "segC" not in seg_cache:
                        seg_cache["segC"] = (e.partition_id() % 4) * (NT // OC)
                    ci = seg_cache["segC"] + tt // OC
                    src = ofs[nm].ap().rearrange("c (k p) t -> p c k t", p=128)[
                        :, bass.ds(ci, 1), :, (tt % OC) * 512:(tt % OC + 1) * 512]
                    return e.dma_start(out=oT[:, k0:k0 + nk, :].rearrange("p (o k) t -> p o k t", o=1), in_=src)
                K.dma(K.sp, fn, [b for b in b_of[nm]], [buf], key=buf)
                k0 += nk
        return o_load

    C = Consts(K, cst, None)
    cfg1 = dict(cfg)
    cfg1["L"] = 1
    K.begin_phase()
    emit_mod(K, C, cfg1, cT, modw, modb, gmix, gffn, vin.ap().rearrange("p (l c) -> p l c", l=1), out_buf=b_vin)
    K.coll("AllGather", vin, vall, [b_vin], [b_vall], GROUPS)
    K.end_phase(last=True)
    K.begin_phase()
    emit_norm0(K, C, cfg, xT_in, vview, None, h_store=h_store, after_tile=flush_colls)
    K.end_phase()
    for l in range(L):
        i = l // 2
        if l % 2 == 0:
            conv_plan(l, 2 * (S // 512))
            K.begin_phase()
            emit_hgrn(K, C, cfg, i, None, ab[i]["wh"], ab[i]["lbl"], ab[i]["again"], None, h_load=h_load,
                      o_store=make_o_store("h"), after_setup=flush_colls)
            K.end_phase()
            K.begin_phase()
            emit_fox(K, C, cfg, None, ab[i]["wf"], ab[i]["qkg"], ab[i]["fbias"], None, h_load=h_load,
                     o_store=make_o_store("f"), after_setup=flush_colls)
            conv_step(len(conv["q"]))
            K.end_phase()
            names = "hf"
        else:
            conv_plan(l, S // 512)
            K.begin_phase()
            emit_gla(K, C, cfg, None, gl[i]["wg"], gl[i]["wgu"], gl[i]["bg"], gl[i]["ggain"], None, h_load=h_load,
                     o_store=make_o_store("g"), after_setup=flush_colls)
            conv_step(len(conv["q"]))
            K.end_phase()
            names = "g"
        last = (l == L - 1)
        K.begin_phase()
        dw = (wbo.ap(), wbi.ap(), wbf.ap()) if PRECAST else (wo[l], wfi[l], wfo[l])
        emit_dense(K, C, cfg, l, xT_in if l == 0 else xs.ap(), None, dw[0], dw[1], dw[2], vview,
                   xTo if last else xs.ap(), None, o_load=make_o_load(names), h_store=None if last else h_store,
                   after_proj=flush_colls, at_start=flush_colls)
        if last:
            flush_colls()
        K.end_phase(last=last)
    gstack.close()
    return nc


def dense_k_perm(kind):
    perm = []
    if kind == "ab":
        for typ in range(2):
            for r in range(4):
                for loc in range(256):
                    perm.append(r * 512 + typ * 256 + loc)
    else:
        perm = list(range(2048))
    return np.asarray(perm)


def kernel_fused(cfg, x, c, mod_w, mod_b, norm_mix_gain, norm_ffn_gain, ab_w_in, ab_w_out, hgrn_lb_logits,
                 hgrn_out_gain, fox_q_gain, fox_k_gain, fox_f_bias, gla_w_in, gla_w_gate_up, gla_b_gate, gla_out_gain,
                 gla_w_out, ffn_w_in, ffn_w_out):
    L, D, KD, T, S, DFF = cfg["L"], cfg["D"], cfg["KD"], cfg["T"], cfg["S"], cfg["DFF"]
    f32 = lambda a: np.ascontiguousarray(np.asarray(a, dtype=np.float32))
    x, c, mod_w, mod_b = f32(x), f32(c), f32(mod_w), f32(mod_b)
    consts = make_consts()
    ab_rows = []
    for j in range(4):
        for typ in range(2):
            for hl in range(2):
                base = typ * 1024 + (2 * j + hl) * 128
                ab_rows.extend(range(base, base + 128))
    ab_rows = np.asarray(ab_rows)
    wfi, wfo = f32(ffn_w_in), f32(ffn_w_out)
    shared = {"consts": consts, "wfi": wfi, "wfo": wfo}
    for l in range(L):
        i = l // 2
        if l % 2 == 0:
            shared["wo%d" % l] = np.ascontiguousarray(f32(ab_w_out[i])[ab_rows[dense_k_perm("ab")]])
        else:
            shared["wo%d" % l] = np.ascontiguousarray(f32(gla_w_out[i])[dense_k_perm("gla")])
    lbl_all = f32(hgrn_lb_logits)
    ims = []
    for cid in range(8):
        b, j = cid // 4, cid % 4
        im = dict(shared)
        im["cT"] = _fm(c[b], KD)
        im["modw"] = mod_w[j:j + 1]
        im["modb"] = mod_b[j:j + 1]
        im["gmix"] = _fm(f32(norm_mix_gain)[j:j + 1], KD)
        im["gffn"] = _fm(f32(norm_ffn_gain)[j:j + 1], KD)
        im["xT"] = np.ascontiguousarray(x[b, j * T:(j + 1) * T, :].T)
        hh = (2 * j, 2 * j + 1)
        for i in range((L + 1) // 2):
            w = f32(ab_w_in[i])
            cs = lambda base, h_: w[:, base + h_ * 128:base + (h_ + 1) * 128]
            im["wh%d" % i] = np.ascontiguousarray(np.concatenate(
                [cs(0, hh[0]), cs(1024, hh[0]), cs(0, hh[1]), cs(1024, hh[1]),
                 cs(2048, hh[0]), cs(2048, hh[1]), cs(3072, hh[0]), cs(3072, hh[1])], axis=1))
            im["lbl%d" % i] = f32(np.stack([np.stack([lbl_all[ly, h_ * 128:(h_ + 1) * 128] for h_ in hh], axis=1)
                                            for ly in range(2)], axis=1))
            im["again%d" % i] = f32(hgrn_out_gain[i][2 * j:2 * j + 2]).reshape(1, 256)
            im["wf%d" % i] = np.ascontiguousarray(np.concatenate(
                [cs(4096, hh[0]), cs(5120, hh[0]), cs(4096, hh[1]), cs(5120, hh[1]),
                 cs(6144, hh[0]), cs(6144, hh[1]), cs(7168, hh[0]), cs(7168, hh[1]),
                 w[:, 8192 + hh[0]:8192 + hh[0] + 1], w[:, 8192 + hh[1]:8192 + hh[1] + 1]], axis=1))
            im["qkg%d" % i] = f32(np.stack([fox_q_gain[i], fox_k_gain[i]], axis=1))
            im["fbias%d" % i] = f32(fox_f_bias[i][2 * j:2 * j + 2]).reshape(1, 2)
        for i in range(L // 2):
            w = f32(gla_w_in[i])
            im["wg%d" % i] = np.ascontiguousarray(np.concatenate(
                [w[:, j * 256:(j + 1) * 256], w[:, 1024 + j * 256:1024 + (j + 1) * 256], w[:, 6144:6160],
                 w[:, 2048 + j * 512:2048 + (j + 1) * 512], w[:, 4096 + j * 512:4096 + (j + 1) * 512]], axis=1))
            im["wgu%d" % i] = f32(gla_w_gate_up[i][:, j * 256:(j + 1) * 256])
            im["bg%d" % i] = f32(np.asarray(gla_b_gate[i][j * 256:(j + 1) * 256]).reshape(2, 128).T)
            im["ggain%d" % i] = f32(gla_out_gain[i]).reshape(1, 512)
        ims.append(im)
    nc = _prog("fused", lambda: build_fused(cfg))
    r = _run(nc, ims)
    out = np.empty((2, S, D), np.float32)
    for cid in range(8):
        b, j = cid // 4, cid % 4
        out[b, j * T:(j + 1) * T, :] = r[cid]["xTo"].T
    return out


def kernel(**inputs):
    return kernel_fused(CFG, **inputs)
```

```python
import contextlib
import numpy as np
import ml_dtypes
import concourse.bass as bass
import concourse.mybir as mybir
from concourse.bass_utils import run_bass_kernel_spmd

F32 = mybir.dt.float32
BF16 = mybir.dt.bfloat16
AF = mybir.ActivationFunctionType
ALU = mybir.AluOpType
EPS = 1e-6


class Sem:
    def __init__(self, h):
        self.h = h
        self.n = 0


class Buf:
    __slots__ = ("w", "r", "ds")

    def __init__(self):
        self.w = None
        self.r = {}
        self.ds = None


class Eng:
    def __init__(self, name, sem, is_pe=False):
        self.name = name
        self.sem = sem
        self.prog = []
        self.seen = {}
        self.is_pe = is_pe


class Kern:
    def __init__(self, nc, stack, n_dma_sems=12):
        self.nc = nc
        self.stack = stack
        mk = lambda n: Sem(stack.enter_context(nc.semaphore(n)))
        self.pe = Eng("tensor", mk("s_pe"), True)
        self.act = Eng("scalar", mk("s_act"))
        self.dve = Eng("vector", mk("s_dve"))
        self.pool = Eng("gpsimd", mk("s_pool"))
        self.sp = Eng("sync", mk("s_sp"))
        self.engs = [self.pe, self.act, self.dve, self.pool, self.sp]
        self.dsems = [None] * n_dma_sems
        self.nbuf = 0
        self.ccs = [mk("s_cc%d" % i) for i in range(4)]
        self.dsems.extend(self.ccs)
        self.ncoll = 0
        self.free_sems = []
        self.phase_sems = []
        self.pstack = None
        self.nphase = 0

    def sb(self, name, shape, dt):
        st = self.pstack if self.pstack is not None else self.stack
        return st.enter_context(self.nc.sbuf_tensor("%s_%d" % (name, self.nphase), list(shape), dt))

    def ps(self, name, shape, dt):
        st = self.pstack if self.pstack is not None else self.stack
        return st.enter_context(self.nc.psum_tensor("%s_%d" % (name, self.nphase), list(shape), dt))

    def begin_phase(self):
        self.pstack = contextlib.ExitStack()
        self.nphase += 1

    def end_phase(self, last=False):
        for e in self.engs:
            self.final_wait(e, skip_cc=not last)
        self.emit()
        for e in self.engs:
            e.prog = []
        self.free_sems.extend(self.phase_sems)
        self.phase_sems = []
        self.pstack.close()
        self.pstack = None

    def coll(self, kind, in_t, out_t, R, W, groups):
        eng = self.pool
        cc = self.ccs[self.ncoll % len(self.ccs)]
        self.ncoll += 1
        waits = self._waits(eng, R, W)
        if cc.n > 0 and eng.seen.get(cc, 0) < cc.n:
            eng.seen[cc] = cc.n
            waits.append((cc, cc.n))
        cc.n += 1
        ev = (cc, cc.n)
        in_ap = in_t if isinstance(in_t, bass.AP) else in_t.ap()
        out_ap = out_t if isinstance(out_t, bass.AP) else out_t.ap()
        fn = lambda e: e.collective_compute(kind, ALU.bypass, replica_groups=groups, ins=[in_ap.opt()],
                                            outs=[out_ap.opt()])
        eng.prog.append((waits, fn, (cc, 1)))
        self._commit(ev, R, W)

    def _waits(self, eng, R, W):
        needs = {}

        def need(ev):
            s, v = ev
            if needs.get(s, 0) < v:
                needs[s] = v

        for b in R:
            if b.w is not None:
                if b.w[0] is eng.sem and eng.is_pe:
                    continue
                need(b.w)
        for b in W:
            if b.w is not None and b.w[0] is not eng.sem:
                need(b.w)
            for s, v in b.r.items():
                if s is not eng.sem:
                    need((s, v))
        out = []
        for s, v in needs.items():
            if eng.seen.get(s, 0) < v:
                eng.seen[s] = v
                out.append((s, v))
        return out

    def _commit(self, ev, R, W):
        for b in R:
            if b.r.get(ev[0], 0) < ev[1]:
                b.r[ev[0]] = ev[1]
        for b in W:
            b.w = ev
            b.r = {}

    def op(self, eng, fn, R=(), W=(), inc=True):
        waits = self._waits(eng, R, W)
        if inc:
            eng.sem.n += 1
            ev = (eng.sem, eng.sem.n)
        else:
            ev = (eng.sem, eng.sem.n + 1)
        eng.prog.append((waits, fn, (eng.sem, 1) if inc else None))
        self._commit(ev, R, W)

    def dma(self, eng, fn, R, W, dsem=None, key=None):
        if key is None:
            key = W[0] if W else R[0]
        if key.ds is None:
            if self.free_sems:
                key.ds = self.free_sems.pop()
            else:
                key.ds = Sem(self.stack.enter_context(self.nc.semaphore("s_d%d" % self.nbuf)))
                self.nbuf += 1
                self.dsems.append(key.ds)
            if self.pstack is not None:
                self.phase_sems.append(key.ds)
        dsem = key.ds
        waits = self._waits(eng, R, W)
        if dsem.n > 0 and eng.seen.get(dsem, 0) < dsem.n:
            eng.seen[dsem] = dsem.n
            waits.append((dsem, dsem.n))
        dsem.n += 16
        ev = (dsem, dsem.n)
        eng.prog.append((waits, fn, (dsem, 16)))
        self._commit(ev, R, W)

    def final_wait(self, eng, skip_cc=False):
        waits = []
        for s in [e.sem for e in self.engs] + [d for d in self.dsems if d is not None]:
            if skip_cc and s in self.ccs:
                continue
            if s is not eng.sem and s.n > 0 and eng.seen.get(s, 0) < s.n:
                eng.seen[s] = s.n
                waits.append((s, s.n))
        eng.prog.append((waits, None, None))

    def emit(self):
        nc = self.nc

        def replay(eng):
            def run(e):
                for waits, fn, inc in eng.prog:
                    for s, v in waits:
                        e.wait_ge(s.h, v)
                    if fn is not None:
                        ins = fn(e)
                        if inc is not None:
                            ins.then_inc(inc[0].h, inc[1])
            return run

        with nc.Block() as block:
            block.tensor(replay(self.pe))
            block.scalar(replay(self.act))
            block.vector(replay(self.dve))
            block.gpsimd(replay(self.pool))
            block.sync(replay(self.sp))


def _chunks(n, m):
    return [(i, min(m, n - i)) for i in range(0, n, m)]


class Consts:
    def __init__(self, K, cdram, dsem):
        nc = K.nc
        self.f = K.sb("c_f32", [128, 5 * 128], F32)
        self.b = K.sb("c_bf16", [128, 5 * 128], BF16)
        self.buf = Buf()
        f, b = self.f, self.b
        K.dma(K.sp, lambda e: e.dma_start(out=f[:], in_=cdram[:, :]), [], [self.buf], dsem)
        K.op(K.dve, lambda e: e.tensor_copy(out=b[:], in_=f[:]), [self.buf], [self.buf])

    def ident_b(self, n=128):
        return self.b[0:n, 0:n]

    def tri_b(self, n=128):
        return self.b[0:n, 128:128 + n]

    def tri_f(self, n=128):
        return self.f[0:n, 128:128 + n]

    def ones_f(self, p=128, n=128):
        return self.f[0:p, 256:256 + n]

    def ones_b(self, p=128, n=128):
        return self.b[0:p, 256:256 + n]

    def sel_f(self):
        return self.f[:, 384:512]

    def ident_f(self):
        return self.f[:, 0:128]

    def negmask_b(self):
        return self.b[:, 512:640]


def make_consts():
    c = np.zeros((128, 5 * 128), np.float32)
    c[:, 0:128] = np.eye(128)
    c[:, 128:256] = np.triu(np.ones((128, 128)))
    c[:, 256:384] = 1.0
    c[127, 384:512] = 1.0
    c[:, 512:640] = -30000.0 * np.tril(np.ones((128, 128)), -1)
    return c


def emit_norm(K, C, xT, xbuf, KD, D, G, Sh, vbuf, outT, obuf, ssbank, ssb, tmp):
    sq, sqb, rs, rsb, tm, tmb = tmp["sq"], tmp["sqb"], tmp["rs"], tmp["rsb"], tmp["tm"], tmp["tmb"]
    for j in range(KD):
        s = j % 2
        K.op(K.act, lambda e, j=j, s=s: e.activation(out=sq[:, s, :], in_=xT[:, j, :], func=AF.Square),
             [xbuf], [sqb[s]])
        K.op(K.pe, lambda e, j=j, s=s: e.matmul(ssbank[:, :], lhsT=C.ones_b(), rhs=sq[:, s, :],
                                               start=(j == 0), stop=(j == KD - 1)),
             [sqb[s], C.buf], [ssb], inc=True)
    K.op(K.act, lambda e: e.activation(out=rs[:, :], in_=ssbank[:, :], func=AF.Sqrt, scale=1.0 / D, bias=EPS),
         [ssb], [rsb])
    K.op(K.dve, lambda e: e.reciprocal(out=rs[:, :], in_=rs[:, :]), [rsb], [rsb])
    for j in range(KD):
        s = j % 2
        K.op(K.dve, lambda e, j=j, s=s: e.scalar_tensor_tensor(out=tm[:, s, :], in0=xT[:, j, :], scalar=G[:, j:j + 1],
                                                             in1=rs[:, :], op0=ALU.mult, op1=ALU.mult),
             [xbuf, rsb, vbuf], [tmb[s]])
        K.op(K.act, lambda e, j=j, s=s: e.activation(out=outT[:, j, :], in_=tm[:, s, :], func=AF.Identity,
                                                   bias=Sh[:, j:j + 1], scale=1.0),
             [tmb[s], vbuf], [obuf])


def emit_mod(K, C, cfg, cT_d, modw_d, modb_d, gmix_d, gffn_d, vec_d, out_buf=None):
    L, D, KD = cfg["L"], cfg["D"], cfg["KD"]
    W6 = 6 * D
    CG = min(2048, W6)
    NB = CG // 512
    assert W6 % CG == 0
    ds_a, ds_w = K.dsems[0], K.dsems[1]
    cT = K.sb("m_cT", [128, KD], F32)
    gm = K.sb("m_gm", [128, L, KD], F32)
    gf = K.sb("m_gf", [128, L, KD], F32)
    row = K.sb("m_row", [1, W6], F32)
    brow = K.sb("m_brow", [1, W6], F32)
    wp = K.sb("m_wp", [128, 3, CG], F32)
    mt = K.sb("m_mt", [128, 6 * KD], F32)
    vec = K.sb("m_vec", [128, L, 6 * KD], F32)
    banks = [K.ps("m_ps%d" % i, [128, 512], F32) for i in range(NB)]
    tp = K.ps("m_tp", [128, 512], F32)
    b_c, b_g, b_row, b_brow, b_mt, b_vec, b_tp = Buf(), Buf(), Buf(), Buf(), Buf(), Buf(), Buf()
    b_wp = [Buf() for _ in range(3)]
    b_bk = [Buf() for _ in range(NB)]
    K.dma(K.sp, lambda e: e.dma_start(out=cT[:], in_=cT_d[:, :]), [], [b_c], ds_a)
    K.dma(K.sp, lambda e: e.dma_start(out=gm[:], in_=gmix_d[:, :, :]), [], [b_g], ds_a)
    K.dma(K.sp, lambda e: e.dma_start(out=gf[:], in_=gffn_d[:, :, :]), [], [b_g], ds_a)
    K.op(K.act, lambda e: e.activation(out=cT[:], in_=cT[:], func=AF.Silu), [b_c], [b_c])
    ip = 0
    for l in range(L):
        K.dma(K.sp, lambda e, l=l: e.dma_start(out=brow[:], in_=modb_d[l:l + 1, :]), [], [b_brow], ds_a)
        for cg in range(W6 // CG):
            for k in range(KD):
                s = ip % 3
                ip += 1
                K.dma(K.sp, lambda e, l=l, cg=cg, k=k, s=s: e.dma_start(
                    out=wp[:, s, :], in_=modw_d[l, k * 128:(k + 1) * 128, cg * CG:(cg + 1) * CG]),
                    [], [b_wp[s]], ds_w)
                for q in range(NB):
                    K.op(K.pe, lambda e, k=k, s=s, q=q: e.matmul(
                        banks[q][0:1, :], lhsT=cT[:, k:k + 1], rhs=wp[:, s, q * 512:(q + 1) * 512],
                        start=(k == 0), stop=(k == KD - 1)), [b_c, b_wp[s]], [b_bk[q]])
            for q in range(NB):
                c0 = cg * CG + q * 512
                K.op(K.dve, lambda e, q=q, c0=c0: e.tensor_tensor(
                    out=row[0:1, c0:c0 + 512], in0=banks[q][0:1, :], in1=brow[0:1, c0:c0 + 512], op=ALU.add),
                    [b_bk[q], b_brow], [b_row])
        for c in range(6 * KD):
            K.op(K.pe, lambda e, c=c: e.matmul(tp[:, c:c + 1], lhsT=row[0:1, c * 128:(c + 1) * 128],
                                               rhs=C.ones_f(1, 1), start=True, stop=True),
                 [b_row, C.buf], [b_tp])
        K.op(K.act, lambda e: e.activation(out=mt[:, :], in_=tp[:, 0:6 * KD], func=AF.Identity), [b_tp], [b_mt])
        for (dst, src_sc, gn) in ((0, 1, gm), (3, 4, gf)):
            K.op(K.dve, lambda e, l=l, dst=dst, src_sc=src_sc, gn=gn: e.scalar_tensor_tensor(
                out=vec[:, l, dst * KD:(dst + 1) * KD], in0=mt[:, src_sc * KD:(src_sc + 1) * KD], scalar=1.0,
                in1=gn[:, l, :], op0=ALU.add, op1=ALU.mult), [b_mt, b_g], [b_vec])
        for (dst, src) in ((1, 0), (2, 2), (4, 3), (5, 5)):
            K.op(K.dve, lambda e, l=l, dst=dst, src=src: e.tensor_copy(
                out=vec[:, l, dst * KD:(dst + 1) * KD], in_=mt[:, src * KD:(src + 1) * KD]), [b_mt], [b_vec])
    K.dma(K.sp, lambda e: e.dma_start(out=vec_d[:, :, :], in_=vec[:]), [b_vec], [out_buf] if out_buf else [], key=b_vec)


def norm_scratch(K, pfx):
    return {
        "sq": K.sb(pfx + "sq", [128, 2, 512], BF16), "sqb": [Buf(), Buf()],
        "rs": K.sb(pfx + "rs", [128, 512], F32), "rsb": Buf(),
        "tm": K.sb(pfx + "tm", [128, 2, 512], F32), "tmb": [Buf(), Buf()],
    }


def emit_norm0(K, C, cfg, xT_d, vec_d, hT_d, h_store=None, after_tile=None):
    D, KD, T = cfg["D"], cfg["KD"], cfg["T"]
    ds_a = K.dsems[0]
    vec = K.sb("n_vec", [128, 6 * KD], F32)
    xT = K.sb("n_xT", [128, 2, KD, 512], F32)
    hT = K.sb("n_hT", [128, 2, KD, 512], BF16)
    ss = K.ps("n_ss", [128, 512], F32)
    b_v, b_ss = Buf(), Buf()
    b_x, b_h = [Buf(), Buf()], [Buf(), Buf()]
    tmp = norm_scratch(K, "n_")
    K.dma(K.sp, lambda e: e.dma_start(out=vec[:], in_=vec_d[:, 0, :]), [], [b_v], ds_a)
    xv = xT_d.rearrange("(k p) t -> p k t", p=128)
    hv = None if h_store is not None else hT_d.rearrange("(k p) t -> p k t", p=128)
    def xload(t):
        K.dma(K.sp, lambda e, t=t: e.dma_start(out=xT[:, t % 2], in_=xv[:, :, t * 512:(t + 1) * 512]),
              [], [b_x[t % 2]], ds_a)

    xload(0)
    for t in range(T // 512):
        s = t % 2
        if t + 1 < T // 512:
            xload(t + 1)
        emit_norm(K, C, xT[:, s], b_x[s], KD, D, vec[:, 0:KD], vec[:, KD:2 * KD], b_v, hT[:, s], b_h[s], ss, b_ss, tmp)
        if h_store is not None:
            h_store(t, hT[:, s], b_h[s])
            if after_tile is not None:
                after_tile()
        else:
            K.dma(K.sp, lambda e, t=t, s=s: e.dma_start(out=hv[:, :, t * 512:(t + 1) * 512], in_=hT[:, s]),
                  [b_h[s]], [], ds_a)


def emit_dense(K, C, cfg, l, xT_d, oT_d, wo_d, wfi_d, wfo_d, vec_d, xTo_d, hTo_d, o_load=None, h_store=None,
               after_proj=None, at_start=None):
    D, KD, T, DFF, KF, KO, L = cfg["D"], cfg["KD"], cfg["T"], cfg["DFF"], cfg["KF"], cfg["KO"], cfg["L"]
    DG = D // 512 if D >= 512 else 1
    GW = min(512, D)
    GC = GW // 128
    FG = DFF // 512
    assert DFF % 512 == 0
    ds_a, ds_w, ds_s = K.dsems[0], K.dsems[1], K.dsems[2]
    last = hTo_d is None and h_store is None
    vec = K.sb("d_vec", [128, 2, 6 * KD], F32)
    xin = K.sb("d_xin", [128, KD, 512], F32)
    xT = K.sb("d_xT", [128, KD, 512], F32)
    oT2 = K.sb("d_oT", [128, 1, KO, 512], BF16)
    oT = K.sb("d_hT", [128, KD, 512], BF16)
    halves = [list(range(0, (FG + 1) // 2)), list(range((FG + 1) // 2, FG))]
    AH = 4 * len(halves[0])
    aT = K.sb("d_aT", [128, AH, 512], BF16)
    sa = K.sb("d_sa", [128, 2, 512], BF16)
    WB = 4
    WSZ = max(KO * GW, KD * 512, 16 * GW)
    wb = K.sb("d_wb", [128, WB, WSZ], BF16)
    banks = [K.ps("d_ps%d" % i, [128, 512], F32) for i in range(8)]
    b_bk = [Buf() for _ in range(8)]
    b_v, b_x, b_o, b_a, b_xin = Buf(), Buf(), Buf(), Buf(), Buf()
    b_o2 = [Buf(), Buf()]
    b_sa = [Buf(), Buf()]
    b_wb = [Buf() for _ in range(WB)]
    tmp = norm_scratch(K, "d_")
    K.dma(K.sp, lambda e: e.dma_start(out=vec[:, 0, :], in_=vec_d[:, l, :]), [], [b_v], ds_a)
    if not last:
        K.dma(K.sp, lambda e: e.dma_start(out=vec[:, 1, :], in_=vec_d[:, l + 1, :]), [], [b_v], ds_a)
    g1 = vec[:, 0, 2 * KD:3 * KD]
    G2 = vec[:, 0, 3 * KD:4 * KD]
    Sh2 = vec[:, 0, 4 * KD:5 * KD]
    g2 = vec[:, 0, 5 * KD:6 * KD]
    G1n = vec[:, 1, 0:KD]
    Sh1n = vec[:, 1, KD:2 * KD]
    xv = xT_d.rearrange("(k p) t -> p k t", p=128)
    ov = None if o_load is not None else oT_d.rearrange("(k p) t -> p k t", p=128)
    xov = xTo_d.rearrange("(k p) t -> p k t", p=128)
    hov = None if (last or h_store is not None) else hTo_d.rearrange("(k p) t -> p k t", p=128)
    wov = wo_d.rearrange("(k p) c -> p k c", p=128)
    wiv = wfi_d.rearrange("(k p) c -> p k c", p=128)
    wfv = wfo_d.rearrange("(k p) c -> p k c", p=128)
    st = {"w": 0, "bank": 0}

    def load_w(src_ap, kk, cc):
        s = st["w"] % WB
        st["w"] += 1
        dst = wb[:, s, 0:kk * cc].rearrange("p (k c) -> p k c", k=kk)
        K.dma(K.pool, lambda e: e.dma_start(out=dst, in_=src_ap), [], [b_wb[s]], ds_w)
        return dst, b_wb[s]

    def proj_group(pieces, rhsT, rbuf, nk, og, gvec, src=None, sbuf=None, kofs=0):
        if src is None:
            src, sbuf = xT, b_x
        base = (st["bank"] % 2) * 4
        st["bank"] += 1
        kdone = 0
        for (wt, wbuf, k0, kn) in pieces:
            for kk in range(kn):
                for i in range(GC):
                    K.op(K.pe, lambda e, wt=wt, kk=kk, i=i, k=k0 + kk - kofs: e.matmul(
                        banks[base + i][:, :], lhsT=wt[:, kk, i * 128:(i + 1) * 128], rhs=rhsT[:, k, :],
                        start=(k == 0), stop=(k == nk - 1)), [wbuf, rbuf], [b_bk[base + i]],
                        inc=(k0 + kk - kofs == nk - 1 or kk == kn - 1))
        for i in range(GC):
            j = og * GC + i
            K.op(K.dve, lambda e, i=i, j=j: e.scalar_tensor_tensor(
                out=xT[:, j, :], in0=banks[base + i][:, :], scalar=gvec[:, j:j + 1], in1=src[:, j, :],
                op0=ALU.mult, op1=ALU.add), [b_bk[base + i], sbuf, b_v], [b_x])

    NTL = T // 512

    def loads(t):
        tsl = slice(t * 512, (t + 1) * 512)
        K.dma(K.sp, lambda e: e.dma_start(out=xin[:], in_=xv[:, :, tsl]), [], [b_xin])
        if o_load is not None:
            o_load(t, oT2[:, 0], b_o2[0])
        else:
            K.dma(K.sp, lambda e: e.dma_start(out=oT2[:, 0], in_=ov[:, :, tsl]), [], [b_o2[0]])

    first_w = load_w(wov[:, :, 0:GW], KO, GW)
    if at_start is not None:
        at_start()
    loads(0)
    for t in range(NTL):
        tsl = slice(t * 512, (t + 1) * 512)
        for og in range(DG):
            if t == 0 and og == 0:
                wt, wbuf = first_w
            else:
                wt, wbuf = load_w(wov[:, :, og * GW:(og + 1) * GW], KO, GW)
            proj_group([(wt, wbuf, 0, KO)], oT2[:, 0], b_o2[0], KO, og, g1, src=xin, sbuf=b_xin)
        if t + 1 < NTL:
            loads(t + 1)
        emit_norm(K, C, xT, b_x, KD, D, G2, Sh2, b_v, oT, b_o, banks[7], b_bk[7], tmp)
        for hg in halves:
            if not hg:
                continue
            for fg in hg:
                wa, wab = load_w(wiv[:, :, fg * 512:(fg + 1) * 512], KD, 512)
                wu, wub = load_w(wiv[:, :, DFF + fg * 512:DFF + (fg + 1) * 512], KD, 512)
                for i in range(4):
                    j = fg * 4 + i
                    jl = j - hg[0] * 4
                    ba = (2 * j) % 6
                    bu = ba + 1
                    for (wt, wbf, bk) in ((wa, wab, ba), (wu, wub, bu)):
                        for k in range(KD):
                            K.op(K.pe, lambda e, wt=wt, k=k, i=i, bk=bk: e.matmul(
                                banks[bk][:, :], lhsT=wt[:, k, i * 128:(i + 1) * 128], rhs=oT[:, k, :],
                                start=(k == 0), stop=(k == KD - 1)), [wbf, b_o], [b_bk[bk]], inc=(k == KD - 1))
                    s_ = j % 2
                    K.op(K.act, lambda e, s_=s_, ba=ba: e.activation(out=sa[:, s_, :], in_=banks[ba][:, :], func=AF.Silu),
                         [b_bk[ba]], [b_sa[s_]])
                    K.op(K.dve, lambda e, s_=s_, jl=jl, bu=bu: e.tensor_tensor(
                        out=aT[:, jl, :], in0=sa[:, s_, :], in1=banks[bu][:, :], op=ALU.mult),
                        [b_sa[s_], b_bk[bu]], [b_a])
            kbase, nkh = hg[0] * 4, len(hg) * 4
            if hg is halves[-1] or not halves[-1]:
                if after_proj is not None:
                    after_proj()
            for og in range(DG):
                pieces = []
                for (k0, kn) in _chunks(nkh, 16):
                    wt, wbuf = load_w(wfv[:, kbase + k0:kbase + k0 + kn, og * GW:(og + 1) * GW], kn, GW)
                    pieces.append((wt, wbuf, kbase + k0, kn))
                proj_group(pieces, aT, b_a, nkh, og, g2, kofs=kbase)
        K.dma(K.sp, lambda e, tsl=tsl: e.dma_start(out=xov[:, :, tsl], in_=xT[:]), [b_x], [], ds_s)
        if not last:
            emit_norm(K, C, xT, b_x, KD, D, G1n, Sh1n, b_v, oT, b_o, banks[7], b_bk[7], tmp)
            if h_store is not None:
                h_store(t, oT, b_o)
            else:
                K.dma(K.sp, lambda e, tsl=tsl: e.dma_start(out=hov[:, :, tsl], in_=oT[:, 0:KD, :]), [b_o], [], ds_s)


def _new():
    nc = bass.Bass("TRN2", target_bir_lowering=False)
    stack = contextlib.ExitStack()
    K = Kern(nc, stack)
    return nc, stack, K


def _din(nc, name, shape, dt=F32):
    return nc.dram_tensor(name, list(shape), dt, kind="ExternalInput").ap()


def _dout(nc, name, shape, dt=F32):
    return nc.dram_tensor(name, list(shape), dt, kind="ExternalOutput").ap()


def _finish(K, stack):
    K.final_wait(K.sp)
    K.emit()
    stack.close()


def build_mod(cfg):
    L, D, KD = cfg["L"], cfg["D"], cfg["KD"]
    nc, stack, K = _new()
    cst = _din(nc, "consts", [128, 640])
    cT = _din(nc, "cT", [128, KD])
    modw = _din(nc, "modw", [L, D, 6 * D])
    modb = _din(nc, "modb", [L, 6 * D])
    gmix = _din(nc, "gmix", [128, L, KD])
    gffn = _din(nc, "gffn", [128, L, KD])
    vec = _dout(nc, "vec", [128, L, 6 * KD])
    C = Consts(K, cst, K.dsems[0])
    emit_mod(K, C, cfg, cT, modw, modb, gmix, gffn, vec)
    _finish(K, stack)
    return nc


def build_norm0(cfg):
    L, D, KD, T = cfg["L"], cfg["D"], cfg["KD"], cfg["T"]
    nc, stack, K = _new()
    cst = _din(nc, "consts", [128, 640])
    xT = _din(nc, "xT", [D, T])
    vec = _din(nc, "vec", [128, L, 6 * KD])
    hT = _dout(nc, "hT", [D, T], BF16)
    C = Consts(K, cst, K.dsems[0])
    emit_norm0(K, C, cfg, xT, vec, hT)
    _finish(K, stack)
    return nc


def build_dense(cfg, last):
    L, D, KD, T, DFF, KO = cfg["L"], cfg["D"], cfg["KD"], cfg["T"], cfg["DFF"], cfg["KO"]
    nc, stack, K = _new()
    cst = _din(nc, "consts", [128, 640])
    xT = _din(nc, "xT", [D, T])
    oT = _din(nc, "oT", [KO * 128, T], BF16)
    wo = _din(nc, "wo", [KO * 128, D])
    wfi = _din(nc, "wfi", [D, 2 * DFF])
    wfo = _din(nc, "wfo", [DFF, D])
    vec = _din(nc, "vec", [128, 2, 6 * KD])
    xTo = _dout(nc, "xTo", [D, T])
    hTo = None if last else _dout(nc, "hTo", [D, T], BF16)
    C = Consts(K, cst, K.dsems[0])
    cfg2 = dict(cfg)
    cfg2["L"] = 2
    emit_dense(K, C, cfg2, 0, xT, oT, wo, wfi, wfo, vec, xTo, hTo)
    _finish(K, stack)
    return nc


def emit_hgrn(K, C, cfg, ab_idx, hT_d, w_d, lbl_d, gain_d, oT_d, h_load=None, o_store=None, after_setup=None):
    D, KD, S = cfg["D"], cfg["KD"], cfg["S"]
    ds_a, ds_w, ds_s = K.dsems[0], K.dsems[1], K.dsems[2]
    CH, NCH = 64, 8
    W = K.sb("h_W", [128, KD, 1024], BF16)
    hT = K.sb("h_hT", [128, 2, KD, 512], BF16)
    lbl = K.sb("h_lbl", [128, 2, 2], F32)
    lbv = K.sb("h_lbv", [128, 3, 2], F32)
    gain = K.sb("h_gain", [64, 256], F32)
    ones = K.sb("h_ones", [128, 64], F32)
    names = ["sig", "lf", "key", "qs", "cum", "cm", "cl", "e0", "e1", "e2"]
    hf = [{n: K.sb("h_%s%d" % (n, h), [128, 512], F32) for n in names} for h in range(2)]
    hb = [{n: K.sb("h_%s%d" % (n, h), [128, 512], BF16) for n in ["qh", "qt", "kt", "kh", "k0"]} for h in range(2)]
    eL = [K.sb("h_eL%d" % h, [128, NCH], F32) for h in range(2)]
    Sst = [K.sb("h_S%d" % h, [128, 128], F32) for h in range(2)]
    Sbf = [K.sb("h_Sb%d" % h, [128, 2, 128], BF16) for h in range(2)]
    vc = K.sb("h_vc", [64, 2, 256], BF16)
    gs = K.sb("h_gs", [64, 2, 256], F32)
    PT = K.sb("h_PT", [64, 2, 2, 64], BF16)
    khs = K.sb("h_khs", [64, 2, 2, 128], BF16)
    junk = K.sb("h_junk", [64, 128], F32)
    ssq = K.sb("h_ssq", [64, 2, 2], F32)
    og = K.sb("h_og", [64, 2, 128], BF16)
    oTt = K.sb("h_oTt", [128, 2, 2, 512], BF16)
    fb2 = [K.ps("h_fb%d" % i, [128, 512], F32) for i in range(2)]
    fb = [fb2[0], fb2[1], fb2[0], fb2[1]]
    tb = [K.ps("h_tb%d" % i, [128, 512], F32) for i in range(2)]
    bkO = K.ps("h_bkO", [128, 512], F32)
    bkU = K.ps("h_bkU", [128, 512], F32)
    bkT = [K.ps("h_bkT%d" % i, [128, 1024], BF16) for i in range(2)]
    B = lambda n: [Buf() for _ in range(n)]
    b_W, b_lb, b_gain, b_ones = Buf(), Buf(), Buf(), Buf()
    b_hT, b_tb = B(2), B(2)
    b_fb2 = B(2)
    b_fb = [b_fb2[0], b_fb2[1], b_fb2[0], b_fb2[1]]
    b_hf = [{n: Buf() for n in names} for _ in range(2)]
    b_hb = [{n: Buf() for n in ["qh", "qt", "kt", "kh", "k0"]} for _ in range(2)]
    b_eL, b_S, b_Sbf = B(2), B(2), [B(2), B(2)]
    b_vc, b_gs, b_PT, b_khs, b_junk, b_ssq, b_og = B(2), B(2), B(2), B(2), Buf(), B(2), B(2)
    b_bkO, b_bkU, b_bkT = Buf(), Buf(), B(2)
    b_oTt = B(2)
    A = K.op
    wv = w_d.rearrange("(k p) c -> p k c", p=128)
    for half in range(2):
        K.dma(K.pool, lambda e, half=half: e.dma_start(out=W[:, :, half * 512:(half + 1) * 512],
                                                       in_=wv[:, :, half * 512:(half + 1) * 512]), [], [b_W], ds_w)
    if after_setup is not None:
        after_setup()
    K.dma(K.sp, lambda e: e.dma_start(out=lbl[:], in_=lbl_d[:, :, :]), [], [b_lb], ds_a)
    K.dma(K.sp, lambda e: e.dma_start(out=gain[:], in_=gain_d[0:1, :].partition_broadcast(64)), [], [b_gain], ds_a)
    A(K.dve, lambda e: e.memset(ones[:], 1.0), [], [b_ones])
    if ab_idx == 0:
        A(K.dve, lambda e: e.memset(lbv[:, 0, :], 0.0), [b_lb], [b_lb])
    else:
        A(K.dve, lambda e: e.tensor_tensor(out=lbv[:, 0, :], in0=lbl[:, 1, :], in1=lbl[:, 0, :], op=ALU.subtract),
          [b_lb], [b_lb])
        A(K.act, lambda e: e.activation(out=lbv[:, 0, :], in_=lbv[:, 0, :], func=AF.Sigmoid), [b_lb], [b_lb])
        A(K.dve, lambda e: e.tensor_scalar(out=lbv[:, 0, :], in0=lbv[:, 0, :], scalar1=1.0 - 1e-6, scalar2=0.0,
                                           op0=ALU.min, op1=ALU.max), [b_lb], [b_lb])
    A(K.dve, lambda e: e.tensor_scalar(out=lbv[:, 1, :], in0=lbv[:, 0, :], scalar1=-1.0, scalar2=1.0,
                                       op0=ALU.mult, op1=ALU.add), [b_lb], [b_lb])
    A(K.dve, lambda e: e.tensor_scalar(out=lbv[:, 2, :], in0=lbv[:, 0, :], scalar1=1.0, scalar2=-1.0,
                                       op0=ALU.mult, op1=ALU.add), [b_lb], [b_lb])
    for h in range(2):
        A(K.dve, lambda e, h=h: e.memset(PT[:, h], 0.0), [], [b_PT[h]])
        A(K.dve, lambda e, h=h: e.memset(hb[h]["k0"][:], 0.0), [], [b_hb[h]["k0"]])
        A(K.dve, lambda e, h=h: e.memset(hf[h]["e2"][:], 0.0), [], [b_hf[h]["e2"]])
        A(K.dve, lambda e, h=h: e.memset(Sst[h][:], 0.0), [], [b_S[h]])
        A(K.dve, lambda e, h=h: e.memset(Sbf[h][:, 0, :], 0.0), [], [b_Sbf[h][0]])
    hv = None if h_load is not None else hT_d.rearrange("(k p) t -> p k t", p=128)
    ov = None if o_store is not None else oT_d.rearrange("(h p) t -> p h t", p=128)
    cidx = 0
    for t in range(S // 512):
        s = t % 2
        if h_load is not None:
            h_load(t, hT[:, s], b_hT[s])
        else:
            K.dma(K.sp, lambda e, t=t, s=s: e.dma_start(out=hT[:, s], in_=hv[:, :, t * 512:(t + 1) * 512]),
                  [], [b_hT[s]], ds_a)
        for h in range(2):
            for blk in (2 * h, 2 * h + 1):
                for k in range(KD):
                    A(K.pe, lambda e, blk=blk, k=k, s=s: e.matmul(fb[blk][:, :], lhsT=W[:, k, blk * 128:(blk + 1) * 128],
                                                                  rhs=hT[:, s, k, :], start=(k == 0), stop=(k == KD - 1)),
                      [b_W, b_hT[s]], [b_fb[blk]], inc=(k == KD - 1))
            f, bf_, bb, bbf = hf[h], hb[h], b_hf[h], b_hb[h]
            lb_, oml, noml = lbv[:, 0, h:h + 1], lbv[:, 1, h:h + 1], lbv[:, 2, h:h + 1]
            A(K.act, lambda e, f=f, h=h: e.activation(out=f["sig"][:], in_=fb[2 * h + 1][:, :], func=AF.Sigmoid),
              [b_fb[2 * h + 1]], [bb["sig"]])
            A(K.act, lambda e, f=f, h=h: e.activation(out=f["qs"][:], in_=fb[2 * h][:, :], func=AF.Silu),
              [b_fb[2 * h]], [bb["qs"]])
            A(K.dve, lambda e, f=f, oml=oml, lb_=lb_: e.tensor_scalar(out=f["lf"][:], in0=f["sig"][:], scalar1=oml,
                                                                     scalar2=lb_, op0=ALU.mult, op1=ALU.add),
              [bb["sig"], b_lb], [bb["lf"]])
            A(K.dve, lambda e, f=f: e.tensor_scalar_max(out=f["lf"][:], in0=f["lf"][:], scalar1=1e-30),
              [bb["lf"]], [bb["lf"]])
            A(K.act, lambda e, f=f: e.activation(out=f["lf"][:], in_=f["lf"][:], func=AF.Ln), [bb["lf"]], [bb["lf"]])
            A(K.dve, lambda e, f=f, oml=oml, noml=noml: e.tensor_scalar(out=f["key"][:], in0=f["sig"][:], scalar1=noml,
                                                                       scalar2=oml, op0=ALU.mult, op1=ALU.add),
              [bb["sig"], b_lb], [bb["key"]])
            for c in range(NCH):
                A(K.dve, lambda e, f=f, c=c: e.tensor_tensor_scan(out=f["cum"][:, c * CH:(c + 1) * CH], data0=ones[:, :],
                                                                  data1=f["lf"][:, c * CH:(c + 1) * CH], initial=0.0,
                                                                  op0=ALU.mult, op1=ALU.add),
                  [bb["lf"], b_ones], [bb["cum"]])
            c3 = lambda a: a[:].rearrange("p (c t) -> p c t", c=NCH)
            A(K.dve, lambda e, f=f: e.tensor_tensor(out=c3(f["cm"]), in0=c3(f["cum"]),
                                                    in1=c3(f["cum"])[:, :, 31:32].broadcast_to([128, NCH, CH]),
                                                    op=ALU.subtract), [bb["cum"]], [bb["cm"]])
            A(K.dve, lambda e, f=f: e.tensor_tensor(out=c3(f["cl"]), in0=c3(f["cum"]),
                                                    in1=c3(f["cum"])[:, :, CH - 1:CH].broadcast_to([128, NCH, CH]),
                                                    op=ALU.subtract), [bb["cum"]], [bb["cl"]])
            A(K.act, lambda e, f=f, h=h: e.activation(out=eL[h][:, :], in_=c3(f["cum"])[:, :, CH - 1], func=AF.Exp),
              [bb["cum"]], [b_eL[h]])
            A(K.act, lambda e, f=f: e.activation(out=c3(f["e2"])[:, :, 0:32], in_=c3(f["cum"])[:, :, 0:32], func=AF.Exp,
                                                 scale=-1.0), [bb["cum"]], [bb["e2"]])
            A(K.dve, lambda e, f=f, bf_=bf_: e.tensor_tensor(out=c3(bf_["k0"])[:, :, 0:32], in0=c3(f["key"])[:, :, 0:32],
                                                             in1=c3(f["e2"])[:, :, 0:32], op=ALU.mult),
              [bb["key"], bb["e2"]], [bbf["k0"]])
            for (src, sc, es, mul, dst) in (("cum", 1.0, "e0", "qs", "qh"), ("cm", 1.0, "e1", "qs", "qt"),
                                            ("cm", -1.0, "e0", "key", "kt"), ("cl", -1.0, "e1", "key", "kh")):
                A(K.act, lambda e, f=f, src=src, sc=sc, es=es: e.activation(out=f[es][:], in_=f[src][:], func=AF.Exp,
                                                                           scale=sc), [bb[src]], [bb[es]])
                A(K.dve, lambda e, f=f, bf_=bf_, es=es, mul=mul, dst=dst: e.tensor_tensor(
                    out=bf_[dst][:], in0=f[mul][:], in1=f[es][:], op=ALU.mult), [bb[mul], bb[es]], [bbf[dst]])
        def prep_proj(c, s=s):
            csl = slice(c * CH, (c + 1) * CH)
            par = c % 2
            for k in range(KD):
                A(K.pe, lambda e, k=k: e.matmul(tb[par][0:CH, :], lhsT=hT[:, s, k, csl], rhs=W[:, k, 512:1024],
                                                start=(k == 0), stop=(k == KD - 1)),
                  [b_W, b_hT[s]], [b_tb[par]], inc=(k == KD - 1))
            A(K.act, lambda e: e.activation(out=vc[:, par, :], in_=tb[par][0:CH, 0:256], func=AF.Identity),
              [b_tb[par]], [b_vc[par]])
            A(K.act, lambda e: e.activation(out=gs[:, par, :], in_=tb[par][0:CH, 256:512], func=AF.Silu),
              [b_tb[par]], [b_gs[par]])
            A(K.dve, lambda e: e.tensor_tensor(out=gs[:, par, :], in0=gs[:, par, :], in1=gain[:, :], op=ALU.mult),
              [b_gs[par], b_gain], [b_gs[par]])

        def prep(c, s=s):
            csl = slice(c * CH, (c + 1) * CH)
            c0_ = c * CH
            par = c % 2
            if c >= 2:
                prep_proj(c)
            for h in range(2):
                bf_, bbf = hb[h], b_hb[h]
                A(K.pe, lambda e, h=h, bf_=bf_: e.matmul(fb2[par][0:64, h * 64 + 32:h * 64 + 64], lhsT=bf_["kt"][:, csl],
                                                         rhs=bf_["qt"][:, c0_ + 32:c0_ + 64], start=True, stop=True),
                  [bbf["kt"], bbf["qt"]], [b_fb2[par]], inc=False)
                A(K.pe, lambda e, h=h, bf_=bf_: e.matmul(fb2[par][0:32, h * 64:h * 64 + 32], lhsT=bf_["k0"][:, c0_:c0_ + 32],
                                                         rhs=bf_["qh"][:, c0_:c0_ + 32], start=True, stop=True),
                  [bbf["k0"], bbf["qh"]], [b_fb2[par]], inc=False)
                A(K.pe, lambda e, h=h, bf_=bf_: e.transpose(bkT[par][0:64, h * 128:(h + 1) * 128], bf_["kh"][:, csl],
                                                            C.ident_b()), [bbf["kh"], C.buf], [b_bkT[par]], inc=(h == 1))
            scv = fb2[par][0:64, 0:128].rearrange("p (h t) -> p h t", h=2)
            A(K.dve, lambda e: e.tensor_tensor(out=PT[:, par, :, 32:64], in0=scv[:, :, 32:64],
                                               in1=C.f[0:64, 160:192].unsqueeze(1).broadcast_to([64, 2, 32]),
                                               op=ALU.mult), [b_fb2[par], C.buf], [b_PT[par]])
            A(K.dve, lambda e: e.tensor_tensor(out=PT[0:32, par, :, 0:32], in0=scv[0:32, :, 0:32],
                                               in1=C.f[0:32, 128:160].unsqueeze(1).broadcast_to([32, 2, 32]),
                                               op=ALU.mult), [b_fb2[par], C.buf], [b_PT[par]])
            A(K.act, lambda e: e.activation(out=khs[:, par].rearrange("p h d -> p (h d)"), in_=bkT[par][0:64, 0:256],
                                            func=AF.Identity), [b_bkT[par]], [b_khs[par]])

        def main(c, s=s):
            csl = slice(c * CH, (c + 1) * CH)
            par = c % 2
            sp_, sn = c % 2, (c + 1) % 2
            for h in range(2):
                bf_, bbf = hb[h], b_hb[h]
                A(K.pe, lambda e, h=h: e.matmul(bkO[0:64, h * 128:(h + 1) * 128], lhsT=PT[:, par, h, :],
                                                rhs=vc[:, par, h * 128:(h + 1) * 128], start=True, stop=False),
                  [b_PT[par], b_vc[par]], [b_bkO], inc=False)
                A(K.pe, lambda e, h=h, bf_=bf_: e.matmul(bkO[0:64, h * 128:(h + 1) * 128], lhsT=bf_["qh"][:, csl],
                                                         rhs=Sbf[h][:, sp_, :], start=False, stop=True),
                  [bbf["qh"], b_Sbf[h][sp_]], [b_bkO], inc=(h == 1))
            for h in range(2):
                A(K.pe, lambda e, h=h: e.matmul(bkU[:, h * 128:(h + 1) * 128], lhsT=khs[:, par, h, :],
                                                rhs=vc[:, par, h * 128:(h + 1) * 128], start=True, stop=True),
                  [b_khs[par], b_vc[par]], [b_bkU], inc=(h == 1))
            for h in range(2):
                A(K.dve, lambda e, h=h: e.scalar_tensor_tensor(out=Sst[h][:], in0=Sst[h][:], scalar=eL[h][:, c:c + 1],
                                                               in1=bkU[:, h * 128:(h + 1) * 128], op0=ALU.mult,
                                                               op1=ALU.add), [b_S[h], b_eL[h], b_bkU], [b_S[h]])
                A(K.act, lambda e, h=h: e.activation(out=Sbf[h][:, sn, :], in_=Sst[h][:], func=AF.Identity),
                  [b_S[h]], [b_Sbf[h][sn]])

        def post_a(c, s=s):
            par = c % 2
            for h in range(2):
                A(K.act, lambda e, h=h: e.activation(out=junk[:, :], in_=bkO[0:64, h * 128:(h + 1) * 128], func=AF.Square,
                                                     accum_out=ssq[:, h, 0:1]), [b_bkO], [b_junk, b_ssq[h]])
                A(K.act, lambda e, h=h: e.activation(out=ssq[:, h, 1:2], in_=ssq[:, h, 0:1], func=AF.Sqrt,
                                                     scale=1.0 / 128, bias=EPS), [b_ssq[h]], [b_ssq[h]])
                A(K.dve, lambda e, h=h: e.reciprocal(out=ssq[:, h, 1:2], in_=ssq[:, h, 1:2]), [b_ssq[h]], [b_ssq[h]])
                A(K.dve, lambda e, h=h: e.scalar_tensor_tensor(out=og[:, h, :], in0=bkO[0:64, h * 128:(h + 1) * 128],
                                                               scalar=ssq[:, h, 1:2],
                                                               in1=gs[:, par, h * 128:(h + 1) * 128],
                                                               op0=ALU.mult, op1=ALU.mult),
                  [b_bkO, b_ssq[h], b_gs[par]], [b_og[h]])

        def post_b(c, s=s):
            csl = slice(c * CH, (c + 1) * CH)
            par = (c + 1) % 2
            for h in range(2):
                A(K.pe, lambda e, h=h: e.transpose(bkT[par][:, 256 + h * 64:256 + (h + 1) * 64], og[:, h, :],
                                                   C.ident_b(64)), [b_og[h], C.buf], [b_bkT[par]], inc=(h == 1))
            A(K.act, lambda e: e.activation(out=oTt[:, s, :, csl],
                                            in_=bkT[par][:, 256:384].rearrange("p (h t) -> p h t", h=2),
                                            func=AF.Identity), [b_bkT[par]], [b_oTt[s]])

        prep_proj(0)
        prep_proj(1)
        prep(0)
        for c in range(NCH):
            if c + 1 < NCH:
                prep(c + 1)
            main(c)
            if c >= 1:
                post_b(c - 1)
            post_a(c)
        post_b(NCH - 1)
        if o_store is not None:
            o_store(t, oTt[:, s], b_oTt[s], 2)
        else:
            K.dma(K.pool, lambda e, t=t, s=s: e.dma_start(out=ov[:, :, t * 512:(t + 1) * 512], in_=oTt[:, s]),
                  [b_oTt[s]], [], ds_s)


def build_hgrn(cfg, ab_idx):
    D, S = cfg["D"], cfg["S"]
    nc, stack, K = _new()
    cst = _din(nc, "consts", [128, 640])
    hT = _din(nc, "hT", [D, S], BF16)
    w = _din(nc, "w", [D, 1024])
    lbl = _din(nc, "lbl", [128, 2, 2])
    gain = _din(nc, "gain", [1, 256])
    oT = _dout(nc, "oT", [256, S], BF16)
    C = Consts(K, cst, K.dsems[0])
    emit_hgrn(K, C, cfg, ab_idx, hT, w, lbl, gain, oT)
    _finish(K, stack)
    return nc


def emit_fox(K, C, cfg, hT_d, w_d, qkg_d, fbias_d, oT_d, h_load=None, o_store=None, after_setup=None):
    D, KD, S = cfg["D"], cfg["KD"], cfg["S"]
    NB = S // 128
    A = K.op
    W = K.sb("x_W", [128, KD, 1026], BF16)
    hT = K.sb("x_hT", [128, 2, KD, 512], BF16)
    kTa = K.sb("x_kTa", [128, 2, S], BF16)
    va = K.sb("x_va", [128, 2, NB, 132], BF16)
    Ga = K.sb("x_Ga", [128, 2, NB], F32)
    qkg = K.sb("x_qkg", [128, 2], F32)
    nfb = K.sb("x_nfb", [128, 2], F32)
    qn = K.sb("x_qn", [128, 2, 512], BF16)
    sq = K.sb("x_sq", [128, 2, 512], F32)
    rt = K.sb("x_rt", [128, 2, 512], F32)
    sg = K.sb("x_sg", [128, 4, 256], F32)
    spv = K.sb("x_spv", [128, 2, 4], F32)
    lcs = K.sb("x_lcs", [128, 2, 4], F32)
    tots = K.sb("x_tots", [128, 2, 4], F32)
    incl = K.sb("x_incl", [128, 2, 2, 4], F32)
    excl = K.sb("x_excl", [128, 2, 4], F32)
    Bq = K.sb("x_Bq", [128, 2, NB], F32)
    PT = K.sb("x_PT", [128, 3, 512], BF16)
    rden = K.sb("x_rden", [128, 4], F32)
    dqc = K.sb("x_dqc", [128, 2, 4], F32)
    dqb = K.sb("x_dqb", [1, 2, 512], F32)
    dqbc = K.sb("x_dqbc", [128, 2, 512], F32)
    stg = K.sb("x_stg", [128, 3, 512], F32)
    b_dqc, b_dqb, b_dqbc = Buf(), Buf(), Buf()
    b_stg = [Buf() for _ in range(3)]
    og = K.sb("x_og", [128, 2, 128], BF16)
    oTt = K.sb("x_oTt", [128, 2, 2, 512], BF16)
    bk = [K.ps("x_bk%d" % i, [128, 512], F32) for i in range(7)]
    bkT = K.ps("x_bkT", [128, 1024], BF16)
    b_bk = [Buf() for _ in range(7)]
    b_T = Buf()
    P0, P1, N_, V_, Fb, O3, P2 = range(7)
    OB = [N_, V_, Fb, O3]
    B = lambda n: [Buf() for _ in range(n)]
    b_W, b_kTa, b_va, b_Ga, b_par = Buf(), Buf(), Buf(), Buf(), Buf()
    b_hT, b_qn, b_sq, b_rt, b_oTt = B(2), B(2), B(2), B(2), B(2)
    b_sg, b_spv, b_lcs, b_tots, b_excl = Buf(), Buf(), Buf(), Buf(), Buf()
    b_incl, b_Bq = B(2), Buf()
    b_PT = B(3)
    b_rden, b_og = Buf(), B(2)
    wv = w_d.rearrange("(k p) c -> p k c", p=128)
    K.dma(K.pool, lambda e: e.dma_start(out=W[:, :, 0:512], in_=wv[:, :, 0:512]), [], [b_W])
    K.dma(K.pool, lambda e: e.dma_start(out=W[:, :, 512:1026], in_=wv[:, :, 512:1026]), [], [b_W])
    if after_setup is not None:
        after_setup()
    K.dma(K.sp, lambda e: e.dma_start(out=qkg[:], in_=qkg_d[:, :]), [], [b_par])
    K.dma(K.sp, lambda e: e.dma_start(out=nfb[:], in_=fbias_d[0:1, :].partition_broadcast(128)), [], [b_par])
    A(K.dve, lambda e: e.tensor_scalar(out=qkg[:, 0:1], in0=qkg[:, 0:1], scalar1=float(128 ** -0.5), scalar2=None,
                                       op0=ALU.mult), [b_par], [b_par])
    A(K.dve, lambda e: e.tensor_scalar(out=nfb[:, :], in0=nfb[:, :], scalar1=-1.0, scalar2=None, op0=ALU.mult),
      [b_par], [b_par])
    A(K.dve, lambda e: e.memset(va[:], 1.0), [], [b_va])
    hv = None if h_load is not None else hT_d.rearrange("(k p) t -> p k t", p=128)
    ov = None if o_store is not None else oT_d.rearrange("(h p) t -> p h t", p=128)
    xs = 0
    for t in range(S // 512):
        s = t % 2
        tsl = slice(t * 512, (t + 1) * 512)
        if h_load is not None:
            h_load(t, hT[:, s], b_hT[s])
        else:
            K.dma(K.sp, lambda e, s=s, tsl=tsl: e.dma_start(out=hT[:, s], in_=hv[:, :, tsl]), [], [b_hT[s]])
        for h in range(2):
            for (which, bank) in ((0, P0), (1, P1)):
                blk = 2 * h + which
                for k in range(KD):
                    A(K.pe, lambda e, blk=blk, k=k, s=s, bank=bank: e.matmul(
                        bk[bank][:, :], lhsT=W[:, k, blk * 128:(blk + 1) * 128], rhs=hT[:, s, k, :],
                        start=(k == 0), stop=(k == KD - 1)), [b_W, b_hT[s]], [b_bk[bank]], inc=(k == KD - 1))
                x = which
                A(K.act, lambda e, x=x, bank=bank: e.activation(out=sq[:, x, :], in_=bk[bank][:, :], func=AF.Square),
                  [b_bk[bank]], [b_sq[x]])
                A(K.pe, lambda e, x=x: e.matmul(bk[N_][:, :], lhsT=C.ones_f(), rhs=sq[:, x, :], start=True, stop=True),
                  [b_sq[x], C.buf], [b_bk[N_]])
                A(K.act, lambda e, x=x: e.activation(out=rt[:, x, :], in_=bk[N_][:, :], func=AF.Sqrt, scale=1.0 / 128,
                                                     bias=EPS), [b_bk[N_]], [b_rt[x]])
                A(K.dve, lambda e, x=x: e.reciprocal(out=rt[:, x, :], in_=rt[:, x, :]), [b_rt[x]], [b_rt[x]])
                if which == 0:
                    A(K.dve, lambda e, h=h, x=x, bank=bank: e.scalar_tensor_tensor(
                        out=qn[:, h, :], in0=bk[bank][:, :], scalar=qkg[:, 0:1], in1=rt[:, x, :], op0=ALU.mult,
                        op1=ALU.mult), [b_bk[bank], b_par, b_rt[x]], [b_qn[h]])
                else:
                    A(K.dve, lambda e, h=h, x=x, bank=bank, tsl=tsl: e.scalar_tensor_tensor(
                        out=kTa[:, h, tsl], in0=bk[bank][:, :], scalar=qkg[:, 1:2], in1=rt[:, x, :], op0=ALU.mult,
                        op1=ALU.mult), [b_bk[bank], b_par, b_rt[x]], [b_kTa])
        for b in range(4):
            blk = 4 * t + b
            bsl = slice(b * 128, (b + 1) * 128)
            for k in range(KD):
                A(K.pe, lambda e, k=k, s=s, bsl=bsl: e.matmul(bk[V_][:, :], lhsT=hT[:, s, k, bsl], rhs=W[:, k, 512:1024],
                                                              start=(k == 0), stop=(k == KD - 1)),
                  [b_W, b_hT[s]], [b_bk[V_]], inc=(k == KD - 1))
            A(K.act, lambda e, blk=blk: e.activation(out=va[:, :, blk, 0:128],
                                                     in_=bk[V_][:, 0:256].rearrange("p (h e) -> p h e", h=2),
                                                     func=AF.Identity), [b_bk[V_]], [b_va])
            A(K.act, lambda e, b=b: e.activation(out=sg[:, b, :], in_=bk[V_][:, 256:512], func=AF.Sigmoid),
              [b_bk[V_]], [b_sg])
            for k in range(KD):
                A(K.pe, lambda e, k=k, s=s, bsl=bsl, b=b: e.matmul(bk[Fb][:, 2 * b:2 * b + 2], lhsT=hT[:, s, k, bsl],
                                                                   rhs=W[:, k, 1024:1026], start=(k == 0),
                                                                   stop=(k == KD - 1)),
                  [b_W, b_hT[s]], [b_bk[Fb]], inc=(k == KD - 1))
        fbv = bk[Fb][:, 0:8].rearrange("p (b h) -> p h b", h=2)
        for h in range(2):
            A(K.act, lambda e, h=h: e.activation(out=spv[:, h, :], in_=fbv[:, h, :], func=AF.Exp, scale=-1.0,
                                                 bias=nfb[:, h:h + 1]), [b_bk[Fb], b_par], [b_spv])
        A(K.act, lambda e: e.activation(out=spv[:], in_=spv[:], func=AF.Ln, bias=1.0, scale=1.0), [b_spv], [b_spv])
        A(K.pe, lambda e: e.matmul(bk[Fb][:, 16:24], lhsT=C.tri_f(), rhs=spv[:].rearrange("p h b -> p (h b)"),
                                   start=True, stop=True), [b_spv, C.buf], [b_bk[Fb]])
        A(K.act, lambda e: e.activation(out=lcs[:].rearrange("p h b -> p (h b)"), in_=bk[Fb][:, 16:24],
                                        func=AF.Identity), [b_bk[Fb]], [b_lcs])
        A(K.pe, lambda e: e.matmul(bk[Fb][:, 32:40], lhsT=C.sel_f(), rhs=lcs[:].rearrange("p h b -> p (h b)"),
                                   start=True, stop=True), [b_lcs, C.buf], [b_bk[Fb]])
        A(K.act, lambda e: e.activation(out=tots[:].rearrange("p h b -> p (h b)"), in_=bk[Fb][:, 32:40],
                                        func=AF.Identity), [b_bk[Fb]], [b_tots])
        for h in range(2):
            init = 0.0 if t == 0 else incl[:, 1 - s, h, 3:4]
            A(K.dve, lambda e, h=h, s=s, init=init: e.tensor_tensor_scan(
                out=incl[:, s, h, :], data0=C.ones_f(128, 4), data1=tots[:, h, :], initial=init, op0=ALU.mult,
                op1=ALU.add), [b_tots, C.buf, b_incl[1 - s]], [b_incl[s]])
        A(K.dve, lambda e, s=s: e.tensor_tensor(out=excl[:], in0=incl[:, s], in1=tots[:], op=ALU.subtract),
          [b_incl[s], b_tots], [b_excl])
        A(K.dve, lambda e, t=t: e.tensor_tensor(out=Ga[:, :, 4 * t:4 * t + 4], in0=lcs[:], in1=excl[:], op=ALU.add),
          [b_lcs, b_excl], [b_Ga])
        nkb = 4 * t + 4
        for h in range(2):
            A(K.dve, lambda e, h=h, s=s, nkb=nkb: e.tensor_scalar(
                out=Bq[:, h, 0:nkb], in0=Ga[:, h, 0:nkb], scalar1=incl[:, s, h, 3:4], scalar2=None,
                op0=ALU.subtract), [b_Ga, b_incl[s]], [b_Bq])
            A(K.dve, lambda e, h=h, s=s, t=t: e.tensor_scalar(
                out=dqc[:, h, :], in0=Ga[:, h, 4 * t:4 * t + 4], scalar1=incl[:, s, h, 3:4], scalar2=-1.0,
                op0=ALU.subtract, op1=ALU.mult), [b_Ga, b_incl[s]], [b_dqc])
        for h in range(2):
            for jl in range(4):
                A(K.pe, lambda e, h=h, jl=jl: e.transpose(bk[O3][0:1, jl * 128:(jl + 1) * 128], dqc[:, h, jl:jl + 1],
                                                          C.ident_f()), [b_dqc, C.buf], [b_bk[O3]])
            A(K.act, lambda e, h=h: e.activation(out=dqb[0:1, h, :], in_=bk[O3][0:1, :], func=AF.Identity),
              [b_bk[O3]], [b_dqb])
            A(K.pe, lambda e, h=h: e.matmul(bk[O3][:, :], lhsT=C.ones_f(1, 128), rhs=dqb[0:1, h, :], start=True,
                                            stop=True), [b_dqb, C.buf], [b_bk[O3]])
            A(K.act, lambda e, h=h: e.activation(out=dqbc[:, h, :], in_=bk[O3][:, :], func=AF.Identity),
              [b_bk[O3]], [b_dqbc])
        SB = (P0, P1, P2)
        for h in range(2):
            def score(kb, h=h):
                m = max(0, kb - 4 * t)
                x3 = kb % 3
                sb_ = SB[x3]
                diag = kb >= 4 * t
                A(K.pe, lambda e: e.matmul(bk[sb_][:, m * 128:512], lhsT=kTa[:, h, kb * 128:(kb + 1) * 128],
                                           rhs=qn[:, h, m * 128:512], start=True, stop=(not diag)),
                  [b_kTa, b_qn[h]], [b_bk[sb_]], inc=(not diag))
                if diag:
                    A(K.pe, lambda e: e.matmul(bk[sb_][:, m * 128:(m + 1) * 128], lhsT=C.ident_b(), rhs=C.negmask_b(),
                                               start=False, stop=True), [C.buf], [b_bk[sb_]])
                A(K.dve, lambda e: e.tensor_tensor(out=stg[:, x3, m * 128:512], in0=bk[sb_][:, m * 128:512],
                                                   in1=dqbc[:, h, m * 128:512], op=ALU.add),
                  [b_bk[sb_], b_dqbc], [b_stg[x3]])
                A(K.act, lambda e: e.activation(out=PT[:, x3, m * 128:512], in_=stg[:, x3, m * 128:512], func=AF.Exp,
                                                bias=Bq[:, h, kb:kb + 1], scale=1.0), [b_stg[x3], b_Bq], [b_PT[x3]])

            def pv(kb, h=h):
                m = max(0, kb - 4 * t)
                x3 = kb % 3
                for jl in range(m, 4):
                    j = 4 * t + jl
                    A(K.pe, lambda e, jl=jl, j=j: e.matmul(bk[OB[jl]][:, 0:129], lhsT=PT[:, x3, jl * 128:(jl + 1) * 128],
                                                           rhs=va[:, h, kb, 0:129], start=(kb == 0), stop=(kb == j)),
                      [b_PT[x3], b_va], [b_bk[OB[jl]]], inc=(jl == 3 or kb == j))
            score(0)
            score(1)
            for kb in range(2, nkb):
                score(kb)
                pv(kb - 2)
            pv(nkb - 2)
            pv(nkb - 1)
            for jl in range(4):
                A(K.dve, lambda e, jl=jl: e.reciprocal(out=rden[:, jl:jl + 1], in_=bk[OB[jl]][:, 128:129]),
                  [b_bk[OB[jl]]], [b_rden])
                x = jl % 2
                A(K.dve, lambda e, jl=jl, h=h, x=x: e.scalar_tensor_tensor(
                    out=og[:, x, :], in0=bk[OB[jl]][:, 0:128], scalar=rden[:, jl:jl + 1],
                    in1=sg[:, jl, h * 128:(h + 1) * 128], op0=ALU.mult, op1=ALU.mult),
                    [b_bk[OB[jl]], b_rden, b_sg], [b_og[x]])
                A(K.pe, lambda e, x=x: e.transpose(bkT[:, 0:128], og[:, x, :], C.ident_b()), [b_og[x], C.buf], [b_T])
                A(K.act, lambda e, s=s, h=h, jl=jl: e.activation(out=oTt[:, s, h, jl * 128:(jl + 1) * 128],
                                                                 in_=bkT[:, 0:128], func=AF.Identity),
                  [b_T], [b_oTt[s]])
        if o_store is not None:
            o_store(t, oTt[:, s], b_oTt[s], 2)
        else:
            K.dma(K.pool, lambda e, s=s, tsl=tsl: e.dma_start(out=ov[:, :, tsl], in_=oTt[:, s]), [b_oTt[s]], [])


def build_fox(cfg):
    D, S = cfg["D"], cfg["S"]
    nc, stack, K = _new()
    cst = _din(nc, "consts", [128, 640])
    hT = _din(nc, "hT", [D, S], BF16)
    w = _din(nc, "w", [D, 1026])
    qkg = _din(nc, "qkg", [128, 2])
    fbias = _din(nc, "fbias", [1, 2])
    oT = _dout(nc, "oT", [256, S], BF16)
    C = Consts(K, cst, None)
    emit_fox(K, C, cfg, hT, w, qkg, fbias, oT)
    _finish(K, stack)
    return nc


def emit_gla(K, C, cfg, hT_d, w_d, wgu_d, bg_d, gain_d, oT_d, h_load=None, o_store=None, after_setup=None):
    D, KD, S = cfg["D"], cfg["KD"], cfg["S"]
    A = K.op
    CH, NCH = 128, 4
    NW = 1552
    W = K.sb("g_W", [128, KD, NW], BF16)
    hT = K.sb("g_hT", [128, 2, KD, 512], BF16)
    wgu = K.sb("g_wgu", [16, 256], F32)
    bg = K.sb("g_bg", [128, 2], F32)
    gain = K.sb("g_gain", [128, 512], F32)
    ones = K.sb("g_ones", [128, 128], F32)
    glT = K.sb("g_glT", [16, 512], F32)
    spt = K.sb("g_sp", [128, 2, 512], F32)
    csp = K.sb("g_csp", [128, 2, 512], F32)
    Eq = K.sb("g_Eq", [128, 2, 512], F32)
    Ek = K.sb("g_Ek", [128, 2, 512], F32)
    eL = K.sb("g_eL", [128, 2, NCH], F32)
    qh = K.sb("g_qh", [128, 2, 512], BF16)
    kt = K.sb("g_kt", [128, 2, 512], BF16)
    kh = K.sb("g_kh", [128, 2, 512], BF16)
    Sst = K.sb("g_S", [128, 2, 512], F32)
    Sbf = K.sb("g_Sbf", [128, 2, 2, 512], BF16)
    vc = K.sb("g_vc", [128, 2, 512], BF16)
    gs = K.sb("g_gs", [128, 2, 512], F32)
    PT = K.sb("g_PT", [128, 2, 128], BF16)
    khs = K.sb("g_khs", [128, 2, 256], BF16)
    junk = K.sb("g_junk", [128, 512], F32)
    ssq = K.sb("g_ssq", [128, 2], F32)
    og = K.sb("g_og", [128, 512], BF16)
    oTt = K.sb("g_oTt", [128, 2, 4, 512], BF16)
    bk = [K.ps("g_bk%d" % i, [128, 512], F32) for i in range(6)]
    bkT = K.ps("g_bkT", [128, 1024], BF16)
    bkT2 = K.ps("g_bkT2", [128, 1024], BF16)
    b_bk = [Buf() for _ in range(6)]
    b_T, b_T2 = Buf(), Buf()
    GA, Z0, Z1, V_, Gt, O_ = range(6)
    B = lambda n: [Buf() for _ in range(n)]
    b_W, b_par, b_ones, b_glT = Buf(), Buf(), Buf(), Buf()
    b_hT, b_sp, b_csp, b_Eq, b_Ek, b_eL = B(2), B(2), B(2), B(2), B(2), B(2)
    b_qh, b_kt, b_kh, b_S = B(2), B(2), B(2), B(2)
    b_Sbf = [B(2), B(2)]
    b_vc, b_gs, b_PT, b_khs, b_oTt = B(2), B(2), B(2), B(2), B(2)
    b_junk, b_ssq, b_og = Buf(), Buf(), Buf()
    wv = w_d.rearrange("(k p) c -> p k c", p=128)
    for (c0, c1) in ((0, 528), (528, 1040), (1040, 1552)):
        K.dma(K.pool, lambda e, c0=c0, c1=c1: e.dma_start(out=W[:, :, c0:c1], in_=wv[:, :, c0:c1]), [], [b_W])
    if after_setup is not None:
        after_setup()
    K.dma(K.sp, lambda e: e.dma_start(out=wgu[:], in_=wgu_d[:, :]), [], [b_par])
    K.dma(K.sp, lambda e: e.dma_start(out=bg[:], in_=bg_d[:, :]), [], [b_par])
    K.dma(K.sp, lambda e: e.dma_start(out=gain[:], in_=gain_d[0:1, :].partition_broadcast(128)), [], [b_par])
    A(K.dve, lambda e: e.tensor_scalar(out=bg[:, :], in0=bg[:, :], scalar1=-1.0, scalar2=None, op0=ALU.mult),
      [b_par], [b_par])
    A(K.dve, lambda e: e.memset(ones[:], 1.0), [], [b_ones])
    for dc in range(2):
        A(K.dve, lambda e, dc=dc: e.memset(Sst[:, dc, :], 0.0), [], [b_S[dc]])
        A(K.dve, lambda e, dc=dc: e.memset(Sbf[:, dc, 0, :], 0.0), [], [b_Sbf[dc][0]])
    hv = None if h_load is not None else hT_d.rearrange("(k p) t -> p k t", p=128)
    ov = None if o_store is not None else oT_d.rearrange("(h p) t -> p h t", p=128)
    c3 = lambda a: a.rearrange("p (c t) -> p c t", c=NCH)
    cg = 0
    for t in range(S // 512):
        s = t % 2
        tsl = slice(t * 512, (t + 1) * 512)
        if h_load is not None:
            h_load(t, hT[:, s], b_hT[s])
        else:
            K.dma(K.sp, lambda e, s=s, tsl=tsl: e.dma_start(out=hT[:, s], in_=hv[:, :, tsl]), [], [b_hT[s]])
        def prep_proj(c, s=s):
            csl = slice(c * CH, (c + 1) * CH)
            ts = c % 2
            for (bank, c0) in ((V_, 528), (Gt, 1040)):
                for k in range(KD):
                    A(K.pe, lambda e, k=k, bank=bank, c0=c0: e.matmul(
                        bk[bank][:, :], lhsT=hT[:, s, k, csl], rhs=W[:, k, c0:c0 + 512], start=(k == 0),
                        stop=(k == KD - 1)), [b_W, b_hT[s]], [b_bk[bank]], inc=(k == KD - 1))
            A(K.act, lambda e: e.activation(out=vc[:, ts, :], in_=bk[V_][:, :], func=AF.Identity),
              [b_bk[V_]], [b_vc[ts]])
            A(K.act, lambda e: e.activation(out=gs[:, ts, :], in_=bk[Gt][:, :], func=AF.Silu),
              [b_bk[Gt]], [b_gs[ts]])
            A(K.dve, lambda e: e.tensor_tensor(out=gs[:, ts, :], in0=gs[:, ts, :], in1=gain[:, :], op=ALU.mult),
              [b_gs[ts], b_par], [b_gs[ts]])

        for k in range(KD):
            A(K.pe, lambda e, k=k, s=s: e.matmul(bk[GA][0:16, :], lhsT=W[:, k, 512:528], rhs=hT[:, s, k, :],
                                                 start=(k == 0), stop=(k == KD - 1)),
              [b_W, b_hT[s]], [b_bk[GA]], inc=(k == KD - 1))
        prep_proj(0)
        prep_proj(1)
        A(K.act, lambda e: e.activation(out=glT[:, :], in_=bk[GA][0:16, :], func=AF.Identity), [b_bk[GA]], [b_glT])
        for dc in range(2):
            zb = (Z0, Z1)[dc]
            A(K.pe, lambda e, dc=dc, zb=zb: e.matmul(bk[zb][:, :], lhsT=wgu[:, dc * 128:(dc + 1) * 128], rhs=glT[:, :],
                                                     start=True, stop=True), [b_par, b_glT], [b_bk[zb]])
            A(K.act, lambda e, dc=dc, zb=zb: e.activation(out=spt[:, dc, :], in_=bk[zb][:, :], func=AF.Exp, scale=-1.0,
                                                          bias=bg[:, dc:dc + 1]), [b_bk[zb], b_par], [b_sp[dc]])
            A(K.act, lambda e, dc=dc: e.activation(out=spt[:, dc, :], in_=spt[:, dc, :], func=AF.Ln, bias=1.0,
                                                   scale=1.0), [b_sp[dc]], [b_sp[dc]])
            for c in range(NCH):
                A(K.dve, lambda e, dc=dc, c=c: e.tensor_tensor_scan(
                    out=csp[:, dc, c * CH:(c + 1) * CH], data0=ones[:, :], data1=spt[:, dc, c * CH:(c + 1) * CH],
                    initial=0.0, op0=ALU.mult, op1=ALU.add), [b_sp[dc], b_ones], [b_csp[dc]])
            A(K.act, lambda e, dc=dc: e.activation(out=Eq[:, dc, :], in_=csp[:, dc, :], func=AF.Exp, scale=-1.0 / 16),
              [b_csp[dc]], [b_Eq[dc]])
            A(K.act, lambda e, dc=dc: e.activation(out=Ek[:, dc, :], in_=csp[:, dc, :], func=AF.Exp, scale=1.0 / 16),
              [b_csp[dc]], [b_Ek[dc]])
            A(K.act, lambda e, dc=dc: e.activation(out=eL[:, dc, :], in_=c3(csp[:, dc, :])[:, :, CH - 1], func=AF.Exp,
                                                   scale=-1.0 / 16), [b_csp[dc]], [b_eL[dc]])
            for (which, dst) in ((0, "q"), (1, "k")):
                blk = which * 2 + dc
                for k in range(KD):
                    A(K.pe, lambda e, blk=blk, k=k, s=s, zb=zb: e.matmul(
                        bk[zb][:, :], lhsT=W[:, k, blk * 128:(blk + 1) * 128], rhs=hT[:, s, k, :], start=(k == 0),
                        stop=(k == KD - 1)), [b_W, b_hT[s]], [b_bk[zb]], inc=(k == KD - 1))
                if which == 0:
                    A(K.dve, lambda e, dc=dc, zb=zb: e.scalar_tensor_tensor(
                        out=qh[:, dc, :], in0=bk[zb][:, :], scalar=float(256 ** -0.5), in1=Eq[:, dc, :], op0=ALU.mult,
                        op1=ALU.mult), [b_bk[zb], b_Eq[dc]], [b_qh[dc]])
                else:
                    A(K.dve, lambda e, dc=dc, zb=zb: e.tensor_tensor(out=kt[:, dc, :], in0=bk[zb][:, :],
                                                                     in1=Ek[:, dc, :], op=ALU.mult),
                      [b_bk[zb], b_Ek[dc]], [b_kt[dc]])
                    A(K.dve, lambda e, dc=dc: e.tensor_tensor(
                        out=c3(kh[:, dc, :]), in0=c3(kt[:, dc, :]),
                        in1=eL[:, dc, :].unsqueeze(2).broadcast_to([128, NCH, CH]), op=ALU.mult),
                        [b_kt[dc], b_eL[dc]], [b_kh[dc]])
        def prep(c, s=s):
            csl = slice(c * CH, (c + 1) * CH)
            ts = c % 2
            if c >= 2:
                prep_proj(c)
            for dc in range(2):
                A(K.pe, lambda e, dc=dc: e.matmul(bk[GA][:, 0:128], lhsT=kt[:, dc, csl], rhs=qh[:, dc, csl],
                                                  start=(dc == 0), stop=(dc == 1)),
                  [b_kt[dc], b_qh[dc]], [b_bk[GA]], inc=(dc == 1))
            A(K.dve, lambda e: e.tensor_tensor(out=PT[:, ts, :], in0=bk[GA][:, 0:128], in1=C.tri_f(), op=ALU.mult),
              [b_bk[GA], C.buf], [b_PT[ts]])
            for dc in range(2):
                A(K.pe, lambda e, dc=dc: e.transpose(bkT[:, dc * 128:(dc + 1) * 128], kh[:, dc, csl], C.ident_b()),
                  [b_kh[dc], C.buf], [b_T], inc=(dc == 1))
            A(K.act, lambda e: e.activation(out=khs[:, ts, :], in_=bkT[:, 0:256], func=AF.Identity),
              [b_T], [b_khs[ts]])

        def main(c, s=s):
            csl = slice(c * CH, (c + 1) * CH)
            ts = c % 2
            sp_, sn = c % 2, (c + 1) % 2
            A(K.pe, lambda e: e.matmul(bk[O_][:, :], lhsT=PT[:, ts, :], rhs=vc[:, ts, :], start=True, stop=False),
              [b_PT[ts], b_vc[ts]], [b_bk[O_]], inc=False)
            for dc in range(2):
                A(K.pe, lambda e, dc=dc: e.matmul(bk[O_][:, :], lhsT=qh[:, dc, csl], rhs=Sbf[:, dc, sp_, :], start=False,
                                                  stop=(dc == 1)),
                  [b_qh[dc], b_Sbf[dc][sp_]], [b_bk[O_]], inc=(dc == 1))
            for dc in range(2):
                ub = (Z0, Z1)[dc]
                A(K.pe, lambda e, dc=dc, ub=ub: e.matmul(bk[ub][:, :], lhsT=khs[:, ts, dc * 128:(dc + 1) * 128],
                                                         rhs=vc[:, ts, :], start=True, stop=True),
                  [b_khs[ts], b_vc[ts]], [b_bk[ub]])
            for dc in range(2):
                ub = (Z0, Z1)[dc]
                A(K.dve, lambda e, dc=dc, ub=ub: e.scalar_tensor_tensor(
                    out=Sst[:, dc, :], in0=Sst[:, dc, :], scalar=eL[:, dc, c:c + 1], in1=bk[ub][:, :], op0=ALU.mult,
                    op1=ALU.add), [b_S[dc], b_eL[dc], b_bk[ub]], [b_S[dc]])
                A(K.act, lambda e, dc=dc: e.activation(out=Sbf[:, dc, sn, :], in_=Sst[:, dc, :], func=AF.Identity),
                  [b_S[dc]], [b_Sbf[dc][sn]])

        def post_a(c, s=s):
            ts = c % 2
            A(K.act, lambda e: e.activation(out=junk[:, :], in_=bk[O_][:, :], func=AF.Square, accum_out=ssq[:, 0:1]),
              [b_bk[O_]], [b_junk, b_ssq])
            A(K.act, lambda e: e.activation(out=ssq[:, 1:2], in_=ssq[:, 0:1], func=AF.Sqrt, scale=1.0 / 512, bias=EPS),
              [b_ssq], [b_ssq])
            A(K.dve, lambda e: e.reciprocal(out=ssq[:, 1:2], in_=ssq[:, 1:2]), [b_ssq], [b_ssq])
            A(K.dve, lambda e: e.scalar_tensor_tensor(out=og[:, :], in0=bk[O_][:, :], scalar=ssq[:, 1:2],
                                                      in1=gs[:, ts, :], op0=ALU.mult, op1=ALU.mult),
              [b_bk[O_], b_ssq, b_gs[ts]], [b_og])

        def post_b(c, s=s):
            csl = slice(c * CH, (c + 1) * CH)
            for ec in range(4):
                A(K.pe, lambda e, ec=ec: e.transpose(bkT2[:, ec * 128:(ec + 1) * 128], og[:, ec * 128:(ec + 1) * 128],
                                                     C.ident_b()), [b_og, C.buf], [b_T2], inc=(ec == 3))
            A(K.act, lambda e: e.activation(out=oTt[:, s, :, csl], in_=bkT2[:, 0:512].rearrange("p (a q) -> p a q", a=4),
                                            func=AF.Identity), [b_T2], [b_oTt[s]])

        prep(0)
        for c in range(NCH):
            if c + 1 < NCH:
                prep(c + 1)
            main(c)
            if c >= 1:
                post_b(c - 1)
            post_a(c)
        post_b(NCH - 1)
        if o_store is not None:
            o_store(t, oTt[:, s], b_oTt[s], 4)
        else:
            K.dma(K.pool, lambda e, s=s, tsl=tsl: e.dma_start(out=ov[:, :, tsl], in_=oTt[:, s]), [b_oTt[s]], [])


def build_gla(cfg):
    D, S = cfg["D"], cfg["S"]
    nc, stack, K = _new()
    cst = _din(nc, "consts", [128, 640])
    hT = _din(nc, "hT", [D, S], BF16)
    w = _din(nc, "w", [D, 1552])
    wgu = _din(nc, "wgu", [16, 256])
    bg = _din(nc, "bg", [128, 2])
    gain = _din(nc, "gain", [1, 512])
    oT = _dout(nc, "oT", [512, S], BF16)
    C = Consts(K, cst, None)
    emit_gla(K, C, cfg, hT, w, wgu, bg, gain, oT)
    _finish(K, stack)
    return nc


CFG = dict(L=4, D=2048, KD=16, T=2048, S=8192, DFF=5632, KF=44, KO=16)
_PROGS = {}


def _prog(key, fn):
    if key not in _PROGS:
        _PROGS[key] = fn()
    return _PROGS[key]


def _run(nc, in_maps):
    res = run_bass_kernel_spmd(nc, in_maps, core_ids=list(range(8)))
    return res.results


def _fm(v, kd):
    v = np.asarray(v)
    lead = v.shape[:-1]
    a = v.reshape(lead + (kd, 128))
    return np.ascontiguousarray(np.moveaxis(a, -1, 0))


def kernel_unfused(x, c, mod_w, mod_b, norm_mix_gain, norm_ffn_gain, ab_w_in, ab_w_out, hgrn_lb_logits, hgrn_out_gain,
                   fox_q_gain, fox_k_gain, fox_f_bias, gla_w_in, gla_w_gate_up, gla_b_gate, gla_out_gain, gla_w_out,
                   ffn_w_in, ffn_w_out):
    cfg = CFG
    L, D, KD, T, S, DFF = cfg["L"], cfg["D"], cfg["KD"], cfg["T"], cfg["S"], cfg["DFF"]
    f32 = lambda a: np.ascontiguousarray(np.asarray(a, dtype=np.float32))
    x, c, mod_w, mod_b = f32(x), f32(c), f32(mod_w), f32(mod_b)
    consts = make_consts()
    cores = [(cid // 4, cid % 4) for cid in range(8)]

    cfg1 = dict(cfg)
    cfg1["L"] = 1
    nc = _prog("mod", lambda: build_mod(cfg1))
    ims = []
    for (b, j) in cores:
        ims.append({"consts": consts, "cT": _fm(c[b], KD), "modw": mod_w[j:j + 1], "modb": mod_b[j:j + 1],
                    "gmix": _fm(f32(norm_mix_gain)[j:j + 1], KD), "gffn": _fm(f32(norm_ffn_gain)[j:j + 1], KD)})
    r = _run(nc, ims)
    vec = [np.ascontiguousarray(np.concatenate([r[b * 4 + j]["vec"] for j in range(4)], axis=1)) for b in range(2)]

    nc = _prog("norm0", lambda: build_norm0(cfg))
    xT = [np.ascontiguousarray(x[b, j * T:(j + 1) * T, :].T) for (b, j) in cores]
    r = _run(nc, [{"consts": consts, "xT": xT[i], "vec": vec[cores[i][0]]} for i in range(8)])
    hT = [r[i]["hT"] for i in range(8)]

    for l in range(L):
        hfull = [np.ascontiguousarray(np.concatenate(hT[b * 4:b * 4 + 4], axis=1)) for b in range(2)]
        if l % 2 == 0:
            i = l // 2
            w = f32(ab_w_in[i])
            lbl_all = f32(hgrn_lb_logits)
            ims_h, ims_f = [], []
            for (b, j) in cores:
                hh = (2 * j, 2 * j + 1)
                cs = lambda base, h_: w[:, base + h_ * 128:base + (h_ + 1) * 128]
                wh = np.concatenate([cs(0, hh[0]), cs(1024, hh[0]), cs(0, hh[1]), cs(1024, hh[1]),
                                     cs(2048, hh[0]), cs(2048, hh[1]), cs(3072, hh[0]), cs(3072, hh[1])], axis=1)
                lbl = np.stack([np.stack([lbl_all[ly, h_ * 128:(h_ + 1) * 128] for h_ in hh], axis=1)
                                for ly in range(2)], axis=1)
                ims_h.append({"consts": consts, "hT": hfull[b], "w": np.ascontiguousarray(wh),
                              "lbl": f32(lbl), "gain": f32(hgrn_out_gain[i][2 * j:2 * j + 2]).reshape(1, 256)})
                wf = np.concatenate([cs(4096, hh[0]), cs(5120, hh[0]), cs(4096, hh[1]), cs(5120, hh[1]),
                                     cs(6144, hh[0]), cs(6144, hh[1]), cs(7168, hh[0]), cs(7168, hh[1]),
                                     w[:, 8192 + hh[0]:8192 + hh[0] + 1], w[:, 8192 + hh[1]:8192 + hh[1] + 1]], axis=1)
                ims_f.append({"consts": consts, "hT": hfull[b], "w": np.ascontiguousarray(wf),
                              "qkg": f32(np.stack([fox_q_gain[i], fox_k_gain[i]], axis=1)),
                              "fbias": f32(fox_f_bias[i][2 * j:2 * j + 2]).reshape(1, 2)})
            rh = _run(_prog(("hgrn", i), lambda: build_hgrn(cfg, i)), ims_h)
            rf = _run(_prog("fox", lambda: build_fox(cfg)), ims_f)
            ofull = [np.concatenate([np.concatenate([rh[b * 4 + j]["oT"], rf[b * 4 + j]["oT"]], axis=0)
                                     for j in range(4)], axis=0) for b in range(2)]
            wo_src = f32(ab_w_out[i])
            perm = []
            for j in range(4):
                for typ in range(2):
                    for hl in range(2):
                        base = typ * 1024 + (2 * j + hl) * 128
                        perm.extend(range(base, base + 128))
            wo = np.ascontiguousarray(wo_src[np.asarray(perm)])
        else:
            i = l // 2
            w = f32(gla_w_in[i])
            ims_g = []
            for (b, j) in cores:
                wg = np.concatenate([w[:, j * 256:(j + 1) * 256], w[:, 1024 + j * 256:1024 + (j + 1) * 256],
                                     w[:, 6144:6160], w[:, 2048 + j * 512:2048 + (j + 1) * 512],
                                     w[:, 4096 + j * 512:4096 + (j + 1) * 512]], axis=1)
                ims_g.append({"consts": consts, "hT": hfull[b], "w": np.ascontiguousarray(wg),
                              "wgu": f32(gla_w_gate_up[i][:, j * 256:(j + 1) * 256]),
                              "bg": f32(np.asarray(gla_b_gate[i][j * 256:(j + 1) * 256]).reshape(2, 128).T),
                              "gain": f32(gla_out_gain[i]).reshape(1, 512)})
            rg = _run(_prog("gla", lambda: build_gla(cfg)), ims_g)
            ofull = [np.concatenate([rg[b * 4 + j]["oT"] for j in range(4)], axis=0) for b in range(2)]
            wo = f32(gla_w_out[i])
        last = (l == L - 1)
        nc = _prog(("dense", last), lambda: build_dense(cfg, last))
        wfi, wfo = f32(ffn_w_in[l]), f32(ffn_w_out[l])
        ims = []
        for ci, (b, j) in enumerate(cores):
            v2 = np.zeros((128, 2, 6 * KD), np.float32)
            v2[:, 0] = vec[b][:, l]
            if not last:
                v2[:, 1] = vec[b][:, l + 1]
            ims.append({"consts": consts, "xT": xT[ci], "oT": np.ascontiguousarray(ofull[b][:, j * T:(j + 1) * T]),
                        "wo": wo, "wfi": wfi, "wfo": wfo, "vec": v2})
        r = _run(nc, ims)
        xT = [r[ci]["xTo"] for ci in range(8)]
        if not last:
            hT = [r[ci]["hTo"] for ci in range(8)]

    out = np.empty((2, S, D), np.float32)
    for ci, (b, j) in enumerate(cores):
        out[b, j * T:(j + 1) * T, :] = xT[ci].T
    return out


GROUPS = [[0, 1, 2, 3], [4, 5, 6, 7]]
PRECAST = False


def build_fused(cfg):
    L, D, KD, T, S, DFF, KO = cfg["L"], cfg["D"], cfg["KD"], cfg["T"], cfg["S"], cfg["DFF"], cfg["KO"]
    NT = T // 512
    NQ = D // 256
    nc = bass.Bass("TRN2", target_bir_lowering=False)
    gstack = contextlib.ExitStack()
    K = Kern(nc, gstack)
    cst = _din(nc, "consts", [128, 640])
    cT = _din(nc, "cT", [128, KD])
    modw = _din(nc, "modw", [1, D, 6 * D])
    modb = _din(nc, "modb", [1, 6 * D])
    gmix = _din(nc, "gmix", [128, 1, KD])
    gffn = _din(nc, "gffn", [128, 1, KD])
    xT_in = _din(nc, "xT", [D, T])
    wfi = _din(nc, "wfi", [L, D, 2 * DFF])
    wfo = _din(nc, "wfo", [L, DFF, D])
    wo = [_din(nc, "wo%d" % l, [KO * 128, D]) for l in range(L)]
    ab, gl = {}, {}
    for i in range((L + 1) // 2):
        ab[i] = dict(wh=_din(nc, "wh%d" % i, [D, 1024]), lbl=_din(nc, "lbl%d" % i, [128, 2, 2]),
                     again=_din(nc, "again%d" % i, [1, 256]), wf=_din(nc, "wf%d" % i, [D, 1026]),
                     qkg=_din(nc, "qkg%d" % i, [128, 2]), fbias=_din(nc, "fbias%d" % i, [1, 2]))
    for i in range(L // 2):
        gl[i] = dict(wg=_din(nc, "wg%d" % i, [D, 1552]), wgu=_din(nc, "wgu%d" % i, [16, 256]),
                     bg=_din(nc, "bg%d" % i, [128, 2]), ggain=_din(nc, "ggain%d" % i, [1, 512]))
    xTo = _dout(nc, "xTo", [D, T])
    vin = nc.dram_tensor("i_vin", [128, 6 * KD], F32)
    vall = nc.dram_tensor("i_vall", [4 * 128, 6 * KD], F32)
    xs = nc.dram_tensor("i_xs", [D, T], F32)
    HK = KD // 2
    hq = nc.dram_tensor("i_hq", [NT * 2, HK * 128, 512], BF16)
    hf = nc.dram_tensor("i_hf", [NT * 2, 4 * HK * 128, 512], BF16)
    OC = 2 if NT % 2 == 0 else 1
    NOC = (S // 512) // OC
    oqs = {nm: nc.dram_tensor("i_oq" + nm, [NOC, rows, OC * 512], BF16) for nm, rows in (("h", 256), ("f", 256), ("g", 512))}
    ofs = {nm: nc.dram_tensor("i_of" + nm, [NOC, 4 * rows, OC * 512], BF16)
           for nm, rows in (("h", 256), ("f", 256), ("g", 512))}
    wbo = nc.dram_tensor("i_wbo", [KO * 128, D], BF16)
    wbi = nc.dram_tensor("i_wbi", [D, 2 * DFF], BF16)
    wbf = nc.dram_tensor("i_wbf", [DFF, D], BF16)
    b_cv = [Buf() for _ in range(4)]
    conv = {"q": [], "per": 1, "n": 0}

    def conv_plan(l, n_calls):
        q = []
        if not PRECAST:
            conv["q"] = q
            return
        for (src, dst, rows, step) in ((wo[l], wbo.ap(), KO * 128, 512), (wfi[l], wbi.ap(), D, 128),
                                       (wfo[l], wbf.ap(), DFF, 512)):
            for r0 in range(0, rows, step):
                r1 = min(rows, r0 + step)
                q.append((src[r0:r1, :], dst[r0:r1, :]))
        conv["q"] = q
        conv["per"] = -(-len(q) // n_calls)

    def conv_step(n=None):
        n = conv["per"] if n is None else n
        for _ in range(min(n, len(conv["q"]))):
            src, dst = conv["q"].pop(0)
            key = b_cv[conv["n"] % 4]
            conv["n"] += 1
            K.dma(K.pool, lambda e, src=src, dst=dst: e.dma_start(out=dst, in_=src), [], [], key=key)

    b_vin, b_vall = Buf(), Buf()
    b_hq = [Buf() for _ in range(NT * 2)]
    b_hf = [Buf() for _ in range(NT * 2)]
    b_oq = {nm: [Buf() for _ in range(NOC)] for nm in "hfg"}
    b_of = {nm: [Buf() for _ in range(NOC)] for nm in "hfg"}
    vview = vall.ap().rearrange("(l p) c -> p l c", p=128)
    pending = []

    def flush_colls():
        while pending:
            pending.pop(0)()

    def h_store(t, tile, buf):
        for half in range(2):
            i = t * 2 + half
            dst = hq.ap()[i].rearrange("(k p) t -> p k t", p=128)
            K.dma(K.sp, lambda e, dst=dst, half=half: e.dma_start(out=dst, in_=tile[:, half * HK:(half + 1) * HK, :]),
                  [buf], [b_hq[i]], key=buf)
            pending.append(lambda i=i: K.coll("AllGather", hq.ap()[i], hf.ap()[i], [b_hq[i]], [b_hf[i]], GROUPS))

    def h_load(t, dst, buf):
        r, tl = t // NT, t % NT
        for half in range(2):
            i = tl * 2 + half
            src = hf.ap()[i].rearrange("(r k p) t -> p r k t", r=4, p=128)[:, r]
            K.dma(K.sp, lambda e, src=src, half=half: e.dma_start(out=dst[:, half * HK:(half + 1) * HK, :], in_=src),
                  [b_hf[i]], [buf], key=buf)

    def make_o_store(nm):
        def o_store(t, src, buf, nh):
            ci, cl = t // OC, t % OC
            dst = oqs[nm].ap()[ci].rearrange("(h p) t -> p h t", p=128)[:, :, cl * 512:(cl + 1) * 512]
            K.dma(K.pool, lambda e: e.dma_start(out=dst, in_=src), [buf], [b_oq[nm][ci]], key=buf)
            flush_colls()
            if cl == OC - 1:
                pending.append(lambda: K.coll("AllGather", oqs[nm].ap()[ci], ofs[nm].ap()[ci], [b_oq[nm][ci]],
                                              [b_of[nm][ci]], GROUPS))
            conv_step()
        return o_store

    seg_cache = {}

    def make_o_load(names):
        def o_load(tt, oT, buf):
            k0 = 0
            for nm in names:
                rows = 4 * (256 if nm in "hf" else 512)
                nk = rows // 128

                def fn(e, nm=nm, nk=nk, k0=k0):
                    if "segC" not in seg_cache:
                        seg_cache["segC"] = (e.partition_id() % 4) * (NT // OC)
                    ci = seg_cache["segC"] + tt // OC
                    src = ofs[nm].ap().rearrange("c (k p) t -> p c k t", p=128)[
                        :, bass.ds(ci, 1), :, (tt % OC) * 512:(tt % OC + 1) * 512]
                    return e.dma_start(out=oT[:, k0:k0 + nk, :].rearrange("p (o k) t -> p o k t", o=1), in_=src)
                K.dma(K.sp, fn, [b for b in b_of[nm]], [buf], key=buf)
                k0 += nk
        return o_load

    C = Consts(K, cst, None)
    cfg1 = dict(cfg)
    cfg1["L"] = 1
    K.begin_phase()
    emit_mod(K, C, cfg1, cT, modw, modb, gmix, gffn, vin.ap().rearrange("p (l c) -> p l c", l=1), out_buf=b_vin)
    K.coll("AllGather", vin, vall, [b_vin], [b_vall], GROUPS)
    K.end_phase(last=True)
    K.begin_phase()
    emit_norm0(K, C, cfg, xT_in, vview, None, h_store=h_store, after_tile=flush_colls)
    K.end_phase()
    for l in range(L):
        i = l // 2
        if l % 2 == 0:
            conv_plan(l, 2 * (S // 512))
            K.begin_phase()
            emit_hgrn(K, C, cfg, i, None, ab[i]["wh"], ab[i]["lbl"], ab[i]["again"], None, h_load=h_load,
                      o_store=make_o_store("h"), after_setup=flush_colls)
            K.end_phase()
            K.begin_phase()
            emit_fox(K, C, cfg, None, ab[i]["wf"], ab[i]["qkg"], ab[i]["fbias"], None, h_load=h_load,
                     o_store=make_o_store("f"), after_setup=flush_colls)
            conv_step(len(conv["q"]))
            K.end_phase()
            names = "hf"
        else:
            conv_plan(l, S // 512)
            K.begin_phase()
            emit_gla(K, C, cfg, None, gl[i]["wg"], gl[i]["wgu"], gl[i]["bg"], gl[i]["ggain"], None, h_load=h_load,
                     o_store=make_o_store("g"), after_setup=flush_colls)
            conv_step(len(conv["q"]))
            K.end_phase()
            names = "g"
        last = (l == L - 1)
        K.begin_phase()
        dw = (wbo.ap(), wbi.ap(), wbf.ap()) if PRECAST else (wo[l], wfi[l], wfo[l])
        emit_dense(K, C, cfg, l, xT_in if l == 0 else xs.ap(), None, dw[0], dw[1], dw[2], vview,
                   xTo if last else xs.ap(), None, o_load=make_o_load(names), h_store=None if last else h_store,
                   after_proj=flush_colls, at_start=flush_colls)
        if last:
            flush_colls()
        K.end_phase(last=last)
    gstack.close()
    return nc


def dense_k_perm(kind):
    perm = []
    if kind == "ab":
        for typ in range(2):
            for r in range(4):
                for loc in range(256):
                    perm.append(r * 512 + typ * 256 + loc)
    else:
        perm = list(range(2048))
    return np.asarray(perm)


def kernel_fused(cfg, x, c, mod_w, mod_b, norm_mix_gain, norm_ffn_gain, ab_w_in, ab_w_out, hgrn_lb_logits,
                 hgrn_out_gain, fox_q_gain, fox_k_gain, fox_f_bias, gla_w_in, gla_w_gate_up, gla_b_gate, gla_out_gain,
                 gla_w_out, ffn_w_in, ffn_w_out):
    L, D, KD, T, S, DFF = cfg["L"], cfg["D"], cfg["KD"], cfg["T"], cfg["S"], cfg["DFF"]
    f32 = lambda a: np.ascontiguousarray(np.asarray(a, dtype=np.float32))
    x, c, mod_w, mod_b = f32(x), f32(c), f32(mod_w), f32(mod_b)
    consts = make_consts()
    ab_rows = []
    for j in range(4):
        for typ in range(2):
            for hl in range(2):
                base = typ * 1024 + (2 * j + hl) * 128
                ab_rows.extend(range(base, base + 128))
    ab_rows = np.asarray(ab_rows)
    wfi, wfo = f32(ffn_w_in), f32(ffn_w_out)
    shared = {"consts": consts, "wfi": wfi, "wfo": wfo}
    for l in range(L):
        i = l // 2
        if l % 2 == 0:
            shared["wo%d" % l] = np.ascontiguousarray(f32(ab_w_out[i])[ab_rows[dense_k_perm("ab")]])
        else:
            shared["wo%d" % l] = np.ascontiguousarray(f32(gla_w_out[i])[dense_k_perm("gla")])
    lbl_all = f32(hgrn_lb_logits)
    ims = []
    for cid in range(8):
        b, j = cid // 4, cid % 4
        im = dict(shared)
        im["cT"] = _fm(c[b], KD)
        im["modw"] = mod_w[j:j + 1]
        im["modb"] = mod_b[j:j + 1]
        im["gmix"] = _fm(f32(norm_mix_gain)[j:j + 1], KD)
        im["gffn"] = _fm(f32(norm_ffn_gain)[j:j + 1], KD)
        im["xT"] = np.ascontiguousarray(x[b, j * T:(j + 1) * T, :].T)
        hh = (2 * j, 2 * j + 1)
        for i in range((L + 1) // 2):
            w = f32(ab_w_in[i])
            cs = lambda base, h_: w[:, base + h_ * 128:base + (h_ + 1) * 128]
            im["wh%d" % i] = np.ascontiguousarray(np.concatenate(
                [cs(0, hh[0]), cs(1024, hh[0]), cs(0, hh[1]), cs(1024, hh[1]),
                 cs(2048, hh[0]), cs(2048, hh[1]), cs(3072, hh[0]), cs(3072, hh[1])], axis=1))
            im["lbl%d" % i] = f32(np.stack([np.stack([lbl_all[ly, h_ * 128:(h_ + 1) * 128] for h_ in hh], axis=1)
                                            for ly in range(2)], axis=1))
            im["again%d" % i] = f32(hgrn_out_gain[i][2 * j:2 * j + 2]).reshape(1, 256)
            im["wf%d" % i] = np.ascontiguousarray(np.concatenate(
                [cs(4096, hh[0]), cs(5120, hh[0]), cs(4096, hh[1]), cs(5120, hh[1]),
                 cs(6144, hh[0]), cs(6144, hh[1]), cs(7168, hh[0]), cs(7168, hh[1]),
                 w[:, 8192 + hh[0]:8192 + hh[0] + 1], w[:, 8192 + hh[1]:8192 + hh[1] + 1]], axis=1))
            im["qkg%d" % i] = f32(np.stack([fox_q_gain[i], fox_k_gain[i]], axis=1))
            im["fbias%d" % i] = f32(fox_f_bias[i][2 * j:2 * j + 2]).reshape(1, 2)
        for i in range(L // 2):
            w = f32(gla_w_in[i])
            im["wg%d" % i] = np.ascontiguousarray(np.concatenate(
                [w[:, j * 256:(j + 1) * 256], w[:, 1024 + j * 256:1024 + (j + 1) * 256], w[:, 6144:6160],
                 w[:, 2048 + j * 512:2048 + (j + 1) * 512], w[:, 4096 + j * 512:4096 + (j + 1) * 512]], axis=1))
            im["wgu%d" % i] = f32(gla_w_gate_up[i][:, j * 256:(j + 1) * 256])
            im["bg%d" % i] = f32(np.asarray(gla_b_gate[i][j * 256:(j + 1) * 256]).reshape(2, 128).T)
            im["ggain%d" % i] = f32(gla_out_gain[i]).reshape(1, 512)
        ims.append(im)
    nc = _prog("fused", lambda: build_fused(cfg))
    r = _run(nc, ims)
    out = np.empty((2, S, D), np.float32)
    for cid in range(8):
        b, j = cid // 4, cid % 4
        out[b, j * T:(j + 1) * T, :] = r[cid]["xTo"].T
    return out


def kernel(**inputs):
    return kernel_fused(CFG, **inputs)
```

```python
import contextlib
import numpy as np
import ml_dtypes
import concourse.bass as bass
import concourse.mybir as mybir
from concourse.bass_utils import run_bass_kernel_spmd

F32 = mybir.dt.float32
BF16 = mybir.dt.bfloat16
AF = mybir.ActivationFunctionType
ALU = mybir.AluOpType
EPS = 1e-6


class Sem:
    def __init__(self, h):
        self.h = h
        self.n = 0


class Buf:
    __slots__ = ("w", "r", "ds")

    def __init__(self):
        self.w = None
        self.r = {}
        self.ds = None


class Eng:
    def __init__(self, name, sem, is_pe=False):
        self.name = name
        self.sem = sem
        self.prog = []
        self.seen = {}
        self.is_pe = is_pe


class Kern:
    def __init__(self, nc, stack, n_dma_sems=12):
        self.nc = nc
        self.stack = stack
        mk = lambda n: Sem(stack.enter_context(nc.semaphore(n)))
        self.pe = Eng("tensor", mk("s_pe"), True)
        self.act = Eng("scalar", mk("s_act"))
        self.dve = Eng("vector", mk("s_dve"))
        self.pool = Eng("gpsimd", mk("s_pool"))
        self.sp = Eng("sync", mk("s_sp"))
        self.engs = [self.pe, self.act, self.dve, self.pool, self.sp]
        self.dsems = [None] * n_dma_sems
        self.nbuf = 0
        self.ccs = [mk("s_cc%d" % i) for i in range(4)]
        self.dsems.extend(self.ccs)
        self.ncoll = 0
        self.free_sems = []
        self.phase_sems = []
        self.pstack = None
        self.nphase = 0

    def sb(self, name, shape, dt):
        st = self.pstack if self.pstack is not None else self.stack
        return st.enter_context(self.nc.sbuf_tensor("%s_%d" % (name, self.nphase), list(shape), dt))

    def ps(self, name, shape, dt):
        st = self.pstack if self.pstack is not None else self.stack
        return st.enter_context(self.nc.psum_tensor("%s_%d" % (name, self.nphase), list(shape), dt))

    def begin_phase(self):
        self.pstack = contextlib.ExitStack()
        self.nphase += 1

    def end_phase(self, last=False):
        for e in self.engs:
            self.final_wait(e, skip_cc=not last)
        self.emit()
        for e in self.engs:
            e.prog = []
        self.free_sems.extend(self.phase_sems)
        self.phase_sems = []
        self.pstack.close()
        self.pstack = None

    def coll(self, kind, in_t, out_t, R, W, groups):
        eng = self.pool
        cc = self.ccs[self.ncoll % len(self.ccs)]
        self.ncoll += 1
        waits = self._waits(eng, R, W)
        if cc.n > 0 and eng.seen.get(cc, 0) < cc.n:
            eng.seen[cc] = cc.n
            waits.append((cc, cc.n))
        cc.n += 1
        ev = (cc, cc.n)
        in_ap = in_t if isinstance(in_t, bass.AP) else in_t.ap()
        out_ap = out_t if isinstance(out_t, bass.AP) else out_t.ap()
        fn = lambda e: e.collective_compute(kind, ALU.bypass, replica_groups=groups, ins=[in_ap.opt()],
                                            outs=[out_ap.opt()])
        eng.prog.append((waits, fn, (cc, 1)))
        self._commit(ev, R, W)

    def _waits(self, eng, R, W):
        needs = {}

        def need(ev):
            s, v = ev
            if needs.get(s, 0) < v:
                needs[s] = v

        for b in R:
            if b.w is not None:
                if b.w[0] is eng.sem and eng.is_pe:
                    continue
                need(b.w)
        for b in W:
            if b.w is not None and b.w[0] is not eng.sem:
                need(b.w)
            for s, v in b.r.items():
                if s is not eng.sem:
                    need((s, v))
        out = []
        for s, v in needs.items():
            if eng.seen.get(s, 0) < v:
                eng.seen[s] = v
                out.append((s, v))
        return out

    def _commit(self, ev, R, W):
        for b in R:
            if b.r.get(ev[0], 0) < ev[1]:
                b.r[ev[0]] = ev[1]
        for b in W:
            b.w = ev
            b.r = {}

    def op(self, eng, fn, R=(), W=(), inc=True):
        waits = self._waits(eng, R, W)
        if inc:
            eng.sem.n += 1
            ev = (eng.sem, eng.sem.n)
        else:
            ev = (eng.sem, eng.sem.n + 1)
        eng.prog.append((waits, fn, (eng.sem, 1) if inc else None))
        self._commit(ev, R, W)

    def dma(self, eng, fn, R, W, dsem=None, key=None):
        if key is None:
            key = W[0] if W else R[0]
        if key.ds is None:
            if self.free_sems:
                key.ds = self.free_sems.pop()
            else:
                key.ds = Sem(self.stack.enter_context(self.nc.semaphore("s_d%d" % self.nbuf)))
                self.nbuf += 1
                self.dsems.append(key.ds)
            if self.pstack is not None:
                self.phase_sems.append(key.ds)
        dsem = key.ds
        waits = self._waits(eng, R, W)
        if dsem.n > 0 and eng.seen.get(dsem, 0) < dsem.n:
            eng.seen[dsem] = dsem.n
            waits.append((dsem, dsem.n))
        dsem.n += 16
        ev = (dsem, dsem.n)
        eng.prog.append((waits, fn, (dsem, 16)))
        self._commit(ev, R, W)

    def final_wait(self, eng, skip_cc=False):
        waits = []
        for s in [e.sem for e in self.engs] + [d for d in self.dsems if d is not None]:
            if skip_cc and s in self.ccs:
                continue
            if s is not eng.sem and s.n > 0 and eng.seen.get(s, 0) < s.n:
                eng.seen[s] = s.n
                waits.append((s, s.n))
        eng.prog.append((waits, None, None))

    def emit(self):
        nc = self.nc

        def replay(eng):
            def run(e):
                for waits, fn, inc in eng.prog:
                    for s, v in waits:
                        e.wait_ge(s.h, v)
                    if fn is not None:
                        ins = fn(e)
                        if inc is not None:
                            ins.then_inc(inc[0].h, inc[1])
            return run

        with nc.Block() as block:
            block.tensor(replay(self.pe))
            block.scalar(replay(self.act))
            block.vector(replay(self.dve))
            block.gpsimd(replay(self.pool))
            block.sync(replay(self.sp))


def _chunks(n, m):
    return [(i, min(m, n - i)) for i in range(0, n, m)]


class Consts:
    def __init__(self, K, cdram, dsem):
        nc = K.nc
        self.f = K.sb("c_f32", [128, 5 * 128], F32)
        self.b = K.sb("c_bf16", [128, 5 * 128], BF16)
        self.buf = Buf()
        f, b = self.f, self.b
        K.dma(K.sp, lambda e: e.dma_start(out=f[:], in_=cdram[:, :]), [], [self.buf], dsem)
        K.op(K.dve, lambda e: e.tensor_copy(out=b[:], in_=f[:]), [self.buf], [self.buf])

    def ident_b(self, n=128):
        return self.b[0:n, 0:n]

    def tri_b(self, n=128):
        return self.b[0:n, 128:128 + n]

    def tri_f(self, n=128):
        return self.f[0:n, 128:128 + n]

    def ones_f(self, p=128, n=128):
        return self.f[0:p, 256:256 + n]

    def ones_b(self, p=128, n=128):
        return self.b[0:p, 256:256 + n]

    def sel_f(self):
        return self.f[:, 384:512]

    def ident_f(self):
        return self.f[:, 0:128]

    def negmask_b(self):
        return self.b[:, 512:640]


def make_consts():
    c = np.zeros((128, 5 * 128), np.float32)
    c[:, 0:128] = np.eye(128)
    c[:, 128:256] = np.triu(np.ones((128, 128)))
    c[:, 256:384] = 1.0
    c[127, 384:512] = 1.0
    c[:, 512:640] = -30000.0 * np.tril(np.ones((128, 128)), -1)
    return c


def emit_norm(K, C, xT, xbuf, KD, D, G, Sh, vbuf, outT, obuf, ssbank, ssb, tmp):
    sq, sqb, rs, rsb, tm, tmb = tmp["sq"], tmp["sqb"], tmp["rs"], tmp["rsb"], tmp["tm"], tmp["tmb"]
    for j in range(KD):
        s = j % 2
        if j % 2 == 0:
            K.op(K.act, lambda e, j=j, s=s: e.activation(out=sq[:, s, :], in_=xT[:, j, :], func=AF.Square),
                 [xbuf], [sqb[s]])
        else:
            K.op(K.dve, lambda e, j=j, s=s: e.tensor_tensor(out=sq[:, s, :], in0=xT[:, j, :], in1=xT[:, j, :],
                                                           op=ALU.mult), [xbuf], [sqb[s]])
        K.op(K.pe, lambda e, j=j, s=s: e.matmul(ssbank[:, :], lhsT=C.ones_b(), rhs=sq[:, s, :],
                                               start=(j == 0), stop=(j == KD - 1)),
             [sqb[s], C.buf], [ssb], inc=True)
    K.op(K.act, lambda e: e.activation(out=rs[:, :], in_=ssbank[:, :], func=AF.Sqrt, scale=1.0 / D, bias=EPS),
         [ssb], [rsb])
    K.op(K.dve, lambda e: e.reciprocal(out=rs[:, :], in_=rs[:, :]), [rsb], [rsb])
    for j in range(KD):
        s = j % 2
        K.op(K.dve, lambda e, j=j, s=s: e.scalar_tensor_tensor(out=tm[:, s, :], in0=xT[:, j, :], scalar=G[:, j:j + 1],
                                                             in1=rs[:, :], op0=ALU.mult, op1=ALU.mult),
             [xbuf, rsb, vbuf], [tmb[s]])
        K.op(K.act, lambda e, j=j, s=s: e.activation(out=outT[:, j, :], in_=tm[:, s, :], func=AF.Identity,
                                                   bias=Sh[:, j:j + 1], scale=1.0),
             [tmb[s], vbuf], [obuf])


def emit_mod(K, C, cfg, cT_d, modw_d, modb_d, gmix_d, gffn_d, vec_d, out_buf=None):
    L, D, KD = cfg["L"], cfg["D"], cfg["KD"]
    W6 = 6 * D
    CG = min(2048, W6)
    NB = CG // 512
    assert W6 % CG == 0
    ds_a, ds_w = K.dsems[0], K.dsems[1]
    cT = K.sb("m_cT", [128, KD], F32)
    gm = K.sb("m_gm", [128, L, KD], F32)
    gf = K.sb("m_gf", [128, L, KD], F32)
    row = K.sb("m_row", [1, W6], F32)
    brow = K.sb("m_brow", [1, W6], F32)
    wp = K.sb("m_wp", [128, 3, CG], F32)
    mt = K.sb("m_mt", [128, 6 * KD], F32)
    vec = K.sb("m_vec", [128, L, 6 * KD], F32)
    banks = [K.ps("m_ps%d" % i, [128, 512], F32) for i in range(NB)]
    tp = K.ps("m_tp", [128, 512], F32)
    b_c, b_g, b_row, b_brow, b_mt, b_vec, b_tp = Buf(), Buf(), Buf(), Buf(), Buf(), Buf(), Buf()
    b_wp = [Buf() for _ in range(3)]
    b_bk = [Buf() for _ in range(NB)]
    K.dma(K.sp, lambda e: e.dma_start(out=cT[:], in_=cT_d[:, :]), [], [b_c], ds_a)
    K.dma(K.sp, lambda e: e.dma_start(out=gm[:], in_=gmix_d[:, :, :]), [], [b_g], ds_a)
    K.dma(K.sp, lambda e: e.dma_start(out=gf[:], in_=gffn_d[:, :, :]), [], [b_g], ds_a)
    K.op(K.act, lambda e: e.activation(out=cT[:], in_=cT[:], func=AF.Silu), [b_c], [b_c])
    ip = 0
    for l in range(L):
        K.dma(K.sp, lambda e, l=l: e.dma_start(out=brow[:], in_=modb_d[l:l + 1, :]), [], [b_brow], ds_a)
        for cg in range(W6 // CG):
            for k in range(KD):
                s = ip % 3
                ip += 1
                K.dma(K.sp, lambda e, l=l, cg=cg, k=k, s=s: e.dma_start(
                    out=wp[:, s, :], in_=modw_d[l, k * 128:(k + 1) * 128, cg * CG:(cg + 1) * CG]),
                    [], [b_wp[s]], ds_w)
                for q in range(NB):
                    K.op(K.pe, lambda e, k=k, s=s, q=q: e.matmul(
                        banks[q][0:1, :], lhsT=cT[:, k:k + 1], rhs=wp[:, s, q * 512:(q + 1) * 512],
                        start=(k == 0), stop=(k == KD - 1)), [b_c, b_wp[s]], [b_bk[q]])
            for q in range(NB):
                c0 = cg * CG + q * 512
                K.op(K.dve, lambda e, q=q, c0=c0: e.tensor_tensor(
                    out=row[0:1, c0:c0 + 512], in0=banks[q][0:1, :], in1=brow[0:1, c0:c0 + 512], op=ALU.add),
                    [b_bk[q], b_brow], [b_row])
        for c in range(6 * KD):
            K.op(K.pe, lambda e, c=c: e.matmul(tp[:, c:c + 1], lhsT=row[0:1, c * 128:(c + 1) * 128],
                                               rhs=C.ones_f(1, 1), start=True, stop=True),
                 [b_row, C.buf], [b_tp])
        K.op(K.act, lambda e: e.activation(out=mt[:, :], in_=tp[:, 0:6 * KD], func=AF.Identity), [b_tp], [b_mt])
        for (dst, src_sc, gn) in ((0, 1, gm), (3, 4, gf)):
            K.op(K.dve, lambda e, l=l, dst=dst, src_sc=src_sc, gn=gn: e.scalar_tensor_tensor(
                out=vec[:, l, dst * KD:(dst + 1) * KD], in0=mt[:, src_sc * KD:(src_sc + 1) * KD], scalar=1.0,
                in1=gn[:, l, :], op0=ALU.add, op1=ALU.mult), [b_mt, b_g], [b_vec])
        for (dst, src) in ((1, 0), (2, 2), (4, 3), (5, 5)):
            K.op(K.dve, lambda e, l=l, dst=dst, src=src: e.tensor_copy(
                out=vec[:, l, dst * KD:(dst + 1) * KD], in_=mt[:, src * KD:(src + 1) * KD]), [b_mt], [b_vec])
    K.dma(K.sp, lambda e: e.dma_start(out=vec_d[:, :, :], in_=vec[:]), [b_vec], [out_buf] if out_buf else [], key=b_vec)


def norm_scratch(K, pfx):
    return {
        "sq": K.sb(pfx + "sq", [128, 2, 512], BF16), "sqb": [Buf(), Buf()],
        "rs": K.sb(pfx + "rs", [128, 512], F32), "rsb": Buf(),
        "tm": K.sb(pfx + "tm", [128, 2, 512], F32), "tmb": [Buf(), Buf()],
    }


def emit_norm0(K, C, cfg, xT_d, vec_d, hT_d, h_store=None, after_tile=None):
    D, KD, T = cfg["D"], cfg["KD"], cfg["T"]
    ds_a = K.dsems[0]
    vec = K.sb("n_vec", [128, 6 * KD], F32)
    xT = K.sb("n_xT", [128, 2, KD, 512], F32)
    hT = K.sb("n_hT", [128, 2, KD, 512], BF16)
    ss = K.ps("n_ss", [128, 512], F32)
    b_v, b_ss = Buf(), Buf()
    b_x, b_h = [Buf(), Buf()], [Buf(), Buf()]
    tmp = norm_scratch(K, "n_")
    K.dma(K.sp, lambda e: e.dma_start(out=vec[:], in_=vec_d[:, 0, :]), [], [b_v], ds_a)
    xv = xT_d.rearrange("(k p) t -> p k t", p=128)
    hv = None if h_store is not None else hT_d.rearrange("(k p) t -> p k t", p=128)
    def xload(t):
        K.dma(K.sp, lambda e, t=t: e.dma_start(out=xT[:, t % 2], in_=xv[:, :, t * 512:(t + 1) * 512]),
              [], [b_x[t % 2]], ds_a)

    xload(0)
    for t in range(T // 512):
        s = t % 2
        if t + 1 < T // 512:
            xload(t + 1)
        emit_norm(K, C, xT[:, s], b_x[s], KD, D, vec[:, 0:KD], vec[:, KD:2 * KD], b_v, hT[:, s], b_h[s], ss, b_ss, tmp)
        if h_store is not None:
            h_store(t, hT[:, s], b_h[s])
            if after_tile is not None:
                after_tile()
        else:
            K.dma(K.sp, lambda e, t=t, s=s: e.dma_start(out=hv[:, :, t * 512:(t + 1) * 512], in_=hT[:, s]),
                  [b_h[s]], [], ds_a)


def emit_dense(K, C, cfg, l, xT_d, oT_d, wo_d, wfi_d, wfo_d, vec_d, xTo_d, hTo_d, o_load=None, h_store=None,
               after_proj=None, at_start=None):
    D, KD, T, DFF, KF, KO, L = cfg["D"], cfg["KD"], cfg["T"], cfg["DFF"], cfg["KF"], cfg["KO"], cfg["L"]
    DG = D // 512 if D >= 512 else 1
    GW = min(512, D)
    GC = GW // 128
    FG = DFF // 512
    assert DFF % 512 == 0
    ds_a, ds_w, ds_s = K.dsems[0], K.dsems[1], K.dsems[2]
    last = hTo_d is None and h_store is None
    vec = K.sb("d_vec", [128, 2, 6 * KD], F32)
    xin = K.sb("d_xin", [128, KD, 512], F32)
    xT = K.sb("d_xT", [128, KD, 512], F32)
    oT2 = K.sb("d_oT", [128, 1, KO, 512], BF16)
    oT = K.sb("d_hT", [128, KD, 512], BF16)
    halves = [list(range(0, (FG + 1) // 2)), list(range((FG + 1) // 2, FG))]
    AH = 4 * len(halves[0])
    aT = K.sb("d_aT", [128, AH, 512], BF16)
    sa = K.sb("d_sa", [128, 2, 512], BF16)
    WB = 4
    WSZ = max(KO * GW, KD * 512, 16 * GW)
    wb = K.sb("d_wb", [128, WB, WSZ], BF16)
    banks = [K.ps("d_ps%d" % i, [128, 512], F32) for i in range(8)]
    b_bk = [Buf() for _ in range(8)]
    b_v, b_x, b_o, b_a, b_xin = Buf(), Buf(), Buf(), Buf(), Buf()
    b_o2 = [Buf(), Buf()]
    b_sa = [Buf(), Buf()]
    b_wb = [Buf() for _ in range(WB)]
    tmp = norm_scratch(K, "d_")
    K.dma(K.sp, lambda e: e.dma_start(out=vec[:, 0, :], in_=vec_d[:, l, :]), [], [b_v], ds_a)
    if not last:
        K.dma(K.sp, lambda e: e.dma_start(out=vec[:, 1, :], in_=vec_d[:, l + 1, :]), [], [b_v], ds_a)
    g1 = vec[:, 0, 2 * KD:3 * KD]
    G2 = vec[:, 0, 3 * KD:4 * KD]
    Sh2 = vec[:, 0, 4 * KD:5 * KD]
    g2 = vec[:, 0, 5 * KD:6 * KD]
    G1n = vec[:, 1, 0:KD]
    Sh1n = vec[:, 1, KD:2 * KD]
    xv = xT_d.rearrange("(k p) t -> p k t", p=128)
    ov = None if o_load is not None else oT_d.rearrange("(k p) t -> p k t", p=128)
    xov = xTo_d.rearrange("(k p) t -> p k t", p=128)
    hov = None if (last or h_store is not None) else hTo_d.rearrange("(k p) t -> p k t", p=128)
    wov = wo_d.rearrange("(k p) c -> p k c", p=128)
    wiv = wfi_d.rearrange("(k p) c -> p k c", p=128)
    wfv = wfo_d.rearrange("(k p) c -> p k c", p=128)
    st = {"w": 0, "bank": 0}

    def load_w(src_ap, kk, cc):
        s = st["w"] % WB
        st["w"] += 1
        dst = wb[:, s, 0:kk * cc].rearrange("p (k c) -> p k c", k=kk)
        K.dma(K.pool, lambda e: e.dma_start(out=dst, in_=src_ap), [], [b_wb[s]], ds_w)
        return dst, b_wb[s]

    def proj_group(pieces, rhsT, rbuf, nk, og, gvec, src=None, sbuf=None, kofs=0):
        if src is None:
            src, sbuf = xT, b_x
        base = (st["bank"] % 2) * 4
        st["bank"] += 1
        kdone = 0
        for (wt, wbuf, k0, kn) in pieces:
            for kk in range(kn):
                for i in range(GC):
                    K.op(K.pe, lambda e, wt=wt, kk=kk, i=i, k=k0 + kk - kofs: e.matmul(
                        banks[base + i][:, :], lhsT=wt[:, kk, i * 128:(i + 1) * 128], rhs=rhsT[:, k, :],
                        start=(k == 0), stop=(k == nk - 1)), [wbuf, rbuf], [b_bk[base + i]],
                        inc=(k0 + kk - kofs == nk - 1 or kk == kn - 1))
        for i in range(GC):
            j = og * GC + i
            K.op(K.dve, lambda e, i=i, j=j: e.scalar_tensor_tensor(
                out=xT[:, j, :], in0=banks[base + i][:, :], scalar=gvec[:, j:j + 1], in1=src[:, j, :],
                op0=ALU.mult, op1=ALU.add), [b_bk[base + i], sbuf, b_v], [b_x])

    NTL = T // 512

    def loads(t):
        tsl = slice(t * 512, (t + 1) * 512)
        K.dma(K.sp, lambda e: e.dma_start(out=xin[:], in_=xv[:, :, tsl]), [], [b_xin])
        if o_load is not None:
            o_load(t, oT2[:, 0], b_o2[0])
        else:
            K.dma(K.sp, lambda e: e.dma_start(out=oT2[:, 0], in_=ov[:, :, tsl]), [], [b_o2[0]])

    first_w = load_w(wov[:, :, 0:GW], KO, GW)
    if at_start is not None:
        at_start()
    loads(0)
    for t in range(NTL):
        tsl = slice(t * 512, (t + 1) * 512)
        for og in range(DG):
            if t == 0 and og == 0:
                wt, wbuf = first_w
            else:
                wt, wbuf = load_w(wov[:, :, og * GW:(og + 1) * GW], KO, GW)
            proj_group([(wt, wbuf, 0, KO)], oT2[:, 0], b_o2[0], KO, og, g1, src=xin, sbuf=b_xin)
        if t + 1 < NTL:
            loads(t + 1)
        emit_norm(K, C, xT, b_x, KD, D, G2, Sh2, b_v, oT, b_o, banks[7], b_bk[7], tmp)
        for hg in halves:
            if not hg:
                continue
            for fg in hg:
                wa, wab = load_w(wiv[:, :, fg * 512:(fg + 1) * 512], KD, 512)
                wu, wub = load_w(wiv[:, :, DFF + fg * 512:DFF + (fg + 1) * 512], KD, 512)
                for i in range(4):
                    j = fg * 4 + i
                    jl = j - hg[0] * 4
                    ba = (2 * j) % 6
                    bu = ba + 1
                    for (wt, wbf, bk) in ((wa, wab, ba), (wu, wub, bu)):
                        for k in range(KD):
                            K.op(K.pe, lambda e, wt=wt, k=k, i=i, bk=bk: e.matmul(
                                banks[bk][:, :], lhsT=wt[:, k, i * 128:(i + 1) * 128], rhs=oT[:, k, :],
                                start=(k == 0), stop=(k == KD - 1)), [wbf, b_o], [b_bk[bk]], inc=(k == KD - 1))
                    s_ = j % 2
                    K.op(K.act, lambda e, s_=s_, ba=ba: e.activation(out=sa[:, s_, :], in_=banks[ba][:, :], func=AF.Silu),
                         [b_bk[ba]], [b_sa[s_]])
                    K.op(K.dve, lambda e, s_=s_, jl=jl, bu=bu: e.tensor_tensor(
                        out=aT[:, jl, :], in0=sa[:, s_, :], in1=banks[bu][:, :], op=ALU.mult),
                        [b_sa[s_], b_bk[bu]], [b_a])
            kbase, nkh = hg[0] * 4, len(hg) * 4
            if hg is halves[-1] or not halves[-1]:
                if after_proj is not None:
                    after_proj()
            for og in range(DG):
                pieces = []
                for (k0, kn) in _chunks(nkh, 16):
                    wt, wbuf = load_w(wfv[:, kbase + k0:kbase + k0 + kn, og * GW:(og + 1) * GW], kn, GW)
                    pieces.append((wt, wbuf, kbase + k0, kn))
                proj_group(pieces, aT, b_a, nkh, og, g2, kofs=kbase)
        K.dma(K.sp, lambda e, tsl=tsl: e.dma_start(out=xov[:, :, tsl], in_=xT[:]), [b_x], [], ds_s)
        if not last:
            emit_norm(K, C, xT, b_x, KD, D, G1n, Sh1n, b_v, oT, b_o, banks[7], b_bk[7], tmp)
            if h_store is not None:
                h_store(t, oT, b_o)
            else:
                K.dma(K.sp, lambda e, tsl=tsl: e.dma_start(out=hov[:, :, tsl], in_=oT[:, 0:KD, :]), [b_o], [], ds_s)


def _new():
    nc = bass.Bass("TRN2", target_bir_lowering=False)
    stack = contextlib.ExitStack()
    K = Kern(nc, stack)
    return nc, stack, K


def _din(nc, name, shape, dt=F32):
    return nc.dram_tensor(name, list(shape), dt, kind="ExternalInput").ap()


def _dout(nc, name, shape, dt=F32):
    return nc.dram_tensor(name, list(shape), dt, kind="ExternalOutput").ap()


def _finish(K, stack):
    K.final_wait(K.sp)
    K.emit()
    stack.close()


def build_mod(cfg):
    L, D, KD = cfg["L"], cfg["D"], cfg["KD"]
    nc, stack, K = _new()
    cst = _din(nc, "consts", [128, 640])
    cT = _din(nc, "cT", [128, KD])
    modw = _din(nc, "modw", [L, D, 6 * D])
    modb = _din(nc, "modb", [L, 6 * D])
    gmix = _din(nc, "gmix", [128, L, KD])
    gffn = _din(nc, "gffn", [128, L, KD])
    vec = _dout(nc, "vec", [128, L, 6 * KD])
    C = Consts(K, cst, K.dsems[0])
    emit_mod(K, C, cfg, cT, modw, modb, gmix, gffn, vec)
    _finish(K, stack)
    return nc


def build_norm0(cfg):
    L, D, KD, T = cfg["L"], cfg["D"], cfg["KD"], cfg["T"]
    nc, stack, K = _new()
    cst = _din(nc, "consts", [128, 640])
    xT = _din(nc, "xT", [D, T])
    vec = _din(nc, "vec", [128, L, 6 * KD])
    hT = _dout(nc, "hT", [D, T], BF16)
    C = Consts(K, cst, K.dsems[0])
    emit_norm0(K, C, cfg, xT, vec, hT)
    _finish(K, stack)
    return nc


def build_dense(cfg, last):
    L, D, KD, T, DFF, KO = cfg["L"], cfg["D"], cfg["KD"], cfg["T"], cfg["DFF"], cfg["KO"]
    nc, stack, K = _new()
    cst = _din(nc, "consts", [128, 640])
    xT = _din(nc, "xT", [D, T])
    oT = _din(nc, "oT", [KO * 128, T], BF16)
    wo = _din(nc, "wo", [KO * 128, D])
    wfi = _din(nc, "wfi", [D, 2 * DFF])
    wfo = _din(nc, "wfo", [DFF, D])
    vec = _din(nc, "vec", [128, 2, 6 * KD])
    xTo = _dout(nc, "xTo", [D, T])
    hTo = None if last else _dout(nc, "hTo", [D, T], BF16)
    C = Consts(K, cst, K.dsems[0])
    cfg2 = dict(cfg)
    cfg2["L"] = 2
    emit_dense(K, C, cfg2, 0, xT, oT, wo, wfi, wfo, vec, xTo, hTo)
    _finish(K, stack)
    return nc


def emit_hgrn(K, C, cfg, ab_idx, hT_d, w_d, lbl_d, gain_d, oT_d, h_load=None, o_store=None, after_setup=None):
    D, KD, S = cfg["D"], cfg["KD"], cfg["S"]
    ds_a, ds_w, ds_s = K.dsems[0], K.dsems[1], K.dsems[2]
    CH, NCH = 64, 8
    W = K.sb("h_W", [128, KD, 1024], BF16)
    hT = K.sb("h_hT", [128, 2, KD, 512], BF16)
    lbl = K.sb("h_lbl", [128, 2, 2], F32)
    lbv = K.sb("h_lbv", [128, 3, 2], F32)
    gain = K.sb("h_gain", [64, 256], F32)
    ones = K.sb("h_ones", [128, 64], F32)
    names = ["sig", "lf", "key", "qs", "cum", "cm", "cl", "e0", "e1", "e2"]
    hf = [{n: K.sb("h_%s%d" % (n, h), [128, 512], F32) for n in names} for h in range(2)]
    hb = [{n: K.sb("h_%s%d" % (n, h), [128, 512], BF16) for n in ["qh", "qt", "kt", "kh", "k0"]} for h in range(2)]
    eL = [K.sb("h_eL%d" % h, [128, NCH], F32) for h in range(2)]
    Sst = [K.sb("h_S%d" % h, [128, 128], F32) for h in range(2)]
    Sbf = [K.sb("h_Sb%d" % h, [128, 2, 128], BF16) for h in range(2)]
    vc = K.sb("h_vc", [64, 2, 256], BF16)
    gs = K.sb("h_gs", [64, 2, 256], F32)
    PT = K.sb("h_PT", [64, 2, 2, 64], BF16)
    khs = K.sb("h_khs", [64, 2, 2, 128], BF16)
    junk = K.sb("h_junk", [64, 128], F32)
    ssq = K.sb("h_ssq", [64, 2, 2], F32)
    og = K.sb("h_og", [64, 2, 128], BF16)
    oTt = K.sb("h_oTt", [128, 2, 2, 512], BF16)
    fb2 = [K.ps("h_fb%d" % i, [128, 512], F32) for i in range(2)]
    fb = [fb2[0], fb2[1], fb2[0], fb2[1]]
    tb = [K.ps("h_tb%d" % i, [128, 512], F32) for i in range(2)]
    bkO = K.ps("h_bkO", [128, 512], F32)
    bkU = K.ps("h_bkU", [128, 512], F32)
    bkT = [K.ps("h_bkT%d" % i, [128, 1024], BF16) for i in range(2)]
    B = lambda n: [Buf() for _ in range(n)]
    b_W, b_lb, b_gain, b_ones = Buf(), Buf(), Buf(), Buf()
    b_hT, b_tb = B(2), B(2)
    b_fb2 = B(2)
    b_fb = [b_fb2[0], b_fb2[1], b_fb2[0], b_fb2[1]]
    b_hf = [{n: Buf() for n in names} for _ in range(2)]
    b_hb = [{n: Buf() for n in ["qh", "qt", "kt", "kh", "k0"]} for _ in range(2)]
    b_eL, b_S, b_Sbf = B(2), B(2), [B(2), B(2)]
    b_vc, b_gs, b_PT, b_khs, b_junk, b_ssq, b_og = B(2), B(2), B(2), B(2), Buf(), B(2), B(2)
    b_bkO, b_bkU, b_bkT = Buf(), Buf(), B(2)
    b_oTt = B(2)
    A = K.op
    wv = w_d.rearrange("(k p) c -> p k c", p=128)
    for half in range(2):
        K.dma(K.pool, lambda e, half=half: e.dma_start(out=W[:, :, half * 512:(half + 1) * 512],
                                                       in_=wv[:, :, half * 512:(half + 1) * 512]), [], [b_W], ds_w)
    if after_setup is not None:
        after_setup()
    K.dma(K.sp, lambda e: e.dma_start(out=lbl[:], in_=lbl_d[:, :, :]), [], [b_lb], ds_a)
    K.dma(K.sp, lambda e: e.dma_start(out=gain[:], in_=gain_d[0:1, :].partition_broadcast(64)), [], [b_gain], ds_a)
    A(K.dve, lambda e: e.memset(ones[:], 1.0), [], [b_ones])
    if ab_idx == 0:
        A(K.dve, lambda e: e.memset(lbv[:, 0, :], 0.0), [b_lb], [b_lb])
    else:
        A(K.dve, lambda e: e.tensor_tensor(out=lbv[:, 0, :], in0=lbl[:, 1, :], in1=lbl[:, 0, :], op=ALU.subtract),
          [b_lb], [b_lb])
        A(K.act, lambda e: e.activation(out=lbv[:, 0, :], in_=lbv[:, 0, :], func=AF.Sigmoid), [b_lb], [b_lb])
        A(K.dve, lambda e: e.tensor_scalar(out=lbv[:, 0, :], in0=lbv[:, 0, :], scalar1=1.0 - 1e-6, scalar2=0.0,
                                           op0=ALU.min, op1=ALU.max), [b_lb], [b_lb])
    A(K.dve, lambda e: e.tensor_scalar(out=lbv[:, 1, :], in0=lbv[:, 0, :], scalar1=-1.0, scalar2=1.0,
                                       op0=ALU.mult, op1=ALU.add), [b_lb], [b_lb])
    A(K.dve, lambda e: e.tensor_scalar(out=lbv[:, 2, :], in0=lbv[:, 0, :], scalar1=1.0, scalar2=-1.0,
                                       op0=ALU.mult, op1=ALU.add), [b_lb], [b_lb])
    for h in range(2):
        A(K.dve, lambda e, h=h: e.memset(PT[:, h], 0.0), [], [b_PT[h]])
        A(K.dve, lambda e, h=h: e.memset(hb[h]["k0"][:], 0.0), [], [b_hb[h]["k0"]])
        A(K.dve, lambda e, h=h: e.memset(hf[h]["e2"][:], 0.0), [], [b_hf[h]["e2"]])
        A(K.dve, lambda e, h=h: e.memset(Sst[h][:], 0.0), [], [b_S[h]])
        A(K.dve, lambda e, h=h: e.memset(Sbf[h][:, 0, :], 0.0), [], [b_Sbf[h][0]])
    hv = None if h_load is not None else hT_d.rearrange("(k p) t -> p k t", p=128)
    ov = None if o_store is not None else oT_d.rearrange("(h p) t -> p h t", p=128)
    cidx = 0
    for t in range(S // 512):
        s = t % 2
        if h_load is not None:
            h_load(t, hT[:, s], b_hT[s])
        else:
            K.dma(K.sp, lambda e, t=t, s=s: e.dma_start(out=hT[:, s], in_=hv[:, :, t * 512:(t + 1) * 512]),
                  [], [b_hT[s]], ds_a)
        for h in range(2):
            for blk in (2 * h, 2 * h + 1):
                for k in range(KD):
                    A(K.pe, lambda e, blk=blk, k=k, s=s: e.matmul(fb[blk][:, :], lhsT=W[:, k, blk * 128:(blk + 1) * 128],
                                                                  rhs=hT[:, s, k, :], start=(k == 0), stop=(k == KD - 1)),
                      [b_W, b_hT[s]], [b_fb[blk]], inc=(k == KD - 1))
            f, bf_, bb, bbf = hf[h], hb[h], b_hf[h], b_hb[h]
            lb_, oml, noml = lbv[:, 0, h:h + 1], lbv[:, 1, h:h + 1], lbv[:, 2, h:h + 1]
            A(K.act, lambda e, f=f, h=h: e.activation(out=f["sig"][:], in_=fb[2 * h + 1][:, :], func=AF.Sigmoid),
              [b_fb[2 * h + 1]], [bb["sig"]])
            A(K.act, lambda e, f=f, h=h: e.activation(out=f["qs"][:], in_=fb[2 * h][:, :], func=AF.Silu),
              [b_fb[2 * h]], [bb["qs"]])
            A(K.dve, lambda e, f=f, oml=oml, lb_=lb_: e.tensor_scalar(out=f["lf"][:], in0=f["sig"][:], scalar1=oml,
                                                                     scalar2=lb_, op0=ALU.mult, op1=ALU.add),
              [bb["sig"], b_lb], [bb["lf"]])
            A(K.dve, lambda e, f=f: e.tensor_scalar_max(out=f["lf"][:], in0=f["lf"][:], scalar1=1e-30),
              [bb["lf"]], [bb["lf"]])
            A(K.act, lambda e, f=f: e.activation(out=f["lf"][:], in_=f["lf"][:], func=AF.Ln), [bb["lf"]], [bb["lf"]])
            A(K.dve, lambda e, f=f, oml=oml, noml=noml: e.tensor_scalar(out=f["key"][:], in0=f["sig"][:], scalar1=noml,
                                                                       scalar2=oml, op0=ALU.mult, op1=ALU.add),
              [bb["sig"], b_lb], [bb["key"]])
            for c in range(NCH):
                A(K.dve, lambda e, f=f, c=c: e.tensor_tensor_scan(out=f["cum"][:, c * CH:(c + 1) * CH], data0=ones[:, :],
                                                                  data1=f["lf"][:, c * CH:(c + 1) * CH], initial=0.0,
                                                                  op0=ALU.mult, op1=ALU.add),
                  [bb["lf"], b_ones], [bb["cum"]])
            c3 = lambda a: a[:].rearrange("p (c t) -> p c t", c=NCH)
            A(K.dve, lambda e, f=f: e.tensor_tensor(out=c3(f["cm"]), in0=c3(f["cum"]),
                                                    in1=c3(f["cum"])[:, :, 31:32].broadcast_to([128, NCH, CH]),
                                                    op=ALU.subtract), [bb["cum"]], [bb["cm"]])
            A(K.dve, lambda e, f=f: e.tensor_tensor(out=c3(f["cl"]), in0=c3(f["cum"]),
                                                    in1=c3(f["cum"])[:, :, CH - 1:CH].broadcast_to([128, NCH, CH]),
                                                    op=ALU.subtract), [bb["cum"]], [bb["cl"]])
            A(K.act, lambda e, f=f, h=h: e.activation(out=eL[h][:, :], in_=c3(f["cum"])[:, :, CH - 1], func=AF.Exp),
              [bb["cum"]], [b_eL[h]])
            A(K.act, lambda e, f=f: e.activation(out=c3(f["e2"])[:, :, 0:32], in_=c3(f["cum"])[:, :, 0:32], func=AF.Exp,
                                                 scale=-1.0), [bb["cum"]], [bb["e2"]])
            A(K.dve, lambda e, f=f, bf_=bf_: e.tensor_tensor(out=c3(bf_["k0"])[:, :, 0:32], in0=c3(f["key"])[:, :, 0:32],
                                                             in1=c3(f["e2"])[:, :, 0:32], op=ALU.mult),
              [bb["key"], bb["e2"]], [bbf["k0"]])
            for (src, sc, es, mul, dst) in (("cum", 1.0, "e0", "qs", "qh"), ("cm", 1.0, "e1", "qs", "qt"),
                                            ("cm", -1.0, "e0", "key", "kt"), ("cl", -1.0, "e1", "key", "kh")):
                A(K.act, lambda e, f=f, src=src, sc=sc, es=es: e.activation(out=f[es][:], in_=f[src][:], func=AF.Exp,
                                                                           scale=sc), [bb[src]], [bb[es]])
                A(K.dve, lambda e, f=f, bf_=bf_, es=es, mul=mul, dst=dst: e.tensor_tensor(
                    out=bf_[dst][:], in0=f[mul][:], in1=f[es][:], op=ALU.mult), [bb[mul], bb[es]], [bbf[dst]])
        def prep_proj(c, s=s):
            csl = slice(c * CH, (c + 1) * CH)
            par = c % 2
            for k in range(KD):
                A(K.pe, lambda e, k=k: e.matmul(tb[par][0:CH, :], lhsT=hT[:, s, k, csl], rhs=W[:, k, 512:1024],
                                                start=(k == 0), stop=(k == KD - 1)),
                  [b_W, b_hT[s]], [b_tb[par]], inc=(k == KD - 1))
            A(K.act, lambda e: e.activation(out=vc[:, par, :], in_=tb[par][0:CH, 0:256], func=AF.Identity),
              [b_tb[par]], [b_vc[par]])
            A(K.act, lambda e: e.activation(out=gs[:, par, :], in_=tb[par][0:CH, 256:512], func=AF.Silu),
              [b_tb[par]], [b_gs[par]])
            A(K.dve, lambda e: e.tensor_tensor(out=gs[:, par, :], in0=gs[:, par, :], in1=gain[:, :], op=ALU.mult),
              [b_gs[par], b_gain], [b_gs[par]])

        def prep(c, s=s):
            csl = slice(c * CH, (c + 1) * CH)
            c0_ = c * CH
            par = c % 2
            if c >= 2:
                prep_proj(c)
            for h in range(2):
                bf_, bbf = hb[h], b_hb[h]
                A(K.pe, lambda e, h=h, bf_=bf_: e.matmul(fb2[par][0:64, h * 64 + 32:h * 64 + 64], lhsT=bf_["kt"][:, csl],
                                                         rhs=bf_["qt"][:, c0_ + 32:c0_ + 64], start=True, stop=True),
                  [bbf["kt"], bbf["qt"]], [b_fb2[par]], inc=False)
                A(K.pe, lambda e, h=h, bf_=bf_: e.matmul(fb2[par][0:32, h * 64:h * 64 + 32], lhsT=bf_["k0"][:, c0_:c0_ + 32],
                                                         rhs=bf_["qh"][:, c0_:c0_ + 32], start=True, stop=True),
                  [bbf["k0"], bbf["qh"]], [b_fb2[par]], inc=False)
                A(K.pe, lambda e, h=h, bf_=bf_: e.transpose(bkT[par][0:64, h * 128:(h + 1) * 128], bf_["kh"][:, csl],
                                                            C.ident_b()), [bbf["kh"], C.buf], [b_bkT[par]], inc=(h == 1))
            scv = fb2[par][0:64, 0:128].rearrange("p (h t) -> p h t", h=2)
            A(K.dve, lambda e: e.tensor_tensor(out=PT[:, par, :, 32:64], in0=scv[:, :, 32:64],
                                               in1=C.f[0:64, 160:192].unsqueeze(1).broadcast_to([64, 2, 32]),
                                               op=ALU.mult), [b_fb2[par], C.buf], [b_PT[par]])
            A(K.dve, lambda e: e.tensor_tensor(out=PT[0:32, par, :, 0:32], in0=scv[0:32, :, 0:32],
                                               in1=C.f[0:32, 128:160].unsqueeze(1).broadcast_to([32, 2, 32]),
                                               op=ALU.mult), [b_fb2[par], C.buf], [b_PT[par]])
            A(K.act, lambda e: e.activation(out=khs[:, par].rearrange("p h d -> p (h d)"), in_=bkT[par][0:64, 0:256],
                                            func=AF.Identity), [b_bkT[par]], [b_khs[par]])

        def main(c, s=s):
            csl = slice(c * CH, (c + 1) * CH)
            par = c % 2
            sp_, sn = c % 2, (c + 1) % 2
            for h in range(2):
                bf_, bbf = hb[h], b_hb[h]
                A(K.pe, lambda e, h=h: e.matmul(bkO[0:64, h * 128:(h + 1) * 128], lhsT=PT[:, par, h, :],
                                                rhs=vc[:, par, h * 128:(h + 1) * 128], start=True, stop=False),
                  [b_PT[par], b_vc[par]], [b_bkO], inc=False)
                A(K.pe, lambda e, h=h, bf_=bf_: e.matmul(bkO[0:64, h * 128:(h + 1) * 128], lhsT=bf_["qh"][:, csl],
                                                         rhs=Sbf[h][:, sp_, :], start=False, stop=True),
                  [bbf["qh"], b_Sbf[h][sp_]], [b_bkO], inc=(h == 1))
            for h in range(2):
                A(K.pe, lambda e, h=h: e.matmul(bkU[:, h * 128:(h + 1) * 128], lhsT=khs[:, par, h, :],
                                                rhs=vc[:, par, h * 128:(h + 1) * 128], start=True, stop=True),
                  [b_khs[par], b_vc[par]], [b_bkU], inc=(h == 1))
            for h in range(2):
                A(K.dve, lambda e, h=h: e.scalar_tensor_tensor(out=Sst[h][:], in0=Sst[h][:], scalar=eL[h][:, c:c + 1],
                                                               in1=bkU[:, h * 128:(h + 1) * 128], op0=ALU.mult,
                                                               op1=ALU.add), [b_S[h], b_eL[h], b_bkU], [b_S[h]])
                A(K.act, lambda e, h=h: e.activation(out=Sbf[h][:, sn, :], in_=Sst[h][:], func=AF.Identity),
                  [b_S[h]], [b_Sbf[h][sn]])

        def post_a(c, s=s):
            par = c % 2
            for h in range(2):
                A(K.act, lambda e, h=h: e.activation(out=junk[:, :], in_=bkO[0:64, h * 128:(h + 1) * 128], func=AF.Square,
                                                     accum_out=ssq[:, h, 0:1]), [b_bkO], [b_junk, b_ssq[h]])
                A(K.act, lambda e, h=h: e.activation(out=ssq[:, h, 1:2], in_=ssq[:, h, 0:1], func=AF.Sqrt,
                                                     scale=1.0 / 128, bias=EPS), [b_ssq[h]], [b_ssq[h]])
                A(K.dve, lambda e, h=h: e.reciprocal(out=ssq[:, h, 1:2], in_=ssq[:, h, 1:2]), [b_ssq[h]], [b_ssq[h]])
                A(K.dve, lambda e, h=h: e.scalar_tensor_tensor(out=og[:, h, :], in0=bkO[0:64, h * 128:(h + 1) * 128],
                                                               scalar=ssq[:, h, 1:2],
                                                               in1=gs[:, par, h * 128:(h + 1) * 128],
                                                               op0=ALU.mult, op1=ALU.mult),
                  [b_bkO, b_ssq[h], b_gs[par]], [b_og[h]])

        def post_b(c, s=s):
            csl = slice(c * CH, (c + 1) * CH)
            par = (c + 1) % 2
            for h in range(2):
                A(K.pe, lambda e, h=h: e.transpose(bkT[par][:, 256 + h * 64:256 + (h + 1) * 64], og[:, h, :],
                                                   C.ident_b(64)), [b_og[h], C.buf], [b_bkT[par]], inc=(h == 1))
            A(K.act, lambda e: e.activation(out=oTt[:, s, :, csl],
                                            in_=bkT[par][:, 256:384].rearrange("p (h t) -> p h t", h=2),
                                            func=AF.Identity), [b_bkT[par]], [b_oTt[s]])

        prep_proj(0)
        prep_proj(1)
        prep(0)
        for c in range(NCH):
            if c + 1 < NCH:
                prep(c + 1)
            main(c)
            if c >= 1:
                post_b(c - 1)
            post_a(c)
        post_b(NCH - 1)
        if o_store is not None:
            o_store(t, oTt[:, s], b_oTt[s], 2)
        else:
            K.dma(K.pool, lambda e, t=t, s=s: e.dma_start(out=ov[:, :, t * 512:(t + 1) * 512], in_=oTt[:, s]),
                  [b_oTt[s]], [], ds_s)


def build_hgrn(cfg, ab_idx):
    D, S = cfg["D"], cfg["S"]
    nc, stack, K = _new()
    cst = _din(nc, "consts", [128, 640])
    hT = _din(nc, "hT", [D, S], BF16)
    w = _din(nc, "w", [D, 1024])
    lbl = _din(nc, "lbl", [128, 2, 2])
    gain = _din(nc, "gain", [1, 256])
    oT = _dout(nc, "oT", [256, S], BF16)
    C = Consts(K, cst, K.dsems[0])
    emit_hgrn(K, C, cfg, ab_idx, hT, w, lbl, gain, oT)
    _finish(K, stack)
    return nc


def emit_fox(K, C, cfg, hT_d, w_d, qkg_d, fbias_d, oT_d, h_load=None, o_store=None, after_setup=None):
    D, KD, S = cfg["D"], cfg["KD"], cfg["S"]
    NB = S // 128
    A = K.op
    W = K.sb("x_W", [128, KD, 1026], BF16)
    hT = K.sb("x_hT", [128, 2, KD, 512], BF16)
    kTa = K.sb("x_kTa", [128, 2, S], BF16)
    va = K.sb("x_va", [128, 2, NB, 132], BF16)
    Ga = K.sb("x_Ga", [128, 2, NB], F32)
    qkg = K.sb("x_qkg", [128, 2], F32)
    nfb = K.sb("x_nfb", [128, 2], F32)
    qn = K.sb("x_qn", [128, 2, 512], BF16)
    sq = K.sb("x_sq", [128, 2, 512], F32)
    rt = K.sb("x_rt", [128, 2, 512], F32)
    sg = K.sb("x_sg", [128, 4, 256], F32)
    spv = K.sb("x_spv", [128, 2, 4], F32)
    lcs = K.sb("x_lcs", [128, 2, 4], F32)
    tots = K.sb("x_tots", [128, 2, 4], F32)
    incl = K.sb("x_incl", [128, 2, 2, 4], F32)
    excl = K.sb("x_excl", [128, 2, 4], F32)
    Bq = K.sb("x_Bq", [128, 2, NB], F32)
    PT = K.sb("x_PT", [128, 3, 512], BF16)
    rden = K.sb("x_rden", [128, 4], F32)
    dqc = K.sb("x_dqc", [128, 2, 4], F32)
    dqb = K.sb("x_dqb", [1, 2, 512], F32)
    dqbc = K.sb("x_dqbc", [128, 2, 512], F32)
    stg = K.sb("x_stg", [128, 3, 512], F32)
    b_dqc, b_dqb, b_dqbc = Buf(), Buf(), Buf()
    b_stg = [Buf() for _ in range(3)]
    og = K.sb("x_og", [128, 2, 128], BF16)
    oTt = K.sb("x_oTt", [128, 2, 2, 512], BF16)
    bk = [K.ps("x_bk%d" % i, [128, 512], F32) for i in range(7)]
    bkT = K.ps("x_bkT", [128, 1024], BF16)
    b_bk = [Buf() for _ in range(7)]
    b_T = Buf()
    P0, P1, N_, V_, Fb, O3, P2 = range(7)
    OB = [N_, V_, Fb, O3]
    B = lambda n: [Buf() for _ in range(n)]
    b_W, b_kTa, b_va, b_Ga, b_par = Buf(), Buf(), Buf(), Buf(), Buf()
    b_hT, b_qn, b_sq, b_rt, b_oTt = B(2), B(2), B(2), B(2), B(2)
    b_sg, b_spv, b_lcs, b_tots, b_excl = Buf(), Buf(), Buf(), Buf(), Buf()
    b_incl, b_Bq = B(2), Buf()
    b_PT = B(3)
    b_rden, b_og = Buf(), B(2)
    wv = w_d.rearrange("(k p) c -> p k c", p=128)
    K.dma(K.pool, lambda e: e.dma_start(out=W[:, :, 0:512], in_=wv[:, :, 0:512]), [], [b_W])
    K.dma(K.pool, lambda e: e.dma_start(out=W[:, :, 512:1026], in_=wv[:, :, 512:1026]), [], [b_W])
    if after_setup is not None:
        after_setup()
    K.dma(K.sp, lambda e: e.dma_start(out=qkg[:], in_=qkg_d[:, :]), [], [b_par])
    K.dma(K.sp, lambda e: e.dma_start(out=nfb[:], in_=fbias_d[0:1, :].partition_broadcast(128)), [], [b_par])
    A(K.dve, lambda e: e.tensor_scalar(out=qkg[:, 0:1], in0=qkg[:, 0:1], scalar1=float(128 ** -0.5), scalar2=None,
                                       op0=ALU.mult), [b_par], [b_par])
    A(K.dve, lambda e: e.tensor_scalar(out=nfb[:, :], in0=nfb[:, :], scalar1=-1.0, scalar2=None, op0=ALU.mult),
      [b_par], [b_par])
    A(K.dve, lambda e: e.memset(va[:], 1.0), [], [b_va])
    hv = None if h_load is not None else hT_d.rearrange("(k p) t -> p k t", p=128)
    ov = None if o_store is not None else oT_d.rearrange("(h p) t -> p h t", p=128)
    xs = 0
    for t in range(S // 512):
        s = t % 2
        tsl = slice(t * 512, (t + 1) * 512)
        if h_load is not None:
            h_load(t, hT[:, s], b_hT[s])
        else:
            K.dma(K.sp, lambda e, s=s, tsl=tsl: e.dma_start(out=hT[:, s], in_=hv[:, :, tsl]), [], [b_hT[s]])
        for h in range(2):
            for (which, bank) in ((0, P0), (1, P1)):
                blk = 2 * h + which
                for k in range(KD):
                    A(K.pe, lambda e, blk=blk, k=k, s=s, bank=bank: e.matmul(
                        bk[bank][:, :], lhsT=W[:, k, blk * 128:(blk + 1) * 128], rhs=hT[:, s, k, :],
                        start=(k == 0), stop=(k == KD - 1)), [b_W, b_hT[s]], [b_bk[bank]], inc=(k == KD - 1))
                x = which
                A(K.act, lambda e, x=x, bank=bank: e.activation(out=sq[:, x, :], in_=bk[bank][:, :], func=AF.Square),
                  [b_bk[bank]], [b_sq[x]])
                A(K.pe, lambda e, x=x: e.matmul(bk[N_][:, :], lhsT=C.ones_f(), rhs=sq[:, x, :], start=True, stop=True),
                  [b_sq[x], C.buf], [b_bk[N_]])
                A(K.act, lambda e, x=x: e.activation(out=rt[:, x, :], in_=bk[N_][:, :], func=AF.Sqrt, scale=1.0 / 128,
                                                     bias=EPS), [b_bk[N_]], [b_rt[x]])
                A(K.dve, lambda e, x=x: e.reciprocal(out=rt[:, x, :], in_=rt[:, x, :]), [b_rt[x]], [b_rt[x]])
                if which == 0:
                    A(K.dve, lambda e, h=h, x=x, bank=bank: e.scalar_tensor_tensor(
                        out=qn[:, h, :], in0=bk[bank][:, :], scalar=qkg[:, 0:1], in1=rt[:, x, :], op0=ALU.mult,
                        op1=ALU.mult), [b_bk[bank], b_par, b_rt[x]], [b_qn[h]])
                else:
                    A(K.dve, lambda e, h=h, x=x, bank=bank, tsl=tsl: e.scalar_tensor_tensor(
                        out=kTa[:, h, tsl], in0=bk[bank][:, :], scalar=qkg[:, 1:2], in1=rt[:, x, :], op0=ALU.mult,
                        op1=ALU.mult), [b_bk[bank], b_par, b_rt[x]], [b_kTa])
        for b in range(4):
            blk = 4 * t + b
            bsl = slice(b * 128, (b + 1) * 128)
            for k in range(KD):
                A(K.pe, lambda e, k=k, s=s, bsl=bsl: e.matmul(bk[V_][:, :], lhsT=hT[:, s, k, bsl], rhs=W[:, k, 512:1024],
                                                              start=(k == 0), stop=(k == KD - 1)),
                  [b_W, b_hT[s]], [b_bk[V_]], inc=(k == KD - 1))
            A(K.act, lambda e, blk=blk: e.activation(out=va[:, :, blk, 0:128],
                                                     in_=bk[V_][:, 0:256].rearrange("p (h e) -> p h e", h=2),
                                                     func=AF.Identity), [b_bk[V_]], [b_va])
            A(K.act, lambda e, b=b: e.activation(out=sg[:, b, :], in_=bk[V_][:, 256:512], func=AF.Sigmoid),
              [b_bk[V_]], [b_sg])
            for k in range(KD):
                A(K.pe, lambda e, k=k, s=s, bsl=bsl, b=b: e.matmul(bk[Fb][:, 2 * b:2 * b + 2], lhsT=hT[:, s, k, bsl],
                                                                   rhs=W[:, k, 1024:1026], start=(k == 0),
                                                                   stop=(k == KD - 1)),
                  [b_W, b_hT[s]], [b_bk[Fb]], inc=(k == KD - 1))
        fbv = bk[Fb][:, 0:8].rearrange("p (b h) -> p h b", h=2)
        for h in range(2):
            A(K.act, lambda e, h=h: e.activation(out=spv[:, h, :], in_=fbv[:, h, :], func=AF.Exp, scale=-1.0,
                                                 bias=nfb[:, h:h + 1]), [b_bk[Fb], b_par], [b_spv])
        A(K.act, lambda e: e.activation(out=spv[:], in_=spv[:], func=AF.Ln, bias=1.0, scale=1.0), [b_spv], [b_spv])
        A(K.pe, lambda e: e.matmul(bk[Fb][:, 16:24], lhsT=C.tri_f(), rhs=spv[:].rearrange("p h b -> p (h b)"),
                                   start=True, stop=True), [b_spv, C.buf], [b_bk[Fb]])
        A(K.act, lambda e: e.activation(out=lcs[:].rearrange("p h b -> p (h b)"), in_=bk[Fb][:, 16:24],
                                        func=AF.Identity), [b_bk[Fb]], [b_lcs])
        A(K.pe, lambda e: e.matmul(bk[Fb][:, 32:40], lhsT=C.sel_f(), rhs=lcs[:].rearrange("p h b -> p (h b)"),
                                   start=True, stop=True), [b_lcs, C.buf], [b_bk[Fb]])
        A(K.act, lambda e: e.activation(out=tots[:].rearrange("p h b -> p (h b)"), in_=bk[Fb][:, 32:40],
                                        func=AF.Identity), [b_bk[Fb]], [b_tots])
        for h in range(2):
            init = 0.0 if t == 0 else incl[:, 1 - s, h, 3:4]
            A(K.dve, lambda e, h=h, s=s, init=init: e.tensor_tensor_scan(
                out=incl[:, s, h, :], data0=C.ones_f(128, 4), data1=tots[:, h, :], initial=init, op0=ALU.mult,
                op1=ALU.add), [b_tots, C.buf, b_incl[1 - s]], [b_incl[s]])
        A(K.dve, lambda e, s=s: e.tensor_tensor(out=excl[:], in0=incl[:, s], in1=tots[:], op=ALU.subtract),
          [b_incl[s], b_tots], [b_excl])
        A(K.dve, lambda e, t=t: e.tensor_tensor(out=Ga[:, :, 4 * t:4 * t + 4], in0=lcs[:], in1=excl[:], op=ALU.add),
          [b_lcs, b_excl], [b_Ga])
        nkb = 4 * t + 4
        for h in range(2):
            A(K.dve, lambda e, h=h, s=s, nkb=nkb: e.tensor_scalar(
                out=Bq[:, h, 0:nkb], in0=Ga[:, h, 0:nkb], scalar1=incl[:, s, h, 3:4], scalar2=None,
                op0=ALU.subtract), [b_Ga, b_incl[s]], [b_Bq])
            A(K.dve, lambda e, h=h, s=s, t=t: e.tensor_scalar(
                out=dqc[:, h, :], in0=Ga[:, h, 4 * t:4 * t + 4], scalar1=incl[:, s, h, 3:4], scalar2=-1.0,
                op0=ALU.subtract, op1=ALU.mult), [b_Ga, b_incl[s]], [b_dqc])
        for h in range(2):
            for jl in range(4):
                A(K.pe, lambda e, h=h, jl=jl: e.transpose(bk[O3][0:1, jl * 128:(jl + 1) * 128], dqc[:, h, jl:jl + 1],
                                                          C.ident_f()), [b_dqc, C.buf], [b_bk[O3]])
            A(K.act, lambda e, h=h: e.activation(out=dqb[0:1, h, :], in_=bk[O3][0:1, :], func=AF.Identity),
              [b_bk[O3]], [b_dqb])
            A(K.pe, lambda e, h=h: e.matmul(bk[O3][:, :], lhsT=C.ones_f(1, 128), rhs=dqb[0:1, h, :], start=True,
                                            stop=True), [b_dqb, C.buf], [b_bk[O3]])
            A(K.act, lambda e, h=h: e.activation(out=dqbc[:, h, :], in_=bk[O3][:, :], func=AF.Identity),
              [b_bk[O3]], [b_dqbc])
        SB = (P0, P1, P2)
        for h in range(2):
            def score(kb, h=h):
                m = max(0, kb - 4 * t)
                x3 = kb % 3
                sb_ = SB[x3]
                diag = kb >= 4 * t
                A(K.pe, lambda e: e.matmul(bk[sb_][:, m * 128:512], lhsT=kTa[:, h, kb * 128:(kb + 1) * 128],
                                           rhs=qn[:, h, m * 128:512], start=True, stop=(not diag)),
                  [b_kTa, b_qn[h]], [b_bk[sb_]], inc=(not diag))
                if diag:
                    A(K.pe, lambda e: e.matmul(bk[sb_][:, m * 128:(m + 1) * 128], lhsT=C.ident_b(), rhs=C.negmask_b(),
                                               start=False, stop=True), [C.buf], [b_bk[sb_]])
                A(K.dve, lambda e: e.tensor_tensor(out=stg[:, x3, m * 128:512], in0=bk[sb_][:, m * 128:512],
                                                   in1=dqbc[:, h, m * 128:512], op=ALU.add),
                  [b_bk[sb_], b_dqbc], [b_stg[x3]])
                A(K.act, lambda e: e.activation(out=PT[:, x3, m * 128:512], in_=stg[:, x3, m * 128:512], func=AF.Exp,
                                                bias=Bq[:, h, kb:kb + 1], scale=1.0), [b_stg[x3], b_Bq], [b_PT[x3]])

            def pv(kb, h=h):
                m = max(0, kb - 4 * t)
                x3 = kb % 3
                for jl in range(m, 4):
                    j = 4 * t + jl
                    A(K.pe, lambda e, jl=jl, j=j: e.matmul(bk[OB[jl]][:, 0:129], lhsT=PT[:, x3, jl * 128:(jl + 1) * 128],
                                                           rhs=va[:, h, kb, 0:129], start=(kb == 0), stop=(kb == j)),
                      [b_PT[x3], b_va], [b_bk[OB[jl]]], inc=(jl == 3 or kb == j))
            score(0)
            score(1)
            for kb in range(2, nkb):
                score(kb)
                pv(kb - 2)
            pv(nkb - 2)
            pv(nkb - 1)
            for jl in range(4):
                A(K.dve, lambda e, jl=jl: e.reciprocal(out=rden[:, jl:jl + 1], in_=bk[OB[jl]][:, 128:129]),
                  [b_bk[OB[jl]]], [b_rden])
                x = jl % 2
                A(K.dve, lambda e, jl=jl, h=h, x=x: e.scalar_tensor_tensor(
                    out=og[:, x, :], in0=bk[OB[jl]][:, 0:128], scalar=rden[:, jl:jl + 1],
                    in1=sg[:, jl, h * 128:(h + 1) * 128], op0=ALU.mult, op1=ALU.mult),
                    [b_bk[OB[jl]], b_rden, b_sg], [b_og[x]])
                A(K.pe, lambda e, x=x: e.transpose(bkT[:, 0:128], og[:, x, :], C.ident_b()), [b_og[x], C.buf], [b_T])
                A(K.act, lambda e, s=s, h=h, jl=jl: e.activation(out=oTt[:, s, h, jl * 128:(jl + 1) * 128],
                                                                 in_=bkT[:, 0:128], func=AF.Identity),
                  [b_T], [b_oTt[s]])
        if o_store is not None:
            o_store(t, oTt[:, s], b_oTt[s], 2)
        else:
            K.dma(K.pool, lambda e, s=s, tsl=tsl: e.dma_start(out=ov[:, :, tsl], in_=oTt[:, s]), [b_oTt[s]], [])


def build_fox(cfg):
    D, S = cfg["D"], cfg["S"]
    nc, stack, K = _new()
    cst = _din(nc, "consts", [128, 640])
    hT = _din(nc, "hT", [D, S], BF16)
    w = _din(nc, "w", [D, 1026])
    qkg = _din(nc, "qkg", [128, 2])
    fbias = _din(nc, "fbias", [1, 2])
    oT = _dout(nc, "oT", [256, S], BF16)
    C = Consts(K, cst, None)
    emit_fox(K, C, cfg, hT, w, qkg, fbias, oT)
    _finish(K, stack)
    return nc


def emit_gla(K, C, cfg, hT_d, w_d, wgu_d, bg_d, gain_d, oT_d, h_load=None, o_store=None, after_setup=None):
    D, KD, S = cfg["D"], cfg["KD"], cfg["S"]
    A = K.op
    CH, NCH = 128, 4
    NW = 1552
    W = K.sb("g_W", [128, KD, NW], BF16)
    hT = K.sb("g_hT", [128, 2, KD, 512], BF16)
    wgu = K.sb("g_wgu", [16, 256], F32)
    bg = K.sb("g_bg", [128, 2], F32)
    gain = K.sb("g_gain", [128, 512], F32)
    ones = K.sb("g_ones", [128, 128], F32)
    glT = K.sb("g_glT", [16, 512], F32)
    spt = K.sb("g_sp", [128, 2, 512], F32)
    csp = K.sb("g_csp", [128, 2, 512], F32)
    Eq = K.sb("g_Eq", [128, 2, 512], F32)
    Ek = K.sb("g_Ek", [128, 2, 512], F32)
    eL = K.sb("g_eL", [128, 2, NCH], F32)
    qh = K.sb("g_qh", [128, 2, 512], BF16)
    kt = K.sb("g_kt", [128, 2, 512], BF16)
    kh = K.sb("g_kh", [128, 2, 512], BF16)
    Sst = K.sb("g_S", [128, 2, 512], F32)
    Sbf = K.sb("g_Sbf", [128, 2, 2, 512], BF16)
    vc = K.sb("g_vc", [128, 2, 512], BF16)
    gs = K.sb("g_gs", [128, 2, 512], F32)
    PT = K.sb("g_PT", [128, 2, 128], BF16)
    khs = K.sb("g_khs", [128, 2, 256], BF16)
    junk = K.sb("g_junk", [128, 512], F32)
    ssq = K.sb("g_ssq", [128, 2], F32)
    og = K.sb("g_og", [128, 512], BF16)
    oTt = K.sb("g_oTt", [128, 2, 4, 512], BF16)
    bk = [K.ps("g_bk%d" % i, [128, 512], F32) for i in range(6)]
    bkT = K.ps("g_bkT", [128, 1024], BF16)
    bkT2 = K.ps("g_bkT2", [128, 1024], BF16)
    b_bk = [Buf() for _ in range(6)]
    b_T, b_T2 = Buf(), Buf()
    GA, Z0, Z1, V_, Gt, O_ = range(6)
    B = lambda n: [Buf() for _ in range(n)]
    b_W, b_par, b_ones, b_glT = Buf(), Buf(), Buf(), Buf()
    b_hT, b_sp, b_csp, b_Eq, b_Ek, b_eL = B(2), B(2), B(2), B(2), B(2), B(2)
    b_qh, b_kt, b_kh, b_S = B(2), B(2), B(2), B(2)
    b_Sbf = [B(2), B(2)]
    b_vc, b_gs, b_PT, b_khs, b_oTt = B(2), B(2), B(2), B(2), B(2)
    b_junk, b_ssq, b_og = Buf(), Buf(), Buf()
    wv = w_d.rearrange("(k p) c -> p k c", p=128)
    for (c0, c1) in ((0, 528), (528, 1040), (1040, 1552)):
        K.dma(K.pool, lambda e, c0=c0, c1=c1: e.dma_start(out=W[:, :, c0:c1], in_=wv[:, :, c0:c1]), [], [b_W])
    if after_setup is not None:
        after_setup()
    K.dma(K.sp, lambda e: e.dma_start(out=wgu[:], in_=wgu_d[:, :]), [], [b_par])
    K.dma(K.sp, lambda e: e.dma_start(out=bg[:], in_=bg_d[:, :]), [], [b_par])
    K.dma(K.sp, lambda e: e.dma_start(out=gain[:], in_=gain_d[0:1, :].partition_broadcast(128)), [], [b_par])
    A(K.dve, lambda e: e.tensor_scalar(out=bg[:, :], in0=bg[:, :], scalar1=-1.0, scalar2=None, op0=ALU.mult),
      [b_par], [b_par])
    A(K.dve, lambda e: e.memset(ones[:], 1.0), [], [b_ones])
    for dc in range(2):
        A(K.dve, lambda e, dc=dc: e.memset(Sst[:, dc, :], 0.0), [], [b_S[dc]])
        A(K.dve, lambda e, dc=dc: e.memset(Sbf[:, dc, 0, :], 0.0), [], [b_Sbf[dc][0]])
    hv = None if h_load is not None else hT_d.rearrange("(k p) t -> p k t", p=128)
    ov = None if o_store is not None else oT_d.rearrange("(h p) t -> p h t", p=128)
    c3 = lambda a: a.rearrange("p (c t) -> p c t", c=NCH)
    cg = 0
    for t in range(S // 512):
        s = t % 2
        tsl = slice(t * 512, (t + 1) * 512)
        if h_load is not None:
            h_load(t, hT[:, s], b_hT[s])
        else:
            K.dma(K.sp, lambda e, s=s, tsl=tsl: e.dma_start(out=hT[:, s], in_=hv[:, :, tsl]), [], [b_hT[s]])
        def prep_proj(c, s=s):
            csl = slice(c * CH, (c + 1) * CH)
            ts = c % 2
            for (bank, c0) in ((V_, 528), (Gt, 1040)):
                for k in range(KD):
                    A(K.pe, lambda e, k=k, bank=bank, c0=c0: e.matmul(
                        bk[bank][:, :], lhsT=hT[:, s, k, csl], rhs=W[:, k, c0:c0 + 512], start=(k == 0),
                        stop=(k == KD - 1)), [b_W, b_hT[s]], [b_bk[bank]], inc=(k == KD - 1))
            A(K.act, lambda e: e.activation(out=vc[:, ts, :], in_=bk[V_][:, :], func=AF.Identity),
              [b_bk[V_]], [b_vc[ts]])
            A(K.act, lambda e: e.activation(out=gs[:, ts, :], in_=bk[Gt][:, :], func=AF.Silu),
              [b_bk[Gt]], [b_gs[ts]])
            A(K.dve, lambda e: e.tensor_tensor(out=gs[:, ts, :], in0=gs[:, ts, :], in1=gain[:, :], op=ALU.mult),
              [b_gs[ts], b_par], [b_gs[ts]])

        for k in range(KD):
            A(K.pe, lambda e, k=k, s=s: e.matmul(bk[GA][0:16, :], lhsT=W[:, k, 512:528], rhs=hT[:, s, k, :],
                                                 start=(k == 0), stop=(k == KD - 1)),
              [b_W, b_hT[s]], [b_bk[GA]], inc=(k == KD - 1))
        prep_proj(0)
        prep_proj(1)
        A(K.act, lambda e: e.activation(out=glT[:, :], in_=bk[GA][0:16, :], func=AF.Identity), [b_bk[GA]], [b_glT])
        for dc in range(2):
            zb = (Z0, Z1)[dc]
            A(K.pe, lambda e, dc=dc, zb=zb: e.matmul(bk[zb][:, :], lhsT=wgu[:, dc * 128:(dc + 1) * 128], rhs=glT[:, :],
                                                     start=True, stop=True), [b_par, b_glT], [b_bk[zb]])
            A(K.act, lambda e, dc=dc, zb=zb: e.activation(out=spt[:, dc, :], in_=bk[zb][:, :], func=AF.Exp, scale=-1.0,
                                                          bias=bg[:, dc:dc + 1]), [b_bk[zb], b_par], [b_sp[dc]])
            A(K.act, lambda e, dc=dc: e.activation(out=spt[:, dc, :], in_=spt[:, dc, :], func=AF.Ln, bias=1.0,
                                                   scale=1.0), [b_sp[dc]], [b_sp[dc]])
            for c in range(NCH):
                A(K.dve, lambda e, dc=dc, c=c: e.tensor_tensor_scan(
                    out=csp[:, dc, c * CH:(c + 1) * CH], data0=ones[:, :], data1=spt[:, dc, c * CH:(c + 1) * CH],
                    initial=0.0, op0=ALU.mult, op1=ALU.add), [b_sp[dc], b_ones], [b_csp[dc]])
            A(K.act, lambda e, dc=dc: e.activation(out=Eq[:, dc, :], in_=csp[:, dc, :], func=AF.Exp, scale=-1.0 / 16),
              [b_csp[dc]], [b_Eq[dc]])
            A(K.act, lambda e, dc=dc: e.activation(out=Ek[:, dc, :], in_=csp[:, dc, :], func=AF.Exp, scale=1.0 / 16),
              [b_csp[dc]], [b_Ek[dc]])
            A(K.act, lambda e, dc=dc: e.activation(out=eL[:, dc, :], in_=c3(csp[:, dc, :])[:, :, CH - 1], func=AF.Exp,
                                                   scale=-1.0 / 16), [b_csp[dc]], [b_eL[dc]])
            for (which, dst) in ((0, "q"), (1, "k")):
                blk = which * 2 + dc
                for k in range(KD):
                    A(K.pe, lambda e, blk=blk, k=k, s=s, zb=zb: e.matmul(
                        bk[zb][:, :], lhsT=W[:, k, blk * 128:(blk + 1) * 128], rhs=hT[:, s, k, :], start=(k == 0),
                        stop=(k == KD - 1)), [b_W, b_hT[s]], [b_bk[zb]], inc=(k == KD - 1))
                if which == 0:
                    A(K.dve, lambda e, dc=dc, zb=zb: e.scalar_tensor_tensor(
                        out=qh[:, dc, :], in0=bk[zb][:, :], scalar=float(256 ** -0.5), in1=Eq[:, dc, :], op0=ALU.mult,
                        op1=ALU.mult), [b_bk[zb], b_Eq[dc]], [b_qh[dc]])
                else:
                    A(K.dve, lambda e, dc=dc, zb=zb: e.tensor_tensor(out=kt[:, dc, :], in0=bk[zb][:, :],
                                                                     in1=Ek[:, dc, :], op=ALU.mult),
                      [b_bk[zb], b_Ek[dc]], [b_kt[dc]])
                    A(K.dve, lambda e, dc=dc: e.tensor_tensor(
                        out=c3(kh[:, dc, :]), in0=c3(kt[:, dc, :]),
                        in1=eL[:, dc, :].unsqueeze(2).broadcast_to([128, NCH, CH]), op=ALU.mult),
                        [b_kt[dc], b_eL[dc]], [b_kh[dc]])
        def prep(c, s=s):
            csl = slice(c * CH, (c + 1) * CH)
            ts = c % 2
            if c >= 2:
                prep_proj(c)
            for dc in range(2):
                A(K.pe, lambda e, dc=dc: e.matmul(bk[GA][:, 0:128], lhsT=kt[:, dc, csl], rhs=qh[:, dc, csl],
                                                  start=(dc == 0), stop=(dc == 1)),
                  [b_kt[dc], b_qh[dc]], [b_bk[GA]], inc=(dc == 1))
            A(K.dve, lambda e: e.tensor_tensor(out=PT[:, ts, :], in0=bk[GA][:, 0:128], in1=C.tri_f(), op=ALU.mult),
              [b_bk[GA], C.buf], [b_PT[ts]])
            for dc in range(2):
                A(K.pe, lambda e, dc=dc: e.transpose(bkT[:, dc * 128:(dc + 1) * 128], kh[:, dc, csl], C.ident_b()),
                  [b_kh[dc], C.buf], [b_T], inc=(dc == 1))
            A(K.act, lambda e: e.activation(out=khs[:, ts, :], in_=bkT[:, 0:256], func=AF.Identity),
              [b_T], [b_khs[ts]])

        def main(c, s=s):
            csl = slice(c * CH, (c + 1) * CH)
            ts = c % 2
            sp_, sn = c % 2, (c + 1) % 2
            A(K.pe, lambda e: e.matmul(bk[O_][:, :], lhsT=PT[:, ts, :], rhs=vc[:, ts, :], start=True, stop=False),
              [b_PT[ts], b_vc[ts]], [b_bk[O_]], inc=False)
            for dc in range(2):
                A(K.pe, lambda e, dc=dc: e.matmul(bk[O_][:, :], lhsT=qh[:, dc, csl], rhs=Sbf[:, dc, sp_, :], start=False,
                                                  stop=(dc == 1)),
                  [b_qh[dc], b_Sbf[dc][sp_]], [b_bk[O_]], inc=(dc == 1))
            for dc in range(2):
                ub = (Z0, Z1)[dc]
                A(K.pe, lambda e, dc=dc, ub=ub: e.matmul(bk[ub][:, :], lhsT=khs[:, ts, dc * 128:(dc + 1) * 128],
                                                         rhs=vc[:, ts, :], start=True, stop=True),
                  [b_khs[ts], b_vc[ts]], [b_bk[ub]])
            for dc in range(2):
                ub = (Z0, Z1)[dc]
                A(K.dve, lambda e, dc=dc, ub=ub: e.scalar_tensor_tensor(
                    out=Sst[:, dc, :], in0=Sst[:, dc, :], scalar=eL[:, dc, c:c + 1], in1=bk[ub][:, :], op0=ALU.mult,
                    op1=ALU.add), [b_S[dc], b_eL[dc], b_bk[ub]], [b_S[dc]])
                A(K.act, lambda e, dc=dc: e.activation(out=Sbf[:, dc, sn, :], in_=Sst[:, dc, :], func=AF.Identity),
                  [b_S[dc]], [b_Sbf[dc][sn]])

        def post_a(c, s=s):
            ts = c % 2
            A(K.act, lambda e: e.activation(out=junk[:, :], in_=bk[O_][:, :], func=AF.Square, accum_out=ssq[:, 0:1]),
              [b_bk[O_]], [b_junk, b_ssq])
            A(K.act, lambda e: e.activation(out=ssq[:, 1:2], in_=ssq[:, 0:1], func=AF.Sqrt, scale=1.0 / 512, bias=EPS),
              [b_ssq], [b_ssq])
            A(K.dve, lambda e: e.reciprocal(out=ssq[:, 1:2], in_=ssq[:, 1:2]), [b_ssq], [b_ssq])
            A(K.dve, lambda e: e.scalar_tensor_tensor(out=og[:, :], in0=bk[O_][:, :], scalar=ssq[:, 1:2],
                                                      in1=gs[:, ts, :], op0=ALU.mult, op1=ALU.mult),
              [b_bk[O_], b_ssq, b_gs[ts]], [b_og])

        def post_b(c, s=s):
            csl = slice(c * CH, (c + 1) * CH)
            for ec in range(4):
                A(K.pe, lambda e, ec=ec: e.transpose(bkT2[:, ec * 128:(ec + 1) * 128], og[:, ec * 128:(ec + 1) * 128],
                                                     C.ident_b()), [b_og, C.buf], [b_T2], inc=(ec == 3))
            A(K.act, lambda e: e.activation(out=oTt[:, s, :, csl], in_=bkT2[:, 0:512].rearrange("p (a q) -> p a q", a=4),
                                            func=AF.Identity), [b_T2], [b_oTt[s]])

        prep(0)
        for c in range(NCH):
            if c + 1 < NCH:
                prep(c + 1)
            main(c)
            if c >= 1:
                post_b(c - 1)
            post_a(c)
        post_b(NCH - 1)
        if o_store is not None:
            o_store(t, oTt[:, s], b_oTt[s], 4)
        else:
            K.dma(K.pool, lambda e, s=s, tsl=tsl: e.dma_start(out=ov[:, :, tsl], in_=oTt[:, s]), [b_oTt[s]], [])


def build_gla(cfg):
    D, S = cfg["D"], cfg["S"]
    nc, stack, K = _new()
    cst = _din(nc, "consts", [128, 640])
    hT = _din(nc, "hT", [D, S], BF16)
    w = _din(nc, "w", [D, 1552])
    wgu = _din(nc, "wgu", [16, 256])
    bg = _din(nc, "bg", [128, 2])
    gain = _din(nc, "gain", [1, 512])
    oT = _dout(nc, "oT", [512, S], BF16)
    C = Consts(K, cst, None)
    emit_gla(K, C, cfg, hT, w, wgu, bg, gain, oT)
    _finish(K, stack)
    return nc


CFG = dict(L=4, D=2048, KD=16, T=2048, S=8192, DFF=5632, KF=44, KO=16)
_PROGS = {}


def _prog(key, fn):
    if key not in _PROGS:
        _PROGS[key] = fn()
    return _PROGS[key]


def _run(nc, in_maps):
    res = run_bass_kernel_spmd(nc, in_maps, core_ids=list(range(8)))
    return res.results


def _fm(v, kd):
    v = np.asarray(v)
    lead = v.shape[:-1]
    a = v.reshape(lead + (kd, 128))
    return np.ascontiguousarray(np.moveaxis(a, -1, 0))


def kernel_unfused(x, c, mod_w, mod_b, norm_mix_gain, norm_ffn_gain, ab_w_in, ab_w_out, hgrn_lb_logits, hgrn_out_gain,
                   fox_q_gain, fox_k_gain, fox_f_bias, gla_w_in, gla_w_gate_up, gla_b_gate, gla_out_gain, gla_w_out,
                   ffn_w_in, ffn_w_out):
    cfg = CFG
    L, D, KD, T, S, DFF = cfg["L"], cfg["D"], cfg["KD"], cfg["T"], cfg["S"], cfg["DFF"]
    f32 = lambda a: np.ascontiguousarray(np.asarray(a, dtype=np.float32))
    x, c, mod_w, mod_b = f32(x), f32(c), f32(mod_w), f32(mod_b)
    consts = make_consts()
    cores = [(cid // 4, cid % 4) for cid in range(8)]

    cfg1 = dict(cfg)
    cfg1["L"] = 1
    nc = _prog("mod", lambda: build_mod(cfg1))
    ims = []
    for (b, j) in cores:
        ims.append({"consts": consts, "cT": _fm(c[b], KD), "modw": mod_w[j:j + 1], "modb": mod_b[j:j + 1],
                    "gmix": _fm(f32(norm_mix_gain)[j:j + 1], KD), "gffn": _fm(f32(norm_ffn_gain)[j:j + 1], KD)})
    r = _run(nc, ims)
    vec = [np.ascontiguousarray(np.concatenate([r[b * 4 + j]["vec"] for j in range(4)], axis=1)) for b in range(2)]

    nc = _prog("norm0", lambda: build_norm0(cfg))
    xT = [np.ascontiguousarray(x[b, j * T:(j + 1) * T, :].T) for (b, j) in cores]
    r = _run(nc, [{"consts": consts, "xT": xT[i], "vec": vec[cores[i][0]]} for i in range(8)])
    hT = [r[i]["hT"] for i in range(8)]

    for l in range(L):
        hfull = [np.ascontiguousarray(np.concatenate(hT[b * 4:b * 4 + 4], axis=1)) for b in range(2)]
        if l % 2 == 0:
            i = l // 2
            w = f32(ab_w_in[i])
            lbl_all = f32(hgrn_lb_logits)
            ims_h, ims_f = [], []
            for (b, j) in cores:
                hh = (2 * j, 2 * j + 1)
                cs = lambda base, h_: w[:, base + h_ * 128:base + (h_ + 1) * 128]
                wh = np.concatenate([cs(0, hh[0]), cs(1024, hh[0]), cs(0, hh[1]), cs(1024, hh[1]),
                                     cs(2048, hh[0]), cs(2048, hh[1]), cs(3072, hh[0]), cs(3072, hh[1])], axis=1)
                lbl = np.stack([np.stack([lbl_all[ly, h_ * 128:(h_ + 1) * 128] for h_ in hh], axis=1)
                                for ly in range(2)], axis=1)
                ims_h.append({"consts": consts, "hT": hfull[b], "w": np.ascontiguousarray(wh),
                              "lbl": f32(lbl), "gain": f32(hgrn_out_gain[i][2 * j:2 * j + 2]).reshape(1, 256)})
                wf = np.concatenate([cs(4096, hh[0]), cs(5120, hh[0]), cs(4096, hh[1]), cs(5120, hh[1]),
                                     cs(6144, hh[0]), cs(6144, hh[1]), cs(7168, hh[0]), cs(7168, hh[1]),
                                     w[:, 8192 + hh[0]:8192 + hh[0] + 1], w[:, 8192 + hh[1]:8192 + hh[1] + 1]], axis=1)
                ims_f.append({"consts": consts, "hT": hfull[b], "w": np.ascontiguousarray(wf),
                              "qkg": f32(np.stack([fox_q_gain[i], fox_k_gain[i]], axis=1)),
                              "fbias": f32(fox_f_bias[i][2 * j:2 * j + 2]).reshape(1, 2)})
            rh = _run(_prog(("hgrn", i), lambda: build_hgrn(cfg, i)), ims_h)
            rf = _run(_prog("fox", lambda: build_fox(cfg)), ims_f)
            ofull = [np.concatenate([np.concatenate([rh[b * 4 + j]["oT"], rf[b * 4 + j]["oT"]], axis=0)
                                     for j in range(4)], axis=0) for b in range(2)]
            wo_src = f32(ab_w_out[i])
            perm = []
            for j in range(4):
                for typ in range(2):
                    for hl in range(2):
                        base = typ * 1024 + (2 * j + hl) * 128
                        perm.extend(range(base, base + 128))
            wo = np.ascontiguousarray(wo_src[np.asarray(perm)])
        else:
            i = l // 2
            w = f32(gla_w_in[i])
            ims_g = []
            for (b, j) in cores:
                wg = np.concatenate([w[:, j * 256:(j + 1) * 256], w[:, 1024 + j * 256:1024 + (j + 1) * 256],
                                     w[:, 6144:6160], w[:, 2048 + j * 512:2048 + (j + 1) * 512],
                                     w[:, 4096 + j * 512:4096 + (j + 1) * 512]], axis=1)
                ims_g.append({"consts": consts, "hT": hfull[b], "w": np.ascontiguousarray(wg),
                              "wgu": f32(gla_w_gate_up[i][:, j * 256:(j + 1) * 256]),
                              "bg": f32(np.asarray(gla_b_gate[i][j * 256:(j + 1) * 256]).reshape(2, 128).T),
                              "gain": f32(gla_out_gain[i]).reshape(1, 512)})
            rg = _run(_prog("gla", lambda: build_gla(cfg)), ims_g)
            ofull = [np.concatenate([rg[b * 4 + j]["oT"] for j in range(4)], axis=0) for b in range(2)]
            wo = f32(gla_w_out[i])
        last = (l == L - 1)
        nc = _prog(("dense", last), lambda: build_dense(cfg, last))
        wfi, wfo = f32(ffn_w_in[l]), f32(ffn_w_out[l])
        ims = []
        for ci, (b, j) in enumerate(cores):
            v2 = np.zeros((128, 2, 6 * KD), np.float32)
            v2[:, 0] = vec[b][:, l]
            if not last:
                v2[:, 1] = vec[b][:, l + 1]
            ims.append({"consts": consts, "xT": xT[ci], "oT": np.ascontiguousarray(ofull[b][:, j * T:(j + 1) * T]),
                        "wo": wo, "wfi": wfi, "wfo": wfo, "vec": v2})
        r = _run(nc, ims)
        xT = [r[ci]["xTo"] for ci in range(8)]
        if not last:
            hT = [r[ci]["hTo"] for ci in range(8)]

    out = np.empty((2, S, D), np.float32)
    for ci, (b, j) in enumerate(cores):
        out[b, j * T:(j + 1) * T, :] = xT[ci].T
    return out


GROUPS = [[0, 1, 2, 3], [4, 5, 6, 7]]
PRECAST = False


def build_fused(cfg):
    L, D, KD, T, S, DFF, KO = cfg["L"], cfg["D"], cfg["KD"], cfg["T"], cfg["S"], cfg["DFF"], cfg["KO"]
    NT = T // 512
    NQ = D // 256
    nc = bass.Bass("TRN2", target_bir_lowering=False)
    gstack = contextlib.ExitStack()
    K = Kern(nc, gstack)
    cst = _din(nc, "consts", [128, 640])
    cT = _din(nc, "cT", [128, KD])
    modw = _din(nc, "modw", [1, D, 6 * D])
    modb = _din(nc, "modb", [1, 6 * D])
    gmix = _din(nc, "gmix", [128, 1, KD])
    gffn = _din(nc, "gffn", [128, 1, KD])
    xT_in = _din(nc, "xT", [D, T])
    wfi = _din(nc, "wfi", [L, D, 2 * DFF])
    wfo = _din(nc, "wfo", [L, DFF, D])
    wo = [_din(nc, "wo%d" % l, [KO * 128, D]) for l in range(L)]
    ab, gl = {}, {}
    for i in range((L + 1) // 2):
        ab[i] = dict(wh=_din(nc, "wh%d" % i, [D, 1024]), lbl=_din(nc, "lbl%d" % i, [128, 2, 2]),
                     again=_din(nc, "again%d" % i, [1, 256]), wf=_din(nc, "wf%d" % i, [D, 1026]),
                     qkg=_din(nc, "qkg%d" % i, [128, 2]), fbias=_din(nc, "fbias%d" % i, [1, 2]))
    for i in range(L // 2):
        gl[i] = dict(wg=_din(nc, "wg%d" % i, [D, 1552]), wgu=_din(nc, "wgu%d" % i, [16, 256]),
                     bg=_din(nc, "bg%d" % i, [128, 2]), ggain=_din(nc, "ggain%d" % i, [1, 512]))
    xTo = _dout(nc, "xTo", [D, T])
    vin = nc.dram_tensor("i_vin", [128, 6 * KD], F32)
    vall = nc.dram_tensor("i_vall", [4 * 128, 6 * KD], F32)
    xs = nc.dram_tensor("i_xs", [D, T], F32)
    HK = KD // 2
    hq = nc.dram_tensor("i_hq", [NT * 2, HK * 128, 512], BF16)
    hf = nc.dram_tensor("i_hf", [NT * 2, 4 * HK * 128, 512], BF16)
    OC = 2 if NT % 2 == 0 else 1
    NOC = (S // 512) // OC
    oqs = {nm: nc.dram_tensor("i_oq" + nm, [NOC, rows, OC * 512], BF16) for nm, rows in (("h", 256), ("f", 256), ("g", 512))}
    ofs = {nm: nc.dram_tensor("i_of" + nm, [NOC, 4 * rows, OC * 512], BF16)
           for nm, rows in (("h", 256), ("f", 256), ("g", 512))}
    wbo = nc.dram_tensor("i_wbo", [KO * 128, D], BF16)
    wbi = nc.dram_tensor("i_wbi", [D, 2 * DFF], BF16)
    wbf = nc.dram_tensor("i_wbf", [DFF, D], BF16)
    b_cv = [Buf() for _ in range(4)]
    conv = {"q": [], "per": 1, "n": 0}

    def conv_plan(l, n_calls):
        q = []
        if not PRECAST:
            conv["q"] = q
            return
        for (src, dst, rows, step) in ((wo[l], wbo.ap(), KO * 128, 512), (wfi[l], wbi.ap(), D, 128),
                                       (wfo[l], wbf.ap(), DFF, 512)):
            for r0 in range(0, rows, step):
                r1 = min(rows, r0 + step)
                q.append((src[r0:r1, :], dst[r0:r1, :]))
        conv["q"] = q
        conv["per"] = -(-len(q) // n_calls)

    def conv_step(n=None):
        n = conv["per"] if n is None else n
        for _ in range(min(n, len(conv["q"]))):
            src, dst = conv["q"].pop(0)
            key = b_cv[conv["n"] % 4]
            conv["n"] += 1
            K.dma(K.pool, lambda e, src=src, dst=dst: e.dma_start(out=dst, in_=src), [], [], key=key)

    b_vin, b_vall = Buf(), Buf()
    b_hq = [Buf() for _ in range(NT * 2)]
    b_hf = [Buf() for _ in range(NT * 2)]
    b_oq = {nm: [Buf() for _ in range(NOC)] for nm in "hfg"}
    b_of = {nm: [Buf() for _ in range(NOC)] for nm in "hfg"}
    vview = vall.ap().rearrange("(l p) c -> p l c", p=128)
    pending = []

    def flush_colls():
        while pending:
            pending.pop(0)()

    def h_store(t, tile, buf):
        for half in range(2):
            i = t * 2 + half
            dst = hq.ap()[i].rearrange("(k p) t -> p k t", p=128)
            K.dma(K.sp, lambda e, dst=dst, half=half: e.dma_start(out=dst, in_=tile[:, half * HK:(half + 1) * HK, :]),
                  [buf], [b_hq[i]], key=buf)
            pending.append(lambda i=i: K.coll("AllGather", hq.ap()[i], hf.ap()[i], [b_hq[i]], [b_hf[i]], GROUPS))

    def h_load(t, dst, buf):
        r, tl = t // NT, t % NT
        for half in range(2):
            i = tl * 2 + half
            src = hf.ap()[i].rearrange("(r k p) t -> p r k t", r=4, p=128)[:, r]
            K.dma(K.sp, lambda e, src=src, half=half: e.dma_start(out=dst[:, half * HK:(half + 1) * HK, :], in_=src),
                  [b_hf[i]], [buf], key=buf)

    def make_o_store(nm):
        def o_store(t, src, buf, nh):
            ci, cl = t // OC, t % OC
            dst = oqs[nm].ap()[ci].rearrange("(h p) t -> p h t", p=128)[:, :, cl * 512:(cl + 1) * 512]
            K.dma(K.pool, lambda e: e.dma_start(out=dst, in_=src), [buf], [b_oq[nm][ci]], key=buf)
            flush_colls()
            if cl == OC - 1:
                pending.append(lambda: K.coll("AllGather", oqs[nm].ap()[ci], ofs[nm].ap()[ci], [b_oq[nm][ci]],
                                              [b_of[nm][ci]], GROUPS))
            conv_step()
        return o_store

    seg_cache = {}

    def make_o_load(names):
        def o_load(tt, oT, buf):
            k0 = 0
            for nm in names:
                rows = 4 * (256 if nm in "hf" else 512)
                nk = rows // 128

                def fn(e, nm=nm, nk=nk, k0=k0):
                    if "segC" not in seg_cache:
                        seg_cache["segC"] = (e.partition_id() % 4) * (NT // OC)
                    ci = seg_cache["segC"] + tt // OC
                    src = ofs[nm].ap().rearrange("c (k p) t -> p c k t", p=128)[
                        :, bass.ds(ci, 1), :, (tt % OC) * 512:(tt % OC + 1) * 512]
                    return e.dma_start(out=oT[:, k0:k0 + nk, :].rearrange("p (o k) t -> p o k t", o=1), in_=src)
                K.dma(K.sp, fn, [b for b in b_of[nm]], [buf], key=buf)
                k0 += nk
        return o_load

    C = Consts(K, cst, None)
    cfg1 = dict(cfg)
    cfg1["L"] = 1
    K.begin_phase()
    emit_mod(K, C, cfg1, cT, modw, modb, gmix, gffn, vin.ap().rearrange("p (l c) -> p l c", l=1), out_buf=b_vin)
    K.coll("AllGather", vin, vall, [b_vin], [b_vall], GROUPS)
    K.end_phase(last=True)
    K.begin_phase()
    emit_norm0(K, C, cfg, xT_in, vview, None, h_store=h_store, after_tile=flush_colls)
    K.end_phase()
    for l in range(L):
        i = l // 2
        if l % 2 == 0:
            conv_plan(l, 2 * (S // 512))
            K.begin_phase()
            emit_hgrn(K, C, cfg, i, None, ab[i]["wh"], ab[i]["lbl"], ab[i]["again"], None, h_load=h_load,
                      o_store=make_o_store("h"), after_setup=flush_colls)
            K.end_phase()
            K.begin_phase()
            emit_fox(K, C, cfg, None, ab[i]["wf"], ab[i]["qkg"], ab[i]["fbias"], None, h_load=h_load,
                     o_store=make_o_store("f"), after_setup=flush_colls)
            conv_step(len(conv["q"]))
            K.end_phase()
            names = "hf"
        else:
            conv_plan(l, S // 512)
            K.begin_phase()
            emit_gla(K, C, cfg, None, gl[i]["wg"], gl[i]["wgu"], gl[i]["bg"], gl[i]["ggain"], None, h_load=h_load,
                     o_store=make_o_store("g"), after_setup=flush_colls)
            conv_step(len(conv["q"]))
            K.end_phase()
            names = "g"
        last = (l == L - 1)
        K.begin_phase()
        dw = (wbo.ap(), wbi.ap(), wbf.ap()) if PRECAST else (wo[l], wfi[l], wfo[l])
        emit_dense(K, C, cfg, l, xT_in if l == 0 else xs.ap(), None, dw[0], dw[1], dw[2], vview,
                   xTo if last else xs.ap(), None, o_load=make_o_load(names), h_store=None if last else h_store,
                   after_proj=flush_colls, at_start=flush_colls)
        if last:
            flush_colls()
        K.end_phase(last=last)
    gstack.close()
    return nc


def dense_k_perm(kind):
    perm = []
    if kind == "ab":
        for typ in range(2):
            for r in range(4):
                for loc in range(256):
                    perm.append(r * 512 + typ * 256 + loc)
    else:
        perm = list(range(2048))
    return np.asarray(perm)


def kernel_fused(cfg, x, c, mod_w, mod_b, norm_mix_gain, norm_ffn_gain, ab_w_in, ab_w_out, hgrn_lb_logits,
                 hgrn_out_gain, fox_q_gain, fox_k_gain, fox_f_bias, gla_w_in, gla_w_gate_up, gla_b_gate, gla_out_gain,
                 gla_w_out, ffn_w_in, ffn_w_out):
    L, D, KD, T, S, DFF = cfg["L"], cfg["D"], cfg["KD"], cfg["T"], cfg["S"], cfg["DFF"]
    f32 = lambda a: np.ascontiguousarray(np.asarray(a, dtype=np.float32))
    x, c, mod_w, mod_b = f32(x), f32(c), f32(mod_w), f32(mod_b)
    consts = make_consts()
    ab_rows = []
    for j in range(4):
        for typ in range(2):
            for hl in range(2):
                base = typ * 1024 + (2 * j + hl) * 128
                ab_rows.extend(range(base, base + 128))
    ab_rows = np.asarray(ab_rows)
    wfi, wfo = f32(ffn_w_in), f32(ffn_w_out)
    shared = {"consts": consts, "wfi": wfi, "wfo": wfo}
    for l in range(L):
        i = l // 2
        if l % 2 == 0:
            shared["wo%d" % l] = np.ascontiguousarray(f32(ab_w_out[i])[ab_rows[dense_k_perm("ab")]])
        else:
            shared["wo%d" % l] = np.ascontiguousarray(f32(gla_w_out[i])[dense_k_perm("gla")])
    lbl_all = f32(hgrn_lb_logits)
    ims = []
    for cid in range(8):
        b, j = cid // 4, cid % 4
        im = dict(shared)
        im["cT"] = _fm(c[b], KD)
        im["modw"] = mod_w[j:j + 1]
        im["modb"] = mod_b[j:j + 1]
        im["gmix"] = _fm(f32(norm_mix_gain)[j:j + 1], KD)
        im["gffn"] = _fm(f32(norm_ffn_gain)[j:j + 1], KD)
        im["xT"] = np.ascontiguousarray(x[b, j * T:(j + 1) * T, :].T)
        hh = (2 * j, 2 * j + 1)
        for i in range((L + 1) // 2):
            w = f32(ab_w_in[i])
            cs = lambda base, h_: w[:, base + h_ * 128:base + (h_ + 1) * 128]
            im["wh%d" % i] = np.ascontiguousarray(np.concatenate(
                [cs(0, hh[0]), cs(1024, hh[0]), cs(0, hh[1]), cs(1024, hh[1]),
                 cs(2048, hh[0]), cs(2048, hh[1]), cs(3072, hh[0]), cs(3072, hh[1])], axis=1))
            im["lbl%d" % i] = f32(np.stack([np.stack([lbl_all[ly, h_ * 128:(h_ + 1) * 128] for h_ in hh], axis=1)
                                            for ly in range(2)], axis=1))
            im["again%d" % i] = f32(hgrn_out_gain[i][2 * j:2 * j + 2]).reshape(1, 256)
            im["wf%d" % i] = np.ascontiguousarray(np.concatenate(
                [cs(4096, hh[0]), cs(5120, hh[0]), cs(4096, hh[1]), cs(5120, hh[1]),
                 cs(6144, hh[0]), cs(6144, hh[1]), cs(7168, hh[0]), cs(7168, hh[1]),
                 w[:, 8192 + hh[0]:8192 + hh[0] + 1], w[:, 8192 + hh[1]:8192 + hh[1] + 1]], axis=1))
            im["qkg%d" % i] = f32(np.stack([fox_q_gain[i], fox_k_gain[i]], axis=1))
            im["fbias%d" % i] = f32(fox_f_bias[i][2 * j:2 * j + 2]).reshape(1, 2)
        for i in range(L // 2):
            w = f32(gla_w_in[i])
            im["wg%d" % i] = np.ascontiguousarray(np.concatenate(
                [w[:, j * 256:(j + 1) * 256], w[:, 1024 + j * 256:1024 + (j + 1) * 256], w[:, 6144:6160],
                 w[:, 2048 + j * 512:2048 + (j + 1) * 512], w[:, 4096 + j * 512:4096 + (j + 1) * 512]], axis=1))
            im["wgu%d" % i] = f32(gla_w_gate_up[i][:, j * 256:(j + 1) * 256])
            im["bg%d" % i] = f32(np.asarray(gla_b_gate[i][j * 256:(j + 1) * 256]).reshape(2, 128).T)
            im["ggain%d" % i] = f32(gla_out_gain[i]).reshape(1, 512)
        ims.append(im)
    nc = _prog("fused", lambda: build_fused(cfg))
    r = _run(nc, ims)
    out = np.empty((2, S, D), np.float32)
    for cid in range(8):
        b, j = cid // 4, cid % 4
        out[b, j * T:(j + 1) * T, :] = r[cid]["xTo"].T
    return out


def kernel(**inputs):
    return kernel_fused(CFG, **inputs)
```
